# Optimizing a Trainium2 kernel written in Bass

```python
import math
import jax, jax.numpy as jnp
from jax import lax
import numpy as np

D_MODEL = 1024
BATCH = 8
SEQ = 2048
DEPTH = 2

GRID_W = 64
CTX_LEN = 256
EPS = 1e-6
NEG_INF = -1e30
f32 = jnp.float32

SSD_HEADS = 16
SSD_HEAD_DIM = 64
SSD_GROUPS = 2
SSD_STATE = 128
SSD_CONV = 5
SSD_CHUNK = 128
SSD_INNER = SSD_HEADS * SSD_HEAD_DIM
SSD_BC = SSD_GROUPS * SSD_STATE
SSD_XBC = SSD_INNER + 2 * SSD_BC
ATT_HEADS = 16
ATT_KV_HEADS = 4
ATT_HEAD_DIM = 64
WINDOW = 128
ATT_BLOCK = 128
ROPE_THETA = 10000.0
ATT_Q = ATT_HEADS * ATT_HEAD_DIM
ATT_KV = 2 * ATT_KV_HEADS * ATT_HEAD_DIM
EVEN_SPLITS = [SSD_INNER,
               SSD_INNER + SSD_XBC,
               SSD_INNER + SSD_XBC + 2 * SSD_HEADS,
               SSD_INNER + SSD_XBC + 2 * SSD_HEADS + ATT_Q,
               SSD_INNER + SSD_XBC + 2 * SSD_HEADS + ATT_Q + ATT_KV]
EVEN_IN = SSD_INNER + SSD_XBC + 2 * SSD_HEADS + ATT_Q + ATT_KV + ATT_Q
EVEN_OUT = SSD_INNER + ATT_Q
S5_WIDTH = 1024
S5_GROUP_CH = 16
S5_GROUPS = S5_WIDTH // S5_GROUP_CH
S5_STATE = 64

N_EVEN = (DEPTH + 1) // 2
N_ODD = DEPTH // 2

kernel_name = "hybrid_ssd_swa_s5_prefix_dit"


def rmsnorm(x, w):
    xf = x.astype(f32)
    y = xf * lax.rsqrt(jnp.mean(xf * xf, axis=-1, keepdims=True) + EPS)
    return (y * w.astype(f32)).astype(x.dtype)


def adaln(cvec, w, b):
    mod = (jax.nn.silu(cvec) @ w + b)[..., None, :]
    shift, scale, gate = jnp.split(mod, 3, axis=-1)
    return shift, scale, gate


def norm_mod(x, w, shift, scale):
    return rmsnorm(x, w) * (1 + scale) + shift


def sink_softmax(scores, sink_b):
    sizes = [s.shape[-1] for s in scores]
    sink_col = jnp.broadcast_to(sink_b, scores[0].shape[:-1] + (1,))
    p = jax.nn.softmax(jnp.concatenate(scores + [sink_col], axis=-1), axis=-1)
    return jnp.split(p[..., :-1], np.cumsum(sizes)[:-1].tolist(), axis=-1)


def dwconv(x, w, b):
    ch = x.shape[-1]
    y = lax.conv_general_dilated(x, w[:, None, :].astype(x.dtype), window_strides=(1,),
                                 padding=[(SSD_CONV // 2, SSD_CONV // 2)],
                                 dimension_numbers=('NWC', 'WIO', 'NWC'),
                                 feature_group_count=ch)
    return y + b


def ssd_inputs(xbc_raw, dt_raw, conv_w, conv_b, dt_bias):
    bsz, L, _ = xbc_raw.shape
    xbc = jax.nn.silu(dwconv(xbc_raw, conv_w, conv_b))
    xs, bm, cm = jnp.split(xbc, [SSD_INNER, SSD_INNER + SSD_BC], axis=-1)
    xs = xs.reshape(bsz, L, SSD_HEADS, SSD_HEAD_DIM)
    bm = bm.reshape(bsz, L, SSD_GROUPS, SSD_STATE)
    cm = cm.reshape(bsz, L, SSD_GROUPS, SSD_STATE)
    dt = jax.nn.softplus((dt_raw.reshape(bsz, L, 2, SSD_HEADS) + dt_bias).astype(f32))
    return xs, bm, cm, dt


def ssd_scan(x, dt, a, bm, cm, h0):
    bsz, L, H, P = x.shape
    G, N = bm.shape[2], bm.shape[3]
    R = H // G
    Q = SSD_CHUNK
    nc = L // Q
    xc = (x * dt[..., None]).reshape(bsz, nc, Q, G, R, P)
    acum = jnp.cumsum((dt * a).reshape(bsz, nc, Q, G, R), axis=2)
    bc = bm.reshape(bsz, nc, Q, G, N)
    cc = cm.reshape(bsz, nc, Q, G, N)
    lower = jnp.tril(jnp.ones((Q, Q), dtype=bool))[:, :, None, None]
    seg = acum[:, :, :, None] - acum[:, :, None, :]
    decay = jnp.exp(jnp.where(lower, seg, NEG_INF))
    cb = jnp.einsum('bclgn,bcsgn->bclsg', cc, bc)
    y_diag = jnp.einsum('bclsg,bclsgr,bcsgrp->bclgrp', cb, decay, xc)
    decay_end = jnp.exp(acum[:, :, -1:] - acum)
    states = jnp.einsum('bclgn,bclgr,bclgrp->bcgrpn', bc, decay_end, xc)
    chunk_decay = jnp.exp(acum[:, :, -1])

    def step(h, inp):
        s, d = inp
        return h * d[..., None, None] + s, h

    h_last, h_prev = lax.scan(step, h0.reshape(bsz, G, R, P, N).astype(states.dtype),
                              (jnp.moveaxis(states, 1, 0), jnp.moveaxis(chunk_decay, 1, 0)))
    h_prev = jnp.moveaxis(h_prev, 0, 1)
    y_off = jnp.einsum('bclgn,bcgrpn,bclgr->bclgrp', cc, h_prev, jnp.exp(acum))
    y = (y_diag + y_off).reshape(bsz, L, H, P)
    return y, h_last.reshape(bsz, H, P, N)


def ssd_bidir(xs, bm, cm, dt, a, h0_f, h0_b):
    flip = lambda t: jnp.flip(t, axis=1)
    y_f, h_f = ssd_scan(xs, dt[:, :, 0], a[0], bm, cm, h0_f)
    y_b, h_b = ssd_scan(flip(xs), flip(dt[:, :, 1]), a[1], flip(bm), flip(cm), h0_b)
    return y_f + flip(y_b), h_f, h_b


def ssd_output(y, xs, z, d_skip, norm_w):
    bsz, L = y.shape[:2]
    y = (y + xs * d_skip[:, None]).reshape(bsz, L, SSD_INNER)
    return rmsnorm(y * jax.nn.silu(z), norm_w)


def axial_rope(L):
    rows = L // GRID_W
    row = jnp.repeat(jnp.arange(rows, dtype=f32), GRID_W)
    col = jnp.tile(jnp.arange(GRID_W, dtype=f32), rows)
    n_freq = ATT_HEAD_DIM // 4
    inv = ROPE_THETA ** (-jnp.arange(n_freq, dtype=f32) / n_freq)
    ang = jnp.concatenate([row[:, None] * inv, col[:, None] * inv], axis=-1)
    return jnp.cos(ang), jnp.sin(ang)


def apply_rope(t, cos, sin):
    half = t.shape[-1] // 2
    t1, t2 = t[..., :half], t[..., half:]
    cos = cos[None, :, None, :].astype(t.dtype)
    sin = sin[None, :, None, :].astype(t.dtype)
    return jnp.concatenate([t1 * cos - t2 * sin, t1 * sin + t2 * cos], axis=-1)


def window_attention(q, k, v, kc, vc, sink):
    bsz, L, Hq, Dh = q.shape
    Hk = k.shape[2]
    R = Hq // Hk
    T = ATT_BLOCK
    nb = L // T
    scale = Dh ** -0.5
    qb = jnp.moveaxis(q.reshape(bsz, nb, T, Hk, R, Dh), 1, 0)

    def band(t):
        tp = jnp.pad(t, ((0, 0), (T, T), (0, 0), (0, 0))).reshape(bsz, nb + 2, T, Hk, Dh)
        w = jnp.concatenate([tp[:, :-2], tp[:, 1:-1], tp[:, 2:]], axis=2)
        return jnp.moveaxis(w, 1, 0)

    kb, vb = band(k), band(v)
    qpos = jnp.arange(nb)[:, None] * T + jnp.arange(T)[None, :]
    kpos = jnp.arange(nb)[:, None] * T - T + jnp.arange(3 * T)[None, :]
    valid = ((jnp.abs(qpos[:, :, None] - kpos[:, None, :]) <= WINDOW)
             & (kpos[:, None, :] >= 0) & (kpos[:, None, :] < L))
    sink_b = sink.reshape(Hk, R)[None, :, :, None, None].astype(f32)

    def block(args):
        qi, ki, vi, mi = args
        s_loc = jnp.einsum('bqhrd,bkhd->bhrqk', qi, ki).astype(f32) * scale
        s_loc = jnp.where(mi[None, None, None], s_loc, NEG_INF)
        s_ctx = jnp.einsum('bqhrd,bmhd->bhrqm', qi, kc).astype(f32) * scale
        p_loc, p_ctx = sink_softmax([s_loc, s_ctx], sink_b)
        return (jnp.einsum('bhrqk,bkhd->bqhrd', p_loc.astype(vi.dtype), vi)
                + jnp.einsum('bhrqm,bmhd->bqhrd', p_ctx.astype(vc.dtype), vc))

    o = lax.map(block, (qb, kb, vb, valid))
    return jnp.moveaxis(o, 0, 1).reshape(bsz, L, Hq * Dh)


def ctx_attention(qc, kc, vc, sink):
    bsz, C, Hq, Dh = qc.shape
    Hk = kc.shape[2]
    R = Hq // Hk
    qg = qc.reshape(bsz, C, Hk, R, Dh)
    s = jnp.einsum('bqhrd,bkhd->bhrqk', qg, kc).astype(f32) * (Dh ** -0.5)
    (p,) = sink_softmax([s], sink.reshape(Hk, R)[None, :, :, None, None].astype(f32))
    o = jnp.einsum('bhrqk,bkhd->bqhrd', p.astype(vc.dtype), vc)
    return o.reshape(bsz, C, Hq * Dh)


def s5_discretise(lam_re, lam_im, log_step, b_re, b_im):
    lam_re, lam_im = lam_re.astype(f32), lam_im.astype(f32)
    dt = jnp.exp(log_step.astype(f32))[:, None]
    mag = jnp.exp(lam_re * dt)
    ab_re, ab_im = mag * jnp.cos(lam_im * dt), mag * jnp.sin(lam_im * dt)
    num_re, num_im = ab_re - 1.0, ab_im
    den = lam_re * lam_re + lam_im * lam_im
    coef_re = (num_re * lam_re + num_im * lam_im) / den
    coef_im = (num_im * lam_re - num_re * lam_im) / den
    b_re, b_im = b_re.astype(f32), b_im.astype(f32)
    bb_re = coef_re[..., None] * b_re - coef_im[..., None] * b_im
    bb_im = coef_re[..., None] * b_im + coef_im[..., None] * b_re
    return ab_re, ab_im, bb_re, bb_im


def s5_scan(u, ab_re, ab_im, bb_re, bb_im, h0_re, h0_im):
    L = u.shape[1]
    bu_re = jnp.einsum('blgc,gnc->blgn', u, bb_re)
    bu_im = jnp.einsum('blgc,gnc->blgn', u, bb_im)
    bu_re = bu_re.at[:, 0].add(ab_re * h0_re - ab_im * h0_im)
    bu_im = bu_im.at[:, 0].add(ab_re * h0_im + ab_im * h0_re)
    a_re = jnp.broadcast_to(ab_re, (1, L) + ab_re.shape)
    a_im = jnp.broadcast_to(ab_im, (1, L) + ab_im.shape)

    def combine(e1, e2):
        a1r, a1i, b1r, b1i = e1
        a2r, a2i, b2r, b2i = e2
        return (a2r * a1r - a2i * a1i, a2r * a1i + a2i * a1r,
                a2r * b1r - a2i * b1i + b2r, a2r * b1i + a2i * b1r + b2i)

    _, _, h_re, h_im = lax.associative_scan(combine, (a_re, a_im, bu_re, bu_im), axis=1)
    return h_re, h_im


def s5_bidir(u, disc, init):
    hf_re, hf_im = s5_scan(u, *disc[0], *init[0])
    hb_re, hb_im = s5_scan(jnp.flip(u, axis=1), *disc[1], *init[1])
    final = [(hf_re[:, -1], hf_im[:, -1]), (hb_re[:, -1], hb_im[:, -1])]
    return hf_re + jnp.flip(hb_re, axis=1), hf_im + jnp.flip(hb_im, axis=1), final


def s5_output(s_re, s_im, u, c_re, c_im, d_skip, glu_w, glu_b):
    bsz, L = u.shape[:2]
    y = (jnp.einsum('blgn,gcn->blgc', s_re, c_re.astype(f32))
         - jnp.einsum('blgn,gcn->blgc', s_im, c_im.astype(f32)))
    y = y.reshape(bsz, L, S5_WIDTH) + d_skip * u.reshape(bsz, L, S5_WIDTH)
    y = jax.nn.gelu(y)
    return y * jax.nn.sigmoid(y @ glu_w + glu_b)


def even_layer(x, xc, c, c_ctx, norm_w, ada_w, ada_b, w_in, conv_w, conv_b, dt_bias, a_log,
               d_skip, ssd_norm_w, sink, w_out, ctx_out):
    bsz, L, _ = x.shape
    n_ctx = xc.shape[1]
    shift, scale, gate = adaln(c, ada_w, ada_b)
    shift_c, scale_c, gate_c = adaln(c_ctx, ada_w, ada_b)
    proj = norm_mod(x, norm_w, shift, scale) @ w_in
    proj_c = norm_mod(xc, norm_w, shift_c, scale_c) @ w_in
    z, xbc, dt_raw, q, kv, g = jnp.split(proj, EVEN_SPLITS, axis=-1)
    z_c, xbc_c, dt_raw_c, q_c, kv_c, g_c = jnp.split(proj_c, EVEN_SPLITS, axis=-1)
    a = -jnp.exp(a_log.astype(f32))

    xs_c, bm_c, cm_c, dt_c = ssd_inputs(xbc_c, dt_raw_c, conv_w, conv_b, dt_bias)
    h0 = jnp.zeros((bsz, SSD_HEADS, SSD_HEAD_DIM, SSD_STATE), f32)
    y_c, h_f, h_b = ssd_bidir(xs_c, bm_c, cm_c, dt_c, a, h0, h0)
    xs, bm, cm, dt = ssd_inputs(xbc, dt_raw, conv_w, conv_b, dt_bias)
    y, _, _ = ssd_bidir(xs, bm, cm, dt, a, h_f, h_b)
    ssd_out = ssd_output(y, xs, z, d_skip, ssd_norm_w)

    k_c, v_c = [t.reshape(bsz, n_ctx, ATT_KV_HEADS, ATT_HEAD_DIM) for t in jnp.split(kv_c, 2, axis=-1)]
    k, v = [t.reshape(bsz, L, ATT_KV_HEADS, ATT_HEAD_DIM) for t in jnp.split(kv, 2, axis=-1)]
    cos, sin = axial_rope(L)
    q = apply_rope(q.reshape(bsz, L, ATT_HEADS, ATT_HEAD_DIM), cos, sin)
    k = apply_rope(k, cos, sin)
    att = window_attention(q, k, v, k_c, v_c, sink) * jax.nn.silu(g)
    x = x + gate * (jnp.concatenate([ssd_out, att], axis=-1) @ w_out)

    if ctx_out:
        att_c = ctx_attention(q_c.reshape(bsz, n_ctx, ATT_HEADS, ATT_HEAD_DIM), k_c, v_c, sink) * jax.nn.silu(g_c)
        ssd_c = ssd_output(y_c, xs_c, z_c, d_skip, ssd_norm_w)
        xc = xc + gate_c * (jnp.concatenate([ssd_c, att_c], axis=-1) @ w_out)
    return x, xc


def odd_layer(x, xc, c, c_ctx, norm_w, ada_w, ada_b, w_in, lam_re, lam_im, log_step, b_re, b_im,
              c_re, c_im, d_skip, glu_w, glu_b, w_out, ctx_out):
    bsz, L, _ = x.shape
    n_ctx = xc.shape[1]
    shift, scale, gate = adaln(c, ada_w, ada_b)
    shift_c, scale_c, gate_c = adaln(c_ctx, ada_w, ada_b)
    u, g = jnp.split(norm_mod(x, norm_w, shift, scale) @ w_in, 2, axis=-1)
    hc = norm_mod(xc, norm_w, shift_c, scale_c)
    if ctx_out:
        u_c, g_c = jnp.split(hc @ w_in, 2, axis=-1)
    else:
        u_c = hc @ w_in[:, :S5_WIDTH]
    disc = [s5_discretise(lam_re[d], lam_im[d], log_step[d], b_re, b_im) for d in range(2)]

    ug_c = u_c.reshape(bsz, n_ctx, S5_GROUPS, S5_GROUP_CH).astype(f32)
    zero = jnp.zeros((bsz, S5_GROUPS, S5_STATE), f32)
    sc_re, sc_im, final_c = s5_bidir(ug_c, disc, [(zero, zero), (zero, zero)])
    ug = u.reshape(bsz, L, S5_GROUPS, S5_GROUP_CH).astype(f32)
    s_re, s_im, _ = s5_bidir(ug, disc, final_c)
    y = s5_output(s_re, s_im, ug, c_re, c_im, d_skip, glu_w, glu_b) * jax.nn.silu(g)
    x = x + gate * (y @ w_out)

    if ctx_out:
        y_c = s5_output(sc_re, sc_im, ug_c, c_re, c_im, d_skip, glu_w, glu_b) * jax.nn.silu(g_c)
        xc = xc + gate_c * (y_c @ w_out)
    return x, xc


def setup_inputs(seed: int = 0) -> dict:
    key = jax.random.key(seed)
    ks = iter(jax.random.split(key, 48))
    nrm = lambda shape, s: jax.random.normal(next(ks), shape, f32) * s
    D = D_MODEL
    inp = {}
    inp['x'] = nrm((BATCH, SEQ, D), 1.0)
    inp['c'] = nrm((BATCH, D), 1.0)
    inp['ctx'] = nrm((BATCH, CTX_LEN, D), 1.0)
    inp['c_ctx'] = nrm((D,), 1.0)
    NE = N_EVEN
    inp['e_norm_w'] = 1.0 + nrm((NE, D), 0.02)
    inp['e_ada_w'] = nrm((NE, D, 3 * D), D ** -0.5)
    inp['e_ada_b'] = nrm((NE, 3 * D), 0.02)
    inp['e_w_in'] = nrm((NE, D, EVEN_IN), D ** -0.5)
    inp['e_conv_w'] = nrm((NE, SSD_CONV, SSD_XBC), SSD_CONV ** -0.5)
    inp['e_conv_b'] = nrm((NE, SSD_XBC), 0.02)
    u = jax.random.uniform(next(ks), (NE, 2, SSD_HEADS), f32)
    dt0 = jnp.exp(u * (math.log(0.1) - math.log(0.001)) + math.log(0.001))
    inp['e_dt_bias'] = dt0 + jnp.log(-jnp.expm1(-dt0))
    inp['e_a_log'] = jnp.log(jax.random.uniform(next(ks), (NE, 2, SSD_HEADS), f32, 1.0, 16.0))
    inp['e_d_skip'] = 1.0 + nrm((NE, SSD_HEADS), 0.02)
    inp['e_ssd_norm_w'] = 1.0 + nrm((NE, SSD_INNER), 0.02)
    inp['e_sink'] = nrm((NE, ATT_HEADS), 0.5)
    inp['e_w_out'] = nrm((NE, EVEN_OUT, D), EVEN_OUT ** -0.5)
    NO = N_ODD
    inp['o_norm_w'] = 1.0 + nrm((NO, D), 0.02)
    inp['o_ada_w'] = nrm((NO, D, 3 * D), D ** -0.5)
    inp['o_ada_b'] = nrm((NO, 3 * D), 0.02)
    inp['o_w_in'] = nrm((NO, D, 2 * S5_WIDTH), D ** -0.5)
    inp['o_lam_re'] = -0.5 + nrm((NO, 2, S5_GROUPS, S5_STATE), 0.01)
    inp['o_lam_im'] = math.pi * jnp.arange(S5_STATE, dtype=f32) + nrm((NO, 2, S5_GROUPS, S5_STATE), 0.01)
    us = jax.random.uniform(next(ks), (NO, 2, S5_GROUPS), f32)
    inp['o_log_step'] = us * (math.log(0.1) - math.log(0.001)) + math.log(0.001)
    inp['o_b_re'] = nrm((NO, S5_GROUPS, S5_STATE, S5_GROUP_CH), (2 * S5_GROUP_CH) ** -0.5)
    inp['o_b_im'] = nrm((NO, S5_GROUPS, S5_STATE, S5_GROUP_CH), (2 * S5_GROUP_CH) ** -0.5)
    inp['o_c_re'] = nrm((NO, S5_GROUPS, S5_GROUP_CH, S5_STATE), S5_STATE ** -0.5)
    inp['o_c_im'] = nrm((NO, S5_GROUPS, S5_GROUP_CH, S5_STATE), S5_STATE ** -0.5)
    inp['o_d_skip'] = nrm((NO, S5_WIDTH), 1.0)
    inp['o_glu_w'] = nrm((NO, S5_WIDTH, S5_WIDTH), S5_WIDTH ** -0.5)
    inp['o_glu_b'] = nrm((NO, S5_WIDTH), 0.02)
    inp['o_w_out'] = nrm((NO, S5_WIDTH, D), S5_WIDTH ** -0.5)
    inp['final_norm_w'] = 1.0 + nrm((D,), 0.02)
    return inp


def reference(x, c, ctx, c_ctx, e_norm_w, e_ada_w, e_ada_b, e_w_in, e_conv_w, e_conv_b, e_dt_bias,
              e_a_log, e_d_skip, e_ssd_norm_w, e_sink, e_w_out, o_norm_w, o_ada_w, o_ada_b, o_w_in,
              o_lam_re, o_lam_im, o_log_step, o_b_re, o_b_im, o_c_re, o_c_im, o_d_skip, o_glu_w,
              o_glu_b, o_w_out, final_norm_w):
    xc = ctx
    for i in range(DEPTH):
        ctx_out = i < DEPTH - 1
        j = i // 2
        if i % 2 == 0:
            x, xc = even_layer(x, xc, c, c_ctx, e_norm_w[j], e_ada_w[j], e_ada_b[j], e_w_in[j],
                               e_conv_w[j], e_conv_b[j], e_dt_bias[j], e_a_log[j], e_d_skip[j],
                               e_ssd_norm_w[j], e_sink[j], e_w_out[j], ctx_out)
        else:
            x, xc = odd_layer(x, xc, c, c_ctx, o_norm_w[j], o_ada_w[j], o_ada_b[j], o_w_in[j],
                              o_lam_re[j], o_lam_im[j], o_log_step[j], o_b_re[j], o_b_im[j],
                              o_c_re[j], o_c_im[j], o_d_skip[j], o_glu_w[j], o_glu_b[j],
                              o_w_out[j], ctx_out)
    return rmsnorm(x, final_norm_w)
```

```python
import math
import os
from contextlib import ExitStack

import numpy as np
import concourse.bass as bass
import concourse.mybir as mybir
from concourse.bass_utils import run_bass_kernel_spmd

F32 = mybir.dt.float32
AF = mybir.ActivationFunctionType
ALU = mybir.AluOpType

D = 1024
T = 2304
NT = 18
CTX = 256
L = 2048
EPS = 1e-6
TG = [(0, 256), (256, 512), (768, 512), (1280, 512), (1792, 512)]

SAME_ENGINE_SYNC = True
SEM_EPOCH = 30000


class V:
    __slots__ = ("buf", "ap")

    def __init__(self, buf, ap):
        self.buf = buf
        self.ap = ap


class Buf:
    def __init__(self, name, h):
        self.name = name
        self.h = h
        self.last_w = None
        self.readers = []

    def __getitem__(self, idx):
        return V(self, self.h[idx])

    def full(self):
        return V(self, self.h.ap())

    def view(self, offset, pattern):
        return V(self, bass.AP(self.h, offset, [list(p) for p in pattern]))


class Sched:
    ENG = ("pe", "act", "dve", "pool", "sp")

    def __init__(self, nc):
        self.nc = nc
        self.prog = {e: [] for e in self.ENG}
        self.sem = {}
        self.cnt = {}
        self.semid = 0
        self.known = {e: {} for e in self.ENG}
        for e in ("pe", "act", "dve", "pool"):
            self._new_engine_sem(e)
        self.nds = 8
        self.dsem = {}
        self.duse = {}
        self.dcnt = {}
        for q in ("sp", "pool"):
            self.dsem[q] = []
            self.duse[q] = []
            for i in range(self.nds):
                key = "d_%s_%d" % (q, i)
                self.dsem[q].append((nc.alloc_semaphore(key), key))
                self.duse[q].append(0)
            self.dcnt[q] = 0
        self.n_ops = 0

    def _new_engine_sem(self, e):
        self.semid += 1
        key = "s_%s_%d" % (e, self.semid)
        self.sem[e] = (self.nc.alloc_semaphore(key), key)
        self.cnt[e] = 0

    def _deps(self, reads, writes):
        deps = {}

        def add(tok):
            if tok is None:
                return
            h, key, val = tok
            if key not in deps or deps[key][1] < val:
                deps[key] = (h, val)

        for r in reads:
            add(r.buf.last_w)
        for w in writes:
            add(w.buf.last_w)
            for t in w.buf.readers:
                add(t)
        return deps

    def _emit_waits(self, eng, deps, own_key=None):
        kn = self.known[eng]
        for key, (h, val) in deps.items():
            if key == own_key and not SAME_ENGINE_SYNC:
                continue
            if kn.get(key, 0) >= val:
                continue
            kn[key] = val
            self.prog[eng].append(("wait", h, val))

    def _update(self, tok, reads, writes):
        for w in writes:
            w.buf.last_w = tok
            w.buf.readers = []
        for r in reads:
            if r.buf.last_w is not tok:
                r.buf.readers.append(tok)

    def op(self, eng, fn, reads=(), writes=()):
        reads = [r for r in reads if r is not None]
        writes = list(writes)
        if self.cnt[eng] >= SEM_EPOCH:
            self._new_engine_sem(eng)
        h, key = self.sem[eng]
        own = None if eng == "pe" else key
        deps = self._deps(reads, writes)
        if eng == "pe":
            deps.pop(key, None)
        self._emit_waits(eng, deps, own_key=own)
        self.cnt[eng] += 1
        self.prog[eng].append(("op", fn, h, 1))
        tok = (h, key, self.cnt[eng])
        self._update(tok, reads, writes)
        self.n_ops += 1
        return tok

    def dma(self, out, in_, q="sp", **kw):
        deps = self._deps([in_], [out])
        self._emit_waits(q, deps)
        k = self.dcnt[q] % self.nds
        self.dcnt[q] += 1
        h, key = self.dsem[q][k]
        prev = 16 * self.duse[q][k]
        if prev > 0 and self.known[q].get(key, 0) < prev:
            self.known[q][key] = prev
            self.prog[q].append(("wait", h, prev))
        self.duse[q][k] += 1
        val = 16 * self.duse[q][k]
        o_ap, i_ap = out.ap, in_.ap
        self.prog[q].append(("op", lambda e: e.dma_start(out=o_ap, in_=i_ap, **kw), h, 16))
        tok = (h, key, val)
        self._update(tok, [in_], [out])
        self.n_ops += 1
        return tok

    def finish_dmas(self):
        for q in ("sp", "pool"):
            for k in range(self.nds):
                h, key = self.dsem[q][k]
                val = 16 * self.duse[q][k]
                if val > 0 and self.known[q].get(key, 0) < val:
                    self.known[q][key] = val
                    self.prog[q].append(("wait", h, val))

    def flush(self, name=None):
        self.finish_dmas()
        nc = self.nc
        prog = self.prog
        self.prog = {e: [] for e in self.ENG}

        def run(items, e):
            for it in items:
                if it[0] == "wait":
                    e.wait_ge(it[1], it[2])
                else:
                    inst = it[1](e)
                    inst.then_inc(it[2], it[3])

        with nc.Block() as block:
            if prog["sp"]:
                @block.sync
                def _(e):
                    run(prog["sp"], e)
            if prog["act"]:
                @block.scalar
                def _(e):
                    run(prog["act"], e)
            if prog["dve"]:
                @block.vector
                def _(e):
                    run(prog["dve"], e)
            if prog["pool"]:
                @block.gpsimd
                def _(e):
                    run(prog["pool"], e)
            if prog["pe"]:
                @block.tensor
                def _(e):
                    run(prog["pe"], e)

    def mm(self, out, pairs):
        n = len(pairs)

        def fn(e):
            inst = None
            for i, (l, r) in enumerate(pairs):
                inst = e.matmul(out.ap, l.ap, r.ap, start=(i == 0), stop=(i == n - 1))
            return inst

        self.op("pe", fn, reads=[p[0] for p in pairs] + [p[1] for p in pairs], writes=[out])

    def transpose(self, out, in_, ident):
        self.op("pe", lambda e: e.transpose(out.ap, in_.ap, ident.ap), reads=[in_, ident], writes=[out])

    def act(self, out, in_, func, bias=None, scale=None, accum=None):
        kw = {}
        reads = [in_]
        writes = [out]
        if bias is not None:
            if isinstance(bias, V):
                kw["bias"] = bias.ap
                reads.append(bias)
            else:
                kw["bias"] = bias
        if scale is not None:
            if isinstance(scale, V):
                kw["scale"] = scale.ap
                reads.append(scale)
            else:
                kw["scale"] = scale
        if accum is not None:
            kw["accum_out"] = accum.ap
            writes.append(accum)
        self.op("act", lambda e: e.activation(out.ap, in_.ap, func, **kw), reads=reads, writes=writes)

    def ts(self, out, in0, s1, s2, op0, op1=None, eng="dve"):
        reads = [in0]
        a1 = s1
        a2 = s2
        if isinstance(s1, V):
            reads.append(s1)
            a1 = s1.ap
        if isinstance(s2, V):
            reads.append(s2)
            a2 = s2.ap
        if op1 is None:
            self.op(eng, lambda e: e.tensor_scalar(out.ap, in0.ap, a1, a2, op0), reads=reads, writes=[out])
        else:
            self.op(eng, lambda e: e.tensor_scalar(out.ap, in0.ap, a1, a2, op0, op1), reads=reads, writes=[out])

    def tt(self, out, in0, in1, op, eng="dve"):
        self.op(eng, lambda e: e.tensor_tensor(out.ap, in0.ap, in1.ap, op), reads=[in0, in1], writes=[out])

    def stt(self, out, in0, scalar, in1, op0, op1):
        reads = [in0, in1]
        sc = scalar
        if isinstance(scalar, V):
            reads.append(scalar)
            sc = scalar.ap
        self.op("dve", lambda e: e.scalar_tensor_tensor(out.ap, in0.ap, sc, in1.ap, op0, op1),
                reads=reads, writes=[out])

    def copy(self, out, in_, eng="dve"):
        if eng == "act":
            self.op("act", lambda e: e.copy(out.ap, in_.ap), reads=[in_], writes=[out])
        else:
            self.op(eng, lambda e: e.tensor_copy(out.ap, in_.ap), reads=[in_], writes=[out])

    def recip(self, out, in_):
        self.op("dve", lambda e: e.reciprocal(out.ap, in_.ap), reads=[in_], writes=[out])

    def memset(self, out, val, eng="dve"):
        self.op(eng, lambda e: e.memset(out.ap, val), reads=[], writes=[out])


class Ctx:
    def __init__(self, nc, sched):
        self.nc = nc
        self.s = sched
        self.uid = 0

    def sb(self, es, name, shape, dtype=F32):
        self.uid += 1
        h = es.enter_context(self.nc.sbuf_tensor("%s_%d" % (name, self.uid), list(shape), dtype))
        return Buf(name, h)

    def ps(self, es, name, shape=(128, 512), dtype=F32):
        self.uid += 1
        h = es.enter_context(self.nc.psum_tensor("%s_%d" % (name, self.uid), list(shape), dtype))
        return Buf(name, h)

    def dram(self, name, shape, dtype=F32, kind="Internal"):
        h = self.nc.dram_tensor(name, list(shape), dtype, kind=kind)
        return Buf(name, h)


def bc_mid(v_buf, base_off, pstep, nparts, n_outer, outer_step, n_inner):
    return v_buf.view(base_off, [[pstep, nparts], [outer_step, n_outer], [0, n_inner]])


E_NCOL = 5152
OFF_Z = 0
OFF_XBC = 1024
OFF_DT = 2560
OFF_Q = 2592
OFF_KV = 3616
OFF_G = 4128
OFF_QS = 5152
OFF_KR = 6176
OFF_KSR = 6688
E_NCOL_EXT = 7200


ORDER = ["p1", "p2a", "p2b", "p2c", "p2d", "p2e", "p2f", "p2g", "p2h", "p3", "p4", "p5", "all"]


def build_program(debug=(), stop="all"):
    def go(tag):
        return ORDER.index(tag) <= ORDER.index(stop)
    nc = bass.Bass("TRN2", target_bir_lowering=False)
    s = Sched(nc)
    cx = Ctx(nc, s)
    dbg = set(debug)

    def din(name, shape):
        return Buf(name, nc.dram_tensor(name, list(shape), F32, kind="ExternalInput"))

    def dout(name, shape):
        return Buf(name, nc.dram_tensor(name, list(shape), F32, kind="ExternalOutput"))

    def scratch(name, shape):
        if name in dbg:
            return dout(name, shape)
        return Buf(name, nc.dram_tensor(name, list(shape), F32))

    xin = din("xin", [T, D])
    cvecT = din("cvecT", [128, 2, 8])
    consts = din("consts", [128, 6, 512])
    rope = din("rope", [128, 2, L])
    e_ada_w = din("e_ada_w", [D, 3 * D])
    e_ada_b = din("e_ada_b", [1, 3 * D])
    e_norm_wT = din("e_norm_wT", [128, 8])
    e_w_in = din("e_w_in", [D, E_NCOL_EXT])
    e_conv_wT = din("e_conv_wT", [128, 12, 5])
    e_conv_bT = din("e_conv_bT", [128, 12])
    e_dt_bias = din("e_dt_bias", [1, 32])
    e_a_log = din("e_a_log", [1, 32])
    e_d_skip = din("e_d_skip", [1, 16])
    e_ssd_norm_wT = din("e_ssd_norm_wT", [128, 8])
    e_sink = din("e_sink", [128, 8])
    e_w_out = din("e_w_out", [2 * D, D])
    o_ada_w = din("o_ada_w", [D, 3 * D])
    o_ada_b = din("o_ada_b", [1, 3 * D])
    o_norm_wT = din("o_norm_wT", [128, 8])
    o_w_in = din("o_w_in", [D, 2 * D])
    s5_lam = din("s5_lam", [128, 2, 3, 32])
    s5_b = din("s5_b", [128, 2, 32, 16])
    s5_c = din("s5_c", [128, 2, 32, 16])
    o_d_skip = din("o_d_skip", [1, D])
    o_glu_w = din("o_glu_w", [D, D])
    o_glu_b = din("o_glu_b", [1, D])
    o_w_out = din("o_w_out", [D, D])
    final_norm_w = din("final_norm_w", [1, D])
    out_t = dout("out", [L, D])

    XS = scratch("XS", [T, 1024])
    BTOK = scratch("BTOK", [T, 256])
    BT = scratch("BT", [2, 128, T])
    CT = scratch("CT", [2, 128, T])
    SZ = scratch("SZ", [T, 1024])
    QR = scratch("QR", [8, 128, L])
    QC = scratch("QC", [8, 128, CTX])
    KR = scratch("KR", [4, 128, L])
    KC = scratch("KC", [4, 128, CTX])
    VT = scratch("VT", [T, 256])
    SG = scratch("SG", [8, 128, T])
    YF = scratch("YF", [T, 1024])
    YT = scratch("YT", [16, 128, T])
    X1 = scratch("X1", [T, 1024])
    U = scratch("U", [T, 1024])
    SG1 = scratch("SG1", [T, 1024])
    YTOK = scratch("YTOK", [T, 1024])
    KFP = scratch("KFP", [64, 16, 15, 16])
    KBR = scratch("KBR", [64, 16, 15, 16])
    HT = scratch("HT", [8, 128, T]) if "HT" in dbg else None
    DTD = scratch("DTD", [T, 32]) if "DTD" in dbg else None
    MODD = scratch("MODD", [4, 128, 24]) if "MODD" in dbg else None

    with ExitStack() as top:
        banks = [cx.ps(top, "bank%d" % i) for i in range(8)]
        cst = cx.sb(top, "cst", [128, 6, 512])
        s.dma(cst.full(), consts.full())
        ident = cst[:, 0, 0:128]
        tri = cst[:, 1, 0:128]
        utri = cst[:, 2, 0:128]
        ones = cst[:, 3, 0:128]
        modT = [[cx.sb(top, "modT%d%d" % (l, w), [128, 24]) for w in range(2)] for l in range(2)]
        gate_bc = [[cx.sb(top, "gate%d%d" % (l, w), [128, 1024]) for w in range(2)] for l in range(2)]
        scs = cx.sb(top, "scs", [128, 2, 8])

        def adaln_phase(layer, ada_w, ada_b):
            with ExitStack() as es:
                aw = [cx.sb(es, "aw%d" % k, [128, 3 * D]) for k in range(8)]
                ab = cx.sb(es, "ab", [1, 3 * D])
                modrow = [cx.sb(es, "modrow%d" % w, [1, 3 * D]) for w in range(2)]
                if layer == 0:
                    cv = cx.sb(es, "cv", [128, 2, 8])
                    s.dma(cv.full(), cvecT.full())
                    s.act(scs.full(), cv.full(), AF.Silu)
                for k in range(8):
                    s.dma(aw[k].full(), ada_w[k * 128:(k + 1) * 128, :])
                s.dma(ab.full(), ada_b.full())
                bi = 0
                for w in range(2):
                    for fg in range(6):
                        bk = banks[bi % 8]
                        bi += 1
                        s.mm(bk[0:1, :], [(scs[:, w, k:k + 1], aw[k][:, fg * 512:(fg + 1) * 512]) for k in range(8)])
                        s.tt(modrow[w][0:1, fg * 512:(fg + 1) * 512], bk[0:1, :], ab[0:1, fg * 512:(fg + 1) * 512], ALU.add)
                for w in range(2):
                    bk = banks[bi % 8]
                    bi += 1
                    for fc in range(24):
                        s.mm(bk[:, 2 * fc:2 * fc + 2], [(modrow[w][0:1, fc * 128:(fc + 1) * 128], cst[0:1, 3, 0:2])])
                    s.copy(modT[layer][w].full(), bk.view(0, [[512, 128], [2, 24]]))
                    for hh in range(2):
                        bk2 = banks[bi % 8]
                        bi += 1
                        s.mm(bk2.full(), [(cst[0:1, 3, 0:128], modrow[w][0:1, 2048 + hh * 512:2048 + (hh + 1) * 512])])
                        s.copy(gate_bc[layer][w][:, hh * 512:(hh + 1) * 512], bk2.full(), eng="act")
                    if MODD is not None:
                        s.dma(MODD[layer * 2 + w], modT[layer][w].full())
                s.flush()

        adaln_phase(0, e_ada_w, e_ada_b)

        with ExitStack() as l0:
            DT = cx.sb(l0, "DT", [128, NT, 32])
            DTA = cx.sb(l0, "DTA", [128, NT, 32])
            nw = cx.sb(l0, "nw", [128, 8])
            sc1 = [cx.sb(l0, "sc1_%d" % w, [128, 8]) for w in range(2)]
            s.dma(nw.full(), e_norm_wT.full())
            for w in range(2):
                s.stt(sc1[w].full(), modT[0][w][:, 8:16], 1.0, nw.full(), ALU.add, ALU.mult)

            hts = ExitStack()
            hT = [cx.sb(hts, "hT%d" % k, [128, T]) for k in range(8)]
            with ExitStack() as es:
                xt = [cx.sb(es, "xt%d" % i, [128, D]) for i in range(2)]
                xn = [cx.sb(es, "xn%d" % i, [128, D]) for i in range(2)]
                junk = cx.sb(es, "junk", [128, D])
                st = [cx.sb(es, "st%d" % i, [128, 4]) for i in range(2)]
                for i in range(NT):
                    w = 1 if i < 2 else 0
                    x_ = xt[i % 2]
                    n_ = xn[i % 2]
                    st_ = st[i % 2]
                    s.dma(x_.full(), xin[i * 128:(i + 1) * 128, :])
                    s.act(junk.full(), x_.full(), AF.Square, accum=st_[:, 0:1])
                    s.ts(st_[:, 1:2], st_[:, 0:1], 1.0 / D, EPS, ALU.mult, ALU.add)
                    s.act(st_[:, 2:3], st_[:, 1:2], AF.Sqrt)
                    s.recip(st_[:, 3:4], st_[:, 2:3])
                    s.ts(n_.full(), x_.full(), st_[:, 3:4], None, ALU.mult)
                    for half in range(2):
                        bk = banks[(2 * i + half) % 8]
                        for kk in range(4):
                            k = half * 4 + kk
                            s.transpose(bk[:, kk * 128:(kk + 1) * 128], n_[:, k * 128:(k + 1) * 128], ident)
                        for kk in range(4):
                            k = half * 4 + kk
                            s.act(hT[k][:, i * 128:(i + 1) * 128], bk[:, kk * 128:(kk + 1) * 128], AF.Identity,
                                  bias=modT[0][w][:, k:k + 1], scale=sc1[w][:, k:k + 1])
                if HT is not None:
                    for k in range(8):
                        s.dma(HT[k], hT[k].full())
                s.flush()

            with ExitStack() as es:
                WB = 256
                wbuf = [cx.sb(es, "wbuf%d" % i, [128, 8, WB]) for i in range(4)]
                wstate = {"i": 0}

                def load_w(col0, ncol=WB):
                    wb = wbuf[wstate["i"] % 4]
                    wstate["i"] += 1
                    s.dma(wb[:, :, 0:ncol], e_w_in.view(col0, [[E_NCOL_EXT, 128], [128 * E_NCOL_EXT, 8], [1, ncol]]))
                    return wb

                bstate = {"i": 0}

                def nbank():
                    bk = banks[bstate["i"] % 8]
                    bstate["i"] += 1
                    return bk

                def fm_mm(wb, cc, t0, n):
                    bk = nbank()
                    s.mm(bk[:, 0:n], [(wb[:, k, cc * 128:(cc + 1) * 128], hT[k][:, t0:t0 + n]) for k in range(8)])
                    return bk

                xraw = cx.sb(es, "xraw", [128, T])
                acc = cx.sb(es, "acc", [128, T])
                tmp1 = cx.sb(es, "tmp1", [128, 512])
                tmp2 = cx.sb(es, "tmp2", [128, 512])
                stg = [cx.sb(es, "stg%d" % i, [128, 4, 128]) for i in range(2)]
                rp = cx.sb(es, "rp", [128, 2, L])
                cw = cx.sb(es, "cw", [128, 12, 5])
                cb = cx.sb(es, "cb", [128, 12])
                dtb = cx.sb(es, "dtb", [128, 32])
                abc = cx.sb(es, "abc", [128, 32])
                s.dma(rp.full(), rope.full())
                s.dma(cw.full(), e_conv_wT.full())
                s.dma(cb.full(), e_conv_bT.full())
                s.dma(dtb.full(), e_dt_bias.view(0, [[0, 128], [1, 32]]))
                s.dma(abc.full(), e_a_log.view(0, [[0, 128], [1, 32]]))
                s.act(abc.full(), abc.full(), AF.Exp)
                s.ts(abc.full(), abc.full(), -1.0, None, ALU.mult)
                stg_i = {"i": 0}

                def transposes_to(dst, col0, src):
                    for i0 in range(0, NT, 4):
                        nb = min(4, NT - i0)
                        bk = nbank()
                        for ii in range(nb):
                            i = i0 + ii
                            s.transpose(bk[:, ii * 128:(ii + 1) * 128], src[:, i * 128:(i + 1) * 128], ident)
                        sg_ = stg[stg_i["i"] % 2]
                        stg_i["i"] += 1
                        s.copy(sg_[:, 0:nb, :], bk.view(0, [[512, 128], [128, nb], [1, 128]]), eng="act")
                        ncols = dst.h.shape[1]
                        s.dma(dst.view(i0 * 128 * ncols + col0, [[ncols, 128], [128 * ncols, nb], [1, 128]]),
                              sg_[:, 0:nb, :])

                for fc in range(12 if go('p2a') else 0):
                    if fc % 2 == 0:
                        wb = load_w(OFF_XBC + fc * 128)
                    cc = fc % 2
                    for (t0, n) in TG:
                        bk = fm_mm(wb, cc, t0, n)
                        s.copy(xraw[:, t0:t0 + n], bk[:, 0:n], eng="act")
                    s.ts(acc.full(), xraw.full(), cw[:, fc, 2:3], cb[:, fc:fc + 1], ALU.mult, ALU.add)
                    for kk in (0, 1, 3, 4):
                        d_ = kk - 2
                        for (s0, sl) in ((0, CTX), (CTX, L)):
                            lo = max(s0, s0 - d_)
                            hi = min(s0 + sl, s0 + sl - d_)
                            s.stt(acc[:, lo:hi], xraw[:, lo + d_:hi + d_], cw[:, fc, kk:kk + 1], acc[:, lo:hi],
                                  ALU.mult, ALU.add)
                    s.act(acc.full(), acc.full(), AF.Silu)
                    if fc < 8:
                        transposes_to(XS, fc * 128, acc)
                    elif fc < 10:
                        s.dma(BT[fc - 8], acc.full())
                        transposes_to(BTOK, (fc - 8) * 128, acc)
                    else:
                        s.dma(CT[fc - 10], acc.full())

                def rope_chunk(col_plain, col_swap, dst_rot, dst_ctx):
                    wa = load_w(col_plain, 128)
                    wsw = load_w(col_swap, 128)
                    for gi, (t0, n) in enumerate(TG):
                        bka = fm_mm(wa, 0, t0, n)
                        if gi == 0:
                            s.copy(acc[:, 0:CTX], bka[:, 0:CTX], eng="act")
                            continue
                        bkb = fm_mm(wsw, 0, t0, n)
                        l0 = t0 - CTX
                        s.tt(tmp1.full(), bka.full(), rp[:, 0, l0:l0 + 512], ALU.mult)
                        s.tt(tmp2.full(), bkb.full(), rp[:, 1, l0:l0 + 512], ALU.mult)
                        s.tt(acc[:, t0:t0 + n], tmp1.full(), tmp2.full(), ALU.add, eng="pool")
                    s.dma(dst_ctx, acc[:, 0:CTX])
                    s.dma(dst_rot, acc[:, CTX:T])

                for qc in range(8 if go('p2b') else 0):
                    rope_chunk(OFF_Q + qc * 128, OFF_QS + qc * 128, QR[qc], QC[qc])
                for j in range(4 if go('p2c') else 0):
                    rope_chunk(OFF_KR + j * 128, OFF_KSR + j * 128, KR[j], KC[j])

                for gc in range(8 if go('p2d') else 0):
                    if gc % 2 == 0:
                        wb = load_w(OFF_G + gc * 128)
                    for (t0, n) in TG:
                        bk = fm_mm(wb, gc % 2, t0, n)
                        s.act(acc[:, t0:t0 + n], bk[:, 0:n], AF.Silu)
                    s.dma(SG[gc], acc.full())

                NT_E = NT if go('p2e') else 0
                wz = [load_w(OFF_Z + i * 256) for i in range(4)]
                zt = [cx.sb(es, "zt%d" % i, [128, D]) for i in range(2)]
                for i in range(NT_E):
                    z_ = zt[i % 2]
                    for half in range(2):
                        bk = nbank()
                        for q4 in range(2):
                            wbz = wz[half * 2 + q4]
                            s.mm(bk[:, q4 * 256:(q4 + 1) * 256],
                                 [(hT[k][:, i * 128:(i + 1) * 128], wbz[:, k, :]) for k in range(8)])
                        s.act(z_[:, half * 512:(half + 1) * 512], bk.full(), AF.Silu)
                    s.dma(SZ[i * 128:(i + 1) * 128, :], z_.full())
                wv = load_w(OFF_KV + 256)
                wdt = load_w(OFF_DT, 32)
                vt = [cx.sb(es, "vt%d" % i, [128, 256]) for i in range(2)]
                for i in range(NT if go('p2f') else 0):
                    bk = nbank()
                    s.mm(bk[:, 0:256], [(hT[k][:, i * 128:(i + 1) * 128], wv[:, k, :]) for k in range(8)])
                    s.copy(vt[i % 2].full(), bk[:, 0:256], eng="act")
                    s.dma(VT[i * 128:(i + 1) * 128, :], vt[i % 2].full())
                for i in range(NT if go('p2g') else 0):
                    bk = nbank()
                    s.mm(bk[:, 0:32], [(hT[k][:, i * 128:(i + 1) * 128], wdt[:, k, 0:32]) for k in range(8)])
                    s.tt(DT[:, i, :], bk[:, 0:32], dtb.full(), ALU.add)
                    if go('p2h'):
                        s.act(DT[:, i, :], DT[:, i, :], AF.Exp)
                        s.ts(DT[:, i, :], DT[:, i, :], 1.0, None, ALU.add)
                        s.act(DT[:, i, :], DT[:, i, :], AF.Ln)
                    s.tt(DTA[:, i, :], DT[:, i, :], abc.full(), ALU.mult)
                    if DTD is not None:
                        s.dma(DTD[i * 128:(i + 1) * 128, :], DT[:, i, :])
                s.flush()
            hts.close()

            with ExitStack() as es:
                nb_ = {"i": 0}

                def nbank():
                    bk = banks[nb_["i"] % 8]
                    nb_["i"] += 1
                    return bk

                xs_t = [cx.sb(es, "xs_t%d" % i, [128, 1024]) for i in range(2)]
                b_t = [cx.sb(es, "b_t%d" % i, [128, 256]) for i in range(2)]
                bt_t = [cx.sb(es, "bt_t%d" % i, [128, 2, 128]) for i in range(2)]
                ct_t = [cx.sb(es, "ct_t%d" % i, [128, 2, 128]) for i in range(2)]
                yf_t = [cx.sb(es, "yf_t%d" % i, [128, 1024]) for i in range(2)]
                sz_t = [cx.sb(es, "sz_t%d" % i, [128, 1024]) for i in range(2)]
                dtatri = cx.sb(es, "dtatri", [128, 2048])
                decT = cx.sb(es, "decT", [128, 2048])
                MT = cx.sb(es, "MT", [128, 2048])
                xc = cx.sb(es, "xc", [128, 1024])
                xcd = cx.sb(es, "xcd", [128, 1024])
                cb_sb = cx.sb(es, "cb_sb", [128, 256])
                tmpo = cx.sb(es, "tmpo", [128, 1024])
                ytot = cx.sb(es, "ytot", [128, 1024])
                junk = cx.sb(es, "junk3", [128, 1024])
                ystg = cx.sb(es, "ystg", [128, 8, 128])
                Hs = [cx.sb(es, "Hs%d" % g, [128, 512]) for g in range(2)]
                sm = cx.sb(es, "sm", [128, 4, 16])
                st3 = cx.sb(es, "st3", [128, 4])
                dsk = cx.sb(es, "dsk", [128, 16])
                snw = cx.sb(es, "snw", [128, 8])
                s.dma(dsk.full(), e_d_skip.view(0, [[0, 128], [1, 16]]))
                s.dma(snw.full(), e_ssd_norm_wT.full())

                def bc3(buf, off, pstep, n1, s1, n2, s2):
                    return buf.view(off, [[pstep, 128], [s1, n1], [s2, n2]])

                n_ch = NT if go("p3") else 0
                for d_ in range(2):
                    order = list(range(NT)) if d_ == 0 else [1, 0] + list(range(NT - 1, 1, -1))
                    order = order[:n_ch]
                    TRIoff = 512 if d_ == 0 else 1024
                    TRIv = tri if d_ == 0 else utri
                    negm = cst[:, 4 + d_, :]
                    for g in range(2):
                        s.memset(Hs[g].full(), 0.0)
                    for ci, i in enumerate(order):
                        pp = ci % 2
                        xs_, b_, bt_, ct_ = xs_t[pp], b_t[pp], bt_t[pp], ct_t[pp]
                        s.dma(xs_.full(), XS[i * 128:(i + 1) * 128, :])
                        s.dma(b_.full(), BTOK[i * 128:(i + 1) * 128, :])
                        s.dma(bt_.full(), BT.view(i * 128, [[T, 128], [128 * T, 2], [1, 128]]))
                        s.dma(ct_.full(), CT.view(i * 128, [[T, 128], [128 * T, 2], [1, 128]]))
                        dta_i = DTA[:, i, d_ * 16:(d_ + 1) * 16]
                        doff = i * 32 + d_ * 16
                        s.tt(bc3(dtatri, 0, 2048, 16, 128, 128, 1), bc3(DTA, doff, NT * 32, 16, 1, 128, 0),
                             bc3(cst, TRIoff, 3072, 16, 0, 128, 1), ALU.mult)
                        bs = nbank()
                        s.mm(bs[:, 0:16], [(TRIv, dta_i)])
                        s.mm(bs[:, 16:32], [(ones, dta_i)])
                        na, ea, de, cd = sm[:, 0, :], sm[:, 1, :], sm[:, 2, :], sm[:, 3, :]
                        s.ts(na, bs[:, 0:16], -1.0, None, ALU.mult)
                        s.act(ea, bs[:, 0:16], AF.Exp)
                        s.tt(de, bs[:, 16:32], na, ALU.add)
                        s.act(de, de, AF.Exp)
                        s.act(cd, bs[:, 16:32], AF.Exp)
                        for hq in range(4):
                            bq = nbank()
                            s.mm(bq.full(), [(ones, dtatri[:, hq * 512:(hq + 1) * 512]), (ident, negm)])
                            for hh in range(4):
                                h = hq * 4 + hh
                                s.act(decT[:, h * 128:(h + 1) * 128], bq[:, hh * 128:(hh + 1) * 128], AF.Exp,
                                      bias=sm[:, 0, h:h + 1])
                        bc = nbank()
                        for g in range(2):
                            s.mm(bc[:, g * 128:(g + 1) * 128], [(bt_[:, g, :], ct_[:, g, :])])
                        s.copy(cb_sb.full(), bc[:, 0:256], eng="act")
                        for g in range(2):
                            s.tt(bc3(MT, g * 1024, 2048, 8, 128, 128, 1), bc3(decT, g * 1024, 2048, 8, 128, 128, 1),
                                 bc3(cb_sb, g * 128, 256, 8, 0, 128, 1), ALU.mult)
                        s.tt(bc3(xc, 0, 1024, 16, 64, 64, 1), bc3(xs_, 0, 1024, 16, 64, 64, 1),
                             bc3(DT, doff, NT * 32, 16, 1, 64, 0), ALU.mult)
                        s.tt(bc3(xcd, 0, 1024, 16, 64, 64, 1), bc3(xc, 0, 1024, 16, 64, 64, 1),
                             bc3(sm, 32, 64, 16, 1, 64, 0), ALU.mult)
                        ydst = yf_t[pp] if d_ == 0 else ytot
                        for g in range(2):
                            by = nbank()
                            for hh in range(8):
                                h = g * 8 + hh
                                s.mm(by[:, hh * 64:(hh + 1) * 64], [(MT[:, h * 128:(h + 1) * 128], xc[:, h * 64:(h + 1) * 64])])
                            bo = nbank()
                            s.mm(bo.full(), [(ct_[:, g, :], Hs[g].full())])
                            s.tt(bc3(tmpo, g * 512, 1024, 8, 64, 64, 1), bc3(bo, 0, 512, 8, 64, 64, 1),
                                 bc3(sm, 16 + g * 8, 64, 8, 1, 64, 0), ALU.mult)
                            s.tt(ydst[:, g * 512:(g + 1) * 512], by.full(), tmpo[:, g * 512:(g + 1) * 512], ALU.add)
                        for g in range(2):
                            bst = nbank()
                            s.mm(bst.full(), [(b_[:, g * 128:(g + 1) * 128], xcd[:, g * 512:(g + 1) * 512])])
                            s.tt(bc3(Hs[g], 0, 512, 8, 64, 64, 1), bc3(Hs[g], 0, 512, 8, 64, 64, 1),
                                 bc3(sm, 48 + g * 8, 64, 8, 1, 64, 0), ALU.mult)
                            s.tt(Hs[g].full(), Hs[g].full(), bst.full(), ALU.add)
                        if d_ == 0:
                            s.dma(YF[i * 128:(i + 1) * 128, :], yf_t[pp].full())
                        else:
                            yf_, sz_ = yf_t[pp], sz_t[pp]
                            s.dma(yf_.full(), YF[i * 128:(i + 1) * 128, :])
                            s.dma(sz_.full(), SZ[i * 128:(i + 1) * 128, :])
                            s.tt(ytot.full(), ytot.full(), yf_.full(), ALU.add)
                            s.tt(bc3(tmpo, 0, 1024, 16, 64, 64, 1), bc3(xs_, 0, 1024, 16, 64, 64, 1),
                                 bc3(dsk, 0, 16, 16, 1, 64, 0), ALU.mult)
                            s.tt(ytot.full(), ytot.full(), tmpo.full(), ALU.add)
                            s.tt(ytot.full(), ytot.full(), sz_.full(), ALU.mult)
                            s.act(junk.full(), ytot.full(), AF.Square, accum=st3[:, 0:1])
                            s.ts(st3[:, 1:2], st3[:, 0:1], 1.0 / 1024, EPS, ALU.mult, ALU.add)
                            s.act(st3[:, 2:3], st3[:, 1:2], AF.Sqrt)
                            s.recip(st3[:, 3:4], st3[:, 2:3])
                            s.ts(ytot.full(), ytot.full(), st3[:, 3:4], None, ALU.mult)
                            for half in range(2):
                                bk = nbank()
                                for kk in range(4):
                                    k = half * 4 + kk
                                    s.transpose(bk[:, kk * 128:(kk + 1) * 128], ytot[:, k * 128:(k + 1) * 128], ident)
                                for kk in range(4):
                                    k = half * 4 + kk
                                    s.act(ystg[:, k, :], bk[:, kk * 128:(kk + 1) * 128], AF.Copy, scale=snw[:, k:k + 1])
                            s.dma(YT.view(i * 128, [[T, 128], [128 * T, 8], [1, 128]]), ystg.full())
                s.flush()

            with ExitStack() as es:
                nb_ = {"i": 0}

                def nbank():
                    bk = banks[nb_["i"] % 8]
                    nb_["i"] += 1
                    return bk

                qr_t = cx.sb(es, "qr_t", [128, 2, L])
                qc_t = cx.sb(es, "qc_t", [128, 2, CTX])
                kr_t = cx.sb(es, "kr_t", [128, L])
                kc_t = cx.sb(es, "kc_t", [128, CTX])
                v_t = cx.sb(es, "v_t", [128, NT, 64])
                v2 = cx.sb(es, "v2", [128, NT, 128])
                sg_t = cx.sb(es, "sg_t", [128, 2, T])
                ast = cx.sb(es, "ast", [128, 2, T])
                pt = [[cx.sb(es, "pt%d_%d" % (a, b), [128, 512]) for b in range(5)] for a in range(2)]
                rd = cx.sb(es, "rd", [128, 256])
                ao = cx.sb(es, "ao", [128, 256])
                es_pp = cx.sb(es, "es_pp", [128, 8])
                c8 = cx.sb(es, "c8", [128, 1])
                s.memset(c8.full(), 0.125)
                s.dma(es_pp.full(), e_sink.full())
                s.act(es_pp.full(), es_pp.full(), AF.Exp)
                qb_i = 0
                ATT_DBG = [int(v) for v in os.environ.get("ATT_DBG", "4,18,4").split(",")]
                for j in range(ATT_DBG[0] if go("p4") else 0):
                    s.dma(qr_t.full(), QR.view(2 * j * 128 * L, [[L, 128], [128 * L, 2], [1, L]]))
                    s.dma(qc_t.full(), QC.view(2 * j * 128 * CTX, [[CTX, 128], [128 * CTX, 2], [1, CTX]]))
                    s.dma(kr_t.full(), KR[j])
                    s.dma(kc_t.full(), KC[j])
                    s.dma(v_t.full(), VT.view(j * 64, [[256, 128], [128 * 256, NT], [1, 64]]))
                    s.dma(sg_t.full(), SG.view(2 * j * 128 * T, [[T, 128], [128 * T, 2], [1, T]]))
                    s.copy(v2[:, :, 0:64], v_t.full(), eng="act")
                    s.copy(v2[:, :, 64:128], v_t.full(), eng="pool")
                    for kind, bi in ([("c", 0), ("c", 1)] + [("l", b) for b in range(16)])[:ATT_DBG[1]]:
                        if kind == "c":
                            qsrc, q0, tok0 = qc_t, bi * 128, bi * 128
                            keys = [("c", 0, None), ("c", 1, None)]
                        else:
                            qsrc, q0, tok0 = qr_t, bi * 128, CTX + bi * 128
                            keys = [("c", 0, None), ("c", 1, None)]
                            if bi > 0:
                                keys.append(("l", bi - 1, "prev"))
                            keys.append(("l", bi, None))
                            if bi < 15:
                                keys.append(("l", bi + 1, "next"))
                        pts = pt[qb_i % 2]
                        qb_i += 1
                        qw = qsrc.h.shape[2]
                        for ki, (kk, kb, msk) in enumerate(keys):
                            ksrc = kc_t if kk == "c" else kr_t
                            for par in range(2):
                                p0 = par * 64
                                bs = nbank()
                                s.mm(bs[:, 0:256],
                                     [(ksrc[p0:p0 + 64, kb * 128:(kb + 1) * 128],
                                       qsrc.view(p0 * 2 * qw + q0, [[2 * qw, 64], [qw, 2], [1, 128]]))])
                                s.act(pts[ki][:, par * 256:(par + 1) * 256], bs[:, 0:256], AF.Exp, scale=c8[:, 0:1])
                            if msk is not None and ATT_DBG[2] >= 2:
                                moff = 1024 if msk == "prev" else 512
                                s.tt(pts[ki].view(0, [[512, 128], [128, 4], [1, 128]]),
                                     pts[ki].view(0, [[512, 128], [128, 4], [1, 128]]),
                                     cst.view(moff, [[3072, 128], [0, 4], [1, 128]]), ALU.mult)
                        if ATT_DBG[2] < 3:
                            continue
                        vt_idx = [(kb if kk == "c" else 2 + kb) for (kk, kb, _) in keys]
                        bn = nbank()
                        s.mm(bn.full(), [(v2[:, vt_idx[ki], :], pts[ki].full()) for ki in range(len(keys))])
                        bd = nbank()
                        s.mm(bd.full(), [(ones, pts[ki].full()) for ki in range(len(keys))])
                        if ATT_DBG[2] < 4:
                            continue
                        for par in range(2):
                            p0 = par * 64
                            for c in range(2):
                                s.ts(rd[p0:p0 + 64, c * 128:(c + 1) * 128],
                                     bd[p0:p0 + 64, par * 256 + c * 128:par * 256 + (c + 1) * 128],
                                     es_pp[p0:p0 + 64, 2 * j + c:2 * j + c + 1], None, ALU.add)
                        s.recip(rd.full(), rd.full())
                        for par in range(2):
                            p0 = par * 64
                            s.tt(ao[p0:p0 + 64, :], bn[p0:p0 + 64, par * 256:(par + 1) * 256], rd[p0:p0 + 64, :], ALU.mult)
                        s.tt(ast.view(tok0, [[2 * T, 128], [T, 2], [1, 128]]),
                             ao.view(0, [[256, 128], [128, 2], [1, 128]]),
                             sg_t.view(tok0, [[2 * T, 128], [T, 2], [1, 128]]), ALU.mult)
                    s.dma(YT.view((8 + 2 * j) * 128 * T, [[T, 128], [128 * T, 2], [1, T]]), ast.full())
                s.flush()

            with ExitStack() as es:
                nb_ = {"i": 0}

                def nbank():
                    bk = banks[nb_["i"] % 8]
                    nb_["i"] += 1
                    return bk

                wo = [cx.sb(es, "wo%d" % k, [128, D]) for k in range(16)]
                yt = [cx.sb(es, "yt%d" % i, [128, 16, 128]) for i in range(2)]
                xt = [cx.sb(es, "xt5_%d" % i, [128, D]) for i in range(2)]
                x1t = [cx.sb(es, "x1t%d" % i, [128, D]) for i in range(2)]
                tmp5 = cx.sb(es, "tmp5", [128, 512])
                for k in range(16):
                    s.dma(wo[k].full(), e_w_out[k * 128:(k + 1) * 128, :])
                for i in range(NT if go("p5") else 0):
                    w = 1 if i < 2 else 0
                    y_, x_, o_ = yt[i % 2], xt[i % 2], x1t[i % 2]
                    s.dma(y_.full(), YT.view(i * 128, [[T, 128], [128 * T, 16], [1, 128]]))
                    s.dma(x_.full(), xin[i * 128:(i + 1) * 128, :])
                    for half in range(2):
                        bk = nbank()
                        s.mm(bk.full(), [(y_[:, fc, :], wo[fc][:, half * 512:(half + 1) * 512]) for fc in range(16)])
                        s.tt(tmp5.full(), bk.full(), gate_bc[0][w][:, half * 512:(half + 1) * 512], ALU.mult)
                        s.tt(o_[:, half * 512:(half + 1) * 512], tmp5.full(), x_[:, half * 512:(half + 1) * 512], ALU.add)
                    s.dma(X1[i * 128:(i + 1) * 128, :], o_.full())
                s.flush()

        if go("all"):
            adaln_phase(1, o_ada_w, o_ada_b)
        with ExitStack() as l1:
            if not go("all"):
                return nc
            nb_ = {"i": 0}

            def nbank():
                bk = banks[nb_["i"] % 8]
                nb_["i"] += 1
                return bk

            with ExitStack() as es:
                nw = cx.sb(es, "nw1", [128, 8])
                sc1 = [cx.sb(es, "sc1b_%d" % w, [128, 8]) for w in range(2)]
                s.dma(nw.full(), o_norm_wT.full())
                for w in range(2):
                    s.stt(sc1[w].full(), modT[1][w][:, 8:16], 1.0, nw.full(), ALU.add, ALU.mult)
                hT = [cx.sb(es, "hTb%d" % k, [128, T]) for k in range(8)]
                xt = [cx.sb(es, "xtb%d" % i, [128, D]) for i in range(2)]
                xn = [cx.sb(es, "xnb%d" % i, [128, D]) for i in range(2)]
                junk = cx.sb(es, "junkb", [128, D])
                st = [cx.sb(es, "stb%d" % i, [128, 4]) for i in range(2)]
                for i in range(NT):
                    w = 1 if i < 2 else 0
                    x_, n_, st_ = xt[i % 2], xn[i % 2], st[i % 2]
                    s.dma(x_.full(), X1[i * 128:(i + 1) * 128, :])
                    s.act(junk.full(), x_.full(), AF.Square, accum=st_[:, 0:1])
                    s.ts(st_[:, 1:2], st_[:, 0:1], 1.0 / D, EPS, ALU.mult, ALU.add)
                    s.act(st_[:, 2:3], st_[:, 1:2], AF.Sqrt)
                    s.recip(st_[:, 3:4], st_[:, 2:3])
                    s.ts(n_.full(), x_.full(), st_[:, 3:4], None, ALU.mult)
                    for half in range(2):
                        bk = nbank()
                        for kk in range(4):
                            k = half * 4 + kk
                            s.transpose(bk[:, kk * 128:(kk + 1) * 128], n_[:, k * 128:(k + 1) * 128], ident)
                        for kk in range(4):
                            k = half * 4 + kk
                            s.act(hT[k][:, i * 128:(i + 1) * 128], bk[:, kk * 128:(kk + 1) * 128], AF.Identity,
                                  bias=modT[1][w][:, k:k + 1], scale=sc1[w][:, k:k + 1])
                wq = [cx.sb(es, "wq%d" % i, [128, 8, 256]) for i in range(4)]
                ot = [cx.sb(es, "ot%d" % i, [128, D]) for i in range(2)]
                oi = 0
                for which in range(2):
                    for q4 in range(4):
                        s.dma(wq[q4].full(), o_w_in.view(which * 1024 + q4 * 256, [[2 * D, 128], [128 * 2 * D, 8], [1, 256]]))
                    for i in range(NT):
                        if which == 1 and i < 2:
                            continue
                        o_ = ot[oi % 2]
                        oi += 1
                        for half in range(2):
                            bk = nbank()
                            for q4 in range(2):
                                s.mm(bk[:, q4 * 256:(q4 + 1) * 256],
                                     [(hT[k][:, i * 128:(i + 1) * 128], wq[half * 2 + q4][:, k, :]) for k in range(8)])
                            if which == 0:
                                s.copy(o_[:, half * 512:(half + 1) * 512], bk.full(), eng="act")
                            else:
                                s.act(o_[:, half * 512:(half + 1) * 512], bk.full(), AF.Silu)
                        s.dma((U if which == 0 else SG1)[i * 128:(i + 1) * 128, :], o_.full())
                s.flush()

            L1S = os.environ.get('L1S', 'z')
            if L1S == 'a':
                return nc
            with ExitStack() as es:
                lam = cx.sb(es, "lam", [128, 2, 3, 32])
                bprm = cx.sb(es, "bprm", [128, 2, 32, 16])
                cprm = cx.sb(es, "cprm", [128, 2, 32, 16])
                s.dma(lam.full(), s5_lam.full())
                s.dma(bprm.full(), s5_b.full())
                s.dma(cprm.full(), s5_c.full())
                kc = cx.sb(es, "kconst", [128, 4])
                s.memset(kc[:, 0:1], 1.0 / 16)
                s.memset(kc[:, 1:2], math.pi / 2)
                s.memset(kc[:, 2:3], 0.0)
                s.memset(kc[:, 3:4], 1.0)
                W64 = [128, 2, 32]

                def t64(name):
                    return cx.sb(es, name, W64)

                def lv(i):
                    return lam.view(i * 32, [[192, 128], [96, 2], [1, 32]])

                dt_ = t64("dt_"); mag = t64("mag"); th = t64("th"); cs = t64("cs"); sn = t64("sn")
                t_a = t64("t_a"); t_b = t64("t_b"); t_c = t64("t_c")
                abre = t64("abre"); abim = t64("abim"); cre = t64("cre"); cim = t64("cim")
                s.act(dt_.full(), lv(2), AF.Exp)
                s.tt(t_a.full(), lv(0), dt_.full(), ALU.mult)
                s.act(mag.full(), t_a.full(), AF.Exp)
                s.tt(th.full(), lv(1), dt_.full(), ALU.mult)
                s.act(sn.full(), th.full(), AF.Sin, scale=kc[:, 0:1])
                s.act(cs.full(), th.full(), AF.Sin, scale=kc[:, 0:1], bias=kc[:, 1:2])
                for _ in range(4):
                    s.tt(t_a.full(), cs.full(), cs.full(), ALU.mult)
                    s.tt(t_b.full(), sn.full(), sn.full(), ALU.mult)
                    s.tt(t_c.full(), sn.full(), cs.full(), ALU.mult)
                    s.tt(cs.full(), t_a.full(), t_b.full(), ALU.subtract)
                    s.ts(sn.full(), t_c.full(), 2.0, None, ALU.mult)
                s.tt(abre.full(), mag.full(), cs.full(), ALU.mult)
                s.tt(abim.full(), mag.full(), sn.full(), ALU.mult)
                PW = cx.sb(es, "PW", [128, 2, 9, 64])

                def pw(ri, k):
                    return PW.view((ri * 9 + k) * 64, [[2 * 9 * 64, 128], [32, 2], [1, 32]])

                s.memset(PW[:, 0, 0, :], 1.0)
                s.memset(PW[:, 1, 0, :], 0.0)
                for k in range(8):
                    s.tt(t_a.full(), pw(0, k), abre.full(), ALU.mult)
                    s.tt(t_b.full(), pw(1, k), abim.full(), ALU.mult)
                    s.tt(pw(0, k + 1), t_a.full(), t_b.full(), ALU.subtract)
                    s.tt(t_a.full(), pw(0, k), abim.full(), ALU.mult)
                    s.tt(t_b.full(), pw(1, k), abre.full(), ALU.mult)
                    s.tt(pw(1, k + 1), t_a.full(), t_b.full(), ALU.add)
                s.ts(t_c.full(), abre.full(), -1.0, None, ALU.add)
                s.tt(t_a.full(), lv(0), lv(0), ALU.mult)
                s.tt(t_b.full(), lv(1), lv(1), ALU.mult)
                s.tt(t_a.full(), t_a.full(), t_b.full(), ALU.add)
                s.recip(dt_.full(), t_a.full())
                s.tt(t_a.full(), t_c.full(), lv(0), ALU.mult)
                s.tt(t_b.full(), abim.full(), lv(1), ALU.mult)
                s.tt(t_a.full(), t_a.full(), t_b.full(), ALU.add)
                s.tt(cre.full(), t_a.full(), dt_.full(), ALU.mult)
                s.tt(t_a.full(), abim.full(), lv(0), ALU.mult)
                s.tt(t_b.full(), t_c.full(), lv(1), ALU.mult)
                s.tt(t_a.full(), t_a.full(), t_b.full(), ALU.subtract)
                s.tt(cim.full(), t_a.full(), dt_.full(), ALU.mult)
                BB = cx.sb(es, "BB", [128, 2, 2, 512])
                tb1 = cx.sb(es, "tb1", [128, 512])
                tb2 = cx.sb(es, "tb2", [128, 512])

                def bb(ri, d_, g0=0, ng=32):
                    return BB.view((ri * 2 + d_) * 512 + g0 * 16, [[2048, 128], [16, ng], [1, 16]])

                def v3(buf, off, pstep, n1, s1, n2, s2):
                    return buf.view(off, [[pstep, 128], [s1, n1], [s2, n2]])

                def prm(buf, ri, g0=0, ng=32):
                    return buf.view(ri * 512 + g0 * 16, [[1024, 128], [16, ng], [1, 16]])

                def cf(buf, d_, g0=0, ng=32, n2=16):
                    return buf.view(d_ * 32 + g0, [[64, 128], [1, ng], [0, n2]])

                t1v = v3(tb1, 0, 512, 32, 16, 16, 1)
                t2v = v3(tb2, 0, 512, 32, 16, 16, 1)
                for d_ in range(2):
                    s.tt(t1v, prm(bprm, 0), cf(cre, d_), ALU.mult)
                    s.tt(t2v, prm(bprm, 1), cf(cim, d_), ALU.mult)
                    s.tt(bb(0, d_), t1v, t2v, ALU.subtract)
                    s.tt(t1v, prm(bprm, 1), cf(cre, d_), ALU.mult)
                    s.tt(t2v, prm(bprm, 0), cf(cim, d_), ALU.mult)
                    s.tt(bb(1, d_), t1v, t2v, ALU.add)
                LA = cx.sb(es, "LA", [128, 2, 32, 2])
                LB = cx.sb(es, "LB", [128, 2, 32, 2])
                for ri in range(2):
                    s.copy(LA.view(ri, [[128, 128], [64, 2], [2, 32]]), pw(0, 8))
                s.ts(LB.view(0, [[128, 128], [64, 2], [2, 32]]), pw(1, 8), -1.0, None, ALU.mult)
                s.copy(LB.view(1, [[128, 128], [64, 2], [2, 32]]), pw(1, 8))
                zt_ = cx.sb(es, "zt_", [16, 16, 112])
                s.memset(zt_.full(), 0.0)
                s.flush()

                if L1S == 'b':
                    return nc
                for b in range(4 if L1S not in ('c1', 'd1', 'e1', 'f1', 'g1') else 1):
                    g0 = 8 * b
                    with ExitStack() as bs_:
                        CAB = cx.sb(bs_, "CAB", [128, 2, 2, 8 * 144])
                        WST = cx.sb(bs_, "WST", [128, 8, 2, 2, 2, 64])
                        TF = cx.sb(bs_, "TF", [128, 16, 128])
                        TB = cx.sb(bs_, "TB", [128, 16, 128])

                        with ExitStack() as tmp:
                            WT = cx.sb(tmp, "WT", [128, 2, 2, 8 * 128])
                            KSB = cx.sb(tmp, "KSB", [16, 2, 16, 128])
                            c1 = cx.sb(tmp, "c1", [128, 128])
                            c2 = cx.sb(tmp, "c2", [128, 128])
                            c1v = v3(c1, 0, 128, 8, 16, 16, 1)
                            c2v = v3(c2, 0, 128, 8, 16, 16, 1)
                            for d_ in range(2):
                                for idx in range(9):
                                    p_ = idx if d_ == 0 else 8 - idx
                                    pr = PW.view((0 * 9 + p_) * 64 + d_ * 32 + g0, [[1152, 128], [1, 8], [0, 16]])
                                    pi_ = PW.view((1 * 9 + p_) * 64 + d_ * 32 + g0, [[1152, 128], [1, 8], [0, 16]])
                                    o_re = CAB.view((0 * 2 + d_) * 1152 + idx * 16, [[4608, 128], [144, 8], [1, 16]])
                                    o_im = CAB.view((1 * 2 + d_) * 1152 + idx * 16, [[4608, 128], [144, 8], [1, 16]])
                                    s.tt(c1v, prm(cprm, 0, g0, 8), pr, ALU.mult)
                                    s.tt(c2v, prm(cprm, 1, g0, 8), pi_, ALU.mult)
                                    s.tt(o_re, c1v, c2v, ALU.subtract)
                                    s.tt(c1v, prm(cprm, 0, g0, 8), pi_, ALU.mult)
                                    s.tt(c2v, prm(cprm, 1, g0, 8), pr, ALU.mult)
                                    s.stt(o_im, c1v, -1.0, c2v, ALU.mult, ALU.subtract)
                                for ss in range(8):
                                    p_ = 7 - ss if d_ == 0 else ss
                                    pr = PW.view((0 * 9 + p_) * 64 + d_ * 32 + g0, [[1152, 128], [1, 8], [0, 16]])
                                    pi_ = PW.view((1 * 9 + p_) * 64 + d_ * 32 + g0, [[1152, 128], [1, 8], [0, 16]])
                                    o_re = WT.view((d_ * 2 + 0) * 1024 + ss * 16, [[4096, 128], [128, 8], [1, 16]])
                                    o_im = WT.view((d_ * 2 + 1) * 1024 + ss * 16, [[4096, 128], [128, 8], [1, 16]])
                                    s.tt(c1v, bb(0, d_, g0, 8), pr, ALU.mult)
                                    s.tt(c2v, bb(1, d_, g0, 8), pi_, ALU.mult)
                                    s.tt(o_re, c1v, c2v, ALU.subtract)
                                    s.tt(c1v, bb(1, d_, g0, 8), pr, ALU.mult)
                                    s.tt(c2v, bb(0, d_, g0, 8), pi_, ALU.mult)
                                    s.tt(o_im, c1v, c2v, ALU.add)
                            for gh in range(2):
                                p0 = gh * 64
                                for gq in range(8):
                                    bk = nbank()
                                    for d_ in range(2):
                                        for ri in range(2):
                                            sl = d_ * 2 + ri
                                            s.transpose(bk[:, sl * 64:(sl + 1) * 64],
                                                        WT.view(p0 * 4096 + (d_ * 2 + ri) * 1024 + gq * 128, [[4096, 64], [1, 128]]),
                                                        cst[p0:p0 + 64, 0, p0:p0 + 64])
                                    s.copy(WST.view(((gq * 2 + gh) * 4) * 64, [[4096, 128], [1, 256]]), bk[:, 0:256], eng="act")
                                for d_ in range(2):
                                    for gqq in range(2):
                                        bk = nbank()
                                        for q4 in range(4):
                                            gq = gqq * 4 + q4
                                            i0 = 0 if d_ == 0 else 1
                                            s.mm(bk[0:16, q4 * 128:(q4 + 1) * 128],
                                                 [(BB.view(p0 * 2048 + (0 * 2 + d_) * 512 + (g0 + gq) * 16, [[2048, 64], [1, 16]]),
                                                   CAB.view(p0 * 4608 + (0 * 2 + d_) * 1152 + gq * 144 + i0 * 16, [[4608, 64], [1, 128]])),
                                                  (BB.view(p0 * 2048 + (1 * 2 + d_) * 512 + (g0 + gq) * 16, [[2048, 64], [1, 16]]),
                                                   CAB.view(p0 * 4608 + (1 * 2 + d_) * 1152 + gq * 144 + i0 * 16, [[4608, 64], [1, 128]]))])
                                        s.copy(KSB.view(d_ * 2048 + (2 * gqq * 4 + gh) * 128, [[4096, 16], [256, 4], [1, 128]]),
                                               bk.view(0, [[512, 16], [128, 4], [1, 128]]), eng="act")
                            gbase = 16 * b
                            s.dma(KFP.view(gbase * 3840 + 7 * 16, [[240, 16], [3840, 16], [1, 128]]), KSB[:, 0, :, :])
                            s.dma(KBR.view(gbase * 3840, [[240, 16], [3840, 16], [1, 128]]), KSB[:, 1, :, :])
                            s.dma(KFP.view(gbase * 3840, [[240, 16], [3840, 16], [1, 112]]), zt_.full())
                            s.dma(KBR.view(gbase * 3840 + 128, [[240, 16], [3840, 16], [1, 112]]), zt_.full())
                            for ss in range(8):
                                s.dma(TF[ss * 16:(ss + 1) * 16, :, :], KFP.view(gbase * 3840 + (7 - ss) * 16, [[240, 16], [3840, 16], [1, 128]]))
                                s.dma(TB[ss * 16:(ss + 1) * 16, :, :], KBR.view(gbase * 3840 + (7 - ss) * 16, [[240, 16], [3840, 16], [1, 128]]))
                            s.flush()

                        if L1S in ('c', 'c1'):
                            continue
                        u8b = cx.sb(bs_, "u8b", [128, 8, 256])
                        u8g = cx.sb(bs_, "u8g", [128, 16, 128])
                        U8T = cx.sb(bs_, "U8T", [128, 16, 288])
                        SS = cx.sb(bs_, "SS", [128, 2, 8, 2, 289])
                        y8b = cx.sb(bs_, "y8b", [128, 8, 256])
                        ysb = cx.sb(bs_, "ysb", [128, 512])
                        T1 = cx.sb(bs_, "T1", [128, 2, 8, 2])
                        T2 = cx.sb(bs_, "T2", [128, 2, 8, 2])
                        for (j0, nj) in ((0, 32), (32, 128), (160, 128)):
                            s.dma(u8b[0:nj, :, :], U.view(8 * j0 * 1024 + 256 * b, [[8192, nj], [1024, 8], [1, 256]]))
                            s.copy(u8g.view(0, [[2048, nj], [128, 16], [16, 8], [1, 16]]),
                                   u8b.view(0, [[2048, nj], [16, 16], [256, 8], [1, 16]]), eng="act")
                            for gq4 in range(4):
                                bk = nbank()
                                for q4 in range(4):
                                    gi = gq4 * 4 + q4
                                    s.transpose(bk[:, q4 * 128:q4 * 128 + nj],
                                                u8g.view(128 * gi, [[2048, nj], [1, 128]]), cst[0:nj, 0, 0:nj])
                                s.copy(U8T.view(gq4 * 4 * 288 + j0, [[16 * 288, 128], [288, 4], [1, nj]]),
                                       bk.view(0, [[512, 128], [128, 4], [1, nj]]), eng="act")
                        if L1S in ('d', 'd1'):
                            s.flush()
                            continue
                        s.memset(SS.view(0, [[9248, 128], [289, 32], [1, 1]]), 0.0)
                        s.memset(SS.view(288, [[9248, 128], [289, 32], [1, 1]]), 0.0)
                        for gq in range(8):
                            for gh in range(2):
                                gi = 2 * gq + gh
                                p0 = gh * 64
                                for d_ in range(2):
                                    for ri in range(2):
                                        bk = nbank()
                                        s.mm(bk[p0:p0 + 64, 0:288],
                                             [(WST.view((((gq * 2 + gh) * 2 + d_) * 2 + ri) * 64, [[4096, 128], [1, 64]]),
                                               U8T[:, gi, :])])
                                        so = p0 * 9248 + ((d_ * 8 + gq) * 2 + ri) * 289
                                        if d_ == 0:
                                            s.copy(SS.view(so + 1, [[9248, 64], [1, 288]]), bk[p0:p0 + 64, 0:288], eng="act")
                                        else:
                                            s.copy(SS.view(so + 256, [[9248, 64], [1, 32]]), bk[p0:p0 + 64, 0:32], eng="act")
                                            s.copy(SS.view(so, [[9248, 64], [1, 256]]), bk[p0:p0 + 64, 32:288], eng="act")
                        if L1S in ('e', 'e1'):
                            s.flush()
                            continue
                        DS = 8 * 2 * 289
                        for k in range(1, 288):
                            prev = SS.view(k, [[9248, 128], [DS + 288 - 2 * k, 2], [578, 8], [289, 2]])
                            prsw = SS.view(k + 289, [[9248, 128], [DS + 288 - 2 * k, 2], [578, 8], [-289, 2]])
                            cur = SS.view(k + 1, [[9248, 128], [DS + 286 - 2 * k, 2], [578, 8], [289, 2]])
                            la = LA.view(g0 * 2, [[128, 128], [64, 2], [2, 8], [1, 2]])
                            lb = LB.view(g0 * 2, [[128, 128], [64, 2], [2, 8], [1, 2]])
                            s.tt(T1.full(), prev, la, ALU.mult)
                            s.tt(T2.full(), prsw, lb, ALU.mult)
                            s.tt(cur, cur, T1.full(), ALU.add)
                            s.tt(cur, cur, T2.full(), ALU.add)
                        if L1S in ('f', 'f1'):
                            s.flush()
                            continue
                        for tt_ in range(2):
                            j0 = 32 + 128 * tt_
                            m0 = 128 * tt_
                            for gh in range(2):
                                p0 = gh * 64
                                for gqq in range(2):
                                    bx = nbank()
                                    by = nbank()
                                    for q4 in range(4):
                                        gq = gqq * 4 + q4
                                        gi = 2 * gq + gh
                                        s.mm(bx[:, q4 * 128:(q4 + 1) * 128],
                                             [(U8T[:, gi, j0:j0 + 128], TF[:, gi, :]), (U8T[:, gi, j0:j0 + 128], TB[:, gi, :])])
                                        pairs = []
                                        for d_ in range(2):
                                            c0 = j0 if d_ == 0 else m0 + 1
                                            i0 = 1 if d_ == 0 else 0
                                            for ri in range(2):
                                                so = p0 * 9248 + ((d_ * 8 + gq) * 2 + ri) * 289 + c0
                                                pairs.append((SS.view(so, [[9248, 64], [1, 128]]),
                                                              CAB.view(p0 * 4608 + (ri * 2 + d_) * 1152 + gq * 144 + i0 * 16, [[4608, 64], [1, 128]])))
                                        s.mm(by[:, q4 * 128:(q4 + 1) * 128], pairs)
                                    s.copy(ysb.full(), by.full(), eng="act")
                                    s.tt(y8b.view(32 * gqq * 4 + 16 * gh, [[2048, 128], [32, 4], [256, 8], [1, 16]]),
                                         bx.view(0, [[512, 128], [128, 4], [16, 8], [1, 16]]),
                                         ysb.view(0, [[512, 128], [128, 4], [16, 8], [1, 16]]), ALU.add)
                            s.dma(YTOK.view((CTX + 8 * m0) * 1024 + 256 * b, [[8192, 128], [1024, 8], [1, 256]]), y8b.full())
                        s.flush()

            if L1S in ('g', 'g1'):
                return nc
            with ExitStack() as es:
                gw = [cx.sb(es, "gw%d" % k, [128, D]) for k in range(8)]
                ow = [cx.sb(es, "ow%d" % k, [128, D]) for k in range(8)]
                dskb = cx.sb(es, "dskb", [128, D])
                glbb = cx.sb(es, "glbb", [128, D])
                fnwb = cx.sb(es, "fnwb", [128, D])
                kg = cx.sb(es, "kg", [128, 1])
                s.memset(kg.full(), 2.0 * math.sqrt(2.0 / math.pi))
                for k in range(8):
                    s.dma(gw[k].full(), o_glu_w[k * 128:(k + 1) * 128, :])
                    s.dma(ow[k].full(), o_w_out[k * 128:(k + 1) * 128, :])
                s.dma(dskb.full(), o_d_skip.view(0, [[0, 128], [1, D]]))
                s.dma(glbb.full(), o_glu_b.view(0, [[0, 128], [1, D]]))
                s.dma(fnwb.full(), final_norm_w.view(0, [[0, 128], [1, D]]))
                ya = [cx.sb(es, "ya%d" % i, [128, D]) for i in range(2)]
                ua = [cx.sb(es, "ua%d" % i, [128, D]) for i in range(2)]
                sga = [cx.sb(es, "sga%d" % i, [128, D]) for i in range(2)]
                xa = [cx.sb(es, "xa%d" % i, [128, D]) for i in range(2)]
                w1 = cx.sb(es, "w1", [128, D])
                w2 = cx.sb(es, "w2", [128, D])
                w3 = cx.sb(es, "w3", [128, D])
                tT = cx.sb(es, "tT", [128, 8, 128])
                st = cx.sb(es, "st10", [128, 4])

                def transp8(src):
                    for half in range(2):
                        bk = nbank()
                        for kk in range(4):
                            k = half * 4 + kk
                            s.transpose(bk[:, kk * 128:(kk + 1) * 128], src[:, k * 128:(k + 1) * 128], ident)
                        s.copy(tT[:, half * 4:(half + 1) * 4, :], bk.view(0, [[512, 128], [128, 4], [1, 128]]), eng="act")

                TAILN = int(os.environ.get('TAILN', NT))
                for i in range(2, TAILN):
                    y_, u_, g_, x_ = ya[i % 2], ua[i % 2], sga[i % 2], xa[i % 2]
                    s.dma(y_.full(), YTOK[i * 128:(i + 1) * 128, :])
                    s.dma(u_.full(), U[i * 128:(i + 1) * 128, :])
                    s.dma(g_.full(), SG1[i * 128:(i + 1) * 128, :])
                    s.dma(x_.full(), X1[i * 128:(i + 1) * 128, :])
                    s.tt(w1.full(), u_.full(), dskb.full(), ALU.mult)
                    s.tt(y_.full(), y_.full(), w1.full(), ALU.add)
                    s.tt(w1.full(), y_.full(), y_.full(), ALU.mult)
                    s.ts(w1.full(), w1.full(), 0.044715, 1.0, ALU.mult, ALU.add)
                    s.tt(w1.full(), w1.full(), y_.full(), ALU.mult)
                    s.act(w1.full(), w1.full(), AF.Sigmoid, scale=kg[:, 0:1])
                    s.tt(w2.full(), y_.full(), w1.full(), ALU.mult)
                    transp8(w2)
                    for half in range(2):
                        bk = nbank()
                        s.mm(bk.full(), [(tT[:, k, :], gw[k][:, half * 512:(half + 1) * 512]) for k in range(8)])
                        s.tt(w1[:, half * 512:(half + 1) * 512], bk.full(), glbb[:, half * 512:(half + 1) * 512], ALU.add)
                    s.act(w1.full(), w1.full(), AF.Sigmoid)
                    s.tt(w2.full(), w2.full(), w1.full(), ALU.mult)
                    s.tt(w2.full(), w2.full(), g_.full(), ALU.mult)
                    transp8(w2)
                    for half in range(2):
                        bk = nbank()
                        s.mm(bk.full(), [(tT[:, k, :], ow[k][:, half * 512:(half + 1) * 512]) for k in range(8)])
                        s.tt(w1[:, half * 512:(half + 1) * 512], bk.full(), gate_bc[1][0][:, half * 512:(half + 1) * 512], ALU.mult)
                    s.tt(w3.full(), w1.full(), x_.full(), ALU.add)
                    s.act(w1.full(), w3.full(), AF.Square, accum=st[:, 0:1])
                    s.ts(st[:, 1:2], st[:, 0:1], 1.0 / D, EPS, ALU.mult, ALU.add)
                    s.act(st[:, 2:3], st[:, 1:2], AF.Sqrt)
                    s.recip(st[:, 3:4], st[:, 2:3])
                    s.ts(w3.full(), w3.full(), st[:, 3:4], None, ALU.mult)
                    s.tt(w2.full(), w3.full(), fnwb.full(), ALU.mult)
                    s.dma(out_t[(i - 2) * 128:(i - 1) * 128, :], w2.full())
                s.flush()

    return nc


def _consts():
    c = np.zeros((128, 6, 512), np.float32)
    j = np.arange(128)[:, None]
    l = np.arange(128)[None, :]
    c[:, 0, :128] = np.eye(128, dtype=np.float32)
    c[:, 1, :128] = (j <= l)
    c[:, 2, :128] = (j >= l)
    c[:, 3, :] = 1.0
    nf = np.where(l < j, -30000.0, 0.0).astype(np.float32)
    nb = np.where(l > j, -30000.0, 0.0).astype(np.float32)
    c[:, 4, :] = np.tile(nf, (1, 4))
    c[:, 5, :] = np.tile(nb, (1, 4))
    return c


def _rope_tables():
    rows = L // 64
    row = np.repeat(np.arange(rows, dtype=np.float32), 64)
    col = np.tile(np.arange(64, dtype=np.float32), rows)
    n_freq = 16
    inv = (np.float32(10000.0) ** (-np.arange(n_freq, dtype=np.float32) / n_freq)).astype(np.float32)
    ang = np.concatenate([row[:, None] * inv, col[:, None] * inv], axis=-1).astype(np.float32)
    cos = np.cos(ang).astype(np.float32)
    sin = np.sin(ang).astype(np.float32)
    cosT = np.zeros((128, L), np.float32)
    sinT = np.zeros((128, L), np.float32)
    for h2 in range(2):
        for half in range(2):
            p0 = h2 * 64 + half * 32
            cosT[p0:p0 + 32] = cos.T
            sinT[p0:p0 + 32] = (-sin.T if half == 0 else sin.T)
    return np.stack([cosT, sinT], axis=1)


def _vecT(v, nchunk):
    return np.ascontiguousarray(np.asarray(v, np.float32).reshape(nchunk, 128).T)


def prep_inputs(b, inp):
    f = lambda a: np.ascontiguousarray(np.asarray(a, np.float32))
    m = {}
    m["xin"] = f(np.concatenate([inp["ctx"][b], inp["x"][b]], axis=0))
    cv = np.stack([inp["c"][b], inp["c_ctx"]], axis=0)
    m["cvecT"] = f(cv.reshape(2, 8, 128).transpose(2, 0, 1))
    m["consts"] = _consts()
    m["rope"] = _rope_tables()
    m["e_ada_w"] = f(inp["e_ada_w"][0])
    m["e_ada_b"] = f(inp["e_ada_b"][0]).reshape(1, -1)
    m["e_norm_wT"] = _vecT(inp["e_norm_w"][0], 8)
    w = f(inp["e_w_in"][0])
    q = w[:, OFF_Q:OFF_Q + 1024].reshape(D, 16, 2, 32)
    qs = q[:, :, ::-1, :].reshape(D, 1024)
    k = w[:, OFF_KV:OFF_KV + 256].reshape(D, 4, 64)
    kr = np.concatenate([k, k], axis=2).reshape(D, 512)
    ks = k.reshape(D, 4, 2, 32)[:, :, ::-1, :].reshape(D, 4, 64)
    ksr = np.concatenate([ks, ks], axis=2).reshape(D, 512)
    m["e_w_in"] = f(np.concatenate([w, qs, kr, ksr], axis=1))
    cw = f(inp["e_conv_w"][0])
    m["e_conv_wT"] = f(cw.reshape(5, 12, 128).transpose(2, 1, 0))
    m["e_conv_bT"] = _vecT(inp["e_conv_b"][0], 12)
    m["e_dt_bias"] = f(inp["e_dt_bias"][0]).reshape(1, 32)
    m["e_a_log"] = f(inp["e_a_log"][0]).reshape(1, 32)
    m["e_d_skip"] = f(inp["e_d_skip"][0]).reshape(1, 16)
    m["e_ssd_norm_wT"] = _vecT(inp["e_ssd_norm_w"][0], 8)
    sk = f(inp["e_sink"][0]).reshape(8, 2)
    m["e_sink"] = f(np.repeat(sk.T[:, None, :], 64, axis=1).reshape(128, 8))
    m["e_w_out"] = f(inp["e_w_out"][0])
    m["o_ada_w"] = f(inp["o_ada_w"][0])
    m["o_ada_b"] = f(inp["o_ada_b"][0]).reshape(1, -1)
    m["o_norm_wT"] = _vecT(inp["o_norm_w"][0], 8)
    m["o_w_in"] = f(inp["o_w_in"][0])

    def gl(a):
        a = np.asarray(a, np.float32)
        rest = a.shape[2:]
        a = a.reshape((32, 2, 64) + rest)
        a = np.moveaxis(a, 0, 2)
        return a.reshape((128, 32) + rest)

    lam = np.zeros((128, 2, 3, 32), np.float32)
    for d_ in range(2):
        lam[:, d_, 0] = gl(inp["o_lam_re"][0][d_])
        lam[:, d_, 1] = gl(inp["o_lam_im"][0][d_])
        lam[:, d_, 2] = gl(np.repeat(np.asarray(inp["o_log_step"][0][d_])[:, None], 64, axis=1))
    m["s5_lam"] = f(lam)
    m["s5_b"] = f(np.stack([gl(inp["o_b_re"][0]), gl(inp["o_b_im"][0])], axis=1))
    cr = np.asarray(inp["o_c_re"][0]).transpose(0, 2, 1)
    ci = np.asarray(inp["o_c_im"][0]).transpose(0, 2, 1)
    m["s5_c"] = f(np.stack([gl(cr), gl(ci)], axis=1))
    m["o_d_skip"] = f(inp["o_d_skip"][0]).reshape(1, -1)
    m["o_glu_w"] = f(inp["o_glu_w"][0])
    m["o_glu_b"] = f(inp["o_glu_b"][0]).reshape(1, -1)
    m["o_w_out"] = f(inp["o_w_out"][0])
    m["final_norm_w"] = f(inp["final_norm_w"]).reshape(1, -1)
    return m


def kernel(**inputs):
    nc = build_program()
    in_maps = [prep_inputs(b, inputs) for b in range(8)]
    res = run_bass_kernel_spmd(nc, in_maps, core_ids=list(range(8)))
    return np.stack([r["out"] for r in res.results], axis=0)
```

```python
import math
import os
from contextlib import ExitStack

import numpy as np
import concourse.bass as bass
import concourse.mybir as mybir
from concourse.bass_utils import run_bass_kernel_spmd

F32 = mybir.dt.float32
AF = mybir.ActivationFunctionType
ALU = mybir.AluOpType

D = 1024
T = 2304
NT = 18
CTX = 256
L = 2048
EPS = 1e-6
TG = [(0, 256), (256, 512), (768, 512), (1280, 512), (1792, 512)]

SAME_ENGINE_SYNC = os.environ.get('SES', '1') == '1'
SEM_EPOCH = 30000


class V:
    __slots__ = ("buf", "ap")

    def __init__(self, buf, ap):
        self.buf = buf
        self.ap = ap


class Buf:
    def __init__(self, name, h):
        self.name = name
        self.h = h
        self.last_w = None
        self.readers = []

    def __getitem__(self, idx):
        return V(self, self.h[idx])

    def full(self):
        return V(self, self.h.ap())

    def view(self, offset, pattern):
        return V(self, bass.AP(self.h, offset, [list(p) for p in pattern]))


class Sched:
    ENG = ("pe", "act", "dve", "pool", "sp")

    def __init__(self, nc):
        self.nc = nc
        self.prog = {e: [] for e in self.ENG}
        self.sem = {}
        self.cnt = {}
        self.semid = 0
        self.known = {e: {} for e in self.ENG}
        for e in ("pe", "act", "dve", "pool"):
            self._new_engine_sem(e)
        self.nds = 8
        self.dsem = {}
        self.duse = {}
        self.dcnt = {}
        for q in ("sp", "pool"):
            self.dsem[q] = []
            self.duse[q] = []
            for i in range(self.nds):
                key = "d_%s_%d" % (q, i)
                self.dsem[q].append((nc.alloc_semaphore(key), key))
                self.duse[q].append(0)
            self.dcnt[q] = 0
        self.n_ops = 0

    def _new_engine_sem(self, e):
        self.semid += 1
        key = "s_%s_%d" % (e, self.semid)
        self.sem[e] = (self.nc.alloc_semaphore(key), key)
        self.cnt[e] = 0

    def _deps(self, reads, writes):
        deps = {}

        def add(tok):
            if tok is None:
                return
            h, key, val = tok
            if key not in deps or deps[key][1] < val:
                deps[key] = (h, val)

        for r in reads:
            add(r.buf.last_w)
        for w in writes:
            add(w.buf.last_w)
            for t in w.buf.readers:
                add(t)
        return deps

    def _emit_waits(self, eng, deps, own_key=None):
        kn = self.known[eng]
        for key, (h, val) in deps.items():
            if key == own_key and not SAME_ENGINE_SYNC:
                continue
            if kn.get(key, 0) >= val:
                continue
            kn[key] = val
            self.prog[eng].append(("wait", h, val))

    def _update(self, tok, reads, writes):
        for w in writes:
            w.buf.last_w = tok
            w.buf.readers = []
        for r in reads:
            if r.buf.last_w is not tok:
                r.buf.readers.append(tok)

    def op(self, eng, fn, reads=(), writes=()):
        reads = [r for r in reads if r is not None]
        writes = list(writes)
        if self.cnt[eng] >= SEM_EPOCH:
            self._new_engine_sem(eng)
        h, key = self.sem[eng]
        own = None if eng == "pe" else key
        deps = self._deps(reads, writes)
        if eng == "pe":
            deps.pop(key, None)
        self._emit_waits(eng, deps, own_key=own)
        self.cnt[eng] += 1
        self.prog[eng].append(("op", fn, h, 1))
        tok = (h, key, self.cnt[eng])
        self._update(tok, reads, writes)
        self.n_ops += 1
        return tok

    def dma(self, out, in_, q="sp", **kw):
        deps = self._deps([in_], [out])
        self._emit_waits(q, deps)
        k = self.dcnt[q] % self.nds
        self.dcnt[q] += 1
        h, key = self.dsem[q][k]
        prev = 16 * self.duse[q][k]
        if prev > 0 and self.known[q].get(key, 0) < prev:
            self.known[q][key] = prev
            self.prog[q].append(("wait", h, prev))
        self.duse[q][k] += 1
        val = 16 * self.duse[q][k]
        o_ap, i_ap = out.ap, in_.ap
        self.prog[q].append(("op", lambda e: e.dma_start(out=o_ap, in_=i_ap, **kw), h, 16))
        tok = (h, key, val)
        self._update(tok, [in_], [out])
        self.n_ops += 1
        return tok

    def finish_dmas(self):
        for q in ("sp", "pool"):
            for k in range(self.nds):
                h, key = self.dsem[q][k]
                val = 16 * self.duse[q][k]
                if val > 0 and self.known[q].get(key, 0) < val:
                    self.known[q][key] = val
                    self.prog[q].append(("wait", h, val))

    def flush(self, name=None):
        self.finish_dmas()
        nc = self.nc
        prog = self.prog
        self.prog = {e: [] for e in self.ENG}

        def run(items, e):
            for it in items:
                if it[0] == "wait":
                    e.wait_ge(it[1], it[2])
                else:
                    inst = it[1](e)
                    inst.then_inc(it[2], it[3])

        with nc.Block() as block:
            if prog["sp"]:
                @block.sync
                def _(e):
                    run(prog["sp"], e)
            if prog["act"]:
                @block.scalar
                def _(e):
                    run(prog["act"], e)
            if prog["dve"]:
                @block.vector
                def _(e):
                    run(prog["dve"], e)
            if prog["pool"]:
                @block.gpsimd
                def _(e):
                    run(prog["pool"], e)
            if prog["pe"]:
                @block.tensor
                def _(e):
                    run(prog["pe"], e)

    def mm(self, out, pairs):
        n = len(pairs)

        def fn(e):
            inst = None
            for i, (l, r) in enumerate(pairs):
                inst = e.matmul(out.ap, l.ap, r.ap, start=(i == 0), stop=(i == n - 1))
            return inst

        self.op("pe", fn, reads=[p[0] for p in pairs] + [p[1] for p in pairs], writes=[out])

    def transpose(self, out, in_, ident):
        self.op("pe", lambda e: e.transpose(out.ap, in_.ap, ident.ap), reads=[in_, ident], writes=[out])

    def act(self, out, in_, func, bias=None, scale=None, accum=None):
        kw = {}
        reads = [in_]
        writes = [out]
        if bias is not None:
            if isinstance(bias, V):
                kw["bias"] = bias.ap
                reads.append(bias)
            else:
                kw["bias"] = bias
        if scale is not None:
            if isinstance(scale, V):
                kw["scale"] = scale.ap
                reads.append(scale)
            else:
                kw["scale"] = scale
        if accum is not None:
            kw["accum_out"] = accum.ap
            writes.append(accum)
        self.op("act", lambda e: e.activation(out.ap, in_.ap, func, **kw), reads=reads, writes=writes)

    def ts(self, out, in0, s1, s2, op0, op1=None, eng="dve"):
        reads = [in0]
        a1 = s1
        a2 = s2
        if isinstance(s1, V):
            reads.append(s1)
            a1 = s1.ap
        if isinstance(s2, V):
            reads.append(s2)
            a2 = s2.ap
        if op1 is None:
            self.op(eng, lambda e: e.tensor_scalar(out.ap, in0.ap, a1, a2, op0), reads=reads, writes=[out])
        else:
            self.op(eng, lambda e: e.tensor_scalar(out.ap, in0.ap, a1, a2, op0, op1), reads=reads, writes=[out])

    def tt(self, out, in0, in1, op, eng="dve"):
        self.op(eng, lambda e: e.tensor_tensor(out.ap, in0.ap, in1.ap, op), reads=[in0, in1], writes=[out])

    def stt(self, out, in0, scalar, in1, op0, op1):
        reads = [in0, in1]
        sc = scalar
        if isinstance(scalar, V):
            reads.append(scalar)
            sc = scalar.ap
        self.op("dve", lambda e: e.scalar_tensor_tensor(out.ap, in0.ap, sc, in1.ap, op0, op1),
                reads=reads, writes=[out])

    def copy(self, out, in_, eng="dve"):
        if eng == "act":
            self.op("act", lambda e: e.copy(out.ap, in_.ap), reads=[in_], writes=[out])
        else:
            self.op(eng, lambda e: e.tensor_copy(out.ap, in_.ap), reads=[in_], writes=[out])

    def recip(self, out, in_):
        self.op("dve", lambda e: e.reciprocal(out.ap, in_.ap), reads=[in_], writes=[out])

    def memset(self, out, val, eng="dve"):
        self.op(eng, lambda e: e.memset(out.ap, val), reads=[], writes=[out])


class Ctx:
    def __init__(self, nc, sched):
        self.nc = nc
        self.s = sched
        self.uid = 0

    def sb(self, es, name, shape, dtype=F32):
        self.uid += 1
        h = es.enter_context(self.nc.sbuf_tensor("%s_%d" % (name, self.uid), list(shape), dtype))
        return Buf(name, h)

    def ps(self, es, name, shape=(128, 512), dtype=F32):
        self.uid += 1
        h = es.enter_context(self.nc.psum_tensor("%s_%d" % (name, self.uid), list(shape), dtype))
        return Buf(name, h)

    def dram(self, name, shape, dtype=F32, kind="Internal"):
        h = self.nc.dram_tensor(name, list(shape), dtype, kind=kind)
        return Buf(name, h)


def bc_mid(v_buf, base_off, pstep, nparts, n_outer, outer_step, n_inner):
    return v_buf.view(base_off, [[pstep, nparts], [outer_step, n_outer], [0, n_inner]])


E_NCOL = 5152
OFF_Z = 0
OFF_XBC = 1024
OFF_DT = 2560
OFF_Q = 2592
OFF_KV = 3616
OFF_G = 4128
OFF_QS = 5152
OFF_KR = 6176
OFF_KSR = 6688
E_NCOL_EXT = 7200


ORDER = ["p1", "p2a", "p2b", "p2c", "p2d", "p2e", "p2f", "p2g", "p2h", "p3", "p4", "p5", "all"]


def build_program(debug=(), stop="all"):
    def go(tag):
        return ORDER.index(tag) <= ORDER.index(stop)
    nc = bass.Bass("TRN2", target_bir_lowering=False)
    s = Sched(nc)
    cx = Ctx(nc, s)
    dbg = set(debug)

    def din(name, shape):
        return Buf(name, nc.dram_tensor(name, list(shape), F32, kind="ExternalInput"))

    def dout(name, shape):
        return Buf(name, nc.dram_tensor(name, list(shape), F32, kind="ExternalOutput"))

    def scratch(name, shape):
        if name in dbg:
            return dout(name, shape)
        return Buf(name, nc.dram_tensor(name, list(shape), F32))

    xin = din("xin", [T, D])
    cvecT = din("cvecT", [128, 2, 8])
    consts = din("consts", [128, 6, 512])
    rope = din("rope", [128, 2, L])
    e_ada_w = din("e_ada_w", [D, 3 * D])
    e_ada_b = din("e_ada_b", [1, 3 * D])
    e_norm_wT = din("e_norm_wT", [128, 8])
    e_w_in = din("e_w_in", [D, E_NCOL_EXT])
    e_conv_wT = din("e_conv_wT", [128, 12, 5])
    e_conv_bT = din("e_conv_bT", [128, 12])
    e_dt_bias = din("e_dt_bias", [1, 32])
    e_a_log = din("e_a_log", [1, 32])
    e_d_skip = din("e_d_skip", [1, 16])
    e_ssd_norm_wT = din("e_ssd_norm_wT", [128, 8])
    e_sink = din("e_sink", [128, 8])
    e_w_out = din("e_w_out", [2 * D, D])
    o_ada_w = din("o_ada_w", [D, 3 * D])
    o_ada_b = din("o_ada_b", [1, 3 * D])
    o_norm_wT = din("o_norm_wT", [128, 8])
    o_w_in = din("o_w_in", [D, 2 * D])
    s5_lam = din("s5_lam", [128, 2, 3, 32])
    s5_b = din("s5_b", [128, 2, 32, 16])
    s5_c = din("s5_c", [128, 2, 32, 16])
    o_d_skip = din("o_d_skip", [1, D])
    o_glu_w = din("o_glu_w", [D, D])
    o_glu_b = din("o_glu_b", [1, D])
    o_w_out = din("o_w_out", [D, D])
    final_norm_w = din("final_norm_w", [1, D])
    out_t = dout("out", [L, D])

    XS = scratch("XS", [T, 1024])
    BTOK = scratch("BTOK", [T, 256])
    BT = scratch("BT", [2, 128, T])
    CT = scratch("CT", [2, 128, T])
    SZ = scratch("SZ", [T, 1024])
    QR = scratch("QR", [8, 128, L])
    QC = scratch("QC", [8, 128, CTX])
    KR = scratch("KR", [4, 128, L])
    KC = scratch("KC", [4, 128, CTX])
    VT = scratch("VT", [T, 256])
    SG = scratch("SG", [8, 128, T])
    YF = scratch("YF", [T, 1024])
    YT = scratch("YT", [16, 128, T])
    X1 = scratch("X1", [T, 1024])
    U = scratch("U", [T, 1024])
    SG1 = scratch("SG1", [T, 1024])
    YTOK = scratch("YTOK", [T, 1024])
    KFP = scratch("KFP", [64, 16, 15, 16])
    KBR = scratch("KBR", [64, 16, 15, 16])
    HT = scratch("HT", [8, 128, T]) if "HT" in dbg else None
    DTD = scratch("DTD", [T, 32]) if "DTD" in dbg else None
    MODD = scratch("MODD", [4, 128, 24]) if "MODD" in dbg else None

    with ExitStack() as top:
        banks = [cx.ps(top, "bank%d" % i) for i in range(8)]
        cst = cx.sb(top, "cst", [128, 6, 512])
        s.dma(cst.full(), consts.full())
        ident = cst[:, 0, 0:128]
        tri = cst[:, 1, 0:128]
        utri = cst[:, 2, 0:128]
        ones = cst[:, 3, 0:128]
        modT = [[cx.sb(top, "modT%d%d" % (l, w), [128, 24]) for w in range(2)] for l in range(2)]
        gate_bc = [[cx.sb(top, "gate%d%d" % (l, w), [128, 1024]) for w in range(2)] for l in range(2)]
        scs = cx.sb(top, "scs", [128, 2, 8])

        def adaln_phase(layer, ada_w, ada_b):
            with ExitStack() as es:
                aw = [cx.sb(es, "aw%d" % k, [128, 3 * D]) for k in range(8)]
                ab = cx.sb(es, "ab", [1, 3 * D])
                modrow = [cx.sb(es, "modrow%d" % w, [1, 3 * D]) for w in range(2)]
                if layer == 0:
                    cv = cx.sb(es, "cv", [128, 2, 8])
                    s.dma(cv.full(), cvecT.full())
                    s.act(scs.full(), cv.full(), AF.Silu)
                for k in range(8):
                    s.dma(aw[k].full(), ada_w[k * 128:(k + 1) * 128, :])
                s.dma(ab.full(), ada_b.full())
                bi = 0
                for w in range(2):
                    for fg in range(6):
                        bk = banks[bi % 8]
                        bi += 1
                        s.mm(bk[0:1, :], [(scs[:, w, k:k + 1], aw[k][:, fg * 512:(fg + 1) * 512]) for k in range(8)])
                        s.tt(modrow[w][0:1, fg * 512:(fg + 1) * 512], bk[0:1, :], ab[0:1, fg * 512:(fg + 1) * 512], ALU.add)
                for w in range(2):
                    bk = banks[bi % 8]
                    bi += 1
                    for fc in range(24):
                        s.mm(bk[:, 2 * fc:2 * fc + 2], [(modrow[w][0:1, fc * 128:(fc + 1) * 128], cst[0:1, 3, 0:2])])
                    s.copy(modT[layer][w].full(), bk.view(0, [[512, 128], [2, 24]]))
                    for hh in range(2):
                        bk2 = banks[bi % 8]
                        bi += 1
                        s.mm(bk2.full(), [(cst[0:1, 3, 0:128], modrow[w][0:1, 2048 + hh * 512:2048 + (hh + 1) * 512])])
                        s.copy(gate_bc[layer][w][:, hh * 512:(hh + 1) * 512], bk2.full(), eng="act")
                    if MODD is not None:
                        s.dma(MODD[layer * 2 + w], modT[layer][w].full())
                s.flush()

        adaln_phase(0, e_ada_w, e_ada_b)

        with ExitStack() as l0:
            DT = cx.sb(l0, "DT", [128, NT, 32])
            DTA = cx.sb(l0, "DTA", [128, NT, 32])
            nw = cx.sb(l0, "nw", [128, 8])
            sc1 = [cx.sb(l0, "sc1_%d" % w, [128, 8]) for w in range(2)]
            s.dma(nw.full(), e_norm_wT.full())
            for w in range(2):
                s.stt(sc1[w].full(), modT[0][w][:, 8:16], 1.0, nw.full(), ALU.add, ALU.mult)

            hts = ExitStack()
            hT = [cx.sb(hts, "hT%d" % k, [128, T]) for k in range(8)]
            with ExitStack() as es:
                xt = [cx.sb(es, "xt%d" % i, [128, D]) for i in range(2)]
                xn = [cx.sb(es, "xn%d" % i, [128, D]) for i in range(2)]
                junk = cx.sb(es, "junk", [128, D])
                st = [cx.sb(es, "st%d" % i, [128, 4]) for i in range(2)]
                for i in range(NT):
                    w = 1 if i < 2 else 0
                    x_ = xt[i % 2]
                    n_ = xn[i % 2]
                    st_ = st[i % 2]
                    s.dma(x_.full(), xin[i * 128:(i + 1) * 128, :])
                    s.act(junk.full(), x_.full(), AF.Square, accum=st_[:, 0:1])
                    s.ts(st_[:, 1:2], st_[:, 0:1], 1.0 / D, EPS, ALU.mult, ALU.add)
                    s.act(st_[:, 2:3], st_[:, 1:2], AF.Sqrt)
                    s.recip(st_[:, 3:4], st_[:, 2:3])
                    s.ts(n_.full(), x_.full(), st_[:, 3:4], None, ALU.mult)
                    for half in range(2):
                        bk = banks[(2 * i + half) % 8]
                        for kk in range(4):
                            k = half * 4 + kk
                            s.transpose(bk[:, kk * 128:(kk + 1) * 128], n_[:, k * 128:(k + 1) * 128], ident)
                        for kk in range(4):
                            k = half * 4 + kk
                            s.act(hT[k][:, i * 128:(i + 1) * 128], bk[:, kk * 128:(kk + 1) * 128], AF.Identity,
                                  bias=modT[0][w][:, k:k + 1], scale=sc1[w][:, k:k + 1])
                if HT is not None:
                    for k in range(8):
                        s.dma(HT[k], hT[k].full())
                s.flush()

            with ExitStack() as es:
                WB = 256
                wbuf = [cx.sb(es, "wbuf%d" % i, [128, 8, WB]) for i in range(4)]
                wstate = {"i": 0}

                def load_w(col0, ncol=WB):
                    wb = wbuf[wstate["i"] % 4]
                    wstate["i"] += 1
                    s.dma(wb[:, :, 0:ncol], e_w_in.view(col0, [[E_NCOL_EXT, 128], [128 * E_NCOL_EXT, 8], [1, ncol]]))
                    return wb

                bstate = {"i": 0}

                def nbank():
                    bk = banks[bstate["i"] % 8]
                    bstate["i"] += 1
                    return bk

                def fm_mm(wb, cc, t0, n):
                    bk = nbank()
                    s.mm(bk[:, 0:n], [(wb[:, k, cc * 128:(cc + 1) * 128], hT[k][:, t0:t0 + n]) for k in range(8)])
                    return bk

                xraw = cx.sb(es, "xraw", [128, T])
                acc = cx.sb(es, "acc", [128, T])
                tmp1 = cx.sb(es, "tmp1", [128, 512])
                tmp2 = cx.sb(es, "tmp2", [128, 512])
                stg = [cx.sb(es, "stg%d" % i, [128, 4, 128]) for i in range(2)]
                rp = cx.sb(es, "rp", [128, 2, L])
                cw = cx.sb(es, "cw", [128, 12, 5])
                cb = cx.sb(es, "cb", [128, 12])
                dtb = cx.sb(es, "dtb", [128, 32])
                abc = cx.sb(es, "abc", [128, 32])
                s.dma(rp.full(), rope.full())
                s.dma(cw.full(), e_conv_wT.full())
                s.dma(cb.full(), e_conv_bT.full())
                s.dma(dtb.full(), e_dt_bias.view(0, [[0, 128], [1, 32]]))
                s.dma(abc.full(), e_a_log.view(0, [[0, 128], [1, 32]]))
                s.act(abc.full(), abc.full(), AF.Exp)
                s.ts(abc.full(), abc.full(), -1.0, None, ALU.mult)
                stg_i = {"i": 0}

                def transposes_to(dst, col0, src):
                    for i0 in range(0, NT, 4):
                        nb = min(4, NT - i0)
                        bk = nbank()
                        for ii in range(nb):
                            i = i0 + ii
                            s.transpose(bk[:, ii * 128:(ii + 1) * 128], src[:, i * 128:(i + 1) * 128], ident)
                        sg_ = stg[stg_i["i"] % 2]
                        stg_i["i"] += 1
                        s.copy(sg_[:, 0:nb, :], bk.view(0, [[512, 128], [128, nb], [1, 128]]), eng="act")
                        ncols = dst.h.shape[1]
                        s.dma(dst.view(i0 * 128 * ncols + col0, [[ncols, 128], [128 * ncols, nb], [1, 128]]),
                              sg_[:, 0:nb, :])

                for fc in range(12 if go('p2a') else 0):
                    if fc % 2 == 0:
                        wb = load_w(OFF_XBC + fc * 128)
                    cc = fc % 2
                    for (t0, n) in TG:
                        bk = fm_mm(wb, cc, t0, n)
                        s.copy(xraw[:, t0:t0 + n], bk[:, 0:n], eng="act")
                    s.ts(acc.full(), xraw.full(), cw[:, fc, 2:3], cb[:, fc:fc + 1], ALU.mult, ALU.add)
                    for kk in (0, 1, 3, 4):
                        d_ = kk - 2
                        for (s0, sl) in ((0, CTX), (CTX, L)):
                            lo = max(s0, s0 - d_)
                            hi = min(s0 + sl, s0 + sl - d_)
                            s.stt(acc[:, lo:hi], xraw[:, lo + d_:hi + d_], cw[:, fc, kk:kk + 1], acc[:, lo:hi],
                                  ALU.mult, ALU.add)
                    s.act(acc.full(), acc.full(), AF.Silu)
                    if fc < 8:
                        transposes_to(XS, fc * 128, acc)
                    elif fc < 10:
                        s.dma(BT[fc - 8], acc.full())
                        transposes_to(BTOK, (fc - 8) * 128, acc)
                    else:
                        s.dma(CT[fc - 10], acc.full())

                def rope_chunk(col_plain, col_swap, dst_rot, dst_ctx):
                    wa = load_w(col_plain, 128)
                    wsw = load_w(col_swap, 128)
                    for gi, (t0, n) in enumerate(TG):
                        bka = fm_mm(wa, 0, t0, n)
                        if gi == 0:
                            s.copy(acc[:, 0:CTX], bka[:, 0:CTX], eng="act")
                            continue
                        bkb = fm_mm(wsw, 0, t0, n)
                        l0 = t0 - CTX
                        s.tt(tmp1.full(), bka.full(), rp[:, 0, l0:l0 + 512], ALU.mult)
                        s.tt(tmp2.full(), bkb.full(), rp[:, 1, l0:l0 + 512], ALU.mult)
                        s.tt(acc[:, t0:t0 + n], tmp1.full(), tmp2.full(), ALU.add, eng="pool")
                    s.dma(dst_ctx, acc[:, 0:CTX])
                    s.dma(dst_rot, acc[:, CTX:T])

                for qc in range(8 if go('p2b') else 0):
                    rope_chunk(OFF_Q + qc * 128, OFF_QS + qc * 128, QR[qc], QC[qc])
                for j in range(4 if go('p2c') else 0):
                    rope_chunk(OFF_KR + j * 128, OFF_KSR + j * 128, KR[j], KC[j])

                for gc in range(8 if go('p2d') else 0):
                    if gc % 2 == 0:
                        wb = load_w(OFF_G + gc * 128)
                    for (t0, n) in TG:
                        bk = fm_mm(wb, gc % 2, t0, n)
                        s.act(acc[:, t0:t0 + n], bk[:, 0:n], AF.Silu)
                    s.dma(SG[gc], acc.full())

                NT_E = NT if go('p2e') else 0
                wz = [load_w(OFF_Z + i * 256) for i in range(4)]
                zt = [cx.sb(es, "zt%d" % i, [128, D]) for i in range(2)]
                for i in range(NT_E):
                    z_ = zt[i % 2]
                    for half in range(2):
                        bk = nbank()
                        for q4 in range(2):
                            wbz = wz[half * 2 + q4]
                            s.mm(bk[:, q4 * 256:(q4 + 1) * 256],
                                 [(hT[k][:, i * 128:(i + 1) * 128], wbz[:, k, :]) for k in range(8)])
                        s.act(z_[:, half * 512:(half + 1) * 512], bk.full(), AF.Silu)
                    s.dma(SZ[i * 128:(i + 1) * 128, :], z_.full())
                wv = load_w(OFF_KV + 256)
                wdt = load_w(OFF_DT, 32)
                vt = [cx.sb(es, "vt%d" % i, [128, 256]) for i in range(2)]
                for i in range(NT if go('p2f') else 0):
                    bk = nbank()
                    s.mm(bk[:, 0:256], [(hT[k][:, i * 128:(i + 1) * 128], wv[:, k, :]) for k in range(8)])
                    s.copy(vt[i % 2].full(), bk[:, 0:256], eng="act")
                    s.dma(VT[i * 128:(i + 1) * 128, :], vt[i % 2].full())
                for i in range(NT if go('p2g') else 0):
                    bk = nbank()
                    s.mm(bk[:, 0:32], [(hT[k][:, i * 128:(i + 1) * 128], wdt[:, k, 0:32]) for k in range(8)])
                    s.tt(DT[:, i, :], bk[:, 0:32], dtb.full(), ALU.add)
                    if go('p2h'):
                        s.act(DT[:, i, :], DT[:, i, :], AF.Exp)
                        s.ts(DT[:, i, :], DT[:, i, :], 1.0, None, ALU.add)
                        s.act(DT[:, i, :], DT[:, i, :], AF.Ln)
                    s.tt(DTA[:, i, :], DT[:, i, :], abc.full(), ALU.mult)
                    if DTD is not None:
                        s.dma(DTD[i * 128:(i + 1) * 128, :], DT[:, i, :])
                s.flush()
            hts.close()

            with ExitStack() as es:
                nb_ = {"i": 0}

                def nbank():
                    bk = banks[nb_["i"] % 8]
                    nb_["i"] += 1
                    return bk

                xs_t = [cx.sb(es, "xs_t%d" % i, [128, 1024]) for i in range(2)]
                b_t = [cx.sb(es, "b_t%d" % i, [128, 256]) for i in range(2)]
                bt_t = [cx.sb(es, "bt_t%d" % i, [128, 2, 128]) for i in range(2)]
                ct_t = [cx.sb(es, "ct_t%d" % i, [128, 2, 128]) for i in range(2)]
                yf_t = [cx.sb(es, "yf_t%d" % i, [128, 1024]) for i in range(2)]
                sz_t = [cx.sb(es, "sz_t%d" % i, [128, 1024]) for i in range(2)]
                dtatri = cx.sb(es, "dtatri", [128, 2048])
                decT = cx.sb(es, "decT", [128, 2048])
                MT = cx.sb(es, "MT", [128, 2048])
                xc = cx.sb(es, "xc", [128, 1024])
                xcd = cx.sb(es, "xcd", [128, 1024])
                cb_sb = cx.sb(es, "cb_sb", [128, 256])
                tmpo = cx.sb(es, "tmpo", [128, 1024])
                ytot = cx.sb(es, "ytot", [128, 1024])
                junk = cx.sb(es, "junk3", [128, 1024])
                ystg = cx.sb(es, "ystg", [128, 8, 128])
                Hs = [cx.sb(es, "Hs%d" % g, [128, 512]) for g in range(2)]
                sm = cx.sb(es, "sm", [128, 4, 16])
                st3 = cx.sb(es, "st3", [128, 4])
                dsk = cx.sb(es, "dsk", [128, 16])
                snw = cx.sb(es, "snw", [128, 8])
                s.dma(dsk.full(), e_d_skip.view(0, [[0, 128], [1, 16]]))
                s.dma(snw.full(), e_ssd_norm_wT.full())

                def bc3(buf, off, pstep, n1, s1, n2, s2):
                    return buf.view(off, [[pstep, 128], [s1, n1], [s2, n2]])

                n_ch = NT if go("p3") else 0
                for d_ in range(2):
                    order = list(range(NT)) if d_ == 0 else [1, 0] + list(range(NT - 1, 1, -1))
                    order = order[:n_ch]
                    TRIoff = 512 if d_ == 0 else 1024
                    TRIv = tri if d_ == 0 else utri
                    negm = cst[:, 4 + d_, :]
                    for g in range(2):
                        s.memset(Hs[g].full(), 0.0)
                    for ci, i in enumerate(order):
                        pp = ci % 2
                        xs_, b_, bt_, ct_ = xs_t[pp], b_t[pp], bt_t[pp], ct_t[pp]
                        s.dma(xs_.full(), XS[i * 128:(i + 1) * 128, :])
                        s.dma(b_.full(), BTOK[i * 128:(i + 1) * 128, :])
                        s.dma(bt_.full(), BT.view(i * 128, [[T, 128], [128 * T, 2], [1, 128]]))
                        s.dma(ct_.full(), CT.view(i * 128, [[T, 128], [128 * T, 2], [1, 128]]))
                        dta_i = DTA[:, i, d_ * 16:(d_ + 1) * 16]
                        doff = i * 32 + d_ * 16
                        s.tt(bc3(dtatri, 0, 2048, 16, 128, 128, 1), bc3(DTA, doff, NT * 32, 16, 1, 128, 0),
                             bc3(cst, TRIoff, 3072, 16, 0, 128, 1), ALU.mult)
                        bs = nbank()
                        s.mm(bs[:, 0:16], [(TRIv, dta_i)])
                        s.mm(bs[:, 16:32], [(ones, dta_i)])
                        na, ea, de, cd = sm[:, 0, :], sm[:, 1, :], sm[:, 2, :], sm[:, 3, :]
                        s.ts(na, bs[:, 0:16], -1.0, None, ALU.mult)
                        s.act(ea, bs[:, 0:16], AF.Exp)
                        s.tt(de, bs[:, 16:32], na, ALU.add)
                        s.act(de, de, AF.Exp)
                        s.act(cd, bs[:, 16:32], AF.Exp)
                        for hq in range(4):
                            bq = nbank()
                            s.mm(bq.full(), [(ones, dtatri[:, hq * 512:(hq + 1) * 512]), (ident, negm)])
                            for hh in range(4):
                                h = hq * 4 + hh
                                s.act(decT[:, h * 128:(h + 1) * 128], bq[:, hh * 128:(hh + 1) * 128], AF.Exp,
                                      bias=sm[:, 0, h:h + 1])
                        bc = nbank()
                        for g in range(2):
                            s.mm(bc[:, g * 128:(g + 1) * 128], [(bt_[:, g, :], ct_[:, g, :])])
                        s.copy(cb_sb.full(), bc[:, 0:256], eng="act")
                        for g in range(2):
                            s.tt(bc3(MT, g * 1024, 2048, 8, 128, 128, 1), bc3(decT, g * 1024, 2048, 8, 128, 128, 1),
                                 bc3(cb_sb, g * 128, 256, 8, 0, 128, 1), ALU.mult)
                        s.tt(bc3(xc, 0, 1024, 16, 64, 64, 1), bc3(xs_, 0, 1024, 16, 64, 64, 1),
                             bc3(DT, doff, NT * 32, 16, 1, 64, 0), ALU.mult)
                        s.tt(bc3(xcd, 0, 1024, 16, 64, 64, 1), bc3(xc, 0, 1024, 16, 64, 64, 1),
                             bc3(sm, 32, 64, 16, 1, 64, 0), ALU.mult)
                        ydst = yf_t[pp] if d_ == 0 else ytot
                        for g in range(2):
                            by = nbank()
                            for hh in range(8):
                                h = g * 8 + hh
                                s.mm(by[:, hh * 64:(hh + 1) * 64], [(MT[:, h * 128:(h + 1) * 128], xc[:, h * 64:(h + 1) * 64])])
                            bo = nbank()
                            s.mm(bo.full(), [(ct_[:, g, :], Hs[g].full())])
                            s.tt(bc3(tmpo, g * 512, 1024, 8, 64, 64, 1), bc3(bo, 0, 512, 8, 64, 64, 1),
                                 bc3(sm, 16 + g * 8, 64, 8, 1, 64, 0), ALU.mult)
                            s.tt(ydst[:, g * 512:(g + 1) * 512], by.full(), tmpo[:, g * 512:(g + 1) * 512], ALU.add)
                        for g in range(2):
                            bst = nbank()
                            s.mm(bst.full(), [(b_[:, g * 128:(g + 1) * 128], xcd[:, g * 512:(g + 1) * 512])])
                            s.tt(bc3(Hs[g], 0, 512, 8, 64, 64, 1), bc3(Hs[g], 0, 512, 8, 64, 64, 1),
                                 bc3(sm, 48 + g * 8, 64, 8, 1, 64, 0), ALU.mult)
                            s.tt(Hs[g].full(), Hs[g].full(), bst.full(), ALU.add)
                        if d_ == 0:
                            s.dma(YF[i * 128:(i + 1) * 128, :], yf_t[pp].full())
                        else:
                            yf_, sz_ = yf_t[pp], sz_t[pp]
                            s.dma(yf_.full(), YF[i * 128:(i + 1) * 128, :])
                            s.dma(sz_.full(), SZ[i * 128:(i + 1) * 128, :])
                            s.tt(ytot.full(), ytot.full(), yf_.full(), ALU.add)
                            s.tt(bc3(tmpo, 0, 1024, 16, 64, 64, 1), bc3(xs_, 0, 1024, 16, 64, 64, 1),
                                 bc3(dsk, 0, 16, 16, 1, 64, 0), ALU.mult)
                            s.tt(ytot.full(), ytot.full(), tmpo.full(), ALU.add)
                            s.tt(ytot.full(), ytot.full(), sz_.full(), ALU.mult)
                            s.act(junk.full(), ytot.full(), AF.Square, accum=st3[:, 0:1])
                            s.ts(st3[:, 1:2], st3[:, 0:1], 1.0 / 1024, EPS, ALU.mult, ALU.add)
                            s.act(st3[:, 2:3], st3[:, 1:2], AF.Sqrt)
                            s.recip(st3[:, 3:4], st3[:, 2:3])
                            s.ts(ytot.full(), ytot.full(), st3[:, 3:4], None, ALU.mult)
                            for half in range(2):
                                bk = nbank()
                                for kk in range(4):
                                    k = half * 4 + kk
                                    s.transpose(bk[:, kk * 128:(kk + 1) * 128], ytot[:, k * 128:(k + 1) * 128], ident)
                                for kk in range(4):
                                    k = half * 4 + kk
                                    s.act(ystg[:, k, :], bk[:, kk * 128:(kk + 1) * 128], AF.Copy, scale=snw[:, k:k + 1])
                            s.dma(YT.view(i * 128, [[T, 128], [128 * T, 8], [1, 128]]), ystg.full())
                s.flush()

            with ExitStack() as es:
                nb_ = {"i": 0}

                def nbank():
                    bk = banks[nb_["i"] % 8]
                    nb_["i"] += 1
                    return bk

                qr_t = cx.sb(es, "qr_t", [128, 2, L])
                qc_t = cx.sb(es, "qc_t", [128, 2, CTX])
                kr_t = cx.sb(es, "kr_t", [128, L])
                kc_t = cx.sb(es, "kc_t", [128, CTX])
                v_t = cx.sb(es, "v_t", [128, NT, 64])
                v2 = cx.sb(es, "v2", [128, NT, 128])
                sg_t = cx.sb(es, "sg_t", [128, 2, T])
                ast = cx.sb(es, "ast", [128, 2, T])
                pt = [[cx.sb(es, "pt%d_%d" % (a, b), [128, 512]) for b in range(5)] for a in range(2)]
                rd = cx.sb(es, "rd", [128, 256])
                ao = cx.sb(es, "ao", [128, 256])
                es_pp = cx.sb(es, "es_pp", [128, 8])
                c8 = cx.sb(es, "c8", [128, 1])
                s.memset(c8.full(), 0.125)
                s.dma(es_pp.full(), e_sink.full())
                s.act(es_pp.full(), es_pp.full(), AF.Exp)
                qb_i = 0
                ATT_DBG = [int(v) for v in os.environ.get("ATT_DBG", "4,18,4").split(",")]
                for j in range(ATT_DBG[0] if go("p4") else 0):
                    s.dma(qr_t.full(), QR.view(2 * j * 128 * L, [[L, 128], [128 * L, 2], [1, L]]))
                    s.dma(qc_t.full(), QC.view(2 * j * 128 * CTX, [[CTX, 128], [128 * CTX, 2], [1, CTX]]))
                    s.dma(kr_t.full(), KR[j])
                    s.dma(kc_t.full(), KC[j])
                    s.dma(v_t.full(), VT.view(j * 64, [[256, 128], [128 * 256, NT], [1, 64]]))
                    s.dma(sg_t.full(), SG.view(2 * j * 128 * T, [[T, 128], [128 * T, 2], [1, T]]))
                    s.copy(v2[:, :, 0:64], v_t.full(), eng="act")
                    s.copy(v2[:, :, 64:128], v_t.full(), eng="pool")
                    for kind, bi in ([("c", 0), ("c", 1)] + [("l", b) for b in range(16)])[:ATT_DBG[1]]:
                        if kind == "c":
                            qsrc, q0, tok0 = qc_t, bi * 128, bi * 128
                            keys = [("c", 0, None), ("c", 1, None)]
                        else:
                            qsrc, q0, tok0 = qr_t, bi * 128, CTX + bi * 128
                            keys = [("c", 0, None), ("c", 1, None)]
                            if bi > 0:
                                keys.append(("l", bi - 1, "prev"))
                            keys.append(("l", bi, None))
                            if bi < 15:
                                keys.append(("l", bi + 1, "next"))
                        pts = pt[qb_i % 2]
                        qb_i += 1
                        qw = qsrc.h.shape[2]
                        for ki, (kk, kb, msk) in enumerate(keys):
                            ksrc = kc_t if kk == "c" else kr_t
                            for par in range(2):
                                p0 = par * 64
                                bs = nbank()
                                s.mm(bs[:, 0:256],
                                     [(ksrc[p0:p0 + 64, kb * 128:(kb + 1) * 128],
                                       qsrc.view(p0 * 2 * qw + q0, [[2 * qw, 64], [qw, 2], [1, 128]]))])
                                s.act(pts[ki][:, par * 256:(par + 1) * 256], bs[:, 0:256], AF.Exp, scale=c8[:, 0:1])
                            if msk is not None and ATT_DBG[2] >= 2:
                                moff = 1024 if msk == "prev" else 512
                                s.tt(pts[ki].view(0, [[512, 128], [128, 4], [1, 128]]),
                                     pts[ki].view(0, [[512, 128], [128, 4], [1, 128]]),
                                     cst.view(moff, [[3072, 128], [0, 4], [1, 128]]), ALU.mult)
                        if ATT_DBG[2] < 3:
                            continue
                        vt_idx = [(kb if kk == "c" else 2 + kb) for (kk, kb, _) in keys]
                        bn = nbank()
                        s.mm(bn.full(), [(v2[:, vt_idx[ki], :], pts[ki].full()) for ki in range(len(keys))])
                        bd = nbank()
                        s.mm(bd.full(), [(ones, pts[ki].full()) for ki in range(len(keys))])
                        if ATT_DBG[2] < 4:
                            continue
                        for par in range(2):
                            p0 = par * 64
                            for c in range(2):
                                s.ts(rd[p0:p0 + 64, c * 128:(c + 1) * 128],
                                     bd[p0:p0 + 64, par * 256 + c * 128:par * 256 + (c + 1) * 128],
                                     es_pp[p0:p0 + 64, 2 * j + c:2 * j + c + 1], None, ALU.add)
                        s.recip(rd.full(), rd.full())
                        for par in range(2):
                            p0 = par * 64
                            s.tt(ao[p0:p0 + 64, :], bn[p0:p0 + 64, par * 256:(par + 1) * 256], rd[p0:p0 + 64, :], ALU.mult)
                        s.tt(ast.view(tok0, [[2 * T, 128], [T, 2], [1, 128]]),
                             ao.view(0, [[256, 128], [128, 2], [1, 128]]),
                             sg_t.view(tok0, [[2 * T, 128], [T, 2], [1, 128]]), ALU.mult)
                    s.dma(YT.view((8 + 2 * j) * 128 * T, [[T, 128], [128 * T, 2], [1, T]]), ast.full())
                s.flush()

            with ExitStack() as es:
                nb_ = {"i": 0}

                def nbank():
                    bk = banks[nb_["i"] % 8]
                    nb_["i"] += 1
                    return bk

                wo = [cx.sb(es, "wo%d" % k, [128, D]) for k in range(16)]
                yt = [cx.sb(es, "yt%d" % i, [128, 16, 128]) for i in range(2)]
                xt = [cx.sb(es, "xt5_%d" % i, [128, D]) for i in range(2)]
                x1t = [cx.sb(es, "x1t%d" % i, [128, D]) for i in range(2)]
                tmp5 = cx.sb(es, "tmp5", [128, 512])
                for k in range(16):
                    s.dma(wo[k].full(), e_w_out[k * 128:(k + 1) * 128, :])
                for i in range(NT if go("p5") else 0):
                    w = 1 if i < 2 else 0
                    y_, x_, o_ = yt[i % 2], xt[i % 2], x1t[i % 2]
                    s.dma(y_.full(), YT.view(i * 128, [[T, 128], [128 * T, 16], [1, 128]]))
                    s.dma(x_.full(), xin[i * 128:(i + 1) * 128, :])
                    for half in range(2):
                        bk = nbank()
                        s.mm(bk.full(), [(y_[:, fc, :], wo[fc][:, half * 512:(half + 1) * 512]) for fc in range(16)])
                        s.tt(tmp5.full(), bk.full(), gate_bc[0][w][:, half * 512:(half + 1) * 512], ALU.mult)
                        s.tt(o_[:, half * 512:(half + 1) * 512], tmp5.full(), x_[:, half * 512:(half + 1) * 512], ALU.add)
                    s.dma(X1[i * 128:(i + 1) * 128, :], o_.full())
                s.flush()

        if go("all"):
            adaln_phase(1, o_ada_w, o_ada_b)
        with ExitStack() as l1:
            if not go("all"):
                return nc
            nb_ = {"i": 0}

            def nbank():
                bk = banks[nb_["i"] % 8]
                nb_["i"] += 1
                return bk

            with ExitStack() as es:
                nw = cx.sb(es, "nw1", [128, 8])
                sc1 = [cx.sb(es, "sc1b_%d" % w, [128, 8]) for w in range(2)]
                s.dma(nw.full(), o_norm_wT.full())
                for w in range(2):
                    s.stt(sc1[w].full(), modT[1][w][:, 8:16], 1.0, nw.full(), ALU.add, ALU.mult)
                hT = [cx.sb(es, "hTb%d" % k, [128, T]) for k in range(8)]
                xt = [cx.sb(es, "xtb%d" % i, [128, D]) for i in range(2)]
                xn = [cx.sb(es, "xnb%d" % i, [128, D]) for i in range(2)]
                junk = cx.sb(es, "junkb", [128, D])
                st = [cx.sb(es, "stb%d" % i, [128, 4]) for i in range(2)]
                for i in range(NT):
                    w = 1 if i < 2 else 0
                    x_, n_, st_ = xt[i % 2], xn[i % 2], st[i % 2]
                    s.dma(x_.full(), X1[i * 128:(i + 1) * 128, :])
                    s.act(junk.full(), x_.full(), AF.Square, accum=st_[:, 0:1])
                    s.ts(st_[:, 1:2], st_[:, 0:1], 1.0 / D, EPS, ALU.mult, ALU.add)
                    s.act(st_[:, 2:3], st_[:, 1:2], AF.Sqrt)
                    s.recip(st_[:, 3:4], st_[:, 2:3])
                    s.ts(n_.full(), x_.full(), st_[:, 3:4], None, ALU.mult)
                    for half in range(2):
                        bk = nbank()
                        for kk in range(4):
                            k = half * 4 + kk
                            s.transpose(bk[:, kk * 128:(kk + 1) * 128], n_[:, k * 128:(k + 1) * 128], ident)
                        for kk in range(4):
                            k = half * 4 + kk
                            s.act(hT[k][:, i * 128:(i + 1) * 128], bk[:, kk * 128:(kk + 1) * 128], AF.Identity,
                                  bias=modT[1][w][:, k:k + 1], scale=sc1[w][:, k:k + 1])
                wq = [cx.sb(es, "wq%d" % i, [128, 8, 256]) for i in range(4)]
                ot = [cx.sb(es, "ot%d" % i, [128, D]) for i in range(2)]
                oi = 0
                for which in range(2):
                    for q4 in range(4):
                        s.dma(wq[q4].full(), o_w_in.view(which * 1024 + q4 * 256, [[2 * D, 128], [128 * 2 * D, 8], [1, 256]]))
                    for i in range(NT):
                        if which == 1 and i < 2:
                            continue
                        o_ = ot[oi % 2]
                        oi += 1
                        for half in range(2):
                            bk = nbank()
                            for q4 in range(2):
                                s.mm(bk[:, q4 * 256:(q4 + 1) * 256],
                                     [(hT[k][:, i * 128:(i + 1) * 128], wq[half * 2 + q4][:, k, :]) for k in range(8)])
                            if which == 0:
                                s.copy(o_[:, half * 512:(half + 1) * 512], bk.full(), eng="act")
                            else:
                                s.act(o_[:, half * 512:(half + 1) * 512], bk.full(), AF.Silu)
                        s.dma((U if which == 0 else SG1)[i * 128:(i + 1) * 128, :], o_.full())
                s.flush()

            L1S = os.environ.get('L1S', 'z')
            if L1S == 'a':
                return nc
            with ExitStack() as es:
                lam = cx.sb(es, "lam", [128, 2, 3, 32])
                bprm = cx.sb(es, "bprm", [128, 2, 32, 16])
                cprm = cx.sb(es, "cprm", [128, 2, 32, 16])
                s.dma(lam.full(), s5_lam.full())
                s.dma(bprm.full(), s5_b.full())
                s.dma(cprm.full(), s5_c.full())
                kc = cx.sb(es, "kconst", [128, 4])
                s.memset(kc[:, 0:1], 1.0 / 16)
                s.memset(kc[:, 1:2], math.pi / 2)
                s.memset(kc[:, 2:3], 0.0)
                s.memset(kc[:, 3:4], 1.0)
                W64 = [128, 2, 32]

                def t64(name):
                    return cx.sb(es, name, W64)

                def lv(i):
                    return lam.view(i * 32, [[192, 128], [96, 2], [1, 32]])

                dt_ = t64("dt_"); mag = t64("mag"); th = t64("th"); cs = t64("cs"); sn = t64("sn")
                t_a = t64("t_a"); t_b = t64("t_b"); t_c = t64("t_c")
                abre = t64("abre"); abim = t64("abim"); cre = t64("cre"); cim = t64("cim")
                s.act(dt_.full(), lv(2), AF.Exp)
                s.tt(t_a.full(), lv(0), dt_.full(), ALU.mult)
                s.act(mag.full(), t_a.full(), AF.Exp)
                s.tt(th.full(), lv(1), dt_.full(), ALU.mult)
                s.act(sn.full(), th.full(), AF.Sin, scale=kc[:, 0:1])
                s.act(cs.full(), th.full(), AF.Sin, scale=kc[:, 0:1], bias=kc[:, 1:2])
                for _ in range(4):
                    s.tt(t_a.full(), cs.full(), cs.full(), ALU.mult)
                    s.tt(t_b.full(), sn.full(), sn.full(), ALU.mult)
                    s.tt(t_c.full(), sn.full(), cs.full(), ALU.mult)
                    s.tt(cs.full(), t_a.full(), t_b.full(), ALU.subtract)
                    s.ts(sn.full(), t_c.full(), 2.0, None, ALU.mult)
                s.tt(abre.full(), mag.full(), cs.full(), ALU.mult)
                s.tt(abim.full(), mag.full(), sn.full(), ALU.mult)
                PW = cx.sb(es, "PW", [128, 2, 9, 64])

                def pw(ri, k):
                    return PW.view((ri * 9 + k) * 64, [[2 * 9 * 64, 128], [32, 2], [1, 32]])

                s.memset(PW[:, 0, 0, :], 1.0)
                s.memset(PW[:, 1, 0, :], 0.0)
                for k in range(8):
                    s.tt(t_a.full(), pw(0, k), abre.full(), ALU.mult)
                    s.tt(t_b.full(), pw(1, k), abim.full(), ALU.mult)
                    s.tt(pw(0, k + 1), t_a.full(), t_b.full(), ALU.subtract)
                    s.tt(t_a.full(), pw(0, k), abim.full(), ALU.mult)
                    s.tt(t_b.full(), pw(1, k), abre.full(), ALU.mult)
                    s.tt(pw(1, k + 1), t_a.full(), t_b.full(), ALU.add)
                s.ts(t_c.full(), abre.full(), -1.0, None, ALU.add)
                s.tt(t_a.full(), lv(0), lv(0), ALU.mult)
                s.tt(t_b.full(), lv(1), lv(1), ALU.mult)
                s.tt(t_a.full(), t_a.full(), t_b.full(), ALU.add)
                s.recip(dt_.full(), t_a.full())
                s.tt(t_a.full(), t_c.full(), lv(0), ALU.mult)
                s.tt(t_b.full(), abim.full(), lv(1), ALU.mult)
                s.tt(t_a.full(), t_a.full(), t_b.full(), ALU.add)
                s.tt(cre.full(), t_a.full(), dt_.full(), ALU.mult)
                s.tt(t_a.full(), abim.full(), lv(0), ALU.mult)
                s.tt(t_b.full(), t_c.full(), lv(1), ALU.mult)
                s.tt(t_a.full(), t_a.full(), t_b.full(), ALU.subtract)
                s.tt(cim.full(), t_a.full(), dt_.full(), ALU.mult)
                BB = cx.sb(es, "BB", [128, 2, 2, 512])
                tb1 = cx.sb(es, "tb1", [128, 512])
                tb2 = cx.sb(es, "tb2", [128, 512])

                def bb(ri, d_, g0=0, ng=32):
                    return BB.view((ri * 2 + d_) * 512 + g0 * 16, [[2048, 128], [16, ng], [1, 16]])

                def v3(buf, off, pstep, n1, s1, n2, s2):
                    return buf.view(off, [[pstep, 128], [s1, n1], [s2, n2]])

                def prm(buf, ri, g0=0, ng=32):
                    return buf.view(ri * 512 + g0 * 16, [[1024, 128], [16, ng], [1, 16]])

                def cf(buf, d_, g0=0, ng=32, n2=16):
                    return buf.view(d_ * 32 + g0, [[64, 128], [1, ng], [0, n2]])

                t1v = v3(tb1, 0, 512, 32, 16, 16, 1)
                t2v = v3(tb2, 0, 512, 32, 16, 16, 1)
                for d_ in range(2):
                    s.tt(t1v, prm(bprm, 0), cf(cre, d_), ALU.mult)
                    s.tt(t2v, prm(bprm, 1), cf(cim, d_), ALU.mult)
                    s.tt(bb(0, d_), t1v, t2v, ALU.subtract)
                    s.tt(t1v, prm(bprm, 1), cf(cre, d_), ALU.mult)
                    s.tt(t2v, prm(bprm, 0), cf(cim, d_), ALU.mult)
                    s.tt(bb(1, d_), t1v, t2v, ALU.add)
                LA = cx.sb(es, "LA", [128, 2, 32, 2])
                LB = cx.sb(es, "LB", [128, 2, 32, 2])
                for ri in range(2):
                    s.copy(LA.view(ri, [[128, 128], [64, 2], [2, 32]]), pw(0, 8))
                s.ts(LB.view(0, [[128, 128], [64, 2], [2, 32]]), pw(1, 8), -1.0, None, ALU.mult)
                s.copy(LB.view(1, [[128, 128], [64, 2], [2, 32]]), pw(1, 8))
                zt_ = cx.sb(es, "zt_", [16, 16, 112])
                s.memset(zt_.full(), 0.0)
                s.flush()

                if L1S == 'b':
                    return nc
                for b in range(4 if L1S not in ('c1', 'd1', 'e1', 'f1', 'g1') else 1):
                    g0 = 8 * b
                    with ExitStack() as bs_:
                        CAB = cx.sb(bs_, "CAB", [128, 2, 2, 8 * 144])
                        WST = cx.sb(bs_, "WST", [128, 8, 2, 2, 2, 64])
                        TF = cx.sb(bs_, "TF", [128, 16, 128])
                        TB = cx.sb(bs_, "TB", [128, 16, 128])

                        with ExitStack() as tmp:
                            WT = cx.sb(tmp, "WT", [128, 2, 2, 8 * 128])
                            KSB = cx.sb(tmp, "KSB", [16, 2, 16, 128])
                            c1 = cx.sb(tmp, "c1", [128, 128])
                            c2 = cx.sb(tmp, "c2", [128, 128])
                            c1v = v3(c1, 0, 128, 8, 16, 16, 1)
                            c2v = v3(c2, 0, 128, 8, 16, 16, 1)
                            for d_ in range(2):
                                for idx in range(9):
                                    p_ = idx if d_ == 0 else 8 - idx
                                    pr = PW.view((0 * 9 + p_) * 64 + d_ * 32 + g0, [[1152, 128], [1, 8], [0, 16]])
                                    pi_ = PW.view((1 * 9 + p_) * 64 + d_ * 32 + g0, [[1152, 128], [1, 8], [0, 16]])
                                    o_re = CAB.view((0 * 2 + d_) * 1152 + idx * 16, [[4608, 128], [144, 8], [1, 16]])
                                    o_im = CAB.view((1 * 2 + d_) * 1152 + idx * 16, [[4608, 128], [144, 8], [1, 16]])
                                    s.tt(c1v, prm(cprm, 0, g0, 8), pr, ALU.mult)
                                    s.tt(c2v, prm(cprm, 1, g0, 8), pi_, ALU.mult)
                                    s.tt(o_re, c1v, c2v, ALU.subtract)
                                    s.tt(c1v, prm(cprm, 0, g0, 8), pi_, ALU.mult)
                                    s.tt(c2v, prm(cprm, 1, g0, 8), pr, ALU.mult)
                                    s.stt(o_im, c1v, -1.0, c2v, ALU.mult, ALU.subtract)
                                for ss in range(8):
                                    p_ = 7 - ss if d_ == 0 else ss
                                    pr = PW.view((0 * 9 + p_) * 64 + d_ * 32 + g0, [[1152, 128], [1, 8], [0, 16]])
                                    pi_ = PW.view((1 * 9 + p_) * 64 + d_ * 32 + g0, [[1152, 128], [1, 8], [0, 16]])
                                    o_re = WT.view((d_ * 2 + 0) * 1024 + ss * 16, [[4096, 128], [128, 8], [1, 16]])
                                    o_im = WT.view((d_ * 2 + 1) * 1024 + ss * 16, [[4096, 128], [128, 8], [1, 16]])
                                    s.tt(c1v, bb(0, d_, g0, 8), pr, ALU.mult)
                                    s.tt(c2v, bb(1, d_, g0, 8), pi_, ALU.mult)
                                    s.tt(o_re, c1v, c2v, ALU.subtract)
                                    s.tt(c1v, bb(1, d_, g0, 8), pr, ALU.mult)
                                    s.tt(c2v, bb(0, d_, g0, 8), pi_, ALU.mult)
                                    s.tt(o_im, c1v, c2v, ALU.add)
                            for gh in range(2):
                                p0 = gh * 64
                                for gq in range(8):
                                    bk = nbank()
                                    for d_ in range(2):
                                        for ri in range(2):
                                            sl = d_ * 2 + ri
                                            s.transpose(bk[:, sl * 64:(sl + 1) * 64],
                                                        WT.view(p0 * 4096 + (d_ * 2 + ri) * 1024 + gq * 128, [[4096, 64], [1, 128]]),
                                                        cst[p0:p0 + 64, 0, p0:p0 + 64])
                                    s.copy(WST.view(((gq * 2 + gh) * 4) * 64, [[4096, 128], [1, 256]]), bk[:, 0:256], eng="act")
                                for d_ in range(2):
                                    for gqq in range(2):
                                        bk = nbank()
                                        for q4 in range(4):
                                            gq = gqq * 4 + q4
                                            i0 = 0 if d_ == 0 else 1
                                            s.mm(bk[0:16, q4 * 128:(q4 + 1) * 128],
                                                 [(BB.view(p0 * 2048 + (0 * 2 + d_) * 512 + (g0 + gq) * 16, [[2048, 64], [1, 16]]),
                                                   CAB.view(p0 * 4608 + (0 * 2 + d_) * 1152 + gq * 144 + i0 * 16, [[4608, 64], [1, 128]])),
                                                  (BB.view(p0 * 2048 + (1 * 2 + d_) * 512 + (g0 + gq) * 16, [[2048, 64], [1, 16]]),
                                                   CAB.view(p0 * 4608 + (1 * 2 + d_) * 1152 + gq * 144 + i0 * 16, [[4608, 64], [1, 128]]))])
                                        s.copy(KSB.view(d_ * 2048 + (2 * gqq * 4 + gh) * 128, [[4096, 16], [256, 4], [1, 128]]),
                                               bk.view(0, [[512, 16], [128, 4], [1, 128]]), eng="act")
                            gbase = 16 * b
                            s.dma(KFP.view(gbase * 3840 + 7 * 16, [[240, 16], [3840, 16], [1, 128]]), KSB[:, 0, :, :])
                            s.dma(KBR.view(gbase * 3840, [[240, 16], [3840, 16], [1, 128]]), KSB[:, 1, :, :])
                            s.dma(KFP.view(gbase * 3840, [[240, 16], [3840, 16], [1, 112]]), zt_.full())
                            s.dma(KBR.view(gbase * 3840 + 128, [[240, 16], [3840, 16], [1, 112]]), zt_.full())
                            for ss in range(8):
                                s.dma(TF[ss * 16:(ss + 1) * 16, :, :], KFP.view(gbase * 3840 + (7 - ss) * 16, [[240, 16], [3840, 16], [1, 128]]))
                                s.dma(TB[ss * 16:(ss + 1) * 16, :, :], KBR.view(gbase * 3840 + (7 - ss) * 16, [[240, 16], [3840, 16], [1, 128]]))
                            s.flush()

                        if L1S in ('c', 'c1'):
                            continue
                        u8b = cx.sb(bs_, "u8b", [128, 8, 256])
                        u8g = cx.sb(bs_, "u8g", [128, 16, 128])
                        U8T = cx.sb(bs_, "U8T", [128, 16, 288])
                        NCOL = 326
                        PS = 16 * NCOL
                        SSD = [cx.sb(bs_, "SS%d" % i, [128, 8, 2, NCOL]) for i in range(2)]
                        CAR = [cx.sb(bs_, "CAR%d" % i, [128, 7, 8, 2]) for i in range(2)]
                        A36 = [cx.sb(bs_, "A36_%d" % i, [128, 8, 2]) for i in range(2)]
                        B36 = [cx.sb(bs_, "B36_%d" % i, [128, 8, 2]) for i in range(2)]
                        y8b = cx.sb(bs_, "y8b", [128, 8, 256])
                        ysb = cx.sb(bs_, "ysb", [128, 512])
                        TT1 = [cx.sb(bs_, "TT1_%d" % i, [128, 9, 8, 2]) for i in range(2)]
                        TT2 = [cx.sb(bs_, "TT2_%d" % i, [128, 9, 8, 2]) for i in range(2)]
                        for (j0, nj) in ((0, 32), (32, 128), (160, 128)):
                            s.dma(u8b[0:nj, :, :], U.view(8 * j0 * 1024 + 256 * b, [[8192, nj], [1024, 8], [1, 256]]))
                            s.copy(u8g.view(0, [[2048, nj], [128, 16], [16, 8], [1, 16]]),
                                   u8b.view(0, [[2048, nj], [16, 16], [256, 8], [1, 16]]), eng="act")
                            for gq4 in range(4):
                                bk = nbank()
                                for q4 in range(4):
                                    gi = gq4 * 4 + q4
                                    s.transpose(bk[:, q4 * 128:q4 * 128 + nj],
                                                u8g.view(128 * gi, [[2048, nj], [1, 128]]), cst[0:nj, 0, 0:nj])
                                s.copy(U8T.view(gq4 * 4 * 288 + j0, [[16 * 288, 128], [288, 4], [1, nj]]),
                                       bk.view(0, [[512, 128], [128, 4], [1, nj]]), eng="act")
                        if L1S in ('d', 'd1'):
                            s.flush()
                            continue
                        s.memset(SSD[0].view(0, [[PS, 128], [NCOL, 16], [1, 1]]), 0.0)
                        s.memset(SSD[0].view(289, [[PS, 128], [NCOL, 16], [1, 37]]), 0.0)
                        s.memset(SSD[1].view(288, [[PS, 128], [NCOL, 16], [1, 38]]), 0.0)
                        s.memset(SSD[0].view(289, [[PS, 128], [2 * NCOL, 8], [1, 1]]), 1.0)
                        s.memset(SSD[1].view(323, [[PS, 128], [2 * NCOL, 8], [1, 1]]), 1.0)
                        for gq in range(8):
                            for gh in range(2):
                                gi = 2 * gq + gh
                                p0 = gh * 64
                                for d_ in range(2):
                                    for ri in range(2):
                                        bk = nbank()
                                        s.mm(bk[p0:p0 + 64, 0:288],
                                             [(WST.view((((gq * 2 + gh) * 2 + d_) * 2 + ri) * 64, [[4096, 128], [1, 64]]),
                                               U8T[:, gi, :])])
                                        so = p0 * PS + (gq * 2 + ri) * NCOL
                                        if d_ == 0:
                                            s.copy(SSD[0].view(so + 1, [[PS, 64], [1, 288]]), bk[p0:p0 + 64, 0:288], eng="act")
                                        else:
                                            s.copy(SSD[1].view(so + 256, [[PS, 64], [1, 32]]), bk[p0:p0 + 64, 0:32], eng="act")
                                            s.copy(SSD[1].view(so, [[PS, 64], [1, 256]]), bk[p0:p0 + 64, 32:288], eng="act")
                        if L1S in ('e', 'e1'):
                            s.flush()
                            continue
                        DS = 8 * 2 * 289
                        RI, GQ = NCOL, 2 * NCOL

                        def cplx_step(items):
                            for (pv, psw, cv, ca, cb_, t1_, t2_) in items:
                                s.tt(t1_, pv, ca, ALU.mult)
                                s.tt(t2_, psw, cb_, ALU.mult)
                            for (pv, psw, cv, ca, cb_, t1_, t2_) in items:
                                s.tt(t1_, t1_, t2_, ALU.add)
                            for (pv, psw, cv, ca, cb_, t1_, t2_) in items:
                                if cv is not None:
                                    s.tt(cv, cv, t1_, ALU.add)

                        def segv(SS, col, nseg):
                            return (SS.view(col, [[PS, 128], [36, nseg], [GQ, 8], [RI, 2]]),
                                    SS.view(col + RI, [[PS, 128], [36, nseg], [GQ, 8], [-RI, 2]]))

                        def coef(buf, d_, nseg):
                            return buf.view(d_ * 64 + g0 * 2, [[128, 128], [0, nseg], [2, 8], [1, 2]])

                        for k in range(1, 36):
                            items = []
                            for d_ in range(2):
                                pc = k if d_ == 0 else 36 - k
                                cc = k + 1 if d_ == 0 else 35 - k
                                pv, psw = segv(SSD[d_], pc, 9)
                                cv, _ = segv(SSD[d_], cc, 9)
                                items.append((pv, psw, cv, coef(LA, d_, 9), coef(LB, d_, 9), TT1[d_].full(), TT2[d_].full()))
                            cplx_step(items)
                        items = []
                        for d_ in range(2):
                            c35 = 324 if d_ == 0 else 288
                            pv, psw = segv(SSD[d_], c35, 1)
                            items.append((pv, psw, None, coef(LA, d_, 1), coef(LB, d_, 1),
                                          TT1[d_].view(0, [[144, 128], [16, 1], [2, 8], [1, 2]]),
                                          TT2[d_].view(0, [[144, 128], [16, 1], [2, 8], [1, 2]])))
                        cplx_step(items)
                        for d_ in range(2):
                            l36re = TT1[d_].view(0, [[144, 128], [2, 8], [0, 2]])
                            s.copy(A36[d_].full(), l36re)
                            s.ts(B36[d_][:, :, 0:1], TT1[d_].view(1, [[144, 128], [2, 8], [1, 1]]), -1.0, None, ALU.mult)
                            s.copy(B36[d_][:, :, 1:2], TT1[d_].view(1, [[144, 128], [2, 8], [1, 1]]))
                        for step in range(1, 8):
                            items = []
                            for d_ in range(2):
                                if d_ == 0:
                                    m = step
                                    cc, pc = 36 * m + 36, 36 * m
                                else:
                                    m = 7 - step
                                    cc, pc = 36 * m, 36 * m + 36
                                pv, psw = segv(SSD[d_], pc, 1)
                                cv, _ = segv(SSD[d_], cc, 1)
                                items.append((pv, psw, cv,
                                              A36[d_].view(0, [[16, 128], [0, 1], [2, 8], [1, 2]]),
                                              B36[d_].view(0, [[16, 128], [0, 1], [2, 8], [1, 2]]),
                                              TT1[d_].view(0, [[144, 128], [16, 1], [2, 8], [1, 2]]),
                                              TT2[d_].view(0, [[144, 128], [16, 1], [2, 8], [1, 2]])))
                            cplx_step(items)
                        items = []
                        for d_ in range(2):
                            pv, psw = segv(SSD[d_], 36, 7)
                            items.append((pv, psw, None, coef(LA, d_, 7), coef(LB, d_, 7),
                                          CAR[d_].full(), TT2[d_].view(0, [[144, 128], [16, 7], [2, 8], [1, 2]])))
                        cplx_step(items)
                        for d_ in range(2):
                            SS = SSD[d_]
                            sb0 = 37 if d_ == 0 else 1

                            def sview(ri):
                                return SS.view(sb0 + ri * RI, [[PS, 128], [36, 7], [GQ, 8], [1, 35]])

                            def tview(ri):
                                return SS.view(289 + ri * RI, [[PS, 128], [0, 7], [GQ, 8], [1, 35]])

                            def cview(ri):
                                return CAR[d_].view(ri, [[112, 128], [16, 7], [2, 8], [0, 35]])

                            w1 = (u8g if d_ == 0 else u8b).view(0, [[2048, 128], [280, 7], [35, 8], [1, 35]])
                            w2 = y8b.view(0, [[2048, 128], [280, 7], [35, 8], [1, 35]])
                            s.tt(w1, tview(0), cview(0), ALU.mult)
                            s.tt(w2, tview(1), cview(1), ALU.mult)
                            s.tt(w1, w1, w2, ALU.subtract)
                            s.tt(sview(0), sview(0), w1, ALU.add)
                            s.tt(w1, tview(0), cview(1), ALU.mult)
                            s.tt(w2, tview(1), cview(0), ALU.mult)
                            s.tt(w1, w1, w2, ALU.add)
                            s.tt(sview(1), sview(1), w1, ALU.add)
                        if L1S in ('f', 'f1'):
                            s.flush()
                            continue
                        for tt_ in range(2):
                            j0 = 32 + 128 * tt_
                            m0 = 128 * tt_
                            for gh in range(2):
                                p0 = gh * 64
                                for gqq in range(2):
                                    bx = nbank()
                                    by = nbank()
                                    for q4 in range(4):
                                        gq = gqq * 4 + q4
                                        gi = 2 * gq + gh
                                        s.mm(bx[:, q4 * 128:(q4 + 1) * 128],
                                             [(U8T[:, gi, j0:j0 + 128], TF[:, gi, :]), (U8T[:, gi, j0:j0 + 128], TB[:, gi, :])])
                                        pairs = []
                                        for d_ in range(2):
                                            c0 = j0 if d_ == 0 else m0 + 1
                                            i0 = 1 if d_ == 0 else 0
                                            for ri in range(2):
                                                so = p0 * PS + (gq * 2 + ri) * NCOL + c0
                                                pairs.append((SSD[d_].view(so, [[PS, 64], [1, 128]]),
                                                              CAB.view(p0 * 4608 + (ri * 2 + d_) * 1152 + gq * 144 + i0 * 16, [[4608, 64], [1, 128]])))
                                        s.mm(by[:, q4 * 128:(q4 + 1) * 128], pairs)
                                    s.copy(ysb.full(), by.full(), eng="act")
                                    s.tt(y8b.view(32 * gqq * 4 + 16 * gh, [[2048, 128], [32, 4], [256, 8], [1, 16]]),
                                         bx.view(0, [[512, 128], [128, 4], [16, 8], [1, 16]]),
                                         ysb.view(0, [[512, 128], [128, 4], [16, 8], [1, 16]]), ALU.add)
                            s.dma(YTOK.view((CTX + 8 * m0) * 1024 + 256 * b, [[8192, 128], [1024, 8], [1, 256]]), y8b.full())
                        s.flush()

            if L1S in ('g', 'g1'):
                return nc
            with ExitStack() as es:
                gw = [cx.sb(es, "gw%d" % k, [128, D]) for k in range(8)]
                ow = [cx.sb(es, "ow%d" % k, [128, D]) for k in range(8)]
                dskb = cx.sb(es, "dskb", [128, D])
                glbb = cx.sb(es, "glbb", [128, D])
                fnwb = cx.sb(es, "fnwb", [128, D])
                kg = cx.sb(es, "kg", [128, 1])
                s.memset(kg.full(), 2.0 * math.sqrt(2.0 / math.pi))
                for k in range(8):
                    s.dma(gw[k].full(), o_glu_w[k * 128:(k + 1) * 128, :])
                    s.dma(ow[k].full(), o_w_out[k * 128:(k + 1) * 128, :])
                s.dma(dskb.full(), o_d_skip.view(0, [[0, 128], [1, D]]))
                s.dma(glbb.full(), o_glu_b.view(0, [[0, 128], [1, D]]))
                s.dma(fnwb.full(), final_norm_w.view(0, [[0, 128], [1, D]]))
                ya = [cx.sb(es, "ya%d" % i, [128, D]) for i in range(2)]
                ua = [cx.sb(es, "ua%d" % i, [128, D]) for i in range(2)]
                sga = [cx.sb(es, "sga%d" % i, [128, D]) for i in range(2)]
                xa = [cx.sb(es, "xa%d" % i, [128, D]) for i in range(2)]
                w1 = cx.sb(es, "w1", [128, D])
                w2 = cx.sb(es, "w2", [128, D])
                w3 = cx.sb(es, "w3", [128, D])
                tT = cx.sb(es, "tT", [128, 8, 128])
                st = cx.sb(es, "st10", [128, 4])

                def transp8(src):
                    for half in range(2):
                        bk = nbank()
                        for kk in range(4):
                            k = half * 4 + kk
                            s.transpose(bk[:, kk * 128:(kk + 1) * 128], src[:, k * 128:(k + 1) * 128], ident)
                        s.copy(tT[:, half * 4:(half + 1) * 4, :], bk.view(0, [[512, 128], [128, 4], [1, 128]]), eng="act")

                TAILN = int(os.environ.get('TAILN', NT))
                for i in range(2, TAILN):
                    y_, u_, g_, x_ = ya[i % 2], ua[i % 2], sga[i % 2], xa[i % 2]
                    s.dma(y_.full(), YTOK[i * 128:(i + 1) * 128, :])
                    s.dma(u_.full(), U[i * 128:(i + 1) * 128, :])
                    s.dma(g_.full(), SG1[i * 128:(i + 1) * 128, :])
                    s.dma(x_.full(), X1[i * 128:(i + 1) * 128, :])
                    s.tt(w1.full(), u_.full(), dskb.full(), ALU.mult)
                    s.tt(y_.full(), y_.full(), w1.full(), ALU.add)
                    s.tt(w1.full(), y_.full(), y_.full(), ALU.mult)
                    s.ts(w1.full(), w1.full(), 0.044715, 1.0, ALU.mult, ALU.add)
                    s.tt(w1.full(), w1.full(), y_.full(), ALU.mult)
                    s.act(w1.full(), w1.full(), AF.Sigmoid, scale=kg[:, 0:1])
                    s.tt(w2.full(), y_.full(), w1.full(), ALU.mult)
                    transp8(w2)
                    for half in range(2):
                        bk = nbank()
                        s.mm(bk.full(), [(tT[:, k, :], gw[k][:, half * 512:(half + 1) * 512]) for k in range(8)])
                        s.tt(w1[:, half * 512:(half + 1) * 512], bk.full(), glbb[:, half * 512:(half + 1) * 512], ALU.add)
                    s.act(w1.full(), w1.full(), AF.Sigmoid)
                    s.tt(w2.full(), w2.full(), w1.full(), ALU.mult)
                    s.tt(w2.full(), w2.full(), g_.full(), ALU.mult)
                    transp8(w2)
                    for half in range(2):
                        bk = nbank()
                        s.mm(bk.full(), [(tT[:, k, :], ow[k][:, half * 512:(half + 1) * 512]) for k in range(8)])
                        s.tt(w1[:, half * 512:(half + 1) * 512], bk.full(), gate_bc[1][0][:, half * 512:(half + 1) * 512], ALU.mult)
                    s.tt(w3.full(), w1.full(), x_.full(), ALU.add)
                    s.act(w1.full(), w3.full(), AF.Square, accum=st[:, 0:1])
                    s.ts(st[:, 1:2], st[:, 0:1], 1.0 / D, EPS, ALU.mult, ALU.add)
                    s.act(st[:, 2:3], st[:, 1:2], AF.Sqrt)
                    s.recip(st[:, 3:4], st[:, 2:3])
                    s.ts(w3.full(), w3.full(), st[:, 3:4], None, ALU.mult)
                    s.tt(w2.full(), w3.full(), fnwb.full(), ALU.mult)
                    s.dma(out_t[(i - 2) * 128:(i - 1) * 128, :], w2.full())
                s.flush()

    return nc


def _consts():
    c = np.zeros((128, 6, 512), np.float32)
    j = np.arange(128)[:, None]
    l = np.arange(128)[None, :]
    c[:, 0, :128] = np.eye(128, dtype=np.float32)
    c[:, 1, :128] = (j <= l)
    c[:, 2, :128] = (j >= l)
    c[:, 3, :] = 1.0
    nf = np.where(l < j, -30000.0, 0.0).astype(np.float32)
    nb = np.where(l > j, -30000.0, 0.0).astype(np.float32)
    c[:, 4, :] = np.tile(nf, (1, 4))
    c[:, 5, :] = np.tile(nb, (1, 4))
    return c


def _rope_tables():
    rows = L // 64
    row = np.repeat(np.arange(rows, dtype=np.float32), 64)
    col = np.tile(np.arange(64, dtype=np.float32), rows)
    n_freq = 16
    inv = (np.float32(10000.0) ** (-np.arange(n_freq, dtype=np.float32) / n_freq)).astype(np.float32)
    ang = np.concatenate([row[:, None] * inv, col[:, None] * inv], axis=-1).astype(np.float32)
    cos = np.cos(ang).astype(np.float32)
    sin = np.sin(ang).astype(np.float32)
    cosT = np.zeros((128, L), np.float32)
    sinT = np.zeros((128, L), np.float32)
    for h2 in range(2):
        for half in range(2):
            p0 = h2 * 64 + half * 32
            cosT[p0:p0 + 32] = cos.T
            sinT[p0:p0 + 32] = (-sin.T if half == 0 else sin.T)
    return np.stack([cosT, sinT], axis=1)


def _vecT(v, nchunk):
    return np.ascontiguousarray(np.asarray(v, np.float32).reshape(nchunk, 128).T)


def prep_inputs(b, inp):
    f = lambda a: np.ascontiguousarray(np.asarray(a, np.float32))
    m = {}
    m["xin"] = f(np.concatenate([inp["ctx"][b], inp["x"][b]], axis=0))
    cv = np.stack([inp["c"][b], inp["c_ctx"]], axis=0)
    m["cvecT"] = f(cv.reshape(2, 8, 128).transpose(2, 0, 1))
    m["consts"] = _consts()
    m["rope"] = _rope_tables()
    m["e_ada_w"] = f(inp["e_ada_w"][0])
    m["e_ada_b"] = f(inp["e_ada_b"][0]).reshape(1, -1)
    m["e_norm_wT"] = _vecT(inp["e_norm_w"][0], 8)
    w = f(inp["e_w_in"][0])
    q = w[:, OFF_Q:OFF_Q + 1024].reshape(D, 16, 2, 32)
    qs = q[:, :, ::-1, :].reshape(D, 1024)
    k = w[:, OFF_KV:OFF_KV + 256].reshape(D, 4, 64)
    kr = np.concatenate([k, k], axis=2).reshape(D, 512)
    ks = k.reshape(D, 4, 2, 32)[:, :, ::-1, :].reshape(D, 4, 64)
    ksr = np.concatenate([ks, ks], axis=2).reshape(D, 512)
    m["e_w_in"] = f(np.concatenate([w, qs, kr, ksr], axis=1))
    cw = f(inp["e_conv_w"][0])
    m["e_conv_wT"] = f(cw.reshape(5, 12, 128).transpose(2, 1, 0))
    m["e_conv_bT"] = _vecT(inp["e_conv_b"][0], 12)
    m["e_dt_bias"] = f(inp["e_dt_bias"][0]).reshape(1, 32)
    m["e_a_log"] = f(inp["e_a_log"][0]).reshape(1, 32)
    m["e_d_skip"] = f(inp["e_d_skip"][0]).reshape(1, 16)
    m["e_ssd_norm_wT"] = _vecT(inp["e_ssd_norm_w"][0], 8)
    sk = f(inp["e_sink"][0]).reshape(8, 2)
    m["e_sink"] = f(np.repeat(sk.T[:, None, :], 64, axis=1).reshape(128, 8))
    m["e_w_out"] = f(inp["e_w_out"][0])
    m["o_ada_w"] = f(inp["o_ada_w"][0])
    m["o_ada_b"] = f(inp["o_ada_b"][0]).reshape(1, -1)
    m["o_norm_wT"] = _vecT(inp["o_norm_w"][0], 8)
    m["o_w_in"] = f(inp["o_w_in"][0])

    def gl(a):
        a = np.asarray(a, np.float32)
        rest = a.shape[2:]
        a = a.reshape((32, 2, 64) + rest)
        a = np.moveaxis(a, 0, 2)
        return a.reshape((128, 32) + rest)

    lam = np.zeros((128, 2, 3, 32), np.float32)
    for d_ in range(2):
        lam[:, d_, 0] = gl(inp["o_lam_re"][0][d_])
        lam[:, d_, 1] = gl(inp["o_lam_im"][0][d_])
        lam[:, d_, 2] = gl(np.repeat(np.asarray(inp["o_log_step"][0][d_])[:, None], 64, axis=1))
    m["s5_lam"] = f(lam)
    m["s5_b"] = f(np.stack([gl(inp["o_b_re"][0]), gl(inp["o_b_im"][0])], axis=1))
    cr = np.asarray(inp["o_c_re"][0]).transpose(0, 2, 1)
    ci = np.asarray(inp["o_c_im"][0]).transpose(0, 2, 1)
    m["s5_c"] = f(np.stack([gl(cr), gl(ci)], axis=1))
    m["o_d_skip"] = f(inp["o_d_skip"][0]).reshape(1, -1)
    m["o_glu_w"] = f(inp["o_glu_w"][0])
    m["o_glu_b"] = f(inp["o_glu_b"][0]).reshape(1, -1)
    m["o_w_out"] = f(inp["o_w_out"][0])
    m["final_norm_w"] = f(inp["final_norm_w"]).reshape(1, -1)
    return m


def kernel(**inputs):
    nc = build_program()
    in_maps = [prep_inputs(b, inputs) for b in range(8)]
    res = run_bass_kernel_spmd(nc, in_maps, core_ids=list(range(8)))
    return np.stack([r["out"] for r in res.results], axis=0)
```

```python
import math
import os
from contextlib import ExitStack

import numpy as np
import concourse.bass as bass
import concourse.mybir as mybir
from concourse.bass_utils import run_bass_kernel_spmd

F32 = mybir.dt.float32
BF16 = mybir.dt.bfloat16
AF = mybir.ActivationFunctionType
ALU = mybir.AluOpType

D = 1024
T = 2304
NT = 18
CTX = 256
L = 2048
EPS = 1e-6
TG = [(0, 256), (256, 512), (768, 512), (1280, 512), (1792, 512)]

SAME_ENGINE_SYNC = os.environ.get('SES', '1') == '1'
SEM_EPOCH = 30000


class V:
    __slots__ = ("buf", "ap")

    def __init__(self, buf, ap):
        self.buf = buf
        self.ap = ap


class Buf:
    def __init__(self, name, h):
        self.name = name
        self.h = h
        self.last_w = None
        self.readers = []

    def __getitem__(self, idx):
        return V(self, self.h[idx])

    def full(self):
        return V(self, self.h.ap())

    def view(self, offset, pattern):
        return V(self, bass.AP(self.h, offset, [list(p) for p in pattern]))


class Sched:
    ENG = ("pe", "act", "dve", "pool", "sp")

    def __init__(self, nc):
        self.nc = nc
        self.prog = {e: [] for e in self.ENG}
        self.sem = {}
        self.cnt = {}
        self.semid = 0
        self.known = {e: {} for e in self.ENG}
        for e in ("pe", "act", "dve", "pool"):
            self._new_engine_sem(e)
        self.nds = 8
        self.dsem = {}
        self.duse = {}
        self.dcnt = {}
        for q in ("sp", "pool"):
            self.dsem[q] = []
            self.duse[q] = []
            for i in range(self.nds):
                key = "d_%s_%d" % (q, i)
                self.dsem[q].append((nc.alloc_semaphore(key), key))
                self.duse[q].append(0)
            self.dcnt[q] = 0
        self.n_ops = 0

    def _new_engine_sem(self, e):
        self.semid += 1
        key = "s_%s_%d" % (e, self.semid)
        self.sem[e] = (self.nc.alloc_semaphore(key), key)
        self.cnt[e] = 0

    def _deps(self, reads, writes):
        deps = {}

        def add(tok):
            if tok is None:
                return
            h, key, val = tok
            if key not in deps or deps[key][1] < val:
                deps[key] = (h, val)

        for r in reads:
            add(r.buf.last_w)
        for w in writes:
            add(w.buf.last_w)
            for t in w.buf.readers:
                add(t)
        return deps

    def _emit_waits(self, eng, deps, own_key=None):
        kn = self.known[eng]
        for key, (h, val) in deps.items():
            if key == own_key and not SAME_ENGINE_SYNC:
                continue
            if kn.get(key, 0) >= val:
                continue
            kn[key] = val
            self.prog[eng].append(("wait", h, val))

    def _update(self, tok, reads, writes):
        for w in writes:
            w.buf.last_w = tok
            w.buf.readers = []
        for r in reads:
            if r.buf.last_w is not tok:
                r.buf.readers.append(tok)

    def op(self, eng, fn, reads=(), writes=()):
        reads = [r for r in reads if r is not None]
        writes = list(writes)
        if self.cnt[eng] >= SEM_EPOCH:
            self._new_engine_sem(eng)
        h, key = self.sem[eng]
        own = None if eng == "pe" else key
        deps = self._deps(reads, writes)
        if eng == "pe":
            deps.pop(key, None)
        self._emit_waits(eng, deps, own_key=own)
        self.cnt[eng] += 1
        self.prog[eng].append(("op", fn, h, 1))
        tok = (h, key, self.cnt[eng])
        self._update(tok, reads, writes)
        self.n_ops += 1
        return tok

    def dma(self, out, in_, q="sp", **kw):
        deps = self._deps([in_], [out])
        self._emit_waits(q, deps)
        k = self.dcnt[q] % self.nds
        self.dcnt[q] += 1
        h, key = self.dsem[q][k]
        prev = 16 * self.duse[q][k]
        if prev > 0 and self.known[q].get(key, 0) < prev:
            self.known[q][key] = prev
            self.prog[q].append(("wait", h, prev))
        self.duse[q][k] += 1
        val = 16 * self.duse[q][k]
        o_ap, i_ap = out.ap, in_.ap
        self.prog[q].append(("op", lambda e: e.dma_start(out=o_ap, in_=i_ap, **kw), h, 16))
        tok = (h, key, val)
        self._update(tok, [in_], [out])
        self.n_ops += 1
        return tok

    def finish_dmas(self):
        for q in ("sp", "pool"):
            for k in range(self.nds):
                h, key = self.dsem[q][k]
                val = 16 * self.duse[q][k]
                if val > 0 and self.known[q].get(key, 0) < val:
                    self.known[q][key] = val
                    self.prog[q].append(("wait", h, val))

    def flush(self, name=None):
        self.finish_dmas()
        nc = self.nc
        prog = self.prog
        self.prog = {e: [] for e in self.ENG}

        def run(items, e):
            for it in items:
                if it[0] == "wait":
                    e.wait_ge(it[1], it[2])
                else:
                    inst = it[1](e)
                    inst.then_inc(it[2], it[3])

        with nc.Block() as block:
            if prog["sp"]:
                @block.sync
                def _(e):
                    run(prog["sp"], e)
            if prog["act"]:
                @block.scalar
                def _(e):
                    run(prog["act"], e)
            if prog["dve"]:
                @block.vector
                def _(e):
                    run(prog["dve"], e)
            if prog["pool"]:
                @block.gpsimd
                def _(e):
                    run(prog["pool"], e)
            if prog["pe"]:
                @block.tensor
                def _(e):
                    run(prog["pe"], e)

    def mm(self, out, pairs):
        n = len(pairs)

        def fn(e):
            inst = None
            for i, (l, r) in enumerate(pairs):
                inst = e.matmul(out.ap, l.ap, r.ap, start=(i == 0), stop=(i == n - 1))
            return inst

        self.op("pe", fn, reads=[p[0] for p in pairs] + [p[1] for p in pairs], writes=[out])

    def transpose(self, out, in_, ident):
        self.op("pe", lambda e: e.transpose(out.ap, in_.ap, ident.ap), reads=[in_, ident], writes=[out])

    def act(self, out, in_, func, bias=None, scale=None, accum=None):
        kw = {}
        reads = [in_]
        writes = [out]
        if bias is not None:
            if isinstance(bias, V):
                kw["bias"] = bias.ap
                reads.append(bias)
            else:
                kw["bias"] = bias
        if scale is not None:
            if isinstance(scale, V):
                kw["scale"] = scale.ap
                reads.append(scale)
            else:
                kw["scale"] = scale
        if accum is not None:
            kw["accum_out"] = accum.ap
            writes.append(accum)
        self.op("act", lambda e: e.activation(out.ap, in_.ap, func, **kw), reads=reads, writes=writes)

    def ts(self, out, in0, s1, s2, op0, op1=None, eng="dve"):
        reads = [in0]
        a1 = s1
        a2 = s2
        if isinstance(s1, V):
            reads.append(s1)
            a1 = s1.ap
        if isinstance(s2, V):
            reads.append(s2)
            a2 = s2.ap
        if op1 is None:
            self.op(eng, lambda e: e.tensor_scalar(out.ap, in0.ap, a1, a2, op0), reads=reads, writes=[out])
        else:
            self.op(eng, lambda e: e.tensor_scalar(out.ap, in0.ap, a1, a2, op0, op1), reads=reads, writes=[out])

    def tt(self, out, in0, in1, op, eng="dve"):
        self.op(eng, lambda e: e.tensor_tensor(out.ap, in0.ap, in1.ap, op), reads=[in0, in1], writes=[out])

    def stt(self, out, in0, scalar, in1, op0, op1):
        reads = [in0, in1]
        sc = scalar
        if isinstance(scalar, V):
            reads.append(scalar)
            sc = scalar.ap
        self.op("dve", lambda e: e.scalar_tensor_tensor(out.ap, in0.ap, sc, in1.ap, op0, op1),
                reads=reads, writes=[out])

    def copy(self, out, in_, eng="dve"):
        if eng == "act":
            self.op("act", lambda e: e.copy(out.ap, in_.ap), reads=[in_], writes=[out])
        else:
            self.op(eng, lambda e: e.tensor_copy(out.ap, in_.ap), reads=[in_], writes=[out])

    def recip(self, out, in_):
        self.op("dve", lambda e: e.reciprocal(out.ap, in_.ap), reads=[in_], writes=[out])

    def memset(self, out, val, eng="dve"):
        self.op(eng, lambda e: e.memset(out.ap, val), reads=[], writes=[out])


class Ctx:
    def __init__(self, nc, sched):
        self.nc = nc
        self.s = sched
        self.uid = 0

    def sb(self, es, name, shape, dtype=F32):
        self.uid += 1
        h = es.enter_context(self.nc.sbuf_tensor("%s_%d" % (name, self.uid), list(shape), dtype))
        return Buf(name, h)

    def ps(self, es, name, shape=(128, 512), dtype=F32):
        self.uid += 1
        h = es.enter_context(self.nc.psum_tensor("%s_%d" % (name, self.uid), list(shape), dtype))
        return Buf(name, h)

    def dram(self, name, shape, dtype=F32, kind="Internal"):
        h = self.nc.dram_tensor(name, list(shape), dtype, kind=kind)
        return Buf(name, h)


def bc_mid(v_buf, base_off, pstep, nparts, n_outer, outer_step, n_inner):
    return v_buf.view(base_off, [[pstep, nparts], [outer_step, n_outer], [0, n_inner]])


E_NCOL = 5152
OFF_Z = 0
OFF_XBC = 1024
OFF_DT = 2560
OFF_Q = 2592
OFF_KV = 3616
OFF_G = 4128
OFF_QS = 5152
OFF_KR = 6176
OFF_KSR = 6688
E_NCOL_EXT = 7200


ORDER = ["p1", "p2a", "p2b", "p2c", "p2d", "p2e", "p2f", "p2g", "p2h", "p3", "p4", "p5", "all"]


def build_program(debug=(), stop="all"):
    def go(tag):
        return ORDER.index(tag) <= ORDER.index(stop)
    nc = bass.Bass("TRN2", target_bir_lowering=False)
    s = Sched(nc)
    cx = Ctx(nc, s)
    dbg = set(debug)

    def din(name, shape):
        return Buf(name, nc.dram_tensor(name, list(shape), F32, kind="ExternalInput"))

    def dout(name, shape):
        return Buf(name, nc.dram_tensor(name, list(shape), F32, kind="ExternalOutput"))

    def scratch(name, shape, dtype=F32):
        if name in dbg:
            return dout(name, shape)
        return Buf(name, nc.dram_tensor(name, list(shape), dtype))

    xin = din("xin", [T, D])
    cvecT = din("cvecT", [128, 2, 8])
    consts = din("consts", [128, 6, 512])
    rope = din("rope", [128, 2, L])
    e_ada_w = din("e_ada_w", [D, 3 * D])
    e_ada_b = din("e_ada_b", [1, 3 * D])
    e_norm_wT = din("e_norm_wT", [128, 8])
    e_w_in = din("e_w_in", [D, E_NCOL_EXT])
    e_conv_wT = din("e_conv_wT", [128, 12, 5])
    e_conv_bT = din("e_conv_bT", [128, 12])
    e_dt_bias = din("e_dt_bias", [1, 32])
    e_a_log = din("e_a_log", [1, 32])
    e_d_skip = din("e_d_skip", [1, 16])
    e_ssd_norm_wT = din("e_ssd_norm_wT", [128, 8])
    e_sink = din("e_sink", [128, 8])
    e_w_out = din("e_w_out", [2 * D, D])
    o_ada_w = din("o_ada_w", [D, 3 * D])
    o_ada_b = din("o_ada_b", [1, 3 * D])
    o_norm_wT = din("o_norm_wT", [128, 8])
    o_w_in = din("o_w_in", [D, 2 * D])
    s5_lam = din("s5_lam", [128, 2, 3, 32])
    s5_b = din("s5_b", [128, 2, 32, 16])
    s5_c = din("s5_c", [128, 2, 32, 16])
    o_d_skip = din("o_d_skip", [1, D])
    o_glu_w = din("o_glu_w", [D, D])
    o_glu_b = din("o_glu_b", [1, D])
    o_w_out = din("o_w_out", [D, D])
    final_norm_w = din("final_norm_w", [1, D])
    out_t = dout("out", [L, D])

    XS = scratch("XS", [T, 1024])
    BTOK = scratch("BTOK", [T, 256], BF16)
    BT = scratch("BT", [2, 128, T], BF16)
    CT = scratch("CT", [2, 128, T], BF16)
    SZ = scratch("SZ", [T, 1024])
    QR = scratch("QR", [8, 128, L], BF16)
    QC = scratch("QC", [8, 128, CTX], BF16)
    KR = scratch("KR", [4, 128, L], BF16)
    KC = scratch("KC", [4, 128, CTX], BF16)
    VT = scratch("VT", [T, 256], BF16)
    SG = scratch("SG", [8, 128, T])
    YF = scratch("YF", [T, 1024])
    YT = scratch("YT", [16, 128, T], BF16)
    X1 = scratch("X1", [T, 1024])
    U = scratch("U", [T, 1024])
    SG1 = scratch("SG1", [T, 1024])
    YTOK = scratch("YTOK", [T, 1024])
    KFP = scratch("KFP", [64, 16, 15, 16])
    KBR = scratch("KBR", [64, 16, 15, 16])
    HT = scratch("HT", [8, 128, T]) if "HT" in dbg else None
    DTD = scratch("DTD", [T, 32]) if "DTD" in dbg else None
    MODD = scratch("MODD", [4, 128, 24]) if "MODD" in dbg else None

    with ExitStack() as top:
        banks = [cx.ps(top, "bank%d" % i) for i in range(8)]
        cst = cx.sb(top, "cst", [128, 6, 512])
        s.dma(cst.full(), consts.full())
        ident = cst[:, 0, 0:128]
        tri = cst[:, 1, 0:128]
        utri = cst[:, 2, 0:128]
        ones = cst[:, 3, 0:128]
        onesb_t = cx.sb(top, "onesb", [128, 128], BF16)
        s.memset(onesb_t.full(), 1.0)
        onesb = onesb_t.full()
        modT = [[cx.sb(top, "modT%d%d" % (l, w), [128, 24]) for w in range(2)] for l in range(2)]
        gate_bc = [[cx.sb(top, "gate%d%d" % (l, w), [128, 1024]) for w in range(2)] for l in range(2)]
        scs = cx.sb(top, "scs", [128, 2, 8])

        def adaln_phase(layer, ada_w, ada_b):
            with ExitStack() as es:
                aw = [cx.sb(es, "aw%d" % k, [128, 3 * D]) for k in range(8)]
                ab = cx.sb(es, "ab", [1, 3 * D])
                modrow = [cx.sb(es, "modrow%d" % w, [1, 3 * D]) for w in range(2)]
                if layer == 0:
                    cv = cx.sb(es, "cv", [128, 2, 8])
                    s.dma(cv.full(), cvecT.full())
                    s.act(scs.full(), cv.full(), AF.Silu)
                for k in range(8):
                    s.dma(aw[k].full(), ada_w[k * 128:(k + 1) * 128, :])
                s.dma(ab.full(), ada_b.full())
                bi = 0
                for w in range(2):
                    for fg in range(6):
                        bk = banks[bi % 8]
                        bi += 1
                        s.mm(bk[0:1, :], [(scs[:, w, k:k + 1], aw[k][:, fg * 512:(fg + 1) * 512]) for k in range(8)])
                        s.tt(modrow[w][0:1, fg * 512:(fg + 1) * 512], bk[0:1, :], ab[0:1, fg * 512:(fg + 1) * 512], ALU.add)
                for w in range(2):
                    bk = banks[bi % 8]
                    bi += 1
                    for fc in range(24):
                        s.mm(bk[:, 2 * fc:2 * fc + 2], [(modrow[w][0:1, fc * 128:(fc + 1) * 128], cst[0:1, 3, 0:2])])
                    s.copy(modT[layer][w].full(), bk.view(0, [[512, 128], [2, 24]]))
                    for hh in range(2):
                        bk2 = banks[bi % 8]
                        bi += 1
                        s.mm(bk2.full(), [(cst[0:1, 3, 0:128], modrow[w][0:1, 2048 + hh * 512:2048 + (hh + 1) * 512])])
                        s.copy(gate_bc[layer][w][:, hh * 512:(hh + 1) * 512], bk2.full(), eng="act")
                    if MODD is not None:
                        s.dma(MODD[layer * 2 + w], modT[layer][w].full())
                s.flush()

        adaln_phase(0, e_ada_w, e_ada_b)

        with ExitStack() as l0:
            DT = cx.sb(l0, "DT", [128, NT, 32])
            DTA = cx.sb(l0, "DTA", [128, NT, 32])
            nw = cx.sb(l0, "nw", [128, 8])
            sc1 = [cx.sb(l0, "sc1_%d" % w, [128, 8]) for w in range(2)]
            s.dma(nw.full(), e_norm_wT.full())
            for w in range(2):
                s.stt(sc1[w].full(), modT[0][w][:, 8:16], 1.0, nw.full(), ALU.add, ALU.mult)

            hts = ExitStack()
            hT = [cx.sb(hts, "hT%d" % k, [128, T], BF16) for k in range(8)]
            with ExitStack() as es:
                xt = [cx.sb(es, "xt%d" % i, [128, D]) for i in range(2)]
                xn = [cx.sb(es, "xn%d" % i, [128, D]) for i in range(2)]
                junk = cx.sb(es, "junk", [128, D])
                st = [cx.sb(es, "st%d" % i, [128, 4]) for i in range(2)]
                for i in range(NT):
                    w = 1 if i < 2 else 0
                    x_ = xt[i % 2]
                    n_ = xn[i % 2]
                    st_ = st[i % 2]
                    s.dma(x_.full(), xin[i * 128:(i + 1) * 128, :])
                    s.act(junk.full(), x_.full(), AF.Square, accum=st_[:, 0:1])
                    s.ts(st_[:, 1:2], st_[:, 0:1], 1.0 / D, EPS, ALU.mult, ALU.add)
                    s.act(st_[:, 2:3], st_[:, 1:2], AF.Sqrt)
                    s.recip(st_[:, 3:4], st_[:, 2:3])
                    s.ts(n_.full(), x_.full(), st_[:, 3:4], None, ALU.mult)
                    for half in range(2):
                        bk = banks[(2 * i + half) % 8]
                        for kk in range(4):
                            k = half * 4 + kk
                            s.transpose(bk[:, kk * 128:(kk + 1) * 128], n_[:, k * 128:(k + 1) * 128], ident)
                        for kk in range(4):
                            k = half * 4 + kk
                            s.act(hT[k][:, i * 128:(i + 1) * 128], bk[:, kk * 128:(kk + 1) * 128], AF.Identity,
                                  bias=modT[0][w][:, k:k + 1], scale=sc1[w][:, k:k + 1])
                if HT is not None:
                    for k in range(8):
                        s.dma(HT[k], hT[k].full())
                s.flush()

            with ExitStack() as es:
                WB = 256
                wbuf = [cx.sb(es, "wbuf%d" % i, [128, 8, WB], BF16) for i in range(4)]
                wstate = {"i": 0}

                def load_w(col0, ncol=WB):
                    wb = wbuf[wstate["i"] % 4]
                    wstate["i"] += 1
                    s.dma(wb[:, :, 0:ncol], e_w_in.view(col0, [[E_NCOL_EXT, 128], [128 * E_NCOL_EXT, 8], [1, ncol]]), q="pool")
                    return wb

                bstate = {"i": 0}

                def nbank():
                    bk = banks[bstate["i"] % 8]
                    bstate["i"] += 1
                    return bk

                def fm_mm(wb, cc, t0, n):
                    bk = nbank()
                    s.mm(bk[:, 0:n], [(wb[:, k, cc * 128:(cc + 1) * 128], hT[k][:, t0:t0 + n]) for k in range(8)])
                    return bk

                xraw = cx.sb(es, "xraw", [128, T])
                acc = cx.sb(es, "acc", [128, T])
                accb = cx.sb(es, "accb", [128, T], BF16)
                tmp1 = cx.sb(es, "tmp1", [128, 512])
                tmp2 = cx.sb(es, "tmp2", [128, 512])
                stg = [cx.sb(es, "stg%d" % i, [128, 4, 128]) for i in range(2)]
                stgb = [cx.sb(es, "stgb%d" % i, [128, 4, 128], BF16) for i in range(2)]
                rp = cx.sb(es, "rp", [128, 2, L])
                cw = cx.sb(es, "cw", [128, 12, 5])
                cb = cx.sb(es, "cb", [128, 12])
                dtb = cx.sb(es, "dtb", [128, 32])
                abc = cx.sb(es, "abc", [128, 32])
                s.dma(rp.full(), rope.full())
                s.dma(cw.full(), e_conv_wT.full())
                s.dma(cb.full(), e_conv_bT.full())
                s.dma(dtb.full(), e_dt_bias.view(0, [[0, 128], [1, 32]]))
                s.dma(abc.full(), e_a_log.view(0, [[0, 128], [1, 32]]))
                s.act(abc.full(), abc.full(), AF.Exp)
                s.ts(abc.full(), abc.full(), -1.0, None, ALU.mult)
                stg_i = {"i": 0}

                def transposes_to(dst, col0, src, lowp=False):
                    for i0 in range(0, NT, 4):
                        nb = min(4, NT - i0)
                        bk = nbank()
                        for ii in range(nb):
                            i = i0 + ii
                            s.transpose(bk[:, ii * 128:(ii + 1) * 128], src[:, i * 128:(i + 1) * 128], ident)
                        sg_ = (stgb if lowp else stg)[stg_i["i"] % 2]
                        stg_i["i"] += 1
                        s.copy(sg_[:, 0:nb, :], bk.view(0, [[512, 128], [128, nb], [1, 128]]), eng="act")
                        ncols = dst.h.shape[1]
                        s.dma(dst.view(i0 * 128 * ncols + col0, [[ncols, 128], [128 * ncols, nb], [1, 128]]),
                              sg_[:, 0:nb, :])

                for fc in range(12 if go('p2a') else 0):
                    if fc % 2 == 0:
                        wb = load_w(OFF_XBC + fc * 128)
                    cc = fc % 2
                    for (t0, n) in TG:
                        bk = fm_mm(wb, cc, t0, n)
                        s.copy(xraw[:, t0:t0 + n], bk[:, 0:n], eng="act")
                    s.ts(acc.full(), xraw.full(), cw[:, fc, 2:3], cb[:, fc:fc + 1], ALU.mult, ALU.add)
                    for kk in (0, 1, 3, 4):
                        d_ = kk - 2
                        for (s0, sl) in ((0, CTX), (CTX, L)):
                            lo = max(s0, s0 - d_)
                            hi = min(s0 + sl, s0 + sl - d_)
                            s.stt(acc[:, lo:hi], xraw[:, lo + d_:hi + d_], cw[:, fc, kk:kk + 1], acc[:, lo:hi],
                                  ALU.mult, ALU.add)
                    s.act(acc.full(), acc.full(), AF.Silu)
                    if fc < 8:
                        transposes_to(XS, fc * 128, acc)
                    elif fc < 10:
                        s.copy(accb.full(), acc.full(), eng="pool")
                        s.dma(BT[fc - 8], accb.full())
                        transposes_to(BTOK, (fc - 8) * 128, acc, lowp=True)
                    else:
                        s.copy(accb.full(), acc.full(), eng="pool")
                        s.dma(CT[fc - 10], accb.full())

                def rope_chunk(col_plain, col_swap, dst_rot, dst_ctx):
                    wa = load_w(col_plain, 128)
                    wsw = load_w(col_swap, 128)
                    for gi, (t0, n) in enumerate(TG):
                        bka = fm_mm(wa, 0, t0, n)
                        if gi == 0:
                            s.copy(accb[:, 0:CTX], bka[:, 0:CTX], eng="act")
                            continue
                        bkb = fm_mm(wsw, 0, t0, n)
                        l0 = t0 - CTX
                        s.tt(tmp1.full(), bka.full(), rp[:, 0, l0:l0 + 512], ALU.mult)
                        s.tt(tmp2.full(), bkb.full(), rp[:, 1, l0:l0 + 512], ALU.mult)
                        s.tt(accb[:, t0:t0 + n], tmp1.full(), tmp2.full(), ALU.add, eng="pool")
                    s.dma(dst_ctx, accb[:, 0:CTX])
                    s.dma(dst_rot, accb[:, CTX:T])

                for qc in range(8 if go('p2b') else 0):
                    rope_chunk(OFF_Q + qc * 128, OFF_QS + qc * 128, QR[qc], QC[qc])
                for j in range(4 if go('p2c') else 0):
                    rope_chunk(OFF_KR + j * 128, OFF_KSR + j * 128, KR[j], KC[j])

                for gc in range(8 if go('p2d') else 0):
                    if gc % 2 == 0:
                        wb = load_w(OFF_G + gc * 128)
                    for (t0, n) in TG:
                        bk = fm_mm(wb, gc % 2, t0, n)
                        s.act(acc[:, t0:t0 + n], bk[:, 0:n], AF.Silu)
                    s.dma(SG[gc], acc.full())

                NT_E = NT if go('p2e') else 0
                wz = [load_w(OFF_Z + i * 256) for i in range(4)]
                zt = [cx.sb(es, "zt%d" % i, [128, D]) for i in range(2)]
                for i in range(NT_E):
                    z_ = zt[i % 2]
                    for half in range(2):
                        bk = nbank()
                        for q4 in range(2):
                            wbz = wz[half * 2 + q4]
                            s.mm(bk[:, q4 * 256:(q4 + 1) * 256],
                                 [(hT[k][:, i * 128:(i + 1) * 128], wbz[:, k, :]) for k in range(8)])
                        s.act(z_[:, half * 512:(half + 1) * 512], bk.full(), AF.Silu)
                    s.dma(SZ[i * 128:(i + 1) * 128, :], z_.full())
                wv = load_w(OFF_KV + 256)
                wdt = load_w(OFF_DT, 32)
                vt = [cx.sb(es, "vt%d" % i, [128, 256], BF16) for i in range(2)]
                for i in range(NT if go('p2f') else 0):
                    bk = nbank()
                    s.mm(bk[:, 0:256], [(hT[k][:, i * 128:(i + 1) * 128], wv[:, k, :]) for k in range(8)])
                    s.copy(vt[i % 2].full(), bk[:, 0:256], eng="act")
                    s.dma(VT[i * 128:(i + 1) * 128, :], vt[i % 2].full())
                for i in range(NT if go('p2g') else 0):
                    bk = nbank()
                    s.mm(bk[:, 0:32], [(hT[k][:, i * 128:(i + 1) * 128], wdt[:, k, 0:32]) for k in range(8)])
                    s.tt(DT[:, i, :], bk[:, 0:32], dtb.full(), ALU.add)
                    if go('p2h'):
                        s.act(DT[:, i, :], DT[:, i, :], AF.Exp)
                        s.ts(DT[:, i, :], DT[:, i, :], 1.0, None, ALU.add)
                        s.act(DT[:, i, :], DT[:, i, :], AF.Ln)
                    s.tt(DTA[:, i, :], DT[:, i, :], abc.full(), ALU.mult)
                    if DTD is not None:
                        s.dma(DTD[i * 128:(i + 1) * 128, :], DT[:, i, :])
                s.flush()
            hts.close()

            with ExitStack() as es:
                nb_ = {"i": 0}

                def nbank():
                    bk = banks[nb_["i"] % 8]
                    nb_["i"] += 1
                    return bk

                xs_t = [cx.sb(es, "xs_t%d" % i, [128, 1024]) for i in range(2)]
                b_t = [cx.sb(es, "b_t%d" % i, [128, 256], BF16) for i in range(2)]
                bt_t = [cx.sb(es, "bt_t%d" % i, [128, 2, 128], BF16) for i in range(2)]
                ct_t = [cx.sb(es, "ct_t%d" % i, [128, 2, 128], BF16) for i in range(2)]
                yf_t = [cx.sb(es, "yf_t%d" % i, [128, 1024]) for i in range(2)]
                sz_t = [cx.sb(es, "sz_t%d" % i, [128, 1024]) for i in range(2)]
                dtatri = cx.sb(es, "dtatri", [128, 2048])
                decT = cx.sb(es, "decT", [128, 2048])
                MT = cx.sb(es, "MT", [128, 2048], BF16)
                xc = cx.sb(es, "xc", [128, 1024], BF16)
                xcd = cx.sb(es, "xcd", [128, 1024], BF16)
                cb_sb = cx.sb(es, "cb_sb", [128, 256])
                tmpo = cx.sb(es, "tmpo", [128, 1024])
                ytot = cx.sb(es, "ytot", [128, 1024])
                junk = cx.sb(es, "junk3", [128, 1024])
                ystg = cx.sb(es, "ystg", [128, 8, 128], BF16)
                Hs = [cx.sb(es, "Hs%d" % g, [128, 512]) for g in range(2)]
                Hb = [cx.sb(es, "Hb%d" % g, [128, 512], BF16) for g in range(2)]
                sm = cx.sb(es, "sm", [128, 4, 16])
                st3 = cx.sb(es, "st3", [128, 4])
                dsk = cx.sb(es, "dsk", [128, 16])
                snw = cx.sb(es, "snw", [128, 8])
                s.dma(dsk.full(), e_d_skip.view(0, [[0, 128], [1, 16]]))
                s.dma(snw.full(), e_ssd_norm_wT.full())

                def bc3(buf, off, pstep, n1, s1, n2, s2):
                    return buf.view(off, [[pstep, 128], [s1, n1], [s2, n2]])

                n_ch = NT if go("p3") else 0
                for d_ in range(2):
                    order = list(range(NT)) if d_ == 0 else [1, 0] + list(range(NT - 1, 1, -1))
                    order = order[:n_ch]
                    TRIoff = 512 if d_ == 0 else 1024
                    TRIv = tri if d_ == 0 else utri
                    negm = cst[:, 4 + d_, :]
                    for g in range(2):
                        s.memset(Hs[g].full(), 0.0)
                        s.memset(Hb[g].full(), 0.0)
                    for ci, i in enumerate(order):
                        pp = ci % 2
                        xs_, b_, bt_, ct_ = xs_t[pp], b_t[pp], bt_t[pp], ct_t[pp]
                        s.dma(xs_.full(), XS[i * 128:(i + 1) * 128, :])
                        s.dma(b_.full(), BTOK[i * 128:(i + 1) * 128, :])
                        s.dma(bt_.full(), BT.view(i * 128, [[T, 128], [128 * T, 2], [1, 128]]))
                        s.dma(ct_.full(), CT.view(i * 128, [[T, 128], [128 * T, 2], [1, 128]]))
                        dta_i = DTA[:, i, d_ * 16:(d_ + 1) * 16]
                        doff = i * 32 + d_ * 16
                        s.tt(bc3(dtatri, 0, 2048, 16, 128, 128, 1), bc3(DTA, doff, NT * 32, 16, 1, 128, 0),
                             bc3(cst, TRIoff, 3072, 16, 0, 128, 1), ALU.mult)
                        bs = nbank()
                        s.mm(bs[:, 0:16], [(TRIv, dta_i)])
                        s.mm(bs[:, 16:32], [(ones, dta_i)])
                        na, ea, de, cd = sm[:, 0, :], sm[:, 1, :], sm[:, 2, :], sm[:, 3, :]
                        s.ts(na, bs[:, 0:16], -1.0, None, ALU.mult)
                        s.act(ea, bs[:, 0:16], AF.Exp)
                        s.tt(de, bs[:, 16:32], na, ALU.add)
                        s.act(de, de, AF.Exp)
                        s.act(cd, bs[:, 16:32], AF.Exp)
                        for hq in range(4):
                            bq = nbank()
                            s.mm(bq.full(), [(ones, dtatri[:, hq * 512:(hq + 1) * 512]), (ident, negm)])
                            for hh in range(4):
                                h = hq * 4 + hh
                                s.act(decT[:, h * 128:(h + 1) * 128], bq[:, hh * 128:(hh + 1) * 128], AF.Exp,
                                      bias=sm[:, 0, h:h + 1])
                        bc = nbank()
                        for g in range(2):
                            s.mm(bc[:, g * 128:(g + 1) * 128], [(bt_[:, g, :], ct_[:, g, :])])
                        s.copy(cb_sb.full(), bc[:, 0:256], eng="act")
                        for g in range(2):
                            s.tt(bc3(MT, g * 1024, 2048, 8, 128, 128, 1), bc3(decT, g * 1024, 2048, 8, 128, 128, 1),
                                 bc3(cb_sb, g * 128, 256, 8, 0, 128, 1), ALU.mult)
                        s.tt(bc3(xc, 0, 1024, 16, 64, 64, 1), bc3(xs_, 0, 1024, 16, 64, 64, 1),
                             bc3(DT, doff, NT * 32, 16, 1, 64, 0), ALU.mult)
                        s.tt(bc3(xcd, 0, 1024, 16, 64, 64, 1), bc3(xc, 0, 1024, 16, 64, 64, 1),
                             bc3(sm, 32, 64, 16, 1, 64, 0), ALU.mult)
                        ydst = yf_t[pp] if d_ == 0 else ytot
                        for g in range(2):
                            by = nbank()
                            for hh in range(8):
                                h = g * 8 + hh
                                s.mm(by[:, hh * 64:(hh + 1) * 64], [(MT[:, h * 128:(h + 1) * 128], xc[:, h * 64:(h + 1) * 64])])
                            bo = nbank()
                            s.mm(bo.full(), [(ct_[:, g, :], Hb[g].full())])
                            s.tt(bc3(tmpo, g * 512, 1024, 8, 64, 64, 1), bc3(bo, 0, 512, 8, 64, 64, 1),
                                 bc3(sm, 16 + g * 8, 64, 8, 1, 64, 0), ALU.mult)
                            s.tt(ydst[:, g * 512:(g + 1) * 512], by.full(), tmpo[:, g * 512:(g + 1) * 512], ALU.add)
                        for g in range(2):
                            bst = nbank()
                            s.mm(bst.full(), [(b_[:, g * 128:(g + 1) * 128], xcd[:, g * 512:(g + 1) * 512])])
                            s.tt(bc3(Hs[g], 0, 512, 8, 64, 64, 1), bc3(Hs[g], 0, 512, 8, 64, 64, 1),
                                 bc3(sm, 48 + g * 8, 64, 8, 1, 64, 0), ALU.mult)
                            s.tt(Hs[g].full(), Hs[g].full(), bst.full(), ALU.add)
                            s.copy(Hb[g].full(), Hs[g].full(), eng="pool")
                        if d_ == 0:
                            s.dma(YF[i * 128:(i + 1) * 128, :], yf_t[pp].full())
                        else:
                            yf_, sz_ = yf_t[pp], sz_t[pp]
                            s.dma(yf_.full(), YF[i * 128:(i + 1) * 128, :])
                            s.dma(sz_.full(), SZ[i * 128:(i + 1) * 128, :])
                            s.tt(ytot.full(), ytot.full(), yf_.full(), ALU.add)
                            s.tt(bc3(tmpo, 0, 1024, 16, 64, 64, 1), bc3(xs_, 0, 1024, 16, 64, 64, 1),
                                 bc3(dsk, 0, 16, 16, 1, 64, 0), ALU.mult)
                            s.tt(ytot.full(), ytot.full(), tmpo.full(), ALU.add)
                            s.tt(ytot.full(), ytot.full(), sz_.full(), ALU.mult)
                            s.act(junk.full(), ytot.full(), AF.Square, accum=st3[:, 0:1])
                            s.ts(st3[:, 1:2], st3[:, 0:1], 1.0 / 1024, EPS, ALU.mult, ALU.add)
                            s.act(st3[:, 2:3], st3[:, 1:2], AF.Sqrt)
                            s.recip(st3[:, 3:4], st3[:, 2:3])
                            s.ts(ytot.full(), ytot.full(), st3[:, 3:4], None, ALU.mult)
                            for half in range(2):
                                bk = nbank()
                                for kk in range(4):
                                    k = half * 4 + kk
                                    s.transpose(bk[:, kk * 128:(kk + 1) * 128], ytot[:, k * 128:(k + 1) * 128], ident)
                                for kk in range(4):
                                    k = half * 4 + kk
                                    s.act(ystg[:, k, :], bk[:, kk * 128:(kk + 1) * 128], AF.Copy, scale=snw[:, k:k + 1])
                            s.dma(YT.view(i * 128, [[T, 128], [128 * T, 8], [1, 128]]), ystg.full())
                s.flush()

            with ExitStack() as es:
                nb_ = {"i": 0}

                def nbank():
                    bk = banks[nb_["i"] % 8]
                    nb_["i"] += 1
                    return bk

                qr_t = cx.sb(es, "qr_t", [128, 2, L], BF16)
                qc_t = cx.sb(es, "qc_t", [128, 2, CTX], BF16)
                kr_t = cx.sb(es, "kr_t", [128, L], BF16)
                kc_t = cx.sb(es, "kc_t", [128, CTX], BF16)
                v_t = cx.sb(es, "v_t", [128, NT, 64], BF16)
                v2 = cx.sb(es, "v2", [128, NT, 128], BF16)
                sg_t = cx.sb(es, "sg_t", [128, 2, T])
                ast = cx.sb(es, "ast", [128, 2, T], BF16)
                pt = [[cx.sb(es, "pt%d_%d" % (a, b), [128, 512], BF16) for b in range(5)] for a in range(2)]
                rd = cx.sb(es, "rd", [128, 256])
                ao = cx.sb(es, "ao", [128, 256])
                es_pp = cx.sb(es, "es_pp", [128, 8])
                c8 = cx.sb(es, "c8", [128, 1])
                s.memset(c8.full(), 0.125)
                s.dma(es_pp.full(), e_sink.full())
                s.act(es_pp.full(), es_pp.full(), AF.Exp)
                qb_i = 0
                ATT_DBG = [int(v) for v in os.environ.get("ATT_DBG", "4,18,4").split(",")]
                for j in range(ATT_DBG[0] if go("p4") else 0):
                    s.dma(qr_t.full(), QR.view(2 * j * 128 * L, [[L, 128], [128 * L, 2], [1, L]]))
                    s.dma(qc_t.full(), QC.view(2 * j * 128 * CTX, [[CTX, 128], [128 * CTX, 2], [1, CTX]]))
                    s.dma(kr_t.full(), KR[j])
                    s.dma(kc_t.full(), KC[j])
                    s.dma(v_t.full(), VT.view(j * 64, [[256, 128], [128 * 256, NT], [1, 64]]))
                    s.dma(sg_t.full(), SG.view(2 * j * 128 * T, [[T, 128], [128 * T, 2], [1, T]]))
                    s.copy(v2[:, :, 0:64], v_t.full(), eng="act")
                    s.copy(v2[:, :, 64:128], v_t.full(), eng="pool")
                    for kind, bi in ([("c", 0), ("c", 1)] + [("l", b) for b in range(16)])[:ATT_DBG[1]]:
                        if kind == "c":
                            qsrc, q0, tok0 = qc_t, bi * 128, bi * 128
                            keys = [("c", 0, None), ("c", 1, None)]
                        else:
                            qsrc, q0, tok0 = qr_t, bi * 128, CTX + bi * 128
                            keys = [("c", 0, None), ("c", 1, None)]
                            if bi > 0:
                                keys.append(("l", bi - 1, "prev"))
                            keys.append(("l", bi, None))
                            if bi < 15:
                                keys.append(("l", bi + 1, "next"))
                        pts = pt[qb_i % 2]
                        qb_i += 1
                        qw = qsrc.h.shape[2]
                        for ki, (kk, kb, msk) in enumerate(keys):
                            ksrc = kc_t if kk == "c" else kr_t
                            for par in range(2):
                                p0 = par * 64
                                bs = nbank()
                                s.mm(bs[:, 0:256],
                                     [(ksrc[p0:p0 + 64, kb * 128:(kb + 1) * 128],
                                       qsrc.view(p0 * 2 * qw + q0, [[2 * qw, 64], [qw, 2], [1, 128]]))])
                                s.act(pts[ki][:, par * 256:(par + 1) * 256], bs[:, 0:256], AF.Exp, scale=c8[:, 0:1])
                            if msk is not None and ATT_DBG[2] >= 2:
                                moff = 1024 if msk == "prev" else 512
                                s.tt(pts[ki].view(0, [[512, 128], [128, 4], [1, 128]]),
                                     pts[ki].view(0, [[512, 128], [128, 4], [1, 128]]),
                                     cst.view(moff, [[3072, 128], [0, 4], [1, 128]]), ALU.mult)
                        if ATT_DBG[2] < 3:
                            continue
                        vt_idx = [(kb if kk == "c" else 2 + kb) for (kk, kb, _) in keys]
                        bn = nbank()
                        s.mm(bn.full(), [(v2[:, vt_idx[ki], :], pts[ki].full()) for ki in range(len(keys))])
                        bd = nbank()
                        s.mm(bd.full(), [(onesb, pts[ki].full()) for ki in range(len(keys))])
                        if ATT_DBG[2] < 4:
                            continue
                        for par in range(2):
                            p0 = par * 64
                            for c in range(2):
                                s.ts(rd[p0:p0 + 64, c * 128:(c + 1) * 128],
                                     bd[p0:p0 + 64, par * 256 + c * 128:par * 256 + (c + 1) * 128],
                                     es_pp[p0:p0 + 64, 2 * j + c:2 * j + c + 1], None, ALU.add)
                        s.recip(rd.full(), rd.full())
                        for par in range(2):
                            p0 = par * 64
                            s.tt(ao[p0:p0 + 64, :], bn[p0:p0 + 64, par * 256:(par + 1) * 256], rd[p0:p0 + 64, :], ALU.mult)
                        s.tt(ast.view(tok0, [[2 * T, 128], [T, 2], [1, 128]]),
                             ao.view(0, [[256, 128], [128, 2], [1, 128]]),
                             sg_t.view(tok0, [[2 * T, 128], [T, 2], [1, 128]]), ALU.mult)
                    s.dma(YT.view((8 + 2 * j) * 128 * T, [[T, 128], [128 * T, 2], [1, T]]), ast.full())
                s.flush()

            with ExitStack() as es:
                nb_ = {"i": 0}

                def nbank():
                    bk = banks[nb_["i"] % 8]
                    nb_["i"] += 1
                    return bk

                wo = [cx.sb(es, "wo%d" % k, [128, D], BF16) for k in range(16)]
                yt = [cx.sb(es, "yt%d" % i, [128, 16, 128], BF16) for i in range(2)]
                xt = [cx.sb(es, "xt5_%d" % i, [128, D]) for i in range(2)]
                x1t = [cx.sb(es, "x1t%d" % i, [128, D]) for i in range(2)]
                tmp5 = cx.sb(es, "tmp5", [128, 512])
                for k in range(16):
                    s.dma(wo[k].full(), e_w_out[k * 128:(k + 1) * 128, :], q="pool")
                for i in range(NT if go("p5") else 0):
                    w = 1 if i < 2 else 0
                    y_, x_, o_ = yt[i % 2], xt[i % 2], x1t[i % 2]
                    s.dma(y_.full(), YT.view(i * 128, [[T, 128], [128 * T, 16], [1, 128]]))
                    s.dma(x_.full(), xin[i * 128:(i + 1) * 128, :])
                    for half in range(2):
                        bk = nbank()
                        s.mm(bk.full(), [(y_[:, fc, :], wo[fc][:, half * 512:(half + 1) * 512]) for fc in range(16)])
                        s.tt(tmp5.full(), bk.full(), gate_bc[0][w][:, half * 512:(half + 1) * 512], ALU.mult)
                        s.tt(o_[:, half * 512:(half + 1) * 512], tmp5.full(), x_[:, half * 512:(half + 1) * 512], ALU.add)
                    s.dma(X1[i * 128:(i + 1) * 128, :], o_.full())
                s.flush()

        if go("all"):
            adaln_phase(1, o_ada_w, o_ada_b)
        with ExitStack() as l1:
            if not go("all"):
                return nc
            nb_ = {"i": 0}

            def nbank():
                bk = banks[nb_["i"] % 8]
                nb_["i"] += 1
                return bk

            with ExitStack() as es:
                nw = cx.sb(es, "nw1", [128, 8])
                sc1 = [cx.sb(es, "sc1b_%d" % w, [128, 8]) for w in range(2)]
                s.dma(nw.full(), o_norm_wT.full())
                for w in range(2):
                    s.stt(sc1[w].full(), modT[1][w][:, 8:16], 1.0, nw.full(), ALU.add, ALU.mult)
                hT = [cx.sb(es, "hTb%d" % k, [128, T], BF16) for k in range(8)]
                xt = [cx.sb(es, "xtb%d" % i, [128, D]) for i in range(2)]
                xn = [cx.sb(es, "xnb%d" % i, [128, D]) for i in range(2)]
                junk = cx.sb(es, "junkb", [128, D])
                st = [cx.sb(es, "stb%d" % i, [128, 4]) for i in range(2)]
                for i in range(NT):
                    w = 1 if i < 2 else 0
                    x_, n_, st_ = xt[i % 2], xn[i % 2], st[i % 2]
                    s.dma(x_.full(), X1[i * 128:(i + 1) * 128, :])
                    s.act(junk.full(), x_.full(), AF.Square, accum=st_[:, 0:1])
                    s.ts(st_[:, 1:2], st_[:, 0:1], 1.0 / D, EPS, ALU.mult, ALU.add)
                    s.act(st_[:, 2:3], st_[:, 1:2], AF.Sqrt)
                    s.recip(st_[:, 3:4], st_[:, 2:3])
                    s.ts(n_.full(), x_.full(), st_[:, 3:4], None, ALU.mult)
                    for half in range(2):
                        bk = nbank()
                        for kk in range(4):
                            k = half * 4 + kk
                            s.transpose(bk[:, kk * 128:(kk + 1) * 128], n_[:, k * 128:(k + 1) * 128], ident)
                        for kk in range(4):
                            k = half * 4 + kk
                            s.act(hT[k][:, i * 128:(i + 1) * 128], bk[:, kk * 128:(kk + 1) * 128], AF.Identity,
                                  bias=modT[1][w][:, k:k + 1], scale=sc1[w][:, k:k + 1])
                wq = [cx.sb(es, "wq%d" % i, [128, 8, 256], BF16) for i in range(4)]
                ot = [cx.sb(es, "ot%d" % i, [128, D]) for i in range(2)]
                oi = 0
                for which in range(2):
                    for q4 in range(4):
                        s.dma(wq[q4].full(), o_w_in.view(which * 1024 + q4 * 256, [[2 * D, 128], [128 * 2 * D, 8], [1, 256]]), q="pool")
                    for i in range(NT):
                        if which == 1 and i < 2:
                            continue
                        o_ = ot[oi % 2]
                        oi += 1
                        for half in range(2):
                            bk = nbank()
                            for q4 in range(2):
                                s.mm(bk[:, q4 * 256:(q4 + 1) * 256],
                                     [(hT[k][:, i * 128:(i + 1) * 128], wq[half * 2 + q4][:, k, :]) for k in range(8)])
                            if which == 0:
                                s.copy(o_[:, half * 512:(half + 1) * 512], bk.full(), eng="act")
                            else:
                                s.act(o_[:, half * 512:(half + 1) * 512], bk.full(), AF.Silu)
                        s.dma((U if which == 0 else SG1)[i * 128:(i + 1) * 128, :], o_.full())
                s.flush()

            L1S = os.environ.get('L1S', 'z')
            if L1S == 'a':
                return nc
            with ExitStack() as es:
                lam = cx.sb(es, "lam", [128, 2, 3, 32])
                bprm = cx.sb(es, "bprm", [128, 2, 32, 16])
                cprm = cx.sb(es, "cprm", [128, 2, 32, 16])
                s.dma(lam.full(), s5_lam.full())
                s.dma(bprm.full(), s5_b.full())
                s.dma(cprm.full(), s5_c.full())
                kc = cx.sb(es, "kconst", [128, 4])
                s.memset(kc[:, 0:1], 1.0 / 16)
                s.memset(kc[:, 1:2], math.pi / 2)
                s.memset(kc[:, 2:3], 0.0)
                s.memset(kc[:, 3:4], 1.0)
                W64 = [128, 2, 32]

                def t64(name):
                    return cx.sb(es, name, W64)

                def lv(i):
                    return lam.view(i * 32, [[192, 128], [96, 2], [1, 32]])

                dt_ = t64("dt_"); mag = t64("mag"); th = t64("th"); cs = t64("cs"); sn = t64("sn")
                t_a = t64("t_a"); t_b = t64("t_b"); t_c = t64("t_c")
                abre = t64("abre"); abim = t64("abim"); cre = t64("cre"); cim = t64("cim")
                s.act(dt_.full(), lv(2), AF.Exp)
                s.tt(t_a.full(), lv(0), dt_.full(), ALU.mult)
                s.act(mag.full(), t_a.full(), AF.Exp)
                s.tt(th.full(), lv(1), dt_.full(), ALU.mult)
                s.act(sn.full(), th.full(), AF.Sin, scale=kc[:, 0:1])
                s.act(cs.full(), th.full(), AF.Sin, scale=kc[:, 0:1], bias=kc[:, 1:2])
                for _ in range(4):
                    s.tt(t_a.full(), cs.full(), cs.full(), ALU.mult)
                    s.tt(t_b.full(), sn.full(), sn.full(), ALU.mult)
                    s.tt(t_c.full(), sn.full(), cs.full(), ALU.mult)
                    s.tt(cs.full(), t_a.full(), t_b.full(), ALU.subtract)
                    s.ts(sn.full(), t_c.full(), 2.0, None, ALU.mult)
                s.tt(abre.full(), mag.full(), cs.full(), ALU.mult)
                s.tt(abim.full(), mag.full(), sn.full(), ALU.mult)
                PW = cx.sb(es, "PW", [128, 2, 9, 64])

                def pw(ri, k):
                    return PW.view((ri * 9 + k) * 64, [[2 * 9 * 64, 128], [32, 2], [1, 32]])

                s.memset(PW[:, 0, 0, :], 1.0)
                s.memset(PW[:, 1, 0, :], 0.0)
                for k in range(8):
                    s.tt(t_a.full(), pw(0, k), abre.full(), ALU.mult)
                    s.tt(t_b.full(), pw(1, k), abim.full(), ALU.mult)
                    s.tt(pw(0, k + 1), t_a.full(), t_b.full(), ALU.subtract)
                    s.tt(t_a.full(), pw(0, k), abim.full(), ALU.mult)
                    s.tt(t_b.full(), pw(1, k), abre.full(), ALU.mult)
                    s.tt(pw(1, k + 1), t_a.full(), t_b.full(), ALU.add)
                s.ts(t_c.full(), abre.full(), -1.0, None, ALU.add)
                s.tt(t_a.full(), lv(0), lv(0), ALU.mult)
                s.tt(t_b.full(), lv(1), lv(1), ALU.mult)
                s.tt(t_a.full(), t_a.full(), t_b.full(), ALU.add)
                s.recip(dt_.full(), t_a.full())
                s.tt(t_a.full(), t_c.full(), lv(0), ALU.mult)
                s.tt(t_b.full(), abim.full(), lv(1), ALU.mult)
                s.tt(t_a.full(), t_a.full(), t_b.full(), ALU.add)
                s.tt(cre.full(), t_a.full(), dt_.full(), ALU.mult)
                s.tt(t_a.full(), abim.full(), lv(0), ALU.mult)
                s.tt(t_b.full(), t_c.full(), lv(1), ALU.mult)
                s.tt(t_a.full(), t_a.full(), t_b.full(), ALU.subtract)
                s.tt(cim.full(), t_a.full(), dt_.full(), ALU.mult)
                BB = cx.sb(es, "BB", [128, 2, 2, 512])
                tb1 = cx.sb(es, "tb1", [128, 512])
                tb2 = cx.sb(es, "tb2", [128, 512])

                def bb(ri, d_, g0=0, ng=32):
                    return BB.view((ri * 2 + d_) * 512 + g0 * 16, [[2048, 128], [16, ng], [1, 16]])

                def v3(buf, off, pstep, n1, s1, n2, s2):
                    return buf.view(off, [[pstep, 128], [s1, n1], [s2, n2]])

                def prm(buf, ri, g0=0, ng=32):
                    return buf.view(ri * 512 + g0 * 16, [[1024, 128], [16, ng], [1, 16]])

                def cf(buf, d_, g0=0, ng=32, n2=16):
                    return buf.view(d_ * 32 + g0, [[64, 128], [1, ng], [0, n2]])

                t1v = v3(tb1, 0, 512, 32, 16, 16, 1)
                t2v = v3(tb2, 0, 512, 32, 16, 16, 1)
                for d_ in range(2):
                    s.tt(t1v, prm(bprm, 0), cf(cre, d_), ALU.mult)
                    s.tt(t2v, prm(bprm, 1), cf(cim, d_), ALU.mult)
                    s.tt(bb(0, d_), t1v, t2v, ALU.subtract)
                    s.tt(t1v, prm(bprm, 1), cf(cre, d_), ALU.mult)
                    s.tt(t2v, prm(bprm, 0), cf(cim, d_), ALU.mult)
                    s.tt(bb(1, d_), t1v, t2v, ALU.add)
                LA = cx.sb(es, "LA", [128, 2, 32, 2])
                LB = cx.sb(es, "LB", [128, 2, 32, 2])
                for ri in range(2):
                    s.copy(LA.view(ri, [[128, 128], [64, 2], [2, 32]]), pw(0, 8))
                s.ts(LB.view(0, [[128, 128], [64, 2], [2, 32]]), pw(1, 8), -1.0, None, ALU.mult)
                s.copy(LB.view(1, [[128, 128], [64, 2], [2, 32]]), pw(1, 8))
                zt_ = cx.sb(es, "zt_", [16, 16, 112])
                s.memset(zt_.full(), 0.0)
                s.flush()

                if L1S == 'b':
                    return nc
                for b in range(4 if L1S not in ('c1', 'd1', 'e1', 'f1', 'g1') else 1):
                    g0 = 8 * b
                    with ExitStack() as bs_:
                        CAB = cx.sb(bs_, "CAB", [128, 2, 2, 8 * 144])
                        WST = cx.sb(bs_, "WST", [128, 8, 2, 2, 2, 64])
                        TF = cx.sb(bs_, "TF", [128, 16, 128])
                        TB = cx.sb(bs_, "TB", [128, 16, 128])

                        with ExitStack() as tmp:
                            WT = cx.sb(tmp, "WT", [128, 2, 2, 8 * 128])
                            KSB = cx.sb(tmp, "KSB", [16, 2, 16, 128])
                            c1 = cx.sb(tmp, "c1", [128, 128])
                            c2 = cx.sb(tmp, "c2", [128, 128])
                            c1v = v3(c1, 0, 128, 8, 16, 16, 1)
                            c2v = v3(c2, 0, 128, 8, 16, 16, 1)
                            for d_ in range(2):
                                for idx in range(9):
                                    p_ = idx if d_ == 0 else 8 - idx
                                    pr = PW.view((0 * 9 + p_) * 64 + d_ * 32 + g0, [[1152, 128], [1, 8], [0, 16]])
                                    pi_ = PW.view((1 * 9 + p_) * 64 + d_ * 32 + g0, [[1152, 128], [1, 8], [0, 16]])
                                    o_re = CAB.view((0 * 2 + d_) * 1152 + idx * 16, [[4608, 128], [144, 8], [1, 16]])
                                    o_im = CAB.view((1 * 2 + d_) * 1152 + idx * 16, [[4608, 128], [144, 8], [1, 16]])
                                    s.tt(c1v, prm(cprm, 0, g0, 8), pr, ALU.mult)
                                    s.tt(c2v, prm(cprm, 1, g0, 8), pi_, ALU.mult)
                                    s.tt(o_re, c1v, c2v, ALU.subtract)
                                    s.tt(c1v, prm(cprm, 0, g0, 8), pi_, ALU.mult)
                                    s.tt(c2v, prm(cprm, 1, g0, 8), pr, ALU.mult)
                                    s.stt(o_im, c1v, -1.0, c2v, ALU.mult, ALU.subtract)
                                for ss in range(8):
                                    p_ = 7 - ss if d_ == 0 else ss
                                    pr = PW.view((0 * 9 + p_) * 64 + d_ * 32 + g0, [[1152, 128], [1, 8], [0, 16]])
                                    pi_ = PW.view((1 * 9 + p_) * 64 + d_ * 32 + g0, [[1152, 128], [1, 8], [0, 16]])
                                    o_re = WT.view((d_ * 2 + 0) * 1024 + ss * 16, [[4096, 128], [128, 8], [1, 16]])
                                    o_im = WT.view((d_ * 2 + 1) * 1024 + ss * 16, [[4096, 128], [128, 8], [1, 16]])
                                    s.tt(c1v, bb(0, d_, g0, 8), pr, ALU.mult)
                                    s.tt(c2v, bb(1, d_, g0, 8), pi_, ALU.mult)
                                    s.tt(o_re, c1v, c2v, ALU.subtract)
                                    s.tt(c1v, bb(1, d_, g0, 8), pr, ALU.mult)
                                    s.tt(c2v, bb(0, d_, g0, 8), pi_, ALU.mult)
                                    s.tt(o_im, c1v, c2v, ALU.add)
                            for gh in range(2):
                                p0 = gh * 64
                                for gq in range(8):
                                    bk = nbank()
                                    for d_ in range(2):
                                        for ri in range(2):
                                            sl = d_ * 2 + ri
                                            s.transpose(bk[:, sl * 64:(sl + 1) * 64],
                                                        WT.view(p0 * 4096 + (d_ * 2 + ri) * 1024 + gq * 128, [[4096, 64], [1, 128]]),
                                                        cst[p0:p0 + 64, 0, p0:p0 + 64])
                                    s.copy(WST.view(((gq * 2 + gh) * 4) * 64, [[4096, 128], [1, 256]]), bk[:, 0:256], eng="act")
                                for d_ in range(2):
                                    for gqq in range(2):
                                        bk = nbank()
                                        for q4 in range(4):
                                            gq = gqq * 4 + q4
                                            i0 = 0 if d_ == 0 else 1
                                            s.mm(bk[0:16, q4 * 128:(q4 + 1) * 128],
                                                 [(BB.view(p0 * 2048 + (0 * 2 + d_) * 512 + (g0 + gq) * 16, [[2048, 64], [1, 16]]),
                                                   CAB.view(p0 * 4608 + (0 * 2 + d_) * 1152 + gq * 144 + i0 * 16, [[4608, 64], [1, 128]])),
                                                  (BB.view(p0 * 2048 + (1 * 2 + d_) * 512 + (g0 + gq) * 16, [[2048, 64], [1, 16]]),
                                                   CAB.view(p0 * 4608 + (1 * 2 + d_) * 1152 + gq * 144 + i0 * 16, [[4608, 64], [1, 128]]))])
                                        s.copy(KSB.view(d_ * 2048 + (2 * gqq * 4 + gh) * 128, [[4096, 16], [256, 4], [1, 128]]),
                                               bk.view(0, [[512, 16], [128, 4], [1, 128]]), eng="act")
                            gbase = 16 * b
                            s.dma(KFP.view(gbase * 3840 + 7 * 16, [[240, 16], [3840, 16], [1, 128]]), KSB[:, 0, :, :])
                            s.dma(KBR.view(gbase * 3840, [[240, 16], [3840, 16], [1, 128]]), KSB[:, 1, :, :])
                            s.dma(KFP.view(gbase * 3840, [[240, 16], [3840, 16], [1, 112]]), zt_.full())
                            s.dma(KBR.view(gbase * 3840 + 128, [[240, 16], [3840, 16], [1, 112]]), zt_.full())
                            for ss in range(8):
                                s.dma(TF[ss * 16:(ss + 1) * 16, :, :], KFP.view(gbase * 3840 + (7 - ss) * 16, [[240, 16], [3840, 16], [1, 128]]))
                                s.dma(TB[ss * 16:(ss + 1) * 16, :, :], KBR.view(gbase * 3840 + (7 - ss) * 16, [[240, 16], [3840, 16], [1, 128]]))
                            s.flush()

                        if L1S in ('c', 'c1'):
                            continue
                        u8b = cx.sb(bs_, "u8b", [128, 8, 256])
                        u8g = cx.sb(bs_, "u8g", [128, 16, 128])
                        U8T = cx.sb(bs_, "U8T", [128, 16, 288])
                        NCOL = 326
                        PS = 16 * NCOL
                        SSD = [cx.sb(bs_, "SS%d" % i, [128, 8, 2, NCOL]) for i in range(2)]
                        CAR = [cx.sb(bs_, "CAR%d" % i, [128, 7, 8, 2]) for i in range(2)]
                        A36 = [cx.sb(bs_, "A36_%d" % i, [128, 8, 2]) for i in range(2)]
                        B36 = [cx.sb(bs_, "B36_%d" % i, [128, 8, 2]) for i in range(2)]
                        y8b = cx.sb(bs_, "y8b", [128, 8, 256])
                        ysb = cx.sb(bs_, "ysb", [128, 512])
                        TT1 = [cx.sb(bs_, "TT1_%d" % i, [128, 9, 8, 2]) for i in range(2)]
                        TT2 = [cx.sb(bs_, "TT2_%d" % i, [128, 9, 8, 2]) for i in range(2)]
                        for (j0, nj) in ((0, 32), (32, 128), (160, 128)):
                            s.dma(u8b[0:nj, :, :], U.view(8 * j0 * 1024 + 256 * b, [[8192, nj], [1024, 8], [1, 256]]))
                            s.copy(u8g.view(0, [[2048, nj], [128, 16], [16, 8], [1, 16]]),
                                   u8b.view(0, [[2048, nj], [16, 16], [256, 8], [1, 16]]), eng="act")
                            for gq4 in range(4):
                                bk = nbank()
                                for q4 in range(4):
                                    gi = gq4 * 4 + q4
                                    s.transpose(bk[:, q4 * 128:q4 * 128 + nj],
                                                u8g.view(128 * gi, [[2048, nj], [1, 128]]), cst[0:nj, 0, 0:nj])
                                s.copy(U8T.view(gq4 * 4 * 288 + j0, [[16 * 288, 128], [288, 4], [1, nj]]),
                                       bk.view(0, [[512, 128], [128, 4], [1, nj]]), eng="act")
                        if L1S in ('d', 'd1'):
                            s.flush()
                            continue
                        s.memset(SSD[0].view(0, [[PS, 128], [NCOL, 16], [1, 1]]), 0.0)
                        s.memset(SSD[0].view(289, [[PS, 128], [NCOL, 16], [1, 37]]), 0.0)
                        s.memset(SSD[1].view(288, [[PS, 128], [NCOL, 16], [1, 38]]), 0.0)
                        s.memset(SSD[0].view(289, [[PS, 128], [2 * NCOL, 8], [1, 1]]), 1.0)
                        s.memset(SSD[1].view(323, [[PS, 128], [2 * NCOL, 8], [1, 1]]), 1.0)
                        for gq in range(8):
                            for gh in range(2):
                                gi = 2 * gq + gh
                                p0 = gh * 64
                                for d_ in range(2):
                                    for ri in range(2):
                                        bk = nbank()
                                        s.mm(bk[p0:p0 + 64, 0:288],
                                             [(WST.view((((gq * 2 + gh) * 2 + d_) * 2 + ri) * 64, [[4096, 128], [1, 64]]),
                                               U8T[:, gi, :])])
                                        so = p0 * PS + (gq * 2 + ri) * NCOL
                                        if d_ == 0:
                                            s.copy(SSD[0].view(so + 1, [[PS, 64], [1, 288]]), bk[p0:p0 + 64, 0:288], eng="act")
                                        else:
                                            s.copy(SSD[1].view(so + 256, [[PS, 64], [1, 32]]), bk[p0:p0 + 64, 0:32], eng="act")
                                            s.copy(SSD[1].view(so, [[PS, 64], [1, 256]]), bk[p0:p0 + 64, 32:288], eng="act")
                        if L1S in ('e', 'e1'):
                            s.flush()
                            continue
                        DS = 8 * 2 * 289
                        RI, GQ = NCOL, 2 * NCOL

                        def cplx_step(items):
                            for (pv, psw, cv, ca, cb_, t1_, t2_) in items:
                                s.tt(t1_, pv, ca, ALU.mult)
                                s.tt(t2_, psw, cb_, ALU.mult)
                            for (pv, psw, cv, ca, cb_, t1_, t2_) in items:
                                s.tt(t1_, t1_, t2_, ALU.add)
                            for (pv, psw, cv, ca, cb_, t1_, t2_) in items:
                                if cv is not None:
                                    s.tt(cv, cv, t1_, ALU.add)

                        def segv(SS, col, nseg):
                            return (SS.view(col, [[PS, 128], [36, nseg], [GQ, 8], [RI, 2]]),
                                    SS.view(col + RI, [[PS, 128], [36, nseg], [GQ, 8], [-RI, 2]]))

                        def coef(buf, d_, nseg):
                            return buf.view(d_ * 64 + g0 * 2, [[128, 128], [0, nseg], [2, 8], [1, 2]])

                        for k in range(1, 36):
                            items = []
                            for d_ in range(2):
                                pc = k if d_ == 0 else 36 - k
                                cc = k + 1 if d_ == 0 else 35 - k
                                pv, psw = segv(SSD[d_], pc, 9)
                                cv, _ = segv(SSD[d_], cc, 9)
                                items.append((pv, psw, cv, coef(LA, d_, 9), coef(LB, d_, 9), TT1[d_].full(), TT2[d_].full()))
                            cplx_step(items)
                        items = []
                        for d_ in range(2):
                            c35 = 324 if d_ == 0 else 288
                            pv, psw = segv(SSD[d_], c35, 1)
                            items.append((pv, psw, None, coef(LA, d_, 1), coef(LB, d_, 1),
                                          TT1[d_].view(0, [[144, 128], [16, 1], [2, 8], [1, 2]]),
                                          TT2[d_].view(0, [[144, 128], [16, 1], [2, 8], [1, 2]])))
                        cplx_step(items)
                        for d_ in range(2):
                            l36re = TT1[d_].view(0, [[144, 128], [2, 8], [0, 2]])
                            s.copy(A36[d_].full(), l36re)
                            s.ts(B36[d_][:, :, 0:1], TT1[d_].view(1, [[144, 128], [2, 8], [1, 1]]), -1.0, None, ALU.mult)
                            s.copy(B36[d_][:, :, 1:2], TT1[d_].view(1, [[144, 128], [2, 8], [1, 1]]))
                        for step in range(1, 8):
                            items = []
                            for d_ in range(2):
                                if d_ == 0:
                                    m = step
                                    cc, pc = 36 * m + 36, 36 * m
                                else:
                                    m = 7 - step
                                    cc, pc = 36 * m, 36 * m + 36
                                pv, psw = segv(SSD[d_], pc, 1)
                                cv, _ = segv(SSD[d_], cc, 1)
                                items.append((pv, psw, cv,
                                              A36[d_].view(0, [[16, 128], [0, 1], [2, 8], [1, 2]]),
                                              B36[d_].view(0, [[16, 128], [0, 1], [2, 8], [1, 2]]),
                                              TT1[d_].view(0, [[144, 128], [16, 1], [2, 8], [1, 2]]),
                                              TT2[d_].view(0, [[144, 128], [16, 1], [2, 8], [1, 2]])))
                            cplx_step(items)
                        items = []
                        for d_ in range(2):
                            pv, psw = segv(SSD[d_], 36, 7)
                            items.append((pv, psw, None, coef(LA, d_, 7), coef(LB, d_, 7),
                                          CAR[d_].full(), TT2[d_].view(0, [[144, 128], [16, 7], [2, 8], [1, 2]])))
                        cplx_step(items)
                        for d_ in range(2):
                            SS = SSD[d_]
                            sb0 = 37 if d_ == 0 else 1

                            def sview(ri):
                                return SS.view(sb0 + ri * RI, [[PS, 128], [36, 7], [GQ, 8], [1, 35]])

                            def tview(ri):
                                return SS.view(289 + ri * RI, [[PS, 128], [0, 7], [GQ, 8], [1, 35]])

                            def cview(ri):
                                return CAR[d_].view(ri, [[112, 128], [16, 7], [2, 8], [0, 35]])

                            w1 = (u8g if d_ == 0 else u8b).view(0, [[2048, 128], [280, 7], [35, 8], [1, 35]])
                            w2 = y8b.view(0, [[2048, 128], [280, 7], [35, 8], [1, 35]])
                            s.tt(w1, tview(0), cview(0), ALU.mult)
                            s.tt(w2, tview(1), cview(1), ALU.mult)
                            s.tt(w1, w1, w2, ALU.subtract)
                            s.tt(sview(0), sview(0), w1, ALU.add)
                            s.tt(w1, tview(0), cview(1), ALU.mult)
                            s.tt(w2, tview(1), cview(0), ALU.mult)
                            s.tt(w1, w1, w2, ALU.add)
                            s.tt(sview(1), sview(1), w1, ALU.add)
                        if L1S in ('f', 'f1'):
                            s.flush()
                            continue
                        for tt_ in range(2):
                            j0 = 32 + 128 * tt_
                            m0 = 128 * tt_
                            for gh in range(2):
                                p0 = gh * 64
                                for gqq in range(2):
                                    bx = nbank()
                                    by = nbank()
                                    for q4 in range(4):
                                        gq = gqq * 4 + q4
                                        gi = 2 * gq + gh
                                        s.mm(bx[:, q4 * 128:(q4 + 1) * 128],
                                             [(U8T[:, gi, j0:j0 + 128], TF[:, gi, :]), (U8T[:, gi, j0:j0 + 128], TB[:, gi, :])])
                                        pairs = []
                                        for d_ in range(2):
                                            c0 = j0 if d_ == 0 else m0 + 1
                                            i0 = 1 if d_ == 0 else 0
                                            for ri in range(2):
                                                so = p0 * PS + (gq * 2 + ri) * NCOL + c0
                                                pairs.append((SSD[d_].view(so, [[PS, 64], [1, 128]]),
                                                              CAB.view(p0 * 4608 + (ri * 2 + d_) * 1152 + gq * 144 + i0 * 16, [[4608, 64], [1, 128]])))
                                        s.mm(by[:, q4 * 128:(q4 + 1) * 128], pairs)
                                    s.copy(ysb.full(), by.full(), eng="act")
                                    s.tt(y8b.view(32 * gqq * 4 + 16 * gh, [[2048, 128], [32, 4], [256, 8], [1, 16]]),
                                         bx.view(0, [[512, 128], [128, 4], [16, 8], [1, 16]]),
                                         ysb.view(0, [[512, 128], [128, 4], [16, 8], [1, 16]]), ALU.add)
                            s.dma(YTOK.view((CTX + 8 * m0) * 1024 + 256 * b, [[8192, 128], [1024, 8], [1, 256]]), y8b.full())
                        s.flush()

            if L1S in ('g', 'g1'):
                return nc
            with ExitStack() as es:
                gw = [cx.sb(es, "gw%d" % k, [128, D], BF16) for k in range(8)]
                ow = [cx.sb(es, "ow%d" % k, [128, D], BF16) for k in range(8)]
                dskb = cx.sb(es, "dskb", [128, D])
                glbb = cx.sb(es, "glbb", [128, D])
                fnwb = cx.sb(es, "fnwb", [128, D])
                kg = cx.sb(es, "kg", [128, 1])
                s.memset(kg.full(), 2.0 * math.sqrt(2.0 / math.pi))
                for k in range(8):
                    s.dma(gw[k].full(), o_glu_w[k * 128:(k + 1) * 128, :], q="pool")
                    s.dma(ow[k].full(), o_w_out[k * 128:(k + 1) * 128, :], q="pool")
                s.dma(dskb.full(), o_d_skip.view(0, [[0, 128], [1, D]]))
                s.dma(glbb.full(), o_glu_b.view(0, [[0, 128], [1, D]]))
                s.dma(fnwb.full(), final_norm_w.view(0, [[0, 128], [1, D]]))
                ya = [cx.sb(es, "ya%d" % i, [128, D]) for i in range(2)]
                ua = [cx.sb(es, "ua%d" % i, [128, D]) for i in range(2)]
                sga = [cx.sb(es, "sga%d" % i, [128, D]) for i in range(2)]
                xa = [cx.sb(es, "xa%d" % i, [128, D]) for i in range(2)]
                w1 = cx.sb(es, "w1", [128, D])
                w2 = cx.sb(es, "w2", [128, D])
                w3 = cx.sb(es, "w3", [128, D])
                tT = cx.sb(es, "tT", [128, 8, 128], BF16)
                st = cx.sb(es, "st10", [128, 4])

                def transp8(src):
                    for half in range(2):
                        bk = nbank()
                        for kk in range(4):
                            k = half * 4 + kk
                            s.transpose(bk[:, kk * 128:(kk + 1) * 128], src[:, k * 128:(k + 1) * 128], ident)
                        s.copy(tT[:, half * 4:(half + 1) * 4, :], bk.view(0, [[512, 128], [128, 4], [1, 128]]), eng="act")

                TAILN = int(os.environ.get('TAILN', NT))
                for i in range(2, TAILN):
                    y_, u_, g_, x_ = ya[i % 2], ua[i % 2], sga[i % 2], xa[i % 2]
                    s.dma(y_.full(), YTOK[i * 128:(i + 1) * 128, :])
                    s.dma(u_.full(), U[i * 128:(i + 1) * 128, :])
                    s.dma(g_.full(), SG1[i * 128:(i + 1) * 128, :])
                    s.dma(x_.full(), X1[i * 128:(i + 1) * 128, :])
                    s.tt(w1.full(), u_.full(), dskb.full(), ALU.mult)
                    s.tt(y_.full(), y_.full(), w1.full(), ALU.add)
                    s.tt(w1.full(), y_.full(), y_.full(), ALU.mult)
                    s.ts(w1.full(), w1.full(), 0.044715, 1.0, ALU.mult, ALU.add)
                    s.tt(w1.full(), w1.full(), y_.full(), ALU.mult)
                    s.act(w1.full(), w1.full(), AF.Sigmoid, scale=kg[:, 0:1])
                    s.tt(w2.full(), y_.full(), w1.full(), ALU.mult)
                    transp8(w2)
                    for half in range(2):
                        bk = nbank()
                        s.mm(bk.full(), [(tT[:, k, :], gw[k][:, half * 512:(half + 1) * 512]) for k in range(8)])
                        s.tt(w1[:, half * 512:(half + 1) * 512], bk.full(), glbb[:, half * 512:(half + 1) * 512], ALU.add)
                    s.act(w1.full(), w1.full(), AF.Sigmoid)
                    s.tt(w2.full(), w2.full(), w1.full(), ALU.mult)
                    s.tt(w2.full(), w2.full(), g_.full(), ALU.mult)
                    transp8(w2)
                    for half in range(2):
                        bk = nbank()
                        s.mm(bk.full(), [(tT[:, k, :], ow[k][:, half * 512:(half + 1) * 512]) for k in range(8)])
                        s.tt(w1[:, half * 512:(half + 1) * 512], bk.full(), gate_bc[1][0][:, half * 512:(half + 1) * 512], ALU.mult)
                    s.tt(w3.full(), w1.full(), x_.full(), ALU.add)
                    s.act(w1.full(), w3.full(), AF.Square, accum=st[:, 0:1])
                    s.ts(st[:, 1:2], st[:, 0:1], 1.0 / D, EPS, ALU.mult, ALU.add)
                    s.act(st[:, 2:3], st[:, 1:2], AF.Sqrt)
                    s.recip(st[:, 3:4], st[:, 2:3])
                    s.ts(w3.full(), w3.full(), st[:, 3:4], None, ALU.mult)
                    s.tt(w2.full(), w3.full(), fnwb.full(), ALU.mult)
                    s.dma(out_t[(i - 2) * 128:(i - 1) * 128, :], w2.full())
                s.flush()

    return nc


def _consts():
    c = np.zeros((128, 6, 512), np.float32)
    j = np.arange(128)[:, None]
    l = np.arange(128)[None, :]
    c[:, 0, :128] = np.eye(128, dtype=np.float32)
    c[:, 1, :128] = (j <= l)
    c[:, 2, :128] = (j >= l)
    c[:, 3, :] = 1.0
    nf = np.where(l < j, -30000.0, 0.0).astype(np.float32)
    nb = np.where(l > j, -30000.0, 0.0).astype(np.float32)
    c[:, 4, :] = np.tile(nf, (1, 4))
    c[:, 5, :] = np.tile(nb, (1, 4))
    return c


def _rope_tables():
    rows = L // 64
    row = np.repeat(np.arange(rows, dtype=np.float32), 64)
    col = np.tile(np.arange(64, dtype=np.float32), rows)
    n_freq = 16
    inv = (np.float32(10000.0) ** (-np.arange(n_freq, dtype=np.float32) / n_freq)).astype(np.float32)
    ang = np.concatenate([row[:, None] * inv, col[:, None] * inv], axis=-1).astype(np.float32)
    cos = np.cos(ang).astype(np.float32)
    sin = np.sin(ang).astype(np.float32)
    cosT = np.zeros((128, L), np.float32)
    sinT = np.zeros((128, L), np.float32)
    for h2 in range(2):
        for half in range(2):
            p0 = h2 * 64 + half * 32
            cosT[p0:p0 + 32] = cos.T
            sinT[p0:p0 + 32] = (-sin.T if half == 0 else sin.T)
    return np.stack([cosT, sinT], axis=1)


def _vecT(v, nchunk):
    return np.ascontiguousarray(np.asarray(v, np.float32).reshape(nchunk, 128).T)


def prep_inputs(b, inp):
    f = lambda a: np.ascontiguousarray(np.asarray(a, np.float32))
    m = {}
    m["xin"] = f(np.concatenate([inp["ctx"][b], inp["x"][b]], axis=0))
    cv = np.stack([inp["c"][b], inp["c_ctx"]], axis=0)
    m["cvecT"] = f(cv.reshape(2, 8, 128).transpose(2, 0, 1))
    m["consts"] = _consts()
    m["rope"] = _rope_tables()
    m["e_ada_w"] = f(inp["e_ada_w"][0])
    m["e_ada_b"] = f(inp["e_ada_b"][0]).reshape(1, -1)
    m["e_norm_wT"] = _vecT(inp["e_norm_w"][0], 8)
    w = f(inp["e_w_in"][0])
    q = w[:, OFF_Q:OFF_Q + 1024].reshape(D, 16, 2, 32)
    qs = q[:, :, ::-1, :].reshape(D, 1024)
    k = w[:, OFF_KV:OFF_KV + 256].reshape(D, 4, 64)
    kr = np.concatenate([k, k], axis=2).reshape(D, 512)
    ks = k.reshape(D, 4, 2, 32)[:, :, ::-1, :].reshape(D, 4, 64)
    ksr = np.concatenate([ks, ks], axis=2).reshape(D, 512)
    m["e_w_in"] = f(np.concatenate([w, qs, kr, ksr], axis=1))
    cw = f(inp["e_conv_w"][0])
    m["e_conv_wT"] = f(cw.reshape(5, 12, 128).transpose(2, 1, 0))
    m["e_conv_bT"] = _vecT(inp["e_conv_b"][0], 12)
    m["e_dt_bias"] = f(inp["e_dt_bias"][0]).reshape(1, 32)
    m["e_a_log"] = f(inp["e_a_log"][0]).reshape(1, 32)
    m["e_d_skip"] = f(inp["e_d_skip"][0]).reshape(1, 16)
    m["e_ssd_norm_wT"] = _vecT(inp["e_ssd_norm_w"][0], 8)
    sk = f(inp["e_sink"][0]).reshape(8, 2)
    m["e_sink"] = f(np.repeat(sk.T[:, None, :], 64, axis=1).reshape(128, 8))
    m["e_w_out"] = f(inp["e_w_out"][0])
    m["o_ada_w"] = f(inp["o_ada_w"][0])
    m["o_ada_b"] = f(inp["o_ada_b"][0]).reshape(1, -1)
    m["o_norm_wT"] = _vecT(inp["o_norm_w"][0], 8)
    m["o_w_in"] = f(inp["o_w_in"][0])

    def gl(a):
        a = np.asarray(a, np.float32)
        rest = a.shape[2:]
        a = a.reshape((32, 2, 64) + rest)
        a = np.moveaxis(a, 0, 2)
        return a.reshape((128, 32) + rest)

    lam = np.zeros((128, 2, 3, 32), np.float32)
    for d_ in range(2):
        lam[:, d_, 0] = gl(inp["o_lam_re"][0][d_])
        lam[:, d_, 1] = gl(inp["o_lam_im"][0][d_])
        lam[:, d_, 2] = gl(np.repeat(np.asarray(inp["o_log_step"][0][d_])[:, None], 64, axis=1))
    m["s5_lam"] = f(lam)
    m["s5_b"] = f(np.stack([gl(inp["o_b_re"][0]), gl(inp["o_b_im"][0])], axis=1))
    cr = np.asarray(inp["o_c_re"][0]).transpose(0, 2, 1)
    ci = np.asarray(inp["o_c_im"][0]).transpose(0, 2, 1)
    m["s5_c"] = f(np.stack([gl(cr), gl(ci)], axis=1))
    m["o_d_skip"] = f(inp["o_d_skip"][0]).reshape(1, -1)
    m["o_glu_w"] = f(inp["o_glu_w"][0])
    m["o_glu_b"] = f(inp["o_glu_b"][0]).reshape(1, -1)
    m["o_w_out"] = f(inp["o_w_out"][0])
    m["final_norm_w"] = f(inp["final_norm_w"]).reshape(1, -1)
    return m


def kernel(**inputs):
    nc = build_program()
    in_maps = [prep_inputs(b, inputs) for b in range(8)]
    res = run_bass_kernel_spmd(nc, in_maps, core_ids=list(range(8)))
    return np.stack([r["out"] for r in res.results], axis=0)
```

```python
import math
import os
from contextlib import ExitStack

import numpy as np
import concourse.bass as bass
import concourse.mybir as mybir
from concourse.bass_utils import run_bass_kernel_spmd

F32 = mybir.dt.float32
BF16 = mybir.dt.bfloat16
AF = mybir.ActivationFunctionType
ALU = mybir.AluOpType

D = 1024
T = 2304
NT = 18
CTX = 256
L = 2048
EPS = 1e-6
TG = [(0, 256), (256, 512), (768, 512), (1280, 512), (1792, 512)]

SAME_ENGINE_SYNC = os.environ.get('SES', '1') == '1'
SEM_EPOCH = 30000


class V:
    __slots__ = ("buf", "ap")

    def __init__(self, buf, ap):
        self.buf = buf
        self.ap = ap


class Buf:
    def __init__(self, name, h):
        self.name = name
        self.h = h
        self.last_w = None
        self.readers = []

    def __getitem__(self, idx):
        return V(self, self.h[idx])

    def full(self):
        return V(self, self.h.ap())

    def view(self, offset, pattern):
        return V(self, bass.AP(self.h, offset, [list(p) for p in pattern]))


class Sched:
    ENG = ("pe", "act", "dve", "pool", "sp")

    def __init__(self, nc):
        self.nc = nc
        self.prog = {e: [] for e in self.ENG}
        self.sem = {}
        self.cnt = {}
        self.semid = 0
        self.known = {e: {} for e in self.ENG}
        for e in ("pe", "act", "dve", "pool"):
            self._new_engine_sem(e)
        self.nds = 8
        self.dsem = {}
        self.duse = {}
        self.dcnt = {}
        for q in ("sp", "pool"):
            self.dsem[q] = []
            self.duse[q] = []
            for i in range(self.nds):
                key = "d_%s_%d" % (q, i)
                self.dsem[q].append((nc.alloc_semaphore(key), key))
                self.duse[q].append(0)
            self.dcnt[q] = 0
        self.n_ops = 0

    def _new_engine_sem(self, e):
        self.semid += 1
        key = "s_%s_%d" % (e, self.semid)
        self.sem[e] = (self.nc.alloc_semaphore(key), key)
        self.cnt[e] = 0

    def _deps(self, reads, writes):
        deps = {}

        def add(tok):
            if tok is None:
                return
            h, key, val = tok
            if key not in deps or deps[key][1] < val:
                deps[key] = (h, val)

        for r in reads:
            add(r.buf.last_w)
        for w in writes:
            add(w.buf.last_w)
            for t in w.buf.readers:
                add(t)
        return deps

    def _emit_waits(self, eng, deps, own_key=None):
        kn = self.known[eng]
        for key, (h, val) in deps.items():
            if key == own_key and not SAME_ENGINE_SYNC:
                continue
            if kn.get(key, 0) >= val:
                continue
            kn[key] = val
            self.prog[eng].append(("wait", h, val))

    def _update(self, tok, reads, writes):
        for w in writes:
            w.buf.last_w = tok
            w.buf.readers = []
        for r in reads:
            if r.buf.last_w is not tok:
                r.buf.readers.append(tok)

    def op(self, eng, fn, reads=(), writes=()):
        reads = [r for r in reads if r is not None]
        writes = list(writes)
        if self.cnt[eng] >= SEM_EPOCH:
            self._new_engine_sem(eng)
        h, key = self.sem[eng]
        own = None if eng == "pe" else key
        deps = self._deps(reads, writes)
        if eng == "pe":
            deps.pop(key, None)
        self._emit_waits(eng, deps, own_key=own)
        self.cnt[eng] += 1
        self.prog[eng].append(("op", fn, h, 1))
        tok = (h, key, self.cnt[eng])
        self._update(tok, reads, writes)
        self.n_ops += 1
        return tok

    def dma(self, out, in_, q="sp", **kw):
        deps = self._deps([in_], [out])
        self._emit_waits(q, deps)
        k = self.dcnt[q] % self.nds
        self.dcnt[q] += 1
        h, key = self.dsem[q][k]
        prev = 16 * self.duse[q][k]
        if prev > 0 and self.known[q].get(key, 0) < prev:
            self.known[q][key] = prev
            self.prog[q].append(("wait", h, prev))
        self.duse[q][k] += 1
        val = 16 * self.duse[q][k]
        o_ap, i_ap = out.ap, in_.ap
        self.prog[q].append(("op", lambda e: e.dma_start(out=o_ap, in_=i_ap, **kw), h, 16))
        tok = (h, key, val)
        self._update(tok, [in_], [out])
        self.n_ops += 1
        return tok

    def finish_dmas(self):
        for q in ("sp", "pool"):
            for k in range(self.nds):
                h, key = self.dsem[q][k]
                val = 16 * self.duse[q][k]
                if val > 0 and self.known[q].get(key, 0) < val:
                    self.known[q][key] = val
                    self.prog[q].append(("wait", h, val))

    def flush(self, name=None):
        self.finish_dmas()
        nc = self.nc
        prog = self.prog
        self.prog = {e: [] for e in self.ENG}

        def run(items, e):
            for it in items:
                if it[0] == "wait":
                    e.wait_ge(it[1], it[2])
                else:
                    inst = it[1](e)
                    inst.then_inc(it[2], it[3])

        with nc.Block() as block:
            if prog["sp"]:
                @block.sync
                def _(e):
                    run(prog["sp"], e)
            if prog["act"]:
                @block.scalar
                def _(e):
                    run(prog["act"], e)
            if prog["dve"]:
                @block.vector
                def _(e):
                    run(prog["dve"], e)
            if prog["pool"]:
                @block.gpsimd
                def _(e):
                    run(prog["pool"], e)
            if prog["pe"]:
                @block.tensor
                def _(e):
                    run(prog["pe"], e)

    def mm(self, out, pairs):
        n = len(pairs)

        def fn(e):
            inst = None
            for i, (l, r) in enumerate(pairs):
                inst = e.matmul(out.ap, l.ap, r.ap, start=(i == 0), stop=(i == n - 1))
            return inst

        self.op("pe", fn, reads=[p[0] for p in pairs] + [p[1] for p in pairs], writes=[out])

    def transpose(self, out, in_, ident):
        self.op("pe", lambda e: e.transpose(out.ap, in_.ap, ident.ap), reads=[in_, ident], writes=[out])

    def act(self, out, in_, func, bias=None, scale=None, accum=None):
        kw = {}
        reads = [in_]
        writes = [out]
        if bias is not None:
            if isinstance(bias, V):
                kw["bias"] = bias.ap
                reads.append(bias)
            else:
                kw["bias"] = bias
        if scale is not None:
            if isinstance(scale, V):
                kw["scale"] = scale.ap
                reads.append(scale)
            else:
                kw["scale"] = scale
        if accum is not None:
            kw["accum_out"] = accum.ap
            writes.append(accum)
        self.op("act", lambda e: e.activation(out.ap, in_.ap, func, **kw), reads=reads, writes=writes)

    def ts(self, out, in0, s1, s2, op0, op1=None, eng="dve"):
        reads = [in0]
        a1 = s1
        a2 = s2
        if isinstance(s1, V):
            reads.append(s1)
            a1 = s1.ap
        if isinstance(s2, V):
            reads.append(s2)
            a2 = s2.ap
        if op1 is None:
            self.op(eng, lambda e: e.tensor_scalar(out.ap, in0.ap, a1, a2, op0), reads=reads, writes=[out])
        else:
            self.op(eng, lambda e: e.tensor_scalar(out.ap, in0.ap, a1, a2, op0, op1), reads=reads, writes=[out])

    def tt(self, out, in0, in1, op, eng="dve"):
        self.op(eng, lambda e: e.tensor_tensor(out.ap, in0.ap, in1.ap, op), reads=[in0, in1], writes=[out])

    def stt(self, out, in0, scalar, in1, op0, op1):
        reads = [in0, in1]
        sc = scalar
        if isinstance(scalar, V):
            reads.append(scalar)
            sc = scalar.ap
        self.op("dve", lambda e: e.scalar_tensor_tensor(out.ap, in0.ap, sc, in1.ap, op0, op1),
                reads=reads, writes=[out])

    def copy(self, out, in_, eng="dve"):
        if eng == "act":
            self.op("act", lambda e: e.copy(out.ap, in_.ap), reads=[in_], writes=[out])
        else:
            self.op(eng, lambda e: e.tensor_copy(out.ap, in_.ap), reads=[in_], writes=[out])

    def recip(self, out, in_):
        self.op("dve", lambda e: e.reciprocal(out.ap, in_.ap), reads=[in_], writes=[out])

    def memset(self, out, val, eng="dve"):
        self.op(eng, lambda e: e.memset(out.ap, val), reads=[], writes=[out])


class Ctx:
    def __init__(self, nc, sched):
        self.nc = nc
        self.s = sched
        self.uid = 0

    def sb(self, es, name, shape, dtype=F32):
        self.uid += 1
        h = es.enter_context(self.nc.sbuf_tensor("%s_%d" % (name, self.uid), list(shape), dtype))
        return Buf(name, h)

    def ps(self, es, name, shape=(128, 512), dtype=F32):
        self.uid += 1
        h = es.enter_context(self.nc.psum_tensor("%s_%d" % (name, self.uid), list(shape), dtype))
        return Buf(name, h)

    def dram(self, name, shape, dtype=F32, kind="Internal"):
        h = self.nc.dram_tensor(name, list(shape), dtype, kind=kind)
        return Buf(name, h)


def pipeline(items, stages):
    n, k = len(items), len(stages)
    for t in range(n + k - 1):
        for j in range(k - 1, -1, -1):
            i = t - j
            if 0 <= i < n:
                stages[j](items[i])


def bc_mid(v_buf, base_off, pstep, nparts, n_outer, outer_step, n_inner):
    return v_buf.view(base_off, [[pstep, nparts], [outer_step, n_outer], [0, n_inner]])


E_NCOL = 5152
OFF_Z = 0
OFF_XBC = 1024
OFF_DT = 2560
OFF_Q = 2592
OFF_KV = 3616
OFF_G = 4128
OFF_QS = 5152
OFF_KR = 6176
OFF_KSR = 6688
E_NCOL_EXT = 7200


ORDER = ["p1", "p2a", "p2b", "p2c", "p2d", "p2e", "p2f", "p2g", "p2h", "p3", "p4", "p5", "all"]


def build_program(debug=(), stop="all"):
    def go(tag):
        return ORDER.index(tag) <= ORDER.index(stop)
    nc = bass.Bass("TRN2", target_bir_lowering=False)
    s = Sched(nc)
    cx = Ctx(nc, s)
    dbg = set(debug)

    def din(name, shape):
        return Buf(name, nc.dram_tensor(name, list(shape), F32, kind="ExternalInput"))

    def dout(name, shape):
        return Buf(name, nc.dram_tensor(name, list(shape), F32, kind="ExternalOutput"))

    def scratch(name, shape, dtype=F32):
        if name in dbg:
            return dout(name, shape)
        return Buf(name, nc.dram_tensor(name, list(shape), dtype))

    xin = din("xin", [T, D])
    cvecT = din("cvecT", [128, 2, 8])
    consts = din("consts", [128, 6, 512])
    rope = din("rope", [128, 2, L])
    e_ada_w = din("e_ada_w", [D, 3 * D])
    e_ada_b = din("e_ada_b", [1, 3 * D])
    e_norm_wT = din("e_norm_wT", [128, 8])
    e_w_in = din("e_w_in", [D, E_NCOL_EXT])
    e_conv_wT = din("e_conv_wT", [128, 12, 5])
    e_conv_bT = din("e_conv_bT", [128, 12])
    e_dt_bias = din("e_dt_bias", [1, 32])
    e_a_log = din("e_a_log", [1, 32])
    e_d_skip = din("e_d_skip", [1, 16])
    e_ssd_norm_wT = din("e_ssd_norm_wT", [128, 8])
    e_sink = din("e_sink", [128, 8])
    e_w_out = din("e_w_out", [2 * D, D])
    o_ada_w = din("o_ada_w", [D, 3 * D])
    o_ada_b = din("o_ada_b", [1, 3 * D])
    o_norm_wT = din("o_norm_wT", [128, 8])
    o_w_in = din("o_w_in", [D, 2 * D])
    s5_lam = din("s5_lam", [128, 2, 3, 32])
    s5_b = din("s5_b", [128, 2, 32, 16])
    s5_c = din("s5_c", [128, 2, 32, 16])
    o_d_skip = din("o_d_skip", [1, D])
    o_glu_w = din("o_glu_w", [D, D])
    o_glu_b = din("o_glu_b", [1, D])
    o_w_out = din("o_w_out", [D, D])
    final_norm_w = din("final_norm_w", [1, D])
    out_t = dout("out", [L, D])

    XS = scratch("XS", [T, 1024])
    BTOK = scratch("BTOK", [T, 256], BF16)
    BT = scratch("BT", [2, 128, T], BF16)
    CT = scratch("CT", [2, 128, T], BF16)
    SZ = scratch("SZ", [T, 1024])
    QR = scratch("QR", [8, 128, L], BF16)
    QC = scratch("QC", [8, 128, CTX], BF16)
    KR = scratch("KR", [4, 128, L], BF16)
    KC = scratch("KC", [4, 128, CTX], BF16)
    VT = scratch("VT", [T, 256], BF16)
    SG = scratch("SG", [8, 128, T])
    YF = scratch("YF", [T, 1024])
    YT = scratch("YT", [16, 128, T], BF16)
    X1 = scratch("X1", [T, 1024])
    U = scratch("U", [T, 1024])
    SG1 = scratch("SG1", [T, 1024])
    YTOK = scratch("YTOK", [T, 1024])
    KFP = scratch("KFP", [64, 16, 15, 16])
    KBR = scratch("KBR", [64, 16, 15, 16])
    HT = scratch("HT", [8, 128, T]) if "HT" in dbg else None
    DTD = scratch("DTD", [T, 32]) if "DTD" in dbg else None
    MODD = scratch("MODD", [4, 128, 24]) if "MODD" in dbg else None

    with ExitStack() as top:
        banks = [cx.ps(top, "bank%d" % i) for i in range(8)]
        cst = cx.sb(top, "cst", [128, 6, 512])
        s.dma(cst.full(), consts.full())
        ident = cst[:, 0, 0:128]
        tri = cst[:, 1, 0:128]
        utri = cst[:, 2, 0:128]
        ones = cst[:, 3, 0:128]
        onesb_t = cx.sb(top, "onesb", [128, 128], BF16)
        s.memset(onesb_t.full(), 1.0)
        onesb = onesb_t.full()
        modT = [[cx.sb(top, "modT%d%d" % (l, w), [128, 24]) for w in range(2)] for l in range(2)]
        gate_bc = [[cx.sb(top, "gate%d%d" % (l, w), [128, 1024]) for w in range(2)] for l in range(2)]
        scs = cx.sb(top, "scs", [128, 2, 8])

        def adaln_phase(layer, ada_w, ada_b):
            with ExitStack() as es:
                aw = [cx.sb(es, "aw%d" % k, [128, 3 * D]) for k in range(8)]
                ab = cx.sb(es, "ab", [1, 3 * D])
                modrow = [cx.sb(es, "modrow%d" % w, [1, 3 * D]) for w in range(2)]
                if layer == 0:
                    cv = cx.sb(es, "cv", [128, 2, 8])
                    s.dma(cv.full(), cvecT.full())
                    s.act(scs.full(), cv.full(), AF.Silu)
                for k in range(8):
                    s.dma(aw[k].full(), ada_w[k * 128:(k + 1) * 128, :])
                s.dma(ab.full(), ada_b.full())
                bi = 0
                for w in range(2):
                    for fg in range(6):
                        bk = banks[bi % 8]
                        bi += 1
                        s.mm(bk[0:1, :], [(scs[:, w, k:k + 1], aw[k][:, fg * 512:(fg + 1) * 512]) for k in range(8)])
                        s.tt(modrow[w][0:1, fg * 512:(fg + 1) * 512], bk[0:1, :], ab[0:1, fg * 512:(fg + 1) * 512], ALU.add)
                for w in range(2):
                    bk = banks[bi % 8]
                    bi += 1
                    for fc in range(24):
                        s.mm(bk[:, 2 * fc:2 * fc + 2], [(modrow[w][0:1, fc * 128:(fc + 1) * 128], cst[0:1, 3, 0:2])])
                    s.copy(modT[layer][w].full(), bk.view(0, [[512, 128], [2, 24]]))
                    for hh in range(2):
                        bk2 = banks[bi % 8]
                        bi += 1
                        s.mm(bk2.full(), [(cst[0:1, 3, 0:128], modrow[w][0:1, 2048 + hh * 512:2048 + (hh + 1) * 512])])
                        s.copy(gate_bc[layer][w][:, hh * 512:(hh + 1) * 512], bk2.full(), eng="act")
                    if MODD is not None:
                        s.dma(MODD[layer * 2 + w], modT[layer][w].full())
                s.flush()

        adaln_phase(0, e_ada_w, e_ada_b)

        with ExitStack() as l0:
            DT = cx.sb(l0, "DT", [128, NT, 32])
            DTA = cx.sb(l0, "DTA", [128, NT, 32])
            nw = cx.sb(l0, "nw", [128, 8])
            sc1 = [cx.sb(l0, "sc1_%d" % w, [128, 8]) for w in range(2)]
            s.dma(nw.full(), e_norm_wT.full())
            for w in range(2):
                s.stt(sc1[w].full(), modT[0][w][:, 8:16], 1.0, nw.full(), ALU.add, ALU.mult)

            hts = ExitStack()
            hT = [cx.sb(hts, "hT%d" % k, [128, T], BF16) for k in range(8)]
            with ExitStack() as es:
                xt = [cx.sb(es, "xt%d" % i, [128, D]) for i in range(2)]
                xn = [cx.sb(es, "xn%d" % i, [128, D]) for i in range(2)]
                junk = cx.sb(es, "junk", [128, D])
                st = [cx.sb(es, "st%d" % i, [128, 4]) for i in range(2)]
                for i in range(NT):
                    w = 1 if i < 2 else 0
                    x_ = xt[i % 2]
                    n_ = xn[i % 2]
                    st_ = st[i % 2]
                    s.dma(x_.full(), xin[i * 128:(i + 1) * 128, :])
                    s.act(junk.full(), x_.full(), AF.Square, accum=st_[:, 0:1])
                    s.ts(st_[:, 1:2], st_[:, 0:1], 1.0 / D, EPS, ALU.mult, ALU.add)
                    s.act(st_[:, 2:3], st_[:, 1:2], AF.Sqrt)
                    s.recip(st_[:, 3:4], st_[:, 2:3])
                    s.ts(n_.full(), x_.full(), st_[:, 3:4], None, ALU.mult)
                    for half in range(2):
                        bk = banks[(2 * i + half) % 8]
                        for kk in range(4):
                            k = half * 4 + kk
                            s.transpose(bk[:, kk * 128:(kk + 1) * 128], n_[:, k * 128:(k + 1) * 128], ident)
                        for kk in range(4):
                            k = half * 4 + kk
                            s.act(hT[k][:, i * 128:(i + 1) * 128], bk[:, kk * 128:(kk + 1) * 128], AF.Identity,
                                  bias=modT[0][w][:, k:k + 1], scale=sc1[w][:, k:k + 1])
                if HT is not None:
                    for k in range(8):
                        s.dma(HT[k], hT[k].full())
                s.flush()

            with ExitStack() as es:
                WB = 256
                wbuf = [cx.sb(es, "wbuf%d" % i, [128, 8, WB], BF16) for i in range(4)]
                wstate = {"i": 0}

                def load_w(col0, ncol=WB):
                    wb = wbuf[wstate["i"] % 4]
                    wstate["i"] += 1
                    s.dma(wb[:, :, 0:ncol], e_w_in.view(col0, [[E_NCOL_EXT, 128], [128 * E_NCOL_EXT, 8], [1, ncol]]), q="pool")
                    return wb

                bstate = {"i": 0}

                def nbank():
                    bk = banks[bstate["i"] % 8]
                    bstate["i"] += 1
                    return bk

                def fm_mm(wb, cc, t0, n):
                    bk = nbank()
                    s.mm(bk[:, 0:n], [(wb[:, k, cc * 128:(cc + 1) * 128], hT[k][:, t0:t0 + n]) for k in range(8)])
                    return bk

                xraws = [cx.sb(es, "xraw%d" % i, [128, T]) for i in range(2)]
                accs = [cx.sb(es, "acc%d" % i, [128, T]) for i in range(2)]
                acc = accs[0]
                accbs = [cx.sb(es, "accb%d" % i, [128, T], BF16) for i in range(2)]
                accb = accbs[0]
                rc_i = {"i": 0}
                tmp1s = [cx.sb(es, "tmp1_%d" % i, [128, 512]) for i in range(2)]
                tmp2s = [cx.sb(es, "tmp2_%d" % i, [128, 512]) for i in range(2)]
                stg = [cx.sb(es, "stg%d" % i, [128, 4, 128]) for i in range(2)]
                stgb = [cx.sb(es, "stgb%d" % i, [128, 4, 128], BF16) for i in range(2)]
                rp = cx.sb(es, "rp", [128, 2, L])
                cw = cx.sb(es, "cw", [128, 12, 5])
                cb = cx.sb(es, "cb", [128, 12])
                dtb = cx.sb(es, "dtb", [128, 32])
                abc = cx.sb(es, "abc", [128, 32])
                s.dma(rp.full(), rope.full())
                s.dma(cw.full(), e_conv_wT.full())
                s.dma(cb.full(), e_conv_bT.full())
                s.dma(dtb.full(), e_dt_bias.view(0, [[0, 128], [1, 32]]))
                s.dma(abc.full(), e_a_log.view(0, [[0, 128], [1, 32]]))
                s.act(abc.full(), abc.full(), AF.Exp)
                s.ts(abc.full(), abc.full(), -1.0, None, ALU.mult)
                stg_i = {"i": 0}

                def transposes_to(dst, col0, src, lowp=False):
                    for i0 in range(0, NT, 4):
                        nb = min(4, NT - i0)
                        bk = nbank()
                        for ii in range(nb):
                            i = i0 + ii
                            s.transpose(bk[:, ii * 128:(ii + 1) * 128], src[:, i * 128:(i + 1) * 128], ident)
                        sg_ = (stgb if lowp else stg)[stg_i["i"] % 2]
                        stg_i["i"] += 1
                        s.copy(sg_[:, 0:nb, :], bk.view(0, [[512, 128], [128, nb], [1, 128]]), eng="act")
                        ncols = dst.h.shape[1]
                        s.dma(dst.view(i0 * 128 * ncols + col0, [[ncols, 128], [128 * ncols, nb], [1, 128]]),
                              sg_[:, 0:nb, :])

                for fc in range(12 if go('p2a') else 0):
                    if fc % 2 == 0:
                        wb = load_w(OFF_XBC + fc * 128)
                    cc = fc % 2
                    xraw, acc = xraws[fc % 2], accs[fc % 2]
                    for (t0, n) in TG:
                        bk = fm_mm(wb, cc, t0, n)
                        s.copy(xraw[:, t0:t0 + n], bk[:, 0:n], eng="act")
                    s.ts(acc.full(), xraw.full(), cw[:, fc, 2:3], cb[:, fc:fc + 1], ALU.mult, ALU.add)
                    for kk in (0, 1, 3, 4):
                        d_ = kk - 2
                        for (s0, sl) in ((0, CTX), (CTX, L)):
                            lo = max(s0, s0 - d_)
                            hi = min(s0 + sl, s0 + sl - d_)
                            s.stt(acc[:, lo:hi], xraw[:, lo + d_:hi + d_], cw[:, fc, kk:kk + 1], acc[:, lo:hi],
                                  ALU.mult, ALU.add)
                    s.act(acc.full(), acc.full(), AF.Silu)
                    if fc < 8:
                        transposes_to(XS, fc * 128, acc)
                    elif fc < 10:
                        s.copy(accb.full(), acc.full(), eng="pool")
                        s.dma(BT[fc - 8], accb.full())
                        transposes_to(BTOK, (fc - 8) * 128, acc, lowp=True)
                    else:
                        s.copy(accb.full(), acc.full(), eng="pool")
                        s.dma(CT[fc - 10], accb.full())

                def rope_chunk(col_plain, col_swap, dst_rot, dst_ctx):
                    accb = accbs[rc_i["i"] % 2]
                    rc_i["i"] += 1
                    wa = load_w(col_plain, 128)
                    wsw = load_w(col_swap, 128)
                    for gi, (t0, n) in enumerate(TG):
                        bka = fm_mm(wa, 0, t0, n)
                        if gi == 0:
                            s.copy(accb[:, 0:CTX], bka[:, 0:CTX], eng="act")
                            continue
                        bkb = fm_mm(wsw, 0, t0, n)
                        l0 = t0 - CTX
                        tmp1, tmp2 = tmp1s[gi % 2], tmp2s[gi % 2]
                        s.tt(tmp1.full(), bka.full(), rp[:, 0, l0:l0 + 512], ALU.mult)
                        s.tt(tmp2.full(), bkb.full(), rp[:, 1, l0:l0 + 512], ALU.mult)
                        s.tt(accb[:, t0:t0 + n], tmp1.full(), tmp2.full(), ALU.add, eng="pool")
                    s.dma(dst_ctx, accb[:, 0:CTX])
                    s.dma(dst_rot, accb[:, CTX:T])

                for qc in range(8 if go('p2b') else 0):
                    rope_chunk(OFF_Q + qc * 128, OFF_QS + qc * 128, QR[qc], QC[qc])
                for j in range(4 if go('p2c') else 0):
                    rope_chunk(OFF_KR + j * 128, OFF_KSR + j * 128, KR[j], KC[j])

                for gc in range(8 if go('p2d') else 0):
                    acc = accs[gc % 2]
                    if gc % 2 == 0:
                        wb = load_w(OFF_G + gc * 128)
                    for (t0, n) in TG:
                        bk = fm_mm(wb, gc % 2, t0, n)
                        s.act(acc[:, t0:t0 + n], bk[:, 0:n], AF.Silu)
                    s.dma(SG[gc], acc.full())

                NT_E = NT if go('p2e') else 0
                wz = [load_w(OFF_Z + i * 256) for i in range(4)]
                zt = [cx.sb(es, "zt%d" % i, [128, D]) for i in range(2)]
                for i in range(NT_E):
                    z_ = zt[i % 2]
                    for half in range(2):
                        bk = nbank()
                        for q4 in range(2):
                            wbz = wz[half * 2 + q4]
                            s.mm(bk[:, q4 * 256:(q4 + 1) * 256],
                                 [(hT[k][:, i * 128:(i + 1) * 128], wbz[:, k, :]) for k in range(8)])
                        s.act(z_[:, half * 512:(half + 1) * 512], bk.full(), AF.Silu)
                    s.dma(SZ[i * 128:(i + 1) * 128, :], z_.full())
                wv = load_w(OFF_KV + 256)
                wdt = load_w(OFF_DT, 32)
                vt = [cx.sb(es, "vt%d" % i, [128, 256], BF16) for i in range(2)]
                for i in range(NT if go('p2f') else 0):
                    bk = nbank()
                    s.mm(bk[:, 0:256], [(hT[k][:, i * 128:(i + 1) * 128], wv[:, k, :]) for k in range(8)])
                    s.copy(vt[i % 2].full(), bk[:, 0:256], eng="act")
                    s.dma(VT[i * 128:(i + 1) * 128, :], vt[i % 2].full())
                for i in range(NT if go('p2g') else 0):
                    bk = nbank()
                    s.mm(bk[:, 0:32], [(hT[k][:, i * 128:(i + 1) * 128], wdt[:, k, 0:32]) for k in range(8)])
                    s.tt(DT[:, i, :], bk[:, 0:32], dtb.full(), ALU.add)
                    if go('p2h'):
                        s.act(DT[:, i, :], DT[:, i, :], AF.Exp)
                        s.ts(DT[:, i, :], DT[:, i, :], 1.0, None, ALU.add)
                        s.act(DT[:, i, :], DT[:, i, :], AF.Ln)
                    s.tt(DTA[:, i, :], DT[:, i, :], abc.full(), ALU.mult)
                    if DTD is not None:
                        s.dma(DTD[i * 128:(i + 1) * 128, :], DT[:, i, :])
                s.flush()
            hts.close()

            with ExitStack() as es:
                nb_ = {"i": 0}

                def nbank():
                    bk = banks[nb_["i"] % 8]
                    nb_["i"] += 1
                    return bk

                N3 = 3
                xs_t = [cx.sb(es, "xs_t%d" % i, [128, 1024]) for i in range(N3)]
                b_t = [cx.sb(es, "b_t%d" % i, [128, 256], BF16) for i in range(N3)]
                bt_t = [cx.sb(es, "bt_t%d" % i, [128, 2, 128], BF16) for i in range(N3)]
                ct_t = [cx.sb(es, "ct_t%d" % i, [128, 2, 128], BF16) for i in range(N3)]
                yf_t = [cx.sb(es, "yf_t%d" % i, [128, 1024]) for i in range(N3)]
                sz_t = [cx.sb(es, "sz_t%d" % i, [128, 1024]) for i in range(N3)]
                MTs = [cx.sb(es, "MT%d" % i, [128, 2048], BF16) for i in range(N3)]
                xcs = [cx.sb(es, "xc%d" % i, [128, 1024], BF16) for i in range(N3)]
                xcds = [cx.sb(es, "xcd%d" % i, [128, 1024], BF16) for i in range(N3)]
                tmpos = [cx.sb(es, "tmpo%d" % i, [128, 1024]) for i in range(N3)]
                ytots = [cx.sb(es, "ytot%d" % i, [128, 1024]) for i in range(N3)]
                sms = [cx.sb(es, "sm%d" % i, [128, 4, 16]) for i in range(N3)]
                st3s = [cx.sb(es, "st3_%d" % i, [128, 4]) for i in range(N3)]
                ystgs = [cx.sb(es, "ystg%d" % i, [128, 8, 128], BF16) for i in range(2)]
                dtatris = [cx.sb(es, "dtatri%d" % i, [128, 2048]) for i in range(2)]
                decTs = [cx.sb(es, "decT%d" % i, [128, 2048]) for i in range(2)]
                cb_sbs = [cx.sb(es, "cb_sb%d" % i, [128, 256]) for i in range(2)]
                tsks = [cx.sb(es, "tsk%d" % i, [128, 1024]) for i in range(2)]
                junk = cx.sb(es, "junk3", [128, 1024])
                Hs = [cx.sb(es, "Hs%d" % g, [128, 512]) for g in range(2)]
                Hb = [cx.sb(es, "Hb%d" % g, [128, 512], BF16) for g in range(2)]
                dsk = cx.sb(es, "dsk", [128, 16])
                snw = cx.sb(es, "snw", [128, 8])
                s.dma(dsk.full(), e_d_skip.view(0, [[0, 128], [1, 16]]))
                s.dma(snw.full(), e_ssd_norm_wT.full())

                def bc3(buf, off, pstep, n1, s1, n2, s2):
                    return buf.view(off, [[pstep, 128], [s1, n1], [s2, n2]])

                n_ch = NT if go("p3") else 0
                for d_ in range(2):
                    order = list(range(NT)) if d_ == 0 else [1, 0] + list(range(NT - 1, 1, -1))
                    order = order[:n_ch]
                    TRIoff = 512 if d_ == 0 else 1024
                    TRIv = tri if d_ == 0 else utri
                    negm = cst[:, 4 + d_, :]
                    for g in range(2):
                        s.memset(Hs[g].full(), 0.0)
                        s.memset(Hb[g].full(), 0.0)

                    def stA(item, d_=d_, TRIoff=TRIoff, TRIv=TRIv, negm=negm):
                        ci, i = item
                        p3, p2 = ci % N3, ci % 2
                        xs_, b_, bt_, ct_ = xs_t[p3], b_t[p3], bt_t[p3], ct_t[p3]
                        MT, xc, xcd, sm = MTs[p3], xcs[p3], xcds[p3], sms[p3]
                        dtatri, decT, cb_sb = dtatris[p2], decTs[p2], cb_sbs[p2]
                        s.dma(xs_.full(), XS[i * 128:(i + 1) * 128, :])
                        s.dma(b_.full(), BTOK[i * 128:(i + 1) * 128, :])
                        s.dma(bt_.full(), BT.view(i * 128, [[T, 128], [128 * T, 2], [1, 128]]))
                        s.dma(ct_.full(), CT.view(i * 128, [[T, 128], [128 * T, 2], [1, 128]]))
                        if d_ == 1:
                            s.dma(yf_t[p3].full(), YF[i * 128:(i + 1) * 128, :])
                            s.dma(sz_t[p3].full(), SZ[i * 128:(i + 1) * 128, :])
                        dta_i = DTA[:, i, d_ * 16:(d_ + 1) * 16]
                        doff = i * 32 + d_ * 16
                        s.tt(bc3(dtatri, 0, 2048, 16, 128, 128, 1), bc3(DTA, doff, NT * 32, 16, 1, 128, 0),
                             bc3(cst, TRIoff, 3072, 16, 0, 128, 1), ALU.mult, eng="pool")
                        bs = nbank()
                        s.mm(bs[:, 0:16], [(TRIv, dta_i)])
                        s.mm(bs[:, 16:32], [(ones, dta_i)])
                        na, ea, de, cd = sm[:, 0, :], sm[:, 1, :], sm[:, 2, :], sm[:, 3, :]
                        s.ts(na, bs[:, 0:16], -1.0, None, ALU.mult)
                        s.act(ea, bs[:, 0:16], AF.Exp)
                        s.tt(de, bs[:, 16:32], na, ALU.add)
                        s.act(de, de, AF.Exp)
                        s.act(cd, bs[:, 16:32], AF.Exp)
                        for hq in range(4):
                            bq = nbank()
                            s.mm(bq.full(), [(ones, dtatri[:, hq * 512:(hq + 1) * 512]), (ident, negm)])
                            for hh in range(4):
                                h = hq * 4 + hh
                                s.act(decT[:, h * 128:(h + 1) * 128], bq[:, hh * 128:(hh + 1) * 128], AF.Exp,
                                      bias=sm[:, 0, h:h + 1])
                        bc = nbank()
                        for g in range(2):
                            s.mm(bc[:, g * 128:(g + 1) * 128], [(bt_[:, g, :], ct_[:, g, :])])
                        s.copy(cb_sb.full(), bc[:, 0:256], eng="act")
                        for g in range(2):
                            s.tt(bc3(MT, g * 1024, 2048, 8, 128, 128, 1), bc3(decT, g * 1024, 2048, 8, 128, 128, 1),
                                 bc3(cb_sb, g * 128, 256, 8, 0, 128, 1), ALU.mult)
                        s.tt(bc3(xc, 0, 1024, 16, 64, 64, 1), bc3(xs_, 0, 1024, 16, 64, 64, 1),
                             bc3(DT, doff, NT * 32, 16, 1, 64, 0), ALU.mult, eng="pool")
                        s.tt(bc3(xcd, 0, 1024, 16, 64, 64, 1), bc3(xc, 0, 1024, 16, 64, 64, 1),
                             bc3(sm, 32, 64, 16, 1, 64, 0), ALU.mult, eng="pool")
                        if d_ == 1:
                            s.tt(bc3(tsks[p2], 0, 1024, 16, 64, 64, 1), bc3(xs_, 0, 1024, 16, 64, 64, 1),
                                 bc3(dsk, 0, 16, 16, 1, 64, 0), ALU.mult, eng="pool")
                            s.tt(tsks[p2].full(), tsks[p2].full(), yf_t[p3].full(), ALU.add, eng="pool")

                    def stB(item, d_=d_):
                        ci, i = item
                        p3 = ci % N3
                        b_, ct_ = b_t[p3], ct_t[p3]
                        MT, xc, xcd, sm, tmpo, ytot = MTs[p3], xcs[p3], xcds[p3], sms[p3], tmpos[p3], ytots[p3]
                        ydst = yf_t[p3] if d_ == 0 else ytot
                        for g in range(2):
                            by = nbank()
                            for hh in range(8):
                                h = g * 8 + hh
                                s.mm(by[:, hh * 64:(hh + 1) * 64], [(MT[:, h * 128:(h + 1) * 128], xc[:, h * 64:(h + 1) * 64])])
                            bo = nbank()
                            s.mm(bo.full(), [(ct_[:, g, :], Hb[g].full())])
                            s.tt(bc3(tmpo, g * 512, 1024, 8, 64, 64, 1), bc3(bo, 0, 512, 8, 64, 64, 1),
                                 bc3(sm, 16 + g * 8, 64, 8, 1, 64, 0), ALU.mult)
                            s.tt(ydst[:, g * 512:(g + 1) * 512], by.full(), tmpo[:, g * 512:(g + 1) * 512], ALU.add)
                        for g in range(2):
                            bst = nbank()
                            s.mm(bst.full(), [(b_[:, g * 128:(g + 1) * 128], xcd[:, g * 512:(g + 1) * 512])])
                            s.tt(bc3(Hs[g], 0, 512, 8, 64, 64, 1), bc3(Hs[g], 0, 512, 8, 64, 64, 1),
                                 bc3(sm, 48 + g * 8, 64, 8, 1, 64, 0), ALU.mult)
                            s.tt(Hs[g].full(), Hs[g].full(), bst.full(), ALU.add)
                            s.copy(Hb[g].full(), Hs[g].full(), eng="act")
                        if d_ == 0:
                            s.dma(YF[i * 128:(i + 1) * 128, :], yf_t[p3].full())

                    def stC(item, d_=d_):
                        if d_ == 0:
                            return
                        ci, i = item
                        p3, p2 = ci % N3, ci % 2
                        ytot, sz_, st3, ystg = ytots[p3], sz_t[p3], st3s[p3], ystgs[p2]
                        s.tt(ytot.full(), ytot.full(), tsks[p2].full(), ALU.add)
                        s.tt(ytot.full(), ytot.full(), sz_.full(), ALU.mult)
                        s.act(junk.full(), ytot.full(), AF.Square, accum=st3[:, 0:1])
                        s.ts(st3[:, 1:2], st3[:, 0:1], 1.0 / 1024, EPS, ALU.mult, ALU.add)
                        s.act(st3[:, 2:3], st3[:, 1:2], AF.Sqrt)
                        s.recip(st3[:, 3:4], st3[:, 2:3])
                        s.act(ytot.full(), ytot.full(), AF.Copy, scale=st3[:, 3:4])
                        for half in range(2):
                            bk = nbank()
                            for kk in range(4):
                                k = half * 4 + kk
                                s.transpose(bk[:, kk * 128:(kk + 1) * 128], ytot[:, k * 128:(k + 1) * 128], ident)
                            for kk in range(4):
                                k = half * 4 + kk
                                s.act(ystg[:, k, :], bk[:, kk * 128:(kk + 1) * 128], AF.Copy, scale=snw[:, k:k + 1])
                        s.dma(YT.view(i * 128, [[T, 128], [128 * T, 8], [1, 128]]), ystg.full())

                    pipeline(list(enumerate(order)), [stA, stB, stC])
                s.flush()

            with ExitStack() as es:
                nb_ = {"i": 0}

                def nbank():
                    bk = banks[nb_["i"] % 8]
                    nb_["i"] += 1
                    return bk

                qr_t = cx.sb(es, "qr_t", [128, 2, L], BF16)
                qc_t = cx.sb(es, "qc_t", [128, 2, CTX], BF16)
                kr_t = cx.sb(es, "kr_t", [128, L], BF16)
                kc_t = cx.sb(es, "kc_t", [128, CTX], BF16)
                v_t = cx.sb(es, "v_t", [128, NT, 64], BF16)
                v2 = cx.sb(es, "v2", [128, NT, 128], BF16)
                sg_t = cx.sb(es, "sg_t", [128, 2, T])
                ast = cx.sb(es, "ast", [128, 2, T], BF16)
                pt = [[cx.sb(es, "pt%d_%d" % (a, b), [128, 512], BF16) for b in range(5)] for a in range(2)]
                rds = [cx.sb(es, "rd%d" % i, [128, 256]) for i in range(2)]
                aos = [cx.sb(es, "ao%d" % i, [128, 256]) for i in range(2)]
                es_pp = cx.sb(es, "es_pp", [128, 8])
                c8 = cx.sb(es, "c8", [128, 1])
                s.memset(c8.full(), 0.125)
                s.dma(es_pp.full(), e_sink.full())
                s.act(es_pp.full(), es_pp.full(), AF.Exp)
                qb_i = 0
                ATT_DBG = [int(v) for v in os.environ.get("ATT_DBG", "4,18,4").split(",")]
                for j in range(ATT_DBG[0] if go("p4") else 0):
                    s.dma(qr_t.full(), QR.view(2 * j * 128 * L, [[L, 128], [128 * L, 2], [1, L]]))
                    s.dma(qc_t.full(), QC.view(2 * j * 128 * CTX, [[CTX, 128], [128 * CTX, 2], [1, CTX]]))
                    s.dma(kr_t.full(), KR[j])
                    s.dma(kc_t.full(), KC[j])
                    s.dma(v_t.full(), VT.view(j * 64, [[256, 128], [128 * 256, NT], [1, 64]]))
                    s.dma(sg_t.full(), SG.view(2 * j * 128 * T, [[T, 128], [128 * T, 2], [1, T]]))
                    s.copy(v2[:, :, 0:64], v_t.full(), eng="act")
                    s.copy(v2[:, :, 64:128], v_t.full(), eng="pool")
                    for kind, bi in ([("c", 0), ("c", 1)] + [("l", b) for b in range(16)])[:ATT_DBG[1]]:
                        if kind == "c":
                            qsrc, q0, tok0 = qc_t, bi * 128, bi * 128
                            keys = [("c", 0, None), ("c", 1, None)]
                        else:
                            qsrc, q0, tok0 = qr_t, bi * 128, CTX + bi * 128
                            keys = [("c", 0, None), ("c", 1, None)]
                            if bi > 0:
                                keys.append(("l", bi - 1, "prev"))
                            keys.append(("l", bi, None))
                            if bi < 15:
                                keys.append(("l", bi + 1, "next"))
                        pts = pt[qb_i % 2]
                        rd, ao = rds[qb_i % 2], aos[qb_i % 2]
                        qb_i += 1
                        qw = qsrc.h.shape[2]
                        for ki, (kk, kb, msk) in enumerate(keys):
                            ksrc = kc_t if kk == "c" else kr_t
                            for par in range(2):
                                p0 = par * 64
                                bs = nbank()
                                s.mm(bs[:, 0:256],
                                     [(ksrc[p0:p0 + 64, kb * 128:(kb + 1) * 128],
                                       qsrc.view(p0 * 2 * qw + q0, [[2 * qw, 64], [qw, 2], [1, 128]]))])
                                s.act(pts[ki][:, par * 256:(par + 1) * 256], bs[:, 0:256], AF.Exp, scale=c8[:, 0:1])
                            if msk is not None and ATT_DBG[2] >= 2:
                                moff = 1024 if msk == "prev" else 512
                                s.tt(pts[ki].view(0, [[512, 128], [128, 4], [1, 128]]),
                                     pts[ki].view(0, [[512, 128], [128, 4], [1, 128]]),
                                     cst.view(moff, [[3072, 128], [0, 4], [1, 128]]), ALU.mult)
                        if ATT_DBG[2] < 3:
                            continue
                        vt_idx = [(kb if kk == "c" else 2 + kb) for (kk, kb, _) in keys]
                        bn = nbank()
                        s.mm(bn.full(), [(v2[:, vt_idx[ki], :], pts[ki].full()) for ki in range(len(keys))])
                        bd = nbank()
                        s.mm(bd.full(), [(onesb, pts[ki].full()) for ki in range(len(keys))])
                        if ATT_DBG[2] < 4:
                            continue
                        for par in range(2):
                            p0 = par * 64
                            for c in range(2):
                                s.ts(rd[p0:p0 + 64, c * 128:(c + 1) * 128],
                                     bd[p0:p0 + 64, par * 256 + c * 128:par * 256 + (c + 1) * 128],
                                     es_pp[p0:p0 + 64, 2 * j + c:2 * j + c + 1], None, ALU.add)
                        s.recip(rd.full(), rd.full())
                        for par in range(2):
                            p0 = par * 64
                            s.tt(ao[p0:p0 + 64, :], bn[p0:p0 + 64, par * 256:(par + 1) * 256], rd[p0:p0 + 64, :], ALU.mult)
                        s.tt(ast.view(tok0, [[2 * T, 128], [T, 2], [1, 128]]),
                             ao.view(0, [[256, 128], [128, 2], [1, 128]]),
                             sg_t.view(tok0, [[2 * T, 128], [T, 2], [1, 128]]), ALU.mult)
                    s.dma(YT.view((8 + 2 * j) * 128 * T, [[T, 128], [128 * T, 2], [1, T]]), ast.full())
                s.flush()

            with ExitStack() as es:
                nb_ = {"i": 0}

                def nbank():
                    bk = banks[nb_["i"] % 8]
                    nb_["i"] += 1
                    return bk

                wo = [cx.sb(es, "wo%d" % k, [128, D], BF16) for k in range(16)]
                yt = [cx.sb(es, "yt%d" % i, [128, 16, 128], BF16) for i in range(2)]
                xt = [cx.sb(es, "xt5_%d" % i, [128, D]) for i in range(2)]
                x1t = [cx.sb(es, "x1t%d" % i, [128, D]) for i in range(2)]
                tmp5s = [cx.sb(es, "tmp5_%d" % i, [128, 512]) for i in range(2)]
                for k in range(16):
                    s.dma(wo[k].full(), e_w_out[k * 128:(k + 1) * 128, :], q="pool")
                for i in range(NT if go("p5") else 0):
                    w = 1 if i < 2 else 0
                    y_, x_, o_ = yt[i % 2], xt[i % 2], x1t[i % 2]
                    s.dma(y_.full(), YT.view(i * 128, [[T, 128], [128 * T, 16], [1, 128]]))
                    s.dma(x_.full(), xin[i * 128:(i + 1) * 128, :])
                    for half in range(2):
                        tmp5 = tmp5s[half]
                        bk = nbank()
                        s.mm(bk.full(), [(y_[:, fc, :], wo[fc][:, half * 512:(half + 1) * 512]) for fc in range(16)])
                        s.tt(tmp5.full(), bk.full(), gate_bc[0][w][:, half * 512:(half + 1) * 512], ALU.mult)
                        s.tt(o_[:, half * 512:(half + 1) * 512], tmp5.full(), x_[:, half * 512:(half + 1) * 512], ALU.add)
                    s.dma(X1[i * 128:(i + 1) * 128, :], o_.full())
                s.flush()

        if go("all"):
            adaln_phase(1, o_ada_w, o_ada_b)
        with ExitStack() as l1:
            if not go("all"):
                return nc
            nb_ = {"i": 0}

            def nbank():
                bk = banks[nb_["i"] % 8]
                nb_["i"] += 1
                return bk

            with ExitStack() as es:
                nw = cx.sb(es, "nw1", [128, 8])
                sc1 = [cx.sb(es, "sc1b_%d" % w, [128, 8]) for w in range(2)]
                s.dma(nw.full(), o_norm_wT.full())
                for w in range(2):
                    s.stt(sc1[w].full(), modT[1][w][:, 8:16], 1.0, nw.full(), ALU.add, ALU.mult)
                hT = [cx.sb(es, "hTb%d" % k, [128, T], BF16) for k in range(8)]
                xt = [cx.sb(es, "xtb%d" % i, [128, D]) for i in range(2)]
                xn = [cx.sb(es, "xnb%d" % i, [128, D]) for i in range(2)]
                junk = cx.sb(es, "junkb", [128, D])
                st = [cx.sb(es, "stb%d" % i, [128, 4]) for i in range(2)]
                for i in range(NT):
                    w = 1 if i < 2 else 0
                    x_, n_, st_ = xt[i % 2], xn[i % 2], st[i % 2]
                    s.dma(x_.full(), X1[i * 128:(i + 1) * 128, :])
                    s.act(junk.full(), x_.full(), AF.Square, accum=st_[:, 0:1])
                    s.ts(st_[:, 1:2], st_[:, 0:1], 1.0 / D, EPS, ALU.mult, ALU.add)
                    s.act(st_[:, 2:3], st_[:, 1:2], AF.Sqrt)
                    s.recip(st_[:, 3:4], st_[:, 2:3])
                    s.ts(n_.full(), x_.full(), st_[:, 3:4], None, ALU.mult)
                    for half in range(2):
                        bk = nbank()
                        for kk in range(4):
                            k = half * 4 + kk
                            s.transpose(bk[:, kk * 128:(kk + 1) * 128], n_[:, k * 128:(k + 1) * 128], ident)
                        for kk in range(4):
                            k = half * 4 + kk
                            s.act(hT[k][:, i * 128:(i + 1) * 128], bk[:, kk * 128:(kk + 1) * 128], AF.Identity,
                                  bias=modT[1][w][:, k:k + 1], scale=sc1[w][:, k:k + 1])
                wq = [cx.sb(es, "wq%d" % i, [128, 8, 256], BF16) for i in range(4)]
                ot = [cx.sb(es, "ot%d" % i, [128, D]) for i in range(2)]
                oi = 0
                for which in range(2):
                    for q4 in range(4):
                        s.dma(wq[q4].full(), o_w_in.view(which * 1024 + q4 * 256, [[2 * D, 128], [128 * 2 * D, 8], [1, 256]]), q="pool")
                    for i in range(NT):
                        if which == 1 and i < 2:
                            continue
                        o_ = ot[oi % 2]
                        oi += 1
                        for half in range(2):
                            bk = nbank()
                            for q4 in range(2):
                                s.mm(bk[:, q4 * 256:(q4 + 1) * 256],
                                     [(hT[k][:, i * 128:(i + 1) * 128], wq[half * 2 + q4][:, k, :]) for k in range(8)])
                            if which == 0:
                                s.copy(o_[:, half * 512:(half + 1) * 512], bk.full(), eng="act")
                            else:
                                s.act(o_[:, half * 512:(half + 1) * 512], bk.full(), AF.Silu)
                        s.dma((U if which == 0 else SG1)[i * 128:(i + 1) * 128, :], o_.full())
                s.flush()

            L1S = os.environ.get('L1S', 'z')
            if L1S == 'a':
                return nc
            with ExitStack() as es:
                lam = cx.sb(es, "lam", [128, 2, 3, 32])
                bprm = cx.sb(es, "bprm", [128, 2, 32, 16])
                cprm = cx.sb(es, "cprm", [128, 2, 32, 16])
                s.dma(lam.full(), s5_lam.full())
                s.dma(bprm.full(), s5_b.full())
                s.dma(cprm.full(), s5_c.full())
                kc = cx.sb(es, "kconst", [128, 4])
                s.memset(kc[:, 0:1], 1.0 / 16)
                s.memset(kc[:, 1:2], math.pi / 2)
                s.memset(kc[:, 2:3], 0.0)
                s.memset(kc[:, 3:4], 1.0)
                W64 = [128, 2, 32]

                def t64(name):
                    return cx.sb(es, name, W64)

                def lv(i):
                    return lam.view(i * 32, [[192, 128], [96, 2], [1, 32]])

                dt_ = t64("dt_"); mag = t64("mag"); th = t64("th"); cs = t64("cs"); sn = t64("sn")
                t_a = t64("t_a"); t_b = t64("t_b"); t_c = t64("t_c")
                abre = t64("abre"); abim = t64("abim"); cre = t64("cre"); cim = t64("cim")
                s.act(dt_.full(), lv(2), AF.Exp)
                s.tt(t_a.full(), lv(0), dt_.full(), ALU.mult)
                s.act(mag.full(), t_a.full(), AF.Exp)
                s.tt(th.full(), lv(1), dt_.full(), ALU.mult)
                s.act(sn.full(), th.full(), AF.Sin, scale=kc[:, 0:1])
                s.act(cs.full(), th.full(), AF.Sin, scale=kc[:, 0:1], bias=kc[:, 1:2])
                for _ in range(4):
                    s.tt(t_a.full(), cs.full(), cs.full(), ALU.mult)
                    s.tt(t_b.full(), sn.full(), sn.full(), ALU.mult)
                    s.tt(t_c.full(), sn.full(), cs.full(), ALU.mult)
                    s.tt(cs.full(), t_a.full(), t_b.full(), ALU.subtract)
                    s.ts(sn.full(), t_c.full(), 2.0, None, ALU.mult)
                s.tt(abre.full(), mag.full(), cs.full(), ALU.mult)
                s.tt(abim.full(), mag.full(), sn.full(), ALU.mult)
                PW = cx.sb(es, "PW", [128, 2, 9, 64])

                def pw(ri, k):
                    return PW.view((ri * 9 + k) * 64, [[2 * 9 * 64, 128], [32, 2], [1, 32]])

                s.memset(PW[:, 0, 0, :], 1.0)
                s.memset(PW[:, 1, 0, :], 0.0)
                for k in range(8):
                    s.tt(t_a.full(), pw(0, k), abre.full(), ALU.mult)
                    s.tt(t_b.full(), pw(1, k), abim.full(), ALU.mult)
                    s.tt(pw(0, k + 1), t_a.full(), t_b.full(), ALU.subtract)
                    s.tt(t_a.full(), pw(0, k), abim.full(), ALU.mult)
                    s.tt(t_b.full(), pw(1, k), abre.full(), ALU.mult)
                    s.tt(pw(1, k + 1), t_a.full(), t_b.full(), ALU.add)
                s.ts(t_c.full(), abre.full(), -1.0, None, ALU.add)
                s.tt(t_a.full(), lv(0), lv(0), ALU.mult)
                s.tt(t_b.full(), lv(1), lv(1), ALU.mult)
                s.tt(t_a.full(), t_a.full(), t_b.full(), ALU.add)
                s.recip(dt_.full(), t_a.full())
                s.tt(t_a.full(), t_c.full(), lv(0), ALU.mult)
                s.tt(t_b.full(), abim.full(), lv(1), ALU.mult)
                s.tt(t_a.full(), t_a.full(), t_b.full(), ALU.add)
                s.tt(cre.full(), t_a.full(), dt_.full(), ALU.mult)
                s.tt(t_a.full(), abim.full(), lv(0), ALU.mult)
                s.tt(t_b.full(), t_c.full(), lv(1), ALU.mult)
                s.tt(t_a.full(), t_a.full(), t_b.full(), ALU.subtract)
                s.tt(cim.full(), t_a.full(), dt_.full(), ALU.mult)
                BB = cx.sb(es, "BB", [128, 2, 2, 512])
                tb1 = cx.sb(es, "tb1", [128, 512])
                tb2 = cx.sb(es, "tb2", [128, 512])

                def bb(ri, d_, g0=0, ng=32):
                    return BB.view((ri * 2 + d_) * 512 + g0 * 16, [[2048, 128], [16, ng], [1, 16]])

                def v3(buf, off, pstep, n1, s1, n2, s2):
                    return buf.view(off, [[pstep, 128], [s1, n1], [s2, n2]])

                def prm(buf, ri, g0=0, ng=32):
                    return buf.view(ri * 512 + g0 * 16, [[1024, 128], [16, ng], [1, 16]])

                def cf(buf, d_, g0=0, ng=32, n2=16):
                    return buf.view(d_ * 32 + g0, [[64, 128], [1, ng], [0, n2]])

                t1v = v3(tb1, 0, 512, 32, 16, 16, 1)
                t2v = v3(tb2, 0, 512, 32, 16, 16, 1)
                for d_ in range(2):
                    s.tt(t1v, prm(bprm, 0), cf(cre, d_), ALU.mult)
                    s.tt(t2v, prm(bprm, 1), cf(cim, d_), ALU.mult)
                    s.tt(bb(0, d_), t1v, t2v, ALU.subtract)
                    s.tt(t1v, prm(bprm, 1), cf(cre, d_), ALU.mult)
                    s.tt(t2v, prm(bprm, 0), cf(cim, d_), ALU.mult)
                    s.tt(bb(1, d_), t1v, t2v, ALU.add)
                LA = cx.sb(es, "LA", [128, 2, 32, 2])
                LB = cx.sb(es, "LB", [128, 2, 32, 2])
                for ri in range(2):
                    s.copy(LA.view(ri, [[128, 128], [64, 2], [2, 32]]), pw(0, 8))
                s.ts(LB.view(0, [[128, 128], [64, 2], [2, 32]]), pw(1, 8), -1.0, None, ALU.mult)
                s.copy(LB.view(1, [[128, 128], [64, 2], [2, 32]]), pw(1, 8))
                zt_ = cx.sb(es, "zt_", [16, 16, 112])
                s.memset(zt_.full(), 0.0)
                s.flush()

                if L1S == 'b':
                    return nc
                for b in range(4 if L1S not in ('c1', 'd1', 'e1', 'f1', 'g1') else 1):
                    g0 = 8 * b
                    with ExitStack() as bs_:
                        CAB = cx.sb(bs_, "CAB", [128, 2, 2, 8 * 144])
                        WST = cx.sb(bs_, "WST", [128, 8, 2, 2, 2, 64])
                        TF = cx.sb(bs_, "TF", [128, 16, 128])
                        TB = cx.sb(bs_, "TB", [128, 16, 128])

                        with ExitStack() as tmp:
                            WT = cx.sb(tmp, "WT", [128, 2, 2, 8 * 128])
                            KSB = cx.sb(tmp, "KSB", [16, 2, 16, 128])
                            c1 = cx.sb(tmp, "c1", [128, 128])
                            c2 = cx.sb(tmp, "c2", [128, 128])
                            c1v = v3(c1, 0, 128, 8, 16, 16, 1)
                            c2v = v3(c2, 0, 128, 8, 16, 16, 1)
                            for d_ in range(2):
                                for idx in range(9):
                                    p_ = idx if d_ == 0 else 8 - idx
                                    pr = PW.view((0 * 9 + p_) * 64 + d_ * 32 + g0, [[1152, 128], [1, 8], [0, 16]])
                                    pi_ = PW.view((1 * 9 + p_) * 64 + d_ * 32 + g0, [[1152, 128], [1, 8], [0, 16]])
                                    o_re = CAB.view((0 * 2 + d_) * 1152 + idx * 16, [[4608, 128], [144, 8], [1, 16]])
                                    o_im = CAB.view((1 * 2 + d_) * 1152 + idx * 16, [[4608, 128], [144, 8], [1, 16]])
                                    s.tt(c1v, prm(cprm, 0, g0, 8), pr, ALU.mult)
                                    s.tt(c2v, prm(cprm, 1, g0, 8), pi_, ALU.mult)
                                    s.tt(o_re, c1v, c2v, ALU.subtract)
                                    s.tt(c1v, prm(cprm, 0, g0, 8), pi_, ALU.mult)
                                    s.tt(c2v, prm(cprm, 1, g0, 8), pr, ALU.mult)
                                    s.stt(o_im, c1v, -1.0, c2v, ALU.mult, ALU.subtract)
                                for ss in range(8):
                                    p_ = 7 - ss if d_ == 0 else ss
                                    pr = PW.view((0 * 9 + p_) * 64 + d_ * 32 + g0, [[1152, 128], [1, 8], [0, 16]])
                                    pi_ = PW.view((1 * 9 + p_) * 64 + d_ * 32 + g0, [[1152, 128], [1, 8], [0, 16]])
                                    o_re = WT.view((d_ * 2 + 0) * 1024 + ss * 16, [[4096, 128], [128, 8], [1, 16]])
                                    o_im = WT.view((d_ * 2 + 1) * 1024 + ss * 16, [[4096, 128], [128, 8], [1, 16]])
                                    s.tt(c1v, bb(0, d_, g0, 8), pr, ALU.mult)
                                    s.tt(c2v, bb(1, d_, g0, 8), pi_, ALU.mult)
                                    s.tt(o_re, c1v, c2v, ALU.subtract)
                                    s.tt(c1v, bb(1, d_, g0, 8), pr, ALU.mult)
                                    s.tt(c2v, bb(0, d_, g0, 8), pi_, ALU.mult)
                                    s.tt(o_im, c1v, c2v, ALU.add)
                            for gh in range(2):
                                p0 = gh * 64
                                for gq in range(8):
                                    bk = nbank()
                                    for d_ in range(2):
                                        for ri in range(2):
                                            sl = d_ * 2 + ri
                                            s.transpose(bk[:, sl * 64:(sl + 1) * 64],
                                                        WT.view(p0 * 4096 + (d_ * 2 + ri) * 1024 + gq * 128, [[4096, 64], [1, 128]]),
                                                        cst[p0:p0 + 64, 0, p0:p0 + 64])
                                    s.copy(WST.view(((gq * 2 + gh) * 4) * 64, [[4096, 128], [1, 256]]), bk[:, 0:256], eng="act")
                                for d_ in range(2):
                                    for gqq in range(2):
                                        bk = nbank()
                                        for q4 in range(4):
                                            gq = gqq * 4 + q4
                                            i0 = 0 if d_ == 0 else 1
                                            s.mm(bk[0:16, q4 * 128:(q4 + 1) * 128],
                                                 [(BB.view(p0 * 2048 + (0 * 2 + d_) * 512 + (g0 + gq) * 16, [[2048, 64], [1, 16]]),
                                                   CAB.view(p0 * 4608 + (0 * 2 + d_) * 1152 + gq * 144 + i0 * 16, [[4608, 64], [1, 128]])),
                                                  (BB.view(p0 * 2048 + (1 * 2 + d_) * 512 + (g0 + gq) * 16, [[2048, 64], [1, 16]]),
                                                   CAB.view(p0 * 4608 + (1 * 2 + d_) * 1152 + gq * 144 + i0 * 16, [[4608, 64], [1, 128]]))])
                                        s.copy(KSB.view(d_ * 2048 + (2 * gqq * 4 + gh) * 128, [[4096, 16], [256, 4], [1, 128]]),
                                               bk.view(0, [[512, 16], [128, 4], [1, 128]]), eng="act")
                            gbase = 16 * b
                            s.dma(KFP.view(gbase * 3840 + 7 * 16, [[240, 16], [3840, 16], [1, 128]]), KSB[:, 0, :, :])
                            s.dma(KBR.view(gbase * 3840, [[240, 16], [3840, 16], [1, 128]]), KSB[:, 1, :, :])
                            s.dma(KFP.view(gbase * 3840, [[240, 16], [3840, 16], [1, 112]]), zt_.full())
                            s.dma(KBR.view(gbase * 3840 + 128, [[240, 16], [3840, 16], [1, 112]]), zt_.full())
                            for ss in range(8):
                                s.dma(TF[ss * 16:(ss + 1) * 16, :, :], KFP.view(gbase * 3840 + (7 - ss) * 16, [[240, 16], [3840, 16], [1, 128]]))
                                s.dma(TB[ss * 16:(ss + 1) * 16, :, :], KBR.view(gbase * 3840 + (7 - ss) * 16, [[240, 16], [3840, 16], [1, 128]]))
                            s.flush()

                        if L1S in ('c', 'c1'):
                            continue
                        u8b = cx.sb(bs_, "u8b", [128, 8, 256])
                        u8g = cx.sb(bs_, "u8g", [128, 16, 128])
                        U8T = cx.sb(bs_, "U8T", [128, 16, 288])
                        NCOL = 326
                        PS = 16 * NCOL
                        SSD = [cx.sb(bs_, "SS%d" % i, [128, 8, 2, NCOL]) for i in range(2)]
                        CAR = [cx.sb(bs_, "CAR%d" % i, [128, 7, 8, 2]) for i in range(2)]
                        A36 = [cx.sb(bs_, "A36_%d" % i, [128, 8, 2]) for i in range(2)]
                        B36 = [cx.sb(bs_, "B36_%d" % i, [128, 8, 2]) for i in range(2)]
                        y8b = cx.sb(bs_, "y8b", [128, 8, 256])
                        ysb = cx.sb(bs_, "ysb", [128, 512])
                        TT1 = [cx.sb(bs_, "TT1_%d" % i, [128, 9, 8, 2]) for i in range(2)]
                        TT2 = [cx.sb(bs_, "TT2_%d" % i, [128, 9, 8, 2]) for i in range(2)]
                        for (j0, nj) in ((0, 32), (32, 128), (160, 128)):
                            s.dma(u8b[0:nj, :, :], U.view(8 * j0 * 1024 + 256 * b, [[8192, nj], [1024, 8], [1, 256]]))
                            s.copy(u8g.view(0, [[2048, nj], [128, 16], [16, 8], [1, 16]]),
                                   u8b.view(0, [[2048, nj], [16, 16], [256, 8], [1, 16]]), eng="act")
                            for gq4 in range(4):
                                bk = nbank()
                                for q4 in range(4):
                                    gi = gq4 * 4 + q4
                                    s.transpose(bk[:, q4 * 128:q4 * 128 + nj],
                                                u8g.view(128 * gi, [[2048, nj], [1, 128]]), cst[0:nj, 0, 0:nj])
                                s.copy(U8T.view(gq4 * 4 * 288 + j0, [[16 * 288, 128], [288, 4], [1, nj]]),
                                       bk.view(0, [[512, 128], [128, 4], [1, nj]]), eng="act")
                        if L1S in ('d', 'd1'):
                            s.flush()
                            continue
                        s.memset(SSD[0].view(0, [[PS, 128], [NCOL, 16], [1, 1]]), 0.0)
                        s.memset(SSD[0].view(289, [[PS, 128], [NCOL, 16], [1, 37]]), 0.0)
                        s.memset(SSD[1].view(288, [[PS, 128], [NCOL, 16], [1, 38]]), 0.0)
                        s.memset(SSD[0].view(289, [[PS, 128], [2 * NCOL, 8], [1, 1]]), 1.0)
                        s.memset(SSD[1].view(323, [[PS, 128], [2 * NCOL, 8], [1, 1]]), 1.0)
                        for gq in range(8):
                            for gh in range(2):
                                gi = 2 * gq + gh
                                p0 = gh * 64
                                for d_ in range(2):
                                    for ri in range(2):
                                        bk = nbank()
                                        s.mm(bk[p0:p0 + 64, 0:288],
                                             [(WST.view((((gq * 2 + gh) * 2 + d_) * 2 + ri) * 64, [[4096, 128], [1, 64]]),
                                               U8T[:, gi, :])])
                                        so = p0 * PS + (gq * 2 + ri) * NCOL
                                        if d_ == 0:
                                            s.copy(SSD[0].view(so + 1, [[PS, 64], [1, 288]]), bk[p0:p0 + 64, 0:288], eng="act")
                                        else:
                                            s.copy(SSD[1].view(so + 256, [[PS, 64], [1, 32]]), bk[p0:p0 + 64, 0:32], eng="act")
                                            s.copy(SSD[1].view(so, [[PS, 64], [1, 256]]), bk[p0:p0 + 64, 32:288], eng="act")
                        if L1S in ('e', 'e1'):
                            s.flush()
                            continue
                        DS = 8 * 2 * 289
                        RI, GQ = NCOL, 2 * NCOL

                        def cplx_step(items):
                            for (pv, psw, cv, ca, cb_, t1_, t2_) in items:
                                s.tt(t1_, pv, ca, ALU.mult)
                                s.tt(t2_, psw, cb_, ALU.mult)
                            for (pv, psw, cv, ca, cb_, t1_, t2_) in items:
                                s.tt(t1_, t1_, t2_, ALU.add)
                            for (pv, psw, cv, ca, cb_, t1_, t2_) in items:
                                if cv is not None:
                                    s.tt(cv, cv, t1_, ALU.add)

                        def segv(SS, col, nseg):
                            return (SS.view(col, [[PS, 128], [36, nseg], [GQ, 8], [RI, 2]]),
                                    SS.view(col + RI, [[PS, 128], [36, nseg], [GQ, 8], [-RI, 2]]))

                        def coef(buf, d_, nseg):
                            return buf.view(d_ * 64 + g0 * 2, [[128, 128], [0, nseg], [2, 8], [1, 2]])

                        for k in range(1, 36):
                            items = []
                            for d_ in range(2):
                                pc = k if d_ == 0 else 36 - k
                                cc = k + 1 if d_ == 0 else 35 - k
                                pv, psw = segv(SSD[d_], pc, 9)
                                cv, _ = segv(SSD[d_], cc, 9)
                                items.append((pv, psw, cv, coef(LA, d_, 9), coef(LB, d_, 9), TT1[d_].full(), TT2[d_].full()))
                            cplx_step(items)
                        items = []
                        for d_ in range(2):
                            c35 = 324 if d_ == 0 else 288
                            pv, psw = segv(SSD[d_], c35, 1)
                            items.append((pv, psw, None, coef(LA, d_, 1), coef(LB, d_, 1),
                                          TT1[d_].view(0, [[144, 128], [16, 1], [2, 8], [1, 2]]),
                                          TT2[d_].view(0, [[144, 128], [16, 1], [2, 8], [1, 2]])))
                        cplx_step(items)
                        for d_ in range(2):
                            l36re = TT1[d_].view(0, [[144, 128], [2, 8], [0, 2]])
                            s.copy(A36[d_].full(), l36re)
                            s.ts(B36[d_][:, :, 0:1], TT1[d_].view(1, [[144, 128], [2, 8], [1, 1]]), -1.0, None, ALU.mult)
                            s.copy(B36[d_][:, :, 1:2], TT1[d_].view(1, [[144, 128], [2, 8], [1, 1]]))
                        for step in range(1, 8):
                            items = []
                            for d_ in range(2):
                                if d_ == 0:
                                    m = step
                                    cc, pc = 36 * m + 36, 36 * m
                                else:
                                    m = 7 - step
                                    cc, pc = 36 * m, 36 * m + 36
                                pv, psw = segv(SSD[d_], pc, 1)
                                cv, _ = segv(SSD[d_], cc, 1)
                                items.append((pv, psw, cv,
                                              A36[d_].view(0, [[16, 128], [0, 1], [2, 8], [1, 2]]),
                                              B36[d_].view(0, [[16, 128], [0, 1], [2, 8], [1, 2]]),
                                              TT1[d_].view(0, [[144, 128], [16, 1], [2, 8], [1, 2]]),
                                              TT2[d_].view(0, [[144, 128], [16, 1], [2, 8], [1, 2]])))
                            cplx_step(items)
                        items = []
                        for d_ in range(2):
                            pv, psw = segv(SSD[d_], 36, 7)
                            items.append((pv, psw, None, coef(LA, d_, 7), coef(LB, d_, 7),
                                          CAR[d_].full(), TT2[d_].view(0, [[144, 128], [16, 7], [2, 8], [1, 2]])))
                        cplx_step(items)
                        for d_ in range(2):
                            SS = SSD[d_]
                            sb0 = 37 if d_ == 0 else 1

                            def sview(ri):
                                return SS.view(sb0 + ri * RI, [[PS, 128], [36, 7], [GQ, 8], [1, 35]])

                            def tview(ri):
                                return SS.view(289 + ri * RI, [[PS, 128], [0, 7], [GQ, 8], [1, 35]])

                            def cview(ri):
                                return CAR[d_].view(ri, [[112, 128], [16, 7], [2, 8], [0, 35]])

                            w1 = (u8g if d_ == 0 else u8b).view(0, [[2048, 128], [280, 7], [35, 8], [1, 35]])
                            w2 = y8b.view(0, [[2048, 128], [280, 7], [35, 8], [1, 35]])
                            s.tt(w1, tview(0), cview(0), ALU.mult)
                            s.tt(w2, tview(1), cview(1), ALU.mult)
                            s.tt(w1, w1, w2, ALU.subtract)
                            s.tt(sview(0), sview(0), w1, ALU.add)
                            s.tt(w1, tview(0), cview(1), ALU.mult)
                            s.tt(w2, tview(1), cview(0), ALU.mult)
                            s.tt(w1, w1, w2, ALU.add)
                            s.tt(sview(1), sview(1), w1, ALU.add)
                        if L1S in ('f', 'f1'):
                            s.flush()
                            continue
                        for tt_ in range(2):
                            j0 = 32 + 128 * tt_
                            m0 = 128 * tt_
                            for gh in range(2):
                                p0 = gh * 64
                                for gqq in range(2):
                                    bx = nbank()
                                    by = nbank()
                                    for q4 in range(4):
                                        gq = gqq * 4 + q4
                                        gi = 2 * gq + gh
                                        s.mm(bx[:, q4 * 128:(q4 + 1) * 128],
                                             [(U8T[:, gi, j0:j0 + 128], TF[:, gi, :]), (U8T[:, gi, j0:j0 + 128], TB[:, gi, :])])
                                        pairs = []
                                        for d_ in range(2):
                                            c0 = j0 if d_ == 0 else m0 + 1
                                            i0 = 1 if d_ == 0 else 0
                                            for ri in range(2):
                                                so = p0 * PS + (gq * 2 + ri) * NCOL + c0
                                                pairs.append((SSD[d_].view(so, [[PS, 64], [1, 128]]),
                                                              CAB.view(p0 * 4608 + (ri * 2 + d_) * 1152 + gq * 144 + i0 * 16, [[4608, 64], [1, 128]])))
                                        s.mm(by[:, q4 * 128:(q4 + 1) * 128], pairs)
                                    s.copy(ysb.full(), by.full(), eng="act")
                                    s.tt(y8b.view(32 * gqq * 4 + 16 * gh, [[2048, 128], [32, 4], [256, 8], [1, 16]]),
                                         bx.view(0, [[512, 128], [128, 4], [16, 8], [1, 16]]),
                                         ysb.view(0, [[512, 128], [128, 4], [16, 8], [1, 16]]), ALU.add)
                            s.dma(YTOK.view((CTX + 8 * m0) * 1024 + 256 * b, [[8192, 128], [1024, 8], [1, 256]]), y8b.full())
                        s.flush()

            if L1S in ('g', 'g1'):
                return nc
            with ExitStack() as es:
                gw = [cx.sb(es, "gw%d" % k, [128, D], BF16) for k in range(8)]
                ow = [cx.sb(es, "ow%d" % k, [128, D], BF16) for k in range(8)]
                dskb = cx.sb(es, "dskb", [128, D])
                glbb = cx.sb(es, "glbb", [128, D])
                fnwb = cx.sb(es, "fnwb", [128, D])
                kg = cx.sb(es, "kg", [128, 1])
                s.memset(kg.full(), 2.0 * math.sqrt(2.0 / math.pi))
                for k in range(8):
                    s.dma(gw[k].full(), o_glu_w[k * 128:(k + 1) * 128, :], q="pool")
                    s.dma(ow[k].full(), o_w_out[k * 128:(k + 1) * 128, :], q="pool")
                s.dma(dskb.full(), o_d_skip.view(0, [[0, 128], [1, D]]))
                s.dma(glbb.full(), o_glu_b.view(0, [[0, 128], [1, D]]))
                s.dma(fnwb.full(), final_norm_w.view(0, [[0, 128], [1, D]]))
                NB3 = 3
                ya = [cx.sb(es, "ya%d" % i, [128, D]) for i in range(NB3)]
                ua = [cx.sb(es, "ua%d" % i, [128, D]) for i in range(NB3)]
                sga = [cx.sb(es, "sga%d" % i, [128, D]) for i in range(NB3)]
                xa = [cx.sb(es, "xa%d" % i, [128, D]) for i in range(NB3)]
                w1s = [cx.sb(es, "w1_%d" % i, [128, D]) for i in range(NB3)]
                w2s = [cx.sb(es, "w2_%d" % i, [128, D]) for i in range(NB3)]
                w3s = [cx.sb(es, "w3_%d" % i, [128, D]) for i in range(NB3)]
                tTs = [cx.sb(es, "tT_%d" % i, [128, 8, 128], BF16) for i in range(2 * NB3)]
                sts = [cx.sb(es, "st10_%d" % i, [128, 4]) for i in range(NB3)]

                def transp8(src, tT):
                    for half in range(2):
                        bk = nbank()
                        for kk in range(4):
                            k = half * 4 + kk
                            s.transpose(bk[:, kk * 128:(kk + 1) * 128], src[:, k * 128:(k + 1) * 128], ident)
                        s.copy(tT[:, half * 4:(half + 1) * 4, :], bk.view(0, [[512, 128], [128, 4], [1, 128]]), eng="act")

                TAILN = int(os.environ.get('TAILN', NT))

                def bufs(i):
                    b_ = i % NB3
                    return ya[b_], ua[b_], sga[b_], xa[b_], w1s[b_], w2s[b_], w3s[b_], tTs[2 * b_], tTs[2 * b_ + 1], sts[b_]

                def stage0(i):
                    y_, u_, g_, x_, w1, w2, w3, tTa, tTb, st = bufs(i)
                    s.dma(y_.full(), YTOK[i * 128:(i + 1) * 128, :])
                    s.dma(u_.full(), U[i * 128:(i + 1) * 128, :])
                    s.dma(g_.full(), SG1[i * 128:(i + 1) * 128, :])
                    s.dma(x_.full(), X1[i * 128:(i + 1) * 128, :])
                    s.tt(w1.full(), u_.full(), dskb.full(), ALU.mult)
                    s.tt(y_.full(), y_.full(), w1.full(), ALU.add)
                    s.tt(w1.full(), y_.full(), y_.full(), ALU.mult)
                    s.ts(w1.full(), w1.full(), 0.044715, 1.0, ALU.mult, ALU.add)
                    s.tt(w1.full(), w1.full(), y_.full(), ALU.mult)
                    s.act(w1.full(), w1.full(), AF.Sigmoid, scale=kg[:, 0:1])
                    s.tt(w2.full(), y_.full(), w1.full(), ALU.mult)
                    transp8(w2, tTa)

                def stage1(i):
                    y_, u_, g_, x_, w1, w2, w3, tTa, tTb, st = bufs(i)
                    for half in range(2):
                        bk = nbank()
                        s.mm(bk.full(), [(tTa[:, k, :], gw[k][:, half * 512:(half + 1) * 512]) for k in range(8)])
                        s.tt(w1[:, half * 512:(half + 1) * 512], bk.full(), glbb[:, half * 512:(half + 1) * 512], ALU.add)
                    s.act(w1.full(), w1.full(), AF.Sigmoid)
                    s.tt(w2.full(), w2.full(), w1.full(), ALU.mult)
                    s.tt(w2.full(), w2.full(), g_.full(), ALU.mult)
                    transp8(w2, tTb)

                def stage2(i):
                    y_, u_, g_, x_, w1, w2, w3, tTa, tTb, st = bufs(i)
                    for half in range(2):
                        bk = nbank()
                        s.mm(bk.full(), [(tTb[:, k, :], ow[k][:, half * 512:(half + 1) * 512]) for k in range(8)])
                        s.tt(w1[:, half * 512:(half + 1) * 512], bk.full(), gate_bc[1][0][:, half * 512:(half + 1) * 512], ALU.mult)
                    s.tt(w3.full(), w1.full(), x_.full(), ALU.add)
                    s.act(w1.full(), w3.full(), AF.Square, accum=st[:, 0:1])
                    s.ts(st[:, 1:2], st[:, 0:1], 1.0 / D, EPS, ALU.mult, ALU.add)
                    s.act(st[:, 2:3], st[:, 1:2], AF.Sqrt)
                    s.recip(st[:, 3:4], st[:, 2:3])
                    s.act(w3.full(), w3.full(), AF.Copy, scale=st[:, 3:4])
                    s.tt(w2.full(), w3.full(), fnwb.full(), ALU.mult)
                    s.dma(out_t[(i - 2) * 128:(i - 1) * 128, :], w2.full())

                pipeline(list(range(2, TAILN)), [stage0, stage1, stage2])
                s.flush()

    return nc


def _consts():
    c = np.zeros((128, 6, 512), np.float32)
    j = np.arange(128)[:, None]
    l = np.arange(128)[None, :]
    c[:, 0, :128] = np.eye(128, dtype=np.float32)
    c[:, 1, :128] = (j <= l)
    c[:, 2, :128] = (j >= l)
    c[:, 3, :] = 1.0
    nf = np.where(l < j, -30000.0, 0.0).astype(np.float32)
    nb = np.where(l > j, -30000.0, 0.0).astype(np.float32)
    c[:, 4, :] = np.tile(nf, (1, 4))
    c[:, 5, :] = np.tile(nb, (1, 4))
    return c


def _rope_tables():
    rows = L // 64
    row = np.repeat(np.arange(rows, dtype=np.float32), 64)
    col = np.tile(np.arange(64, dtype=np.float32), rows)
    n_freq = 16
    inv = (np.float32(10000.0) ** (-np.arange(n_freq, dtype=np.float32) / n_freq)).astype(np.float32)
    ang = np.concatenate([row[:, None] * inv, col[:, None] * inv], axis=-1).astype(np.float32)
    cos = np.cos(ang).astype(np.float32)
    sin = np.sin(ang).astype(np.float32)
    cosT = np.zeros((128, L), np.float32)
    sinT = np.zeros((128, L), np.float32)
    for h2 in range(2):
        for half in range(2):
            p0 = h2 * 64 + half * 32
            cosT[p0:p0 + 32] = cos.T
            sinT[p0:p0 + 32] = (-sin.T if half == 0 else sin.T)
    return np.stack([cosT, sinT], axis=1)


def _vecT(v, nchunk):
    return np.ascontiguousarray(np.asarray(v, np.float32).reshape(nchunk, 128).T)


def prep_inputs(b, inp):
    f = lambda a: np.ascontiguousarray(np.asarray(a, np.float32))
    m = {}
    m["xin"] = f(np.concatenate([inp["ctx"][b], inp["x"][b]], axis=0))
    cv = np.stack([inp["c"][b], inp["c_ctx"]], axis=0)
    m["cvecT"] = f(cv.reshape(2, 8, 128).transpose(2, 0, 1))
    m["consts"] = _consts()
    m["rope"] = _rope_tables()
    m["e_ada_w"] = f(inp["e_ada_w"][0])
    m["e_ada_b"] = f(inp["e_ada_b"][0]).reshape(1, -1)
    m["e_norm_wT"] = _vecT(inp["e_norm_w"][0], 8)
    w = f(inp["e_w_in"][0])
    q = w[:, OFF_Q:OFF_Q + 1024].reshape(D, 16, 2, 32)
    qs = q[:, :, ::-1, :].reshape(D, 1024)
    k = w[:, OFF_KV:OFF_KV + 256].reshape(D, 4, 64)
    kr = np.concatenate([k, k], axis=2).reshape(D, 512)
    ks = k.reshape(D, 4, 2, 32)[:, :, ::-1, :].reshape(D, 4, 64)
    ksr = np.concatenate([ks, ks], axis=2).reshape(D, 512)
    m["e_w_in"] = f(np.concatenate([w, qs, kr, ksr], axis=1))
    cw = f(inp["e_conv_w"][0])
    m["e_conv_wT"] = f(cw.reshape(5, 12, 128).transpose(2, 1, 0))
    m["e_conv_bT"] = _vecT(inp["e_conv_b"][0], 12)
    m["e_dt_bias"] = f(inp["e_dt_bias"][0]).reshape(1, 32)
    m["e_a_log"] = f(inp["e_a_log"][0]).reshape(1, 32)
    m["e_d_skip"] = f(inp["e_d_skip"][0]).reshape(1, 16)
    m["e_ssd_norm_wT"] = _vecT(inp["e_ssd_norm_w"][0], 8)
    sk = f(inp["e_sink"][0]).reshape(8, 2)
    m["e_sink"] = f(np.repeat(sk.T[:, None, :], 64, axis=1).reshape(128, 8))
    m["e_w_out"] = f(inp["e_w_out"][0])
    m["o_ada_w"] = f(inp["o_ada_w"][0])
    m["o_ada_b"] = f(inp["o_ada_b"][0]).reshape(1, -1)
    m["o_norm_wT"] = _vecT(inp["o_norm_w"][0], 8)
    m["o_w_in"] = f(inp["o_w_in"][0])

    def gl(a):
        a = np.asarray(a, np.float32)
        rest = a.shape[2:]
        a = a.reshape((32, 2, 64) + rest)
        a = np.moveaxis(a, 0, 2)
        return a.reshape((128, 32) + rest)

    lam = np.zeros((128, 2, 3, 32), np.float32)
    for d_ in range(2):
        lam[:, d_, 0] = gl(inp["o_lam_re"][0][d_])
        lam[:, d_, 1] = gl(inp["o_lam_im"][0][d_])
        lam[:, d_, 2] = gl(np.repeat(np.asarray(inp["o_log_step"][0][d_])[:, None], 64, axis=1))
    m["s5_lam"] = f(lam)
    m["s5_b"] = f(np.stack([gl(inp["o_b_re"][0]), gl(inp["o_b_im"][0])], axis=1))
    cr = np.asarray(inp["o_c_re"][0]).transpose(0, 2, 1)
    ci = np.asarray(inp["o_c_im"][0]).transpose(0, 2, 1)
    m["s5_c"] = f(np.stack([gl(cr), gl(ci)], axis=1))
    m["o_d_skip"] = f(inp["o_d_skip"][0]).reshape(1, -1)
    m["o_glu_w"] = f(inp["o_glu_w"][0])
    m["o_glu_b"] = f(inp["o_glu_b"][0]).reshape(1, -1)
    m["o_w_out"] = f(inp["o_w_out"][0])
    m["final_norm_w"] = f(inp["final_norm_w"]).reshape(1, -1)
    return m


def kernel(**inputs):
    nc = build_program()
    in_maps = [prep_inputs(b, inputs) for b in range(8)]
    res = run_bass_kernel_spmd(nc, in_maps, core_ids=list(range(8)))
    return np.stack([r["out"] for r in res.results], axis=0)
```

```python
import math
import os
from contextlib import ExitStack

import numpy as np
import concourse.bass as bass
import concourse.mybir as mybir
from concourse.bass_utils import run_bass_kernel_spmd

F32 = mybir.dt.float32
BF16 = mybir.dt.bfloat16
AF = mybir.ActivationFunctionType
ALU = mybir.AluOpType

D = 1024
T = 2304
NT = 18
CTX = 256
L = 2048
EPS = 1e-6
TG = [(0, 256), (256, 512), (768, 512), (1280, 512), (1792, 512)]

SES_ALL = os.environ.get('SES', '0') == '1'
SAME_ENGINE_SYNC = {'act': SES_ALL, 'dve': SES_ALL, 'pool': True, 'pe': False, 'sp': True}
SEM_EPOCH = 30000


class V:
    __slots__ = ("buf", "ap")

    def __init__(self, buf, ap):
        self.buf = buf
        self.ap = ap


class Buf:
    def __init__(self, name, h):
        self.name = name
        self.h = h
        self.last_w = None
        self.readers = []

    def __getitem__(self, idx):
        return V(self, self.h[idx])

    def full(self):
        return V(self, self.h.ap())

    def view(self, offset, pattern):
        return V(self, bass.AP(self.h, offset, [list(p) for p in pattern]))


class Sched:
    ENG = ("pe", "act", "dve", "pool", "sp")

    def __init__(self, nc):
        self.nc = nc
        self.prog = {e: [] for e in self.ENG}
        self.sem = {}
        self.cnt = {}
        self.semid = 0
        self.known = {e: {} for e in self.ENG}
        for e in ("pe", "act", "dve", "pool"):
            self._new_engine_sem(e)
        self.nds = 8
        self.dsem = {}
        self.duse = {}
        self.dcnt = {}
        for q in ("sp", "pool"):
            self.dsem[q] = []
            self.duse[q] = []
            for i in range(self.nds):
                key = "d_%s_%d" % (q, i)
                self.dsem[q].append((nc.alloc_semaphore(key), key))
                self.duse[q].append(0)
            self.dcnt[q] = 0
        self.n_ops = 0

    def _new_engine_sem(self, e):
        self.semid += 1
        key = "s_%s_%d" % (e, self.semid)
        self.sem[e] = (self.nc.alloc_semaphore(key), key)
        self.cnt[e] = 0

    def _deps(self, reads, writes):
        deps = {}

        def add(tok):
            if tok is None:
                return
            h, key, val = tok
            if key not in deps or deps[key][1] < val:
                deps[key] = (h, val)

        for r in reads:
            add(r.buf.last_w)
        for w in writes:
            add(w.buf.last_w)
            for t in w.buf.readers:
                add(t)
        return deps

    def _emit_waits(self, eng, deps, own_key=None):
        kn = self.known[eng]
        for key, (h, val) in deps.items():
            if key == own_key and not SAME_ENGINE_SYNC[eng]:
                continue
            if kn.get(key, 0) >= val:
                continue
            kn[key] = val
            self.prog[eng].append(("wait", h, val))

    def _update(self, tok, reads, writes):
        for w in writes:
            w.buf.last_w = tok
            w.buf.readers = []
        for r in reads:
            if r.buf.last_w is not tok:
                r.buf.readers.append(tok)

    def op(self, eng, fn, reads=(), writes=()):
        reads = [r for r in reads if r is not None]
        writes = list(writes)
        if self.cnt[eng] >= SEM_EPOCH:
            self._new_engine_sem(eng)
        h, key = self.sem[eng]
        own = None if eng == "pe" else key
        deps = self._deps(reads, writes)
        if eng == "pe":
            deps.pop(key, None)
        self._emit_waits(eng, deps, own_key=own)
        self.cnt[eng] += 1
        self.prog[eng].append(("op", fn, h, 1))
        tok = (h, key, self.cnt[eng])
        self._update(tok, reads, writes)
        self.n_ops += 1
        return tok

    def dma(self, out, in_, q="sp", **kw):
        deps = self._deps([in_], [out])
        self._emit_waits(q, deps)
        k = self.dcnt[q] % self.nds
        self.dcnt[q] += 1
        h, key = self.dsem[q][k]
        prev = 16 * self.duse[q][k]
        if prev > 0 and self.known[q].get(key, 0) < prev:
            self.known[q][key] = prev
            self.prog[q].append(("wait", h, prev))
        self.duse[q][k] += 1
        val = 16 * self.duse[q][k]
        o_ap, i_ap = out.ap, in_.ap
        self.prog[q].append(("op", lambda e: e.dma_start(out=o_ap, in_=i_ap, **kw), h, 16))
        tok = (h, key, val)
        self._update(tok, [in_], [out])
        self.n_ops += 1
        return tok

    def finish_dmas(self):
        for q in ("sp", "pool"):
            for k in range(self.nds):
                h, key = self.dsem[q][k]
                val = 16 * self.duse[q][k]
                if val > 0 and self.known[q].get(key, 0) < val:
                    self.known[q][key] = val
                    self.prog[q].append(("wait", h, val))

    def flush(self, name=None):
        self.finish_dmas()
        nc = self.nc
        prog = self.prog
        self.prog = {e: [] for e in self.ENG}

        def run(items, e):
            for it in items:
                if it[0] == "wait":
                    e.wait_ge(it[1], it[2])
                else:
                    inst = it[1](e)
                    inst.then_inc(it[2], it[3])

        with nc.Block() as block:
            if prog["sp"]:
                @block.sync
                def _(e):
                    run(prog["sp"], e)
            if prog["act"]:
                @block.scalar
                def _(e):
                    run(prog["act"], e)
            if prog["dve"]:
                @block.vector
                def _(e):
                    run(prog["dve"], e)
            if prog["pool"]:
                @block.gpsimd
                def _(e):
                    run(prog["pool"], e)
            if prog["pe"]:
                @block.tensor
                def _(e):
                    run(prog["pe"], e)

    def mm(self, out, pairs):
        n = len(pairs)

        def fn(e):
            inst = None
            for i, (l, r) in enumerate(pairs):
                inst = e.matmul(out.ap, l.ap, r.ap, start=(i == 0), stop=(i == n - 1))
            return inst

        self.op("pe", fn, reads=[p[0] for p in pairs] + [p[1] for p in pairs], writes=[out])

    def transpose(self, out, in_, ident):
        self.op("pe", lambda e: e.transpose(out.ap, in_.ap, ident.ap), reads=[in_, ident], writes=[out])

    def act(self, out, in_, func, bias=None, scale=None, accum=None):
        kw = {}
        reads = [in_]
        writes = [out]
        if bias is not None:
            if isinstance(bias, V):
                kw["bias"] = bias.ap
                reads.append(bias)
            else:
                kw["bias"] = bias
        if scale is not None:
            if isinstance(scale, V):
                kw["scale"] = scale.ap
                reads.append(scale)
            else:
                kw["scale"] = scale
        if accum is not None:
            kw["accum_out"] = accum.ap
            writes.append(accum)
        self.op("act", lambda e: e.activation(out.ap, in_.ap, func, **kw), reads=reads, writes=writes)

    def ts(self, out, in0, s1, s2, op0, op1=None, eng="dve"):
        reads = [in0]
        a1 = s1
        a2 = s2
        if isinstance(s1, V):
            reads.append(s1)
            a1 = s1.ap
        if isinstance(s2, V):
            reads.append(s2)
            a2 = s2.ap
        if op1 is None:
            self.op(eng, lambda e: e.tensor_scalar(out.ap, in0.ap, a1, a2, op0), reads=reads, writes=[out])
        else:
            self.op(eng, lambda e: e.tensor_scalar(out.ap, in0.ap, a1, a2, op0, op1), reads=reads, writes=[out])

    def tt(self, out, in0, in1, op, eng="dve"):
        self.op(eng, lambda e: e.tensor_tensor(out.ap, in0.ap, in1.ap, op), reads=[in0, in1], writes=[out])

    def stt(self, out, in0, scalar, in1, op0, op1):
        reads = [in0, in1]
        sc = scalar
        if isinstance(scalar, V):
            reads.append(scalar)
            sc = scalar.ap
        self.op("dve", lambda e: e.scalar_tensor_tensor(out.ap, in0.ap, sc, in1.ap, op0, op1),
                reads=reads, writes=[out])

    def copy(self, out, in_, eng="dve"):
        if eng == "act":
            self.op("act", lambda e: e.copy(out.ap, in_.ap), reads=[in_], writes=[out])
        else:
            self.op(eng, lambda e: e.tensor_copy(out.ap, in_.ap), reads=[in_], writes=[out])

    def recip(self, out, in_):
        self.op("dve", lambda e: e.reciprocal(out.ap, in_.ap), reads=[in_], writes=[out])

    def memset(self, out, val, eng="dve"):
        self.op(eng, lambda e: e.memset(out.ap, val), reads=[], writes=[out])


class Ctx:
    def __init__(self, nc, sched):
        self.nc = nc
        self.s = sched
        self.uid = 0

    def sb(self, es, name, shape, dtype=F32):
        self.uid += 1
        h = es.enter_context(self.nc.sbuf_tensor("%s_%d" % (name, self.uid), list(shape), dtype))
        return Buf(name, h)

    def ps(self, es, name, shape=(128, 512), dtype=F32):
        self.uid += 1
        h = es.enter_context(self.nc.psum_tensor("%s_%d" % (name, self.uid), list(shape), dtype))
        return Buf(name, h)

    def dram(self, name, shape, dtype=F32, kind="Internal"):
        h = self.nc.dram_tensor(name, list(shape), dtype, kind=kind)
        return Buf(name, h)


def pipeline(items, stages):
    n, k = len(items), len(stages)
    for t in range(n + k - 1):
        for j in range(k - 1, -1, -1):
            i = t - j
            if 0 <= i < n:
                stages[j](items[i])


def bc_mid(v_buf, base_off, pstep, nparts, n_outer, outer_step, n_inner):
    return v_buf.view(base_off, [[pstep, nparts], [outer_step, n_outer], [0, n_inner]])


E_NCOL = 5152
OFF_Z = 0
OFF_XBC = 1024
OFF_DT = 2560
OFF_Q = 2592
OFF_KV = 3616
OFF_G = 4128
OFF_QS = 5152
OFF_KR = 6176
OFF_KSR = 6688
E_NCOL_EXT = 7200


ORDER = ["p1", "p2a", "p2b", "p2c", "p2d", "p2e", "p2f", "p2g", "p2h", "p3", "p4", "p5", "all"]


def build_program(debug=(), stop="all"):
    def go(tag):
        return ORDER.index(tag) <= ORDER.index(stop)
    nc = bass.Bass("TRN2", target_bir_lowering=False)
    s = Sched(nc)
    cx = Ctx(nc, s)
    dbg = set(debug)

    def din(name, shape):
        return Buf(name, nc.dram_tensor(name, list(shape), F32, kind="ExternalInput"))

    def dout(name, shape):
        return Buf(name, nc.dram_tensor(name, list(shape), F32, kind="ExternalOutput"))

    def scratch(name, shape, dtype=F32):
        if name in dbg:
            return dout(name, shape)
        return Buf(name, nc.dram_tensor(name, list(shape), dtype))

    xin = din("xin", [T, D])
    cvecT = din("cvecT", [128, 2, 8])
    consts = din("consts", [128, 6, 512])
    rope = din("rope", [128, 2, L])
    e_ada_w = din("e_ada_w", [D, 3 * D])
    e_ada_b = din("e_ada_b", [1, 3 * D])
    e_norm_wT = din("e_norm_wT", [128, 8])
    e_w_in = din("e_w_in", [D, E_NCOL_EXT])
    e_conv_wT = din("e_conv_wT", [128, 12, 5])
    e_conv_bT = din("e_conv_bT", [128, 12])
    e_dt_bias = din("e_dt_bias", [1, 32])
    e_a_log = din("e_a_log", [1, 32])
    e_d_skip = din("e_d_skip", [1, 16])
    e_ssd_norm_wT = din("e_ssd_norm_wT", [128, 8])
    e_sink = din("e_sink", [128, 8])
    e_w_out = din("e_w_out", [2 * D, D])
    o_ada_w = din("o_ada_w", [D, 3 * D])
    o_ada_b = din("o_ada_b", [1, 3 * D])
    o_norm_wT = din("o_norm_wT", [128, 8])
    o_w_in = din("o_w_in", [D, 2 * D])
    s5_lam = din("s5_lam", [128, 2, 3, 32])
    s5_b = din("s5_b", [128, 2, 32, 16])
    s5_c = din("s5_c", [128, 2, 32, 16])
    o_d_skip = din("o_d_skip", [1, D])
    o_glu_w = din("o_glu_w", [D, D])
    o_glu_b = din("o_glu_b", [1, D])
    o_w_out = din("o_w_out", [D, D])
    final_norm_w = din("final_norm_w", [1, D])
    out_t = dout("out", [L, D])

    XS = scratch("XS", [T, 1024])
    BTOK = scratch("BTOK", [T, 256], BF16)
    BT = scratch("BT", [2, 128, T], BF16)
    CT = scratch("CT", [2, 128, T], BF16)
    SZ = scratch("SZ", [T, 1024])
    QR = scratch("QR", [8, 128, L], BF16)
    QC = scratch("QC", [8, 128, CTX], BF16)
    KR = scratch("KR", [4, 128, L], BF16)
    KC = scratch("KC", [4, 128, CTX], BF16)
    VT = scratch("VT", [T, 256], BF16)
    SG = scratch("SG", [8, 128, T])
    YF = scratch("YF", [T, 1024])
    YT = scratch("YT", [16, 128, T], BF16)
    X1 = scratch("X1", [T, 1024])
    U = scratch("U", [T, 1024])
    SG1 = scratch("SG1", [T, 1024])
    YTOK = scratch("YTOK", [T, 1024])
    KFP = scratch("KFP", [64, 16, 15, 16])
    KBR = scratch("KBR", [64, 16, 15, 16])
    HT = scratch("HT", [8, 128, T]) if "HT" in dbg else None
    DTD = scratch("DTD", [T, 32]) if "DTD" in dbg else None
    MODD = scratch("MODD", [4, 128, 24]) if "MODD" in dbg else None

    with ExitStack() as top:
        banks = [cx.ps(top, "bank%d" % i) for i in range(8)]
        cst = cx.sb(top, "cst", [128, 6, 512])
        s.dma(cst.full(), consts.full())
        ident = cst[:, 0, 0:128]
        tri = cst[:, 1, 0:128]
        utri = cst[:, 2, 0:128]
        ones = cst[:, 3, 0:128]
        onesb_t = cx.sb(top, "onesb", [128, 128], BF16)
        s.memset(onesb_t.full(), 1.0)
        onesb = onesb_t.full()
        modT = [[cx.sb(top, "modT%d%d" % (l, w), [128, 24]) for w in range(2)] for l in range(2)]
        gate_bc = [[cx.sb(top, "gate%d%d" % (l, w), [128, 1024]) for w in range(2)] for l in range(2)]
        scs = cx.sb(top, "scs", [128, 2, 8])

        def adaln_phase(layer, ada_w, ada_b):
            with ExitStack() as es:
                aw = [cx.sb(es, "aw%d" % k, [128, 3 * D]) for k in range(8)]
                ab = cx.sb(es, "ab", [1, 3 * D])
                modrow = [cx.sb(es, "modrow%d" % w, [1, 3 * D]) for w in range(2)]
                if layer == 0:
                    cv = cx.sb(es, "cv", [128, 2, 8])
                    s.dma(cv.full(), cvecT.full())
                    s.act(scs.full(), cv.full(), AF.Silu)
                for k in range(8):
                    s.dma(aw[k].full(), ada_w[k * 128:(k + 1) * 128, :])
                s.dma(ab.full(), ada_b.full())
                bi = 0
                for w in range(2):
                    for fg in range(6):
                        bk = banks[bi % 8]
                        bi += 1
                        s.mm(bk[0:1, :], [(scs[:, w, k:k + 1], aw[k][:, fg * 512:(fg + 1) * 512]) for k in range(8)])
                        s.tt(modrow[w][0:1, fg * 512:(fg + 1) * 512], bk[0:1, :], ab[0:1, fg * 512:(fg + 1) * 512], ALU.add)
                for w in range(2):
                    bk = banks[bi % 8]
                    bi += 1
                    for fc in range(24):
                        s.mm(bk[:, 2 * fc:2 * fc + 2], [(modrow[w][0:1, fc * 128:(fc + 1) * 128], cst[0:1, 3, 0:2])])
                    s.copy(modT[layer][w].full(), bk.view(0, [[512, 128], [2, 24]]))
                    for hh in range(2):
                        bk2 = banks[bi % 8]
                        bi += 1
                        s.mm(bk2.full(), [(cst[0:1, 3, 0:128], modrow[w][0:1, 2048 + hh * 512:2048 + (hh + 1) * 512])])
                        s.copy(gate_bc[layer][w][:, hh * 512:(hh + 1) * 512], bk2.full(), eng="act")
                    if MODD is not None:
                        s.dma(MODD[layer * 2 + w], modT[layer][w].full())
                s.flush()

        adaln_phase(0, e_ada_w, e_ada_b)

        with ExitStack() as l0:
            DT = cx.sb(l0, "DT", [128, NT, 32])
            DTA = cx.sb(l0, "DTA", [128, NT, 32])
            nw = cx.sb(l0, "nw", [128, 8])
            sc1 = [cx.sb(l0, "sc1_%d" % w, [128, 8]) for w in range(2)]
            s.dma(nw.full(), e_norm_wT.full())
            for w in range(2):
                s.stt(sc1[w].full(), modT[0][w][:, 8:16], 1.0, nw.full(), ALU.add, ALU.mult)

            hts = ExitStack()
            hT = [cx.sb(hts, "hT%d" % k, [128, T], BF16) for k in range(8)]
            with ExitStack() as es:
                xt = [cx.sb(es, "xt%d" % i, [128, D]) for i in range(2)]
                xn = [cx.sb(es, "xn%d" % i, [128, D]) for i in range(2)]
                junk = cx.sb(es, "junk", [128, D])
                st = [cx.sb(es, "st%d" % i, [128, 4]) for i in range(2)]
                for i in range(NT):
                    w = 1 if i < 2 else 0
                    x_ = xt[i % 2]
                    n_ = xn[i % 2]
                    st_ = st[i % 2]
                    s.dma(x_.full(), xin[i * 128:(i + 1) * 128, :])
                    s.act(junk.full(), x_.full(), AF.Square, accum=st_[:, 0:1])
                    s.ts(st_[:, 1:2], st_[:, 0:1], 1.0 / D, EPS, ALU.mult, ALU.add)
                    s.act(st_[:, 2:3], st_[:, 1:2], AF.Sqrt)
                    s.recip(st_[:, 3:4], st_[:, 2:3])
                    s.ts(n_.full(), x_.full(), st_[:, 3:4], None, ALU.mult)
                    for half in range(2):
                        bk = banks[(2 * i + half) % 8]
                        for kk in range(4):
                            k = half * 4 + kk
                            s.transpose(bk[:, kk * 128:(kk + 1) * 128], n_[:, k * 128:(k + 1) * 128], ident)
                        for kk in range(4):
                            k = half * 4 + kk
                            s.act(hT[k][:, i * 128:(i + 1) * 128], bk[:, kk * 128:(kk + 1) * 128], AF.Identity,
                                  bias=modT[0][w][:, k:k + 1], scale=sc1[w][:, k:k + 1])
                if HT is not None:
                    for k in range(8):
                        s.dma(HT[k], hT[k].full())
                s.flush()

            with ExitStack() as es:
                WB = 256
                wbuf = [cx.sb(es, "wbuf%d" % i, [128, 8, WB], BF16) for i in range(4)]
                wstate = {"i": 0}

                def load_w(col0, ncol=WB):
                    wb = wbuf[wstate["i"] % 4]
                    wstate["i"] += 1
                    s.dma(wb[:, :, 0:ncol], e_w_in.view(col0, [[E_NCOL_EXT, 128], [128 * E_NCOL_EXT, 8], [1, ncol]]), q="pool")
                    return wb

                bstate = {"i": 0}

                def nbank():
                    bk = banks[bstate["i"] % 8]
                    bstate["i"] += 1
                    return bk

                def fm_mm(wb, cc, t0, n):
                    bk = nbank()
                    s.mm(bk[:, 0:n], [(wb[:, k, cc * 128:(cc + 1) * 128], hT[k][:, t0:t0 + n]) for k in range(8)])
                    return bk

                xraws = [cx.sb(es, "xraw%d" % i, [128, T]) for i in range(2)]
                accs = [cx.sb(es, "acc%d" % i, [128, T]) for i in range(2)]
                acc = accs[0]
                accbs = [cx.sb(es, "accb%d" % i, [128, T], BF16) for i in range(2)]
                accb = accbs[0]
                rc_i = {"i": 0}
                tmp1s = [cx.sb(es, "tmp1_%d" % i, [128, 512]) for i in range(2)]
                tmp2s = [cx.sb(es, "tmp2_%d" % i, [128, 512]) for i in range(2)]
                stg = [cx.sb(es, "stg%d" % i, [128, 4, 128]) for i in range(2)]
                stgb = [cx.sb(es, "stgb%d" % i, [128, 4, 128], BF16) for i in range(2)]
                rp = cx.sb(es, "rp", [128, 2, L])
                cw = cx.sb(es, "cw", [128, 12, 5])
                cb = cx.sb(es, "cb", [128, 12])
                dtb = cx.sb(es, "dtb", [128, 32])
                abc = cx.sb(es, "abc", [128, 32])
                s.dma(rp.full(), rope.full())
                s.dma(cw.full(), e_conv_wT.full())
                s.dma(cb.full(), e_conv_bT.full())
                s.dma(dtb.full(), e_dt_bias.view(0, [[0, 128], [1, 32]]))
                s.dma(abc.full(), e_a_log.view(0, [[0, 128], [1, 32]]))
                s.act(abc.full(), abc.full(), AF.Exp)
                s.ts(abc.full(), abc.full(), -1.0, None, ALU.mult)
                stg_i = {"i": 0}

                def transposes_to(dst, col0, src, lowp=False):
                    for i0 in range(0, NT, 4):
                        nb = min(4, NT - i0)
                        bk = nbank()
                        for ii in range(nb):
                            i = i0 + ii
                            s.transpose(bk[:, ii * 128:(ii + 1) * 128], src[:, i * 128:(i + 1) * 128], ident)
                        sg_ = (stgb if lowp else stg)[stg_i["i"] % 2]
                        stg_i["i"] += 1
                        s.copy(sg_[:, 0:nb, :], bk.view(0, [[512, 128], [128, nb], [1, 128]]), eng="act")
                        ncols = dst.h.shape[1]
                        s.dma(dst.view(i0 * 128 * ncols + col0, [[ncols, 128], [128 * ncols, nb], [1, 128]]),
                              sg_[:, 0:nb, :])

                for fc in range(12 if go('p2a') else 0):
                    if fc % 2 == 0:
                        wb = load_w(OFF_XBC + fc * 128)
                    cc = fc % 2
                    xraw, acc = xraws[fc % 2], accs[fc % 2]
                    for (t0, n) in TG:
                        bk = fm_mm(wb, cc, t0, n)
                        s.copy(xraw[:, t0:t0 + n], bk[:, 0:n], eng="act")
                    s.ts(acc.full(), xraw.full(), cw[:, fc, 2:3], cb[:, fc:fc + 1], ALU.mult, ALU.add)
                    for kk in (0, 1, 3, 4):
                        d_ = kk - 2
                        for (s0, sl) in ((0, CTX), (CTX, L)):
                            lo = max(s0, s0 - d_)
                            hi = min(s0 + sl, s0 + sl - d_)
                            s.stt(acc[:, lo:hi], xraw[:, lo + d_:hi + d_], cw[:, fc, kk:kk + 1], acc[:, lo:hi],
                                  ALU.mult, ALU.add)
                    s.act(acc.full(), acc.full(), AF.Silu)
                    if fc < 8:
                        transposes_to(XS, fc * 128, acc)
                    elif fc < 10:
                        s.copy(accb.full(), acc.full(), eng="pool")
                        s.dma(BT[fc - 8], accb.full())
                        transposes_to(BTOK, (fc - 8) * 128, acc, lowp=True)
                    else:
                        s.copy(accb.full(), acc.full(), eng="pool")
                        s.dma(CT[fc - 10], accb.full())

                def rope_chunk(col_plain, col_swap, dst_rot, dst_ctx):
                    accb = accbs[rc_i["i"] % 2]
                    rc_i["i"] += 1
                    wa = load_w(col_plain, 128)
                    wsw = load_w(col_swap, 128)
                    for gi, (t0, n) in enumerate(TG):
                        bka = fm_mm(wa, 0, t0, n)
                        if gi == 0:
                            s.copy(accb[:, 0:CTX], bka[:, 0:CTX], eng="act")
                            continue
                        bkb = fm_mm(wsw, 0, t0, n)
                        l0 = t0 - CTX
                        tmp1, tmp2 = tmp1s[gi % 2], tmp2s[gi % 2]
                        s.tt(tmp1.full(), bka.full(), rp[:, 0, l0:l0 + 512], ALU.mult)
                        s.tt(tmp2.full(), bkb.full(), rp[:, 1, l0:l0 + 512], ALU.mult)
                        s.tt(accb[:, t0:t0 + n], tmp1.full(), tmp2.full(), ALU.add, eng="pool")
                    s.dma(dst_ctx, accb[:, 0:CTX])
                    s.dma(dst_rot, accb[:, CTX:T])

                for qc in range(8 if go('p2b') else 0):
                    rope_chunk(OFF_Q + qc * 128, OFF_QS + qc * 128, QR[qc], QC[qc])
                for j in range(4 if go('p2c') else 0):
                    rope_chunk(OFF_KR + j * 128, OFF_KSR + j * 128, KR[j], KC[j])

                for gc in range(8 if go('p2d') else 0):
                    acc = accs[gc % 2]
                    if gc % 2 == 0:
                        wb = load_w(OFF_G + gc * 128)
                    for (t0, n) in TG:
                        bk = fm_mm(wb, gc % 2, t0, n)
                        s.act(acc[:, t0:t0 + n], bk[:, 0:n], AF.Silu)
                    s.dma(SG[gc], acc.full())

                NT_E = NT if go('p2e') else 0
                wz = [load_w(OFF_Z + i * 256) for i in range(4)]
                zt = [cx.sb(es, "zt%d" % i, [128, D]) for i in range(2)]
                for i in range(NT_E):
                    z_ = zt[i % 2]
                    for half in range(2):
                        bk = nbank()
                        for q4 in range(2):
                            wbz = wz[half * 2 + q4]
                            s.mm(bk[:, q4 * 256:(q4 + 1) * 256],
                                 [(hT[k][:, i * 128:(i + 1) * 128], wbz[:, k, :]) for k in range(8)])
                        s.act(z_[:, half * 512:(half + 1) * 512], bk.full(), AF.Silu)
                    s.dma(SZ[i * 128:(i + 1) * 128, :], z_.full())
                wv = load_w(OFF_KV + 256)
                wdt = load_w(OFF_DT, 32)
                vt = [cx.sb(es, "vt%d" % i, [128, 256], BF16) for i in range(2)]
                for i in range(NT if go('p2f') else 0):
                    bk = nbank()
                    s.mm(bk[:, 0:256], [(hT[k][:, i * 128:(i + 1) * 128], wv[:, k, :]) for k in range(8)])
                    s.copy(vt[i % 2].full(), bk[:, 0:256], eng="act")
                    s.dma(VT[i * 128:(i + 1) * 128, :], vt[i % 2].full())
                for i in range(NT if go('p2g') else 0):
                    bk = nbank()
                    s.mm(bk[:, 0:32], [(hT[k][:, i * 128:(i + 1) * 128], wdt[:, k, 0:32]) for k in range(8)])
                    s.tt(DT[:, i, :], bk[:, 0:32], dtb.full(), ALU.add)
                    if go('p2h'):
                        s.act(DT[:, i, :], DT[:, i, :], AF.Exp)
                        s.ts(DT[:, i, :], DT[:, i, :], 1.0, None, ALU.add)
                        s.act(DT[:, i, :], DT[:, i, :], AF.Ln)
                    s.tt(DTA[:, i, :], DT[:, i, :], abc.full(), ALU.mult)
                    if DTD is not None:
                        s.dma(DTD[i * 128:(i + 1) * 128, :], DT[:, i, :])
                s.flush()
            hts.close()

            with ExitStack() as es:
                nb_ = {"i": 0}

                def nbank():
                    bk = banks[nb_["i"] % 8]
                    nb_["i"] += 1
                    return bk

                N3 = 3
                xs_t = [cx.sb(es, "xs_t%d" % i, [128, 1024]) for i in range(N3)]
                b_t = [cx.sb(es, "b_t%d" % i, [128, 256], BF16) for i in range(N3)]
                bt_t = [cx.sb(es, "bt_t%d" % i, [128, 2, 128], BF16) for i in range(N3)]
                ct_t = [cx.sb(es, "ct_t%d" % i, [128, 2, 128], BF16) for i in range(N3)]
                yf_t = [cx.sb(es, "yf_t%d" % i, [128, 1024]) for i in range(N3)]
                sz_t = [cx.sb(es, "sz_t%d" % i, [128, 1024]) for i in range(N3)]
                MTs = [cx.sb(es, "MT%d" % i, [128, 2048], BF16) for i in range(N3)]
                xcs = [cx.sb(es, "xc%d" % i, [128, 1024], BF16) for i in range(N3)]
                xcds = [cx.sb(es, "xcd%d" % i, [128, 1024], BF16) for i in range(N3)]
                tmpos = [cx.sb(es, "tmpo%d" % i, [128, 1024]) for i in range(N3)]
                ytots = [cx.sb(es, "ytot%d" % i, [128, 1024]) for i in range(N3)]
                sms = [cx.sb(es, "sm%d" % i, [128, 4, 16]) for i in range(N3)]
                st3s = [cx.sb(es, "st3_%d" % i, [128, 4]) for i in range(N3)]
                ystgs = [cx.sb(es, "ystg%d" % i, [128, 8, 128], BF16) for i in range(2)]
                dtatris = [cx.sb(es, "dtatri%d" % i, [128, 2048]) for i in range(2)]
                decTs = [cx.sb(es, "decT%d" % i, [128, 2048]) for i in range(2)]
                cb_sbs = [cx.sb(es, "cb_sb%d" % i, [128, 256]) for i in range(2)]
                tsks = [cx.sb(es, "tsk%d" % i, [128, 1024]) for i in range(2)]
                junk = cx.sb(es, "junk3", [128, 1024])
                Hs = [cx.sb(es, "Hs%d" % g, [128, 512]) for g in range(2)]
                Hb = [cx.sb(es, "Hb%d" % g, [128, 512], BF16) for g in range(2)]
                dsk = cx.sb(es, "dsk", [128, 16])
                snw = cx.sb(es, "snw", [128, 8])
                s.dma(dsk.full(), e_d_skip.view(0, [[0, 128], [1, 16]]))
                s.dma(snw.full(), e_ssd_norm_wT.full())

                def bc3(buf, off, pstep, n1, s1, n2, s2):
                    return buf.view(off, [[pstep, 128], [s1, n1], [s2, n2]])

                n_ch = NT if go("p3") else 0
                for d_ in range(2):
                    order = list(range(NT)) if d_ == 0 else [1, 0] + list(range(NT - 1, 1, -1))
                    order = order[:n_ch]
                    TRIoff = 512 if d_ == 0 else 1024
                    TRIv = tri if d_ == 0 else utri
                    negm = cst[:, 4 + d_, :]
                    for g in range(2):
                        s.memset(Hs[g].full(), 0.0)
                        s.memset(Hb[g].full(), 0.0)

                    def stA(item, d_=d_, TRIoff=TRIoff, TRIv=TRIv, negm=negm):
                        ci, i = item
                        p3, p2 = ci % N3, ci % 2
                        xs_, b_, bt_, ct_ = xs_t[p3], b_t[p3], bt_t[p3], ct_t[p3]
                        MT, xc, xcd, sm = MTs[p3], xcs[p3], xcds[p3], sms[p3]
                        dtatri, decT, cb_sb = dtatris[p2], decTs[p2], cb_sbs[p2]
                        s.dma(xs_.full(), XS[i * 128:(i + 1) * 128, :])
                        s.dma(b_.full(), BTOK[i * 128:(i + 1) * 128, :])
                        s.dma(bt_.full(), BT.view(i * 128, [[T, 128], [128 * T, 2], [1, 128]]))
                        s.dma(ct_.full(), CT.view(i * 128, [[T, 128], [128 * T, 2], [1, 128]]))
                        if d_ == 1:
                            s.dma(yf_t[p3].full(), YF[i * 128:(i + 1) * 128, :])
                            s.dma(sz_t[p3].full(), SZ[i * 128:(i + 1) * 128, :])
                        dta_i = DTA[:, i, d_ * 16:(d_ + 1) * 16]
                        doff = i * 32 + d_ * 16
                        s.tt(bc3(dtatri, 0, 2048, 16, 128, 128, 1), bc3(DTA, doff, NT * 32, 16, 1, 128, 0),
                             bc3(cst, TRIoff, 3072, 16, 0, 128, 1), ALU.mult, eng="pool")
                        bs = nbank()
                        s.mm(bs[:, 0:16], [(TRIv, dta_i)])
                        s.mm(bs[:, 16:32], [(ones, dta_i)])
                        na, ea, de, cd = sm[:, 0, :], sm[:, 1, :], sm[:, 2, :], sm[:, 3, :]
                        s.ts(na, bs[:, 0:16], -1.0, None, ALU.mult)
                        s.act(ea, bs[:, 0:16], AF.Exp)
                        s.tt(de, bs[:, 16:32], na, ALU.add)
                        s.act(de, de, AF.Exp)
                        s.act(cd, bs[:, 16:32], AF.Exp)
                        for hq in range(4):
                            bq = nbank()
                            s.mm(bq.full(), [(ones, dtatri[:, hq * 512:(hq + 1) * 512]), (ident, negm)])
                            for hh in range(4):
                                h = hq * 4 + hh
                                s.act(decT[:, h * 128:(h + 1) * 128], bq[:, hh * 128:(hh + 1) * 128], AF.Exp,
                                      bias=sm[:, 0, h:h + 1])
                        bc = nbank()
                        for g in range(2):
                            s.mm(bc[:, g * 128:(g + 1) * 128], [(bt_[:, g, :], ct_[:, g, :])])
                        s.copy(cb_sb.full(), bc[:, 0:256], eng="act")
                        for g in range(2):
                            s.tt(bc3(MT, g * 1024, 2048, 8, 128, 128, 1), bc3(decT, g * 1024, 2048, 8, 128, 128, 1),
                                 bc3(cb_sb, g * 128, 256, 8, 0, 128, 1), ALU.mult)
                        s.tt(bc3(xc, 0, 1024, 16, 64, 64, 1), bc3(xs_, 0, 1024, 16, 64, 64, 1),
                             bc3(DT, doff, NT * 32, 16, 1, 64, 0), ALU.mult, eng="pool")
                        s.tt(bc3(xcd, 0, 1024, 16, 64, 64, 1), bc3(xc, 0, 1024, 16, 64, 64, 1),
                             bc3(sm, 32, 64, 16, 1, 64, 0), ALU.mult, eng="pool")
                        if d_ == 1:
                            s.tt(bc3(tsks[p2], 0, 1024, 16, 64, 64, 1), bc3(xs_, 0, 1024, 16, 64, 64, 1),
                                 bc3(dsk, 0, 16, 16, 1, 64, 0), ALU.mult, eng="pool")
                            s.tt(tsks[p2].full(), tsks[p2].full(), yf_t[p3].full(), ALU.add, eng="pool")

                    def stB(item, d_=d_):
                        ci, i = item
                        p3 = ci % N3
                        b_, ct_ = b_t[p3], ct_t[p3]
                        MT, xc, xcd, sm, tmpo, ytot = MTs[p3], xcs[p3], xcds[p3], sms[p3], tmpos[p3], ytots[p3]
                        ydst = yf_t[p3] if d_ == 0 else ytot
                        for g in range(2):
                            by = nbank()
                            for hh in range(8):
                                h = g * 8 + hh
                                s.mm(by[:, hh * 64:(hh + 1) * 64], [(MT[:, h * 128:(h + 1) * 128], xc[:, h * 64:(h + 1) * 64])])
                            bo = nbank()
                            s.mm(bo.full(), [(ct_[:, g, :], Hb[g].full())])
                            s.tt(bc3(tmpo, g * 512, 1024, 8, 64, 64, 1), bc3(bo, 0, 512, 8, 64, 64, 1),
                                 bc3(sm, 16 + g * 8, 64, 8, 1, 64, 0), ALU.mult)
                            s.tt(ydst[:, g * 512:(g + 1) * 512], by.full(), tmpo[:, g * 512:(g + 1) * 512], ALU.add)
                        for g in range(2):
                            bst = nbank()
                            s.mm(bst.full(), [(b_[:, g * 128:(g + 1) * 128], xcd[:, g * 512:(g + 1) * 512])])
                            s.tt(bc3(Hs[g], 0, 512, 8, 64, 64, 1), bc3(Hs[g], 0, 512, 8, 64, 64, 1),
                                 bc3(sm, 48 + g * 8, 64, 8, 1, 64, 0), ALU.mult)
                            s.tt(Hs[g].full(), Hs[g].full(), bst.full(), ALU.add)
                            s.copy(Hb[g].full(), Hs[g].full(), eng="act")
                        if d_ == 0:
                            s.dma(YF[i * 128:(i + 1) * 128, :], yf_t[p3].full())

                    def stC(item, d_=d_):
                        if d_ == 0:
                            return
                        ci, i = item
                        p3, p2 = ci % N3, ci % 2
                        ytot, sz_, st3, ystg = ytots[p3], sz_t[p3], st3s[p3], ystgs[p2]
                        s.tt(ytot.full(), ytot.full(), tsks[p2].full(), ALU.add)
                        s.tt(ytot.full(), ytot.full(), sz_.full(), ALU.mult)
                        s.act(junk.full(), ytot.full(), AF.Square, accum=st3[:, 0:1])
                        s.ts(st3[:, 1:2], st3[:, 0:1], 1.0 / 1024, EPS, ALU.mult, ALU.add)
                        s.act(st3[:, 2:3], st3[:, 1:2], AF.Sqrt)
                        s.recip(st3[:, 3:4], st3[:, 2:3])
                        s.act(ytot.full(), ytot.full(), AF.Copy, scale=st3[:, 3:4])
                        for half in range(2):
                            bk = nbank()
                            for kk in range(4):
                                k = half * 4 + kk
                                s.transpose(bk[:, kk * 128:(kk + 1) * 128], ytot[:, k * 128:(k + 1) * 128], ident)
                            for kk in range(4):
                                k = half * 4 + kk
                                s.act(ystg[:, k, :], bk[:, kk * 128:(kk + 1) * 128], AF.Copy, scale=snw[:, k:k + 1])
                        s.dma(YT.view(i * 128, [[T, 128], [128 * T, 8], [1, 128]]), ystg.full())

                    pipeline(list(enumerate(order)), [stA, stB, stC])
                s.flush()

            with ExitStack() as es:
                nb_ = {"i": 0}

                def nbank():
                    bk = banks[nb_["i"] % 8]
                    nb_["i"] += 1
                    return bk

                J2 = 2
                qr_ts = [cx.sb(es, "qr_t%d" % i, [128, 2, L], BF16) for i in range(J2)]
                qc_ts = [cx.sb(es, "qc_t%d" % i, [128, 2, CTX], BF16) for i in range(J2)]
                kr_ts = [cx.sb(es, "kr_t%d" % i, [128, L], BF16) for i in range(J2)]
                kc_ts = [cx.sb(es, "kc_t%d" % i, [128, CTX], BF16) for i in range(J2)]
                v_ts = [cx.sb(es, "v_t%d" % i, [128, NT, 64], BF16) for i in range(J2)]
                v2s = [cx.sb(es, "v2_%d" % i, [128, NT, 128], BF16) for i in range(J2)]
                sg_ts = [cx.sb(es, "sg_t%d" % i, [128, 2, T]) for i in range(J2)]
                asts = [cx.sb(es, "ast%d" % i, [128, 2, T], BF16) for i in range(J2)]
                NP = 3
                pt = [[cx.sb(es, "pt%d_%d" % (a, b), [128, 512], BF16) for b in range(5)] for a in range(NP)]
                rds = [cx.sb(es, "rd%d" % i, [128, 256]) for i in range(2)]
                aos = [cx.sb(es, "ao%d" % i, [128, 256]) for i in range(2)]
                es_pp = cx.sb(es, "es_pp", [128, 8])
                c8 = cx.sb(es, "c8", [128, 1])
                s.memset(c8.full(), 0.125)
                s.dma(es_pp.full(), e_sink.full())
                s.act(es_pp.full(), es_pp.full(), AF.Exp)
                ATT_DBG = [int(v) for v in os.environ.get("ATT_DBG", "4,18,4").split(",")]
                items = []
                for j in range(ATT_DBG[0] if go("p4") else 0):
                    qbs = ([("c", 0), ("c", 1)] + [("l", b) for b in range(16)])[:ATT_DBG[1]]
                    for qi, (kind, bi) in enumerate(qbs):
                        items.append((len(items), j, kind, bi, qi == 0, qi == len(qbs) - 1))

                def keys_of(kind, bi):
                    keys = [("c", 0, None), ("c", 1, None)]
                    if kind == "l":
                        if bi > 0:
                            keys.append(("l", bi - 1, "prev"))
                        keys.append(("l", bi, None))
                        if bi < 15:
                            keys.append(("l", bi + 1, "next"))
                    return keys

                def atA(item):
                    n, j, kind, bi, first, last = item
                    js = j % J2
                    qr_t, qc_t, kr_t, kc_t, v_t, v2, sg_t = qr_ts[js], qc_ts[js], kr_ts[js], kc_ts[js], v_ts[js], v2s[js], sg_ts[js]
                    if first:
                        s.dma(qr_t.full(), QR.view(2 * j * 128 * L, [[L, 128], [128 * L, 2], [1, L]]))
                        s.dma(qc_t.full(), QC.view(2 * j * 128 * CTX, [[CTX, 128], [128 * CTX, 2], [1, CTX]]))
                        s.dma(kr_t.full(), KR[j])
                        s.dma(kc_t.full(), KC[j])
                        s.dma(v_t.full(), VT.view(j * 64, [[256, 128], [128 * 256, NT], [1, 64]]))
                        s.dma(sg_t.full(), SG.view(2 * j * 128 * T, [[T, 128], [128 * T, 2], [1, T]]))
                        s.copy(v2[:, :, 0:64], v_t.full(), eng="pool")
                        s.copy(v2[:, :, 64:128], v_t.full(), eng="pool")
                    qsrc, q0 = (qc_t, bi * 128) if kind == "c" else (qr_t, bi * 128)
                    pts = pt[n % NP]
                    qw = qsrc.h.shape[2]
                    for ki, (kk, kb, msk) in enumerate(keys_of(kind, bi)):
                        ksrc = kc_t if kk == "c" else kr_t
                        for par in range(2):
                            p0 = par * 64
                            bs = nbank()
                            s.mm(bs[:, 0:256],
                                 [(ksrc[p0:p0 + 64, kb * 128:(kb + 1) * 128],
                                   qsrc.view(p0 * 2 * qw + q0, [[2 * qw, 64], [qw, 2], [1, 128]]))])
                            s.act(pts[ki][:, par * 256:(par + 1) * 256], bs[:, 0:256], AF.Exp, scale=c8[:, 0:1])
                        if msk is not None:
                            moff = 1024 if msk == "prev" else 512
                            s.tt(pts[ki].view(0, [[512, 128], [128, 4], [1, 128]]),
                                 pts[ki].view(0, [[512, 128], [128, 4], [1, 128]]),
                                 cst.view(moff, [[3072, 128], [0, 4], [1, 128]]), ALU.mult, eng="pool")

                def atB(item):
                    n, j, kind, bi, first, last = item
                    js = j % J2
                    v2, sg_t, ast = v2s[js], sg_ts[js], asts[js]
                    tok0 = bi * 128 if kind == "c" else CTX + bi * 128
                    keys = keys_of(kind, bi)
                    pts = pt[n % NP]
                    rd, ao = rds[n % 2], aos[n % 2]
                    vt_idx = [(kb if kk == "c" else 2 + kb) for (kk, kb, _) in keys]
                    bn = nbank()
                    s.mm(bn.full(), [(v2[:, vt_idx[ki], :], pts[ki].full()) for ki in range(len(keys))])
                    bd = nbank()
                    s.mm(bd.full(), [(onesb, pts[ki].full()) for ki in range(len(keys))])
                    for par in range(2):
                        p0 = par * 64
                        for c in range(2):
                            s.ts(rd[p0:p0 + 64, c * 128:(c + 1) * 128],
                                 bd[p0:p0 + 64, par * 256 + c * 128:par * 256 + (c + 1) * 128],
                                 es_pp[p0:p0 + 64, 2 * j + c:2 * j + c + 1], None, ALU.add)
                    s.recip(rd.full(), rd.full())
                    for par in range(2):
                        p0 = par * 64
                        s.tt(ao[p0:p0 + 64, :], bn[p0:p0 + 64, par * 256:(par + 1) * 256], rd[p0:p0 + 64, :], ALU.mult)
                    s.tt(ast.view(tok0, [[2 * T, 128], [T, 2], [1, 128]]),
                         ao.view(0, [[256, 128], [128, 2], [1, 128]]),
                         sg_t.view(tok0, [[2 * T, 128], [T, 2], [1, 128]]), ALU.mult)
                    if last:
                        s.dma(YT.view((8 + 2 * j) * 128 * T, [[T, 128], [128 * T, 2], [1, T]]), ast.full())

                pipeline(items, [atA, atB])
                s.flush()

            with ExitStack() as es:
                nb_ = {"i": 0}

                def nbank():
                    bk = banks[nb_["i"] % 8]
                    nb_["i"] += 1
                    return bk

                wo = [cx.sb(es, "wo%d" % k, [128, D], BF16) for k in range(16)]
                yt = [cx.sb(es, "yt%d" % i, [128, 16, 128], BF16) for i in range(2)]
                xt = [cx.sb(es, "xt5_%d" % i, [128, D]) for i in range(2)]
                x1t = [cx.sb(es, "x1t%d" % i, [128, D]) for i in range(2)]
                tmp5s = [cx.sb(es, "tmp5_%d" % i, [128, 512]) for i in range(2)]
                for k in range(16):
                    s.dma(wo[k].full(), e_w_out[k * 128:(k + 1) * 128, :], q="pool")
                for i in range(NT if go("p5") else 0):
                    w = 1 if i < 2 else 0
                    y_, x_, o_ = yt[i % 2], xt[i % 2], x1t[i % 2]
                    s.dma(y_.full(), YT.view(i * 128, [[T, 128], [128 * T, 16], [1, 128]]))
                    s.dma(x_.full(), xin[i * 128:(i + 1) * 128, :])
                    for half in range(2):
                        tmp5 = tmp5s[half]
                        bk = nbank()
                        s.mm(bk.full(), [(y_[:, fc, :], wo[fc][:, half * 512:(half + 1) * 512]) for fc in range(16)])
                        s.tt(tmp5.full(), bk.full(), gate_bc[0][w][:, half * 512:(half + 1) * 512], ALU.mult)
                        s.tt(o_[:, half * 512:(half + 1) * 512], tmp5.full(), x_[:, half * 512:(half + 1) * 512], ALU.add)
                    s.dma(X1[i * 128:(i + 1) * 128, :], o_.full())
                s.flush()

        if go("all"):
            adaln_phase(1, o_ada_w, o_ada_b)
        with ExitStack() as l1:
            if not go("all"):
                return nc
            nb_ = {"i": 0}

            def nbank():
                bk = banks[nb_["i"] % 8]
                nb_["i"] += 1
                return bk

            with ExitStack() as es:
                nw = cx.sb(es, "nw1", [128, 8])
                sc1 = [cx.sb(es, "sc1b_%d" % w, [128, 8]) for w in range(2)]
                s.dma(nw.full(), o_norm_wT.full())
                for w in range(2):
                    s.stt(sc1[w].full(), modT[1][w][:, 8:16], 1.0, nw.full(), ALU.add, ALU.mult)
                hT = [cx.sb(es, "hTb%d" % k, [128, T], BF16) for k in range(8)]
                xt = [cx.sb(es, "xtb%d" % i, [128, D]) for i in range(2)]
                xn = [cx.sb(es, "xnb%d" % i, [128, D]) for i in range(2)]
                junk = cx.sb(es, "junkb", [128, D])
                st = [cx.sb(es, "stb%d" % i, [128, 4]) for i in range(2)]
                for i in range(NT):
                    w = 1 if i < 2 else 0
                    x_, n_, st_ = xt[i % 2], xn[i % 2], st[i % 2]
                    s.dma(x_.full(), X1[i * 128:(i + 1) * 128, :])
                    s.act(junk.full(), x_.full(), AF.Square, accum=st_[:, 0:1])
                    s.ts(st_[:, 1:2], st_[:, 0:1], 1.0 / D, EPS, ALU.mult, ALU.add)
                    s.act(st_[:, 2:3], st_[:, 1:2], AF.Sqrt)
                    s.recip(st_[:, 3:4], st_[:, 2:3])
                    s.ts(n_.full(), x_.full(), st_[:, 3:4], None, ALU.mult)
                    for half in range(2):
                        bk = nbank()
                        for kk in range(4):
                            k = half * 4 + kk
                            s.transpose(bk[:, kk * 128:(kk + 1) * 128], n_[:, k * 128:(k + 1) * 128], ident)
                        for kk in range(4):
                            k = half * 4 + kk
                            s.act(hT[k][:, i * 128:(i + 1) * 128], bk[:, kk * 128:(kk + 1) * 128], AF.Identity,
                                  bias=modT[1][w][:, k:k + 1], scale=sc1[w][:, k:k + 1])
                wq = [cx.sb(es, "wq%d" % i, [128, 8, 256], BF16) for i in range(4)]
                ot = [cx.sb(es, "ot%d" % i, [128, D]) for i in range(2)]
                oi = 0
                for which in range(2):
                    for q4 in range(4):
                        s.dma(wq[q4].full(), o_w_in.view(which * 1024 + q4 * 256, [[2 * D, 128], [128 * 2 * D, 8], [1, 256]]), q="pool")
                    for i in range(NT):
                        if which == 1 and i < 2:
                            continue
                        o_ = ot[oi % 2]
                        oi += 1
                        for half in range(2):
                            bk = nbank()
                            for q4 in range(2):
                                s.mm(bk[:, q4 * 256:(q4 + 1) * 256],
                                     [(hT[k][:, i * 128:(i + 1) * 128], wq[half * 2 + q4][:, k, :]) for k in range(8)])
                            if which == 0:
                                s.copy(o_[:, half * 512:(half + 1) * 512], bk.full(), eng="act")
                            else:
                                s.act(o_[:, half * 512:(half + 1) * 512], bk.full(), AF.Silu)
                        s.dma((U if which == 0 else SG1)[i * 128:(i + 1) * 128, :], o_.full())
                s.flush()

            L1S = os.environ.get('L1S', 'z')
            if L1S == 'a':
                return nc
            with ExitStack() as es:
                lam = cx.sb(es, "lam", [128, 2, 3, 32])
                bprm = cx.sb(es, "bprm", [128, 2, 32, 16])
                cprm = cx.sb(es, "cprm", [128, 2, 32, 16])
                s.dma(lam.full(), s5_lam.full())
                s.dma(bprm.full(), s5_b.full())
                s.dma(cprm.full(), s5_c.full())
                kc = cx.sb(es, "kconst", [128, 4])
                s.memset(kc[:, 0:1], 1.0 / 16)
                s.memset(kc[:, 1:2], math.pi / 2)
                s.memset(kc[:, 2:3], 0.0)
                s.memset(kc[:, 3:4], 1.0)
                W64 = [128, 2, 32]

                def t64(name):
                    return cx.sb(es, name, W64)

                def lv(i):
                    return lam.view(i * 32, [[192, 128], [96, 2], [1, 32]])

                dt_ = t64("dt_"); mag = t64("mag"); th = t64("th"); cs = t64("cs"); sn = t64("sn")
                t_a = t64("t_a"); t_b = t64("t_b"); t_c = t64("t_c")
                abre = t64("abre"); abim = t64("abim"); cre = t64("cre"); cim = t64("cim")
                s.act(dt_.full(), lv(2), AF.Exp)
                s.tt(t_a.full(), lv(0), dt_.full(), ALU.mult)
                s.act(mag.full(), t_a.full(), AF.Exp)
                s.tt(th.full(), lv(1), dt_.full(), ALU.mult)
                s.act(sn.full(), th.full(), AF.Sin, scale=kc[:, 0:1])
                s.act(cs.full(), th.full(), AF.Sin, scale=kc[:, 0:1], bias=kc[:, 1:2])
                for _ in range(4):
                    s.tt(t_a.full(), cs.full(), cs.full(), ALU.mult)
                    s.tt(t_b.full(), sn.full(), sn.full(), ALU.mult)
                    s.tt(t_c.full(), sn.full(), cs.full(), ALU.mult)
                    s.tt(cs.full(), t_a.full(), t_b.full(), ALU.subtract)
                    s.ts(sn.full(), t_c.full(), 2.0, None, ALU.mult)
                s.tt(abre.full(), mag.full(), cs.full(), ALU.mult)
                s.tt(abim.full(), mag.full(), sn.full(), ALU.mult)
                PW = cx.sb(es, "PW", [128, 2, 9, 64])

                def pw(ri, k):
                    return PW.view((ri * 9 + k) * 64, [[2 * 9 * 64, 128], [32, 2], [1, 32]])

                s.memset(PW[:, 0, 0, :], 1.0)
                s.memset(PW[:, 1, 0, :], 0.0)
                for k in range(8):
                    s.tt(t_a.full(), pw(0, k), abre.full(), ALU.mult)
                    s.tt(t_b.full(), pw(1, k), abim.full(), ALU.mult)
                    s.tt(pw(0, k + 1), t_a.full(), t_b.full(), ALU.subtract)
                    s.tt(t_a.full(), pw(0, k), abim.full(), ALU.mult)
                    s.tt(t_b.full(), pw(1, k), abre.full(), ALU.mult)
                    s.tt(pw(1, k + 1), t_a.full(), t_b.full(), ALU.add)
                s.ts(t_c.full(), abre.full(), -1.0, None, ALU.add)
                s.tt(t_a.full(), lv(0), lv(0), ALU.mult)
                s.tt(t_b.full(), lv(1), lv(1), ALU.mult)
                s.tt(t_a.full(), t_a.full(), t_b.full(), ALU.add)
                s.recip(dt_.full(), t_a.full())
                s.tt(t_a.full(), t_c.full(), lv(0), ALU.mult)
                s.tt(t_b.full(), abim.full(), lv(1), ALU.mult)
                s.tt(t_a.full(), t_a.full(), t_b.full(), ALU.add)
                s.tt(cre.full(), t_a.full(), dt_.full(), ALU.mult)
                s.tt(t_a.full(), abim.full(), lv(0), ALU.mult)
                s.tt(t_b.full(), t_c.full(), lv(1), ALU.mult)
                s.tt(t_a.full(), t_a.full(), t_b.full(), ALU.subtract)
                s.tt(cim.full(), t_a.full(), dt_.full(), ALU.mult)
                BB = cx.sb(es, "BB", [128, 2, 2, 512])
                tb1 = cx.sb(es, "tb1", [128, 512])
                tb2 = cx.sb(es, "tb2", [128, 512])

                def bb(ri, d_, g0=0, ng=32):
                    return BB.view((ri * 2 + d_) * 512 + g0 * 16, [[2048, 128], [16, ng], [1, 16]])

                def v3(buf, off, pstep, n1, s1, n2, s2):
                    return buf.view(off, [[pstep, 128], [s1, n1], [s2, n2]])

                def prm(buf, ri, g0=0, ng=32):
                    return buf.view(ri * 512 + g0 * 16, [[1024, 128], [16, ng], [1, 16]])

                def cf(buf, d_, g0=0, ng=32, n2=16):
                    return buf.view(d_ * 32 + g0, [[64, 128], [1, ng], [0, n2]])

                t1v = v3(tb1, 0, 512, 32, 16, 16, 1)
                t2v = v3(tb2, 0, 512, 32, 16, 16, 1)
                for d_ in range(2):
                    s.tt(t1v, prm(bprm, 0), cf(cre, d_), ALU.mult)
                    s.tt(t2v, prm(bprm, 1), cf(cim, d_), ALU.mult)
                    s.tt(bb(0, d_), t1v, t2v, ALU.subtract)
                    s.tt(t1v, prm(bprm, 1), cf(cre, d_), ALU.mult)
                    s.tt(t2v, prm(bprm, 0), cf(cim, d_), ALU.mult)
                    s.tt(bb(1, d_), t1v, t2v, ALU.add)
                LA = cx.sb(es, "LA", [128, 2, 32, 2])
                LB = cx.sb(es, "LB", [128, 2, 32, 2])
                for ri in range(2):
                    s.copy(LA.view(ri, [[128, 128], [64, 2], [2, 32]]), pw(0, 8))
                s.ts(LB.view(0, [[128, 128], [64, 2], [2, 32]]), pw(1, 8), -1.0, None, ALU.mult)
                s.copy(LB.view(1, [[128, 128], [64, 2], [2, 32]]), pw(1, 8))
                zt_ = cx.sb(es, "zt_", [16, 16, 112])
                s.memset(zt_.full(), 0.0)
                s.flush()

                if L1S == 'b':
                    return nc
                for b in range(4 if L1S not in ('c1', 'd1', 'e1', 'f1', 'g1') else 1):
                    g0 = 8 * b
                    with ExitStack() as bs_:
                        CAB = cx.sb(bs_, "CAB", [128, 2, 2, 8 * 144])
                        WST = cx.sb(bs_, "WST", [128, 8, 2, 2, 2, 64])
                        TF = cx.sb(bs_, "TF", [128, 16, 128])
                        TB = cx.sb(bs_, "TB", [128, 16, 128])

                        with ExitStack() as tmp:
                            WT = cx.sb(tmp, "WT", [128, 2, 2, 8 * 128])
                            KSB = cx.sb(tmp, "KSB", [16, 2, 16, 128])
                            c1 = cx.sb(tmp, "c1", [128, 128])
                            c2 = cx.sb(tmp, "c2", [128, 128])
                            c1v = v3(c1, 0, 128, 8, 16, 16, 1)
                            c2v = v3(c2, 0, 128, 8, 16, 16, 1)
                            for d_ in range(2):
                                for idx in range(9):
                                    p_ = idx if d_ == 0 else 8 - idx
                                    pr = PW.view((0 * 9 + p_) * 64 + d_ * 32 + g0, [[1152, 128], [1, 8], [0, 16]])
                                    pi_ = PW.view((1 * 9 + p_) * 64 + d_ * 32 + g0, [[1152, 128], [1, 8], [0, 16]])
                                    o_re = CAB.view((0 * 2 + d_) * 1152 + idx * 16, [[4608, 128], [144, 8], [1, 16]])
                                    o_im = CAB.view((1 * 2 + d_) * 1152 + idx * 16, [[4608, 128], [144, 8], [1, 16]])
                                    s.tt(c1v, prm(cprm, 0, g0, 8), pr, ALU.mult)
                                    s.tt(c2v, prm(cprm, 1, g0, 8), pi_, ALU.mult)
                                    s.tt(o_re, c1v, c2v, ALU.subtract)
                                    s.tt(c1v, prm(cprm, 0, g0, 8), pi_, ALU.mult)
                                    s.tt(c2v, prm(cprm, 1, g0, 8), pr, ALU.mult)
                                    s.stt(o_im, c1v, -1.0, c2v, ALU.mult, ALU.subtract)
                                for ss in range(8):
                                    p_ = 7 - ss if d_ == 0 else ss
                                    pr = PW.view((0 * 9 + p_) * 64 + d_ * 32 + g0, [[1152, 128], [1, 8], [0, 16]])
                                    pi_ = PW.view((1 * 9 + p_) * 64 + d_ * 32 + g0, [[1152, 128], [1, 8], [0, 16]])
                                    o_re = WT.view((d_ * 2 + 0) * 1024 + ss * 16, [[4096, 128], [128, 8], [1, 16]])
                                    o_im = WT.view((d_ * 2 + 1) * 1024 + ss * 16, [[4096, 128], [128, 8], [1, 16]])
                                    s.tt(c1v, bb(0, d_, g0, 8), pr, ALU.mult)
                                    s.tt(c2v, bb(1, d_, g0, 8), pi_, ALU.mult)
                                    s.tt(o_re, c1v, c2v, ALU.subtract)
                                    s.tt(c1v, bb(1, d_, g0, 8), pr, ALU.mult)
                                    s.tt(c2v, bb(0, d_, g0, 8), pi_, ALU.mult)
                                    s.tt(o_im, c1v, c2v, ALU.add)
                            for gh in range(2):
                                p0 = gh * 64
                                for gq in range(8):
                                    bk = nbank()
                                    for d_ in range(2):
                                        for ri in range(2):
                                            sl = d_ * 2 + ri
                                            s.transpose(bk[:, sl * 64:(sl + 1) * 64],
                                                        WT.view(p0 * 4096 + (d_ * 2 + ri) * 1024 + gq * 128, [[4096, 64], [1, 128]]),
                                                        cst[p0:p0 + 64, 0, p0:p0 + 64])
                                    s.copy(WST.view(((gq * 2 + gh) * 4) * 64, [[4096, 128], [1, 256]]), bk[:, 0:256], eng="act")
                                for d_ in range(2):
                                    for gqq in range(2):
                                        bk = nbank()
                                        for q4 in range(4):
                                            gq = gqq * 4 + q4
                                            i0 = 0 if d_ == 0 else 1
                                            s.mm(bk[0:16, q4 * 128:(q4 + 1) * 128],
                                                 [(BB.view(p0 * 2048 + (0 * 2 + d_) * 512 + (g0 + gq) * 16, [[2048, 64], [1, 16]]),
                                                   CAB.view(p0 * 4608 + (0 * 2 + d_) * 1152 + gq * 144 + i0 * 16, [[4608, 64], [1, 128]])),
                                                  (BB.view(p0 * 2048 + (1 * 2 + d_) * 512 + (g0 + gq) * 16, [[2048, 64], [1, 16]]),
                                                   CAB.view(p0 * 4608 + (1 * 2 + d_) * 1152 + gq * 144 + i0 * 16, [[4608, 64], [1, 128]]))])
                                        s.copy(KSB.view(d_ * 2048 + (2 * gqq * 4 + gh) * 128, [[4096, 16], [256, 4], [1, 128]]),
                                               bk.view(0, [[512, 16], [128, 4], [1, 128]]), eng="act")
                            gbase = 16 * b
                            s.dma(KFP.view(gbase * 3840 + 7 * 16, [[240, 16], [3840, 16], [1, 128]]), KSB[:, 0, :, :])
                            s.dma(KBR.view(gbase * 3840, [[240, 16], [3840, 16], [1, 128]]), KSB[:, 1, :, :])
                            s.dma(KFP.view(gbase * 3840, [[240, 16], [3840, 16], [1, 112]]), zt_.full())
                            s.dma(KBR.view(gbase * 3840 + 128, [[240, 16], [3840, 16], [1, 112]]), zt_.full())
                            for ss in range(8):
                                s.dma(TF[ss * 16:(ss + 1) * 16, :, :], KFP.view(gbase * 3840 + (7 - ss) * 16, [[240, 16], [3840, 16], [1, 128]]))
                                s.dma(TB[ss * 16:(ss + 1) * 16, :, :], KBR.view(gbase * 3840 + (7 - ss) * 16, [[240, 16], [3840, 16], [1, 128]]))
                            s.flush()

                        if L1S in ('c', 'c1'):
                            continue
                        u8b = cx.sb(bs_, "u8b", [128, 8, 256])
                        u8g = cx.sb(bs_, "u8g", [128, 16, 128])
                        U8T = cx.sb(bs_, "U8T", [128, 16, 288])
                        NCOL = 326
                        PS = 16 * NCOL
                        SSD = [cx.sb(bs_, "SS%d" % i, [128, 8, 2, NCOL]) for i in range(2)]
                        CAR = [cx.sb(bs_, "CAR%d" % i, [128, 7, 8, 2]) for i in range(2)]
                        A36 = [cx.sb(bs_, "A36_%d" % i, [128, 8, 2]) for i in range(2)]
                        B36 = [cx.sb(bs_, "B36_%d" % i, [128, 8, 2]) for i in range(2)]
                        y8b = cx.sb(bs_, "y8b", [128, 8, 256])
                        ysb = cx.sb(bs_, "ysb", [128, 512])
                        TT1 = [cx.sb(bs_, "TT1_%d" % i, [128, 9, 8, 2]) for i in range(2)]
                        TT2 = [cx.sb(bs_, "TT2_%d" % i, [128, 9, 8, 2]) for i in range(2)]
                        for (j0, nj) in ((0, 32), (32, 128), (160, 128)):
                            s.dma(u8b[0:nj, :, :], U.view(8 * j0 * 1024 + 256 * b, [[8192, nj], [1024, 8], [1, 256]]))
                            s.copy(u8g.view(0, [[2048, nj], [128, 16], [16, 8], [1, 16]]),
                                   u8b.view(0, [[2048, nj], [16, 16], [256, 8], [1, 16]]), eng="act")
                            for gq4 in range(4):
                                bk = nbank()
                                for q4 in range(4):
                                    gi = gq4 * 4 + q4
                                    s.transpose(bk[:, q4 * 128:q4 * 128 + nj],
                                                u8g.view(128 * gi, [[2048, nj], [1, 128]]), cst[0:nj, 0, 0:nj])
                                s.copy(U8T.view(gq4 * 4 * 288 + j0, [[16 * 288, 128], [288, 4], [1, nj]]),
                                       bk.view(0, [[512, 128], [128, 4], [1, nj]]), eng="act")
                        if L1S in ('d', 'd1'):
                            s.flush()
                            continue
                        s.memset(SSD[0].view(0, [[PS, 128], [NCOL, 16], [1, 1]]), 0.0)
                        s.memset(SSD[0].view(289, [[PS, 128], [NCOL, 16], [1, 37]]), 0.0)
                        s.memset(SSD[1].view(288, [[PS, 128], [NCOL, 16], [1, 38]]), 0.0)
                        s.memset(SSD[0].view(289, [[PS, 128], [2 * NCOL, 8], [1, 1]]), 1.0)
                        s.memset(SSD[1].view(323, [[PS, 128], [2 * NCOL, 8], [1, 1]]), 1.0)
                        for gq in range(8):
                            for gh in range(2):
                                gi = 2 * gq + gh
                                p0 = gh * 64
                                for d_ in range(2):
                                    for ri in range(2):
                                        bk = nbank()
                                        s.mm(bk[p0:p0 + 64, 0:288],
                                             [(WST.view((((gq * 2 + gh) * 2 + d_) * 2 + ri) * 64, [[4096, 128], [1, 64]]),
                                               U8T[:, gi, :])])
                                        so = p0 * PS + (gq * 2 + ri) * NCOL
                                        if d_ == 0:
                                            s.copy(SSD[0].view(so + 1, [[PS, 64], [1, 288]]), bk[p0:p0 + 64, 0:288], eng="act")
                                        else:
                                            s.copy(SSD[1].view(so + 256, [[PS, 64], [1, 32]]), bk[p0:p0 + 64, 0:32], eng="act")
                                            s.copy(SSD[1].view(so, [[PS, 64], [1, 256]]), bk[p0:p0 + 64, 32:288], eng="act")
                        if L1S in ('e', 'e1'):
                            s.flush()
                            continue
                        DS = 8 * 2 * 289
                        RI, GQ = NCOL, 2 * NCOL

                        def cplx_step(items):
                            for (pv, psw, cv, ca, cb_, t1_, t2_) in items:
                                s.tt(t1_, pv, ca, ALU.mult)
                                s.tt(t2_, psw, cb_, ALU.mult)
                            for (pv, psw, cv, ca, cb_, t1_, t2_) in items:
                                s.tt(t1_, t1_, t2_, ALU.add)
                            for (pv, psw, cv, ca, cb_, t1_, t2_) in items:
                                if cv is not None:
                                    s.tt(cv, cv, t1_, ALU.add)

                        def segv(SS, col, nseg):
                            return (SS.view(col, [[PS, 128], [36, nseg], [GQ, 8], [RI, 2]]),
                                    SS.view(col + RI, [[PS, 128], [36, nseg], [GQ, 8], [-RI, 2]]))

                        def coef(buf, d_, nseg):
                            return buf.view(d_ * 64 + g0 * 2, [[128, 128], [0, nseg], [2, 8], [1, 2]])

                        for k in range(1, 36):
                            items = []
                            for d_ in range(2):
                                pc = k if d_ == 0 else 36 - k
                                cc = k + 1 if d_ == 0 else 35 - k
                                pv, psw = segv(SSD[d_], pc, 9)
                                cv, _ = segv(SSD[d_], cc, 9)
                                items.append((pv, psw, cv, coef(LA, d_, 9), coef(LB, d_, 9), TT1[d_].full(), TT2[d_].full()))
                            cplx_step(items)
                        items = []
                        for d_ in range(2):
                            c35 = 324 if d_ == 0 else 288
                            pv, psw = segv(SSD[d_], c35, 1)
                            items.append((pv, psw, None, coef(LA, d_, 1), coef(LB, d_, 1),
                                          TT1[d_].view(0, [[144, 128], [16, 1], [2, 8], [1, 2]]),
                                          TT2[d_].view(0, [[144, 128], [16, 1], [2, 8], [1, 2]])))
                        cplx_step(items)
                        for d_ in range(2):
                            l36re = TT1[d_].view(0, [[144, 128], [2, 8], [0, 2]])
                            s.copy(A36[d_].full(), l36re)
                            s.ts(B36[d_][:, :, 0:1], TT1[d_].view(1, [[144, 128], [2, 8], [1, 1]]), -1.0, None, ALU.mult)
                            s.copy(B36[d_][:, :, 1:2], TT1[d_].view(1, [[144, 128], [2, 8], [1, 1]]))
                        for step in range(1, 8):
                            items = []
                            for d_ in range(2):
                                if d_ == 0:
                                    m = step
                                    cc, pc = 36 * m + 36, 36 * m
                                else:
                                    m = 7 - step
                                    cc, pc = 36 * m, 36 * m + 36
                                pv, psw = segv(SSD[d_], pc, 1)
                                cv, _ = segv(SSD[d_], cc, 1)
                                items.append((pv, psw, cv,
                                              A36[d_].view(0, [[16, 128], [0, 1], [2, 8], [1, 2]]),
                                              B36[d_].view(0, [[16, 128], [0, 1], [2, 8], [1, 2]]),
                                              TT1[d_].view(0, [[144, 128], [16, 1], [2, 8], [1, 2]]),
                                              TT2[d_].view(0, [[144, 128], [16, 1], [2, 8], [1, 2]])))
                            cplx_step(items)
                        items = []
                        for d_ in range(2):
                            pv, psw = segv(SSD[d_], 36, 7)
                            items.append((pv, psw, None, coef(LA, d_, 7), coef(LB, d_, 7),
                                          CAR[d_].full(), TT2[d_].view(0, [[144, 128], [16, 7], [2, 8], [1, 2]])))
                        cplx_step(items)
                        for d_ in range(2):
                            SS = SSD[d_]
                            sb0 = 37 if d_ == 0 else 1

                            def sview(ri):
                                return SS.view(sb0 + ri * RI, [[PS, 128], [36, 7], [GQ, 8], [1, 35]])

                            def tview(ri):
                                return SS.view(289 + ri * RI, [[PS, 128], [0, 7], [GQ, 8], [1, 35]])

                            def cview(ri):
                                return CAR[d_].view(ri, [[112, 128], [16, 7], [2, 8], [0, 35]])

                            w1 = (u8g if d_ == 0 else u8b).view(0, [[2048, 128], [280, 7], [35, 8], [1, 35]])
                            w2 = y8b.view(0, [[2048, 128], [280, 7], [35, 8], [1, 35]])
                            s.tt(w1, tview(0), cview(0), ALU.mult)
                            s.tt(w2, tview(1), cview(1), ALU.mult)
                            s.tt(w1, w1, w2, ALU.subtract)
                            s.tt(sview(0), sview(0), w1, ALU.add)
                            s.tt(w1, tview(0), cview(1), ALU.mult)
                            s.tt(w2, tview(1), cview(0), ALU.mult)
                            s.tt(w1, w1, w2, ALU.add)
                            s.tt(sview(1), sview(1), w1, ALU.add)
                        if L1S in ('f', 'f1'):
                            s.flush()
                            continue
                        for tt_ in range(2):
                            j0 = 32 + 128 * tt_
                            m0 = 128 * tt_
                            for gh in range(2):
                                p0 = gh * 64
                                for gqq in range(2):
                                    bx = nbank()
                                    by = nbank()
                                    for q4 in range(4):
                                        gq = gqq * 4 + q4
                                        gi = 2 * gq + gh
                                        s.mm(bx[:, q4 * 128:(q4 + 1) * 128],
                                             [(U8T[:, gi, j0:j0 + 128], TF[:, gi, :]), (U8T[:, gi, j0:j0 + 128], TB[:, gi, :])])
                                        pairs = []
                                        for d_ in range(2):
                                            c0 = j0 if d_ == 0 else m0 + 1
                                            i0 = 1 if d_ == 0 else 0
                                            for ri in range(2):
                                                so = p0 * PS + (gq * 2 + ri) * NCOL + c0
                                                pairs.append((SSD[d_].view(so, [[PS, 64], [1, 128]]),
                                                              CAB.view(p0 * 4608 + (ri * 2 + d_) * 1152 + gq * 144 + i0 * 16, [[4608, 64], [1, 128]])))
                                        s.mm(by[:, q4 * 128:(q4 + 1) * 128], pairs)
                                    s.copy(ysb.full(), by.full(), eng="act")
                                    s.tt(y8b.view(32 * gqq * 4 + 16 * gh, [[2048, 128], [32, 4], [256, 8], [1, 16]]),
                                         bx.view(0, [[512, 128], [128, 4], [16, 8], [1, 16]]),
                                         ysb.view(0, [[512, 128], [128, 4], [16, 8], [1, 16]]), ALU.add)
                            s.dma(YTOK.view((CTX + 8 * m0) * 1024 + 256 * b, [[8192, 128], [1024, 8], [1, 256]]), y8b.full())
                        s.flush()

            if L1S in ('g', 'g1'):
                return nc
            with ExitStack() as es:
                gw = [cx.sb(es, "gw%d" % k, [128, D], BF16) for k in range(8)]
                ow = [cx.sb(es, "ow%d" % k, [128, D], BF16) for k in range(8)]
                dskb = cx.sb(es, "dskb", [128, D])
                glbb = cx.sb(es, "glbb", [128, D])
                fnwb = cx.sb(es, "fnwb", [128, D])
                kg = cx.sb(es, "kg", [128, 1])
                s.memset(kg.full(), 2.0 * math.sqrt(2.0 / math.pi))
                for k in range(8):
                    s.dma(gw[k].full(), o_glu_w[k * 128:(k + 1) * 128, :], q="pool")
                    s.dma(ow[k].full(), o_w_out[k * 128:(k + 1) * 128, :], q="pool")
                s.dma(dskb.full(), o_d_skip.view(0, [[0, 128], [1, D]]))
                s.dma(glbb.full(), o_glu_b.view(0, [[0, 128], [1, D]]))
                s.dma(fnwb.full(), final_norm_w.view(0, [[0, 128], [1, D]]))
                NB3 = 3
                ya = [cx.sb(es, "ya%d" % i, [128, D]) for i in range(NB3)]
                ua = [cx.sb(es, "ua%d" % i, [128, D]) for i in range(NB3)]
                sga = [cx.sb(es, "sga%d" % i, [128, D]) for i in range(NB3)]
                xa = [cx.sb(es, "xa%d" % i, [128, D]) for i in range(NB3)]
                w1s = [cx.sb(es, "w1_%d" % i, [128, D]) for i in range(NB3)]
                w2s = [cx.sb(es, "w2_%d" % i, [128, D]) for i in range(NB3)]
                w3s = [cx.sb(es, "w3_%d" % i, [128, D]) for i in range(NB3)]
                tTs = [cx.sb(es, "tT_%d" % i, [128, 8, 128], BF16) for i in range(2 * NB3)]
                sts = [cx.sb(es, "st10_%d" % i, [128, 4]) for i in range(NB3)]

                def transp8(src, tT):
                    for half in range(2):
                        bk = nbank()
                        for kk in range(4):
                            k = half * 4 + kk
                            s.transpose(bk[:, kk * 128:(kk + 1) * 128], src[:, k * 128:(k + 1) * 128], ident)
                        s.copy(tT[:, half * 4:(half + 1) * 4, :], bk.view(0, [[512, 128], [128, 4], [1, 128]]), eng="act")

                TAILN = int(os.environ.get('TAILN', NT))

                def bufs(i):
                    b_ = i % NB3
                    return ya[b_], ua[b_], sga[b_], xa[b_], w1s[b_], w2s[b_], w3s[b_], tTs[2 * b_], tTs[2 * b_ + 1], sts[b_]

                def stage0(i):
                    y_, u_, g_, x_, w1, w2, w3, tTa, tTb, st = bufs(i)
                    s.dma(y_.full(), YTOK[i * 128:(i + 1) * 128, :])
                    s.dma(u_.full(), U[i * 128:(i + 1) * 128, :])
                    s.dma(g_.full(), SG1[i * 128:(i + 1) * 128, :])
                    s.dma(x_.full(), X1[i * 128:(i + 1) * 128, :])
                    s.tt(w1.full(), u_.full(), dskb.full(), ALU.mult)
                    s.tt(y_.full(), y_.full(), w1.full(), ALU.add)
                    s.tt(w1.full(), y_.full(), y_.full(), ALU.mult)
                    s.ts(w1.full(), w1.full(), 0.044715, 1.0, ALU.mult, ALU.add)
                    s.tt(w1.full(), w1.full(), y_.full(), ALU.mult)
                    s.act(w1.full(), w1.full(), AF.Sigmoid, scale=kg[:, 0:1])
                    s.tt(w2.full(), y_.full(), w1.full(), ALU.mult)
                    transp8(w2, tTa)

                def stage1(i):
                    y_, u_, g_, x_, w1, w2, w3, tTa, tTb, st = bufs(i)
                    for half in range(2):
                        bk = nbank()
                        s.mm(bk.full(), [(tTa[:, k, :], gw[k][:, half * 512:(half + 1) * 512]) for k in range(8)])
                        s.tt(w1[:, half * 512:(half + 1) * 512], bk.full(), glbb[:, half * 512:(half + 1) * 512], ALU.add)
                    s.act(w1.full(), w1.full(), AF.Sigmoid)
                    s.tt(w2.full(), w2.full(), w1.full(), ALU.mult)
                    s.tt(w2.full(), w2.full(), g_.full(), ALU.mult)
                    transp8(w2, tTb)

                def stage2(i):
                    y_, u_, g_, x_, w1, w2, w3, tTa, tTb, st = bufs(i)
                    for half in range(2):
                        bk = nbank()
                        s.mm(bk.full(), [(tTb[:, k, :], ow[k][:, half * 512:(half + 1) * 512]) for k in range(8)])
                        s.tt(w1[:, half * 512:(half + 1) * 512], bk.full(), gate_bc[1][0][:, half * 512:(half + 1) * 512], ALU.mult)
                    s.tt(w3.full(), w1.full(), x_.full(), ALU.add)
                    s.act(w1.full(), w3.full(), AF.Square, accum=st[:, 0:1])
                    s.ts(st[:, 1:2], st[:, 0:1], 1.0 / D, EPS, ALU.mult, ALU.add)
                    s.act(st[:, 2:3], st[:, 1:2], AF.Sqrt)
                    s.recip(st[:, 3:4], st[:, 2:3])
                    s.act(w3.full(), w3.full(), AF.Copy, scale=st[:, 3:4])
                    s.tt(w2.full(), w3.full(), fnwb.full(), ALU.mult)
                    s.dma(out_t[(i - 2) * 128:(i - 1) * 128, :], w2.full())

                pipeline(list(range(2, TAILN)), [stage0, stage1, stage2])
                s.flush()

    return nc


def _consts():
    c = np.zeros((128, 6, 512), np.float32)
    j = np.arange(128)[:, None]
    l = np.arange(128)[None, :]
    c[:, 0, :128] = np.eye(128, dtype=np.float32)
    c[:, 1, :128] = (j <= l)
    c[:, 2, :128] = (j >= l)
    c[:, 3, :] = 1.0
    nf = np.where(l < j, -30000.0, 0.0).astype(np.float32)
    nb = np.where(l > j, -30000.0, 0.0).astype(np.float32)
    c[:, 4, :] = np.tile(nf, (1, 4))
    c[:, 5, :] = np.tile(nb, (1, 4))
    return c


def _rope_tables():
    rows = L // 64
    row = np.repeat(np.arange(rows, dtype=np.float32), 64)
    col = np.tile(np.arange(64, dtype=np.float32), rows)
    n_freq = 16
    inv = (np.float32(10000.0) ** (-np.arange(n_freq, dtype=np.float32) / n_freq)).astype(np.float32)
    ang = np.concatenate([row[:, None] * inv, col[:, None] * inv], axis=-1).astype(np.float32)
    cos = np.cos(ang).astype(np.float32)
    sin = np.sin(ang).astype(np.float32)
    cosT = np.zeros((128, L), np.float32)
    sinT = np.zeros((128, L), np.float32)
    for h2 in range(2):
        for half in range(2):
            p0 = h2 * 64 + half * 32
            cosT[p0:p0 + 32] = cos.T
            sinT[p0:p0 + 32] = (-sin.T if half == 0 else sin.T)
    return np.stack([cosT, sinT], axis=1)


def _vecT(v, nchunk):
    return np.ascontiguousarray(np.asarray(v, np.float32).reshape(nchunk, 128).T)


def prep_inputs(b, inp):
    f = lambda a: np.ascontiguousarray(np.asarray(a, np.float32))
    m = {}
    m["xin"] = f(np.concatenate([inp["ctx"][b], inp["x"][b]], axis=0))
    cv = np.stack([inp["c"][b], inp["c_ctx"]], axis=0)
    m["cvecT"] = f(cv.reshape(2, 8, 128).transpose(2, 0, 1))
    m["consts"] = _consts()
    m["rope"] = _rope_tables()
    m["e_ada_w"] = f(inp["e_ada_w"][0])
    m["e_ada_b"] = f(inp["e_ada_b"][0]).reshape(1, -1)
    m["e_norm_wT"] = _vecT(inp["e_norm_w"][0], 8)
    w = f(inp["e_w_in"][0])
    q = w[:, OFF_Q:OFF_Q + 1024].reshape(D, 16, 2, 32)
    qs = q[:, :, ::-1, :].reshape(D, 1024)
    k = w[:, OFF_KV:OFF_KV + 256].reshape(D, 4, 64)
    kr = np.concatenate([k, k], axis=2).reshape(D, 512)
    ks = k.reshape(D, 4, 2, 32)[:, :, ::-1, :].reshape(D, 4, 64)
    ksr = np.concatenate([ks, ks], axis=2).reshape(D, 512)
    m["e_w_in"] = f(np.concatenate([w, qs, kr, ksr], axis=1))
    cw = f(inp["e_conv_w"][0])
    m["e_conv_wT"] = f(cw.reshape(5, 12, 128).transpose(2, 1, 0))
    m["e_conv_bT"] = _vecT(inp["e_conv_b"][0], 12)
    m["e_dt_bias"] = f(inp["e_dt_bias"][0]).reshape(1, 32)
    m["e_a_log"] = f(inp["e_a_log"][0]).reshape(1, 32)
    m["e_d_skip"] = f(inp["e_d_skip"][0]).reshape(1, 16)
    m["e_ssd_norm_wT"] = _vecT(inp["e_ssd_norm_w"][0], 8)
    sk = f(inp["e_sink"][0]).reshape(8, 2)
    m["e_sink"] = f(np.repeat(sk.T[:, None, :], 64, axis=1).reshape(128, 8))
    m["e_w_out"] = f(inp["e_w_out"][0])
    m["o_ada_w"] = f(inp["o_ada_w"][0])
    m["o_ada_b"] = f(inp["o_ada_b"][0]).reshape(1, -1)
    m["o_norm_wT"] = _vecT(inp["o_norm_w"][0], 8)
    m["o_w_in"] = f(inp["o_w_in"][0])

    def gl(a):
        a = np.asarray(a, np.float32)
        rest = a.shape[2:]
        a = a.reshape((32, 2, 64) + rest)
        a = np.moveaxis(a, 0, 2)
        return a.reshape((128, 32) + rest)

    lam = np.zeros((128, 2, 3, 32), np.float32)
    for d_ in range(2):
        lam[:, d_, 0] = gl(inp["o_lam_re"][0][d_])
        lam[:, d_, 1] = gl(inp["o_lam_im"][0][d_])
        lam[:, d_, 2] = gl(np.repeat(np.asarray(inp["o_log_step"][0][d_])[:, None], 64, axis=1))
    m["s5_lam"] = f(lam)
    m["s5_b"] = f(np.stack([gl(inp["o_b_re"][0]), gl(inp["o_b_im"][0])], axis=1))
    cr = np.asarray(inp["o_c_re"][0]).transpose(0, 2, 1)
    ci = np.asarray(inp["o_c_im"][0]).transpose(0, 2, 1)
    m["s5_c"] = f(np.stack([gl(cr), gl(ci)], axis=1))
    m["o_d_skip"] = f(inp["o_d_skip"][0]).reshape(1, -1)
    m["o_glu_w"] = f(inp["o_glu_w"][0])
    m["o_glu_b"] = f(inp["o_glu_b"][0]).reshape(1, -1)
    m["o_w_out"] = f(inp["o_w_out"][0])
    m["final_norm_w"] = f(inp["final_norm_w"]).reshape(1, -1)
    return m


def kernel(**inputs):
    nc = build_program()
    in_maps = [prep_inputs(b, inputs) for b in range(8)]
    res = run_bass_kernel_spmd(nc, in_maps, core_ids=list(range(8)))
    return np.stack([r["out"] for r in res.results], axis=0)
```

```python
import math
import os
from contextlib import ExitStack

import numpy as np
import concourse.bass as bass
import concourse.mybir as mybir
from concourse.bass_utils import run_bass_kernel_spmd

F32 = mybir.dt.float32
BF16 = mybir.dt.bfloat16
AF = mybir.ActivationFunctionType
ALU = mybir.AluOpType

D = 1024
T = 2304
NT = 18
CTX = 256
L = 2048
EPS = 1e-6
TG = [(0, 256), (256, 512), (768, 512), (1280, 512), (1792, 512)]

SES_ALL = os.environ.get('SES', '0') == '1'
SAME_ENGINE_SYNC = {'act': SES_ALL, 'dve': SES_ALL, 'pool': True, 'pe': False, 'sp': True}
SEM_EPOCH = 30000


class V:
    __slots__ = ("buf", "ap")

    def __init__(self, buf, ap):
        self.buf = buf
        self.ap = ap


class Buf:
    def __init__(self, name, h):
        self.name = name
        self.h = h
        self.last_w = None
        self.readers = []

    def __getitem__(self, idx):
        return V(self, self.h[idx])

    def full(self):
        return V(self, self.h.ap())

    def view(self, offset, pattern):
        return V(self, bass.AP(self.h, offset, [list(p) for p in pattern]))


class Sched:
    ENG = ("pe", "act", "dve", "pool", "sp")

    def __init__(self, nc):
        self.nc = nc
        self.prog = {e: [] for e in self.ENG}
        self.sem = {}
        self.cnt = {}
        self.semid = 0
        self.known = {e: {} for e in self.ENG}
        for e in ("pe", "act", "dve", "pool"):
            self._new_engine_sem(e)
        self.nds = 8
        self.dsem = {}
        self.duse = {}
        self.dcnt = {}
        for q in ("sp", "pool"):
            self.dsem[q] = []
            self.duse[q] = []
            for i in range(self.nds):
                key = "d_%s_%d" % (q, i)
                self.dsem[q].append((nc.alloc_semaphore(key), key))
                self.duse[q].append(0)
            self.dcnt[q] = 0
        self.n_ops = 0

    def _new_engine_sem(self, e):
        self.semid += 1
        key = "s_%s_%d" % (e, self.semid)
        self.sem[e] = (self.nc.alloc_semaphore(key), key)
        self.cnt[e] = 0

    def _deps(self, reads, writes):
        deps = {}

        def add(tok):
            if tok is None:
                return
            h, key, val = tok
            if key not in deps or deps[key][1] < val:
                deps[key] = (h, val)

        for r in reads:
            add(r.buf.last_w)
        for w in writes:
            add(w.buf.last_w)
            for t in w.buf.readers:
                add(t)
        return deps

    def _emit_waits(self, eng, deps, own_key=None):
        kn = self.known[eng]
        for key, (h, val) in deps.items():
            if key == own_key and not SAME_ENGINE_SYNC[eng]:
                continue
            if kn.get(key, 0) >= val:
                continue
            kn[key] = val
            self.prog[eng].append(("wait", h, val))

    def _update(self, tok, reads, writes):
        for w in writes:
            w.buf.last_w = tok
            w.buf.readers = []
        for r in reads:
            if r.buf.last_w is not tok:
                r.buf.readers.append(tok)

    def op(self, eng, fn, reads=(), writes=()):
        reads = [r for r in reads if r is not None]
        writes = list(writes)
        if self.cnt[eng] >= SEM_EPOCH:
            self._new_engine_sem(eng)
        h, key = self.sem[eng]
        own = None if eng == "pe" else key
        deps = self._deps(reads, writes)
        if eng == "pe":
            deps.pop(key, None)
        self._emit_waits(eng, deps, own_key=own)
        self.cnt[eng] += 1
        self.prog[eng].append(("op", fn, h, 1))
        tok = (h, key, self.cnt[eng])
        self._update(tok, reads, writes)
        self.n_ops += 1
        return tok

    def dma(self, out, in_, q="sp", **kw):
        deps = self._deps([in_], [out])
        self._emit_waits(q, deps)
        k = self.dcnt[q] % self.nds
        self.dcnt[q] += 1
        h, key = self.dsem[q][k]
        prev = 16 * self.duse[q][k]
        if prev > 0 and self.known[q].get(key, 0) < prev:
            self.known[q][key] = prev
            self.prog[q].append(("wait", h, prev))
        self.duse[q][k] += 1
        val = 16 * self.duse[q][k]
        o_ap, i_ap = out.ap, in_.ap
        self.prog[q].append(("op", lambda e: e.dma_start(out=o_ap, in_=i_ap, **kw), h, 16))
        tok = (h, key, val)
        self._update(tok, [in_], [out])
        self.n_ops += 1
        return tok

    def finish_dmas(self):
        for q in ("sp", "pool"):
            for k in range(self.nds):
                h, key = self.dsem[q][k]
                val = 16 * self.duse[q][k]
                if val > 0 and self.known[q].get(key, 0) < val:
                    self.known[q][key] = val
                    self.prog[q].append(("wait", h, val))

    def flush(self, name=None):
        self.finish_dmas()
        nc = self.nc
        prog = self.prog
        self.prog = {e: [] for e in self.ENG}

        def run(items, e):
            for it in items:
                if it[0] == "wait":
                    e.wait_ge(it[1], it[2])
                else:
                    inst = it[1](e)
                    inst.then_inc(it[2], it[3])

        with nc.Block() as block:
            if prog["sp"]:
                @block.sync
                def _(e):
                    run(prog["sp"], e)
            if prog["act"]:
                @block.scalar
                def _(e):
                    run(prog["act"], e)
            if prog["dve"]:
                @block.vector
                def _(e):
                    run(prog["dve"], e)
            if prog["pool"]:
                @block.gpsimd
                def _(e):
                    run(prog["pool"], e)
            if prog["pe"]:
                @block.tensor
                def _(e):
                    run(prog["pe"], e)

    def mm(self, out, pairs):
        n = len(pairs)

        def fn(e):
            inst = None
            for i, (l, r) in enumerate(pairs):
                inst = e.matmul(out.ap, l.ap, r.ap, start=(i == 0), stop=(i == n - 1))
            return inst

        self.op("pe", fn, reads=[p[0] for p in pairs] + [p[1] for p in pairs], writes=[out])

    def transpose(self, out, in_, ident):
        self.op("pe", lambda e: e.transpose(out.ap, in_.ap, ident.ap), reads=[in_, ident], writes=[out])

    def act(self, out, in_, func, bias=None, scale=None, accum=None):
        kw = {}
        reads = [in_]
        writes = [out]
        if bias is not None:
            if isinstance(bias, V):
                kw["bias"] = bias.ap
                reads.append(bias)
            else:
                kw["bias"] = bias
        if scale is not None:
            if isinstance(scale, V):
                kw["scale"] = scale.ap
                reads.append(scale)
            else:
                kw["scale"] = scale
        if accum is not None:
            kw["accum_out"] = accum.ap
            writes.append(accum)
        self.op("act", lambda e: e.activation(out.ap, in_.ap, func, **kw), reads=reads, writes=writes)

    def ts(self, out, in0, s1, s2, op0, op1=None, eng="dve"):
        reads = [in0]
        a1 = s1
        a2 = s2
        if isinstance(s1, V):
            reads.append(s1)
            a1 = s1.ap
        if isinstance(s2, V):
            reads.append(s2)
            a2 = s2.ap
        if op1 is None:
            self.op(eng, lambda e: e.tensor_scalar(out.ap, in0.ap, a1, a2, op0), reads=reads, writes=[out])
        else:
            self.op(eng, lambda e: e.tensor_scalar(out.ap, in0.ap, a1, a2, op0, op1), reads=reads, writes=[out])

    def tt(self, out, in0, in1, op, eng="dve"):
        self.op(eng, lambda e: e.tensor_tensor(out.ap, in0.ap, in1.ap, op), reads=[in0, in1], writes=[out])

    def stt(self, out, in0, scalar, in1, op0, op1):
        reads = [in0, in1]
        sc = scalar
        if isinstance(scalar, V):
            reads.append(scalar)
            sc = scalar.ap
        self.op("dve", lambda e: e.scalar_tensor_tensor(out.ap, in0.ap, sc, in1.ap, op0, op1),
                reads=reads, writes=[out])

    def copy(self, out, in_, eng="dve"):
        if eng == "act":
            self.op("act", lambda e: e.copy(out.ap, in_.ap), reads=[in_], writes=[out])
        else:
            self.op(eng, lambda e: e.tensor_copy(out.ap, in_.ap), reads=[in_], writes=[out])

    def recip(self, out, in_):
        self.op("dve", lambda e: e.reciprocal(out.ap, in_.ap), reads=[in_], writes=[out])

    def memset(self, out, val, eng="dve"):
        self.op(eng, lambda e: e.memset(out.ap, val), reads=[], writes=[out])


class Ctx:
    def __init__(self, nc, sched):
        self.nc = nc
        self.s = sched
        self.uid = 0

    def sb(self, es, name, shape, dtype=F32):
        self.uid += 1
        h = es.enter_context(self.nc.sbuf_tensor("%s_%d" % (name, self.uid), list(shape), dtype))
        return Buf(name, h)

    def ps(self, es, name, shape=(128, 512), dtype=F32):
        self.uid += 1
        h = es.enter_context(self.nc.psum_tensor("%s_%d" % (name, self.uid), list(shape), dtype))
        return Buf(name, h)

    def dram(self, name, shape, dtype=F32, kind="Internal"):
        h = self.nc.dram_tensor(name, list(shape), dtype, kind=kind)
        return Buf(name, h)


def pipeline(items, stages):
    n, k = len(items), len(stages)
    for t in range(n + k - 1):
        for j in range(k - 1, -1, -1):
            i = t - j
            if 0 <= i < n:
                stages[j](items[i])


def bc_mid(v_buf, base_off, pstep, nparts, n_outer, outer_step, n_inner):
    return v_buf.view(base_off, [[pstep, nparts], [outer_step, n_outer], [0, n_inner]])


E_NCOL = 5152
OFF_Z = 0
OFF_XBC = 1024
OFF_DT = 2560
OFF_Q = 2592
OFF_KV = 3616
OFF_G = 4128
OFF_QS = 5152
OFF_KR = 6176
OFF_KSR = 6688
E_NCOL_EXT = 7200


ORDER = ["p1", "p2a", "p2b", "p2c", "p2d", "p2e", "p2f", "p2g", "p2h", "p3", "p4", "p5", "all"]


def build_program(debug=(), stop="all"):
    def go(tag):
        return ORDER.index(tag) <= ORDER.index(stop)
    nc = bass.Bass("TRN2", target_bir_lowering=False)
    s = Sched(nc)
    cx = Ctx(nc, s)
    dbg = set(debug)

    def din(name, shape):
        return Buf(name, nc.dram_tensor(name, list(shape), F32, kind="ExternalInput"))

    def dout(name, shape):
        return Buf(name, nc.dram_tensor(name, list(shape), F32, kind="ExternalOutput"))

    def scratch(name, shape, dtype=F32):
        if name in dbg:
            return dout(name, shape)
        return Buf(name, nc.dram_tensor(name, list(shape), dtype))

    xin = din("xin", [T, D])
    cvecT = din("cvecT", [128, 2, 8])
    consts = din("consts", [128, 6, 512])
    rope = din("rope", [128, 2, L])
    e_ada_w = din("e_ada_w", [D, 3 * D])
    e_ada_b = din("e_ada_b", [1, 3 * D])
    e_norm_wT = din("e_norm_wT", [128, 8])
    e_w_in = din("e_w_in", [D, E_NCOL_EXT])
    e_conv_wT = din("e_conv_wT", [128, 12, 5])
    e_conv_bT = din("e_conv_bT", [128, 12])
    e_dt_bias = din("e_dt_bias", [1, 32])
    e_a_log = din("e_a_log", [1, 32])
    e_d_skip = din("e_d_skip", [1, 16])
    e_ssd_norm_wT = din("e_ssd_norm_wT", [128, 8])
    e_sink = din("e_sink", [128, 8])
    e_w_out = din("e_w_out", [2 * D, D])
    o_ada_w = din("o_ada_w", [D, 3 * D])
    o_ada_b = din("o_ada_b", [1, 3 * D])
    o_norm_wT = din("o_norm_wT", [128, 8])
    o_w_in = din("o_w_in", [D, 2 * D])
    s5_lam = din("s5_lam", [128, 2, 3, 32])
    s5_b = din("s5_b", [128, 2, 32, 16])
    s5_c = din("s5_c", [128, 2, 32, 16])
    o_d_skip = din("o_d_skip", [1, D])
    o_glu_w = din("o_glu_w", [D, D])
    o_glu_b = din("o_glu_b", [1, D])
    o_w_out = din("o_w_out", [D, D])
    final_norm_w = din("final_norm_w", [1, D])
    out_t = dout("out", [L, D])

    XS = scratch("XS", [T, 1024])
    BTOK = scratch("BTOK", [T, 256], BF16)
    BT = scratch("BT", [2, 128, T], BF16)
    CT = scratch("CT", [2, 128, T], BF16)
    SZ = scratch("SZ", [T, 1024])
    QR = scratch("QR", [8, 128, L], BF16)
    QC = scratch("QC", [8, 128, CTX], BF16)
    KR = scratch("KR", [4, 128, L], BF16)
    KC = scratch("KC", [4, 128, CTX], BF16)
    VT = scratch("VT", [T, 256], BF16)
    SG = scratch("SG", [8, 128, T])
    YF = scratch("YF", [T, 1024])
    YT = scratch("YT", [16, 128, T], BF16)
    X1 = scratch("X1", [T, 1024])
    U = scratch("U", [T, 1024])
    SG1 = scratch("SG1", [T, 1024])
    YTOK = scratch("YTOK", [T, 1024])
    KFP = scratch("KFP", [64, 16, 15, 16])
    KBR = scratch("KBR", [64, 16, 15, 16])
    HT = scratch("HT", [8, 128, T]) if "HT" in dbg else None
    DTD = scratch("DTD", [T, 32]) if "DTD" in dbg else None
    MODD = scratch("MODD", [4, 128, 24]) if "MODD" in dbg else None

    with ExitStack() as top:
        banks = [cx.ps(top, "bank%d" % i) for i in range(8)]
        cst = cx.sb(top, "cst", [128, 6, 512])
        s.dma(cst.full(), consts.full())
        ident = cst[:, 0, 0:128]
        tri = cst[:, 1, 0:128]
        utri = cst[:, 2, 0:128]
        ones = cst[:, 3, 0:128]
        onesb_t = cx.sb(top, "onesb", [128, 128], BF16)
        s.memset(onesb_t.full(), 1.0)
        onesb = onesb_t.full()
        modT = [[cx.sb(top, "modT%d%d" % (l, w), [128, 24]) for w in range(2)] for l in range(2)]
        gate_bc = [[cx.sb(top, "gate%d%d" % (l, w), [128, 1024]) for w in range(2)] for l in range(2)]
        scs = cx.sb(top, "scs", [128, 2, 8])

        def adaln_phase(layer, ada_w, ada_b):
            with ExitStack() as es:
                aw = [cx.sb(es, "aw%d" % k, [128, 3 * D]) for k in range(8)]
                ab = cx.sb(es, "ab", [1, 3 * D])
                modrow = [cx.sb(es, "modrow%d" % w, [1, 3 * D]) for w in range(2)]
                if layer == 0:
                    cv = cx.sb(es, "cv", [128, 2, 8])
                    s.dma(cv.full(), cvecT.full())
                    s.act(scs.full(), cv.full(), AF.Silu)
                for k in range(8):
                    s.dma(aw[k].full(), ada_w[k * 128:(k + 1) * 128, :])
                s.dma(ab.full(), ada_b.full())
                bi = 0
                for w in range(2):
                    for fg in range(6):
                        bk = banks[bi % 8]
                        bi += 1
                        s.mm(bk[0:1, :], [(scs[:, w, k:k + 1], aw[k][:, fg * 512:(fg + 1) * 512]) for k in range(8)])
                        s.tt(modrow[w][0:1, fg * 512:(fg + 1) * 512], bk[0:1, :], ab[0:1, fg * 512:(fg + 1) * 512], ALU.add)
                for w in range(2):
                    bk = banks[bi % 8]
                    bi += 1
                    for fc in range(24):
                        s.mm(bk[:, 2 * fc:2 * fc + 2], [(modrow[w][0:1, fc * 128:(fc + 1) * 128], cst[0:1, 3, 0:2])])
                    s.copy(modT[layer][w].full(), bk.view(0, [[512, 128], [2, 24]]))
                    for hh in range(2):
                        bk2 = banks[bi % 8]
                        bi += 1
                        s.mm(bk2.full(), [(cst[0:1, 3, 0:128], modrow[w][0:1, 2048 + hh * 512:2048 + (hh + 1) * 512])])
                        s.copy(gate_bc[layer][w][:, hh * 512:(hh + 1) * 512], bk2.full(), eng="act")
                    if MODD is not None:
                        s.dma(MODD[layer * 2 + w], modT[layer][w].full())
                s.flush()

        adaln_phase(0, e_ada_w, e_ada_b)

        with ExitStack() as l0:
            DT = cx.sb(l0, "DT", [128, NT, 32])
            DTA = cx.sb(l0, "DTA", [128, NT, 32])
            nw = cx.sb(l0, "nw", [128, 8])
            sc1 = [cx.sb(l0, "sc1_%d" % w, [128, 8]) for w in range(2)]
            s.dma(nw.full(), e_norm_wT.full())
            for w in range(2):
                s.stt(sc1[w].full(), modT[0][w][:, 8:16], 1.0, nw.full(), ALU.add, ALU.mult)

            wo = [cx.sb(l0, "wo%d" % k, [128, D], BF16) for k in range(16)]
            hts = ExitStack()
            hT = [cx.sb(hts, "hT%d" % k, [128, T], BF16) for k in range(8)]
            with ExitStack() as es:
                xt = [cx.sb(es, "xt%d" % i, [128, D]) for i in range(2)]
                xn = [cx.sb(es, "xn%d" % i, [128, D]) for i in range(2)]
                junk = cx.sb(es, "junk", [128, D])
                st = [cx.sb(es, "st%d" % i, [128, 4]) for i in range(2)]
                for i in range(NT):
                    w = 1 if i < 2 else 0
                    x_ = xt[i % 2]
                    n_ = xn[i % 2]
                    st_ = st[i % 2]
                    s.dma(x_.full(), xin[i * 128:(i + 1) * 128, :])
                    s.act(junk.full(), x_.full(), AF.Square, accum=st_[:, 0:1])
                    s.ts(st_[:, 1:2], st_[:, 0:1], 1.0 / D, EPS, ALU.mult, ALU.add)
                    s.act(st_[:, 2:3], st_[:, 1:2], AF.Sqrt)
                    s.recip(st_[:, 3:4], st_[:, 2:3])
                    s.ts(n_.full(), x_.full(), st_[:, 3:4], None, ALU.mult)
                    for half in range(2):
                        bk = banks[(2 * i + half) % 8]
                        for kk in range(4):
                            k = half * 4 + kk
                            s.transpose(bk[:, kk * 128:(kk + 1) * 128], n_[:, k * 128:(k + 1) * 128], ident)
                        for kk in range(4):
                            k = half * 4 + kk
                            s.act(hT[k][:, i * 128:(i + 1) * 128], bk[:, kk * 128:(kk + 1) * 128], AF.Identity,
                                  bias=modT[0][w][:, k:k + 1], scale=sc1[w][:, k:k + 1])
                if HT is not None:
                    for k in range(8):
                        s.dma(HT[k], hT[k].full())
                s.flush()

            with ExitStack() as es:
                WB = 256
                NWB, PF = 6, 4
                wbuf = [cx.sb(es, "wbuf%d" % i, [128, 8, WB], BF16) for i in range(NWB)]
                wplan = [(OFF_XBC + 256 * k, 256) for k in range(6)]
                for qc in range(8):
                    wplan += [(OFF_Q + qc * 128, 128), (OFF_QS + qc * 128, 128)]
                for j in range(4):
                    wplan += [(OFF_KR + j * 128, 128), (OFF_KSR + j * 128, 128)]
                wplan += [(OFF_G + 256 * k, 256) for k in range(4)]
                wplan += [(OFF_Z + 256 * k, 256) for k in range(4)]
                wplan += [(OFF_KV + 256, 256), (OFF_DT, 32)]
                wstate = {"i": 0, "issued": 0}

                def _issue(n):
                    col0, ncol = wplan[n]
                    wb = wbuf[n % NWB]
                    s.dma(wb[:, :, 0:ncol], e_w_in.view(col0, [[E_NCOL_EXT, 128], [128 * E_NCOL_EXT, 8], [1, ncol]]), q="pool")

                def load_w(col0, ncol=WB):
                    i = wstate["i"]
                    wstate["i"] += 1
                    assert wplan[i] == (col0, ncol), (i, wplan[i], col0, ncol)
                    while wstate["issued"] < min(i + PF + 1, len(wplan)):
                        _issue(wstate["issued"])
                        wstate["issued"] += 1
                    return wbuf[i % NWB]

                bstate = {"i": 0}

                def nbank():
                    bk = banks[bstate["i"] % 8]
                    bstate["i"] += 1
                    return bk

                def fm_mm(wb, cc, t0, n):
                    bk = nbank()
                    s.mm(bk[:, 0:n], [(wb[:, k, cc * 128:(cc + 1) * 128], hT[k][:, t0:t0 + n]) for k in range(8)])
                    return bk

                xraws = [cx.sb(es, "xraw%d" % i, [128, T]) for i in range(2)]
                accs = [cx.sb(es, "acc%d" % i, [128, T]) for i in range(2)]
                acc = accs[0]
                accbs = [cx.sb(es, "accb%d" % i, [128, T], BF16) for i in range(2)]
                accb = accbs[0]
                rc_i = {"i": 0}
                tmp1s = [cx.sb(es, "tmp1_%d" % i, [128, 512]) for i in range(2)]
                tmp2s = [cx.sb(es, "tmp2_%d" % i, [128, 512]) for i in range(2)]
                stg = [cx.sb(es, "stg%d" % i, [128, 4, 128]) for i in range(2)]
                stgb = [cx.sb(es, "stgb%d" % i, [128, 4, 128], BF16) for i in range(2)]
                rp = cx.sb(es, "rp", [128, 2, L])
                cw = cx.sb(es, "cw", [128, 12, 5])
                cb = cx.sb(es, "cb", [128, 12])
                dtb = cx.sb(es, "dtb", [128, 32])
                abc = cx.sb(es, "abc", [128, 32])
                s.dma(rp.full(), rope.full())
                s.dma(cw.full(), e_conv_wT.full())
                s.dma(cb.full(), e_conv_bT.full())
                s.dma(dtb.full(), e_dt_bias.view(0, [[0, 128], [1, 32]]))
                s.dma(abc.full(), e_a_log.view(0, [[0, 128], [1, 32]]))
                s.act(abc.full(), abc.full(), AF.Exp)
                s.ts(abc.full(), abc.full(), -1.0, None, ALU.mult)
                stg_i = {"i": 0}

                def transposes_to(dst, col0, src, lowp=False):
                    for i0 in range(0, NT, 4):
                        nb = min(4, NT - i0)
                        bk = nbank()
                        for ii in range(nb):
                            i = i0 + ii
                            s.transpose(bk[:, ii * 128:(ii + 1) * 128], src[:, i * 128:(i + 1) * 128], ident)
                        sg_ = (stgb if lowp else stg)[stg_i["i"] % 2]
                        stg_i["i"] += 1
                        s.copy(sg_[:, 0:nb, :], bk.view(0, [[512, 128], [128, nb], [1, 128]]), eng="act")
                        ncols = dst.h.shape[1]
                        s.dma(dst.view(i0 * 128 * ncols + col0, [[ncols, 128], [128 * ncols, nb], [1, 128]]),
                              sg_[:, 0:nb, :])

                wb_of = {}

                def xa(fc):
                    if fc % 2 == 0:
                        wb_of[fc // 2] = load_w(OFF_XBC + fc * 128)
                    wb = wb_of[fc // 2]
                    xraw = xraws[fc % 2]
                    for (t0, n) in TG:
                        bk = fm_mm(wb, fc % 2, t0, n)
                        s.copy(xraw[:, t0:t0 + n], bk[:, 0:n], eng="act")

                def xb(fc):
                    xraw, acc = xraws[fc % 2], accs[fc % 2]
                    s.ts(acc.full(), xraw.full(), cw[:, fc, 2:3], cb[:, fc:fc + 1], ALU.mult, ALU.add)
                    for kk in (0, 1, 3, 4):
                        d_ = kk - 2
                        for (s0, sl) in ((0, CTX), (CTX, L)):
                            lo = max(s0, s0 - d_)
                            hi = min(s0 + sl, s0 + sl - d_)
                            s.stt(acc[:, lo:hi], xraw[:, lo + d_:hi + d_], cw[:, fc, kk:kk + 1], acc[:, lo:hi],
                                  ALU.mult, ALU.add)
                    s.act(acc.full(), acc.full(), AF.Silu)
                    if fc < 8:
                        transposes_to(XS, fc * 128, acc)
                    elif fc < 10:
                        s.copy(accb.full(), acc.full(), eng="act")
                        s.dma(BT[fc - 8], accb.full())
                        transposes_to(BTOK, (fc - 8) * 128, acc, lowp=True)
                    else:
                        s.copy(accb.full(), acc.full(), eng="act")
                        s.dma(CT[fc - 10], accb.full())

                pipeline(list(range(12 if go('p2a') else 0)), [xa, xb])

                def rope_chunk(col_plain, col_swap, dst_rot, dst_ctx):
                    accb = accbs[rc_i["i"] % 2]
                    rc_i["i"] += 1
                    wa = load_w(col_plain, 128)
                    wsw = load_w(col_swap, 128)
                    for gi, (t0, n) in enumerate(TG):
                        bka = fm_mm(wa, 0, t0, n)
                        if gi == 0:
                            s.copy(accb[:, 0:CTX], bka[:, 0:CTX], eng="act")
                            continue
                        bkb = fm_mm(wsw, 0, t0, n)
                        l0 = t0 - CTX
                        tmp1, tmp2 = tmp1s[gi % 2], tmp2s[gi % 2]
                        s.tt(tmp1.full(), bka.full(), rp[:, 0, l0:l0 + 512], ALU.mult)
                        s.tt(tmp2.full(), bkb.full(), rp[:, 1, l0:l0 + 512], ALU.mult)
                        s.tt(accb[:, t0:t0 + n], tmp1.full(), tmp2.full(), ALU.add)
                    s.dma(dst_ctx, accb[:, 0:CTX])
                    s.dma(dst_rot, accb[:, CTX:T])

                for qc in range(8 if go('p2b') else 0):
                    rope_chunk(OFF_Q + qc * 128, OFF_QS + qc * 128, QR[qc], QC[qc])
                for j in range(4 if go('p2c') else 0):
                    rope_chunk(OFF_KR + j * 128, OFF_KSR + j * 128, KR[j], KC[j])

                for gc in range(8 if go('p2d') else 0):
                    acc = accs[gc % 2]
                    if gc % 2 == 0:
                        wb = load_w(OFF_G + gc * 128)
                    for (t0, n) in TG:
                        bk = fm_mm(wb, gc % 2, t0, n)
                        s.act(acc[:, t0:t0 + n], bk[:, 0:n], AF.Silu)
                    s.dma(SG[gc], acc.full())

                NT_E = NT if go('p2e') else 0
                wz = [load_w(OFF_Z + i * 256) for i in range(4)]
                for i in range(NT_E):
                    z_a = accs[i % 2]
                    for half in range(2):
                        bk = nbank()
                        for q4 in range(2):
                            wbz = wz[half * 2 + q4]
                            s.mm(bk[:, q4 * 256:(q4 + 1) * 256],
                                 [(hT[k][:, i * 128:(i + 1) * 128], wbz[:, k, :]) for k in range(8)])
                        s.act(z_a[:, half * 512:(half + 1) * 512], bk.full(), AF.Silu)
                    s.dma(SZ[i * 128:(i + 1) * 128, :], z_a[:, 0:1024])
                wv = load_w(OFF_KV + 256)
                wdt = load_w(OFF_DT, 32)
                vt = [cx.sb(es, "vt%d" % i, [128, 256], BF16) for i in range(2)]
                for i in range(NT if go('p2f') else 0):
                    bk = nbank()
                    s.mm(bk[:, 0:256], [(hT[k][:, i * 128:(i + 1) * 128], wv[:, k, :]) for k in range(8)])
                    s.copy(vt[i % 2].full(), bk[:, 0:256], eng="act")
                    s.dma(VT[i * 128:(i + 1) * 128, :], vt[i % 2].full())
                for i in range(NT if go('p2g') else 0):
                    bk = nbank()
                    s.mm(bk[:, 0:32], [(hT[k][:, i * 128:(i + 1) * 128], wdt[:, k, 0:32]) for k in range(8)])
                    s.tt(DT[:, i, :], bk[:, 0:32], dtb.full(), ALU.add)
                    if go('p2h'):
                        s.act(DT[:, i, :], DT[:, i, :], AF.Exp)
                        s.ts(DT[:, i, :], DT[:, i, :], 1.0, None, ALU.add)
                        s.act(DT[:, i, :], DT[:, i, :], AF.Ln)
                    s.tt(DTA[:, i, :], DT[:, i, :], abc.full(), ALU.mult)
                    if DTD is not None:
                        s.dma(DTD[i * 128:(i + 1) * 128, :], DT[:, i, :])
                s.flush()
            hts.close()
            for k in range(16):
                s.dma(wo[k].full(), e_w_out[k * 128:(k + 1) * 128, :], q="pool")

            with ExitStack() as es:
                nb_ = {"i": 0}

                def nbank():
                    bk = banks[nb_["i"] % 8]
                    nb_["i"] += 1
                    return bk

                N3 = 3
                xs_t = [cx.sb(es, "xs_t%d" % i, [128, 1024]) for i in range(N3)]
                b_t = [cx.sb(es, "b_t%d" % i, [128, 256], BF16) for i in range(N3)]
                bt_t = [cx.sb(es, "bt_t%d" % i, [128, 2, 128], BF16) for i in range(N3)]
                ct_t = [cx.sb(es, "ct_t%d" % i, [128, 2, 128], BF16) for i in range(N3)]
                yf_t = [cx.sb(es, "yf_t%d" % i, [128, 1024]) for i in range(N3)]
                sz_t = [cx.sb(es, "sz_t%d" % i, [128, 1024]) for i in range(N3)]
                MTs = [cx.sb(es, "MT%d" % i, [128, 2048], BF16) for i in range(N3)]
                xcs = [cx.sb(es, "xc%d" % i, [128, 1024], BF16) for i in range(N3)]
                xcds = [cx.sb(es, "xcd%d" % i, [128, 1024], BF16) for i in range(N3)]
                tmpos = [cx.sb(es, "tmpo%d" % i, [128, 1024]) for i in range(N3)]
                ytots = [cx.sb(es, "ytot%d" % i, [128, 1024]) for i in range(N3)]
                sms = [cx.sb(es, "sm%d" % i, [128, 4, 16]) for i in range(N3)]
                st3s = [cx.sb(es, "st3_%d" % i, [128, 4]) for i in range(N3)]
                ystgs = [cx.sb(es, "ystg%d" % i, [128, 8, 128], BF16) for i in range(2)]
                dtatris = [cx.sb(es, "dtatri%d" % i, [128, 2048]) for i in range(2)]
                decTs = [cx.sb(es, "decT%d" % i, [128, 2048]) for i in range(2)]
                cb_sbs = [cx.sb(es, "cb_sb%d" % i, [128, 256]) for i in range(2)]
                junk = cx.sb(es, "junk3", [128, 1024])
                Hs = [cx.sb(es, "Hs%d" % g, [128, 512]) for g in range(2)]
                Hb = [cx.sb(es, "Hb%d" % g, [128, 512], BF16) for g in range(2)]
                dsk = cx.sb(es, "dsk", [128, 16])
                snw = cx.sb(es, "snw", [128, 8])
                s.dma(dsk.full(), e_d_skip.view(0, [[0, 128], [1, 16]]))
                s.dma(snw.full(), e_ssd_norm_wT.full())

                def bc3(buf, off, pstep, n1, s1, n2, s2):
                    return buf.view(off, [[pstep, 128], [s1, n1], [s2, n2]])

                n_ch = NT if go("p3") else 0
                for d_ in range(2):
                    order = list(range(NT)) if d_ == 0 else [1, 0] + list(range(NT - 1, 1, -1))
                    order = order[:n_ch]
                    TRIoff = 512 if d_ == 0 else 1024
                    TRIv = tri if d_ == 0 else utri
                    negm = cst[:, 4 + d_, :]
                    for g in range(2):
                        s.memset(Hs[g].full(), 0.0)
                        s.memset(Hb[g].full(), 0.0)

                    def stA(item, d_=d_, TRIoff=TRIoff, TRIv=TRIv, negm=negm):
                        ci, i = item
                        p3, p2 = ci % N3, ci % 2
                        xs_, b_, bt_, ct_ = xs_t[p3], b_t[p3], bt_t[p3], ct_t[p3]
                        MT, xc, xcd, sm = MTs[p3], xcs[p3], xcds[p3], sms[p3]
                        dtatri, decT, cb_sb = dtatris[p2], decTs[p2], cb_sbs[p2]
                        s.dma(xs_.full(), XS[i * 128:(i + 1) * 128, :])
                        s.dma(b_.full(), BTOK[i * 128:(i + 1) * 128, :])
                        s.dma(bt_.full(), BT.view(i * 128, [[T, 128], [128 * T, 2], [1, 128]]))
                        s.dma(ct_.full(), CT.view(i * 128, [[T, 128], [128 * T, 2], [1, 128]]))
                        if d_ == 1:
                            s.dma(yf_t[p3].full(), YF[i * 128:(i + 1) * 128, :])
                            s.dma(sz_t[p3].full(), SZ[i * 128:(i + 1) * 128, :])
                        dta_i = DTA[:, i, d_ * 16:(d_ + 1) * 16]
                        doff = i * 32 + d_ * 16
                        s.tt(bc3(dtatri, 0, 2048, 16, 128, 128, 1), bc3(DTA, doff, NT * 32, 16, 1, 128, 0),
                             bc3(cst, TRIoff, 3072, 16, 0, 128, 1), ALU.mult, eng="pool")
                        bs = nbank()
                        s.mm(bs[:, 0:16], [(TRIv, dta_i)])
                        s.mm(bs[:, 16:32], [(ones, dta_i)])
                        na, ea, de, cd = sm[:, 0, :], sm[:, 1, :], sm[:, 2, :], sm[:, 3, :]
                        s.ts(na, bs[:, 0:16], -1.0, None, ALU.mult)
                        s.act(ea, bs[:, 0:16], AF.Exp)
                        s.tt(de, bs[:, 16:32], na, ALU.add)
                        s.act(de, de, AF.Exp)
                        s.act(cd, bs[:, 16:32], AF.Exp)
                        for hq in range(4):
                            bq = nbank()
                            s.mm(bq.full(), [(ones, dtatri[:, hq * 512:(hq + 1) * 512]), (ident, negm)])
                            for hh in range(4):
                                h = hq * 4 + hh
                                s.act(decT[:, h * 128:(h + 1) * 128], bq[:, hh * 128:(hh + 1) * 128], AF.Exp,
                                      bias=sm[:, 0, h:h + 1])
                        bc = nbank()
                        for g in range(2):
                            s.mm(bc[:, g * 128:(g + 1) * 128], [(bt_[:, g, :], ct_[:, g, :])])
                        s.copy(cb_sb.full(), bc[:, 0:256], eng="act")
                        for g in range(2):
                            s.tt(bc3(MT, g * 1024, 2048, 8, 128, 128, 1), bc3(decT, g * 1024, 2048, 8, 128, 128, 1),
                                 bc3(cb_sb, g * 128, 256, 8, 0, 128, 1), ALU.mult)
                        s.tt(bc3(xc, 0, 1024, 16, 64, 64, 1), bc3(xs_, 0, 1024, 16, 64, 64, 1),
                             bc3(DT, doff, NT * 32, 16, 1, 64, 0), ALU.mult, eng="pool")
                        s.tt(bc3(xcd, 0, 1024, 16, 64, 64, 1), bc3(xc, 0, 1024, 16, 64, 64, 1),
                             bc3(sm, 32, 64, 16, 1, 64, 0), ALU.mult, eng="pool")
                        if d_ == 1:
                            s.tt(bc3(tmpos[p3], 0, 1024, 16, 64, 64, 1), bc3(xs_, 0, 1024, 16, 64, 64, 1),
                                 bc3(dsk, 0, 16, 16, 1, 64, 0), ALU.mult, eng="pool")
                            s.tt(yf_t[p3].full(), yf_t[p3].full(), tmpos[p3].full(), ALU.add, eng="pool")

                    def stB(item, d_=d_):
                        ci, i = item
                        p3 = ci % N3
                        b_, ct_ = b_t[p3], ct_t[p3]
                        MT, xc, xcd, sm, tmpo, ytot = MTs[p3], xcs[p3], xcds[p3], sms[p3], tmpos[p3], ytots[p3]
                        ydst = yf_t[p3] if d_ == 0 else ytot
                        for g in range(2):
                            by = nbank()
                            for hh in range(8):
                                h = g * 8 + hh
                                s.mm(by[:, hh * 64:(hh + 1) * 64], [(MT[:, h * 128:(h + 1) * 128], xc[:, h * 64:(h + 1) * 64])])
                            bo = nbank()
                            s.mm(bo.full(), [(ct_[:, g, :], Hb[g].full())])
                            s.tt(bc3(tmpo, g * 512, 1024, 8, 64, 64, 1), bc3(bo, 0, 512, 8, 64, 64, 1),
                                 bc3(sm, 16 + g * 8, 64, 8, 1, 64, 0), ALU.mult)
                            s.tt(ydst[:, g * 512:(g + 1) * 512], by.full(), tmpo[:, g * 512:(g + 1) * 512], ALU.add)
                        for g in range(2):
                            bst = nbank()
                            s.mm(bst.full(), [(b_[:, g * 128:(g + 1) * 128], xcd[:, g * 512:(g + 1) * 512])])
                            s.tt(bc3(Hs[g], 0, 512, 8, 64, 64, 1), bc3(Hs[g], 0, 512, 8, 64, 64, 1),
                                 bc3(sm, 48 + g * 8, 64, 8, 1, 64, 0), ALU.mult)
                            s.tt(Hs[g].full(), Hs[g].full(), bst.full(), ALU.add)
                            s.copy(Hb[g].full(), Hs[g].full(), eng="act")
                        if d_ == 0:
                            s.dma(YF[i * 128:(i + 1) * 128, :], yf_t[p3].full())

                    def stC(item, d_=d_):
                        if d_ == 0:
                            return
                        ci, i = item
                        p3, p2 = ci % N3, ci % 2
                        ytot, sz_, st3, ystg = ytots[p3], sz_t[p3], st3s[p3], ystgs[p2]
                        s.tt(ytot.full(), ytot.full(), yf_t[p3].full(), ALU.add)
                        s.tt(ytot.full(), ytot.full(), sz_.full(), ALU.mult)
                        s.act(junk.full(), ytot.full(), AF.Square, accum=st3[:, 0:1])
                        s.ts(st3[:, 1:2], st3[:, 0:1], 1.0 / 1024, EPS, ALU.mult, ALU.add)
                        s.act(st3[:, 2:3], st3[:, 1:2], AF.Sqrt)
                        s.recip(st3[:, 3:4], st3[:, 2:3])
                        s.act(ytot.full(), ytot.full(), AF.Copy, scale=st3[:, 3:4])
                        for half in range(2):
                            bk = nbank()
                            for kk in range(4):
                                k = half * 4 + kk
                                s.transpose(bk[:, kk * 128:(kk + 1) * 128], ytot[:, k * 128:(k + 1) * 128], ident)
                            for kk in range(4):
                                k = half * 4 + kk
                                s.act(ystg[:, k, :], bk[:, kk * 128:(kk + 1) * 128], AF.Copy, scale=snw[:, k:k + 1])
                        s.dma(YT.view(i * 128, [[T, 128], [128 * T, 8], [1, 128]]), ystg.full())

                    pipeline(list(enumerate(order)), [stA, stB, stC])
                s.flush()

            with ExitStack() as es:
                nb_ = {"i": 0}

                def nbank():
                    bk = banks[nb_["i"] % 8]
                    nb_["i"] += 1
                    return bk

                J2 = 2
                qr_ts = [cx.sb(es, "qr_t%d" % i, [128, 2, L], BF16) for i in range(J2)]
                qc_ts = [cx.sb(es, "qc_t%d" % i, [128, 2, CTX], BF16) for i in range(J2)]
                kr_ts = [cx.sb(es, "kr_t%d" % i, [128, L], BF16) for i in range(J2)]
                kc_ts = [cx.sb(es, "kc_t%d" % i, [128, CTX], BF16) for i in range(J2)]
                v_ts = [cx.sb(es, "v_t%d" % i, [128, NT, 64], BF16) for i in range(J2)]
                v2s = [cx.sb(es, "v2_%d" % i, [128, NT, 128], BF16) for i in range(J2)]
                sg_ts = [cx.sb(es, "sg_t%d" % i, [128, 2, T]) for i in range(J2)]
                asts = [cx.sb(es, "ast%d" % i, [128, 2, T], BF16) for i in range(J2)]
                NP = 3
                pt = [[cx.sb(es, "pt%d_%d" % (a, b), [128, 512], BF16) for b in range(5)] for a in range(NP)]
                rds = [cx.sb(es, "rd%d" % i, [128, 256]) for i in range(2)]
                aos = [cx.sb(es, "ao%d" % i, [128, 256]) for i in range(2)]
                es_pp = cx.sb(es, "es_pp", [128, 8])
                c8 = cx.sb(es, "c8", [128, 1])
                s.memset(c8.full(), 0.125)
                s.dma(es_pp.full(), e_sink.full())
                s.act(es_pp.full(), es_pp.full(), AF.Exp)
                ATT_DBG = [int(v) for v in os.environ.get("ATT_DBG", "4,18,4").split(",")]
                items = []
                for j in range(ATT_DBG[0] if go("p4") else 0):
                    qbs = ([("c", 0), ("c", 1)] + [("l", b) for b in range(16)])[:ATT_DBG[1]]
                    for qi, (kind, bi) in enumerate(qbs):
                        items.append((len(items), j, kind, bi, qi == 0, qi == len(qbs) - 1))

                def keys_of(kind, bi):
                    keys = [("c", 0, None), ("c", 1, None)]
                    if kind == "l":
                        if bi > 0:
                            keys.append(("l", bi - 1, "prev"))
                        keys.append(("l", bi, None))
                        if bi < 15:
                            keys.append(("l", bi + 1, "next"))
                    return keys

                def atA(item):
                    n, j, kind, bi, first, last = item
                    js = j % J2
                    qr_t, qc_t, kr_t, kc_t, v_t, v2, sg_t = qr_ts[js], qc_ts[js], kr_ts[js], kc_ts[js], v_ts[js], v2s[js], sg_ts[js]
                    if first:
                        s.dma(qr_t.full(), QR.view(2 * j * 128 * L, [[L, 128], [128 * L, 2], [1, L]]))
                        s.dma(qc_t.full(), QC.view(2 * j * 128 * CTX, [[CTX, 128], [128 * CTX, 2], [1, CTX]]))
                        s.dma(kr_t.full(), KR[j])
                        s.dma(kc_t.full(), KC[j])
                        s.dma(v_t.full(), VT.view(j * 64, [[256, 128], [128 * 256, NT], [1, 64]]))
                        s.dma(sg_t.full(), SG.view(2 * j * 128 * T, [[T, 128], [128 * T, 2], [1, T]]))
                        s.copy(v2[:, :, 0:64], v_t.full(), eng="pool")
                        s.copy(v2[:, :, 64:128], v_t.full(), eng="pool")
                    qsrc, q0 = (qc_t, bi * 128) if kind == "c" else (qr_t, bi * 128)
                    pts = pt[n % NP]
                    qw = qsrc.h.shape[2]
                    for ki, (kk, kb, msk) in enumerate(keys_of(kind, bi)):
                        ksrc = kc_t if kk == "c" else kr_t
                        for par in range(2):
                            p0 = par * 64
                            bs = nbank()
                            s.mm(bs[:, 0:256],
                                 [(ksrc[p0:p0 + 64, kb * 128:(kb + 1) * 128],
                                   qsrc.view(p0 * 2 * qw + q0, [[2 * qw, 64], [qw, 2], [1, 128]]))])
                            s.act(pts[ki][:, par * 256:(par + 1) * 256], bs[:, 0:256], AF.Exp, scale=c8[:, 0:1])
                        if msk is not None:
                            moff = 1024 if msk == "prev" else 512
                            s.tt(pts[ki].view(0, [[512, 128], [128, 4], [1, 128]]),
                                 pts[ki].view(0, [[512, 128], [128, 4], [1, 128]]),
                                 cst.view(moff, [[3072, 128], [0, 4], [1, 128]]), ALU.mult, eng="pool")

                def atB(item):
                    n, j, kind, bi, first, last = item
                    js = j % J2
                    v2, sg_t, ast = v2s[js], sg_ts[js], asts[js]
                    tok0 = bi * 128 if kind == "c" else CTX + bi * 128
                    keys = keys_of(kind, bi)
                    pts = pt[n % NP]
                    rd, ao = rds[n % 2], aos[n % 2]
                    vt_idx = [(kb if kk == "c" else 2 + kb) for (kk, kb, _) in keys]
                    bn = nbank()
                    s.mm(bn.full(), [(v2[:, vt_idx[ki], :], pts[ki].full()) for ki in range(len(keys))])
                    bd = nbank()
                    s.mm(bd.full(), [(onesb, pts[ki].full()) for ki in range(len(keys))])
                    for par in range(2):
                        p0 = par * 64
                        for c in range(2):
                            s.ts(rd[p0:p0 + 64, c * 128:(c + 1) * 128],
                                 bd[p0:p0 + 64, par * 256 + c * 128:par * 256 + (c + 1) * 128],
                                 es_pp[p0:p0 + 64, 2 * j + c:2 * j + c + 1], None, ALU.add)
                    s.recip(rd.full(), rd.full())
                    for par in range(2):
                        p0 = par * 64
                        s.tt(ao[p0:p0 + 64, :], bn[p0:p0 + 64, par * 256:(par + 1) * 256], rd[p0:p0 + 64, :], ALU.mult)
                    s.tt(ast.view(tok0, [[2 * T, 128], [T, 2], [1, 128]]),
                         ao.view(0, [[256, 128], [128, 2], [1, 128]]),
                         sg_t.view(tok0, [[2 * T, 128], [T, 2], [1, 128]]), ALU.mult)
                    if last:
                        s.dma(YT.view((8 + 2 * j) * 128 * T, [[T, 128], [128 * T, 2], [1, T]]), ast.full())

                pipeline(items, [atA, atB])
                s.flush()

            with ExitStack() as es:
                nb_ = {"i": 0}

                def nbank():
                    bk = banks[nb_["i"] % 8]
                    nb_["i"] += 1
                    return bk

                yt = [cx.sb(es, "yt%d" % i, [128, 16, 128], BF16) for i in range(2)]
                xt = [cx.sb(es, "xt5_%d" % i, [128, D]) for i in range(2)]
                x1t = [cx.sb(es, "x1t%d" % i, [128, D]) for i in range(2)]
                tmp5s = [cx.sb(es, "tmp5_%d" % i, [128, 512]) for i in range(2)]
                for i in range(NT if go("p5") else 0):
                    w = 1 if i < 2 else 0
                    y_, x_, o_ = yt[i % 2], xt[i % 2], x1t[i % 2]
                    s.dma(y_.full(), YT.view(i * 128, [[T, 128], [128 * T, 16], [1, 128]]))
                    s.dma(x_.full(), xin[i * 128:(i + 1) * 128, :])
                    for half in range(2):
                        tmp5 = tmp5s[half]
                        bk = nbank()
                        s.mm(bk.full(), [(y_[:, fc, :], wo[fc][:, half * 512:(half + 1) * 512]) for fc in range(16)])
                        s.tt(tmp5.full(), bk.full(), gate_bc[0][w][:, half * 512:(half + 1) * 512], ALU.mult)
                        s.tt(o_[:, half * 512:(half + 1) * 512], tmp5.full(), x_[:, half * 512:(half + 1) * 512], ALU.add)
                    s.dma(X1[i * 128:(i + 1) * 128, :], o_.full())
                s.flush()

        if go("all"):
            adaln_phase(1, o_ada_w, o_ada_b)
        with ExitStack() as l1:
            if not go("all"):
                return nc
            nb_ = {"i": 0}

            def nbank():
                bk = banks[nb_["i"] % 8]
                nb_["i"] += 1
                return bk

            with ExitStack() as es:
                nw = cx.sb(es, "nw1", [128, 8])
                sc1 = [cx.sb(es, "sc1b_%d" % w, [128, 8]) for w in range(2)]
                s.dma(nw.full(), o_norm_wT.full())
                for w in range(2):
                    s.stt(sc1[w].full(), modT[1][w][:, 8:16], 1.0, nw.full(), ALU.add, ALU.mult)
                hT = [cx.sb(es, "hTb%d" % k, [128, T], BF16) for k in range(8)]
                xt = [cx.sb(es, "xtb%d" % i, [128, D]) for i in range(2)]
                xn = [cx.sb(es, "xnb%d" % i, [128, D]) for i in range(2)]
                junk = cx.sb(es, "junkb", [128, D])
                st = [cx.sb(es, "stb%d" % i, [128, 4]) for i in range(2)]
                for i in range(NT):
                    w = 1 if i < 2 else 0
                    x_, n_, st_ = xt[i % 2], xn[i % 2], st[i % 2]
                    s.dma(x_.full(), X1[i * 128:(i + 1) * 128, :])
                    s.act(junk.full(), x_.full(), AF.Square, accum=st_[:, 0:1])
                    s.ts(st_[:, 1:2], st_[:, 0:1], 1.0 / D, EPS, ALU.mult, ALU.add)
                    s.act(st_[:, 2:3], st_[:, 1:2], AF.Sqrt)
                    s.recip(st_[:, 3:4], st_[:, 2:3])
                    s.ts(n_.full(), x_.full(), st_[:, 3:4], None, ALU.mult)
                    for half in range(2):
                        bk = nbank()
                        for kk in range(4):
                            k = half * 4 + kk
                            s.transpose(bk[:, kk * 128:(kk + 1) * 128], n_[:, k * 128:(k + 1) * 128], ident)
                        for kk in range(4):
                            k = half * 4 + kk
                            s.act(hT[k][:, i * 128:(i + 1) * 128], bk[:, kk * 128:(kk + 1) * 128], AF.Identity,
                                  bias=modT[1][w][:, k:k + 1], scale=sc1[w][:, k:k + 1])
                wq = [cx.sb(es, "wq%d" % i, [128, 8, 256], BF16) for i in range(8)]
                for q8 in range(8):
                    s.dma(wq[q8].full(), o_w_in.view(q8 * 256, [[2 * D, 128], [128 * 2 * D, 8], [1, 256]]), q="pool")
                ot = [cx.sb(es, "ot%d" % i, [128, D]) for i in range(2)]
                oi = 0
                for which in range(2):
                    for i in range(NT):
                        if which == 1 and i < 2:
                            continue
                        o_ = ot[oi % 2]
                        oi += 1
                        for half in range(2):
                            bk = nbank()
                            for q4 in range(2):
                                s.mm(bk[:, q4 * 256:(q4 + 1) * 256],
                                     [(hT[k][:, i * 128:(i + 1) * 128], wq[which * 4 + half * 2 + q4][:, k, :]) for k in range(8)])
                            if which == 0:
                                s.copy(o_[:, half * 512:(half + 1) * 512], bk.full(), eng="act")
                            else:
                                s.act(o_[:, half * 512:(half + 1) * 512], bk.full(), AF.Silu)
                        s.dma((U if which == 0 else SG1)[i * 128:(i + 1) * 128, :], o_.full())
                s.flush()

            L1S = os.environ.get('L1S', 'z')
            if L1S == 'a':
                return nc
            with ExitStack() as es:
                lam = cx.sb(es, "lam", [128, 2, 3, 32])
                bprm = cx.sb(es, "bprm", [128, 2, 32, 16])
                cprm = cx.sb(es, "cprm", [128, 2, 32, 16])
                s.dma(lam.full(), s5_lam.full())
                s.dma(bprm.full(), s5_b.full())
                s.dma(cprm.full(), s5_c.full())
                kc = cx.sb(es, "kconst", [128, 4])
                s.memset(kc[:, 0:1], 1.0 / 16)
                s.memset(kc[:, 1:2], math.pi / 2)
                s.memset(kc[:, 2:3], 0.0)
                s.memset(kc[:, 3:4], 1.0)
                W64 = [128, 2, 32]

                def t64(name):
                    return cx.sb(es, name, W64)

                def lv(i):
                    return lam.view(i * 32, [[192, 128], [96, 2], [1, 32]])

                dt_ = t64("dt_"); mag = t64("mag"); th = t64("th"); cs = t64("cs"); sn = t64("sn")
                t_a = t64("t_a"); t_b = t64("t_b"); t_c = t64("t_c")
                abre = t64("abre"); abim = t64("abim"); cre = t64("cre"); cim = t64("cim")
                s.act(dt_.full(), lv(2), AF.Exp)
                s.tt(t_a.full(), lv(0), dt_.full(), ALU.mult)
                s.act(mag.full(), t_a.full(), AF.Exp)
                s.tt(th.full(), lv(1), dt_.full(), ALU.mult)
                s.act(sn.full(), th.full(), AF.Sin, scale=kc[:, 0:1])
                s.act(cs.full(), th.full(), AF.Sin, scale=kc[:, 0:1], bias=kc[:, 1:2])
                for _ in range(4):
                    s.tt(t_a.full(), cs.full(), cs.full(), ALU.mult)
                    s.tt(t_b.full(), sn.full(), sn.full(), ALU.mult)
                    s.tt(t_c.full(), sn.full(), cs.full(), ALU.mult)
                    s.tt(cs.full(), t_a.full(), t_b.full(), ALU.subtract)
                    s.ts(sn.full(), t_c.full(), 2.0, None, ALU.mult)
                s.tt(abre.full(), mag.full(), cs.full(), ALU.mult)
                s.tt(abim.full(), mag.full(), sn.full(), ALU.mult)
                PW = cx.sb(es, "PW", [128, 2, 9, 64])

                def pw(ri, k):
                    return PW.view((ri * 9 + k) * 64, [[2 * 9 * 64, 128], [32, 2], [1, 32]])

                s.memset(PW[:, 0, 0, :], 1.0)
                s.memset(PW[:, 1, 0, :], 0.0)
                for k in range(8):
                    s.tt(t_a.full(), pw(0, k), abre.full(), ALU.mult)
                    s.tt(t_b.full(), pw(1, k), abim.full(), ALU.mult)
                    s.tt(pw(0, k + 1), t_a.full(), t_b.full(), ALU.subtract)
                    s.tt(t_a.full(), pw(0, k), abim.full(), ALU.mult)
                    s.tt(t_b.full(), pw(1, k), abre.full(), ALU.mult)
                    s.tt(pw(1, k + 1), t_a.full(), t_b.full(), ALU.add)
                s.ts(t_c.full(), abre.full(), -1.0, None, ALU.add)
                s.tt(t_a.full(), lv(0), lv(0), ALU.mult)
                s.tt(t_b.full(), lv(1), lv(1), ALU.mult)
                s.tt(t_a.full(), t_a.full(), t_b.full(), ALU.add)
                s.recip(dt_.full(), t_a.full())
                s.tt(t_a.full(), t_c.full(), lv(0), ALU.mult)
                s.tt(t_b.full(), abim.full(), lv(1), ALU.mult)
                s.tt(t_a.full(), t_a.full(), t_b.full(), ALU.add)
                s.tt(cre.full(), t_a.full(), dt_.full(), ALU.mult)
                s.tt(t_a.full(), abim.full(), lv(0), ALU.mult)
                s.tt(t_b.full(), t_c.full(), lv(1), ALU.mult)
                s.tt(t_a.full(), t_a.full(), t_b.full(), ALU.subtract)
                s.tt(cim.full(), t_a.full(), dt_.full(), ALU.mult)
                BB = cx.sb(es, "BB", [128, 2, 2, 512])
                tb1 = cx.sb(es, "tb1", [128, 512])
                tb2 = cx.sb(es, "tb2", [128, 512])

                def bb(ri, d_, g0=0, ng=32):
                    return BB.view((ri * 2 + d_) * 512 + g0 * 16, [[2048, 128], [16, ng], [1, 16]])

                def v3(buf, off, pstep, n1, s1, n2, s2):
                    return buf.view(off, [[pstep, 128], [s1, n1], [s2, n2]])

                def prm(buf, ri, g0=0, ng=32):
                    return buf.view(ri * 512 + g0 * 16, [[1024, 128], [16, ng], [1, 16]])

                def cf(buf, d_, g0=0, ng=32, n2=16):
                    return buf.view(d_ * 32 + g0, [[64, 128], [1, ng], [0, n2]])

                t1v = v3(tb1, 0, 512, 32, 16, 16, 1)
                t2v = v3(tb2, 0, 512, 32, 16, 16, 1)
                for d_ in range(2):
                    s.tt(t1v, prm(bprm, 0), cf(cre, d_), ALU.mult)
                    s.tt(t2v, prm(bprm, 1), cf(cim, d_), ALU.mult)
                    s.tt(bb(0, d_), t1v, t2v, ALU.subtract)
                    s.tt(t1v, prm(bprm, 1), cf(cre, d_), ALU.mult)
                    s.tt(t2v, prm(bprm, 0), cf(cim, d_), ALU.mult)
                    s.tt(bb(1, d_), t1v, t2v, ALU.add)
                LA = cx.sb(es, "LA", [128, 2, 32, 2])
                LB = cx.sb(es, "LB", [128, 2, 32, 2])
                for ri in range(2):
                    s.copy(LA.view(ri, [[128, 128], [64, 2], [2, 32]]), pw(0, 8))
                s.ts(LB.view(0, [[128, 128], [64, 2], [2, 32]]), pw(1, 8), -1.0, None, ALU.mult)
                s.copy(LB.view(1, [[128, 128], [64, 2], [2, 32]]), pw(1, 8))
                zt_ = cx.sb(es, "zt_", [16, 16, 112])
                s.memset(zt_.full(), 0.0)
                s.flush()

                if L1S == 'b':
                    return nc
                for b in range(4 if L1S not in ('c1', 'd1', 'e1', 'f1', 'g1') else 1):
                    g0 = 8 * b
                    with ExitStack() as bs_:
                        CAB = cx.sb(bs_, "CAB", [128, 2, 2, 8 * 144])
                        WST = cx.sb(bs_, "WST", [128, 8, 2, 2, 2, 64])
                        TF = cx.sb(bs_, "TF", [128, 16, 128])
                        TB = cx.sb(bs_, "TB", [128, 16, 128])

                        with ExitStack() as tmp:
                            WT = cx.sb(tmp, "WT", [128, 2, 2, 8 * 128])
                            KSB = cx.sb(tmp, "KSB", [16, 2, 16, 128])
                            c1 = cx.sb(tmp, "c1", [128, 128])
                            c2 = cx.sb(tmp, "c2", [128, 128])
                            c1v = v3(c1, 0, 128, 8, 16, 16, 1)
                            c2v = v3(c2, 0, 128, 8, 16, 16, 1)
                            for d_ in range(2):
                                for idx in range(9):
                                    p_ = idx if d_ == 0 else 8 - idx
                                    pr = PW.view((0 * 9 + p_) * 64 + d_ * 32 + g0, [[1152, 128], [1, 8], [0, 16]])
                                    pi_ = PW.view((1 * 9 + p_) * 64 + d_ * 32 + g0, [[1152, 128], [1, 8], [0, 16]])
                                    o_re = CAB.view((0 * 2 + d_) * 1152 + idx * 16, [[4608, 128], [144, 8], [1, 16]])
                                    o_im = CAB.view((1 * 2 + d_) * 1152 + idx * 16, [[4608, 128], [144, 8], [1, 16]])
                                    s.tt(c1v, prm(cprm, 0, g0, 8), pr, ALU.mult)
                                    s.tt(c2v, prm(cprm, 1, g0, 8), pi_, ALU.mult)
                                    s.tt(o_re, c1v, c2v, ALU.subtract)
                                    s.tt(c1v, prm(cprm, 0, g0, 8), pi_, ALU.mult)
                                    s.tt(c2v, prm(cprm, 1, g0, 8), pr, ALU.mult)
                                    s.stt(o_im, c1v, -1.0, c2v, ALU.mult, ALU.subtract)
                                for ss in range(8):
                                    p_ = 7 - ss if d_ == 0 else ss
                                    pr = PW.view((0 * 9 + p_) * 64 + d_ * 32 + g0, [[1152, 128], [1, 8], [0, 16]])
                                    pi_ = PW.view((1 * 9 + p_) * 64 + d_ * 32 + g0, [[1152, 128], [1, 8], [0, 16]])
                                    o_re = WT.view((d_ * 2 + 0) * 1024 + ss * 16, [[4096, 128], [128, 8], [1, 16]])
                                    o_im = WT.view((d_ * 2 + 1) * 1024 + ss * 16, [[4096, 128], [128, 8], [1, 16]])
                                    s.tt(c1v, bb(0, d_, g0, 8), pr, ALU.mult)
                                    s.tt(c2v, bb(1, d_, g0, 8), pi_, ALU.mult)
                                    s.tt(o_re, c1v, c2v, ALU.subtract)
                                    s.tt(c1v, bb(1, d_, g0, 8), pr, ALU.mult)
                                    s.tt(c2v, bb(0, d_, g0, 8), pi_, ALU.mult)
                                    s.tt(o_im, c1v, c2v, ALU.add)
                            for gh in range(2):
                                p0 = gh * 64
                                for gq in range(8):
                                    bk = nbank()
                                    for d_ in range(2):
                                        for ri in range(2):
                                            sl = d_ * 2 + ri
                                            s.transpose(bk[:, sl * 64:(sl + 1) * 64],
                                                        WT.view(p0 * 4096 + (d_ * 2 + ri) * 1024 + gq * 128, [[4096, 64], [1, 128]]),
                                                        cst[p0:p0 + 64, 0, p0:p0 + 64])
                                    s.copy(WST.view(((gq * 2 + gh) * 4) * 64, [[4096, 128], [1, 256]]), bk[:, 0:256], eng="act")
                                for d_ in range(2):
                                    for gqq in range(2):
                                        bk = nbank()
                                        for q4 in range(4):
                                            gq = gqq * 4 + q4
                                            i0 = 0 if d_ == 0 else 1
                                            s.mm(bk[0:16, q4 * 128:(q4 + 1) * 128],
                                                 [(BB.view(p0 * 2048 + (0 * 2 + d_) * 512 + (g0 + gq) * 16, [[2048, 64], [1, 16]]),
                                                   CAB.view(p0 * 4608 + (0 * 2 + d_) * 1152 + gq * 144 + i0 * 16, [[4608, 64], [1, 128]])),
                                                  (BB.view(p0 * 2048 + (1 * 2 + d_) * 512 + (g0 + gq) * 16, [[2048, 64], [1, 16]]),
                                                   CAB.view(p0 * 4608 + (1 * 2 + d_) * 1152 + gq * 144 + i0 * 16, [[4608, 64], [1, 128]]))])
                                        s.copy(KSB.view(d_ * 2048 + (2 * gqq * 4 + gh) * 128, [[4096, 16], [256, 4], [1, 128]]),
                                               bk.view(0, [[512, 16], [128, 4], [1, 128]]), eng="act")
                            gbase = 16 * b
                            s.dma(KFP.view(gbase * 3840 + 7 * 16, [[240, 16], [3840, 16], [1, 128]]), KSB[:, 0, :, :])
                            s.dma(KBR.view(gbase * 3840, [[240, 16], [3840, 16], [1, 128]]), KSB[:, 1, :, :])
                            s.dma(KFP.view(gbase * 3840, [[240, 16], [3840, 16], [1, 112]]), zt_.full())
                            s.dma(KBR.view(gbase * 3840 + 128, [[240, 16], [3840, 16], [1, 112]]), zt_.full())
                            for ss in range(8):
                                s.dma(TF[ss * 16:(ss + 1) * 16, :, :], KFP.view(gbase * 3840 + (7 - ss) * 16, [[240, 16], [3840, 16], [1, 128]]))
                                s.dma(TB[ss * 16:(ss + 1) * 16, :, :], KBR.view(gbase * 3840 + (7 - ss) * 16, [[240, 16], [3840, 16], [1, 128]]))
                            s.flush()

                        if L1S in ('c', 'c1'):
                            continue
                        u8b = cx.sb(bs_, "u8b", [128, 8, 256])
                        u8g = cx.sb(bs_, "u8g", [128, 16, 128])
                        U8T = cx.sb(bs_, "U8T", [128, 16, 288])
                        NCOL = 326
                        PS = 16 * NCOL
                        SSD = [cx.sb(bs_, "SS%d" % i, [128, 8, 2, NCOL]) for i in range(2)]
                        CAR = [cx.sb(bs_, "CAR%d" % i, [128, 7, 8, 2]) for i in range(2)]
                        A36 = [cx.sb(bs_, "A36_%d" % i, [128, 8, 2]) for i in range(2)]
                        B36 = [cx.sb(bs_, "B36_%d" % i, [128, 8, 2]) for i in range(2)]
                        y8b = cx.sb(bs_, "y8b", [128, 8, 256])
                        ysb = cx.sb(bs_, "ysb", [128, 512])
                        TT1 = [cx.sb(bs_, "TT1_%d" % i, [128, 9, 8, 2]) for i in range(2)]
                        TT2 = [cx.sb(bs_, "TT2_%d" % i, [128, 9, 8, 2]) for i in range(2)]
                        for (j0, nj) in ((0, 32), (32, 128), (160, 128)):
                            s.dma(u8b[0:nj, :, :], U.view(8 * j0 * 1024 + 256 * b, [[8192, nj], [1024, 8], [1, 256]]))
                            s.copy(u8g.view(0, [[2048, nj], [128, 16], [16, 8], [1, 16]]),
                                   u8b.view(0, [[2048, nj], [16, 16], [256, 8], [1, 16]]), eng="act")
                            for gq4 in range(4):
                                bk = nbank()
                                for q4 in range(4):
                                    gi = gq4 * 4 + q4
                                    s.transpose(bk[:, q4 * 128:q4 * 128 + nj],
                                                u8g.view(128 * gi, [[2048, nj], [1, 128]]), cst[0:nj, 0, 0:nj])
                                s.copy(U8T.view(gq4 * 4 * 288 + j0, [[16 * 288, 128], [288, 4], [1, nj]]),
                                       bk.view(0, [[512, 128], [128, 4], [1, nj]]), eng="act")
                        if L1S in ('d', 'd1'):
                            s.flush()
                            continue
                        s.memset(SSD[0].view(0, [[PS, 128], [NCOL, 16], [1, 1]]), 0.0)
                        s.memset(SSD[0].view(289, [[PS, 128], [NCOL, 16], [1, 37]]), 0.0)
                        s.memset(SSD[1].view(288, [[PS, 128], [NCOL, 16], [1, 38]]), 0.0)
                        s.memset(SSD[0].view(289, [[PS, 128], [2 * NCOL, 8], [1, 1]]), 1.0)
                        s.memset(SSD[1].view(323, [[PS, 128], [2 * NCOL, 8], [1, 1]]), 1.0)
                        for gq in range(8):
                            for gh in range(2):
                                gi = 2 * gq + gh
                                p0 = gh * 64
                                for d_ in range(2):
                                    for ri in range(2):
                                        bk = nbank()
                                        s.mm(bk[p0:p0 + 64, 0:288],
                                             [(WST.view((((gq * 2 + gh) * 2 + d_) * 2 + ri) * 64, [[4096, 128], [1, 64]]),
                                               U8T[:, gi, :])])
                                        so = p0 * PS + (gq * 2 + ri) * NCOL
                                        if d_ == 0:
                                            s.copy(SSD[0].view(so + 1, [[PS, 64], [1, 288]]), bk[p0:p0 + 64, 0:288], eng="act")
                                        else:
                                            s.copy(SSD[1].view(so + 256, [[PS, 64], [1, 32]]), bk[p0:p0 + 64, 0:32], eng="act")
                                            s.copy(SSD[1].view(so, [[PS, 64], [1, 256]]), bk[p0:p0 + 64, 32:288], eng="act")
                        if L1S in ('e', 'e1'):
                            s.flush()
                            continue
                        DS = 8 * 2 * 289
                        RI, GQ = NCOL, 2 * NCOL
                        REC_ENG2 = os.environ.get('REC2', 'dve')

                        def cplx_step(items):
                            engs = ("dve", REC_ENG2)
                            for n_, (pv, psw, cv, ca, cb_, t1_, t2_) in enumerate(items):
                                s.tt(t1_, pv, ca, ALU.mult, eng=engs[n_ % 2])
                                s.tt(t2_, psw, cb_, ALU.mult, eng=engs[n_ % 2])
                            for n_, (pv, psw, cv, ca, cb_, t1_, t2_) in enumerate(items):
                                s.tt(t1_, t1_, t2_, ALU.add, eng=engs[n_ % 2])
                            for n_, (pv, psw, cv, ca, cb_, t1_, t2_) in enumerate(items):
                                if cv is not None:
                                    s.tt(cv, cv, t1_, ALU.add, eng=engs[n_ % 2])

                        def segv(SS, col, nseg):
                            return (SS.view(col, [[PS, 128], [36, nseg], [GQ, 8], [RI, 2]]),
                                    SS.view(col + RI, [[PS, 128], [36, nseg], [GQ, 8], [-RI, 2]]))

                        def coef(buf, d_, nseg):
                            return buf.view(d_ * 64 + g0 * 2, [[128, 128], [0, nseg], [2, 8], [1, 2]])

                        for k in range(1, 36):
                            items = []
                            for d_ in range(2):
                                pc = k if d_ == 0 else 36 - k
                                cc = k + 1 if d_ == 0 else 35 - k
                                pv, psw = segv(SSD[d_], pc, 9)
                                cv, _ = segv(SSD[d_], cc, 9)
                                items.append((pv, psw, cv, coef(LA, d_, 9), coef(LB, d_, 9), TT1[d_].full(), TT2[d_].full()))
                            cplx_step(items)
                        items = []
                        for d_ in range(2):
                            c35 = 324 if d_ == 0 else 288
                            pv, psw = segv(SSD[d_], c35, 1)
                            items.append((pv, psw, None, coef(LA, d_, 1), coef(LB, d_, 1),
                                          TT1[d_].view(0, [[144, 128], [16, 1], [2, 8], [1, 2]]),
                                          TT2[d_].view(0, [[144, 128], [16, 1], [2, 8], [1, 2]])))
                        cplx_step(items)
                        for d_ in range(2):
                            l36re = TT1[d_].view(0, [[144, 128], [2, 8], [0, 2]])
                            s.copy(A36[d_].full(), l36re)
                            s.ts(B36[d_][:, :, 0:1], TT1[d_].view(1, [[144, 128], [2, 8], [1, 1]]), -1.0, None, ALU.mult)
                            s.copy(B36[d_][:, :, 1:2], TT1[d_].view(1, [[144, 128], [2, 8], [1, 1]]))
                        for step in range(1, 8):
                            items = []
                            for d_ in range(2):
                                if d_ == 0:
                                    m = step
                                    cc, pc = 36 * m + 36, 36 * m
                                else:
                                    m = 7 - step
                                    cc, pc = 36 * m, 36 * m + 36
                                pv, psw = segv(SSD[d_], pc, 1)
                                cv, _ = segv(SSD[d_], cc, 1)
                                items.append((pv, psw, cv,
                                              A36[d_].view(0, [[16, 128], [0, 1], [2, 8], [1, 2]]),
                                              B36[d_].view(0, [[16, 128], [0, 1], [2, 8], [1, 2]]),
                                              TT1[d_].view(0, [[144, 128], [16, 1], [2, 8], [1, 2]]),
                                              TT2[d_].view(0, [[144, 128], [16, 1], [2, 8], [1, 2]])))
                            cplx_step(items)
                        items = []
                        for d_ in range(2):
                            pv, psw = segv(SSD[d_], 36, 7)
                            items.append((pv, psw, None, coef(LA, d_, 7), coef(LB, d_, 7),
                                          CAR[d_].full(), TT2[d_].view(0, [[144, 128], [16, 7], [2, 8], [1, 2]])))
                        cplx_step(items)
                        for d_ in range(2):
                            SS = SSD[d_]
                            sb0 = 37 if d_ == 0 else 1

                            def sview(ri):
                                return SS.view(sb0 + ri * RI, [[PS, 128], [36, 7], [GQ, 8], [1, 35]])

                            def tview(ri):
                                return SS.view(289 + ri * RI, [[PS, 128], [0, 7], [GQ, 8], [1, 35]])

                            def cview(ri):
                                return CAR[d_].view(ri, [[112, 128], [16, 7], [2, 8], [0, 35]])

                            w1 = (u8g if d_ == 0 else u8b).view(0, [[2048, 128], [280, 7], [35, 8], [1, 35]])
                            w2 = y8b.view(0, [[2048, 128], [280, 7], [35, 8], [1, 35]])
                            s.tt(w1, tview(0), cview(0), ALU.mult)
                            s.tt(w2, tview(1), cview(1), ALU.mult)
                            s.tt(w1, w1, w2, ALU.subtract)
                            s.tt(sview(0), sview(0), w1, ALU.add)
                            s.tt(w1, tview(0), cview(1), ALU.mult)
                            s.tt(w2, tview(1), cview(0), ALU.mult)
                            s.tt(w1, w1, w2, ALU.add)
                            s.tt(sview(1), sview(1), w1, ALU.add)
                        if L1S in ('f', 'f1'):
                            s.flush()
                            continue
                        for tt_ in range(2):
                            j0 = 32 + 128 * tt_
                            m0 = 128 * tt_
                            for gh in range(2):
                                p0 = gh * 64
                                for gqq in range(2):
                                    bx = nbank()
                                    by = nbank()
                                    for q4 in range(4):
                                        gq = gqq * 4 + q4
                                        gi = 2 * gq + gh
                                        s.mm(bx[:, q4 * 128:(q4 + 1) * 128],
                                             [(U8T[:, gi, j0:j0 + 128], TF[:, gi, :]), (U8T[:, gi, j0:j0 + 128], TB[:, gi, :])])
                                        pairs = []
                                        for d_ in range(2):
                                            c0 = j0 if d_ == 0 else m0 + 1
                                            i0 = 1 if d_ == 0 else 0
                                            for ri in range(2):
                                                so = p0 * PS + (gq * 2 + ri) * NCOL + c0
                                                pairs.append((SSD[d_].view(so, [[PS, 64], [1, 128]]),
                                                              CAB.view(p0 * 4608 + (ri * 2 + d_) * 1152 + gq * 144 + i0 * 16, [[4608, 64], [1, 128]])))
                                        s.mm(by[:, q4 * 128:(q4 + 1) * 128], pairs)
                                    s.copy(ysb.full(), by.full(), eng="act")
                                    s.tt(y8b.view(32 * gqq * 4 + 16 * gh, [[2048, 128], [32, 4], [256, 8], [1, 16]]),
                                         bx.view(0, [[512, 128], [128, 4], [16, 8], [1, 16]]),
                                         ysb.view(0, [[512, 128], [128, 4], [16, 8], [1, 16]]), ALU.add)
                            s.dma(YTOK.view((CTX + 8 * m0) * 1024 + 256 * b, [[8192, 128], [1024, 8], [1, 256]]), y8b.full())
                        s.flush()

            if L1S in ('g', 'g1'):
                return nc
            with ExitStack() as es:
                gw = [cx.sb(es, "gw%d" % k, [128, D], BF16) for k in range(8)]
                ow = [cx.sb(es, "ow%d" % k, [128, D], BF16) for k in range(8)]
                dskb = cx.sb(es, "dskb", [128, D])
                glbb = cx.sb(es, "glbb", [128, D])
                fnwb = cx.sb(es, "fnwb", [128, D])
                kg = cx.sb(es, "kg", [128, 1])
                s.memset(kg.full(), 2.0 * math.sqrt(2.0 / math.pi))
                for k in range(8):
                    s.dma(gw[k].full(), o_glu_w[k * 128:(k + 1) * 128, :], q="pool")
                    s.dma(ow[k].full(), o_w_out[k * 128:(k + 1) * 128, :], q="pool")
                s.dma(dskb.full(), o_d_skip.view(0, [[0, 128], [1, D]]))
                s.dma(glbb.full(), o_glu_b.view(0, [[0, 128], [1, D]]))
                s.dma(fnwb.full(), final_norm_w.view(0, [[0, 128], [1, D]]))
                NB3 = 3
                ya = [cx.sb(es, "ya%d" % i, [128, D]) for i in range(NB3)]
                ua = [cx.sb(es, "ua%d" % i, [128, D]) for i in range(NB3)]
                sga = [cx.sb(es, "sga%d" % i, [128, D]) for i in range(NB3)]
                xa = [cx.sb(es, "xa%d" % i, [128, D]) for i in range(NB3)]
                w1s = [cx.sb(es, "w1_%d" % i, [128, D]) for i in range(NB3)]
                w2s = [cx.sb(es, "w2_%d" % i, [128, D]) for i in range(NB3)]
                w3s = [cx.sb(es, "w3_%d" % i, [128, D]) for i in range(NB3)]
                tTs = [cx.sb(es, "tT_%d" % i, [128, 8, 128], BF16) for i in range(2 * NB3)]
                sts = [cx.sb(es, "st10_%d" % i, [128, 4]) for i in range(NB3)]

                def transp8(src, tT):
                    for half in range(2):
                        bk = nbank()
                        for kk in range(4):
                            k = half * 4 + kk
                            s.transpose(bk[:, kk * 128:(kk + 1) * 128], src[:, k * 128:(k + 1) * 128], ident)
                        s.copy(tT[:, half * 4:(half + 1) * 4, :], bk.view(0, [[512, 128], [128, 4], [1, 128]]), eng="act")

                TAILN = int(os.environ.get('TAILN', NT))

                def bufs(i):
                    b_ = i % NB3
                    return ya[b_], ua[b_], sga[b_], xa[b_], w1s[b_], w2s[b_], w3s[b_], tTs[2 * b_], tTs[2 * b_ + 1], sts[b_]

                def stage0(i):
                    y_, u_, g_, x_, w1, w2, w3, tTa, tTb, st = bufs(i)
                    s.dma(y_.full(), YTOK[i * 128:(i + 1) * 128, :])
                    s.dma(u_.full(), U[i * 128:(i + 1) * 128, :])
                    s.dma(g_.full(), SG1[i * 128:(i + 1) * 128, :])
                    s.dma(x_.full(), X1[i * 128:(i + 1) * 128, :])
                    s.tt(w1.full(), u_.full(), dskb.full(), ALU.mult)
                    s.tt(y_.full(), y_.full(), w1.full(), ALU.add)
                    s.tt(w1.full(), y_.full(), y_.full(), ALU.mult)
                    s.ts(w1.full(), w1.full(), 0.044715, 1.0, ALU.mult, ALU.add)
                    s.tt(w1.full(), w1.full(), y_.full(), ALU.mult)
                    s.act(w1.full(), w1.full(), AF.Sigmoid, scale=kg[:, 0:1])
                    s.tt(w2.full(), y_.full(), w1.full(), ALU.mult)
                    transp8(w2, tTa)

                def stage1(i):
                    y_, u_, g_, x_, w1, w2, w3, tTa, tTb, st = bufs(i)
                    for half in range(2):
                        bk = nbank()
                        s.mm(bk.full(), [(tTa[:, k, :], gw[k][:, half * 512:(half + 1) * 512]) for k in range(8)])
                        s.tt(w1[:, half * 512:(half + 1) * 512], bk.full(), glbb[:, half * 512:(half + 1) * 512], ALU.add)
                    s.act(w1.full(), w1.full(), AF.Sigmoid)
                    s.tt(w2.full(), w2.full(), w1.full(), ALU.mult)
                    s.tt(w2.full(), w2.full(), g_.full(), ALU.mult)
                    transp8(w2, tTb)

                def stage2(i):
                    y_, u_, g_, x_, w1, w2, w3, tTa, tTb, st = bufs(i)
                    for half in range(2):
                        bk = nbank()
                        s.mm(bk.full(), [(tTb[:, k, :], ow[k][:, half * 512:(half + 1) * 512]) for k in range(8)])
                        s.tt(w1[:, half * 512:(half + 1) * 512], bk.full(), gate_bc[1][0][:, half * 512:(half + 1) * 512], ALU.mult)
                    s.tt(w3.full(), w1.full(), x_.full(), ALU.add)
                    s.act(w1.full(), w3.full(), AF.Square, accum=st[:, 0:1])
                    s.ts(st[:, 1:2], st[:, 0:1], 1.0 / D, EPS, ALU.mult, ALU.add)
                    s.act(st[:, 2:3], st[:, 1:2], AF.Sqrt)
                    s.recip(st[:, 3:4], st[:, 2:3])
                    s.act(w3.full(), w3.full(), AF.Copy, scale=st[:, 3:4])
                    s.tt(w2.full(), w3.full(), fnwb.full(), ALU.mult)
                    s.dma(out_t[(i - 2) * 128:(i - 1) * 128, :], w2.full())

                pipeline(list(range(2, TAILN)), [stage0, stage1, stage2])
                s.flush()

    return nc


def _consts():
    c = np.zeros((128, 6, 512), np.float32)
    j = np.arange(128)[:, None]
    l = np.arange(128)[None, :]
    c[:, 0, :128] = np.eye(128, dtype=np.float32)
    c[:, 1, :128] = (j <= l)
    c[:, 2, :128] = (j >= l)
    c[:, 3, :] = 1.0
    nf = np.where(l < j, -30000.0, 0.0).astype(np.float32)
    nb = np.where(l > j, -30000.0, 0.0).astype(np.float32)
    c[:, 4, :] = np.tile(nf, (1, 4))
    c[:, 5, :] = np.tile(nb, (1, 4))
    return c


def _rope_tables():
    rows = L // 64
    row = np.repeat(np.arange(rows, dtype=np.float32), 64)
    col = np.tile(np.arange(64, dtype=np.float32), rows)
    n_freq = 16
    inv = (np.float32(10000.0) ** (-np.arange(n_freq, dtype=np.float32) / n_freq)).astype(np.float32)
    ang = np.concatenate([row[:, None] * inv, col[:, None] * inv], axis=-1).astype(np.float32)
    cos = np.cos(ang).astype(np.float32)
    sin = np.sin(ang).astype(np.float32)
    cosT = np.zeros((128, L), np.float32)
    sinT = np.zeros((128, L), np.float32)
    for h2 in range(2):
        for half in range(2):
            p0 = h2 * 64 + half * 32
            cosT[p0:p0 + 32] = cos.T
            sinT[p0:p0 + 32] = (-sin.T if half == 0 else sin.T)
    return np.stack([cosT, sinT], axis=1)


def _vecT(v, nchunk):
    return np.ascontiguousarray(np.asarray(v, np.float32).reshape(nchunk, 128).T)


def prep_inputs(b, inp):
    f = lambda a: np.ascontiguousarray(np.asarray(a, np.float32))
    m = {}
    m["xin"] = f(np.concatenate([inp["ctx"][b], inp["x"][b]], axis=0))
    cv = np.stack([inp["c"][b], inp["c_ctx"]], axis=0)
    m["cvecT"] = f(cv.reshape(2, 8, 128).transpose(2, 0, 1))
    m["consts"] = _consts()
    m["rope"] = _rope_tables()
    m["e_ada_w"] = f(inp["e_ada_w"][0])
    m["e_ada_b"] = f(inp["e_ada_b"][0]).reshape(1, -1)
    m["e_norm_wT"] = _vecT(inp["e_norm_w"][0], 8)
    w = f(inp["e_w_in"][0])
    q = w[:, OFF_Q:OFF_Q + 1024].reshape(D, 16, 2, 32)
    qs = q[:, :, ::-1, :].reshape(D, 1024)
    k = w[:, OFF_KV:OFF_KV + 256].reshape(D, 4, 64)
    kr = np.concatenate([k, k], axis=2).reshape(D, 512)
    ks = k.reshape(D, 4, 2, 32)[:, :, ::-1, :].reshape(D, 4, 64)
    ksr = np.concatenate([ks, ks], axis=2).reshape(D, 512)
    m["e_w_in"] = f(np.concatenate([w, qs, kr, ksr], axis=1))
    cw = f(inp["e_conv_w"][0])
    m["e_conv_wT"] = f(cw.reshape(5, 12, 128).transpose(2, 1, 0))
    m["e_conv_bT"] = _vecT(inp["e_conv_b"][0], 12)
    m["e_dt_bias"] = f(inp["e_dt_bias"][0]).reshape(1, 32)
    m["e_a_log"] = f(inp["e_a_log"][0]).reshape(1, 32)
    m["e_d_skip"] = f(inp["e_d_skip"][0]).reshape(1, 16)
    m["e_ssd_norm_wT"] = _vecT(inp["e_ssd_norm_w"][0], 8)
    sk = f(inp["e_sink"][0]).reshape(8, 2)
    m["e_sink"] = f(np.repeat(sk.T[:, None, :], 64, axis=1).reshape(128, 8))
    m["e_w_out"] = f(inp["e_w_out"][0])
    m["o_ada_w"] = f(inp["o_ada_w"][0])
    m["o_ada_b"] = f(inp["o_ada_b"][0]).reshape(1, -1)
    m["o_norm_wT"] = _vecT(inp["o_norm_w"][0], 8)
    m["o_w_in"] = f(inp["o_w_in"][0])

    def gl(a):
        a = np.asarray(a, np.float32)
        rest = a.shape[2:]
        a = a.reshape((32, 2, 64) + rest)
        a = np.moveaxis(a, 0, 2)
        return a.reshape((128, 32) + rest)

    lam = np.zeros((128, 2, 3, 32), np.float32)
    for d_ in range(2):
        lam[:, d_, 0] = gl(inp["o_lam_re"][0][d_])
        lam[:, d_, 1] = gl(inp["o_lam_im"][0][d_])
        lam[:, d_, 2] = gl(np.repeat(np.asarray(inp["o_log_step"][0][d_])[:, None], 64, axis=1))
    m["s5_lam"] = f(lam)
    m["s5_b"] = f(np.stack([gl(inp["o_b_re"][0]), gl(inp["o_b_im"][0])], axis=1))
    cr = np.asarray(inp["o_c_re"][0]).transpose(0, 2, 1)
    ci = np.asarray(inp["o_c_im"][0]).transpose(0, 2, 1)
    m["s5_c"] = f(np.stack([gl(cr), gl(ci)], axis=1))
    m["o_d_skip"] = f(inp["o_d_skip"][0]).reshape(1, -1)
    m["o_glu_w"] = f(inp["o_glu_w"][0])
    m["o_glu_b"] = f(inp["o_glu_b"][0]).reshape(1, -1)
    m["o_w_out"] = f(inp["o_w_out"][0])
    m["final_norm_w"] = f(inp["final_norm_w"]).reshape(1, -1)
    return m


def kernel(**inputs):
    nc = build_program()
    in_maps = [prep_inputs(b, inputs) for b in range(8)]
    res = run_bass_kernel_spmd(nc, in_maps, core_ids=list(range(8)))
    return np.stack([r["out"] for r in res.results], axis=0)
```

```python
import math
import os
from contextlib import ExitStack

import numpy as np
import concourse.bass as bass
import concourse.mybir as mybir
from concourse.bass_utils import run_bass_kernel_spmd

F32 = mybir.dt.float32
BF16 = mybir.dt.bfloat16
AF = mybir.ActivationFunctionType
ALU = mybir.AluOpType

D = 1024
T = 2304
NT = 18
CTX = 256
L = 2048
EPS = 1e-6
TG = [(0, 256), (256, 512), (768, 512), (1280, 512), (1792, 512)]

SES_ALL = os.environ.get('SES', '0') == '1'
SAME_ENGINE_SYNC = {'act': SES_ALL, 'dve': SES_ALL, 'pool': True, 'pe': False, 'sp': True}
SEM_EPOCH = 30000


class V:
    __slots__ = ("buf", "ap")

    def __init__(self, buf, ap):
        self.buf = buf
        self.ap = ap


class Buf:
    def __init__(self, name, h):
        self.name = name
        self.h = h
        self.last_w = None
        self.readers = []
        self.is_psum = False

    def __getitem__(self, idx):
        return V(self, self.h[idx])

    def full(self):
        return V(self, self.h.ap())

    def view(self, offset, pattern):
        return V(self, bass.AP(self.h, offset, [list(p) for p in pattern]))


class Sched:
    ENG = ("pe", "act", "dve", "pool", "sp")

    def __init__(self, nc):
        self.nc = nc
        self.prog = {e: [] for e in self.ENG}
        self.sem = {}
        self.cnt = {}
        self.semid = 0
        self.known = {e: {} for e in self.ENG}
        for e in ("pe", "act", "dve", "pool"):
            self._new_engine_sem(e)
        self.nds = 8
        self.dsem = {}
        self.duse = {}
        self.dcnt = {}
        for q in ("sp", "pool"):
            self.dsem[q] = []
            self.duse[q] = []
            for i in range(self.nds):
                key = "d_%s_%d" % (q, i)
                self.dsem[q].append((nc.alloc_semaphore(key), key))
                self.duse[q].append(0)
            self.dcnt[q] = 0
        self.n_ops = 0

    def _new_engine_sem(self, e):
        self.semid += 1
        key = "s_%s_%d" % (e, self.semid)
        self.sem[e] = (self.nc.alloc_semaphore(key), key)
        self.cnt[e] = 0

    def _deps(self, reads, writes):
        deps = {}

        def add(tok):
            if tok is None:
                return
            h, key, val = tok
            if key not in deps or deps[key][1] < val:
                deps[key] = (h, val)

        for r in reads:
            add(r.buf.last_w)
            if r.buf.is_psum:
                for t in r.buf.readers:
                    add(t)
        for w in writes:
            add(w.buf.last_w)
            for t in w.buf.readers:
                add(t)
        return deps

    def _emit_waits(self, eng, deps, own_key=None):
        kn = self.known[eng]
        for key, (h, val) in deps.items():
            if key == own_key and not SAME_ENGINE_SYNC[eng]:
                continue
            if kn.get(key, 0) >= val:
                continue
            kn[key] = val
            self.prog[eng].append(("wait", h, val))

    def _update(self, tok, reads, writes):
        for w in writes:
            w.buf.last_w = tok
            w.buf.readers = []
        for r in reads:
            if r.buf.last_w is not tok:
                r.buf.readers.append(tok)

    def op(self, eng, fn, reads=(), writes=()):
        reads = [r for r in reads if r is not None]
        writes = list(writes)
        if self.cnt[eng] >= SEM_EPOCH:
            self._new_engine_sem(eng)
        h, key = self.sem[eng]
        own = None if eng == "pe" else key
        deps = self._deps(reads, writes)
        if eng == "pe":
            deps.pop(key, None)
        self._emit_waits(eng, deps, own_key=own)
        self.cnt[eng] += 1
        self.prog[eng].append(("op", fn, h, 1))
        tok = (h, key, self.cnt[eng])
        self._update(tok, reads, writes)
        self.n_ops += 1
        return tok

    def dma(self, out, in_, q="sp", **kw):
        deps = self._deps([in_], [out])
        self._emit_waits(q, deps)
        k = self.dcnt[q] % self.nds
        self.dcnt[q] += 1
        h, key = self.dsem[q][k]
        prev = 16 * self.duse[q][k]
        if prev > 0 and self.known[q].get(key, 0) < prev:
            self.known[q][key] = prev
            self.prog[q].append(("wait", h, prev))
        self.duse[q][k] += 1
        val = 16 * self.duse[q][k]
        o_ap, i_ap = out.ap, in_.ap
        self.prog[q].append(("op", lambda e: e.dma_start(out=o_ap, in_=i_ap, **kw), h, 16))
        tok = (h, key, val)
        self._update(tok, [in_], [out])
        self.n_ops += 1
        return tok

    def finish_dmas(self):
        for q in ("sp", "pool"):
            for k in range(self.nds):
                h, key = self.dsem[q][k]
                val = 16 * self.duse[q][k]
                if val > 0 and self.known[q].get(key, 0) < val:
                    self.known[q][key] = val
                    self.prog[q].append(("wait", h, val))

    def flush(self, name=None):
        self.finish_dmas()
        nc = self.nc
        prog = self.prog
        self.prog = {e: [] for e in self.ENG}

        def run(items, e):
            for it in items:
                if it[0] == "wait":
                    e.wait_ge(it[1], it[2])
                else:
                    inst = it[1](e)
                    inst.then_inc(it[2], it[3])

        with nc.Block() as block:
            if prog["sp"]:
                @block.sync
                def _(e):
                    run(prog["sp"], e)
            if prog["act"]:
                @block.scalar
                def _(e):
                    run(prog["act"], e)
            if prog["dve"]:
                @block.vector
                def _(e):
                    run(prog["dve"], e)
            if prog["pool"]:
                @block.gpsimd
                def _(e):
                    run(prog["pool"], e)
            if prog["pe"]:
                @block.tensor
                def _(e):
                    run(prog["pe"], e)

    def mm(self, out, pairs):
        n = len(pairs)

        def fn(e):
            inst = None
            for i, (l, r) in enumerate(pairs):
                inst = e.matmul(out.ap, l.ap, r.ap, start=(i == 0), stop=(i == n - 1))
            return inst

        self.op("pe", fn, reads=[p[0] for p in pairs] + [p[1] for p in pairs], writes=[out])

    def transpose(self, out, in_, ident):
        self.op("pe", lambda e: e.transpose(out.ap, in_.ap, ident.ap), reads=[in_, ident], writes=[out])

    def act(self, out, in_, func, bias=None, scale=None, accum=None):
        kw = {}
        reads = [in_]
        writes = [out]
        if bias is not None:
            if isinstance(bias, V):
                kw["bias"] = bias.ap
                reads.append(bias)
            else:
                kw["bias"] = bias
        if scale is not None:
            if isinstance(scale, V):
                kw["scale"] = scale.ap
                reads.append(scale)
            else:
                kw["scale"] = scale
        if accum is not None:
            kw["accum_out"] = accum.ap
            writes.append(accum)
        self.op("act", lambda e: e.activation(out.ap, in_.ap, func, **kw), reads=reads, writes=writes)

    def ts(self, out, in0, s1, s2, op0, op1=None, eng="dve"):
        reads = [in0]
        a1 = s1
        a2 = s2
        if isinstance(s1, V):
            reads.append(s1)
            a1 = s1.ap
        if isinstance(s2, V):
            reads.append(s2)
            a2 = s2.ap
        if op1 is None:
            self.op(eng, lambda e: e.tensor_scalar(out.ap, in0.ap, a1, a2, op0), reads=reads, writes=[out])
        else:
            self.op(eng, lambda e: e.tensor_scalar(out.ap, in0.ap, a1, a2, op0, op1), reads=reads, writes=[out])

    def tt(self, out, in0, in1, op, eng="dve"):
        self.op(eng, lambda e: e.tensor_tensor(out.ap, in0.ap, in1.ap, op), reads=[in0, in1], writes=[out])

    def stt(self, out, in0, scalar, in1, op0, op1):
        reads = [in0, in1]
        sc = scalar
        if isinstance(scalar, V):
            reads.append(scalar)
            sc = scalar.ap
        self.op("dve", lambda e: e.scalar_tensor_tensor(out.ap, in0.ap, sc, in1.ap, op0, op1),
                reads=reads, writes=[out])

    def copy(self, out, in_, eng="dve"):
        if eng == "act":
            self.op("act", lambda e: e.copy(out.ap, in_.ap), reads=[in_], writes=[out])
        else:
            self.op(eng, lambda e: e.tensor_copy(out.ap, in_.ap), reads=[in_], writes=[out])

    def recip(self, out, in_):
        self.op("dve", lambda e: e.reciprocal(out.ap, in_.ap), reads=[in_], writes=[out])

    def memset(self, out, val, eng="dve"):
        self.op(eng, lambda e: e.memset(out.ap, val), reads=[], writes=[out])


class Ctx:
    def __init__(self, nc, sched):
        self.nc = nc
        self.s = sched
        self.uid = 0

    def sb(self, es, name, shape, dtype=F32):
        self.uid += 1
        h = es.enter_context(self.nc.sbuf_tensor("%s_%d" % (name, self.uid), list(shape), dtype))
        return Buf(name, h)

    def ps(self, es, name, shape=(128, 512), dtype=F32):
        self.uid += 1
        h = es.enter_context(self.nc.psum_tensor("%s_%d" % (name, self.uid), list(shape), dtype))
        b = Buf(name, h)
        b.is_psum = True
        return b

    def dram(self, name, shape, dtype=F32, kind="Internal"):
        h = self.nc.dram_tensor(name, list(shape), dtype, kind=kind)
        return Buf(name, h)


def pipeline(items, stages):
    n, k = len(items), len(stages)
    for t in range(n + k - 1):
        for j in range(k - 1, -1, -1):
            i = t - j
            if 0 <= i < n:
                stages[j](items[i])


def bc_mid(v_buf, base_off, pstep, nparts, n_outer, outer_step, n_inner):
    return v_buf.view(base_off, [[pstep, nparts], [outer_step, n_outer], [0, n_inner]])


E_NCOL = 5152
OFF_Z = 0
OFF_XBC = 1024
OFF_DT = 2560
OFF_Q = 2592
OFF_KV = 3616
OFF_G = 4128
OFF_QS = 5152
OFF_KR = 6176
OFF_KSR = 6688
E_NCOL_EXT = 7200


ORDER = ["p1", "p2a", "p2b", "p2c", "p2d", "p2e", "p2f", "p2g", "p2h", "p3", "p4", "p5", "all"]


def build_program(debug=(), stop="all"):
    def go(tag):
        return ORDER.index(tag) <= ORDER.index(stop)
    nc = bass.Bass("TRN2", target_bir_lowering=False)
    s = Sched(nc)
    cx = Ctx(nc, s)
    dbg = set(debug)

    def din(name, shape):
        return Buf(name, nc.dram_tensor(name, list(shape), F32, kind="ExternalInput"))

    def dout(name, shape):
        return Buf(name, nc.dram_tensor(name, list(shape), F32, kind="ExternalOutput"))

    def scratch(name, shape, dtype=F32):
        if name in dbg:
            return dout(name, shape)
        return Buf(name, nc.dram_tensor(name, list(shape), dtype))

    xin = din("xin", [T, D])
    cvecT = din("cvecT", [128, 2, 8])
    consts = din("consts", [128, 6, 512])
    rope = din("rope", [128, 2, L])
    e_ada_w = din("e_ada_w", [D, 3 * D])
    e_ada_b = din("e_ada_b", [1, 3 * D])
    e_norm_wT = din("e_norm_wT", [128, 8])
    e_w_in = din("e_w_in", [D, E_NCOL_EXT])
    e_conv_wT = din("e_conv_wT", [128, 12, 5])
    e_conv_bT = din("e_conv_bT", [128, 12])
    e_dt_bias = din("e_dt_bias", [1, 32])
    e_a_log = din("e_a_log", [1, 32])
    e_d_skip = din("e_d_skip", [1, 16])
    e_ssd_norm_wT = din("e_ssd_norm_wT", [128, 8])
    e_sink = din("e_sink", [128, 8])
    e_w_out = din("e_w_out", [2 * D, D])
    o_ada_w = din("o_ada_w", [D, 3 * D])
    o_ada_b = din("o_ada_b", [1, 3 * D])
    o_norm_wT = din("o_norm_wT", [128, 8])
    o_w_in = din("o_w_in", [D, 2 * D])
    s5_lam = din("s5_lam", [128, 2, 3, 32])
    s5_b = din("s5_b", [128, 2, 32, 16])
    s5_c = din("s5_c", [128, 2, 32, 16])
    o_d_skip = din("o_d_skip", [1, D])
    o_glu_w = din("o_glu_w", [D, D])
    o_glu_b = din("o_glu_b", [1, D])
    o_w_out = din("o_w_out", [D, D])
    final_norm_w = din("final_norm_w", [1, D])
    out_t = dout("out", [L, D])

    XS = scratch("XS", [T, 1024])
    BTOK = scratch("BTOK", [T, 256], BF16)
    BT = scratch("BT", [2, 128, T], BF16)
    CT = scratch("CT", [2, 128, T], BF16)
    SZ = scratch("SZ", [T, 1024])
    QR = scratch("QR", [8, 128, L], BF16)
    QC = scratch("QC", [8, 128, CTX], BF16)
    KR = scratch("KR", [4, 128, L], BF16)
    KC = scratch("KC", [4, 128, CTX], BF16)
    VT = scratch("VT", [T, 256], BF16)
    SG = scratch("SG", [8, 128, T])
    YF = scratch("YF", [T, 1024])
    YT = scratch("YT", [16, 128, T], BF16)
    X1 = scratch("X1", [T, 1024])
    U = scratch("U", [T, 1024])
    SG1 = scratch("SG1", [T, 1024])
    YTOK = scratch("YTOK", [T, 1024])
    KFP = scratch("KFP", [64, 16, 15, 16])
    KBR = scratch("KBR", [64, 16, 15, 16])
    HT = scratch("HT", [8, 128, T]) if "HT" in dbg else None
    DTD = scratch("DTD", [T, 32]) if "DTD" in dbg else None
    MODD = scratch("MODD", [4, 128, 24]) if "MODD" in dbg else None

    with ExitStack() as top:
        banks = [cx.ps(top, "bank%d" % i) for i in range(8)]
        cst = cx.sb(top, "cst", [128, 6, 512])
        s.dma(cst.full(), consts.full())
        ident = cst[:, 0, 0:128]
        tri = cst[:, 1, 0:128]
        utri = cst[:, 2, 0:128]
        ones = cst[:, 3, 0:128]
        onesb_t = cx.sb(top, "onesb", [128, 128], BF16)
        s.memset(onesb_t.full(), 1.0)
        onesb = onesb_t.full()
        modT = [[cx.sb(top, "modT%d%d" % (l, w), [128, 24]) for w in range(2)] for l in range(2)]
        gate_bc = [[cx.sb(top, "gate%d%d" % (l, w), [128, 1024]) for w in range(2)] for l in range(2)]
        scs = cx.sb(top, "scs", [128, 2, 8])

        def adaln_phase(layer, ada_w, ada_b):
            with ExitStack() as es:
                aw = [cx.sb(es, "aw%d" % k, [128, 3 * D]) for k in range(8)]
                ab = cx.sb(es, "ab", [1, 3 * D])
                modrow = [cx.sb(es, "modrow%d" % w, [1, 3 * D]) for w in range(2)]
                if layer == 0:
                    cv = cx.sb(es, "cv", [128, 2, 8])
                    s.dma(cv.full(), cvecT.full())
                    s.act(scs.full(), cv.full(), AF.Silu)
                for k in range(8):
                    s.dma(aw[k].full(), ada_w[k * 128:(k + 1) * 128, :])
                s.dma(ab.full(), ada_b.full())
                bi = 0
                for w in range(2):
                    for fg in range(6):
                        bk = banks[bi % 8]
                        bi += 1
                        s.mm(bk[0:1, :], [(scs[:, w, k:k + 1], aw[k][:, fg * 512:(fg + 1) * 512]) for k in range(8)])
                        s.tt(modrow[w][0:1, fg * 512:(fg + 1) * 512], bk[0:1, :], ab[0:1, fg * 512:(fg + 1) * 512], ALU.add)
                for w in range(2):
                    bk = banks[bi % 8]
                    bi += 1
                    for fc in range(24):
                        s.mm(bk[:, 2 * fc:2 * fc + 2], [(modrow[w][0:1, fc * 128:(fc + 1) * 128], cst[0:1, 3, 0:2])])
                    s.copy(modT[layer][w].full(), bk.view(0, [[512, 128], [2, 24]]))
                    for hh in range(2):
                        bk2 = banks[bi % 8]
                        bi += 1
                        s.mm(bk2.full(), [(cst[0:1, 3, 0:128], modrow[w][0:1, 2048 + hh * 512:2048 + (hh + 1) * 512])])
                        s.copy(gate_bc[layer][w][:, hh * 512:(hh + 1) * 512], bk2.full(), eng="act")
                    if MODD is not None:
                        s.dma(MODD[layer * 2 + w], modT[layer][w].full())
                s.flush()

        adaln_phase(0, e_ada_w, e_ada_b)

        with ExitStack() as l0:
            DT = cx.sb(l0, "DT", [128, NT, 32])
            DTA = cx.sb(l0, "DTA", [128, NT, 32])
            nw = cx.sb(l0, "nw", [128, 8])
            sc1 = [cx.sb(l0, "sc1_%d" % w, [128, 8]) for w in range(2)]
            s.dma(nw.full(), e_norm_wT.full())
            for w in range(2):
                s.stt(sc1[w].full(), modT[0][w][:, 8:16], 1.0, nw.full(), ALU.add, ALU.mult)

            wo = [cx.sb(l0, "wo%d" % k, [128, D], BF16) for k in range(16)]
            hts = ExitStack()
            hT = [cx.sb(hts, "hT%d" % k, [128, T], BF16) for k in range(8)]
            with ExitStack() as es:
                xt = [cx.sb(es, "xt%d" % i, [128, D]) for i in range(2)]
                xn = [cx.sb(es, "xn%d" % i, [128, D]) for i in range(2)]
                junk = cx.sb(es, "junk", [128, D])
                st = [cx.sb(es, "st%d" % i, [128, 4]) for i in range(2)]
                for i in range(NT):
                    w = 1 if i < 2 else 0
                    x_ = xt[i % 2]
                    n_ = xn[i % 2]
                    st_ = st[i % 2]
                    s.dma(x_.full(), xin[i * 128:(i + 1) * 128, :])
                    s.act(junk.full(), x_.full(), AF.Square, accum=st_[:, 0:1])
                    s.ts(st_[:, 1:2], st_[:, 0:1], 1.0 / D, EPS, ALU.mult, ALU.add)
                    s.act(st_[:, 2:3], st_[:, 1:2], AF.Sqrt)
                    s.recip(st_[:, 3:4], st_[:, 2:3])
                    s.ts(n_.full(), x_.full(), st_[:, 3:4], None, ALU.mult)
                    for half in range(2):
                        bk = banks[(2 * i + half) % 8]
                        for kk in range(4):
                            k = half * 4 + kk
                            s.transpose(bk[:, kk * 128:(kk + 1) * 128], n_[:, k * 128:(k + 1) * 128], ident)
                        for kk in range(4):
                            k = half * 4 + kk
                            s.act(hT[k][:, i * 128:(i + 1) * 128], bk[:, kk * 128:(kk + 1) * 128], AF.Identity,
                                  bias=modT[0][w][:, k:k + 1], scale=sc1[w][:, k:k + 1])
                if HT is not None:
                    for k in range(8):
                        s.dma(HT[k], hT[k].full())
                s.flush()

            with ExitStack() as es:
                WB = 256
                NWB, PF = 6, 4
                wbuf = [cx.sb(es, "wbuf%d" % i, [128, 8, WB], BF16) for i in range(NWB)]
                wplan = [(OFF_XBC + 256 * k, 256) for k in range(6)]
                for qc in range(8):
                    wplan += [(OFF_Q + qc * 128, 128), (OFF_QS + qc * 128, 128)]
                for j in range(4):
                    wplan += [(OFF_KR + j * 128, 128), (OFF_KSR + j * 128, 128)]
                wplan += [(OFF_G + 256 * k, 256) for k in range(4)]
                wplan += [(OFF_Z + 256 * k, 256) for k in range(4)]
                wplan += [(OFF_KV + 256, 256), (OFF_DT, 32)]
                wstate = {"i": 0, "issued": 0}

                def _issue(n):
                    col0, ncol = wplan[n]
                    wb = wbuf[n % NWB]
                    s.dma(wb[:, :, 0:ncol], e_w_in.view(col0, [[E_NCOL_EXT, 128], [128 * E_NCOL_EXT, 8], [1, ncol]]), q="pool")

                def load_w(col0, ncol=WB):
                    i = wstate["i"]
                    wstate["i"] += 1
                    assert wplan[i] == (col0, ncol), (i, wplan[i], col0, ncol)
                    while wstate["issued"] < min(i + PF + 1, len(wplan)):
                        _issue(wstate["issued"])
                        wstate["issued"] += 1
                    return wbuf[i % NWB]

                bstate = {"i": 0}

                def nbank():
                    bk = banks[bstate["i"] % 8]
                    bstate["i"] += 1
                    return bk

                def fm_mm(wb, cc, t0, n):
                    bk = nbank()
                    s.mm(bk[:, 0:n], [(wb[:, k, cc * 128:(cc + 1) * 128], hT[k][:, t0:t0 + n]) for k in range(8)])
                    return bk

                xraws = [cx.sb(es, "xraw%d" % i, [128, T]) for i in range(2)]
                accs = [cx.sb(es, "acc%d" % i, [128, T]) for i in range(2)]
                acc = accs[0]
                accbs = [cx.sb(es, "accb%d" % i, [128, T], BF16) for i in range(2)]
                accb = accbs[0]
                rc_i = {"i": 0}
                tmp1s = [cx.sb(es, "tmp1_%d" % i, [128, 512]) for i in range(2)]
                tmp2s = [cx.sb(es, "tmp2_%d" % i, [128, 512]) for i in range(2)]
                stg = [cx.sb(es, "stg%d" % i, [128, 4, 128]) for i in range(2)]
                stgb = [cx.sb(es, "stgb%d" % i, [128, 4, 128], BF16) for i in range(2)]
                rp = cx.sb(es, "rp", [128, 2, L])
                cw = cx.sb(es, "cw", [128, 12, 5])
                cb = cx.sb(es, "cb", [128, 12])
                dtb = cx.sb(es, "dtb", [128, 32])
                abc = cx.sb(es, "abc", [128, 32])
                s.dma(rp.full(), rope.full())
                s.dma(cw.full(), e_conv_wT.full())
                s.dma(cb.full(), e_conv_bT.full())
                s.dma(dtb.full(), e_dt_bias.view(0, [[0, 128], [1, 32]]))
                s.dma(abc.full(), e_a_log.view(0, [[0, 128], [1, 32]]))
                s.act(abc.full(), abc.full(), AF.Exp)
                s.ts(abc.full(), abc.full(), -1.0, None, ALU.mult)
                stg_i = {"i": 0}

                def transposes_to(dst, col0, src, lowp=False):
                    for i0 in range(0, NT, 4):
                        nb = min(4, NT - i0)
                        bk = nbank()
                        for ii in range(nb):
                            i = i0 + ii
                            s.transpose(bk[:, ii * 128:(ii + 1) * 128], src[:, i * 128:(i + 1) * 128], ident)
                        sg_ = (stgb if lowp else stg)[stg_i["i"] % 2]
                        stg_i["i"] += 1
                        s.copy(sg_[:, 0:nb, :], bk.view(0, [[512, 128], [128, nb], [1, 128]]), eng="act")
                        ncols = dst.h.shape[1]
                        s.dma(dst.view(i0 * 128 * ncols + col0, [[ncols, 128], [128 * ncols, nb], [1, 128]]),
                              sg_[:, 0:nb, :])

                wb_of = {}

                def xa(fc):
                    if fc % 2 == 0:
                        wb_of[fc // 2] = load_w(OFF_XBC + fc * 128)
                    wb = wb_of[fc // 2]
                    xraw = xraws[fc % 2]
                    for (t0, n) in TG:
                        bk = fm_mm(wb, fc % 2, t0, n)
                        s.copy(xraw[:, t0:t0 + n], bk[:, 0:n], eng="act")

                def xb(fc):
                    xraw, acc = xraws[fc % 2], accs[fc % 2]
                    s.ts(acc.full(), xraw.full(), cw[:, fc, 2:3], cb[:, fc:fc + 1], ALU.mult, ALU.add)
                    for kk in (0, 1, 3, 4):
                        d_ = kk - 2
                        for (s0, sl) in ((0, CTX), (CTX, L)):
                            lo = max(s0, s0 - d_)
                            hi = min(s0 + sl, s0 + sl - d_)
                            s.stt(acc[:, lo:hi], xraw[:, lo + d_:hi + d_], cw[:, fc, kk:kk + 1], acc[:, lo:hi],
                                  ALU.mult, ALU.add)
                    s.act(acc.full(), acc.full(), AF.Silu)
                    if fc < 8:
                        transposes_to(XS, fc * 128, acc)
                    elif fc < 10:
                        s.copy(accb.full(), acc.full(), eng="act")
                        s.dma(BT[fc - 8], accb.full())
                        transposes_to(BTOK, (fc - 8) * 128, acc, lowp=True)
                    else:
                        s.copy(accb.full(), acc.full(), eng="act")
                        s.dma(CT[fc - 10], accb.full())

                pipeline(list(range(12 if go('p2a') else 0)), [xa, xb])

                def rope_chunk(col_plain, col_swap, dst_rot, dst_ctx):
                    accb = accbs[rc_i["i"] % 2]
                    rc_i["i"] += 1
                    wa = load_w(col_plain, 128)
                    wsw = load_w(col_swap, 128)
                    for gi, (t0, n) in enumerate(TG):
                        bka = fm_mm(wa, 0, t0, n)
                        if gi == 0:
                            s.copy(accb[:, 0:CTX], bka[:, 0:CTX], eng="act")
                            continue
                        bkb = fm_mm(wsw, 0, t0, n)
                        l0 = t0 - CTX
                        tmp1, tmp2 = tmp1s[gi % 2], tmp2s[gi % 2]
                        s.tt(tmp1.full(), bka.full(), rp[:, 0, l0:l0 + 512], ALU.mult)
                        s.tt(tmp2.full(), bkb.full(), rp[:, 1, l0:l0 + 512], ALU.mult)
                        s.tt(accb[:, t0:t0 + n], tmp1.full(), tmp2.full(), ALU.add)
                    s.dma(dst_ctx, accb[:, 0:CTX])
                    s.dma(dst_rot, accb[:, CTX:T])

                for qc in range(8 if go('p2b') else 0):
                    rope_chunk(OFF_Q + qc * 128, OFF_QS + qc * 128, QR[qc], QC[qc])
                for j in range(4 if go('p2c') else 0):
                    rope_chunk(OFF_KR + j * 128, OFF_KSR + j * 128, KR[j], KC[j])

                for gc in range(8 if go('p2d') else 0):
                    acc = accs[gc % 2]
                    if gc % 2 == 0:
                        wb = load_w(OFF_G + gc * 128)
                    for (t0, n) in TG:
                        bk = fm_mm(wb, gc % 2, t0, n)
                        s.act(acc[:, t0:t0 + n], bk[:, 0:n], AF.Silu)
                    s.dma(SG[gc], acc.full())

                NT_E = NT if go('p2e') else 0
                wz = [load_w(OFF_Z + i * 256) for i in range(4)]
                for i in range(NT_E):
                    z_a = accs[i % 2]
                    for half in range(2):
                        bk = nbank()
                        for q4 in range(2):
                            wbz = wz[half * 2 + q4]
                            s.mm(bk[:, q4 * 256:(q4 + 1) * 256],
                                 [(hT[k][:, i * 128:(i + 1) * 128], wbz[:, k, :]) for k in range(8)])
                        s.act(z_a[:, half * 512:(half + 1) * 512], bk.full(), AF.Silu)
                    s.dma(SZ[i * 128:(i + 1) * 128, :], z_a[:, 0:1024])
                wv = load_w(OFF_KV + 256)
                wdt = load_w(OFF_DT, 32)
                vt = [cx.sb(es, "vt%d" % i, [128, 256], BF16) for i in range(2)]
                for i in range(NT if go('p2f') else 0):
                    bk = nbank()
                    s.mm(bk[:, 0:256], [(hT[k][:, i * 128:(i + 1) * 128], wv[:, k, :]) for k in range(8)])
                    s.copy(vt[i % 2].full(), bk[:, 0:256], eng="act")
                    s.dma(VT[i * 128:(i + 1) * 128, :], vt[i % 2].full())
                for i in range(NT if go('p2g') else 0):
                    bk = nbank()
                    s.mm(bk[:, 0:32], [(hT[k][:, i * 128:(i + 1) * 128], wdt[:, k, 0:32]) for k in range(8)])
                    s.tt(DT[:, i, :], bk[:, 0:32], dtb.full(), ALU.add)
                    if go('p2h'):
                        s.act(DT[:, i, :], DT[:, i, :], AF.Exp)
                        s.ts(DT[:, i, :], DT[:, i, :], 1.0, None, ALU.add)
                        s.act(DT[:, i, :], DT[:, i, :], AF.Ln)
                    s.tt(DTA[:, i, :], DT[:, i, :], abc.full(), ALU.mult)
                    if DTD is not None:
                        s.dma(DTD[i * 128:(i + 1) * 128, :], DT[:, i, :])
                s.flush()
            hts.close()
            for k in range(16):
                s.dma(wo[k].full(), e_w_out[k * 128:(k + 1) * 128, :], q="pool")

            with ExitStack() as es:
                nb_ = {"i": 0}

                def nbank():
                    bk = banks[nb_["i"] % 8]
                    nb_["i"] += 1
                    return bk

                N3 = 3
                xs_t = [cx.sb(es, "xs_t%d" % i, [128, 1024]) for i in range(N3)]
                b_t = [cx.sb(es, "b_t%d" % i, [128, 256], BF16) for i in range(N3)]
                bt_t = [cx.sb(es, "bt_t%d" % i, [128, 2, 128], BF16) for i in range(N3)]
                ct_t = [cx.sb(es, "ct_t%d" % i, [128, 2, 128], BF16) for i in range(N3)]
                yf_t = [cx.sb(es, "yf_t%d" % i, [128, 1024]) for i in range(N3)]
                sz_t = [cx.sb(es, "sz_t%d" % i, [128, 1024]) for i in range(N3)]
                MTs = [cx.sb(es, "MT%d" % i, [128, 2048], BF16) for i in range(N3)]
                xcs = [cx.sb(es, "xc%d" % i, [128, 1024], BF16) for i in range(N3)]
                xcds = [cx.sb(es, "xcd%d" % i, [128, 1024], BF16) for i in range(N3)]
                tmpos = [cx.sb(es, "tmpo%d" % i, [128, 1024]) for i in range(N3)]
                ytots = [cx.sb(es, "ytot%d" % i, [128, 1024]) for i in range(N3)]
                sms = [cx.sb(es, "sm%d" % i, [128, 4, 16]) for i in range(N3)]
                st3s = [cx.sb(es, "st3_%d" % i, [128, 4]) for i in range(N3)]
                ystgs = [cx.sb(es, "ystg%d" % i, [128, 8, 128], BF16) for i in range(2)]
                dtatris = [cx.sb(es, "dtatri%d" % i, [128, 2048]) for i in range(2)]
                decTs = [cx.sb(es, "decT%d" % i, [128, 2048]) for i in range(2)]
                cb_sbs = [cx.sb(es, "cb_sb%d" % i, [128, 256]) for i in range(2)]
                junk = cx.sb(es, "junk3", [128, 1024])
                Hs = [cx.sb(es, "Hs%d" % g, [128, 512]) for g in range(2)]
                Hb = [cx.sb(es, "Hb%d" % g, [128, 512], BF16) for g in range(2)]
                dsk = cx.sb(es, "dsk", [128, 16])
                snw = cx.sb(es, "snw", [128, 8])
                cm1 = cx.sb(es, "cm1", [128, 1])
                s.memset(cm1.full(), -1.0)
                s.dma(dsk.full(), e_d_skip.view(0, [[0, 128], [1, 16]]))
                s.dma(snw.full(), e_ssd_norm_wT.full())

                def bc3(buf, off, pstep, n1, s1, n2, s2):
                    return buf.view(off, [[pstep, 128], [s1, n1], [s2, n2]])

                n_ch = NT if go("p3") else 0
                for d_ in range(2):
                    order = list(range(NT)) if d_ == 0 else [1, 0] + list(range(NT - 1, 1, -1))
                    order = order[:n_ch]
                    TRIoff = 512 if d_ == 0 else 1024
                    TRIv = tri if d_ == 0 else utri
                    negm = cst[:, 4 + d_, :]
                    for g in range(2):
                        s.memset(Hs[g].full(), 0.0)
                        s.memset(Hb[g].full(), 0.0)

                    def stA(item, d_=d_, TRIoff=TRIoff, TRIv=TRIv, negm=negm):
                        ci, i = item
                        p3, p2 = ci % N3, ci % 2
                        xs_, b_, bt_, ct_ = xs_t[p3], b_t[p3], bt_t[p3], ct_t[p3]
                        MT, xc, xcd, sm = MTs[p3], xcs[p3], xcds[p3], sms[p3]
                        dtatri, decT, cb_sb = dtatris[p2], decTs[p2], cb_sbs[p2]
                        s.dma(xs_.full(), XS[i * 128:(i + 1) * 128, :])
                        s.dma(b_.full(), BTOK[i * 128:(i + 1) * 128, :])
                        s.dma(bt_.full(), BT.view(i * 128, [[T, 128], [128 * T, 2], [1, 128]]))
                        s.dma(ct_.full(), CT.view(i * 128, [[T, 128], [128 * T, 2], [1, 128]]))
                        if d_ == 1:
                            s.dma(yf_t[p3].full(), YF[i * 128:(i + 1) * 128, :])
                            s.dma(sz_t[p3].full(), SZ[i * 128:(i + 1) * 128, :])
                        dta_i = DTA[:, i, d_ * 16:(d_ + 1) * 16]
                        doff = i * 32 + d_ * 16
                        s.tt(bc3(dtatri, 0, 2048, 16, 128, 128, 1), bc3(DTA, doff, NT * 32, 16, 1, 128, 0),
                             bc3(cst, TRIoff, 3072, 16, 0, 128, 1), ALU.mult, eng="pool")
                        bs = nbank()
                        s.mm(bs[:, 0:16], [(TRIv, dta_i)])
                        s.mm(bs[:, 16:32], [(ones, dta_i)])
                        na, ea, de, cd = sm[:, 0, :], sm[:, 1, :], sm[:, 2, :], sm[:, 3, :]
                        s.ts(na, bs[:, 0:16], -1.0, None, ALU.mult)
                        s.act(ea, bs[:, 0:16], AF.Exp)
                        s.tt(de, bs[:, 16:32], na, ALU.add)
                        s.act(de, de, AF.Exp)
                        s.act(cd, bs[:, 16:32], AF.Exp)
                        for hq in range(4):
                            bq = nbank()
                            s.mm(bq.full(), [(ones, dtatri[:, hq * 512:(hq + 1) * 512]), (ident, negm)])
                            for hh in range(4):
                                h = hq * 4 + hh
                                s.act(decT[:, h * 128:(h + 1) * 128], bq[:, hh * 128:(hh + 1) * 128], AF.Exp,
                                      bias=sm[:, 0, h:h + 1])
                        bc = nbank()
                        for g in range(2):
                            s.mm(bc[:, g * 128:(g + 1) * 128], [(bt_[:, g, :], ct_[:, g, :])])
                        s.copy(cb_sb.full(), bc[:, 0:256], eng="act")
                        for g in range(2):
                            s.tt(bc3(MT, g * 1024, 2048, 8, 128, 128, 1), bc3(decT, g * 1024, 2048, 8, 128, 128, 1),
                                 bc3(cb_sb, g * 128, 256, 8, 0, 128, 1), ALU.mult)
                        s.tt(bc3(xc, 0, 1024, 16, 64, 64, 1), bc3(xs_, 0, 1024, 16, 64, 64, 1),
                             bc3(DT, doff, NT * 32, 16, 1, 64, 0), ALU.mult, eng="pool")
                        s.tt(bc3(xcd, 0, 1024, 16, 64, 64, 1), bc3(xc, 0, 1024, 16, 64, 64, 1),
                             bc3(sm, 32, 64, 16, 1, 64, 0), ALU.mult, eng="pool")
                        if d_ == 1:
                            s.tt(bc3(tmpos[p3], 0, 1024, 16, 64, 64, 1), bc3(xs_, 0, 1024, 16, 64, 64, 1),
                                 bc3(dsk, 0, 16, 16, 1, 64, 0), ALU.mult, eng="pool")
                            s.tt(yf_t[p3].full(), yf_t[p3].full(), tmpos[p3].full(), ALU.add, eng="pool")

                    def stB(item, d_=d_):
                        ci, i = item
                        p3 = ci % N3
                        b_, ct_ = b_t[p3], ct_t[p3]
                        MT, xc, xcd, sm, tmpo, ytot = MTs[p3], xcs[p3], xcds[p3], sms[p3], tmpos[p3], ytots[p3]
                        ydst = yf_t[p3] if d_ == 0 else ytot
                        for g in range(2):
                            by = nbank()
                            for hh in range(8):
                                h = g * 8 + hh
                                s.mm(by[:, hh * 64:(hh + 1) * 64], [(MT[:, h * 128:(h + 1) * 128], xc[:, h * 64:(h + 1) * 64])])
                            bo = nbank()
                            s.mm(bo.full(), [(ct_[:, g, :], Hb[g].full())])
                            s.tt(bc3(tmpo, g * 512, 1024, 8, 64, 64, 1), bc3(bo, 0, 512, 8, 64, 64, 1),
                                 bc3(sm, 16 + g * 8, 64, 8, 1, 64, 0), ALU.mult)
                            s.tt(ydst[:, g * 512:(g + 1) * 512], by.full(), tmpo[:, g * 512:(g + 1) * 512], ALU.add)
                        for g in range(2):
                            bst = nbank()
                            s.mm(bst.full(), [(b_[:, g * 128:(g + 1) * 128], xcd[:, g * 512:(g + 1) * 512])])
                            s.tt(bc3(Hs[g], 0, 512, 8, 64, 64, 1), bc3(Hs[g], 0, 512, 8, 64, 64, 1),
                                 bc3(sm, 48 + g * 8, 64, 8, 1, 64, 0), ALU.mult)
                            s.tt(Hs[g].full(), Hs[g].full(), bst.full(), ALU.add)
                            s.copy(Hb[g].full(), Hs[g].full(), eng="act")
                        if d_ == 0:
                            s.dma(YF[i * 128:(i + 1) * 128, :], yf_t[p3].full())

                    def stC(item, d_=d_):
                        if d_ == 0:
                            return
                        ci, i = item
                        p3, p2 = ci % N3, ci % 2
                        ytot, sz_, st3, ystg = ytots[p3], sz_t[p3], st3s[p3], ystgs[p2]
                        s.tt(ytot.full(), ytot.full(), yf_t[p3].full(), ALU.add)
                        s.tt(ytot.full(), ytot.full(), sz_.full(), ALU.mult)
                        s.act(junk.full(), ytot.full(), AF.Square, accum=st3[:, 0:1])
                        s.ts(st3[:, 1:2], st3[:, 0:1], 1.0 / 1024, EPS, ALU.mult, ALU.add)
                        s.act(st3[:, 2:3], st3[:, 1:2], AF.Sqrt)
                        s.recip(st3[:, 3:4], st3[:, 2:3])
                        s.act(ytot.full(), ytot.full(), AF.Copy, scale=st3[:, 3:4])
                        for half in range(2):
                            bk = nbank()
                            for kk in range(4):
                                k = half * 4 + kk
                                s.transpose(bk[:, kk * 128:(kk + 1) * 128], ytot[:, k * 128:(k + 1) * 128], ident)
                            for kk in range(4):
                                k = half * 4 + kk
                                s.act(ystg[:, k, :], bk[:, kk * 128:(kk + 1) * 128], AF.Copy, scale=snw[:, k:k + 1])
                        s.dma(YT.view(i * 128, [[T, 128], [128 * T, 8], [1, 128]]), ystg.full())

                    pipeline(list(enumerate(order)), [stA, stB, stC])
                s.flush()

            with ExitStack() as es:
                nb_ = {"i": 0}

                def nbank():
                    bk = banks[nb_["i"] % 8]
                    nb_["i"] += 1
                    return bk

                J2 = 2
                qr_ts = [cx.sb(es, "qr_t%d" % i, [128, 2, L], BF16) for i in range(J2)]
                qc_ts = [cx.sb(es, "qc_t%d" % i, [128, 2, CTX], BF16) for i in range(J2)]
                kr_ts = [cx.sb(es, "kr_t%d" % i, [128, L], BF16) for i in range(J2)]
                kc_ts = [cx.sb(es, "kc_t%d" % i, [128, CTX], BF16) for i in range(J2)]
                v_ts = [cx.sb(es, "v_t%d" % i, [128, NT, 64], BF16) for i in range(J2)]
                v2s = [cx.sb(es, "v2_%d" % i, [128, NT, 128], BF16) for i in range(J2)]
                sg_ts = [cx.sb(es, "sg_t%d" % i, [128, 2, T]) for i in range(J2)]
                asts = [cx.sb(es, "ast%d" % i, [128, 2, T], BF16) for i in range(J2)]
                NP = 3
                pt = [[cx.sb(es, "pt%d_%d" % (a, b), [128, 512], BF16) for b in range(5)] for a in range(NP)]
                rds = [cx.sb(es, "rd%d" % i, [128, 256]) for i in range(2)]
                aos = [cx.sb(es, "ao%d" % i, [128, 256]) for i in range(2)]
                es_pp = cx.sb(es, "es_pp", [128, 8])
                c8 = cx.sb(es, "c8", [128, 1])
                s.memset(c8.full(), 0.125)
                s.dma(es_pp.full(), e_sink.full())
                s.act(es_pp.full(), es_pp.full(), AF.Exp)
                ATT_DBG = [int(v) for v in os.environ.get("ATT_DBG", "4,18,4").split(",")]
                items = []
                for j in range(ATT_DBG[0] if go("p4") else 0):
                    qbs = ([("c", 0), ("c", 1)] + [("l", b) for b in range(16)])[:ATT_DBG[1]]
                    for qi, (kind, bi) in enumerate(qbs):
                        items.append((len(items), j, kind, bi, qi == 0, qi == len(qbs) - 1))

                def keys_of(kind, bi):
                    keys = [("c", 0, None), ("c", 1, None)]
                    if kind == "l":
                        if bi > 0:
                            keys.append(("l", bi - 1, "prev"))
                        keys.append(("l", bi, None))
                        if bi < 15:
                            keys.append(("l", bi + 1, "next"))
                    return keys

                def atA(item):
                    n, j, kind, bi, first, last = item
                    js = j % J2
                    qr_t, qc_t, kr_t, kc_t, v_t, v2, sg_t = qr_ts[js], qc_ts[js], kr_ts[js], kc_ts[js], v_ts[js], v2s[js], sg_ts[js]
                    if first:
                        s.dma(qr_t.full(), QR.view(2 * j * 128 * L, [[L, 128], [128 * L, 2], [1, L]]))
                        s.dma(qc_t.full(), QC.view(2 * j * 128 * CTX, [[CTX, 128], [128 * CTX, 2], [1, CTX]]))
                        s.dma(kr_t.full(), KR[j])
                        s.dma(kc_t.full(), KC[j])
                        s.dma(v_t.full(), VT.view(j * 64, [[256, 128], [128 * 256, NT], [1, 64]]))
                        s.dma(sg_t.full(), SG.view(2 * j * 128 * T, [[T, 128], [128 * T, 2], [1, T]]))
                        s.copy(v2[:, :, 0:64], v_t.full(), eng="pool")
                        s.copy(v2[:, :, 64:128], v_t.full(), eng="pool")
                    qsrc, q0 = (qc_t, bi * 128) if kind == "c" else (qr_t, bi * 128)
                    pts = pt[n % NP]
                    qw = qsrc.h.shape[2]
                    for ki, (kk, kb, msk) in enumerate(keys_of(kind, bi)):
                        ksrc = kc_t if kk == "c" else kr_t
                        for par in range(2):
                            p0 = par * 64
                            bs = nbank()
                            s.mm(bs[:, 0:256],
                                 [(ksrc[p0:p0 + 64, kb * 128:(kb + 1) * 128],
                                   qsrc.view(p0 * 2 * qw + q0, [[2 * qw, 64], [qw, 2], [1, 128]]))])
                            s.act(pts[ki][:, par * 256:(par + 1) * 256], bs[:, 0:256], AF.Exp, scale=c8[:, 0:1])
                        if msk is not None:
                            moff = 1024 if msk == "prev" else 512
                            s.tt(pts[ki].view(0, [[512, 128], [128, 4], [1, 128]]),
                                 pts[ki].view(0, [[512, 128], [128, 4], [1, 128]]),
                                 cst.view(moff, [[3072, 128], [0, 4], [1, 128]]), ALU.mult, eng="pool")

                def atB(item):
                    n, j, kind, bi, first, last = item
                    js = j % J2
                    v2, sg_t, ast = v2s[js], sg_ts[js], asts[js]
                    tok0 = bi * 128 if kind == "c" else CTX + bi * 128
                    keys = keys_of(kind, bi)
                    pts = pt[n % NP]
                    rd, ao = rds[n % 2], aos[n % 2]
                    vt_idx = [(kb if kk == "c" else 2 + kb) for (kk, kb, _) in keys]
                    bn = nbank()
                    s.mm(bn.full(), [(v2[:, vt_idx[ki], :], pts[ki].full()) for ki in range(len(keys))])
                    bd = nbank()
                    s.mm(bd.full(), [(onesb, pts[ki].full()) for ki in range(len(keys))])
                    for par in range(2):
                        p0 = par * 64
                        for c in range(2):
                            s.ts(rd[p0:p0 + 64, c * 128:(c + 1) * 128],
                                 bd[p0:p0 + 64, par * 256 + c * 128:par * 256 + (c + 1) * 128],
                                 es_pp[p0:p0 + 64, 2 * j + c:2 * j + c + 1], None, ALU.add)
                    s.recip(rd.full(), rd.full())
                    for par in range(2):
                        p0 = par * 64
                        s.tt(ao[p0:p0 + 64, :], bn[p0:p0 + 64, par * 256:(par + 1) * 256], rd[p0:p0 + 64, :], ALU.mult)
                    s.tt(ast.view(tok0, [[2 * T, 128], [T, 2], [1, 128]]),
                         ao.view(0, [[256, 128], [128, 2], [1, 128]]),
                         sg_t.view(tok0, [[2 * T, 128], [T, 2], [1, 128]]), ALU.mult)
                    if last:
                        s.dma(YT.view((8 + 2 * j) * 128 * T, [[T, 128], [128 * T, 2], [1, T]]), ast.full())

                pipeline(items, [atA, atB])
                s.flush()

            with ExitStack() as es:
                nb_ = {"i": 0}

                def nbank():
                    bk = banks[nb_["i"] % 8]
                    nb_["i"] += 1
                    return bk

                yt = [cx.sb(es, "yt%d" % i, [128, 16, 128], BF16) for i in range(2)]
                xt = [cx.sb(es, "xt5_%d" % i, [128, D]) for i in range(2)]
                x1t = [cx.sb(es, "x1t%d" % i, [128, D]) for i in range(2)]
                tmp5s = [cx.sb(es, "tmp5_%d" % i, [128, 512]) for i in range(2)]
                for i in range(NT if go("p5") else 0):
                    w = 1 if i < 2 else 0
                    y_, x_, o_ = yt[i % 2], xt[i % 2], x1t[i % 2]
                    s.dma(y_.full(), YT.view(i * 128, [[T, 128], [128 * T, 16], [1, 128]]))
                    s.dma(x_.full(), xin[i * 128:(i + 1) * 128, :])
                    for half in range(2):
                        tmp5 = tmp5s[half]
                        bk = nbank()
                        s.mm(bk.full(), [(y_[:, fc, :], wo[fc][:, half * 512:(half + 1) * 512]) for fc in range(16)])
                        s.tt(tmp5.full(), bk.full(), gate_bc[0][w][:, half * 512:(half + 1) * 512], ALU.mult)
                        s.tt(o_[:, half * 512:(half + 1) * 512], tmp5.full(), x_[:, half * 512:(half + 1) * 512], ALU.add)
                    s.dma(X1[i * 128:(i + 1) * 128, :], o_.full())
                s.flush()

        if go("all"):
            adaln_phase(1, o_ada_w, o_ada_b)
        with ExitStack() as l1:
            if not go("all"):
                return nc
            nb_ = {"i": 0}

            def nbank():
                bk = banks[nb_["i"] % 8]
                nb_["i"] += 1
                return bk

            with ExitStack() as es:
                nw = cx.sb(es, "nw1", [128, 8])
                sc1 = [cx.sb(es, "sc1b_%d" % w, [128, 8]) for w in range(2)]
                s.dma(nw.full(), o_norm_wT.full())
                for w in range(2):
                    s.stt(sc1[w].full(), modT[1][w][:, 8:16], 1.0, nw.full(), ALU.add, ALU.mult)
                hT = [cx.sb(es, "hTb%d" % k, [128, T], BF16) for k in range(8)]
                xt = [cx.sb(es, "xtb%d" % i, [128, D]) for i in range(2)]
                xn = [cx.sb(es, "xnb%d" % i, [128, D]) for i in range(2)]
                junk = cx.sb(es, "junkb", [128, D])
                st = [cx.sb(es, "stb%d" % i, [128, 4]) for i in range(2)]
                for i in range(NT):
                    w = 1 if i < 2 else 0
                    x_, n_, st_ = xt[i % 2], xn[i % 2], st[i % 2]
                    s.dma(x_.full(), X1[i * 128:(i + 1) * 128, :])
                    s.act(junk.full(), x_.full(), AF.Square, accum=st_[:, 0:1])
                    s.ts(st_[:, 1:2], st_[:, 0:1], 1.0 / D, EPS, ALU.mult, ALU.add)
                    s.act(st_[:, 2:3], st_[:, 1:2], AF.Sqrt)
                    s.recip(st_[:, 3:4], st_[:, 2:3])
                    s.ts(n_.full(), x_.full(), st_[:, 3:4], None, ALU.mult)
                    for half in range(2):
                        bk = nbank()
                        for kk in range(4):
                            k = half * 4 + kk
                            s.transpose(bk[:, kk * 128:(kk + 1) * 128], n_[:, k * 128:(k + 1) * 128], ident)
                        for kk in range(4):
                            k = half * 4 + kk
                            s.act(hT[k][:, i * 128:(i + 1) * 128], bk[:, kk * 128:(kk + 1) * 128], AF.Identity,
                                  bias=modT[1][w][:, k:k + 1], scale=sc1[w][:, k:k + 1])
                wq = [cx.sb(es, "wq%d" % i, [128, 8, 256], BF16) for i in range(8)]
                for q8 in range(8):
                    s.dma(wq[q8].full(), o_w_in.view(q8 * 256, [[2 * D, 128], [128 * 2 * D, 8], [1, 256]]), q="pool")
                ot = [cx.sb(es, "ot%d" % i, [128, D]) for i in range(2)]
                oi = 0
                for which in range(2):
                    for i in range(NT):
                        if which == 1 and i < 2:
                            continue
                        o_ = ot[oi % 2]
                        oi += 1
                        for half in range(2):
                            bk = nbank()
                            for q4 in range(2):
                                s.mm(bk[:, q4 * 256:(q4 + 1) * 256],
                                     [(hT[k][:, i * 128:(i + 1) * 128], wq[which * 4 + half * 2 + q4][:, k, :]) for k in range(8)])
                            if which == 0:
                                s.copy(o_[:, half * 512:(half + 1) * 512], bk.full(), eng="act")
                            else:
                                s.act(o_[:, half * 512:(half + 1) * 512], bk.full(), AF.Silu)
                        s.dma((U if which == 0 else SG1)[i * 128:(i + 1) * 128, :], o_.full())
                s.flush()

            L1S = os.environ.get('L1S', 'z')
            if L1S == 'a':
                return nc
            with ExitStack() as es:
                lam = cx.sb(es, "lam", [128, 2, 3, 32])
                bprm = cx.sb(es, "bprm", [128, 2, 32, 16])
                cprm = cx.sb(es, "cprm", [128, 2, 32, 16])
                s.dma(lam.full(), s5_lam.full())
                s.dma(bprm.full(), s5_b.full())
                s.dma(cprm.full(), s5_c.full())
                kc = cx.sb(es, "kconst", [128, 4])
                s.memset(kc[:, 0:1], 1.0 / 16)
                s.memset(kc[:, 1:2], math.pi / 2)
                s.memset(kc[:, 2:3], 0.0)
                s.memset(kc[:, 3:4], 1.0)
                W64 = [128, 2, 32]

                def t64(name):
                    return cx.sb(es, name, W64)

                def lv(i):
                    return lam.view(i * 32, [[192, 128], [96, 2], [1, 32]])

                dt_ = t64("dt_"); mag = t64("mag"); th = t64("th"); cs = t64("cs"); sn = t64("sn")
                t_a = t64("t_a"); t_b = t64("t_b"); t_c = t64("t_c")
                abre = t64("abre"); abim = t64("abim"); cre = t64("cre"); cim = t64("cim")
                s.act(dt_.full(), lv(2), AF.Exp)
                s.tt(t_a.full(), lv(0), dt_.full(), ALU.mult)
                s.act(mag.full(), t_a.full(), AF.Exp)
                s.tt(th.full(), lv(1), dt_.full(), ALU.mult)
                s.act(sn.full(), th.full(), AF.Sin, scale=kc[:, 0:1])
                s.act(cs.full(), th.full(), AF.Sin, scale=kc[:, 0:1], bias=kc[:, 1:2])
                for _ in range(4):
                    s.tt(t_a.full(), cs.full(), cs.full(), ALU.mult)
                    s.tt(t_b.full(), sn.full(), sn.full(), ALU.mult)
                    s.tt(t_c.full(), sn.full(), cs.full(), ALU.mult)
                    s.tt(cs.full(), t_a.full(), t_b.full(), ALU.subtract)
                    s.ts(sn.full(), t_c.full(), 2.0, None, ALU.mult)
                s.tt(abre.full(), mag.full(), cs.full(), ALU.mult)
                s.tt(abim.full(), mag.full(), sn.full(), ALU.mult)
                PW = cx.sb(es, "PW", [128, 2, 9, 64])

                def pw(ri, k):
                    return PW.view((ri * 9 + k) * 64, [[2 * 9 * 64, 128], [32, 2], [1, 32]])

                s.memset(PW[:, 0, 0, :], 1.0)
                s.memset(PW[:, 1, 0, :], 0.0)
                for k in range(8):
                    s.tt(t_a.full(), pw(0, k), abre.full(), ALU.mult)
                    s.tt(t_b.full(), pw(1, k), abim.full(), ALU.mult)
                    s.tt(pw(0, k + 1), t_a.full(), t_b.full(), ALU.subtract)
                    s.tt(t_a.full(), pw(0, k), abim.full(), ALU.mult)
                    s.tt(t_b.full(), pw(1, k), abre.full(), ALU.mult)
                    s.tt(pw(1, k + 1), t_a.full(), t_b.full(), ALU.add)
                s.ts(t_c.full(), abre.full(), -1.0, None, ALU.add)
                s.tt(t_a.full(), lv(0), lv(0), ALU.mult)
                s.tt(t_b.full(), lv(1), lv(1), ALU.mult)
                s.tt(t_a.full(), t_a.full(), t_b.full(), ALU.add)
                s.recip(dt_.full(), t_a.full())
                s.tt(t_a.full(), t_c.full(), lv(0), ALU.mult)
                s.tt(t_b.full(), abim.full(), lv(1), ALU.mult)
                s.tt(t_a.full(), t_a.full(), t_b.full(), ALU.add)
                s.tt(cre.full(), t_a.full(), dt_.full(), ALU.mult)
                s.tt(t_a.full(), abim.full(), lv(0), ALU.mult)
                s.tt(t_b.full(), t_c.full(), lv(1), ALU.mult)
                s.tt(t_a.full(), t_a.full(), t_b.full(), ALU.subtract)
                s.tt(cim.full(), t_a.full(), dt_.full(), ALU.mult)
                BB = cx.sb(es, "BB", [128, 2, 2, 512])
                tb1 = cx.sb(es, "tb1", [128, 512])
                tb2 = cx.sb(es, "tb2", [128, 512])

                def bb(ri, d_, g0=0, ng=32):
                    return BB.view((ri * 2 + d_) * 512 + g0 * 16, [[2048, 128], [16, ng], [1, 16]])

                def v3(buf, off, pstep, n1, s1, n2, s2):
                    return buf.view(off, [[pstep, 128], [s1, n1], [s2, n2]])

                def prm(buf, ri, g0=0, ng=32):
                    return buf.view(ri * 512 + g0 * 16, [[1024, 128], [16, ng], [1, 16]])

                def cf(buf, d_, g0=0, ng=32, n2=16):
                    return buf.view(d_ * 32 + g0, [[64, 128], [1, ng], [0, n2]])

                t1v = v3(tb1, 0, 512, 32, 16, 16, 1)
                t2v = v3(tb2, 0, 512, 32, 16, 16, 1)
                for d_ in range(2):
                    s.tt(t1v, prm(bprm, 0), cf(cre, d_), ALU.mult)
                    s.tt(t2v, prm(bprm, 1), cf(cim, d_), ALU.mult)
                    s.tt(bb(0, d_), t1v, t2v, ALU.subtract)
                    s.tt(t1v, prm(bprm, 1), cf(cre, d_), ALU.mult)
                    s.tt(t2v, prm(bprm, 0), cf(cim, d_), ALU.mult)
                    s.tt(bb(1, d_), t1v, t2v, ALU.add)
                LA = cx.sb(es, "LA", [128, 2, 32, 2])
                LB = cx.sb(es, "LB", [128, 2, 32, 2])
                for ri in range(2):
                    s.copy(LA.view(ri, [[128, 128], [64, 2], [2, 32]]), pw(0, 8))
                s.ts(LB.view(0, [[128, 128], [64, 2], [2, 32]]), pw(1, 8), -1.0, None, ALU.mult)
                s.copy(LB.view(1, [[128, 128], [64, 2], [2, 32]]), pw(1, 8))
                zt_ = cx.sb(es, "zt_", [16, 16, 112])
                s.memset(zt_.full(), 0.0)
                s.flush()

                if L1S == 'b':
                    return nc
                for b in range(4 if L1S not in ('c1', 'd1', 'e1', 'f1', 'g1') else 1):
                    g0 = 8 * b
                    with ExitStack() as bs_:
                        CAB = cx.sb(bs_, "CAB", [128, 2, 2, 8 * 144])
                        WST = cx.sb(bs_, "WST", [128, 8, 2, 2, 2, 64])
                        TF = cx.sb(bs_, "TF", [128, 16, 128])
                        TB = cx.sb(bs_, "TB", [128, 16, 128])

                        with ExitStack() as tmp:
                            WT = cx.sb(tmp, "WT", [128, 2, 2, 8 * 128])
                            KSB = cx.sb(tmp, "KSB", [16, 2, 16, 128])
                            c1 = cx.sb(tmp, "c1", [128, 128])
                            c2 = cx.sb(tmp, "c2", [128, 128])
                            c1v = v3(c1, 0, 128, 8, 16, 16, 1)
                            c2v = v3(c2, 0, 128, 8, 16, 16, 1)
                            for d_ in range(2):
                                for idx in range(9):
                                    p_ = idx if d_ == 0 else 8 - idx
                                    pr = PW.view((0 * 9 + p_) * 64 + d_ * 32 + g0, [[1152, 128], [1, 8], [0, 16]])
                                    pi_ = PW.view((1 * 9 + p_) * 64 + d_ * 32 + g0, [[1152, 128], [1, 8], [0, 16]])
                                    o_re = CAB.view((0 * 2 + d_) * 1152 + idx * 16, [[4608, 128], [144, 8], [1, 16]])
                                    o_im = CAB.view((1 * 2 + d_) * 1152 + idx * 16, [[4608, 128], [144, 8], [1, 16]])
                                    s.tt(c1v, prm(cprm, 0, g0, 8), pr, ALU.mult)
                                    s.tt(c2v, prm(cprm, 1, g0, 8), pi_, ALU.mult)
                                    s.tt(o_re, c1v, c2v, ALU.subtract)
                                    s.tt(c1v, prm(cprm, 0, g0, 8), pi_, ALU.mult)
                                    s.tt(c2v, prm(cprm, 1, g0, 8), pr, ALU.mult)
                                    s.stt(o_im, c1v, -1.0, c2v, ALU.mult, ALU.subtract)
                                for ss in range(8):
                                    p_ = 7 - ss if d_ == 0 else ss
                                    pr = PW.view((0 * 9 + p_) * 64 + d_ * 32 + g0, [[1152, 128], [1, 8], [0, 16]])
                                    pi_ = PW.view((1 * 9 + p_) * 64 + d_ * 32 + g0, [[1152, 128], [1, 8], [0, 16]])
                                    o_re = WT.view((d_ * 2 + 0) * 1024 + ss * 16, [[4096, 128], [128, 8], [1, 16]])
                                    o_im = WT.view((d_ * 2 + 1) * 1024 + ss * 16, [[4096, 128], [128, 8], [1, 16]])
                                    s.tt(c1v, bb(0, d_, g0, 8), pr, ALU.mult)
                                    s.tt(c2v, bb(1, d_, g0, 8), pi_, ALU.mult)
                                    s.tt(o_re, c1v, c2v, ALU.subtract)
                                    s.tt(c1v, bb(1, d_, g0, 8), pr, ALU.mult)
                                    s.tt(c2v, bb(0, d_, g0, 8), pi_, ALU.mult)
                                    s.tt(o_im, c1v, c2v, ALU.add)
                            for gh in range(2):
                                p0 = gh * 64
                                for gq in range(8):
                                    bk = nbank()
                                    for d_ in range(2):
                                        for ri in range(2):
                                            sl = d_ * 2 + ri
                                            s.transpose(bk[:, sl * 64:(sl + 1) * 64],
                                                        WT.view(p0 * 4096 + (d_ * 2 + ri) * 1024 + gq * 128, [[4096, 64], [1, 128]]),
                                                        cst[p0:p0 + 64, 0, p0:p0 + 64])
                                    s.copy(WST.view(((gq * 2 + gh) * 4) * 64, [[4096, 128], [1, 256]]), bk[:, 0:256], eng="act")
                                for d_ in range(2):
                                    for gqq in range(2):
                                        bk = nbank()
                                        for q4 in range(4):
                                            gq = gqq * 4 + q4
                                            i0 = 0 if d_ == 0 else 1
                                            s.mm(bk[0:16, q4 * 128:(q4 + 1) * 128],
                                                 [(BB.view(p0 * 2048 + (0 * 2 + d_) * 512 + (g0 + gq) * 16, [[2048, 64], [1, 16]]),
                                                   CAB.view(p0 * 4608 + (0 * 2 + d_) * 1152 + gq * 144 + i0 * 16, [[4608, 64], [1, 128]])),
                                                  (BB.view(p0 * 2048 + (1 * 2 + d_) * 512 + (g0 + gq) * 16, [[2048, 64], [1, 16]]),
                                                   CAB.view(p0 * 4608 + (1 * 2 + d_) * 1152 + gq * 144 + i0 * 16, [[4608, 64], [1, 128]]))])
                                        s.copy(KSB.view(d_ * 2048 + (2 * gqq * 4 + gh) * 128, [[4096, 16], [256, 4], [1, 128]]),
                                               bk.view(0, [[512, 16], [128, 4], [1, 128]]), eng="act")
                            gbase = 16 * b
                            s.dma(KFP.view(gbase * 3840 + 7 * 16, [[240, 16], [3840, 16], [1, 128]]), KSB[:, 0, :, :])
                            s.dma(KBR.view(gbase * 3840, [[240, 16], [3840, 16], [1, 128]]), KSB[:, 1, :, :])
                            s.dma(KFP.view(gbase * 3840, [[240, 16], [3840, 16], [1, 112]]), zt_.full())
                            s.dma(KBR.view(gbase * 3840 + 128, [[240, 16], [3840, 16], [1, 112]]), zt_.full())
                            for ss in range(8):
                                s.dma(TF[ss * 16:(ss + 1) * 16, :, :], KFP.view(gbase * 3840 + (7 - ss) * 16, [[240, 16], [3840, 16], [1, 128]]))
                                s.dma(TB[ss * 16:(ss + 1) * 16, :, :], KBR.view(gbase * 3840 + (7 - ss) * 16, [[240, 16], [3840, 16], [1, 128]]))
                            s.flush()

                        if L1S in ('c', 'c1'):
                            continue
                        u8b = cx.sb(bs_, "u8b", [128, 8, 256])
                        u8g = cx.sb(bs_, "u8g", [128, 16, 128])
                        U8T = cx.sb(bs_, "U8T", [128, 16, 288])
                        NCOL = 326
                        PS = 16 * NCOL
                        SSD = [cx.sb(bs_, "SS%d" % i, [128, 8, 2, NCOL]) for i in range(2)]
                        CAR = [cx.sb(bs_, "CAR%d" % i, [128, 7, 8, 2]) for i in range(2)]
                        A36 = [cx.sb(bs_, "A36_%d" % i, [128, 8, 2]) for i in range(2)]
                        B36 = [cx.sb(bs_, "B36_%d" % i, [128, 8, 2]) for i in range(2)]
                        y8b = cx.sb(bs_, "y8b", [128, 8, 256])
                        ysb = cx.sb(bs_, "ysb", [128, 512])
                        TT1 = [cx.sb(bs_, "TT1_%d" % i, [128, 9, 8, 2]) for i in range(2)]
                        TT2 = [cx.sb(bs_, "TT2_%d" % i, [128, 9, 8, 2]) for i in range(2)]
                        for (j0, nj) in ((0, 32), (32, 128), (160, 128)):
                            s.dma(u8b[0:nj, :, :], U.view(8 * j0 * 1024 + 256 * b, [[8192, nj], [1024, 8], [1, 256]]))
                            s.copy(u8g.view(0, [[2048, nj], [128, 16], [16, 8], [1, 16]]),
                                   u8b.view(0, [[2048, nj], [16, 16], [256, 8], [1, 16]]), eng="act")
                            for gq4 in range(4):
                                bk = nbank()
                                for q4 in range(4):
                                    gi = gq4 * 4 + q4
                                    s.transpose(bk[:, q4 * 128:q4 * 128 + nj],
                                                u8g.view(128 * gi, [[2048, nj], [1, 128]]), cst[0:nj, 0, 0:nj])
                                s.copy(U8T.view(gq4 * 4 * 288 + j0, [[16 * 288, 128], [288, 4], [1, nj]]),
                                       bk.view(0, [[512, 128], [128, 4], [1, nj]]), eng="act")
                        if L1S in ('d', 'd1'):
                            s.flush()
                            continue
                        s.memset(SSD[0].view(0, [[PS, 128], [NCOL, 16], [1, 1]]), 0.0)
                        s.memset(SSD[0].view(289, [[PS, 128], [NCOL, 16], [1, 37]]), 0.0)
                        s.memset(SSD[1].view(288, [[PS, 128], [NCOL, 16], [1, 38]]), 0.0)
                        s.memset(SSD[0].view(289, [[PS, 128], [2 * NCOL, 8], [1, 1]]), 1.0)
                        s.memset(SSD[1].view(323, [[PS, 128], [2 * NCOL, 8], [1, 1]]), 1.0)
                        for gq in range(8):
                            for gh in range(2):
                                gi = 2 * gq + gh
                                p0 = gh * 64
                                for d_ in range(2):
                                    for ri in range(2):
                                        bk = nbank()
                                        s.mm(bk[p0:p0 + 64, 0:288],
                                             [(WST.view((((gq * 2 + gh) * 2 + d_) * 2 + ri) * 64, [[4096, 128], [1, 64]]),
                                               U8T[:, gi, :])])
                                        so = p0 * PS + (gq * 2 + ri) * NCOL
                                        if d_ == 0:
                                            s.copy(SSD[0].view(so + 1, [[PS, 64], [1, 288]]), bk[p0:p0 + 64, 0:288], eng="act")
                                        else:
                                            s.copy(SSD[1].view(so + 256, [[PS, 64], [1, 32]]), bk[p0:p0 + 64, 0:32], eng="act")
                                            s.copy(SSD[1].view(so, [[PS, 64], [1, 256]]), bk[p0:p0 + 64, 32:288], eng="act")
                        if L1S in ('e', 'e1'):
                            s.flush()
                            continue
                        DS = 8 * 2 * 289
                        RI, GQ = NCOL, 2 * NCOL
                        REC_ENG2 = os.environ.get('REC2', 'dve')

                        def cplx_step(items):
                            engs = ("dve", REC_ENG2)
                            for n_, (pv, psw, cv, ca, cb_, t1_, t2_) in enumerate(items):
                                s.tt(t1_, pv, ca, ALU.mult, eng=engs[n_ % 2])
                                s.tt(t2_, psw, cb_, ALU.mult, eng=engs[n_ % 2])
                            for n_, (pv, psw, cv, ca, cb_, t1_, t2_) in enumerate(items):
                                s.tt(t1_, t1_, t2_, ALU.add, eng=engs[n_ % 2])
                            for n_, (pv, psw, cv, ca, cb_, t1_, t2_) in enumerate(items):
                                if cv is not None:
                                    s.tt(cv, cv, t1_, ALU.add, eng=engs[n_ % 2])

                        def segv(SS, col, nseg):
                            return (SS.view(col, [[PS, 128], [36, nseg], [GQ, 8], [RI, 2]]),
                                    SS.view(col + RI, [[PS, 128], [36, nseg], [GQ, 8], [-RI, 2]]))

                        def coef(buf, d_, nseg):
                            return buf.view(d_ * 64 + g0 * 2, [[128, 128], [0, nseg], [2, 8], [1, 2]])

                        for k in range(1, 36):
                            items = []
                            for d_ in range(2):
                                pc = k if d_ == 0 else 36 - k
                                cc = k + 1 if d_ == 0 else 35 - k
                                pv, psw = segv(SSD[d_], pc, 9)
                                cv, _ = segv(SSD[d_], cc, 9)
                                items.append((pv, psw, cv, coef(LA, d_, 9), coef(LB, d_, 9), TT1[d_].full(), TT2[d_].full()))
                            cplx_step(items)
                        items = []
                        for d_ in range(2):
                            c35 = 324 if d_ == 0 else 288
                            pv, psw = segv(SSD[d_], c35, 1)
                            items.append((pv, psw, None, coef(LA, d_, 1), coef(LB, d_, 1),
                                          TT1[d_].view(0, [[144, 128], [16, 1], [2, 8], [1, 2]]),
                                          TT2[d_].view(0, [[144, 128], [16, 1], [2, 8], [1, 2]])))
                        cplx_step(items)
                        for d_ in range(2):
                            l36re = TT1[d_].view(0, [[144, 128], [2, 8], [0, 2]])
                            s.copy(A36[d_].full(), l36re)
                            s.ts(B36[d_][:, :, 0:1], TT1[d_].view(1, [[144, 128], [2, 8], [1, 1]]), -1.0, None, ALU.mult)
                            s.copy(B36[d_][:, :, 1:2], TT1[d_].view(1, [[144, 128], [2, 8], [1, 1]]))
                        for step in range(1, 8):
                            items = []
                            for d_ in range(2):
                                if d_ == 0:
                                    m = step
                                    cc, pc = 36 * m + 36, 36 * m
                                else:
                                    m = 7 - step
                                    cc, pc = 36 * m, 36 * m + 36
                                pv, psw = segv(SSD[d_], pc, 1)
                                cv, _ = segv(SSD[d_], cc, 1)
                                items.append((pv, psw, cv,
                                              A36[d_].view(0, [[16, 128], [0, 1], [2, 8], [1, 2]]),
                                              B36[d_].view(0, [[16, 128], [0, 1], [2, 8], [1, 2]]),
                                              TT1[d_].view(0, [[144, 128], [16, 1], [2, 8], [1, 2]]),
                                              TT2[d_].view(0, [[144, 128], [16, 1], [2, 8], [1, 2]])))
                            cplx_step(items)
                        items = []
                        for d_ in range(2):
                            pv, psw = segv(SSD[d_], 36, 7)
                            items.append((pv, psw, None, coef(LA, d_, 7), coef(LB, d_, 7),
                                          CAR[d_].full(), TT2[d_].view(0, [[144, 128], [16, 7], [2, 8], [1, 2]])))
                        cplx_step(items)
                        for d_ in range(2):
                            SS = SSD[d_]
                            sb0 = 37 if d_ == 0 else 1

                            def sview(ri):
                                return SS.view(sb0 + ri * RI, [[PS, 128], [36, 7], [GQ, 8], [1, 35]])

                            def tview(ri):
                                return SS.view(289 + ri * RI, [[PS, 128], [0, 7], [GQ, 8], [1, 35]])

                            def cview(ri):
                                return CAR[d_].view(ri, [[112, 128], [16, 7], [2, 8], [0, 35]])

                            w1 = (u8g if d_ == 0 else u8b).view(0, [[2048, 128], [280, 7], [35, 8], [1, 35]])
                            w2 = y8b.view(0, [[2048, 128], [280, 7], [35, 8], [1, 35]])
                            s.tt(w1, tview(0), cview(0), ALU.mult)
                            s.tt(w2, tview(1), cview(1), ALU.mult)
                            s.tt(w1, w1, w2, ALU.subtract)
                            s.tt(sview(0), sview(0), w1, ALU.add)
                            s.tt(w1, tview(0), cview(1), ALU.mult)
                            s.tt(w2, tview(1), cview(0), ALU.mult)
                            s.tt(w1, w1, w2, ALU.add)
                            s.tt(sview(1), sview(1), w1, ALU.add)
                        if L1S in ('f', 'f1'):
                            s.flush()
                            continue
                        for tt_ in range(2):
                            j0 = 32 + 128 * tt_
                            m0 = 128 * tt_
                            for gh in range(2):
                                p0 = gh * 64
                                for gqq in range(2):
                                    bx = nbank()
                                    by = nbank()
                                    for q4 in range(4):
                                        gq = gqq * 4 + q4
                                        gi = 2 * gq + gh
                                        s.mm(bx[:, q4 * 128:(q4 + 1) * 128],
                                             [(U8T[:, gi, j0:j0 + 128], TF[:, gi, :]), (U8T[:, gi, j0:j0 + 128], TB[:, gi, :])])
                                        pairs = []
                                        for d_ in range(2):
                                            c0 = j0 if d_ == 0 else m0 + 1
                                            i0 = 1 if d_ == 0 else 0
                                            for ri in range(2):
                                                so = p0 * PS + (gq * 2 + ri) * NCOL + c0
                                                pairs.append((SSD[d_].view(so, [[PS, 64], [1, 128]]),
                                                              CAB.view(p0 * 4608 + (ri * 2 + d_) * 1152 + gq * 144 + i0 * 16, [[4608, 64], [1, 128]])))
                                        s.mm(by[:, q4 * 128:(q4 + 1) * 128], pairs)
                                    s.copy(ysb.full(), by.full(), eng="act")
                                    s.tt(y8b.view(32 * gqq * 4 + 16 * gh, [[2048, 128], [32, 4], [256, 8], [1, 16]]),
                                         bx.view(0, [[512, 128], [128, 4], [16, 8], [1, 16]]),
                                         ysb.view(0, [[512, 128], [128, 4], [16, 8], [1, 16]]), ALU.add)
                            s.dma(YTOK.view((CTX + 8 * m0) * 1024 + 256 * b, [[8192, 128], [1024, 8], [1, 256]]), y8b.full())
                        s.flush()

            if L1S in ('g', 'g1'):
                return nc
            with ExitStack() as es:
                gw = [cx.sb(es, "gw%d" % k, [128, D], BF16) for k in range(8)]
                ow = [cx.sb(es, "ow%d" % k, [128, D], BF16) for k in range(8)]
                dskb = cx.sb(es, "dskb", [128, D])
                glbb = cx.sb(es, "glbb", [128, D])
                fnwb = cx.sb(es, "fnwb", [128, D])
                kg = cx.sb(es, "kg", [128, 1])
                s.memset(kg.full(), 2.0 * math.sqrt(2.0 / math.pi))
                for k in range(8):
                    s.dma(gw[k].full(), o_glu_w[k * 128:(k + 1) * 128, :], q="pool")
                    s.dma(ow[k].full(), o_w_out[k * 128:(k + 1) * 128, :], q="pool")
                s.dma(dskb.full(), o_d_skip.view(0, [[0, 128], [1, D]]))
                s.dma(glbb.full(), o_glu_b.view(0, [[0, 128], [1, D]]))
                s.dma(fnwb.full(), final_norm_w.view(0, [[0, 128], [1, D]]))
                NB3 = 3
                ya = [cx.sb(es, "ya%d" % i, [128, D]) for i in range(NB3)]
                ua = [cx.sb(es, "ua%d" % i, [128, D]) for i in range(NB3)]
                sga = [cx.sb(es, "sga%d" % i, [128, D]) for i in range(NB3)]
                xa = [cx.sb(es, "xa%d" % i, [128, D]) for i in range(NB3)]
                w1s = [cx.sb(es, "w1_%d" % i, [128, D]) for i in range(NB3)]
                w2s = [cx.sb(es, "w2_%d" % i, [128, D]) for i in range(NB3)]
                w3s = [cx.sb(es, "w3_%d" % i, [128, D]) for i in range(NB3)]
                tTs = [cx.sb(es, "tT_%d" % i, [128, 8, 128], BF16) for i in range(2 * NB3)]
                sts = [cx.sb(es, "st10_%d" % i, [128, 4]) for i in range(NB3)]

                def transp8(src, tT):
                    for half in range(2):
                        bk = nbank()
                        for kk in range(4):
                            k = half * 4 + kk
                            s.transpose(bk[:, kk * 128:(kk + 1) * 128], src[:, k * 128:(k + 1) * 128], ident)
                        s.copy(tT[:, half * 4:(half + 1) * 4, :], bk.view(0, [[512, 128], [128, 4], [1, 128]]), eng="act")

                TAILN = int(os.environ.get('TAILN', NT))

                def bufs(i):
                    b_ = i % NB3
                    return ya[b_], ua[b_], sga[b_], xa[b_], w1s[b_], w2s[b_], w3s[b_], tTs[2 * b_], tTs[2 * b_ + 1], sts[b_]

                def stage0(i):
                    y_, u_, g_, x_, w1, w2, w3, tTa, tTb, st = bufs(i)
                    s.dma(y_.full(), YTOK[i * 128:(i + 1) * 128, :])
                    s.dma(u_.full(), U[i * 128:(i + 1) * 128, :])
                    s.dma(g_.full(), SG1[i * 128:(i + 1) * 128, :])
                    s.dma(x_.full(), X1[i * 128:(i + 1) * 128, :])
                    s.tt(w1.full(), u_.full(), dskb.full(), ALU.mult)
                    s.tt(y_.full(), y_.full(), w1.full(), ALU.add)
                    s.tt(w1.full(), y_.full(), y_.full(), ALU.mult)
                    s.ts(w1.full(), w1.full(), 0.044715, 1.0, ALU.mult, ALU.add)
                    s.tt(w1.full(), w1.full(), y_.full(), ALU.mult)
                    s.act(w1.full(), w1.full(), AF.Sigmoid, scale=kg[:, 0:1])
                    s.tt(w2.full(), y_.full(), w1.full(), ALU.mult)
                    transp8(w2, tTa)

                def stage1(i):
                    y_, u_, g_, x_, w1, w2, w3, tTa, tTb, st = bufs(i)
                    for half in range(2):
                        bk = nbank()
                        s.mm(bk.full(), [(tTa[:, k, :], gw[k][:, half * 512:(half + 1) * 512]) for k in range(8)])
                        s.tt(w1[:, half * 512:(half + 1) * 512], bk.full(), glbb[:, half * 512:(half + 1) * 512], ALU.add)
                    s.act(w1.full(), w1.full(), AF.Sigmoid)
                    s.tt(w2.full(), w2.full(), w1.full(), ALU.mult)
                    s.tt(w2.full(), w2.full(), g_.full(), ALU.mult)
                    transp8(w2, tTb)

                def stage2(i):
                    y_, u_, g_, x_, w1, w2, w3, tTa, tTb, st = bufs(i)
                    for half in range(2):
                        bk = nbank()
                        s.mm(bk.full(), [(tTb[:, k, :], ow[k][:, half * 512:(half + 1) * 512]) for k in range(8)])
                        s.tt(w1[:, half * 512:(half + 1) * 512], bk.full(), gate_bc[1][0][:, half * 512:(half + 1) * 512], ALU.mult)
                    s.tt(w3.full(), w1.full(), x_.full(), ALU.add)
                    s.act(w1.full(), w3.full(), AF.Square, accum=st[:, 0:1])
                    s.ts(st[:, 1:2], st[:, 0:1], 1.0 / D, EPS, ALU.mult, ALU.add)
                    s.act(st[:, 2:3], st[:, 1:2], AF.Sqrt)
                    s.recip(st[:, 3:4], st[:, 2:3])
                    s.act(w3.full(), w3.full(), AF.Copy, scale=st[:, 3:4])
                    s.tt(w2.full(), w3.full(), fnwb.full(), ALU.mult)
                    s.dma(out_t[(i - 2) * 128:(i - 1) * 128, :], w2.full())

                pipeline(list(range(2, TAILN)), [stage0, stage1, stage2])
                s.flush()

    return nc


def _consts():
    c = np.zeros((128, 6, 512), np.float32)
    j = np.arange(128)[:, None]
    l = np.arange(128)[None, :]
    c[:, 0, :128] = np.eye(128, dtype=np.float32)
    c[:, 1, :128] = (j <= l)
    c[:, 2, :128] = (j >= l)
    c[:, 3, :] = 1.0
    nf = np.where(l < j, -30000.0, 0.0).astype(np.float32)
    nb = np.where(l > j, -30000.0, 0.0).astype(np.float32)
    c[:, 4, :] = np.tile(nf, (1, 4))
    c[:, 5, :] = np.tile(nb, (1, 4))
    return c


def _rope_tables():
    rows = L // 64
    row = np.repeat(np.arange(rows, dtype=np.float32), 64)
    col = np.tile(np.arange(64, dtype=np.float32), rows)
    n_freq = 16
    inv = (np.float32(10000.0) ** (-np.arange(n_freq, dtype=np.float32) / n_freq)).astype(np.float32)
    ang = np.concatenate([row[:, None] * inv, col[:, None] * inv], axis=-1).astype(np.float32)
    cos = np.cos(ang).astype(np.float32)
    sin = np.sin(ang).astype(np.float32)
    cosT = np.zeros((128, L), np.float32)
    sinT = np.zeros((128, L), np.float32)
    for h2 in range(2):
        for half in range(2):
            p0 = h2 * 64 + half * 32
            cosT[p0:p0 + 32] = cos.T
            sinT[p0:p0 + 32] = (-sin.T if half == 0 else sin.T)
    return np.stack([cosT, sinT], axis=1)


def _vecT(v, nchunk):
    return np.ascontiguousarray(np.asarray(v, np.float32).reshape(nchunk, 128).T)


def prep_inputs(b, inp):
    f = lambda a: np.ascontiguousarray(np.asarray(a, np.float32))
    m = {}
    m["xin"] = f(np.concatenate([inp["ctx"][b], inp["x"][b]], axis=0))
    cv = np.stack([inp["c"][b], inp["c_ctx"]], axis=0)
    m["cvecT"] = f(cv.reshape(2, 8, 128).transpose(2, 0, 1))
    m["consts"] = _consts()
    m["rope"] = _rope_tables()
    m["e_ada_w"] = f(inp["e_ada_w"][0])
    m["e_ada_b"] = f(inp["e_ada_b"][0]).reshape(1, -1)
    m["e_norm_wT"] = _vecT(inp["e_norm_w"][0], 8)
    w = f(inp["e_w_in"][0])
    q = w[:, OFF_Q:OFF_Q + 1024].reshape(D, 16, 2, 32)
    qs = q[:, :, ::-1, :].reshape(D, 1024)
    k = w[:, OFF_KV:OFF_KV + 256].reshape(D, 4, 64)
    kr = np.concatenate([k, k], axis=2).reshape(D, 512)
    ks = k.reshape(D, 4, 2, 32)[:, :, ::-1, :].reshape(D, 4, 64)
    ksr = np.concatenate([ks, ks], axis=2).reshape(D, 512)
    m["e_w_in"] = f(np.concatenate([w, qs, kr, ksr], axis=1))
    cw = f(inp["e_conv_w"][0])
    m["e_conv_wT"] = f(cw.reshape(5, 12, 128).transpose(2, 1, 0))
    m["e_conv_bT"] = _vecT(inp["e_conv_b"][0], 12)
    m["e_dt_bias"] = f(inp["e_dt_bias"][0]).reshape(1, 32)
    m["e_a_log"] = f(inp["e_a_log"][0]).reshape(1, 32)
    m["e_d_skip"] = f(inp["e_d_skip"][0]).reshape(1, 16)
    m["e_ssd_norm_wT"] = _vecT(inp["e_ssd_norm_w"][0], 8)
    sk = f(inp["e_sink"][0]).reshape(8, 2)
    m["e_sink"] = f(np.repeat(sk.T[:, None, :], 64, axis=1).reshape(128, 8))
    m["e_w_out"] = f(inp["e_w_out"][0])
    m["o_ada_w"] = f(inp["o_ada_w"][0])
    m["o_ada_b"] = f(inp["o_ada_b"][0]).reshape(1, -1)
    m["o_norm_wT"] = _vecT(inp["o_norm_w"][0], 8)
    m["o_w_in"] = f(inp["o_w_in"][0])

    def gl(a):
        a = np.asarray(a, np.float32)
        rest = a.shape[2:]
        a = a.reshape((32, 2, 64) + rest)
        a = np.moveaxis(a, 0, 2)
        return a.reshape((128, 32) + rest)

    lam = np.zeros((128, 2, 3, 32), np.float32)
    for d_ in range(2):
        lam[:, d_, 0] = gl(inp["o_lam_re"][0][d_])
        lam[:, d_, 1] = gl(inp["o_lam_im"][0][d_])
        lam[:, d_, 2] = gl(np.repeat(np.asarray(inp["o_log_step"][0][d_])[:, None], 64, axis=1))
    m["s5_lam"] = f(lam)
    m["s5_b"] = f(np.stack([gl(inp["o_b_re"][0]), gl(inp["o_b_im"][0])], axis=1))
    cr = np.asarray(inp["o_c_re"][0]).transpose(0, 2, 1)
    ci = np.asarray(inp["o_c_im"][0]).transpose(0, 2, 1)
    m["s5_c"] = f(np.stack([gl(cr), gl(ci)], axis=1))
    m["o_d_skip"] = f(inp["o_d_skip"][0]).reshape(1, -1)
    m["o_glu_w"] = f(inp["o_glu_w"][0])
    m["o_glu_b"] = f(inp["o_glu_b"][0]).reshape(1, -1)
    m["o_w_out"] = f(inp["o_w_out"][0])
    m["final_norm_w"] = f(inp["final_norm_w"]).reshape(1, -1)
    return m


def kernel(**inputs):
    nc = build_program()
    in_maps = [prep_inputs(b, inputs) for b in range(8)]
    res = run_bass_kernel_spmd(nc, in_maps, core_ids=list(range(8)))
    return np.stack([r["out"] for r in res.results], axis=0)
```

```python
import math
import os
from contextlib import ExitStack

import numpy as np
import concourse.bass as bass
import concourse.mybir as mybir
from concourse.bass_utils import run_bass_kernel_spmd

F32 = mybir.dt.float32
BF16 = mybir.dt.bfloat16
AF = mybir.ActivationFunctionType
ALU = mybir.AluOpType

D = 1024
T = 2304
NT = 18
CTX = 256
L = 2048
EPS = 1e-6
TG = [(0, 256), (256, 512), (768, 512), (1280, 512), (1792, 512)]

SES_ALL = os.environ.get('SES', '0') == '1'
SAME_ENGINE_SYNC = {'act': SES_ALL, 'dve': SES_ALL, 'pool': True, 'pe': False, 'sp': True}
SEM_EPOCH = 30000


class V:
    __slots__ = ("buf", "ap")

    def __init__(self, buf, ap):
        self.buf = buf
        self.ap = ap


class Buf:
    def __init__(self, name, h):
        self.name = name
        self.h = h
        self.last_w = None
        self.readers = []
        self.is_psum = False

    def __getitem__(self, idx):
        return V(self, self.h[idx])

    def full(self):
        return V(self, self.h.ap())

    def view(self, offset, pattern):
        return V(self, bass.AP(self.h, offset, [list(p) for p in pattern]))


class Sched:
    ENG = ("pe", "act", "dve", "pool", "sp")

    def __init__(self, nc):
        self.nc = nc
        self.prog = {e: [] for e in self.ENG}
        self.sem = {}
        self.cnt = {}
        self.semid = 0
        self.known = {e: {} for e in self.ENG}
        for e in ("pe", "act", "dve", "pool"):
            self._new_engine_sem(e)
        self.nds = 8
        self.dsem = {}
        self.duse = {}
        self.dcnt = {}
        for q in ("sp", "pool"):
            self.dsem[q] = []
            self.duse[q] = []
            for i in range(self.nds):
                key = "d_%s_%d" % (q, i)
                self.dsem[q].append((nc.alloc_semaphore(key), key))
                self.duse[q].append(0)
            self.dcnt[q] = 0
        self.n_ops = 0

    def _new_engine_sem(self, e):
        self.semid += 1
        key = "s_%s_%d" % (e, self.semid)
        self.sem[e] = (self.nc.alloc_semaphore(key), key)
        self.cnt[e] = 0

    def _deps(self, reads, writes):
        deps = {}

        def add(tok):
            if tok is None:
                return
            h, key, val = tok
            if key not in deps or deps[key][1] < val:
                deps[key] = (h, val)

        for r in reads:
            add(r.buf.last_w)
            if r.buf.is_psum:
                for t in r.buf.readers:
                    add(t)
        for w in writes:
            add(w.buf.last_w)
            for t in w.buf.readers:
                add(t)
        return deps

    def _emit_waits(self, eng, deps, own_key=None):
        kn = self.known[eng]
        for key, (h, val) in deps.items():
            if key == own_key and not SAME_ENGINE_SYNC[eng]:
                continue
            if kn.get(key, 0) >= val:
                continue
            kn[key] = val
            self.prog[eng].append(("wait", h, val))

    def _update(self, tok, reads, writes):
        for w in writes:
            w.buf.last_w = tok
            w.buf.readers = []
        for r in reads:
            if r.buf.last_w is not tok:
                r.buf.readers.append(tok)

    def op(self, eng, fn, reads=(), writes=()):
        reads = [r for r in reads if r is not None]
        writes = list(writes)
        if self.cnt[eng] >= SEM_EPOCH:
            self._new_engine_sem(eng)
        h, key = self.sem[eng]
        own = None if eng == "pe" else key
        deps = self._deps(reads, writes)
        if eng == "pe":
            deps.pop(key, None)
        self._emit_waits(eng, deps, own_key=own)
        self.cnt[eng] += 1
        self.prog[eng].append(("op", fn, h, 1))
        tok = (h, key, self.cnt[eng])
        self._update(tok, reads, writes)
        self.n_ops += 1
        return tok

    def dma(self, out, in_, q="sp", **kw):
        deps = self._deps([in_], [out])
        self._emit_waits(q, deps)
        k = self.dcnt[q] % self.nds
        self.dcnt[q] += 1
        h, key = self.dsem[q][k]
        prev = 16 * self.duse[q][k]
        if prev > 0 and self.known[q].get(key, 0) < prev:
            self.known[q][key] = prev
            self.prog[q].append(("wait", h, prev))
        self.duse[q][k] += 1
        val = 16 * self.duse[q][k]
        o_ap, i_ap = out.ap, in_.ap
        self.prog[q].append(("op", lambda e: e.dma_start(out=o_ap, in_=i_ap, **kw), h, 16))
        tok = (h, key, val)
        self._update(tok, [in_], [out])
        self.n_ops += 1
        return tok

    def finish_dmas(self):
        for q in ("sp", "pool"):
            for k in range(self.nds):
                h, key = self.dsem[q][k]
                val = 16 * self.duse[q][k]
                if val > 0 and self.known[q].get(key, 0) < val:
                    self.known[q][key] = val
                    self.prog[q].append(("wait", h, val))

    def flush(self, name=None):
        self.finish_dmas()
        nc = self.nc
        prog = self.prog
        self.prog = {e: [] for e in self.ENG}

        def run(items, e):
            for it in items:
                if it[0] == "wait":
                    e.wait_ge(it[1], it[2])
                else:
                    inst = it[1](e)
                    inst.then_inc(it[2], it[3])

        with nc.Block() as block:
            if prog["sp"]:
                @block.sync
                def _(e):
                    run(prog["sp"], e)
            if prog["act"]:
                @block.scalar
                def _(e):
                    run(prog["act"], e)
            if prog["dve"]:
                @block.vector
                def _(e):
                    run(prog["dve"], e)
            if prog["pool"]:
                @block.gpsimd
                def _(e):
                    run(prog["pool"], e)
            if prog["pe"]:
                @block.tensor
                def _(e):
                    run(prog["pe"], e)

    def mm(self, out, pairs):
        n = len(pairs)

        def fn(e):
            inst = None
            for i, (l, r) in enumerate(pairs):
                inst = e.matmul(out.ap, l.ap, r.ap, start=(i == 0), stop=(i == n - 1))
            return inst

        self.op("pe", fn, reads=[p[0] for p in pairs] + [p[1] for p in pairs], writes=[out])

    def transpose(self, out, in_, ident):
        self.op("pe", lambda e: e.transpose(out.ap, in_.ap, ident.ap), reads=[in_, ident], writes=[out])

    def act(self, out, in_, func, bias=None, scale=None, accum=None):
        kw = {}
        reads = [in_]
        writes = [out]
        if bias is not None:
            if isinstance(bias, V):
                kw["bias"] = bias.ap
                reads.append(bias)
            else:
                kw["bias"] = bias
        if scale is not None:
            if isinstance(scale, V):
                kw["scale"] = scale.ap
                reads.append(scale)
            else:
                kw["scale"] = scale
        if accum is not None:
            kw["accum_out"] = accum.ap
            writes.append(accum)
        self.op("act", lambda e: e.activation(out.ap, in_.ap, func, **kw), reads=reads, writes=writes)

    def ts(self, out, in0, s1, s2, op0, op1=None, eng="dve"):
        reads = [in0]
        a1 = s1
        a2 = s2
        if isinstance(s1, V):
            reads.append(s1)
            a1 = s1.ap
        if isinstance(s2, V):
            reads.append(s2)
            a2 = s2.ap
        if op1 is None:
            self.op(eng, lambda e: e.tensor_scalar(out.ap, in0.ap, a1, a2, op0), reads=reads, writes=[out])
        else:
            self.op(eng, lambda e: e.tensor_scalar(out.ap, in0.ap, a1, a2, op0, op1), reads=reads, writes=[out])

    def tt(self, out, in0, in1, op, eng="dve"):
        self.op(eng, lambda e: e.tensor_tensor(out.ap, in0.ap, in1.ap, op), reads=[in0, in1], writes=[out])

    def stt(self, out, in0, scalar, in1, op0, op1):
        reads = [in0, in1]
        sc = scalar
        if isinstance(scalar, V):
            reads.append(scalar)
            sc = scalar.ap
        self.op("dve", lambda e: e.scalar_tensor_tensor(out.ap, in0.ap, sc, in1.ap, op0, op1),
                reads=reads, writes=[out])

    def copy(self, out, in_, eng="dve"):
        if eng == "act":
            self.op("act", lambda e: e.copy(out.ap, in_.ap), reads=[in_], writes=[out])
        else:
            self.op(eng, lambda e: e.tensor_copy(out.ap, in_.ap), reads=[in_], writes=[out])

    def recip(self, out, in_):
        self.op("dve", lambda e: e.reciprocal(out.ap, in_.ap), reads=[in_], writes=[out])

    def memset(self, out, val, eng="dve"):
        self.op(eng, lambda e: e.memset(out.ap, val), reads=[], writes=[out])


class Ctx:
    def __init__(self, nc, sched):
        self.nc = nc
        self.s = sched
        self.uid = 0

    def sb(self, es, name, shape, dtype=F32):
        self.uid += 1
        h = es.enter_context(self.nc.sbuf_tensor("%s_%d" % (name, self.uid), list(shape), dtype))
        return Buf(name, h)

    def ps(self, es, name, shape=(128, 512), dtype=F32):
        self.uid += 1
        h = es.enter_context(self.nc.psum_tensor("%s_%d" % (name, self.uid), list(shape), dtype))
        b = Buf(name, h)
        b.is_psum = True
        return b

    def dram(self, name, shape, dtype=F32, kind="Internal"):
        h = self.nc.dram_tensor(name, list(shape), dtype, kind=kind)
        return Buf(name, h)


def pipeline(items, stages):
    n, k = len(items), len(stages)
    for t in range(n + k - 1):
        for j in range(k - 1, -1, -1):
            i = t - j
            if 0 <= i < n:
                stages[j](items[i])


def bc_mid(v_buf, base_off, pstep, nparts, n_outer, outer_step, n_inner):
    return v_buf.view(base_off, [[pstep, nparts], [outer_step, n_outer], [0, n_inner]])


E_NCOL = 5152
OFF_Z = 0
OFF_XBC = 1024
OFF_DT = 2560
OFF_Q = 2592
OFF_KV = 3616
OFF_G = 4128
OFF_QS = 5152
OFF_KR = 6176
OFF_KSR = 6688
E_NCOL_EXT = 7200


ORDER = ["p1", "p2a", "p2b", "p2c", "p2d", "p2e", "p2f", "p2g", "p2h", "p3", "p4", "p5", "all"]


def build_program(debug=(), stop="all"):
    def go(tag):
        return ORDER.index(tag) <= ORDER.index(stop)
    nc = bass.Bass("TRN2", target_bir_lowering=False)
    s = Sched(nc)
    cx = Ctx(nc, s)
    dbg = set(debug)

    def din(name, shape):
        return Buf(name, nc.dram_tensor(name, list(shape), F32, kind="ExternalInput"))

    def dout(name, shape):
        return Buf(name, nc.dram_tensor(name, list(shape), F32, kind="ExternalOutput"))

    def scratch(name, shape, dtype=F32):
        if name in dbg:
            return dout(name, shape)
        return Buf(name, nc.dram_tensor(name, list(shape), dtype))

    xin = din("xin", [T, D])
    cvecT = din("cvecT", [128, 2, 8])
    consts = din("consts", [128, 6, 512])
    rope = din("rope", [128, 2, L])
    e_ada_w = din("e_ada_w", [D, 3 * D])
    e_ada_b = din("e_ada_b", [1, 3 * D])
    e_norm_wT = din("e_norm_wT", [128, 8])
    e_w_in = din("e_w_in", [D, E_NCOL_EXT])
    e_conv_wT = din("e_conv_wT", [128, 12, 5])
    e_conv_bT = din("e_conv_bT", [128, 12])
    e_dt_bias = din("e_dt_bias", [1, 32])
    e_a_log = din("e_a_log", [1, 32])
    e_d_skip = din("e_d_skip", [1, 16])
    e_ssd_norm_wT = din("e_ssd_norm_wT", [128, 8])
    e_sink = din("e_sink", [128, 8])
    e_w_out = din("e_w_out", [2 * D, D])
    o_ada_w = din("o_ada_w", [D, 3 * D])
    o_ada_b = din("o_ada_b", [1, 3 * D])
    o_norm_wT = din("o_norm_wT", [128, 8])
    o_w_in = din("o_w_in", [D, 2 * D])
    s5_lam = din("s5_lam", [128, 2, 3, 32])
    s5_b = din("s5_b", [128, 2, 32, 16])
    s5_c = din("s5_c", [128, 2, 32, 16])
    o_d_skip = din("o_d_skip", [1, D])
    o_glu_w = din("o_glu_w", [D, D])
    o_glu_b = din("o_glu_b", [1, D])
    o_w_out = din("o_w_out", [D, D])
    final_norm_w = din("final_norm_w", [1, D])
    out_t = dout("out", [L, D])

    XS = scratch("XS", [T, 1024])
    BTOK = scratch("BTOK", [T, 256], BF16)
    BT = scratch("BT", [2, 128, T], BF16)
    CT = scratch("CT", [2, 128, T], BF16)
    SZ = scratch("SZ", [T, 1024])
    QR = scratch("QR", [8, 128, L], BF16)
    QC = scratch("QC", [8, 128, CTX], BF16)
    KR = scratch("KR", [4, 128, L], BF16)
    KC = scratch("KC", [4, 128, CTX], BF16)
    VT = scratch("VT", [T, 256], BF16)
    SG = scratch("SG", [8, 128, T])
    YF = scratch("YF", [T, 1024])
    YT = scratch("YT", [16, 128, T], BF16)
    X1 = scratch("X1", [T, 1024])
    U = scratch("U", [T, 1024])
    SG1 = scratch("SG1", [T, 1024])
    YTOK = scratch("YTOK", [T, 1024])
    KFP = scratch("KFP", [64, 16, 15, 16], BF16)
    KBR = scratch("KBR", [64, 16, 15, 16], BF16)
    HT = scratch("HT", [8, 128, T]) if "HT" in dbg else None
    DTD = scratch("DTD", [T, 32]) if "DTD" in dbg else None
    MODD = scratch("MODD", [4, 128, 24]) if "MODD" in dbg else None

    with ExitStack() as top:
        banks = [cx.ps(top, "bank%d" % i) for i in range(8)]
        cst = cx.sb(top, "cst", [128, 6, 512])
        s.dma(cst.full(), consts.full())
        ident = cst[:, 0, 0:128]
        tri = cst[:, 1, 0:128]
        utri = cst[:, 2, 0:128]
        ones = cst[:, 3, 0:128]
        onesb_t = cx.sb(top, "onesb", [128, 128], BF16)
        s.memset(onesb_t.full(), 1.0)
        onesb = onesb_t.full()
        modT = [[cx.sb(top, "modT%d%d" % (l, w), [128, 24]) for w in range(2)] for l in range(2)]
        gate_bc = [[cx.sb(top, "gate%d%d" % (l, w), [128, 1024]) for w in range(2)] for l in range(2)]
        scs = cx.sb(top, "scs", [128, 2, 8])

        def adaln_phase(layer, ada_w, ada_b):
            with ExitStack() as es:
                aw = [cx.sb(es, "aw%d" % k, [128, 3 * D]) for k in range(8)]
                ab = cx.sb(es, "ab", [1, 3 * D])
                modrow = [cx.sb(es, "modrow%d" % w, [1, 3 * D]) for w in range(2)]
                if layer == 0:
                    cv = cx.sb(es, "cv", [128, 2, 8])
                    s.dma(cv.full(), cvecT.full())
                    s.act(scs.full(), cv.full(), AF.Silu)
                for k in range(8):
                    s.dma(aw[k].full(), ada_w[k * 128:(k + 1) * 128, :])
                s.dma(ab.full(), ada_b.full())
                bi = 0
                for w in range(2):
                    for fg in range(6):
                        bk = banks[bi % 8]
                        bi += 1
                        s.mm(bk[0:1, :], [(scs[:, w, k:k + 1], aw[k][:, fg * 512:(fg + 1) * 512]) for k in range(8)])
                        s.tt(modrow[w][0:1, fg * 512:(fg + 1) * 512], bk[0:1, :], ab[0:1, fg * 512:(fg + 1) * 512], ALU.add)
                for w in range(2):
                    bk = banks[bi % 8]
                    bi += 1
                    for fc in range(24):
                        s.mm(bk[:, 2 * fc:2 * fc + 2], [(modrow[w][0:1, fc * 128:(fc + 1) * 128], cst[0:1, 3, 0:2])])
                    s.copy(modT[layer][w].full(), bk.view(0, [[512, 128], [2, 24]]))
                    for hh in range(2):
                        bk2 = banks[bi % 8]
                        bi += 1
                        s.mm(bk2.full(), [(cst[0:1, 3, 0:128], modrow[w][0:1, 2048 + hh * 512:2048 + (hh + 1) * 512])])
                        s.copy(gate_bc[layer][w][:, hh * 512:(hh + 1) * 512], bk2.full(), eng="act")
                    if MODD is not None:
                        s.dma(MODD[layer * 2 + w], modT[layer][w].full())
                s.flush()

        adaln_phase(0, e_ada_w, e_ada_b)

        with ExitStack() as l0:
            DT = cx.sb(l0, "DT", [128, NT, 32])
            DTA = cx.sb(l0, "DTA", [128, NT, 32])
            nw = cx.sb(l0, "nw", [128, 8])
            sc1 = [cx.sb(l0, "sc1_%d" % w, [128, 8]) for w in range(2)]
            s.dma(nw.full(), e_norm_wT.full())
            for w in range(2):
                s.stt(sc1[w].full(), modT[0][w][:, 8:16], 1.0, nw.full(), ALU.add, ALU.mult)

            wo = [cx.sb(l0, "wo%d" % k, [128, D], BF16) for k in range(16)]
            hts = ExitStack()
            hT = [cx.sb(hts, "hT%d" % k, [128, T], BF16) for k in range(8)]
            with ExitStack() as es:
                xt = [cx.sb(es, "xt%d" % i, [128, D]) for i in range(3)]
                xn = [cx.sb(es, "xn%d" % i, [128, D]) for i in range(3)]
                junk = cx.sb(es, "junk", [128, D])
                st = [cx.sb(es, "st%d" % i, [128, 4]) for i in range(3)]
                def n0(i):
                    x_, n_, st_ = xt[i % 3], xn[i % 3], st[i % 3]
                    s.dma(x_.full(), xin[i * 128:(i + 1) * 128, :])
                    s.act(junk.full(), x_.full(), AF.Square, accum=st_[:, 0:1])
                    s.ts(st_[:, 1:2], st_[:, 0:1], 1.0 / D, EPS, ALU.mult, ALU.add)
                    s.act(st_[:, 2:3], st_[:, 1:2], AF.Sqrt)
                    s.recip(st_[:, 3:4], st_[:, 2:3])
                    s.ts(n_.full(), x_.full(), st_[:, 3:4], None, ALU.mult)

                def n1(i):
                    w = 1 if i < 2 else 0
                    n_ = xn[i % 3]
                    for half in range(2):
                        bk = banks[(2 * i + half) % 8]
                        for kk in range(4):
                            k = half * 4 + kk
                            s.transpose(bk[:, kk * 128:(kk + 1) * 128], n_[:, k * 128:(k + 1) * 128], ident)
                        for kk in range(4):
                            k = half * 4 + kk
                            s.act(hT[k][:, i * 128:(i + 1) * 128], bk[:, kk * 128:(kk + 1) * 128], AF.Identity,
                                  bias=modT[0][w][:, k:k + 1], scale=sc1[w][:, k:k + 1])

                pipeline(list(range(NT)), [n0, n1])
                if HT is not None:
                    for k in range(8):
                        s.dma(HT[k], hT[k].full())
                s.flush()

            with ExitStack() as es:
                WB = 256
                NWB, PF = 6, 4
                wbuf = [cx.sb(es, "wbuf%d" % i, [128, 8, WB], BF16) for i in range(NWB)]
                wplan = [(OFF_XBC + 256 * k, 256) for k in range(6)]
                for qc in range(8):
                    wplan += [(OFF_Q + qc * 128, 128), (OFF_QS + qc * 128, 128)]
                for j in range(4):
                    wplan += [(OFF_KR + j * 128, 128), (OFF_KSR + j * 128, 128)]
                wplan += [(OFF_G + 256 * k, 256) for k in range(4)]
                wplan += [(OFF_Z + 256 * k, 256) for k in range(4)]
                wplan += [(OFF_KV + 256, 256), (OFF_DT, 32)]
                wstate = {"i": 0, "issued": 0}

                def _issue(n):
                    col0, ncol = wplan[n]
                    wb = wbuf[n % NWB]
                    s.dma(wb[:, :, 0:ncol], e_w_in.view(col0, [[E_NCOL_EXT, 128], [128 * E_NCOL_EXT, 8], [1, ncol]]), q="pool")

                def load_w(col0, ncol=WB):
                    i = wstate["i"]
                    wstate["i"] += 1
                    assert wplan[i] == (col0, ncol), (i, wplan[i], col0, ncol)
                    while wstate["issued"] < min(i + PF + 1, len(wplan)):
                        _issue(wstate["issued"])
                        wstate["issued"] += 1
                    return wbuf[i % NWB]

                bstate = {"i": 0}

                def nbank():
                    bk = banks[bstate["i"] % 8]
                    bstate["i"] += 1
                    return bk

                def fm_mm(wb, cc, t0, n):
                    bk = nbank()
                    s.mm(bk[:, 0:n], [(wb[:, k, cc * 128:(cc + 1) * 128], hT[k][:, t0:t0 + n]) for k in range(8)])
                    return bk

                xraws = [cx.sb(es, "xraw%d" % i, [128, T]) for i in range(2)]
                accs = [cx.sb(es, "acc%d" % i, [128, T]) for i in range(2)]
                acc = accs[0]
                accbs = [cx.sb(es, "accb%d" % i, [128, T], BF16) for i in range(2)]
                accb = accbs[0]
                rc_i = {"i": 0}
                tmp1s = [cx.sb(es, "tmp1_%d" % i, [128, 512]) for i in range(2)]
                tmp2s = [cx.sb(es, "tmp2_%d" % i, [128, 512]) for i in range(2)]
                stg = [cx.sb(es, "stg%d" % i, [128, 4, 128]) for i in range(2)]
                stgb = [cx.sb(es, "stgb%d" % i, [128, 4, 128], BF16) for i in range(2)]
                rp = cx.sb(es, "rp", [128, 2, L])
                cw = cx.sb(es, "cw", [128, 12, 5])
                cb = cx.sb(es, "cb", [128, 12])
                dtb = cx.sb(es, "dtb", [128, 32])
                abc = cx.sb(es, "abc", [128, 32])
                s.dma(rp.full(), rope.full())
                s.dma(cw.full(), e_conv_wT.full())
                s.dma(cb.full(), e_conv_bT.full())
                s.dma(dtb.full(), e_dt_bias.view(0, [[0, 128], [1, 32]]))
                s.dma(abc.full(), e_a_log.view(0, [[0, 128], [1, 32]]))
                s.act(abc.full(), abc.full(), AF.Exp)
                s.ts(abc.full(), abc.full(), -1.0, None, ALU.mult)
                stg_i = {"i": 0}

                def transposes_to(dst, col0, src, lowp=False):
                    for i0 in range(0, NT, 4):
                        nb = min(4, NT - i0)
                        bk = nbank()
                        for ii in range(nb):
                            i = i0 + ii
                            s.transpose(bk[:, ii * 128:(ii + 1) * 128], src[:, i * 128:(i + 1) * 128], ident)
                        sg_ = (stgb if lowp else stg)[stg_i["i"] % 2]
                        stg_i["i"] += 1
                        s.copy(sg_[:, 0:nb, :], bk.view(0, [[512, 128], [128, nb], [1, 128]]), eng="act")
                        ncols = dst.h.shape[1]
                        s.dma(dst.view(i0 * 128 * ncols + col0, [[ncols, 128], [128 * ncols, nb], [1, 128]]),
                              sg_[:, 0:nb, :])

                wb_of = {}

                def xa(fc):
                    if fc % 2 == 0:
                        wb_of[fc // 2] = load_w(OFF_XBC + fc * 128)
                    wb = wb_of[fc // 2]
                    xraw = xraws[fc % 2]
                    for (t0, n) in TG:
                        bk = fm_mm(wb, fc % 2, t0, n)
                        s.copy(xraw[:, t0:t0 + n], bk[:, 0:n], eng="act")

                def xb(fc):
                    xraw, acc = xraws[fc % 2], accs[fc % 2]
                    s.ts(acc.full(), xraw.full(), cw[:, fc, 2:3], cb[:, fc:fc + 1], ALU.mult, ALU.add)
                    for kk in (0, 1, 3, 4):
                        d_ = kk - 2
                        for (s0, sl) in ((0, CTX), (CTX, L)):
                            lo = max(s0, s0 - d_)
                            hi = min(s0 + sl, s0 + sl - d_)
                            s.stt(acc[:, lo:hi], xraw[:, lo + d_:hi + d_], cw[:, fc, kk:kk + 1], acc[:, lo:hi],
                                  ALU.mult, ALU.add)
                    s.act(acc.full(), acc.full(), AF.Silu)
                    if fc < 8:
                        transposes_to(XS, fc * 128, acc)
                    elif fc < 10:
                        s.copy(accb.full(), acc.full(), eng="act")
                        s.dma(BT[fc - 8], accb.full())
                        transposes_to(BTOK, (fc - 8) * 128, acc, lowp=True)
                    else:
                        s.copy(accb.full(), acc.full(), eng="act")
                        s.dma(CT[fc - 10], accb.full())

                pipeline(list(range(12 if go('p2a') else 0)), [xa, xb])

                def rope_chunk(col_plain, col_swap, dst_rot, dst_ctx):
                    accb = accbs[rc_i["i"] % 2]
                    rc_i["i"] += 1
                    wa = load_w(col_plain, 128)
                    wsw = load_w(col_swap, 128)
                    for gi, (t0, n) in enumerate(TG):
                        bka = fm_mm(wa, 0, t0, n)
                        if gi == 0:
                            s.copy(accb[:, 0:CTX], bka[:, 0:CTX], eng="act")
                            continue
                        bkb = fm_mm(wsw, 0, t0, n)
                        l0 = t0 - CTX
                        tmp1, tmp2 = tmp1s[gi % 2], tmp2s[gi % 2]
                        s.tt(tmp1.full(), bka.full(), rp[:, 0, l0:l0 + 512], ALU.mult)
                        s.tt(tmp2.full(), bkb.full(), rp[:, 1, l0:l0 + 512], ALU.mult)
                        s.tt(accb[:, t0:t0 + n], tmp1.full(), tmp2.full(), ALU.add)
                    s.dma(dst_ctx, accb[:, 0:CTX])
                    s.dma(dst_rot, accb[:, CTX:T])

                for qc in range(8 if go('p2b') else 0):
                    rope_chunk(OFF_Q + qc * 128, OFF_QS + qc * 128, QR[qc], QC[qc])
                for j in range(4 if go('p2c') else 0):
                    rope_chunk(OFF_KR + j * 128, OFF_KSR + j * 128, KR[j], KC[j])

                for gc in range(8 if go('p2d') else 0):
                    acc = accs[gc % 2]
                    if gc % 2 == 0:
                        wb = load_w(OFF_G + gc * 128)
                    for (t0, n) in TG:
                        bk = fm_mm(wb, gc % 2, t0, n)
                        s.act(acc[:, t0:t0 + n], bk[:, 0:n], AF.Silu)
                    s.dma(SG[gc], acc.full())

                NT_E = NT if go('p2e') else 0
                wz = [load_w(OFF_Z + i * 256) for i in range(4)]
                for i in range(NT_E):
                    z_a = accs[i % 2]
                    for half in range(2):
                        bk = nbank()
                        for q4 in range(2):
                            wbz = wz[half * 2 + q4]
                            s.mm(bk[:, q4 * 256:(q4 + 1) * 256],
                                 [(hT[k][:, i * 128:(i + 1) * 128], wbz[:, k, :]) for k in range(8)])
                        s.act(z_a[:, half * 512:(half + 1) * 512], bk.full(), AF.Silu)
                    s.dma(SZ[i * 128:(i + 1) * 128, :], z_a[:, 0:1024])
                wv = load_w(OFF_KV + 256)
                wdt = load_w(OFF_DT, 32)
                vt = [cx.sb(es, "vt%d" % i, [128, 256], BF16) for i in range(2)]
                for i in range(NT if go('p2f') else 0):
                    bk = nbank()
                    s.mm(bk[:, 0:256], [(hT[k][:, i * 128:(i + 1) * 128], wv[:, k, :]) for k in range(8)])
                    s.copy(vt[i % 2].full(), bk[:, 0:256], eng="act")
                    s.dma(VT[i * 128:(i + 1) * 128, :], vt[i % 2].full())
                for i in range(NT if go('p2g') else 0):
                    bk = nbank()
                    s.mm(bk[:, 0:32], [(hT[k][:, i * 128:(i + 1) * 128], wdt[:, k, 0:32]) for k in range(8)])
                    s.tt(DT[:, i, :], bk[:, 0:32], dtb.full(), ALU.add)
                    if go('p2h'):
                        s.act(DT[:, i, :], DT[:, i, :], AF.Exp)
                        s.ts(DT[:, i, :], DT[:, i, :], 1.0, None, ALU.add)
                        s.act(DT[:, i, :], DT[:, i, :], AF.Ln)
                    s.tt(DTA[:, i, :], DT[:, i, :], abc.full(), ALU.mult)
                    if DTD is not None:
                        s.dma(DTD[i * 128:(i + 1) * 128, :], DT[:, i, :])
                s.flush()
            hts.close()
            for k in range(16):
                s.dma(wo[k].full(), e_w_out[k * 128:(k + 1) * 128, :], q="pool")

            with ExitStack() as es:
                nb_ = {"i": 0}

                def nbank():
                    bk = banks[nb_["i"] % 8]
                    nb_["i"] += 1
                    return bk

                N3 = 3
                xs_t = [cx.sb(es, "xs_t%d" % i, [128, 1024]) for i in range(N3)]
                b_t = [cx.sb(es, "b_t%d" % i, [128, 256], BF16) for i in range(N3)]
                bt_t = [cx.sb(es, "bt_t%d" % i, [128, 2, 128], BF16) for i in range(N3)]
                ct_t = [cx.sb(es, "ct_t%d" % i, [128, 2, 128], BF16) for i in range(N3)]
                yf_t = [cx.sb(es, "yf_t%d" % i, [128, 1024]) for i in range(N3)]
                sz_t = [cx.sb(es, "sz_t%d" % i, [128, 1024]) for i in range(N3)]
                MTs = [cx.sb(es, "MT%d" % i, [128, 2048], BF16) for i in range(N3)]
                xcs = [cx.sb(es, "xc%d" % i, [128, 1024], BF16) for i in range(N3)]
                xcds = [cx.sb(es, "xcd%d" % i, [128, 1024], BF16) for i in range(N3)]
                tmpos = [cx.sb(es, "tmpo%d" % i, [128, 1024]) for i in range(N3)]
                ytots = [cx.sb(es, "ytot%d" % i, [128, 1024]) for i in range(N3)]
                sms = [cx.sb(es, "sm%d" % i, [128, 4, 16]) for i in range(N3)]
                st3s = [cx.sb(es, "st3_%d" % i, [128, 4]) for i in range(N3)]
                ystgs = [cx.sb(es, "ystg%d" % i, [128, 8, 128], BF16) for i in range(2)]
                dtatris = [cx.sb(es, "dtatri%d" % i, [128, 2048]) for i in range(2)]
                decTs = [cx.sb(es, "decT%d" % i, [128, 2048]) for i in range(2)]
                cb_sbs = [cx.sb(es, "cb_sb%d" % i, [128, 256]) for i in range(2)]
                junk = cx.sb(es, "junk3", [128, 1024])
                Hs = [cx.sb(es, "Hs%d" % g, [128, 512]) for g in range(2)]
                Hb = [cx.sb(es, "Hb%d" % g, [128, 512], BF16) for g in range(2)]
                dsk = cx.sb(es, "dsk", [128, 16])
                snw = cx.sb(es, "snw", [128, 8])
                cm1 = cx.sb(es, "cm1", [128, 1])
                s.memset(cm1.full(), -1.0)
                s.dma(dsk.full(), e_d_skip.view(0, [[0, 128], [1, 16]]))
                s.dma(snw.full(), e_ssd_norm_wT.full())

                def bc3(buf, off, pstep, n1, s1, n2, s2):
                    return buf.view(off, [[pstep, 128], [s1, n1], [s2, n2]])

                n_ch = NT if go("p3") else 0
                for d_ in range(2):
                    order = list(range(NT)) if d_ == 0 else [1, 0] + list(range(NT - 1, 1, -1))
                    order = order[:n_ch]
                    TRIoff = 512 if d_ == 0 else 1024
                    TRIv = tri if d_ == 0 else utri
                    negm = cst[:, 4 + d_, :]
                    for g in range(2):
                        s.memset(Hs[g].full(), 0.0)
                        s.memset(Hb[g].full(), 0.0)

                    def stA(item, d_=d_, TRIoff=TRIoff, TRIv=TRIv, negm=negm):
                        ci, i = item
                        p3, p2 = ci % N3, ci % 2
                        xs_, b_, bt_, ct_ = xs_t[p3], b_t[p3], bt_t[p3], ct_t[p3]
                        MT, xc, xcd, sm = MTs[p3], xcs[p3], xcds[p3], sms[p3]
                        dtatri, decT, cb_sb = dtatris[p2], decTs[p2], cb_sbs[p2]
                        s.dma(xs_.full(), XS[i * 128:(i + 1) * 128, :])
                        s.dma(b_.full(), BTOK[i * 128:(i + 1) * 128, :])
                        s.dma(bt_.full(), BT.view(i * 128, [[T, 128], [128 * T, 2], [1, 128]]))
                        s.dma(ct_.full(), CT.view(i * 128, [[T, 128], [128 * T, 2], [1, 128]]))
                        if d_ == 1:
                            s.dma(yf_t[p3].full(), YF[i * 128:(i + 1) * 128, :])
                            s.dma(sz_t[p3].full(), SZ[i * 128:(i + 1) * 128, :])
                        dta_i = DTA[:, i, d_ * 16:(d_ + 1) * 16]
                        doff = i * 32 + d_ * 16
                        s.tt(bc3(dtatri, 0, 2048, 16, 128, 128, 1), bc3(DTA, doff, NT * 32, 16, 1, 128, 0),
                             bc3(cst, TRIoff, 3072, 16, 0, 128, 1), ALU.mult, eng="pool")
                        bs = nbank()
                        s.mm(bs[:, 0:16], [(TRIv, dta_i)])
                        s.mm(bs[:, 16:32], [(ones, dta_i)])
                        na, ea, de, cd = sm[:, 0, :], sm[:, 1, :], sm[:, 2, :], sm[:, 3, :]
                        s.ts(na, bs[:, 0:16], -1.0, None, ALU.mult)
                        s.act(ea, bs[:, 0:16], AF.Exp)
                        s.tt(de, bs[:, 16:32], na, ALU.add)
                        s.act(de, de, AF.Exp)
                        s.act(cd, bs[:, 16:32], AF.Exp)
                        for hq in range(4):
                            bq = nbank()
                            s.mm(bq.full(), [(ones, dtatri[:, hq * 512:(hq + 1) * 512]), (ident, negm)])
                            for hh in range(4):
                                h = hq * 4 + hh
                                s.act(decT[:, h * 128:(h + 1) * 128], bq[:, hh * 128:(hh + 1) * 128], AF.Exp,
                                      bias=sm[:, 0, h:h + 1])
                        bc = nbank()
                        for g in range(2):
                            s.mm(bc[:, g * 128:(g + 1) * 128], [(bt_[:, g, :], ct_[:, g, :])])
                        s.copy(cb_sb.full(), bc[:, 0:256], eng="act")
                        for g in range(2):
                            s.tt(bc3(MT, g * 1024, 2048, 8, 128, 128, 1), bc3(decT, g * 1024, 2048, 8, 128, 128, 1),
                                 bc3(cb_sb, g * 128, 256, 8, 0, 128, 1), ALU.mult)
                        s.tt(bc3(xc, 0, 1024, 16, 64, 64, 1), bc3(xs_, 0, 1024, 16, 64, 64, 1),
                             bc3(DT, doff, NT * 32, 16, 1, 64, 0), ALU.mult, eng="pool")
                        s.tt(bc3(xcd, 0, 1024, 16, 64, 64, 1), bc3(xc, 0, 1024, 16, 64, 64, 1),
                             bc3(sm, 32, 64, 16, 1, 64, 0), ALU.mult, eng="pool")
                        if d_ == 1:
                            s.tt(bc3(tmpos[p3], 0, 1024, 16, 64, 64, 1), bc3(xs_, 0, 1024, 16, 64, 64, 1),
                                 bc3(dsk, 0, 16, 16, 1, 64, 0), ALU.mult, eng="pool")
                            s.tt(yf_t[p3].full(), yf_t[p3].full(), tmpos[p3].full(), ALU.add, eng="pool")

                    def stB(item, d_=d_):
                        ci, i = item
                        p3 = ci % N3
                        b_, ct_ = b_t[p3], ct_t[p3]
                        MT, xc, xcd, sm, tmpo, ytot = MTs[p3], xcs[p3], xcds[p3], sms[p3], tmpos[p3], ytots[p3]
                        ydst = yf_t[p3] if d_ == 0 else ytot
                        for g in range(2):
                            by = nbank()
                            for hh in range(8):
                                h = g * 8 + hh
                                s.mm(by[:, hh * 64:(hh + 1) * 64], [(MT[:, h * 128:(h + 1) * 128], xc[:, h * 64:(h + 1) * 64])])
                            bo = nbank()
                            s.mm(bo.full(), [(ct_[:, g, :], Hb[g].full())])
                            s.tt(bc3(tmpo, g * 512, 1024, 8, 64, 64, 1), bc3(bo, 0, 512, 8, 64, 64, 1),
                                 bc3(sm, 16 + g * 8, 64, 8, 1, 64, 0), ALU.mult)
                            s.tt(ydst[:, g * 512:(g + 1) * 512], by.full(), tmpo[:, g * 512:(g + 1) * 512], ALU.add)
                        for g in range(2):
                            bst = nbank()
                            s.mm(bst.full(), [(b_[:, g * 128:(g + 1) * 128], xcd[:, g * 512:(g + 1) * 512])])
                            s.tt(bc3(Hs[g], 0, 512, 8, 64, 64, 1), bc3(Hs[g], 0, 512, 8, 64, 64, 1),
                                 bc3(sm, 48 + g * 8, 64, 8, 1, 64, 0), ALU.mult)
                            s.tt(Hs[g].full(), Hs[g].full(), bst.full(), ALU.add)
                            s.copy(Hb[g].full(), Hs[g].full(), eng="act")
                        if d_ == 0:
                            s.dma(YF[i * 128:(i + 1) * 128, :], yf_t[p3].full())

                    def stC(item, d_=d_):
                        if d_ == 0:
                            return
                        ci, i = item
                        p3, p2 = ci % N3, ci % 2
                        ytot, sz_, st3, ystg = ytots[p3], sz_t[p3], st3s[p3], ystgs[p2]
                        s.tt(ytot.full(), ytot.full(), yf_t[p3].full(), ALU.add)
                        s.tt(ytot.full(), ytot.full(), sz_.full(), ALU.mult)
                        s.act(junk.full(), ytot.full(), AF.Square, accum=st3[:, 0:1])
                        s.ts(st3[:, 1:2], st3[:, 0:1], 1.0 / 1024, EPS, ALU.mult, ALU.add)
                        s.act(st3[:, 2:3], st3[:, 1:2], AF.Sqrt)
                        s.recip(st3[:, 3:4], st3[:, 2:3])
                        s.act(ytot.full(), ytot.full(), AF.Copy, scale=st3[:, 3:4])
                        for half in range(2):
                            bk = nbank()
                            for kk in range(4):
                                k = half * 4 + kk
                                s.transpose(bk[:, kk * 128:(kk + 1) * 128], ytot[:, k * 128:(k + 1) * 128], ident)
                            for kk in range(4):
                                k = half * 4 + kk
                                s.act(ystg[:, k, :], bk[:, kk * 128:(kk + 1) * 128], AF.Copy, scale=snw[:, k:k + 1])
                        s.dma(YT.view(i * 128, [[T, 128], [128 * T, 8], [1, 128]]), ystg.full())

                    pipeline(list(enumerate(order)), [stA, stB, stC])
                s.flush()

            with ExitStack() as es:
                nb_ = {"i": 0}

                def nbank():
                    bk = banks[nb_["i"] % 8]
                    nb_["i"] += 1
                    return bk

                J2 = 2
                qr_ts = [cx.sb(es, "qr_t%d" % i, [128, 2, L], BF16) for i in range(J2)]
                qc_ts = [cx.sb(es, "qc_t%d" % i, [128, 2, CTX], BF16) for i in range(J2)]
                kr_ts = [cx.sb(es, "kr_t%d" % i, [128, L], BF16) for i in range(J2)]
                kc_ts = [cx.sb(es, "kc_t%d" % i, [128, CTX], BF16) for i in range(J2)]
                v_ts = [cx.sb(es, "v_t%d" % i, [128, NT, 64], BF16) for i in range(J2)]
                v2s = [cx.sb(es, "v2_%d" % i, [128, NT, 128], BF16) for i in range(J2)]
                sg_ts = [cx.sb(es, "sg_t%d" % i, [128, 2, T]) for i in range(J2)]
                asts = [cx.sb(es, "ast%d" % i, [128, 2, T], BF16) for i in range(J2)]
                NP = 3
                pt = [[cx.sb(es, "pt%d_%d" % (a, b), [128, 512], BF16) for b in range(5)] for a in range(NP)]
                rds = [cx.sb(es, "rd%d" % i, [128, 256]) for i in range(2)]
                aos = [cx.sb(es, "ao%d" % i, [128, 256]) for i in range(2)]
                es_pp = cx.sb(es, "es_pp", [128, 8])
                c8 = cx.sb(es, "c8", [128, 1])
                s.memset(c8.full(), 0.125)
                s.dma(es_pp.full(), e_sink.full())
                s.act(es_pp.full(), es_pp.full(), AF.Exp)
                ATT_DBG = [int(v) for v in os.environ.get("ATT_DBG", "4,18,4").split(",")]
                items = []
                for j in range(ATT_DBG[0] if go("p4") else 0):
                    qbs = ([("c", 0), ("c", 1)] + [("l", b) for b in range(16)])[:ATT_DBG[1]]
                    for qi, (kind, bi) in enumerate(qbs):
                        items.append((len(items), j, kind, bi, qi == 0, qi == len(qbs) - 1))

                def keys_of(kind, bi):
                    keys = [("c", 0, None), ("c", 1, None)]
                    if kind == "l":
                        if bi > 0:
                            keys.append(("l", bi - 1, "prev"))
                        keys.append(("l", bi, None))
                        if bi < 15:
                            keys.append(("l", bi + 1, "next"))
                    return keys

                def atA(item):
                    n, j, kind, bi, first, last = item
                    js = j % J2
                    qr_t, qc_t, kr_t, kc_t, v_t, v2, sg_t = qr_ts[js], qc_ts[js], kr_ts[js], kc_ts[js], v_ts[js], v2s[js], sg_ts[js]
                    if first:
                        s.dma(qr_t.full(), QR.view(2 * j * 128 * L, [[L, 128], [128 * L, 2], [1, L]]))
                        s.dma(qc_t.full(), QC.view(2 * j * 128 * CTX, [[CTX, 128], [128 * CTX, 2], [1, CTX]]))
                        s.dma(kr_t.full(), KR[j])
                        s.dma(kc_t.full(), KC[j])
                        s.dma(v_t.full(), VT.view(j * 64, [[256, 128], [128 * 256, NT], [1, 64]]))
                        s.dma(sg_t.full(), SG.view(2 * j * 128 * T, [[T, 128], [128 * T, 2], [1, T]]))
                        s.copy(v2[:, :, 0:64], v_t.full(), eng="pool")
                        s.copy(v2[:, :, 64:128], v_t.full(), eng="pool")
                    qsrc, q0 = (qc_t, bi * 128) if kind == "c" else (qr_t, bi * 128)
                    pts = pt[n % NP]
                    qw = qsrc.h.shape[2]
                    for ki, (kk, kb, msk) in enumerate(keys_of(kind, bi)):
                        ksrc = kc_t if kk == "c" else kr_t
                        for par in range(2):
                            p0 = par * 64
                            bs = nbank()
                            s.mm(bs[:, 0:256],
                                 [(ksrc[p0:p0 + 64, kb * 128:(kb + 1) * 128],
                                   qsrc.view(p0 * 2 * qw + q0, [[2 * qw, 64], [qw, 2], [1, 128]]))])
                            s.act(pts[ki][:, par * 256:(par + 1) * 256], bs[:, 0:256], AF.Exp, scale=c8[:, 0:1])
                        if msk is not None:
                            moff = 1024 if msk == "prev" else 512
                            s.tt(pts[ki].view(0, [[512, 128], [128, 4], [1, 128]]),
                                 pts[ki].view(0, [[512, 128], [128, 4], [1, 128]]),
                                 cst.view(moff, [[3072, 128], [0, 4], [1, 128]]), ALU.mult, eng="pool")

                def atB(item):
                    n, j, kind, bi, first, last = item
                    js = j % J2
                    v2, sg_t, ast = v2s[js], sg_ts[js], asts[js]
                    tok0 = bi * 128 if kind == "c" else CTX + bi * 128
                    keys = keys_of(kind, bi)
                    pts = pt[n % NP]
                    rd, ao = rds[n % 2], aos[n % 2]
                    vt_idx = [(kb if kk == "c" else 2 + kb) for (kk, kb, _) in keys]
                    bn = nbank()
                    s.mm(bn.full(), [(v2[:, vt_idx[ki], :], pts[ki].full()) for ki in range(len(keys))])
                    bd = nbank()
                    s.mm(bd.full(), [(onesb, pts[ki].full()) for ki in range(len(keys))])
                    for par in range(2):
                        p0 = par * 64
                        for c in range(2):
                            s.ts(rd[p0:p0 + 64, c * 128:(c + 1) * 128],
                                 bd[p0:p0 + 64, par * 256 + c * 128:par * 256 + (c + 1) * 128],
                                 es_pp[p0:p0 + 64, 2 * j + c:2 * j + c + 1], None, ALU.add)
                    s.recip(rd.full(), rd.full())
                    for par in range(2):
                        p0 = par * 64
                        s.tt(ao[p0:p0 + 64, :], bn[p0:p0 + 64, par * 256:(par + 1) * 256], rd[p0:p0 + 64, :], ALU.mult)
                    s.tt(ast.view(tok0, [[2 * T, 128], [T, 2], [1, 128]]),
                         ao.view(0, [[256, 128], [128, 2], [1, 128]]),
                         sg_t.view(tok0, [[2 * T, 128], [T, 2], [1, 128]]), ALU.mult)
                    if last:
                        s.dma(YT.view((8 + 2 * j) * 128 * T, [[T, 128], [128 * T, 2], [1, T]]), ast.full())

                pipeline(items, [atA, atB])
                s.flush()

            with ExitStack() as es:
                nb_ = {"i": 0}

                def nbank():
                    bk = banks[nb_["i"] % 8]
                    nb_["i"] += 1
                    return bk

                yt = [cx.sb(es, "yt%d" % i, [128, 16, 128], BF16) for i in range(2)]
                xt = [cx.sb(es, "xt5_%d" % i, [128, D]) for i in range(2)]
                x1t = [cx.sb(es, "x1t%d" % i, [128, D]) for i in range(2)]
                tmp5s = [cx.sb(es, "tmp5_%d" % i, [128, 512]) for i in range(2)]
                for i in range(NT if go("p5") else 0):
                    w = 1 if i < 2 else 0
                    y_, x_, o_ = yt[i % 2], xt[i % 2], x1t[i % 2]
                    s.dma(y_.full(), YT.view(i * 128, [[T, 128], [128 * T, 16], [1, 128]]))
                    s.dma(x_.full(), xin[i * 128:(i + 1) * 128, :])
                    for half in range(2):
                        tmp5 = tmp5s[half]
                        bk = nbank()
                        s.mm(bk.full(), [(y_[:, fc, :], wo[fc][:, half * 512:(half + 1) * 512]) for fc in range(16)])
                        s.tt(tmp5.full(), bk.full(), gate_bc[0][w][:, half * 512:(half + 1) * 512], ALU.mult)
                        s.tt(o_[:, half * 512:(half + 1) * 512], tmp5.full(), x_[:, half * 512:(half + 1) * 512], ALU.add)
                    s.dma(X1[i * 128:(i + 1) * 128, :], o_.full())
                s.flush()

        if go("all"):
            adaln_phase(1, o_ada_w, o_ada_b)
        with ExitStack() as l1:
            if not go("all"):
                return nc
            nb_ = {"i": 0}

            def nbank():
                bk = banks[nb_["i"] % 8]
                nb_["i"] += 1
                return bk

            with ExitStack() as es:
                nw = cx.sb(es, "nw1", [128, 8])
                sc1 = [cx.sb(es, "sc1b_%d" % w, [128, 8]) for w in range(2)]
                s.dma(nw.full(), o_norm_wT.full())
                for w in range(2):
                    s.stt(sc1[w].full(), modT[1][w][:, 8:16], 1.0, nw.full(), ALU.add, ALU.mult)
                hT = [cx.sb(es, "hTb%d" % k, [128, T], BF16) for k in range(8)]
                xt = [cx.sb(es, "xtb%d" % i, [128, D]) for i in range(3)]
                xn = [cx.sb(es, "xnb%d" % i, [128, D]) for i in range(3)]
                junk = cx.sb(es, "junkb", [128, D])
                st = [cx.sb(es, "stb%d" % i, [128, 4]) for i in range(3)]
                def m0(i):
                    x_, n_, st_ = xt[i % 3], xn[i % 3], st[i % 3]
                    s.dma(x_.full(), X1[i * 128:(i + 1) * 128, :])
                    s.act(junk.full(), x_.full(), AF.Square, accum=st_[:, 0:1])
                    s.ts(st_[:, 1:2], st_[:, 0:1], 1.0 / D, EPS, ALU.mult, ALU.add)
                    s.act(st_[:, 2:3], st_[:, 1:2], AF.Sqrt)
                    s.recip(st_[:, 3:4], st_[:, 2:3])
                    s.ts(n_.full(), x_.full(), st_[:, 3:4], None, ALU.mult)

                def m1(i):
                    w = 1 if i < 2 else 0
                    n_ = xn[i % 3]
                    for half in range(2):
                        bk = nbank()
                        for kk in range(4):
                            k = half * 4 + kk
                            s.transpose(bk[:, kk * 128:(kk + 1) * 128], n_[:, k * 128:(k + 1) * 128], ident)
                        for kk in range(4):
                            k = half * 4 + kk
                            s.act(hT[k][:, i * 128:(i + 1) * 128], bk[:, kk * 128:(kk + 1) * 128], AF.Identity,
                                  bias=modT[1][w][:, k:k + 1], scale=sc1[w][:, k:k + 1])

                pipeline(list(range(NT)), [m0, m1])
                wq = [cx.sb(es, "wq%d" % i, [128, 8, 256], BF16) for i in range(8)]
                for q8 in range(8):
                    s.dma(wq[q8].full(), o_w_in.view(q8 * 256, [[2 * D, 128], [128 * 2 * D, 8], [1, 256]]), q="pool")
                ot = [cx.sb(es, "ot%d" % i, [128, D]) for i in range(2)]
                oi = 0
                for which in range(2):
                    for i in range(NT):
                        if which == 1 and i < 2:
                            continue
                        o_ = ot[oi % 2]
                        oi += 1
                        for half in range(2):
                            bk = nbank()
                            for q4 in range(2):
                                s.mm(bk[:, q4 * 256:(q4 + 1) * 256],
                                     [(hT[k][:, i * 128:(i + 1) * 128], wq[which * 4 + half * 2 + q4][:, k, :]) for k in range(8)])
                            if which == 0:
                                s.copy(o_[:, half * 512:(half + 1) * 512], bk.full(), eng="act")
                            else:
                                s.act(o_[:, half * 512:(half + 1) * 512], bk.full(), AF.Silu)
                        s.dma((U if which == 0 else SG1)[i * 128:(i + 1) * 128, :], o_.full())
                s.flush()

            L1S = os.environ.get('L1S', 'z')
            if L1S == 'a':
                return nc
            with ExitStack() as es:
                lam = cx.sb(es, "lam", [128, 2, 3, 32])
                bprm = cx.sb(es, "bprm", [128, 2, 32, 16])
                cprm = cx.sb(es, "cprm", [128, 2, 32, 16])
                s.dma(lam.full(), s5_lam.full())
                s.dma(bprm.full(), s5_b.full())
                s.dma(cprm.full(), s5_c.full())
                kc = cx.sb(es, "kconst", [128, 4])
                s.memset(kc[:, 0:1], 1.0 / 16)
                s.memset(kc[:, 1:2], math.pi / 2)
                s.memset(kc[:, 2:3], 0.0)
                s.memset(kc[:, 3:4], 1.0)
                W64 = [128, 2, 32]

                def t64(name):
                    return cx.sb(es, name, W64)

                def lv(i):
                    return lam.view(i * 32, [[192, 128], [96, 2], [1, 32]])

                dt_ = t64("dt_"); mag = t64("mag"); th = t64("th"); cs = t64("cs"); sn = t64("sn")
                t_a = t64("t_a"); t_b = t64("t_b"); t_c = t64("t_c")
                abre = t64("abre"); abim = t64("abim"); cre = t64("cre"); cim = t64("cim")
                s.act(dt_.full(), lv(2), AF.Exp)
                s.tt(t_a.full(), lv(0), dt_.full(), ALU.mult)
                s.act(mag.full(), t_a.full(), AF.Exp)
                s.tt(th.full(), lv(1), dt_.full(), ALU.mult)
                s.act(sn.full(), th.full(), AF.Sin, scale=kc[:, 0:1])
                s.act(cs.full(), th.full(), AF.Sin, scale=kc[:, 0:1], bias=kc[:, 1:2])
                for _ in range(4):
                    s.tt(t_a.full(), cs.full(), cs.full(), ALU.mult)
                    s.tt(t_b.full(), sn.full(), sn.full(), ALU.mult)
                    s.tt(t_c.full(), sn.full(), cs.full(), ALU.mult)
                    s.tt(cs.full(), t_a.full(), t_b.full(), ALU.subtract)
                    s.ts(sn.full(), t_c.full(), 2.0, None, ALU.mult)
                s.tt(abre.full(), mag.full(), cs.full(), ALU.mult)
                s.tt(abim.full(), mag.full(), sn.full(), ALU.mult)
                PW = cx.sb(es, "PW", [128, 2, 9, 64])

                def pw(ri, k):
                    return PW.view((ri * 9 + k) * 64, [[2 * 9 * 64, 128], [32, 2], [1, 32]])

                s.memset(PW[:, 0, 0, :], 1.0)
                s.memset(PW[:, 1, 0, :], 0.0)
                for k in range(8):
                    s.tt(t_a.full(), pw(0, k), abre.full(), ALU.mult)
                    s.tt(t_b.full(), pw(1, k), abim.full(), ALU.mult)
                    s.tt(pw(0, k + 1), t_a.full(), t_b.full(), ALU.subtract)
                    s.tt(t_a.full(), pw(0, k), abim.full(), ALU.mult)
                    s.tt(t_b.full(), pw(1, k), abre.full(), ALU.mult)
                    s.tt(pw(1, k + 1), t_a.full(), t_b.full(), ALU.add)
                s.ts(t_c.full(), abre.full(), -1.0, None, ALU.add)
                s.tt(t_a.full(), lv(0), lv(0), ALU.mult)
                s.tt(t_b.full(), lv(1), lv(1), ALU.mult)
                s.tt(t_a.full(), t_a.full(), t_b.full(), ALU.add)
                s.recip(dt_.full(), t_a.full())
                s.tt(t_a.full(), t_c.full(), lv(0), ALU.mult)
                s.tt(t_b.full(), abim.full(), lv(1), ALU.mult)
                s.tt(t_a.full(), t_a.full(), t_b.full(), ALU.add)
                s.tt(cre.full(), t_a.full(), dt_.full(), ALU.mult)
                s.tt(t_a.full(), abim.full(), lv(0), ALU.mult)
                s.tt(t_b.full(), t_c.full(), lv(1), ALU.mult)
                s.tt(t_a.full(), t_a.full(), t_b.full(), ALU.subtract)
                s.tt(cim.full(), t_a.full(), dt_.full(), ALU.mult)
                BB = cx.sb(es, "BB", [128, 2, 2, 512])
                tb1 = cx.sb(es, "tb1", [128, 512])
                tb2 = cx.sb(es, "tb2", [128, 512])

                def bb(ri, d_, g0=0, ng=32):
                    return BB.view((ri * 2 + d_) * 512 + g0 * 16, [[2048, 128], [16, ng], [1, 16]])

                def v3(buf, off, pstep, n1, s1, n2, s2):
                    return buf.view(off, [[pstep, 128], [s1, n1], [s2, n2]])

                def prm(buf, ri, g0=0, ng=32):
                    return buf.view(ri * 512 + g0 * 16, [[1024, 128], [16, ng], [1, 16]])

                def cf(buf, d_, g0=0, ng=32, n2=16):
                    return buf.view(d_ * 32 + g0, [[64, 128], [1, ng], [0, n2]])

                t1v = v3(tb1, 0, 512, 32, 16, 16, 1)
                t2v = v3(tb2, 0, 512, 32, 16, 16, 1)
                for d_ in range(2):
                    s.tt(t1v, prm(bprm, 0), cf(cre, d_), ALU.mult)
                    s.tt(t2v, prm(bprm, 1), cf(cim, d_), ALU.mult)
                    s.tt(bb(0, d_), t1v, t2v, ALU.subtract)
                    s.tt(t1v, prm(bprm, 1), cf(cre, d_), ALU.mult)
                    s.tt(t2v, prm(bprm, 0), cf(cim, d_), ALU.mult)
                    s.tt(bb(1, d_), t1v, t2v, ALU.add)
                LA = cx.sb(es, "LA", [128, 2, 32, 2])
                LB = cx.sb(es, "LB", [128, 2, 32, 2])
                for ri in range(2):
                    s.copy(LA.view(ri, [[128, 128], [64, 2], [2, 32]]), pw(0, 8))
                s.ts(LB.view(0, [[128, 128], [64, 2], [2, 32]]), pw(1, 8), -1.0, None, ALU.mult)
                s.copy(LB.view(1, [[128, 128], [64, 2], [2, 32]]), pw(1, 8))
                zt_ = cx.sb(es, "zt_", [16, 16, 112], BF16)
                s.memset(zt_.full(), 0.0)
                s.flush()

                if L1S == 'b':
                    return nc
                for b in range(4 if L1S not in ('c1', 'd1', 'e1', 'f1', 'g1') else 1):
                    g0 = 8 * b
                    with ExitStack() as bs_:
                        CAB = cx.sb(bs_, "CAB", [128, 2, 2, 8 * 144])
                        WST = cx.sb(bs_, "WST", [128, 8, 2, 2, 2, 64], BF16)
                        TF = cx.sb(bs_, "TF", [128, 16, 128], BF16)
                        TB = cx.sb(bs_, "TB", [128, 16, 128], BF16)
                        CABb = cx.sb(bs_, "CABb", [128, 2, 2, 8 * 144], BF16)

                        with ExitStack() as tmp:
                            WT = cx.sb(tmp, "WT", [128, 2, 2, 8 * 128])
                            KSB = cx.sb(tmp, "KSB", [16, 2, 16, 128], BF16)
                            c1 = cx.sb(tmp, "c1", [128, 128])
                            c2 = cx.sb(tmp, "c2", [128, 128])
                            c1v = v3(c1, 0, 128, 8, 16, 16, 1)
                            c2v = v3(c2, 0, 128, 8, 16, 16, 1)
                            for d_ in range(2):
                                for idx in range(9):
                                    p_ = idx if d_ == 0 else 8 - idx
                                    pr = PW.view((0 * 9 + p_) * 64 + d_ * 32 + g0, [[1152, 128], [1, 8], [0, 16]])
                                    pi_ = PW.view((1 * 9 + p_) * 64 + d_ * 32 + g0, [[1152, 128], [1, 8], [0, 16]])
                                    o_re = CAB.view((0 * 2 + d_) * 1152 + idx * 16, [[4608, 128], [144, 8], [1, 16]])
                                    o_im = CAB.view((1 * 2 + d_) * 1152 + idx * 16, [[4608, 128], [144, 8], [1, 16]])
                                    s.tt(c1v, prm(cprm, 0, g0, 8), pr, ALU.mult)
                                    s.tt(c2v, prm(cprm, 1, g0, 8), pi_, ALU.mult)
                                    s.tt(o_re, c1v, c2v, ALU.subtract)
                                    s.tt(c1v, prm(cprm, 0, g0, 8), pi_, ALU.mult)
                                    s.tt(c2v, prm(cprm, 1, g0, 8), pr, ALU.mult)
                                    s.stt(o_im, c1v, -1.0, c2v, ALU.mult, ALU.subtract)
                                for ss in range(8):
                                    p_ = 7 - ss if d_ == 0 else ss
                                    pr = PW.view((0 * 9 + p_) * 64 + d_ * 32 + g0, [[1152, 128], [1, 8], [0, 16]])
                                    pi_ = PW.view((1 * 9 + p_) * 64 + d_ * 32 + g0, [[1152, 128], [1, 8], [0, 16]])
                                    o_re = WT.view((d_ * 2 + 0) * 1024 + ss * 16, [[4096, 128], [128, 8], [1, 16]])
                                    o_im = WT.view((d_ * 2 + 1) * 1024 + ss * 16, [[4096, 128], [128, 8], [1, 16]])
                                    s.tt(c1v, bb(0, d_, g0, 8), pr, ALU.mult)
                                    s.tt(c2v, bb(1, d_, g0, 8), pi_, ALU.mult)
                                    s.tt(o_re, c1v, c2v, ALU.subtract)
                                    s.tt(c1v, bb(1, d_, g0, 8), pr, ALU.mult)
                                    s.tt(c2v, bb(0, d_, g0, 8), pi_, ALU.mult)
                                    s.tt(o_im, c1v, c2v, ALU.add)
                            s.copy(CABb.full(), CAB.full(), eng="pool")
                            for gh in range(2):
                                p0 = gh * 64
                                for gq in range(8):
                                    bk = nbank()
                                    for d_ in range(2):
                                        for ri in range(2):
                                            sl = d_ * 2 + ri
                                            s.transpose(bk[:, sl * 64:(sl + 1) * 64],
                                                        WT.view(p0 * 4096 + (d_ * 2 + ri) * 1024 + gq * 128, [[4096, 64], [1, 128]]),
                                                        cst[p0:p0 + 64, 0, p0:p0 + 64])
                                    s.copy(WST.view(((gq * 2 + gh) * 4) * 64, [[4096, 128], [1, 256]]), bk[:, 0:256], eng="act")
                                for d_ in range(2):
                                    for gqq in range(2):
                                        bk = nbank()
                                        for q4 in range(4):
                                            gq = gqq * 4 + q4
                                            i0 = 0 if d_ == 0 else 1
                                            s.mm(bk[0:16, q4 * 128:(q4 + 1) * 128],
                                                 [(BB.view(p0 * 2048 + (0 * 2 + d_) * 512 + (g0 + gq) * 16, [[2048, 64], [1, 16]]),
                                                   CAB.view(p0 * 4608 + (0 * 2 + d_) * 1152 + gq * 144 + i0 * 16, [[4608, 64], [1, 128]])),
                                                  (BB.view(p0 * 2048 + (1 * 2 + d_) * 512 + (g0 + gq) * 16, [[2048, 64], [1, 16]]),
                                                   CAB.view(p0 * 4608 + (1 * 2 + d_) * 1152 + gq * 144 + i0 * 16, [[4608, 64], [1, 128]]))])
                                        s.copy(KSB.view(d_ * 2048 + (2 * gqq * 4 + gh) * 128, [[4096, 16], [256, 4], [1, 128]]),
                                               bk.view(0, [[512, 16], [128, 4], [1, 128]]), eng="act")
                            gbase = 16 * b
                            s.dma(KFP.view(gbase * 3840 + 7 * 16, [[240, 16], [3840, 16], [1, 128]]), KSB[:, 0, :, :])
                            s.dma(KBR.view(gbase * 3840, [[240, 16], [3840, 16], [1, 128]]), KSB[:, 1, :, :])
                            s.dma(KFP.view(gbase * 3840, [[240, 16], [3840, 16], [1, 112]]), zt_.full())
                            s.dma(KBR.view(gbase * 3840 + 128, [[240, 16], [3840, 16], [1, 112]]), zt_.full())
                            for ss in range(8):
                                s.dma(TF[ss * 16:(ss + 1) * 16, :, :], KFP.view(gbase * 3840 + (7 - ss) * 16, [[240, 16], [3840, 16], [1, 128]]))
                                s.dma(TB[ss * 16:(ss + 1) * 16, :, :], KBR.view(gbase * 3840 + (7 - ss) * 16, [[240, 16], [3840, 16], [1, 128]]))
                            s.flush()

                        if L1S in ('c', 'c1'):
                            continue
                        u8b = cx.sb(bs_, "u8b", [128, 8, 256])
                        u8g = cx.sb(bs_, "u8g", [128, 16, 128])
                        U8T = cx.sb(bs_, "U8T", [128, 16, 288], BF16)
                        SSb = [cx.sb(bs_, "SSb%d" % i, [128, 16, 256], BF16) for i in range(2)]
                        NCOL = 326
                        PS = 16 * NCOL
                        SSD = [cx.sb(bs_, "SS%d" % i, [128, 8, 2, NCOL]) for i in range(2)]
                        CAR = [cx.sb(bs_, "CAR%d" % i, [128, 7, 8, 2]) for i in range(2)]
                        A36 = [cx.sb(bs_, "A36_%d" % i, [128, 8, 2]) for i in range(2)]
                        B36 = [cx.sb(bs_, "B36_%d" % i, [128, 8, 2]) for i in range(2)]
                        y8b = cx.sb(bs_, "y8b", [128, 8, 256])
                        ysb = cx.sb(bs_, "ysb", [128, 512])
                        TT1 = [cx.sb(bs_, "TT1_%d" % i, [128, 9, 8, 2]) for i in range(2)]
                        TT2 = [cx.sb(bs_, "TT2_%d" % i, [128, 9, 8, 2]) for i in range(2)]
                        for (j0, nj) in ((0, 32), (32, 128), (160, 128)):
                            s.dma(u8b[0:nj, :, :], U.view(8 * j0 * 1024 + 256 * b, [[8192, nj], [1024, 8], [1, 256]]))
                            s.copy(u8g.view(0, [[2048, nj], [128, 16], [16, 8], [1, 16]]),
                                   u8b.view(0, [[2048, nj], [16, 16], [256, 8], [1, 16]]), eng="act")
                            for gq4 in range(4):
                                bk = nbank()
                                for q4 in range(4):
                                    gi = gq4 * 4 + q4
                                    s.transpose(bk[:, q4 * 128:q4 * 128 + nj],
                                                u8g.view(128 * gi, [[2048, nj], [1, 128]]), cst[0:nj, 0, 0:nj])
                                s.copy(U8T.view(gq4 * 4 * 288 + j0, [[16 * 288, 128], [288, 4], [1, nj]]),
                                       bk.view(0, [[512, 128], [128, 4], [1, nj]]), eng="act")
                        if L1S in ('d', 'd1'):
                            s.flush()
                            continue
                        s.memset(SSD[0].view(0, [[PS, 128], [NCOL, 16], [1, 1]]), 0.0)
                        s.memset(SSD[0].view(289, [[PS, 128], [NCOL, 16], [1, 37]]), 0.0)
                        s.memset(SSD[1].view(288, [[PS, 128], [NCOL, 16], [1, 38]]), 0.0)
                        s.memset(SSD[0].view(289, [[PS, 128], [2 * NCOL, 8], [1, 1]]), 1.0)
                        s.memset(SSD[1].view(323, [[PS, 128], [2 * NCOL, 8], [1, 1]]), 1.0)
                        for gq in range(8):
                            for gh in range(2):
                                gi = 2 * gq + gh
                                p0 = gh * 64
                                for d_ in range(2):
                                    for ri in range(2):
                                        bk = nbank()
                                        s.mm(bk[p0:p0 + 64, 0:288],
                                             [(WST.view((((gq * 2 + gh) * 2 + d_) * 2 + ri) * 64, [[4096, 128], [1, 64]]),
                                               U8T[:, gi, :])])
                                        so = p0 * PS + (gq * 2 + ri) * NCOL
                                        if d_ == 0:
                                            s.copy(SSD[0].view(so + 1, [[PS, 64], [1, 288]]), bk[p0:p0 + 64, 0:288], eng="act")
                                        else:
                                            s.copy(SSD[1].view(so + 256, [[PS, 64], [1, 32]]), bk[p0:p0 + 64, 0:32], eng="act")
                                            s.copy(SSD[1].view(so, [[PS, 64], [1, 256]]), bk[p0:p0 + 64, 32:288], eng="act")
                        if L1S in ('e', 'e1'):
                            s.flush()
                            continue
                        DS = 8 * 2 * 289
                        RI, GQ = NCOL, 2 * NCOL
                        REC_ENG2 = os.environ.get('REC2', 'dve')

                        def cplx_step(items):
                            engs = ("dve", REC_ENG2)
                            for n_, (pv, psw, cv, ca, cb_, t1_, t2_) in enumerate(items):
                                s.tt(t1_, pv, ca, ALU.mult, eng=engs[n_ % 2])
                                s.tt(t2_, psw, cb_, ALU.mult, eng=engs[n_ % 2])
                            for n_, (pv, psw, cv, ca, cb_, t1_, t2_) in enumerate(items):
                                s.tt(t1_, t1_, t2_, ALU.add, eng=engs[n_ % 2])
                            for n_, (pv, psw, cv, ca, cb_, t1_, t2_) in enumerate(items):
                                if cv is not None:
                                    s.tt(cv, cv, t1_, ALU.add, eng=engs[n_ % 2])

                        def segv(SS, col, nseg):
                            return (SS.view(col, [[PS, 128], [36, nseg], [GQ, 8], [RI, 2]]),
                                    SS.view(col + RI, [[PS, 128], [36, nseg], [GQ, 8], [-RI, 2]]))

                        def coef(buf, d_, nseg):
                            return buf.view(d_ * 64 + g0 * 2, [[128, 128], [0, nseg], [2, 8], [1, 2]])

                        for k in range(1, 36):
                            items = []
                            for d_ in range(2):
                                pc = k if d_ == 0 else 36 - k
                                cc = k + 1 if d_ == 0 else 35 - k
                                pv, psw = segv(SSD[d_], pc, 9)
                                cv, _ = segv(SSD[d_], cc, 9)
                                items.append((pv, psw, cv, coef(LA, d_, 9), coef(LB, d_, 9), TT1[d_].full(), TT2[d_].full()))
                            cplx_step(items)
                        items = []
                        for d_ in range(2):
                            c35 = 324 if d_ == 0 else 288
                            pv, psw = segv(SSD[d_], c35, 1)
                            items.append((pv, psw, None, coef(LA, d_, 1), coef(LB, d_, 1),
                                          TT1[d_].view(0, [[144, 128], [16, 1], [2, 8], [1, 2]]),
                                          TT2[d_].view(0, [[144, 128], [16, 1], [2, 8], [1, 2]])))
                        cplx_step(items)
                        for d_ in range(2):
                            l36re = TT1[d_].view(0, [[144, 128], [2, 8], [0, 2]])
                            s.copy(A36[d_].full(), l36re)
                            s.ts(B36[d_][:, :, 0:1], TT1[d_].view(1, [[144, 128], [2, 8], [1, 1]]), -1.0, None, ALU.mult)
                            s.copy(B36[d_][:, :, 1:2], TT1[d_].view(1, [[144, 128], [2, 8], [1, 1]]))
                        for step in range(1, 8):
                            items = []
                            for d_ in range(2):
                                if d_ == 0:
                                    m = step
                                    cc, pc = 36 * m + 36, 36 * m
                                else:
                                    m = 7 - step
                                    cc, pc = 36 * m, 36 * m + 36
                                pv, psw = segv(SSD[d_], pc, 1)
                                cv, _ = segv(SSD[d_], cc, 1)
                                items.append((pv, psw, cv,
                                              A36[d_].view(0, [[16, 128], [0, 1], [2, 8], [1, 2]]),
                                              B36[d_].view(0, [[16, 128], [0, 1], [2, 8], [1, 2]]),
                                              TT1[d_].view(0, [[144, 128], [16, 1], [2, 8], [1, 2]]),
                                              TT2[d_].view(0, [[144, 128], [16, 1], [2, 8], [1, 2]])))
                            cplx_step(items)
                        items = []
                        for d_ in range(2):
                            pv, psw = segv(SSD[d_], 36, 7)
                            items.append((pv, psw, None, coef(LA, d_, 7), coef(LB, d_, 7),
                                          CAR[d_].full(), TT2[d_].view(0, [[144, 128], [16, 7], [2, 8], [1, 2]])))
                        cplx_step(items)
                        for d_ in range(2):
                            SS = SSD[d_]
                            sb0 = 37 if d_ == 0 else 1

                            def sview(ri):
                                return SS.view(sb0 + ri * RI, [[PS, 128], [36, 7], [GQ, 8], [1, 35]])

                            def tview(ri):
                                return SS.view(289 + ri * RI, [[PS, 128], [0, 7], [GQ, 8], [1, 35]])

                            def cview(ri):
                                return CAR[d_].view(ri, [[112, 128], [16, 7], [2, 8], [0, 35]])

                            w1 = (u8g if d_ == 0 else u8b).view(0, [[2048, 128], [280, 7], [35, 8], [1, 35]])
                            w2 = y8b.view(0, [[2048, 128], [280, 7], [35, 8], [1, 35]])
                            s.tt(w1, tview(0), cview(0), ALU.mult)
                            s.tt(w2, tview(1), cview(1), ALU.mult)
                            s.tt(w1, w1, w2, ALU.subtract)
                            s.tt(sview(0), sview(0), w1, ALU.add)
                            s.tt(w1, tview(0), cview(1), ALU.mult)
                            s.tt(w2, tview(1), cview(0), ALU.mult)
                            s.tt(w1, w1, w2, ALU.add)
                            s.tt(sview(1), sview(1), w1, ALU.add)
                        s.copy(SSb[0].full(), SSD[0].view(32, [[PS, 128], [NCOL, 16], [1, 256]]), eng="act")
                        s.copy(SSb[1].full(), SSD[1].view(1, [[PS, 128], [NCOL, 16], [1, 256]]), eng="pool")
                        if L1S in ('f', 'f1'):
                            s.flush()
                            continue
                        for tt_ in range(2):
                            j0 = 32 + 128 * tt_
                            m0 = 128 * tt_
                            for gh in range(2):
                                p0 = gh * 64
                                for gqq in range(2):
                                    bx = nbank()
                                    by = nbank()
                                    for q4 in range(4):
                                        gq = gqq * 4 + q4
                                        gi = 2 * gq + gh
                                        s.mm(bx[:, q4 * 128:(q4 + 1) * 128],
                                             [(U8T[:, gi, j0:j0 + 128], TF[:, gi, :]), (U8T[:, gi, j0:j0 + 128], TB[:, gi, :])])
                                        pairs = []
                                        for d_ in range(2):
                                            c0 = m0
                                            i0 = 1 if d_ == 0 else 0
                                            for ri in range(2):
                                                so = p0 * 4096 + (gq * 2 + ri) * 256 + c0
                                                pairs.append((SSb[d_].view(so, [[4096, 64], [1, 128]]),
                                                              CABb.view(p0 * 4608 + (ri * 2 + d_) * 1152 + gq * 144 + i0 * 16, [[4608, 64], [1, 128]])))
                                        s.mm(by[:, q4 * 128:(q4 + 1) * 128], pairs)
                                    s.copy(ysb.full(), by.full(), eng="act")
                                    s.tt(y8b.view(32 * gqq * 4 + 16 * gh, [[2048, 128], [32, 4], [256, 8], [1, 16]]),
                                         bx.view(0, [[512, 128], [128, 4], [16, 8], [1, 16]]),
                                         ysb.view(0, [[512, 128], [128, 4], [16, 8], [1, 16]]), ALU.add)
                            s.dma(YTOK.view((CTX + 8 * m0) * 1024 + 256 * b, [[8192, 128], [1024, 8], [1, 256]]), y8b.full())
                        s.flush()

            if L1S in ('g', 'g1'):
                return nc
            with ExitStack() as es:
                gw = [cx.sb(es, "gw%d" % k, [128, D], BF16) for k in range(8)]
                ow = [cx.sb(es, "ow%d" % k, [128, D], BF16) for k in range(8)]
                dskb = cx.sb(es, "dskb", [128, D])
                glbb = cx.sb(es, "glbb", [128, D])
                fnwb = cx.sb(es, "fnwb", [128, D])
                kg = cx.sb(es, "kg", [128, 1])
                s.memset(kg.full(), 2.0 * math.sqrt(2.0 / math.pi))
                for k in range(8):
                    s.dma(gw[k].full(), o_glu_w[k * 128:(k + 1) * 128, :], q="pool")
                    s.dma(ow[k].full(), o_w_out[k * 128:(k + 1) * 128, :], q="pool")
                s.dma(dskb.full(), o_d_skip.view(0, [[0, 128], [1, D]]))
                s.dma(glbb.full(), o_glu_b.view(0, [[0, 128], [1, D]]))
                s.dma(fnwb.full(), final_norm_w.view(0, [[0, 128], [1, D]]))
                NB3 = 3
                ya = [cx.sb(es, "ya%d" % i, [128, D]) for i in range(NB3)]
                ua = [cx.sb(es, "ua%d" % i, [128, D]) for i in range(NB3)]
                sga = [cx.sb(es, "sga%d" % i, [128, D]) for i in range(NB3)]
                xa = [cx.sb(es, "xa%d" % i, [128, D]) for i in range(NB3)]
                w1s = [cx.sb(es, "w1_%d" % i, [128, D]) for i in range(NB3)]
                w2s = [cx.sb(es, "w2_%d" % i, [128, D]) for i in range(NB3)]
                w3s = [cx.sb(es, "w3_%d" % i, [128, D]) for i in range(NB3)]
                tTs = [cx.sb(es, "tT_%d" % i, [128, 8, 128], BF16) for i in range(2 * NB3)]
                sts = [cx.sb(es, "st10_%d" % i, [128, 4]) for i in range(NB3)]

                def transp8(src, tT):
                    for half in range(2):
                        bk = nbank()
                        for kk in range(4):
                            k = half * 4 + kk
                            s.transpose(bk[:, kk * 128:(kk + 1) * 128], src[:, k * 128:(k + 1) * 128], ident)
                        s.copy(tT[:, half * 4:(half + 1) * 4, :], bk.view(0, [[512, 128], [128, 4], [1, 128]]), eng="act")

                TAILN = int(os.environ.get('TAILN', NT))

                def bufs(i):
                    b_ = i % NB3
                    return ya[b_], ua[b_], sga[b_], xa[b_], w1s[b_], w2s[b_], w3s[b_], tTs[2 * b_], tTs[2 * b_ + 1], sts[b_]

                def stage0(i):
                    y_, u_, g_, x_, w1, w2, w3, tTa, tTb, st = bufs(i)
                    s.dma(y_.full(), YTOK[i * 128:(i + 1) * 128, :])
                    s.dma(u_.full(), U[i * 128:(i + 1) * 128, :])
                    s.dma(g_.full(), SG1[i * 128:(i + 1) * 128, :])
                    s.dma(x_.full(), X1[i * 128:(i + 1) * 128, :])
                    s.tt(w1.full(), u_.full(), dskb.full(), ALU.mult)
                    s.tt(y_.full(), y_.full(), w1.full(), ALU.add)
                    s.tt(w1.full(), y_.full(), y_.full(), ALU.mult)
                    s.ts(w1.full(), w1.full(), 0.044715, 1.0, ALU.mult, ALU.add)
                    s.tt(w1.full(), w1.full(), y_.full(), ALU.mult)
                    s.act(w1.full(), w1.full(), AF.Sigmoid, scale=kg[:, 0:1])
                    s.tt(w2.full(), y_.full(), w1.full(), ALU.mult)
                    transp8(w2, tTa)

                def stage1(i):
                    y_, u_, g_, x_, w1, w2, w3, tTa, tTb, st = bufs(i)
                    for half in range(2):
                        bk = nbank()
                        s.mm(bk.full(), [(tTa[:, k, :], gw[k][:, half * 512:(half + 1) * 512]) for k in range(8)])
                        s.tt(w1[:, half * 512:(half + 1) * 512], bk.full(), glbb[:, half * 512:(half + 1) * 512], ALU.add)
                    s.act(w1.full(), w1.full(), AF.Sigmoid)
                    s.tt(w2.full(), w2.full(), w1.full(), ALU.mult)
                    s.tt(w2.full(), w2.full(), g_.full(), ALU.mult)
                    transp8(w2, tTb)

                def stage2(i):
                    y_, u_, g_, x_, w1, w2, w3, tTa, tTb, st = bufs(i)
                    for half in range(2):
                        bk = nbank()
                        s.mm(bk.full(), [(tTb[:, k, :], ow[k][:, half * 512:(half + 1) * 512]) for k in range(8)])
                        s.tt(w1[:, half * 512:(half + 1) * 512], bk.full(), gate_bc[1][0][:, half * 512:(half + 1) * 512], ALU.mult)
                    s.tt(w3.full(), w1.full(), x_.full(), ALU.add)
                    s.act(w1.full(), w3.full(), AF.Square, accum=st[:, 0:1])
                    s.ts(st[:, 1:2], st[:, 0:1], 1.0 / D, EPS, ALU.mult, ALU.add)
                    s.act(st[:, 2:3], st[:, 1:2], AF.Sqrt)
                    s.recip(st[:, 3:4], st[:, 2:3])
                    s.act(w3.full(), w3.full(), AF.Copy, scale=st[:, 3:4])
                    s.tt(w2.full(), w3.full(), fnwb.full(), ALU.mult)
                    s.dma(out_t[(i - 2) * 128:(i - 1) * 128, :], w2.full())

                pipeline(list(range(2, TAILN)), [stage0, stage1, stage2])
                s.flush()

    return nc


def _consts():
    c = np.zeros((128, 6, 512), np.float32)
    j = np.arange(128)[:, None]
    l = np.arange(128)[None, :]
    c[:, 0, :128] = np.eye(128, dtype=np.float32)
    c[:, 1, :128] = (j <= l)
    c[:, 2, :128] = (j >= l)
    c[:, 3, :] = 1.0
    nf = np.where(l < j, -30000.0, 0.0).astype(np.float32)
    nb = np.where(l > j, -30000.0, 0.0).astype(np.float32)
    c[:, 4, :] = np.tile(nf, (1, 4))
    c[:, 5, :] = np.tile(nb, (1, 4))
    return c


def _rope_tables():
    rows = L // 64
    row = np.repeat(np.arange(rows, dtype=np.float32), 64)
    col = np.tile(np.arange(64, dtype=np.float32), rows)
    n_freq = 16
    inv = (np.float32(10000.0) ** (-np.arange(n_freq, dtype=np.float32) / n_freq)).astype(np.float32)
    ang = np.concatenate([row[:, None] * inv, col[:, None] * inv], axis=-1).astype(np.float32)
    cos = np.cos(ang).astype(np.float32)
    sin = np.sin(ang).astype(np.float32)
    cosT = np.zeros((128, L), np.float32)
    sinT = np.zeros((128, L), np.float32)
    for h2 in range(2):
        for half in range(2):
            p0 = h2 * 64 + half * 32
            cosT[p0:p0 + 32] = cos.T
            sinT[p0:p0 + 32] = (-sin.T if half == 0 else sin.T)
    return np.stack([cosT, sinT], axis=1)


def _vecT(v, nchunk):
    return np.ascontiguousarray(np.asarray(v, np.float32).reshape(nchunk, 128).T)


def prep_inputs(b, inp):
    f = lambda a: np.ascontiguousarray(np.asarray(a, np.float32))
    m = {}
    m["xin"] = f(np.concatenate([inp["ctx"][b], inp["x"][b]], axis=0))
    cv = np.stack([inp["c"][b], inp["c_ctx"]], axis=0)
    m["cvecT"] = f(cv.reshape(2, 8, 128).transpose(2, 0, 1))
    m["consts"] = _consts()
    m["rope"] = _rope_tables()
    m["e_ada_w"] = f(inp["e_ada_w"][0])
    m["e_ada_b"] = f(inp["e_ada_b"][0]).reshape(1, -1)
    m["e_norm_wT"] = _vecT(inp["e_norm_w"][0], 8)
    w = f(inp["e_w_in"][0])
    q = w[:, OFF_Q:OFF_Q + 1024].reshape(D, 16, 2, 32)
    qs = q[:, :, ::-1, :].reshape(D, 1024)
    k = w[:, OFF_KV:OFF_KV + 256].reshape(D, 4, 64)
    kr = np.concatenate([k, k], axis=2).reshape(D, 512)
    ks = k.reshape(D, 4, 2, 32)[:, :, ::-1, :].reshape(D, 4, 64)
    ksr = np.concatenate([ks, ks], axis=2).reshape(D, 512)
    m["e_w_in"] = f(np.concatenate([w, qs, kr, ksr], axis=1))
    cw = f(inp["e_conv_w"][0])
    m["e_conv_wT"] = f(cw.reshape(5, 12, 128).transpose(2, 1, 0))
    m["e_conv_bT"] = _vecT(inp["e_conv_b"][0], 12)
    m["e_dt_bias"] = f(inp["e_dt_bias"][0]).reshape(1, 32)
    m["e_a_log"] = f(inp["e_a_log"][0]).reshape(1, 32)
    m["e_d_skip"] = f(inp["e_d_skip"][0]).reshape(1, 16)
    m["e_ssd_norm_wT"] = _vecT(inp["e_ssd_norm_w"][0], 8)
    sk = f(inp["e_sink"][0]).reshape(8, 2)
    m["e_sink"] = f(np.repeat(sk.T[:, None, :], 64, axis=1).reshape(128, 8))
    m["e_w_out"] = f(inp["e_w_out"][0])
    m["o_ada_w"] = f(inp["o_ada_w"][0])
    m["o_ada_b"] = f(inp["o_ada_b"][0]).reshape(1, -1)
    m["o_norm_wT"] = _vecT(inp["o_norm_w"][0], 8)
    m["o_w_in"] = f(inp["o_w_in"][0])

    def gl(a):
        a = np.asarray(a, np.float32)
        rest = a.shape[2:]
        a = a.reshape((32, 2, 64) + rest)
        a = np.moveaxis(a, 0, 2)
        return a.reshape((128, 32) + rest)

    lam = np.zeros((128, 2, 3, 32), np.float32)
    for d_ in range(2):
        lam[:, d_, 0] = gl(inp["o_lam_re"][0][d_])
        lam[:, d_, 1] = gl(inp["o_lam_im"][0][d_])
        lam[:, d_, 2] = gl(np.repeat(np.asarray(inp["o_log_step"][0][d_])[:, None], 64, axis=1))
    m["s5_lam"] = f(lam)
    m["s5_b"] = f(np.stack([gl(inp["o_b_re"][0]), gl(inp["o_b_im"][0])], axis=1))
    cr = np.asarray(inp["o_c_re"][0]).transpose(0, 2, 1)
    ci = np.asarray(inp["o_c_im"][0]).transpose(0, 2, 1)
    m["s5_c"] = f(np.stack([gl(cr), gl(ci)], axis=1))
    m["o_d_skip"] = f(inp["o_d_skip"][0]).reshape(1, -1)
    m["o_glu_w"] = f(inp["o_glu_w"][0])
    m["o_glu_b"] = f(inp["o_glu_b"][0]).reshape(1, -1)
    m["o_w_out"] = f(inp["o_w_out"][0])
    m["final_norm_w"] = f(inp["final_norm_w"]).reshape(1, -1)
    return m


def kernel(**inputs):
    nc = build_program()
    in_maps = [prep_inputs(b, inputs) for b in range(8)]
    res = run_bass_kernel_spmd(nc, in_maps, core_ids=list(range(8)))
    return np.stack([r["out"] for r in res.results], axis=0)
```

```python
import math
import os
from contextlib import ExitStack

import numpy as np
import concourse.bass as bass
import concourse.mybir as mybir
from concourse.bass_utils import run_bass_kernel_spmd

F32 = mybir.dt.float32
BF16 = mybir.dt.bfloat16
AF = mybir.ActivationFunctionType
ALU = mybir.AluOpType

D = 1024
T = 2304
NT = 18
CTX = 256
L = 2048
EPS = 1e-6
TG = [(0, 256), (256, 512), (768, 512), (1280, 512), (1792, 512)]

SES_ALL = os.environ.get('SES', '0') == '1'
SAME_ENGINE_SYNC = {'act': SES_ALL, 'dve': SES_ALL, 'pool': True, 'pe': False, 'sp': True}
SEM_EPOCH = 30000


class V:
    __slots__ = ("buf", "ap")

    def __init__(self, buf, ap):
        self.buf = buf
        self.ap = ap


class Buf:
    def __init__(self, name, h):
        self.name = name
        self.h = h
        self.last_w = None
        self.readers = []
        self.is_psum = False

    def __getitem__(self, idx):
        return V(self, self.h[idx])

    def full(self):
        return V(self, self.h.ap())

    def view(self, offset, pattern):
        return V(self, bass.AP(self.h, offset, [list(p) for p in pattern]))


class Sched:
    ENG = ("pe", "act", "dve", "pool", "sp")

    def __init__(self, nc):
        self.nc = nc
        self.prog = {e: [] for e in self.ENG}
        self.sem = {}
        self.cnt = {}
        self.semid = 0
        self.known = {e: {} for e in self.ENG}
        for e in ("pe", "act", "dve", "pool"):
            self._new_engine_sem(e)
        self.nds = 8
        self.dsem = {}
        self.duse = {}
        self.dcnt = {}
        for q in ("sp", "pool"):
            self.dsem[q] = []
            self.duse[q] = []
            for i in range(self.nds):
                key = "d_%s_%d" % (q, i)
                self.dsem[q].append((nc.alloc_semaphore(key), key))
                self.duse[q].append(0)
            self.dcnt[q] = 0
        self.n_ops = 0

    def _new_engine_sem(self, e):
        self.semid += 1
        key = "s_%s_%d" % (e, self.semid)
        self.sem[e] = (self.nc.alloc_semaphore(key), key)
        self.cnt[e] = 0

    def _deps(self, reads, writes):
        deps = {}

        def add(tok):
            if tok is None:
                return
            h, key, val = tok
            if key not in deps or deps[key][1] < val:
                deps[key] = (h, val)

        for r in reads:
            add(r.buf.last_w)
            if r.buf.is_psum:
                for t in r.buf.readers:
                    add(t)
        for w in writes:
            add(w.buf.last_w)
            for t in w.buf.readers:
                add(t)
        return deps

    def _emit_waits(self, eng, deps, own_key=None):
        kn = self.known[eng]
        for key, (h, val) in deps.items():
            if key == own_key and not SAME_ENGINE_SYNC[eng]:
                continue
            if kn.get(key, 0) >= val:
                continue
            kn[key] = val
            self.prog[eng].append(("wait", h, val))

    def _update(self, tok, reads, writes):
        for w in writes:
            w.buf.last_w = tok
            w.buf.readers = []
        for r in reads:
            if r.buf.last_w is not tok:
                r.buf.readers.append(tok)

    def op(self, eng, fn, reads=(), writes=()):
        reads = [r for r in reads if r is not None]
        writes = list(writes)
        if self.cnt[eng] >= SEM_EPOCH:
            self._new_engine_sem(eng)
        h, key = self.sem[eng]
        own = None if eng == "pe" else key
        deps = self._deps(reads, writes)
        if eng == "pe":
            deps.pop(key, None)
        self._emit_waits(eng, deps, own_key=own)
        self.cnt[eng] += 1
        self.prog[eng].append(("op", fn, h, 1))
        tok = (h, key, self.cnt[eng])
        self._update(tok, reads, writes)
        self.n_ops += 1
        return tok

    def dma(self, out, in_, q="sp", **kw):
        deps = self._deps([in_], [out])
        self._emit_waits(q, deps)
        k = self.dcnt[q] % self.nds
        self.dcnt[q] += 1
        h, key = self.dsem[q][k]
        prev = 16 * self.duse[q][k]
        if prev > 0 and self.known[q].get(key, 0) < prev:
            self.known[q][key] = prev
            self.prog[q].append(("wait", h, prev))
        self.duse[q][k] += 1
        val = 16 * self.duse[q][k]
        o_ap, i_ap = out.ap, in_.ap
        self.prog[q].append(("op", lambda e: e.dma_start(out=o_ap, in_=i_ap, **kw), h, 16))
        tok = (h, key, val)
        self._update(tok, [in_], [out])
        self.n_ops += 1
        return tok

    def finish_dmas(self):
        for q in ("sp", "pool"):
            for k in range(self.nds):
                h, key = self.dsem[q][k]
                val = 16 * self.duse[q][k]
                if val > 0 and self.known[q].get(key, 0) < val:
                    self.known[q][key] = val
                    self.prog[q].append(("wait", h, val))

    def flush(self, name=None):
        self.finish_dmas()
        nc = self.nc
        prog = self.prog
        self.prog = {e: [] for e in self.ENG}

        def run(items, e):
            for it in items:
                if it[0] == "wait":
                    e.wait_ge(it[1], it[2])
                else:
                    inst = it[1](e)
                    inst.then_inc(it[2], it[3])

        with nc.Block() as block:
            if prog["sp"]:
                @block.sync
                def _(e):
                    run(prog["sp"], e)
            if prog["act"]:
                @block.scalar
                def _(e):
                    run(prog["act"], e)
            if prog["dve"]:
                @block.vector
                def _(e):
                    run(prog["dve"], e)
            if prog["pool"]:
                @block.gpsimd
                def _(e):
                    run(prog["pool"], e)
            if prog["pe"]:
                @block.tensor
                def _(e):
                    run(prog["pe"], e)

    def mm(self, out, pairs):
        n = len(pairs)

        def fn(e):
            inst = None
            for i, (l, r) in enumerate(pairs):
                inst = e.matmul(out.ap, l.ap, r.ap, start=(i == 0), stop=(i == n - 1))
            return inst

        self.op("pe", fn, reads=[p[0] for p in pairs] + [p[1] for p in pairs], writes=[out])

    def transpose(self, out, in_, ident):
        self.op("pe", lambda e: e.transpose(out.ap, in_.ap, ident.ap), reads=[in_, ident], writes=[out])

    def act(self, out, in_, func, bias=None, scale=None, accum=None):
        kw = {}
        reads = [in_]
        writes = [out]
        if bias is not None:
            if isinstance(bias, V):
                kw["bias"] = bias.ap
                reads.append(bias)
            else:
                kw["bias"] = bias
        if scale is not None:
            if isinstance(scale, V):
                kw["scale"] = scale.ap
                reads.append(scale)
            else:
                kw["scale"] = scale
        if accum is not None:
            kw["accum_out"] = accum.ap
            writes.append(accum)
        self.op("act", lambda e: e.activation(out.ap, in_.ap, func, **kw), reads=reads, writes=writes)

    def ts(self, out, in0, s1, s2, op0, op1=None, eng="dve"):
        reads = [in0]
        a1 = s1
        a2 = s2
        if isinstance(s1, V):
            reads.append(s1)
            a1 = s1.ap
        if isinstance(s2, V):
            reads.append(s2)
            a2 = s2.ap
        if op1 is None:
            self.op(eng, lambda e: e.tensor_scalar(out.ap, in0.ap, a1, a2, op0), reads=reads, writes=[out])
        else:
            self.op(eng, lambda e: e.tensor_scalar(out.ap, in0.ap, a1, a2, op0, op1), reads=reads, writes=[out])

    def tt(self, out, in0, in1, op, eng="dve"):
        self.op(eng, lambda e: e.tensor_tensor(out.ap, in0.ap, in1.ap, op), reads=[in0, in1], writes=[out])

    def stt(self, out, in0, scalar, in1, op0, op1):
        reads = [in0, in1]
        sc = scalar
        if isinstance(scalar, V):
            reads.append(scalar)
            sc = scalar.ap
        self.op("dve", lambda e: e.scalar_tensor_tensor(out.ap, in0.ap, sc, in1.ap, op0, op1),
                reads=reads, writes=[out])

    def copy(self, out, in_, eng="dve"):
        if eng == "act":
            self.op("act", lambda e: e.copy(out.ap, in_.ap), reads=[in_], writes=[out])
        else:
            self.op(eng, lambda e: e.tensor_copy(out.ap, in_.ap), reads=[in_], writes=[out])

    def recip(self, out, in_):
        self.op("dve", lambda e: e.reciprocal(out.ap, in_.ap), reads=[in_], writes=[out])

    def memset(self, out, val, eng="dve"):
        self.op(eng, lambda e: e.memset(out.ap, val), reads=[], writes=[out])


class Ctx:
    def __init__(self, nc, sched):
        self.nc = nc
        self.s = sched
        self.uid = 0

    def sb(self, es, name, shape, dtype=F32):
        self.uid += 1
        h = es.enter_context(self.nc.sbuf_tensor("%s_%d" % (name, self.uid), list(shape), dtype))
        return Buf(name, h)

    def ps(self, es, name, shape=(128, 512), dtype=F32):
        self.uid += 1
        h = es.enter_context(self.nc.psum_tensor("%s_%d" % (name, self.uid), list(shape), dtype))
        b = Buf(name, h)
        b.is_psum = True
        return b

    def dram(self, name, shape, dtype=F32, kind="Internal"):
        h = self.nc.dram_tensor(name, list(shape), dtype, kind=kind)
        return Buf(name, h)


def pipeline(items, stages):
    n, k = len(items), len(stages)
    for t in range(n + k - 1):
        for j in range(k - 1, -1, -1):
            i = t - j
            if 0 <= i < n:
                stages[j](items[i])


def bc_mid(v_buf, base_off, pstep, nparts, n_outer, outer_step, n_inner):
    return v_buf.view(base_off, [[pstep, nparts], [outer_step, n_outer], [0, n_inner]])


E_NCOL = 5152
OFF_Z = 0
OFF_XBC = 1024
OFF_DT = 2560
OFF_Q = 2592
OFF_KV = 3616
OFF_G = 4128
OFF_QS = 5152
OFF_KR = 6176
OFF_KSR = 6688
E_NCOL_EXT = 7200


ORDER = ["p1", "p2a", "p2b", "p2c", "p2d", "p2e", "p2f", "p2g", "p2h", "p3", "p4", "p5", "all"]


def build_program(debug=(), stop="all"):
    def go(tag):
        return ORDER.index(tag) <= ORDER.index(stop)
    nc = bass.Bass("TRN2", target_bir_lowering=False)
    s = Sched(nc)
    cx = Ctx(nc, s)
    dbg = set(debug)

    def din(name, shape):
        return Buf(name, nc.dram_tensor(name, list(shape), F32, kind="ExternalInput"))

    def dout(name, shape):
        return Buf(name, nc.dram_tensor(name, list(shape), F32, kind="ExternalOutput"))

    def scratch(name, shape, dtype=F32):
        if name in dbg:
            return dout(name, shape)
        return Buf(name, nc.dram_tensor(name, list(shape), dtype))

    xin = din("xin", [T, D])
    cvecT = din("cvecT", [128, 2, 8])
    consts = din("consts", [128, 6, 512])
    rope = din("rope", [128, 2, L])
    e_ada_w = din("e_ada_w", [D, 3 * D])
    e_ada_b = din("e_ada_b", [1, 3 * D])
    e_norm_wT = din("e_norm_wT", [128, 8])
    e_w_in = din("e_w_in", [D, E_NCOL_EXT])
    e_conv_wT = din("e_conv_wT", [128, 12, 5])
    e_conv_bT = din("e_conv_bT", [128, 12])
    e_dt_bias = din("e_dt_bias", [1, 32])
    e_a_log = din("e_a_log", [1, 32])
    e_d_skip = din("e_d_skip", [1, 16])
    e_ssd_norm_wT = din("e_ssd_norm_wT", [128, 8])
    e_sink = din("e_sink", [128, 8])
    e_w_out = din("e_w_out", [2 * D, D])
    o_ada_w = din("o_ada_w", [D, 3 * D])
    o_ada_b = din("o_ada_b", [1, 3 * D])
    o_norm_wT = din("o_norm_wT", [128, 8])
    o_w_in = din("o_w_in", [D, 2 * D])
    s5_lam = din("s5_lam", [128, 2, 3, 32])
    s5_b = din("s5_b", [128, 2, 32, 16])
    s5_c = din("s5_c", [128, 2, 32, 16])
    o_d_skip = din("o_d_skip", [1, D])
    o_glu_w = din("o_glu_w", [D, D])
    o_glu_b = din("o_glu_b", [1, D])
    o_w_out = din("o_w_out", [D, D])
    final_norm_w = din("final_norm_w", [1, D])
    out_t = dout("out", [L, D])

    XS = scratch("XS", [T, 1024])
    BTOK = scratch("BTOK", [T, 256], BF16)
    BT = scratch("BT", [2, 128, T], BF16)
    CT = scratch("CT", [2, 128, T], BF16)
    SZ = scratch("SZ", [T, 1024])
    QR = scratch("QR", [8, 128, L], BF16)
    QC = scratch("QC", [8, 128, CTX], BF16)
    KR = scratch("KR", [4, 128, L], BF16)
    KC = scratch("KC", [4, 128, CTX], BF16)
    VT = scratch("VT", [T, 256], BF16)
    SG = scratch("SG", [8, 128, T])
    YF = scratch("YF", [T, 1024])
    YT = scratch("YT", [16, 128, T], BF16)
    X1 = scratch("X1", [T, 1024])
    U = scratch("U", [T, 1024])
    SG1 = scratch("SG1", [T, 1024])
    YTOK = scratch("YTOK", [T, 1024])
    KFP = scratch("KFP", [64, 16, 15, 16], BF16)
    KBR = scratch("KBR", [64, 16, 15, 16], BF16)
    HT = scratch("HT", [8, 128, T]) if "HT" in dbg else None
    DTD = scratch("DTD", [T, 32]) if "DTD" in dbg else None
    MODD = scratch("MODD", [4, 128, 24]) if "MODD" in dbg else None

    with ExitStack() as top:
        banks = [cx.ps(top, "bank%d" % i) for i in range(8)]
        cst = cx.sb(top, "cst", [128, 6, 512])
        s.dma(cst.full(), consts.full())
        ident = cst[:, 0, 0:128]
        tri = cst[:, 1, 0:128]
        utri = cst[:, 2, 0:128]
        ones = cst[:, 3, 0:128]
        onesb_t = cx.sb(top, "onesb", [128, 128], BF16)
        s.memset(onesb_t.full(), 1.0)
        onesb = onesb_t.full()
        modT = [[cx.sb(top, "modT%d%d" % (l, w), [128, 24]) for w in range(2)] for l in range(2)]
        gate_bc = [[cx.sb(top, "gate%d%d" % (l, w), [128, 1024]) for w in range(2)] for l in range(2)]
        scs = cx.sb(top, "scs", [128, 2, 8])

        def adaln_phase(layer, ada_w, ada_b):
            with ExitStack() as es:
                aw = [cx.sb(es, "aw%d" % k, [128, 3 * D]) for k in range(8)]
                ab = cx.sb(es, "ab", [1, 3 * D])
                modrow = [cx.sb(es, "modrow%d" % w, [1, 3 * D]) for w in range(2)]
                if layer == 0:
                    cv = cx.sb(es, "cv", [128, 2, 8])
                    s.dma(cv.full(), cvecT.full())
                    s.act(scs.full(), cv.full(), AF.Silu)
                for k in range(8):
                    s.dma(aw[k].full(), ada_w[k * 128:(k + 1) * 128, :])
                s.dma(ab.full(), ada_b.full())
                bi = 0
                for w in range(2):
                    for fg in range(6):
                        bk = banks[bi % 8]
                        bi += 1
                        s.mm(bk[0:1, :], [(scs[:, w, k:k + 1], aw[k][:, fg * 512:(fg + 1) * 512]) for k in range(8)])
                        s.tt(modrow[w][0:1, fg * 512:(fg + 1) * 512], bk[0:1, :], ab[0:1, fg * 512:(fg + 1) * 512], ALU.add)
                for w in range(2):
                    bk = banks[bi % 8]
                    bi += 1
                    for fc in range(24):
                        s.mm(bk[:, 2 * fc:2 * fc + 2], [(modrow[w][0:1, fc * 128:(fc + 1) * 128], cst[0:1, 3, 0:2])])
                    s.copy(modT[layer][w].full(), bk.view(0, [[512, 128], [2, 24]]))
                    for hh in range(2):
                        bk2 = banks[bi % 8]
                        bi += 1
                        s.mm(bk2.full(), [(cst[0:1, 3, 0:128], modrow[w][0:1, 2048 + hh * 512:2048 + (hh + 1) * 512])])
                        s.copy(gate_bc[layer][w][:, hh * 512:(hh + 1) * 512], bk2.full(), eng="act")
                    if MODD is not None:
                        s.dma(MODD[layer * 2 + w], modT[layer][w].full())
                s.flush()

        adaln_phase(0, e_ada_w, e_ada_b)

        with ExitStack() as l0:
            DT = cx.sb(l0, "DT", [128, NT, 32])
            DTA = cx.sb(l0, "DTA", [128, NT, 32])
            nw = cx.sb(l0, "nw", [128, 8])
            sc1 = [cx.sb(l0, "sc1_%d" % w, [128, 8]) for w in range(2)]
            s.dma(nw.full(), e_norm_wT.full())
            for w in range(2):
                s.stt(sc1[w].full(), modT[0][w][:, 8:16], 1.0, nw.full(), ALU.add, ALU.mult)

            wo = [cx.sb(l0, "wo%d" % k, [128, D], BF16) for k in range(16)]
            hts = ExitStack()
            hT = [cx.sb(hts, "hT%d" % k, [128, T], BF16) for k in range(8)]
            with ExitStack() as es:
                xt = [cx.sb(es, "xt%d" % i, [128, D]) for i in range(3)]
                xn = [cx.sb(es, "xn%d" % i, [128, D]) for i in range(3)]
                junk = cx.sb(es, "junk", [128, D])
                st = [cx.sb(es, "st%d" % i, [128, 4]) for i in range(3)]
                def n0(i):
                    x_, n_, st_ = xt[i % 3], xn[i % 3], st[i % 3]
                    s.dma(x_.full(), xin[i * 128:(i + 1) * 128, :])
                    s.act(junk.full(), x_.full(), AF.Square, accum=st_[:, 0:1])
                    s.ts(st_[:, 1:2], st_[:, 0:1], 1.0 / D, EPS, ALU.mult, ALU.add)
                    s.act(st_[:, 2:3], st_[:, 1:2], AF.Sqrt)
                    s.recip(st_[:, 3:4], st_[:, 2:3])
                    s.ts(n_.full(), x_.full(), st_[:, 3:4], None, ALU.mult)

                def n1(i):
                    w = 1 if i < 2 else 0
                    n_ = xn[i % 3]
                    for half in range(2):
                        bk = banks[(2 * i + half) % 8]
                        for kk in range(4):
                            k = half * 4 + kk
                            s.transpose(bk[:, kk * 128:(kk + 1) * 128], n_[:, k * 128:(k + 1) * 128], ident)
                        for kk in range(4):
                            k = half * 4 + kk
                            s.act(hT[k][:, i * 128:(i + 1) * 128], bk[:, kk * 128:(kk + 1) * 128], AF.Identity,
                                  bias=modT[0][w][:, k:k + 1], scale=sc1[w][:, k:k + 1])

                pipeline(list(range(NT)), [n0, n1])
                if HT is not None:
                    for k in range(8):
                        s.dma(HT[k], hT[k].full())
                s.flush()

            with ExitStack() as es:
                WB = 256
                NWB, PF = 6, 4
                wbuf = [cx.sb(es, "wbuf%d" % i, [128, 8, WB], BF16) for i in range(NWB)]
                wplan = [(OFF_XBC + 256 * k, 256) for k in range(6)]
                for qc in range(8):
                    wplan += [(OFF_Q + qc * 128, 128), (OFF_QS + qc * 128, 128)]
                for j in range(4):
                    wplan += [(OFF_KR + j * 128, 128), (OFF_KSR + j * 128, 128)]
                wplan += [(OFF_G + 256 * k, 256) for k in range(4)]
                wplan += [(OFF_Z + 256 * k, 256) for k in range(4)]
                wplan += [(OFF_KV + 256, 256), (OFF_DT, 32)]
                wstate = {"i": 0, "issued": 0}

                def _issue(n):
                    col0, ncol = wplan[n]
                    wb = wbuf[n % NWB]
                    s.dma(wb[:, :, 0:ncol], e_w_in.view(col0, [[E_NCOL_EXT, 128], [128 * E_NCOL_EXT, 8], [1, ncol]]), q="pool")

                def load_w(col0, ncol=WB):
                    i = wstate["i"]
                    wstate["i"] += 1
                    assert wplan[i] == (col0, ncol), (i, wplan[i], col0, ncol)
                    while wstate["issued"] < min(i + PF + 1, len(wplan)):
                        _issue(wstate["issued"])
                        wstate["issued"] += 1
                    return wbuf[i % NWB]

                bstate = {"i": 0}

                def nbank():
                    bk = banks[bstate["i"] % 8]
                    bstate["i"] += 1
                    return bk

                def fm_mm(wb, cc, t0, n):
                    bk = nbank()
                    s.mm(bk[:, 0:n], [(wb[:, k, cc * 128:(cc + 1) * 128], hT[k][:, t0:t0 + n]) for k in range(8)])
                    return bk

                xraws = [cx.sb(es, "xraw%d" % i, [128, T]) for i in range(2)]
                accs = [cx.sb(es, "acc%d" % i, [128, T]) for i in range(2)]
                acc = accs[0]
                accbs = [cx.sb(es, "accb%d" % i, [128, T], BF16) for i in range(2)]
                accb = accbs[0]
                rc_i = {"i": 0}
                tmp1s = [cx.sb(es, "tmp1_%d" % i, [128, 512]) for i in range(2)]
                tmp2s = [cx.sb(es, "tmp2_%d" % i, [128, 512]) for i in range(2)]
                stg = [cx.sb(es, "stg%d" % i, [128, 4, 128]) for i in range(2)]
                stgb = [cx.sb(es, "stgb%d" % i, [128, 4, 128], BF16) for i in range(2)]
                rp = cx.sb(es, "rp", [128, 2, L])
                cw = cx.sb(es, "cw", [128, 12, 5])
                cb = cx.sb(es, "cb", [128, 12])
                dtb = cx.sb(es, "dtb", [128, 32])
                abc = cx.sb(es, "abc", [128, 32])
                s.dma(rp.full(), rope.full())
                s.dma(cw.full(), e_conv_wT.full())
                s.dma(cb.full(), e_conv_bT.full())
                s.dma(dtb.full(), e_dt_bias.view(0, [[0, 128], [1, 32]]))
                s.dma(abc.full(), e_a_log.view(0, [[0, 128], [1, 32]]))
                s.act(abc.full(), abc.full(), AF.Exp)
                s.ts(abc.full(), abc.full(), -1.0, None, ALU.mult)
                stg_i = {"i": 0}

                def transposes_to(dst, col0, src, lowp=False):
                    for i0 in range(0, NT, 4):
                        nb = min(4, NT - i0)
                        bk = nbank()
                        for ii in range(nb):
                            i = i0 + ii
                            s.transpose(bk[:, ii * 128:(ii + 1) * 128], src[:, i * 128:(i + 1) * 128], ident)
                        sg_ = (stgb if lowp else stg)[stg_i["i"] % 2]
                        stg_i["i"] += 1
                        s.copy(sg_[:, 0:nb, :], bk.view(0, [[512, 128], [128, nb], [1, 128]]), eng="act")
                        ncols = dst.h.shape[1]
                        s.dma(dst.view(i0 * 128 * ncols + col0, [[ncols, 128], [128 * ncols, nb], [1, 128]]),
                              sg_[:, 0:nb, :])

                wb_of = {}

                def xa(fc):
                    if fc % 2 == 0:
                        wb_of[fc // 2] = load_w(OFF_XBC + fc * 128)
                    wb = wb_of[fc // 2]
                    xraw = xraws[fc % 2]
                    for (t0, n) in TG:
                        bk = fm_mm(wb, fc % 2, t0, n)
                        s.copy(xraw[:, t0:t0 + n], bk[:, 0:n], eng="act")

                def xb(fc):
                    xraw, acc = xraws[fc % 2], accs[fc % 2]
                    s.ts(acc.full(), xraw.full(), cw[:, fc, 2:3], cb[:, fc:fc + 1], ALU.mult, ALU.add)
                    for kk in (0, 1, 3, 4):
                        d_ = kk - 2
                        for (s0, sl) in ((0, CTX), (CTX, L)):
                            lo = max(s0, s0 - d_)
                            hi = min(s0 + sl, s0 + sl - d_)
                            s.stt(acc[:, lo:hi], xraw[:, lo + d_:hi + d_], cw[:, fc, kk:kk + 1], acc[:, lo:hi],
                                  ALU.mult, ALU.add)
                    s.act(acc.full(), acc.full(), AF.Silu)
                    if fc < 8:
                        transposes_to(XS, fc * 128, acc)
                    elif fc < 10:
                        s.copy(accb.full(), acc.full(), eng="act")
                        s.dma(BT[fc - 8], accb.full())
                        transposes_to(BTOK, (fc - 8) * 128, acc, lowp=True)
                    else:
                        s.copy(accb.full(), acc.full(), eng="act")
                        s.dma(CT[fc - 10], accb.full())

                pipeline(list(range(12 if go('p2a') else 0)), [xa, xb])

                def rope_chunk(col_plain, col_swap, dst_rot, dst_ctx):
                    accb = accbs[rc_i["i"] % 2]
                    rc_i["i"] += 1
                    wa = load_w(col_plain, 128)
                    wsw = load_w(col_swap, 128)
                    for gi, (t0, n) in enumerate(TG):
                        bka = fm_mm(wa, 0, t0, n)
                        if gi == 0:
                            s.copy(accb[:, 0:CTX], bka[:, 0:CTX], eng="act")
                            continue
                        bkb = fm_mm(wsw, 0, t0, n)
                        l0 = t0 - CTX
                        tmp1, tmp2 = tmp1s[gi % 2], tmp2s[gi % 2]
                        s.tt(tmp1.full(), bka.full(), rp[:, 0, l0:l0 + 512], ALU.mult)
                        s.tt(tmp2.full(), bkb.full(), rp[:, 1, l0:l0 + 512], ALU.mult)
                        s.tt(accb[:, t0:t0 + n], tmp1.full(), tmp2.full(), ALU.add)
                    s.dma(dst_ctx, accb[:, 0:CTX])
                    s.dma(dst_rot, accb[:, CTX:T])

                for qc in range(8 if go('p2b') else 0):
                    rope_chunk(OFF_Q + qc * 128, OFF_QS + qc * 128, QR[qc], QC[qc])
                for j in range(4 if go('p2c') else 0):
                    rope_chunk(OFF_KR + j * 128, OFF_KSR + j * 128, KR[j], KC[j])

                for gc in range(8 if go('p2d') else 0):
                    acc = accs[gc % 2]
                    if gc % 2 == 0:
                        wb = load_w(OFF_G + gc * 128)
                    for (t0, n) in TG:
                        bk = fm_mm(wb, gc % 2, t0, n)
                        s.act(acc[:, t0:t0 + n], bk[:, 0:n], AF.Silu)
                    s.dma(SG[gc], acc.full())

                NT_E = NT if go('p2e') else 0
                wz = [load_w(OFF_Z + i * 256) for i in range(4)]
                for i in range(NT_E):
                    z_a = accs[i % 2]
                    for half in range(2):
                        bk = nbank()
                        for q4 in range(2):
                            wbz = wz[half * 2 + q4]
                            s.mm(bk[:, q4 * 256:(q4 + 1) * 256],
                                 [(hT[k][:, i * 128:(i + 1) * 128], wbz[:, k, :]) for k in range(8)])
                        s.act(z_a[:, half * 512:(half + 1) * 512], bk.full(), AF.Silu)
                    s.dma(SZ[i * 128:(i + 1) * 128, :], z_a[:, 0:1024])
                wv = load_w(OFF_KV + 256)
                wdt = load_w(OFF_DT, 32)
                vt = [cx.sb(es, "vt%d" % i, [128, 256], BF16) for i in range(2)]
                for i in range(NT if go('p2f') else 0):
                    bk = nbank()
                    s.mm(bk[:, 0:256], [(hT[k][:, i * 128:(i + 1) * 128], wv[:, k, :]) for k in range(8)])
                    s.copy(vt[i % 2].full(), bk[:, 0:256], eng="act")
                    s.dma(VT[i * 128:(i + 1) * 128, :], vt[i % 2].full())
                for i in range(NT if go('p2g') else 0):
                    bk = nbank()
                    s.mm(bk[:, 0:32], [(hT[k][:, i * 128:(i + 1) * 128], wdt[:, k, 0:32]) for k in range(8)])
                    s.tt(DT[:, i, :], bk[:, 0:32], dtb.full(), ALU.add)
                    if go('p2h'):
                        s.act(DT[:, i, :], DT[:, i, :], AF.Exp)
                        s.ts(DT[:, i, :], DT[:, i, :], 1.0, None, ALU.add)
                        s.act(DT[:, i, :], DT[:, i, :], AF.Ln)
                    s.tt(DTA[:, i, :], DT[:, i, :], abc.full(), ALU.mult)
                    if DTD is not None:
                        s.dma(DTD[i * 128:(i + 1) * 128, :], DT[:, i, :])
                s.flush()
            hts.close()
            for k in range(16):
                s.dma(wo[k].full(), e_w_out[k * 128:(k + 1) * 128, :], q="pool")

            with ExitStack() as es:
                nb_ = {"i": 0}

                def nbank():
                    bk = banks[nb_["i"] % 8]
                    nb_["i"] += 1
                    return bk

                N3 = 3
                xs_t = [cx.sb(es, "xs_t%d" % i, [128, 1024]) for i in range(N3)]
                b_t = [cx.sb(es, "b_t%d" % i, [128, 256], BF16) for i in range(N3)]
                bt_t = [cx.sb(es, "bt_t%d" % i, [128, 2, 128], BF16) for i in range(N3)]
                ct_t = [cx.sb(es, "ct_t%d" % i, [128, 2, 128], BF16) for i in range(N3)]
                yf_t = [cx.sb(es, "yf_t%d" % i, [128, 1024]) for i in range(N3)]
                sz_t = [cx.sb(es, "sz_t%d" % i, [128, 1024]) for i in range(N3)]
                MTs = [cx.sb(es, "MT%d" % i, [128, 2048], BF16) for i in range(N3)]
                xcs = [cx.sb(es, "xc%d" % i, [128, 1024], BF16) for i in range(N3)]
                xcds = [cx.sb(es, "xcd%d" % i, [128, 1024], BF16) for i in range(N3)]
                tmpos = [cx.sb(es, "tmpo%d" % i, [128, 1024]) for i in range(N3)]
                ytots = [cx.sb(es, "ytot%d" % i, [128, 1024]) for i in range(N3)]
                sms = [cx.sb(es, "sm%d" % i, [128, 4, 16]) for i in range(N3)]
                st3s = [cx.sb(es, "st3_%d" % i, [128, 4]) for i in range(N3)]
                ystgs = [cx.sb(es, "ystg%d" % i, [128, 8, 128], BF16) for i in range(2)]
                dtatris = [cx.sb(es, "dtatri%d" % i, [128, 2048]) for i in range(2)]
                decTs = [cx.sb(es, "decT%d" % i, [128, 2048]) for i in range(2)]
                cb_sbs = [cx.sb(es, "cb_sb%d" % i, [128, 256]) for i in range(2)]
                junk = cx.sb(es, "junk3", [128, 1024])
                Hs = [cx.sb(es, "Hs%d" % g, [128, 512]) for g in range(2)]
                Hb = [cx.sb(es, "Hb%d" % g, [128, 512], BF16) for g in range(2)]
                dsk = cx.sb(es, "dsk", [128, 16])
                snw = cx.sb(es, "snw", [128, 8])
                cm1 = cx.sb(es, "cm1", [128, 1])
                s.memset(cm1.full(), -1.0)
                s.dma(dsk.full(), e_d_skip.view(0, [[0, 128], [1, 16]]))
                s.dma(snw.full(), e_ssd_norm_wT.full())

                def bc3(buf, off, pstep, n1, s1, n2, s2):
                    return buf.view(off, [[pstep, 128], [s1, n1], [s2, n2]])

                n_ch = NT if go("p3") else 0
                for d_ in range(2):
                    order = list(range(NT)) if d_ == 0 else [1, 0] + list(range(NT - 1, 1, -1))
                    order = order[:n_ch]
                    TRIoff = 512 if d_ == 0 else 1024
                    TRIv = tri if d_ == 0 else utri
                    negm = cst[:, 4 + d_, :]
                    for g in range(2):
                        s.memset(Hs[g].full(), 0.0)
                        s.memset(Hb[g].full(), 0.0)

                    def stA(item, d_=d_, TRIoff=TRIoff, TRIv=TRIv, negm=negm):
                        ci, i = item
                        p3, p2 = ci % N3, ci % 2
                        xs_, b_, bt_, ct_ = xs_t[p3], b_t[p3], bt_t[p3], ct_t[p3]
                        MT, xc, xcd, sm = MTs[p3], xcs[p3], xcds[p3], sms[p3]
                        dtatri, decT, cb_sb = dtatris[p2], decTs[p2], cb_sbs[p2]
                        s.dma(xs_.full(), XS[i * 128:(i + 1) * 128, :])
                        s.dma(b_.full(), BTOK[i * 128:(i + 1) * 128, :])
                        s.dma(bt_.full(), BT.view(i * 128, [[T, 128], [128 * T, 2], [1, 128]]))
                        s.dma(ct_.full(), CT.view(i * 128, [[T, 128], [128 * T, 2], [1, 128]]))
                        if d_ == 1:
                            s.dma(yf_t[p3].full(), YF[i * 128:(i + 1) * 128, :])
                            s.dma(sz_t[p3].full(), SZ[i * 128:(i + 1) * 128, :])
                        dta_i = DTA[:, i, d_ * 16:(d_ + 1) * 16]
                        doff = i * 32 + d_ * 16
                        s.tt(bc3(dtatri, 0, 2048, 16, 128, 128, 1), bc3(DTA, doff, NT * 32, 16, 1, 128, 0),
                             bc3(cst, TRIoff, 3072, 16, 0, 128, 1), ALU.mult, eng="pool")
                        bs = nbank()
                        s.mm(bs[:, 0:16], [(TRIv, dta_i)])
                        s.mm(bs[:, 16:32], [(ones, dta_i)])
                        na, ea, de, cd = sm[:, 0, :], sm[:, 1, :], sm[:, 2, :], sm[:, 3, :]
                        s.ts(na, bs[:, 0:16], -1.0, None, ALU.mult)
                        s.act(ea, bs[:, 0:16], AF.Exp)
                        s.tt(de, bs[:, 16:32], na, ALU.add)
                        s.act(de, de, AF.Exp)
                        s.act(cd, bs[:, 16:32], AF.Exp)
                        for hq in range(4):
                            bq = nbank()
                            s.mm(bq.full(), [(ones, dtatri[:, hq * 512:(hq + 1) * 512]), (ident, negm)])
                            for hh in range(4):
                                h = hq * 4 + hh
                                s.act(decT[:, h * 128:(h + 1) * 128], bq[:, hh * 128:(hh + 1) * 128], AF.Exp,
                                      bias=sm[:, 0, h:h + 1])
                        bc = nbank()
                        for g in range(2):
                            s.mm(bc[:, g * 128:(g + 1) * 128], [(bt_[:, g, :], ct_[:, g, :])])
                        s.copy(cb_sb.full(), bc[:, 0:256], eng="act")
                        for g in range(2):
                            s.tt(bc3(MT, g * 1024, 2048, 8, 128, 128, 1), bc3(decT, g * 1024, 2048, 8, 128, 128, 1),
                                 bc3(cb_sb, g * 128, 256, 8, 0, 128, 1), ALU.mult)
                        s.tt(bc3(xc, 0, 1024, 16, 64, 64, 1), bc3(xs_, 0, 1024, 16, 64, 64, 1),
                             bc3(DT, doff, NT * 32, 16, 1, 64, 0), ALU.mult, eng="pool")
                        s.tt(bc3(xcd, 0, 1024, 16, 64, 64, 1), bc3(xc, 0, 1024, 16, 64, 64, 1),
                             bc3(sm, 32, 64, 16, 1, 64, 0), ALU.mult, eng="pool")
                        if d_ == 1:
                            s.tt(bc3(tmpos[p3], 0, 1024, 16, 64, 64, 1), bc3(xs_, 0, 1024, 16, 64, 64, 1),
                                 bc3(dsk, 0, 16, 16, 1, 64, 0), ALU.mult, eng="pool")
                            s.tt(yf_t[p3].full(), yf_t[p3].full(), tmpos[p3].full(), ALU.add, eng="pool")

                    def stB(item, d_=d_):
                        ci, i = item
                        p3 = ci % N3
                        b_, ct_ = b_t[p3], ct_t[p3]
                        MT, xc, xcd, sm, tmpo, ytot = MTs[p3], xcs[p3], xcds[p3], sms[p3], tmpos[p3], ytots[p3]
                        ydst = yf_t[p3] if d_ == 0 else ytot
                        for g in range(2):
                            by = nbank()
                            for hh in range(8):
                                h = g * 8 + hh
                                s.mm(by[:, hh * 64:(hh + 1) * 64], [(MT[:, h * 128:(h + 1) * 128], xc[:, h * 64:(h + 1) * 64])])
                            bo = nbank()
                            s.mm(bo.full(), [(ct_[:, g, :], Hb[g].full())])
                            s.tt(bc3(tmpo, g * 512, 1024, 8, 64, 64, 1), bc3(bo, 0, 512, 8, 64, 64, 1),
                                 bc3(sm, 16 + g * 8, 64, 8, 1, 64, 0), ALU.mult)
                            s.tt(ydst[:, g * 512:(g + 1) * 512], by.full(), tmpo[:, g * 512:(g + 1) * 512], ALU.add)
                        for g in range(2):
                            bst = nbank()
                            s.mm(bst.full(), [(b_[:, g * 128:(g + 1) * 128], xcd[:, g * 512:(g + 1) * 512])])
                            s.tt(bc3(Hs[g], 0, 512, 8, 64, 64, 1), bc3(Hs[g], 0, 512, 8, 64, 64, 1),
                                 bc3(sm, 48 + g * 8, 64, 8, 1, 64, 0), ALU.mult)
                            s.tt(Hs[g].full(), Hs[g].full(), bst.full(), ALU.add)
                            s.copy(Hb[g].full(), Hs[g].full(), eng="act")
                        if d_ == 0:
                            s.dma(YF[i * 128:(i + 1) * 128, :], yf_t[p3].full())

                    def stC(item, d_=d_):
                        if d_ == 0:
                            return
                        ci, i = item
                        p3, p2 = ci % N3, ci % 2
                        ytot, sz_, st3, ystg = ytots[p3], sz_t[p3], st3s[p3], ystgs[p2]
                        s.tt(ytot.full(), ytot.full(), yf_t[p3].full(), ALU.add)
                        s.tt(ytot.full(), ytot.full(), sz_.full(), ALU.mult)
                        s.act(junk.full(), ytot.full(), AF.Square, accum=st3[:, 0:1])
                        s.ts(st3[:, 1:2], st3[:, 0:1], 1.0 / 1024, EPS, ALU.mult, ALU.add)
                        s.act(st3[:, 2:3], st3[:, 1:2], AF.Sqrt)
                        s.recip(st3[:, 3:4], st3[:, 2:3])
                        s.act(ytot.full(), ytot.full(), AF.Copy, scale=st3[:, 3:4])
                        for half in range(2):
                            bk = nbank()
                            for kk in range(4):
                                k = half * 4 + kk
                                s.transpose(bk[:, kk * 128:(kk + 1) * 128], ytot[:, k * 128:(k + 1) * 128], ident)
                            for kk in range(4):
                                k = half * 4 + kk
                                s.act(ystg[:, k, :], bk[:, kk * 128:(kk + 1) * 128], AF.Copy, scale=snw[:, k:k + 1])
                        s.dma(YT.view(i * 128, [[T, 128], [128 * T, 8], [1, 128]]), ystg.full())

                    pipeline(list(enumerate(order)), [stA, stB, stC])
                s.flush()

            with ExitStack() as es:
                nb_ = {"i": 0}

                def nbank():
                    bk = banks[nb_["i"] % 8]
                    nb_["i"] += 1
                    return bk

                J2 = 2
                qr_ts = [cx.sb(es, "qr_t%d" % i, [128, 2, L], BF16) for i in range(J2)]
                qc_ts = [cx.sb(es, "qc_t%d" % i, [128, 2, CTX], BF16) for i in range(J2)]
                kr_ts = [cx.sb(es, "kr_t%d" % i, [128, L], BF16) for i in range(J2)]
                kc_ts = [cx.sb(es, "kc_t%d" % i, [128, CTX], BF16) for i in range(J2)]
                v_ts = [cx.sb(es, "v_t%d" % i, [128, NT, 64], BF16) for i in range(J2)]
                v2s = [cx.sb(es, "v2_%d" % i, [128, NT, 128], BF16) for i in range(J2)]
                sg_ts = [cx.sb(es, "sg_t%d" % i, [128, 2, T]) for i in range(J2)]
                asts = [cx.sb(es, "ast%d" % i, [128, 2, T], BF16) for i in range(J2)]
                NP = 3
                pt = [[cx.sb(es, "pt%d_%d" % (a, b), [128, 512], BF16) for b in range(5)] for a in range(NP)]
                rds = [cx.sb(es, "rd%d" % i, [128, 256]) for i in range(2)]
                aos = [cx.sb(es, "ao%d" % i, [128, 256]) for i in range(2)]
                es_pp = cx.sb(es, "es_pp", [128, 8])
                c8 = cx.sb(es, "c8", [128, 1])
                s.memset(c8.full(), 0.125)
                s.dma(es_pp.full(), e_sink.full())
                s.act(es_pp.full(), es_pp.full(), AF.Exp)
                ATT_DBG = [int(v) for v in os.environ.get("ATT_DBG", "4,18,4").split(",")]
                items = []
                for j in range(ATT_DBG[0] if go("p4") else 0):
                    qbs = ([("c", 0), ("c", 1)] + [("l", b) for b in range(16)])[:ATT_DBG[1]]
                    for qi, (kind, bi) in enumerate(qbs):
                        items.append((len(items), j, kind, bi, qi == 0, qi == len(qbs) - 1))

                def keys_of(kind, bi):
                    keys = [("c", 0, None), ("c", 1, None)]
                    if kind == "l":
                        if bi > 0:
                            keys.append(("l", bi - 1, "prev"))
                        keys.append(("l", bi, None))
                        if bi < 15:
                            keys.append(("l", bi + 1, "next"))
                    return keys

                def atA(item):
                    n, j, kind, bi, first, last = item
                    js = j % J2
                    qr_t, qc_t, kr_t, kc_t, v_t, v2, sg_t = qr_ts[js], qc_ts[js], kr_ts[js], kc_ts[js], v_ts[js], v2s[js], sg_ts[js]
                    if first:
                        s.dma(qr_t.full(), QR.view(2 * j * 128 * L, [[L, 128], [128 * L, 2], [1, L]]))
                        s.dma(qc_t.full(), QC.view(2 * j * 128 * CTX, [[CTX, 128], [128 * CTX, 2], [1, CTX]]))
                        s.dma(kr_t.full(), KR[j])
                        s.dma(kc_t.full(), KC[j])
                        s.dma(v_t.full(), VT.view(j * 64, [[256, 128], [128 * 256, NT], [1, 64]]))
                        s.dma(sg_t.full(), SG.view(2 * j * 128 * T, [[T, 128], [128 * T, 2], [1, T]]))
                        s.copy(v2[:, :, 0:64], v_t.full(), eng="pool")
                        s.copy(v2[:, :, 64:128], v_t.full(), eng="pool")
                    qsrc, q0 = (qc_t, bi * 128) if kind == "c" else (qr_t, bi * 128)
                    pts = pt[n % NP]
                    qw = qsrc.h.shape[2]
                    for ki, (kk, kb, msk) in enumerate(keys_of(kind, bi)):
                        ksrc = kc_t if kk == "c" else kr_t
                        for par in range(2):
                            p0 = par * 64
                            bs = nbank()
                            s.mm(bs[:, 0:256],
                                 [(ksrc[p0:p0 + 64, kb * 128:(kb + 1) * 128],
                                   qsrc.view(p0 * 2 * qw + q0, [[2 * qw, 64], [qw, 2], [1, 128]]))])
                            s.act(pts[ki][:, par * 256:(par + 1) * 256], bs[:, 0:256], AF.Exp, scale=c8[:, 0:1])
                        if msk is not None:
                            moff = 1024 if msk == "prev" else 512
                            s.tt(pts[ki].view(0, [[512, 128], [128, 4], [1, 128]]),
                                 pts[ki].view(0, [[512, 128], [128, 4], [1, 128]]),
                                 cst.view(moff, [[3072, 128], [0, 4], [1, 128]]), ALU.mult, eng="pool")

                def atB(item):
                    n, j, kind, bi, first, last = item
                    js = j % J2
                    v2, sg_t, ast = v2s[js], sg_ts[js], asts[js]
                    tok0 = bi * 128 if kind == "c" else CTX + bi * 128
                    keys = keys_of(kind, bi)
                    pts = pt[n % NP]
                    rd, ao = rds[n % 2], aos[n % 2]
                    vt_idx = [(kb if kk == "c" else 2 + kb) for (kk, kb, _) in keys]
                    bn = nbank()
                    s.mm(bn.full(), [(v2[:, vt_idx[ki], :], pts[ki].full()) for ki in range(len(keys))])
                    bd = nbank()
                    s.mm(bd.full(), [(onesb, pts[ki].full()) for ki in range(len(keys))])
                    for par in range(2):
                        p0 = par * 64
                        for c in range(2):
                            s.ts(rd[p0:p0 + 64, c * 128:(c + 1) * 128],
                                 bd[p0:p0 + 64, par * 256 + c * 128:par * 256 + (c + 1) * 128],
                                 es_pp[p0:p0 + 64, 2 * j + c:2 * j + c + 1], None, ALU.add)
                    s.recip(rd.full(), rd.full())
                    for par in range(2):
                        p0 = par * 64
                        s.tt(ao[p0:p0 + 64, :], bn[p0:p0 + 64, par * 256:(par + 1) * 256], rd[p0:p0 + 64, :], ALU.mult)
                    s.tt(ast.view(tok0, [[2 * T, 128], [T, 2], [1, 128]]),
                         ao.view(0, [[256, 128], [128, 2], [1, 128]]),
                         sg_t.view(tok0, [[2 * T, 128], [T, 2], [1, 128]]), ALU.mult)
                    if last:
                        s.dma(YT.view((8 + 2 * j) * 128 * T, [[T, 128], [128 * T, 2], [1, T]]), ast.full())

                pipeline(items, [atA, atB])
                s.flush()

            with ExitStack() as es:
                nb_ = {"i": 0}

                def nbank():
                    bk = banks[nb_["i"] % 8]
                    nb_["i"] += 1
                    return bk

                yt = [cx.sb(es, "yt%d" % i, [128, 16, 128], BF16) for i in range(2)]
                xt = [cx.sb(es, "xt5_%d" % i, [128, D]) for i in range(2)]
                x1t = [cx.sb(es, "x1t%d" % i, [128, D]) for i in range(2)]
                tmp5s = [cx.sb(es, "tmp5_%d" % i, [128, 512]) for i in range(2)]
                for i in range(NT if go("p5") else 0):
                    w = 1 if i < 2 else 0
                    y_, x_, o_ = yt[i % 2], xt[i % 2], x1t[i % 2]
                    s.dma(y_.full(), YT.view(i * 128, [[T, 128], [128 * T, 16], [1, 128]]))
                    s.dma(x_.full(), xin[i * 128:(i + 1) * 128, :])
                    for half in range(2):
                        tmp5 = tmp5s[half]
                        bk = nbank()
                        s.mm(bk.full(), [(y_[:, fc, :], wo[fc][:, half * 512:(half + 1) * 512]) for fc in range(16)])
                        s.tt(tmp5.full(), bk.full(), gate_bc[0][w][:, half * 512:(half + 1) * 512], ALU.mult)
                        s.tt(o_[:, half * 512:(half + 1) * 512], tmp5.full(), x_[:, half * 512:(half + 1) * 512], ALU.add)
                    s.dma(X1[i * 128:(i + 1) * 128, :], o_.full())
                s.flush()

        if go("all"):
            adaln_phase(1, o_ada_w, o_ada_b)
        with ExitStack() as l1:
            if not go("all"):
                return nc
            nb_ = {"i": 0}

            def nbank():
                bk = banks[nb_["i"] % 8]
                nb_["i"] += 1
                return bk

            with ExitStack() as es:
                nw = cx.sb(es, "nw1", [128, 8])
                sc1 = [cx.sb(es, "sc1b_%d" % w, [128, 8]) for w in range(2)]
                s.dma(nw.full(), o_norm_wT.full())
                for w in range(2):
                    s.stt(sc1[w].full(), modT[1][w][:, 8:16], 1.0, nw.full(), ALU.add, ALU.mult)
                hT = [cx.sb(es, "hTb%d" % k, [128, T], BF16) for k in range(8)]
                xt = [cx.sb(es, "xtb%d" % i, [128, D]) for i in range(3)]
                xn = [cx.sb(es, "xnb%d" % i, [128, D]) for i in range(3)]
                junk = cx.sb(es, "junkb", [128, D])
                st = [cx.sb(es, "stb%d" % i, [128, 4]) for i in range(3)]
                def m0(i):
                    x_, n_, st_ = xt[i % 3], xn[i % 3], st[i % 3]
                    s.dma(x_.full(), X1[i * 128:(i + 1) * 128, :])
                    s.act(junk.full(), x_.full(), AF.Square, accum=st_[:, 0:1])
                    s.ts(st_[:, 1:2], st_[:, 0:1], 1.0 / D, EPS, ALU.mult, ALU.add)
                    s.act(st_[:, 2:3], st_[:, 1:2], AF.Sqrt)
                    s.recip(st_[:, 3:4], st_[:, 2:3])
                    s.ts(n_.full(), x_.full(), st_[:, 3:4], None, ALU.mult)

                def m1(i):
                    w = 1 if i < 2 else 0
                    n_ = xn[i % 3]
                    for half in range(2):
                        bk = nbank()
                        for kk in range(4):
                            k = half * 4 + kk
                            s.transpose(bk[:, kk * 128:(kk + 1) * 128], n_[:, k * 128:(k + 1) * 128], ident)
                        for kk in range(4):
                            k = half * 4 + kk
                            s.act(hT[k][:, i * 128:(i + 1) * 128], bk[:, kk * 128:(kk + 1) * 128], AF.Identity,
                                  bias=modT[1][w][:, k:k + 1], scale=sc1[w][:, k:k + 1])

                pipeline(list(range(NT)), [m0, m1])
                wq = [cx.sb(es, "wq%d" % i, [128, 8, 256], BF16) for i in range(8)]
                for q8 in range(8):
                    s.dma(wq[q8].full(), o_w_in.view(q8 * 256, [[2 * D, 128], [128 * 2 * D, 8], [1, 256]]), q="pool")
                ot = [cx.sb(es, "ot%d" % i, [128, D]) for i in range(2)]
                oi = 0
                for which in range(2):
                    for i in range(NT):
                        if which == 1 and i < 2:
                            continue
                        o_ = ot[oi % 2]
                        oi += 1
                        for half in range(2):
                            bk = nbank()
                            for q4 in range(2):
                                s.mm(bk[:, q4 * 256:(q4 + 1) * 256],
                                     [(hT[k][:, i * 128:(i + 1) * 128], wq[which * 4 + half * 2 + q4][:, k, :]) for k in range(8)])
                            if which == 0:
                                s.copy(o_[:, half * 512:(half + 1) * 512], bk.full(), eng="act")
                            else:
                                s.act(o_[:, half * 512:(half + 1) * 512], bk.full(), AF.Silu)
                        s.dma((U if which == 0 else SG1)[i * 128:(i + 1) * 128, :], o_.full())
                s.flush()

            L1S = os.environ.get('L1S', 'z')
            if L1S == 'a':
                return nc
            with ExitStack() as es:
                lam = cx.sb(es, "lam", [128, 2, 3, 32])
                bprm = cx.sb(es, "bprm", [128, 2, 32, 16])
                cprm = cx.sb(es, "cprm", [128, 2, 32, 16])
                s.dma(lam.full(), s5_lam.full())
                s.dma(bprm.full(), s5_b.full())
                s.dma(cprm.full(), s5_c.full())
                kc = cx.sb(es, "kconst", [128, 4])
                s.memset(kc[:, 0:1], 1.0 / 16)
                s.memset(kc[:, 1:2], math.pi / 2)
                s.memset(kc[:, 2:3], 0.0)
                s.memset(kc[:, 3:4], 1.0)
                W64 = [128, 2, 32]

                def t64(name):
                    return cx.sb(es, name, W64)

                def lv(i):
                    return lam.view(i * 32, [[192, 128], [96, 2], [1, 32]])

                dt_ = t64("dt_"); mag = t64("mag"); th = t64("th"); cs = t64("cs"); sn = t64("sn")
                t_a = t64("t_a"); t_b = t64("t_b"); t_c = t64("t_c")
                abre = t64("abre"); abim = t64("abim"); cre = t64("cre"); cim = t64("cim")
                s.act(dt_.full(), lv(2), AF.Exp)
                s.tt(t_a.full(), lv(0), dt_.full(), ALU.mult)
                s.act(mag.full(), t_a.full(), AF.Exp)
                s.tt(th.full(), lv(1), dt_.full(), ALU.mult)
                s.act(sn.full(), th.full(), AF.Sin, scale=kc[:, 0:1])
                s.act(cs.full(), th.full(), AF.Sin, scale=kc[:, 0:1], bias=kc[:, 1:2])
                for _ in range(4):
                    s.tt(t_a.full(), cs.full(), cs.full(), ALU.mult)
                    s.tt(t_b.full(), sn.full(), sn.full(), ALU.mult)
                    s.tt(t_c.full(), sn.full(), cs.full(), ALU.mult)
                    s.tt(cs.full(), t_a.full(), t_b.full(), ALU.subtract)
                    s.ts(sn.full(), t_c.full(), 2.0, None, ALU.mult)
                s.tt(abre.full(), mag.full(), cs.full(), ALU.mult)
                s.tt(abim.full(), mag.full(), sn.full(), ALU.mult)
                PW = cx.sb(es, "PW", [128, 2, 9, 64])

                def pw(ri, k):
                    return PW.view((ri * 9 + k) * 64, [[2 * 9 * 64, 128], [32, 2], [1, 32]])

                s.memset(PW[:, 0, 0, :], 1.0)
                s.memset(PW[:, 1, 0, :], 0.0)
                for k in range(8):
                    s.tt(t_a.full(), pw(0, k), abre.full(), ALU.mult)
                    s.tt(t_b.full(), pw(1, k), abim.full(), ALU.mult)
                    s.tt(pw(0, k + 1), t_a.full(), t_b.full(), ALU.subtract)
                    s.tt(t_a.full(), pw(0, k), abim.full(), ALU.mult)
                    s.tt(t_b.full(), pw(1, k), abre.full(), ALU.mult)
                    s.tt(pw(1, k + 1), t_a.full(), t_b.full(), ALU.add)
                s.ts(t_c.full(), abre.full(), -1.0, None, ALU.add)
                s.tt(t_a.full(), lv(0), lv(0), ALU.mult)
                s.tt(t_b.full(), lv(1), lv(1), ALU.mult)
                s.tt(t_a.full(), t_a.full(), t_b.full(), ALU.add)
                s.recip(dt_.full(), t_a.full())
                s.tt(t_a.full(), t_c.full(), lv(0), ALU.mult)
                s.tt(t_b.full(), abim.full(), lv(1), ALU.mult)
                s.tt(t_a.full(), t_a.full(), t_b.full(), ALU.add)
                s.tt(cre.full(), t_a.full(), dt_.full(), ALU.mult)
                s.tt(t_a.full(), abim.full(), lv(0), ALU.mult)
                s.tt(t_b.full(), t_c.full(), lv(1), ALU.mult)
                s.tt(t_a.full(), t_a.full(), t_b.full(), ALU.subtract)
                s.tt(cim.full(), t_a.full(), dt_.full(), ALU.mult)
                BB = cx.sb(es, "BB", [128, 2, 2, 512])
                tb1 = cx.sb(es, "tb1", [128, 512])
                tb2 = cx.sb(es, "tb2", [128, 512])

                def bb(ri, d_, g0=0, ng=32):
                    return BB.view((ri * 2 + d_) * 512 + g0 * 16, [[2048, 128], [16, ng], [1, 16]])

                def v3(buf, off, pstep, n1, s1, n2, s2):
                    return buf.view(off, [[pstep, 128], [s1, n1], [s2, n2]])

                def prm(buf, ri, g0=0, ng=32):
                    return buf.view(ri * 512 + g0 * 16, [[1024, 128], [16, ng], [1, 16]])

                def cf(buf, d_, g0=0, ng=32, n2=16):
                    return buf.view(d_ * 32 + g0, [[64, 128], [1, ng], [0, n2]])

                t1v = v3(tb1, 0, 512, 32, 16, 16, 1)
                t2v = v3(tb2, 0, 512, 32, 16, 16, 1)
                for d_ in range(2):
                    s.tt(t1v, prm(bprm, 0), cf(cre, d_), ALU.mult)
                    s.tt(t2v, prm(bprm, 1), cf(cim, d_), ALU.mult)
                    s.tt(bb(0, d_), t1v, t2v, ALU.subtract)
                    s.tt(t1v, prm(bprm, 1), cf(cre, d_), ALU.mult)
                    s.tt(t2v, prm(bprm, 0), cf(cim, d_), ALU.mult)
                    s.tt(bb(1, d_), t1v, t2v, ALU.add)
                LA = cx.sb(es, "LA", [128, 2, 32, 2])
                LB = cx.sb(es, "LB", [128, 2, 32, 2])
                for ri in range(2):
                    s.copy(LA.view(ri, [[128, 128], [64, 2], [2, 32]]), pw(0, 8))
                s.ts(LB.view(0, [[128, 128], [64, 2], [2, 32]]), pw(1, 8), -1.0, None, ALU.mult)
                s.copy(LB.view(1, [[128, 128], [64, 2], [2, 32]]), pw(1, 8))
                zt_ = cx.sb(es, "zt_", [16, 16, 112], BF16)
                s.memset(zt_.full(), 0.0)
                s.flush()

                if L1S == 'b':
                    return nc
                for b in range(4 if L1S not in ('c1', 'd1', 'e1', 'f1', 'g1') else 1):
                    g0 = 8 * b
                    with ExitStack() as bs_:
                        CAB = cx.sb(bs_, "CAB", [128, 2, 2, 8 * 144])
                        WST = cx.sb(bs_, "WST", [128, 8, 2, 2, 2, 64], BF16)
                        TF = cx.sb(bs_, "TF", [128, 16, 128], BF16)
                        TB = cx.sb(bs_, "TB", [128, 16, 128], BF16)
                        CABb = cx.sb(bs_, "CABb", [128, 2, 2, 8 * 144], BF16)

                        with ExitStack() as tmp:
                            WT = cx.sb(tmp, "WT", [128, 2, 2, 8 * 128])
                            c1 = cx.sb(tmp, "c1", [128, 128])
                            c2 = cx.sb(tmp, "c2", [128, 128])
                            c1v = v3(c1, 0, 128, 8, 16, 16, 1)
                            c2v = v3(c2, 0, 128, 8, 16, 16, 1)
                            for d_ in range(2):
                                for ss in range(8):
                                    p_ = 7 - ss if d_ == 0 else ss
                                    pr = PW.view((0 * 9 + p_) * 64 + d_ * 32 + g0, [[1152, 128], [1, 8], [0, 16]])
                                    pi_ = PW.view((1 * 9 + p_) * 64 + d_ * 32 + g0, [[1152, 128], [1, 8], [0, 16]])
                                    o_re = WT.view((d_ * 2 + 0) * 1024 + ss * 16, [[4096, 128], [128, 8], [1, 16]])
                                    o_im = WT.view((d_ * 2 + 1) * 1024 + ss * 16, [[4096, 128], [128, 8], [1, 16]])
                                    s.tt(c1v, bb(0, d_, g0, 8), pr, ALU.mult)
                                    s.tt(c2v, bb(1, d_, g0, 8), pi_, ALU.mult)
                                    s.tt(o_re, c1v, c2v, ALU.subtract)
                                    s.tt(c1v, bb(1, d_, g0, 8), pr, ALU.mult)
                                    s.tt(c2v, bb(0, d_, g0, 8), pi_, ALU.mult)
                                    s.tt(o_im, c1v, c2v, ALU.add)
                            for gh in range(2):
                                p0 = gh * 64
                                for gq in range(8):
                                    bk = nbank()
                                    for d_ in range(2):
                                        for ri in range(2):
                                            sl = d_ * 2 + ri
                                            s.transpose(bk[:, sl * 64:(sl + 1) * 64],
                                                        WT.view(p0 * 4096 + (d_ * 2 + ri) * 1024 + gq * 128, [[4096, 64], [1, 128]]),
                                                        cst[p0:p0 + 64, 0, p0:p0 + 64])
                                    s.copy(WST.view(((gq * 2 + gh) * 4) * 64, [[4096, 128], [1, 256]]), bk[:, 0:256], eng="act")
                            s.flush()

                        KSB = cx.sb(bs_, "KSB", [16, 2, 16, 128], BF16)

                        def setup_part2():
                            c1v = ysb.view(0, [[512, 128], [16, 8], [1, 16]])
                            c2v = ysb.view(128, [[512, 128], [16, 8], [1, 16]])
                            for d_ in range(2):
                                for idx in range(9):
                                    p_ = idx if d_ == 0 else 8 - idx
                                    pr = PW.view((0 * 9 + p_) * 64 + d_ * 32 + g0, [[1152, 128], [1, 8], [0, 16]])
                                    pi_ = PW.view((1 * 9 + p_) * 64 + d_ * 32 + g0, [[1152, 128], [1, 8], [0, 16]])
                                    o_re = CAB.view((0 * 2 + d_) * 1152 + idx * 16, [[4608, 128], [144, 8], [1, 16]])
                                    o_im = CAB.view((1 * 2 + d_) * 1152 + idx * 16, [[4608, 128], [144, 8], [1, 16]])
                                    s.tt(c1v, prm(cprm, 0, g0, 8), pr, ALU.mult, eng="pool")
                                    s.tt(c2v, prm(cprm, 1, g0, 8), pi_, ALU.mult, eng="pool")
                                    s.tt(o_re, c1v, c2v, ALU.subtract, eng="pool")
                                    s.tt(c1v, prm(cprm, 0, g0, 8), pi_, ALU.mult, eng="pool")
                                    s.tt(c2v, prm(cprm, 1, g0, 8), pr, ALU.mult, eng="pool")
                                    s.tt(c1v, c1v, c2v, ALU.add, eng="pool")
                                    s.ts(o_im, c1v, -1.0, None, ALU.mult, eng="pool")
                            s.copy(CABb.full(), CAB.full(), eng="pool")
                            for gh in range(2):
                                p0 = gh * 64
                                for d_ in range(2):
                                    for gqq in range(2):
                                        bk = nbank()
                                        for q4 in range(4):
                                            gq = gqq * 4 + q4
                                            i0 = 0 if d_ == 0 else 1
                                            s.mm(bk[0:16, q4 * 128:(q4 + 1) * 128],
                                                 [(BB.view(p0 * 2048 + (0 * 2 + d_) * 512 + (g0 + gq) * 16, [[2048, 64], [1, 16]]),
                                                   CAB.view(p0 * 4608 + (0 * 2 + d_) * 1152 + gq * 144 + i0 * 16, [[4608, 64], [1, 128]])),
                                                  (BB.view(p0 * 2048 + (1 * 2 + d_) * 512 + (g0 + gq) * 16, [[2048, 64], [1, 16]]),
                                                   CAB.view(p0 * 4608 + (1 * 2 + d_) * 1152 + gq * 144 + i0 * 16, [[4608, 64], [1, 128]]))])
                                        s.copy(KSB.view(d_ * 2048 + (2 * gqq * 4 + gh) * 128, [[4096, 16], [256, 4], [1, 128]]),
                                               bk.view(0, [[512, 16], [128, 4], [1, 128]]), eng="act")
                            gbase = 16 * b
                            s.dma(KFP.view(gbase * 3840 + 7 * 16, [[240, 16], [3840, 16], [1, 128]]), KSB[:, 0, :, :])
                            s.dma(KBR.view(gbase * 3840, [[240, 16], [3840, 16], [1, 128]]), KSB[:, 1, :, :])
                            s.dma(KFP.view(gbase * 3840, [[240, 16], [3840, 16], [1, 112]]), zt_.full())
                            s.dma(KBR.view(gbase * 3840 + 128, [[240, 16], [3840, 16], [1, 112]]), zt_.full())
                            for ss in range(8):
                                s.dma(TF[ss * 16:(ss + 1) * 16, :, :], KFP.view(gbase * 3840 + (7 - ss) * 16, [[240, 16], [3840, 16], [1, 128]]))
                                s.dma(TB[ss * 16:(ss + 1) * 16, :, :], KBR.view(gbase * 3840 + (7 - ss) * 16, [[240, 16], [3840, 16], [1, 128]]))

                        u8b = cx.sb(bs_, "u8b", [128, 8, 256])
                        u8g = cx.sb(bs_, "u8g", [128, 16, 128])
                        U8T = cx.sb(bs_, "U8T", [128, 16, 288], BF16)
                        SSb = [cx.sb(bs_, "SSb%d" % i, [128, 16, 256], BF16) for i in range(2)]
                        NCOL = 326
                        PS = 16 * NCOL
                        SSD = [cx.sb(bs_, "SS%d" % i, [128, 8, 2, NCOL]) for i in range(2)]
                        CAR = [cx.sb(bs_, "CAR%d" % i, [128, 7, 8, 2]) for i in range(2)]
                        A36 = [cx.sb(bs_, "A36_%d" % i, [128, 8, 2]) for i in range(2)]
                        B36 = [cx.sb(bs_, "B36_%d" % i, [128, 8, 2]) for i in range(2)]
                        y8b = cx.sb(bs_, "y8b", [128, 8, 256])
                        ysb = cx.sb(bs_, "ysb", [128, 512])
                        TT1 = [cx.sb(bs_, "TT1_%d" % i, [128, 9, 8, 2]) for i in range(2)]
                        TT2 = [cx.sb(bs_, "TT2_%d" % i, [128, 9, 8, 2]) for i in range(2)]
                        for (j0, nj) in ((0, 32), (32, 128), (160, 128)):
                            s.dma(u8b[0:nj, :, :], U.view(8 * j0 * 1024 + 256 * b, [[8192, nj], [1024, 8], [1, 256]]))
                            s.copy(u8g.view(0, [[2048, nj], [128, 16], [16, 8], [1, 16]]),
                                   u8b.view(0, [[2048, nj], [16, 16], [256, 8], [1, 16]]), eng="act")
                            for gq4 in range(4):
                                bk = nbank()
                                for q4 in range(4):
                                    gi = gq4 * 4 + q4
                                    s.transpose(bk[:, q4 * 128:q4 * 128 + nj],
                                                u8g.view(128 * gi, [[2048, nj], [1, 128]]), cst[0:nj, 0, 0:nj])
                                s.copy(U8T.view(gq4 * 4 * 288 + j0, [[16 * 288, 128], [288, 4], [1, nj]]),
                                       bk.view(0, [[512, 128], [128, 4], [1, nj]]), eng="act")
                        if L1S in ('d', 'd1'):
                            s.flush()
                            continue
                        s.memset(SSD[0].view(0, [[PS, 128], [NCOL, 16], [1, 1]]), 0.0)
                        s.memset(SSD[0].view(289, [[PS, 128], [NCOL, 16], [1, 37]]), 0.0)
                        s.memset(SSD[1].view(288, [[PS, 128], [NCOL, 16], [1, 38]]), 0.0)
                        s.memset(SSD[0].view(289, [[PS, 128], [2 * NCOL, 8], [1, 1]]), 1.0)
                        s.memset(SSD[1].view(323, [[PS, 128], [2 * NCOL, 8], [1, 1]]), 1.0)
                        for gq in range(8):
                            for gh in range(2):
                                gi = 2 * gq + gh
                                p0 = gh * 64
                                for d_ in range(2):
                                    for ri in range(2):
                                        bk = nbank()
                                        s.mm(bk[p0:p0 + 64, 0:288],
                                             [(WST.view((((gq * 2 + gh) * 2 + d_) * 2 + ri) * 64, [[4096, 128], [1, 64]]),
                                               U8T[:, gi, :])])
                                        so = p0 * PS + (gq * 2 + ri) * NCOL
                                        if d_ == 0:
                                            s.copy(SSD[0].view(so + 1, [[PS, 64], [1, 288]]), bk[p0:p0 + 64, 0:288], eng="act")
                                        else:
                                            s.copy(SSD[1].view(so + 256, [[PS, 64], [1, 32]]), bk[p0:p0 + 64, 0:32], eng="act")
                                            s.copy(SSD[1].view(so, [[PS, 64], [1, 256]]), bk[p0:p0 + 64, 32:288], eng="act")
                        if L1S in ('e', 'e1'):
                            s.flush()
                            continue
                        DS = 8 * 2 * 289
                        setup_part2()
                        RI, GQ = NCOL, 2 * NCOL
                        REC_ENG2 = os.environ.get('REC2', 'dve')

                        def cplx_step(items):
                            engs = ("dve", REC_ENG2)
                            for n_, (pv, psw, cv, ca, cb_, t1_, t2_) in enumerate(items):
                                s.tt(t1_, pv, ca, ALU.mult, eng=engs[n_ % 2])
                                s.tt(t2_, psw, cb_, ALU.mult, eng=engs[n_ % 2])
                            for n_, (pv, psw, cv, ca, cb_, t1_, t2_) in enumerate(items):
                                s.tt(t1_, t1_, t2_, ALU.add, eng=engs[n_ % 2])
                            for n_, (pv, psw, cv, ca, cb_, t1_, t2_) in enumerate(items):
                                if cv is not None:
                                    s.tt(cv, cv, t1_, ALU.add, eng=engs[n_ % 2])

                        def segv(SS, col, nseg):
                            return (SS.view(col, [[PS, 128], [36, nseg], [GQ, 8], [RI, 2]]),
                                    SS.view(col + RI, [[PS, 128], [36, nseg], [GQ, 8], [-RI, 2]]))

                        def coef(buf, d_, nseg):
                            return buf.view(d_ * 64 + g0 * 2, [[128, 128], [0, nseg], [2, 8], [1, 2]])

                        for k in range(1, 36):
                            items = []
                            for d_ in range(2):
                                pc = k if d_ == 0 else 36 - k
                                cc = k + 1 if d_ == 0 else 35 - k
                                pv, psw = segv(SSD[d_], pc, 9)
                                cv, _ = segv(SSD[d_], cc, 9)
                                items.append((pv, psw, cv, coef(LA, d_, 9), coef(LB, d_, 9), TT1[d_].full(), TT2[d_].full()))
                            cplx_step(items)
                        items = []
                        for d_ in range(2):
                            c35 = 324 if d_ == 0 else 288
                            pv, psw = segv(SSD[d_], c35, 1)
                            items.append((pv, psw, None, coef(LA, d_, 1), coef(LB, d_, 1),
                                          TT1[d_].view(0, [[144, 128], [16, 1], [2, 8], [1, 2]]),
                                          TT2[d_].view(0, [[144, 128], [16, 1], [2, 8], [1, 2]])))
                        cplx_step(items)
                        for d_ in range(2):
                            l36re = TT1[d_].view(0, [[144, 128], [2, 8], [0, 2]])
                            s.copy(A36[d_].full(), l36re)
                            s.ts(B36[d_][:, :, 0:1], TT1[d_].view(1, [[144, 128], [2, 8], [1, 1]]), -1.0, None, ALU.mult)
                            s.copy(B36[d_][:, :, 1:2], TT1[d_].view(1, [[144, 128], [2, 8], [1, 1]]))
                        for step in range(1, 8):
                            items = []
                            for d_ in range(2):
                                if d_ == 0:
                                    m = step
                                    cc, pc = 36 * m + 36, 36 * m
                                else:
                                    m = 7 - step
                                    cc, pc = 36 * m, 36 * m + 36
                                pv, psw = segv(SSD[d_], pc, 1)
                                cv, _ = segv(SSD[d_], cc, 1)
                                items.append((pv, psw, cv,
                                              A36[d_].view(0, [[16, 128], [0, 1], [2, 8], [1, 2]]),
                                              B36[d_].view(0, [[16, 128], [0, 1], [2, 8], [1, 2]]),
                                              TT1[d_].view(0, [[144, 128], [16, 1], [2, 8], [1, 2]]),
                                              TT2[d_].view(0, [[144, 128], [16, 1], [2, 8], [1, 2]])))
                            cplx_step(items)
                        items = []
                        for d_ in range(2):
                            pv, psw = segv(SSD[d_], 36, 7)
                            items.append((pv, psw, None, coef(LA, d_, 7), coef(LB, d_, 7),
                                          CAR[d_].full(), TT2[d_].view(0, [[144, 128], [16, 7], [2, 8], [1, 2]])))
                        cplx_step(items)
                        for d_ in range(2):
                            SS = SSD[d_]
                            sb0 = 37 if d_ == 0 else 1

                            def sview(ri):
                                return SS.view(sb0 + ri * RI, [[PS, 128], [36, 7], [GQ, 8], [1, 35]])

                            def tview(ri):
                                return SS.view(289 + ri * RI, [[PS, 128], [0, 7], [GQ, 8], [1, 35]])

                            def cview(ri):
                                return CAR[d_].view(ri, [[112, 128], [16, 7], [2, 8], [0, 35]])

                            w1 = (u8g if d_ == 0 else u8b).view(0, [[2048, 128], [280, 7], [35, 8], [1, 35]])
                            w2 = y8b.view(0, [[2048, 128], [280, 7], [35, 8], [1, 35]])
                            s.tt(w1, tview(0), cview(0), ALU.mult)
                            s.tt(w2, tview(1), cview(1), ALU.mult)
                            s.tt(w1, w1, w2, ALU.subtract)
                            s.tt(sview(0), sview(0), w1, ALU.add)
                            s.tt(w1, tview(0), cview(1), ALU.mult)
                            s.tt(w2, tview(1), cview(0), ALU.mult)
                            s.tt(w1, w1, w2, ALU.add)
                            s.tt(sview(1), sview(1), w1, ALU.add)
                        s.copy(SSb[0].full(), SSD[0].view(32, [[PS, 128], [NCOL, 16], [1, 256]]), eng="act")
                        s.copy(SSb[1].full(), SSD[1].view(1, [[PS, 128], [NCOL, 16], [1, 256]]), eng="pool")
                        if L1S in ('f', 'f1'):
                            s.flush()
                            continue
                        for tt_ in range(2):
                            j0 = 32 + 128 * tt_
                            m0 = 128 * tt_
                            for gh in range(2):
                                p0 = gh * 64
                                for gqq in range(2):
                                    bx = nbank()
                                    by = nbank()
                                    for q4 in range(4):
                                        gq = gqq * 4 + q4
                                        gi = 2 * gq + gh
                                        s.mm(bx[:, q4 * 128:(q4 + 1) * 128],
                                             [(U8T[:, gi, j0:j0 + 128], TF[:, gi, :]), (U8T[:, gi, j0:j0 + 128], TB[:, gi, :])])
                                        pairs = []
                                        for d_ in range(2):
                                            c0 = m0
                                            i0 = 1 if d_ == 0 else 0
                                            for ri in range(2):
                                                so = p0 * 4096 + (gq * 2 + ri) * 256 + c0
                                                pairs.append((SSb[d_].view(so, [[4096, 64], [1, 128]]),
                                                              CABb.view(p0 * 4608 + (ri * 2 + d_) * 1152 + gq * 144 + i0 * 16, [[4608, 64], [1, 128]])))
                                        s.mm(by[:, q4 * 128:(q4 + 1) * 128], pairs)
                                    s.copy(ysb.full(), by.full(), eng="act")
                                    s.tt(y8b.view(32 * gqq * 4 + 16 * gh, [[2048, 128], [32, 4], [256, 8], [1, 16]]),
                                         bx.view(0, [[512, 128], [128, 4], [16, 8], [1, 16]]),
                                         ysb.view(0, [[512, 128], [128, 4], [16, 8], [1, 16]]), ALU.add)
                            s.dma(YTOK.view((CTX + 8 * m0) * 1024 + 256 * b, [[8192, 128], [1024, 8], [1, 256]]), y8b.full())
                        s.flush()

            if L1S in ('g', 'g1'):
                return nc
            with ExitStack() as es:
                gw = [cx.sb(es, "gw%d" % k, [128, D], BF16) for k in range(8)]
                ow = [cx.sb(es, "ow%d" % k, [128, D], BF16) for k in range(8)]
                dskb = cx.sb(es, "dskb", [128, D])
                glbb = cx.sb(es, "glbb", [128, D])
                fnwb = cx.sb(es, "fnwb", [128, D])
                kg = cx.sb(es, "kg", [128, 1])
                s.memset(kg.full(), 2.0 * math.sqrt(2.0 / math.pi))
                for k in range(8):
                    s.dma(gw[k].full(), o_glu_w[k * 128:(k + 1) * 128, :], q="pool")
                    s.dma(ow[k].full(), o_w_out[k * 128:(k + 1) * 128, :], q="pool")
                s.dma(dskb.full(), o_d_skip.view(0, [[0, 128], [1, D]]))
                s.dma(glbb.full(), o_glu_b.view(0, [[0, 128], [1, D]]))
                s.dma(fnwb.full(), final_norm_w.view(0, [[0, 128], [1, D]]))
                NB3 = 3
                ya = [cx.sb(es, "ya%d" % i, [128, D]) for i in range(NB3)]
                ua = [cx.sb(es, "ua%d" % i, [128, D]) for i in range(NB3)]
                sga = [cx.sb(es, "sga%d" % i, [128, D]) for i in range(NB3)]
                xa = [cx.sb(es, "xa%d" % i, [128, D]) for i in range(NB3)]
                w1s = [cx.sb(es, "w1_%d" % i, [128, D]) for i in range(NB3)]
                w2s = [cx.sb(es, "w2_%d" % i, [128, D]) for i in range(NB3)]
                w3s = [cx.sb(es, "w3_%d" % i, [128, D]) for i in range(NB3)]
                tTs = [cx.sb(es, "tT_%d" % i, [128, 8, 128], BF16) for i in range(2 * NB3)]
                sts = [cx.sb(es, "st10_%d" % i, [128, 4]) for i in range(NB3)]

                def transp8(src, tT):
                    for half in range(2):
                        bk = nbank()
                        for kk in range(4):
                            k = half * 4 + kk
                            s.transpose(bk[:, kk * 128:(kk + 1) * 128], src[:, k * 128:(k + 1) * 128], ident)
                        s.copy(tT[:, half * 4:(half + 1) * 4, :], bk.view(0, [[512, 128], [128, 4], [1, 128]]), eng="act")

                TAILN = int(os.environ.get('TAILN', NT))

                def bufs(i):
                    b_ = i % NB3
                    return ya[b_], ua[b_], sga[b_], xa[b_], w1s[b_], w2s[b_], w3s[b_], tTs[2 * b_], tTs[2 * b_ + 1], sts[b_]

                def stage0(i):
                    y_, u_, g_, x_, w1, w2, w3, tTa, tTb, st = bufs(i)
                    s.dma(y_.full(), YTOK[i * 128:(i + 1) * 128, :])
                    s.dma(u_.full(), U[i * 128:(i + 1) * 128, :])
                    s.dma(g_.full(), SG1[i * 128:(i + 1) * 128, :])
                    s.dma(x_.full(), X1[i * 128:(i + 1) * 128, :])
                    s.tt(w1.full(), u_.full(), dskb.full(), ALU.mult)
                    s.tt(y_.full(), y_.full(), w1.full(), ALU.add)
                    s.tt(w1.full(), y_.full(), y_.full(), ALU.mult)
                    s.ts(w1.full(), w1.full(), 0.044715, 1.0, ALU.mult, ALU.add)
                    s.tt(w1.full(), w1.full(), y_.full(), ALU.mult)
                    s.act(w1.full(), w1.full(), AF.Sigmoid, scale=kg[:, 0:1])
                    s.tt(w2.full(), y_.full(), w1.full(), ALU.mult)
                    transp8(w2, tTa)

                def stage1(i):
                    y_, u_, g_, x_, w1, w2, w3, tTa, tTb, st = bufs(i)
                    for half in range(2):
                        bk = nbank()
                        s.mm(bk.full(), [(tTa[:, k, :], gw[k][:, half * 512:(half + 1) * 512]) for k in range(8)])
                        s.tt(w1[:, half * 512:(half + 1) * 512], bk.full(), glbb[:, half * 512:(half + 1) * 512], ALU.add)
                    s.act(w1.full(), w1.full(), AF.Sigmoid)
                    s.tt(w2.full(), w2.full(), w1.full(), ALU.mult)
                    s.tt(w2.full(), w2.full(), g_.full(), ALU.mult)
                    transp8(w2, tTb)

                def stage2(i):
                    y_, u_, g_, x_, w1, w2, w3, tTa, tTb, st = bufs(i)
                    for half in range(2):
                        bk = nbank()
                        s.mm(bk.full(), [(tTb[:, k, :], ow[k][:, half * 512:(half + 1) * 512]) for k in range(8)])
                        s.tt(w1[:, half * 512:(half + 1) * 512], bk.full(), gate_bc[1][0][:, half * 512:(half + 1) * 512], ALU.mult)
                    s.tt(w3.full(), w1.full(), x_.full(), ALU.add)
                    s.act(w1.full(), w3.full(), AF.Square, accum=st[:, 0:1])
                    s.ts(st[:, 1:2], st[:, 0:1], 1.0 / D, EPS, ALU.mult, ALU.add)
                    s.act(st[:, 2:3], st[:, 1:2], AF.Sqrt)
                    s.recip(st[:, 3:4], st[:, 2:3])
                    s.act(w3.full(), w3.full(), AF.Copy, scale=st[:, 3:4])
                    s.tt(w2.full(), w3.full(), fnwb.full(), ALU.mult)
                    s.dma(out_t[(i - 2) * 128:(i - 1) * 128, :], w2.full())

                pipeline(list(range(2, TAILN)), [stage0, stage1, stage2])
                s.flush()

    return nc


def _consts():
    c = np.zeros((128, 6, 512), np.float32)
    j = np.arange(128)[:, None]
    l = np.arange(128)[None, :]
    c[:, 0, :128] = np.eye(128, dtype=np.float32)
    c[:, 1, :128] = (j <= l)
    c[:, 2, :128] = (j >= l)
    c[:, 3, :] = 1.0
    nf = np.where(l < j, -30000.0, 0.0).astype(np.float32)
    nb = np.where(l > j, -30000.0, 0.0).astype(np.float32)
    c[:, 4, :] = np.tile(nf, (1, 4))
    c[:, 5, :] = np.tile(nb, (1, 4))
    return c


def _rope_tables():
    rows = L // 64
    row = np.repeat(np.arange(rows, dtype=np.float32), 64)
    col = np.tile(np.arange(64, dtype=np.float32), rows)
    n_freq = 16
    inv = (np.float32(10000.0) ** (-np.arange(n_freq, dtype=np.float32) / n_freq)).astype(np.float32)
    ang = np.concatenate([row[:, None] * inv, col[:, None] * inv], axis=-1).astype(np.float32)
    cos = np.cos(ang).astype(np.float32)
    sin = np.sin(ang).astype(np.float32)
    cosT = np.zeros((128, L), np.float32)
    sinT = np.zeros((128, L), np.float32)
    for h2 in range(2):
        for half in range(2):
            p0 = h2 * 64 + half * 32
            cosT[p0:p0 + 32] = cos.T
            sinT[p0:p0 + 32] = (-sin.T if half == 0 else sin.T)
    return np.stack([cosT, sinT], axis=1)


def _vecT(v, nchunk):
    return np.ascontiguousarray(np.asarray(v, np.float32).reshape(nchunk, 128).T)


def prep_inputs(b, inp):
    f = lambda a: np.ascontiguousarray(np.asarray(a, np.float32))
    m = {}
    m["xin"] = f(np.concatenate([inp["ctx"][b], inp["x"][b]], axis=0))
    cv = np.stack([inp["c"][b], inp["c_ctx"]], axis=0)
    m["cvecT"] = f(cv.reshape(2, 8, 128).transpose(2, 0, 1))
    m["consts"] = _consts()
    m["rope"] = _rope_tables()
    m["e_ada_w"] = f(inp["e_ada_w"][0])
    m["e_ada_b"] = f(inp["e_ada_b"][0]).reshape(1, -1)
    m["e_norm_wT"] = _vecT(inp["e_norm_w"][0], 8)
    w = f(inp["e_w_in"][0])
    q = w[:, OFF_Q:OFF_Q + 1024].reshape(D, 16, 2, 32)
    qs = q[:, :, ::-1, :].reshape(D, 1024)
    k = w[:, OFF_KV:OFF_KV + 256].reshape(D, 4, 64)
    kr = np.concatenate([k, k], axis=2).reshape(D, 512)
    ks = k.reshape(D, 4, 2, 32)[:, :, ::-1, :].reshape(D, 4, 64)
    ksr = np.concatenate([ks, ks], axis=2).reshape(D, 512)
    m["e_w_in"] = f(np.concatenate([w, qs, kr, ksr], axis=1))
    cw = f(inp["e_conv_w"][0])
    m["e_conv_wT"] = f(cw.reshape(5, 12, 128).transpose(2, 1, 0))
    m["e_conv_bT"] = _vecT(inp["e_conv_b"][0], 12)
    m["e_dt_bias"] = f(inp["e_dt_bias"][0]).reshape(1, 32)
    m["e_a_log"] = f(inp["e_a_log"][0]).reshape(1, 32)
    m["e_d_skip"] = f(inp["e_d_skip"][0]).reshape(1, 16)
    m["e_ssd_norm_wT"] = _vecT(inp["e_ssd_norm_w"][0], 8)
    sk = f(inp["e_sink"][0]).reshape(8, 2)
    m["e_sink"] = f(np.repeat(sk.T[:, None, :], 64, axis=1).reshape(128, 8))
    m["e_w_out"] = f(inp["e_w_out"][0])
    m["o_ada_w"] = f(inp["o_ada_w"][0])
    m["o_ada_b"] = f(inp["o_ada_b"][0]).reshape(1, -1)
    m["o_norm_wT"] = _vecT(inp["o_norm_w"][0], 8)
    m["o_w_in"] = f(inp["o_w_in"][0])

    def gl(a):
        a = np.asarray(a, np.float32)
        rest = a.shape[2:]
        a = a.reshape((32, 2, 64) + rest)
        a = np.moveaxis(a, 0, 2)
        return a.reshape((128, 32) + rest)

    lam = np.zeros((128, 2, 3, 32), np.float32)
    for d_ in range(2):
        lam[:, d_, 0] = gl(inp["o_lam_re"][0][d_])
        lam[:, d_, 1] = gl(inp["o_lam_im"][0][d_])
        lam[:, d_, 2] = gl(np.repeat(np.asarray(inp["o_log_step"][0][d_])[:, None], 64, axis=1))
    m["s5_lam"] = f(lam)
    m["s5_b"] = f(np.stack([gl(inp["o_b_re"][0]), gl(inp["o_b_im"][0])], axis=1))
    cr = np.asarray(inp["o_c_re"][0]).transpose(0, 2, 1)
    ci = np.asarray(inp["o_c_im"][0]).transpose(0, 2, 1)
    m["s5_c"] = f(np.stack([gl(cr), gl(ci)], axis=1))
    m["o_d_skip"] = f(inp["o_d_skip"][0]).reshape(1, -1)
    m["o_glu_w"] = f(inp["o_glu_w"][0])
    m["o_glu_b"] = f(inp["o_glu_b"][0]).reshape(1, -1)
    m["o_w_out"] = f(inp["o_w_out"][0])
    m["final_norm_w"] = f(inp["final_norm_w"]).reshape(1, -1)
    return m


def kernel(**inputs):
    nc = build_program()
    in_maps = [prep_inputs(b, inputs) for b in range(8)]
    res = run_bass_kernel_spmd(nc, in_maps, core_ids=list(range(8)))
    return np.stack([r["out"] for r in res.results], axis=0)
```

```python
import math
import os
from contextlib import ExitStack

import numpy as np
import concourse.bass as bass
import concourse.mybir as mybir
from concourse.bass_utils import run_bass_kernel_spmd

F32 = mybir.dt.float32
BF16 = mybir.dt.bfloat16
AF = mybir.ActivationFunctionType
ALU = mybir.AluOpType

D = 1024
T = 2304
NT = 18
CTX = 256
L = 2048
EPS = 1e-6
TG = [(0, 256), (256, 512), (768, 512), (1280, 512), (1792, 512)]

SES_ALL = os.environ.get('SES', '0') == '1'
SAME_ENGINE_SYNC = {'act': SES_ALL, 'dve': SES_ALL, 'pool': True, 'pe': False, 'sp': True}
SEM_EPOCH = 30000


class V:
    __slots__ = ("buf", "ap")

    def __init__(self, buf, ap):
        self.buf = buf
        self.ap = ap


class Buf:
    def __init__(self, name, h):
        self.name = name
        self.h = h
        self.last_w = None
        self.readers = []
        self.is_psum = False

    def __getitem__(self, idx):
        return V(self, self.h[idx])

    def full(self):
        return V(self, self.h.ap())

    def view(self, offset, pattern):
        return V(self, bass.AP(self.h, offset, [list(p) for p in pattern]))


class Sched:
    ENG = ("pe", "act", "dve", "pool", "sp")

    def __init__(self, nc):
        self.nc = nc
        self.prog = {e: [] for e in self.ENG}
        self.sem = {}
        self.cnt = {}
        self.semid = 0
        self.known = {e: {} for e in self.ENG}
        for e in ("pe", "act", "dve", "pool"):
            self._new_engine_sem(e)
        self.nds = 8
        self.dsem = {}
        self.duse = {}
        self.dcnt = {}
        for q in ("sp", "pool"):
            self.dsem[q] = []
            self.duse[q] = []
            for i in range(self.nds):
                key = "d_%s_%d" % (q, i)
                self.dsem[q].append((nc.alloc_semaphore(key), key))
                self.duse[q].append(0)
            self.dcnt[q] = 0
        self.n_ops = 0

    def _new_engine_sem(self, e):
        self.semid += 1
        key = "s_%s_%d" % (e, self.semid)
        self.sem[e] = (self.nc.alloc_semaphore(key), key)
        self.cnt[e] = 0

    def _deps(self, reads, writes):
        deps = {}

        def add(tok):
            if tok is None:
                return
            h, key, val = tok
            if key not in deps or deps[key][1] < val:
                deps[key] = (h, val)

        for r in reads:
            add(r.buf.last_w)
            if r.buf.is_psum:
                for t in r.buf.readers:
                    add(t)
        for w in writes:
            add(w.buf.last_w)
            for t in w.buf.readers:
                add(t)
        return deps

    def _emit_waits(self, eng, deps, own_key=None):
        kn = self.known[eng]
        for key, (h, val) in deps.items():
            if key == own_key and not SAME_ENGINE_SYNC[eng]:
                continue
            if kn.get(key, 0) >= val:
                continue
            kn[key] = val
            self.prog[eng].append(("wait", h, val))

    def _update(self, tok, reads, writes):
        for w in writes:
            w.buf.last_w = tok
            w.buf.readers = []
        for r in reads:
            if r.buf.last_w is not tok:
                r.buf.readers.append(tok)

    def op(self, eng, fn, reads=(), writes=()):
        reads = [r for r in reads if r is not None]
        writes = list(writes)
        if self.cnt[eng] >= SEM_EPOCH:
            self._new_engine_sem(eng)
        h, key = self.sem[eng]
        own = None if eng == "pe" else key
        deps = self._deps(reads, writes)
        if eng == "pe":
            deps.pop(key, None)
        self._emit_waits(eng, deps, own_key=own)
        self.cnt[eng] += 1
        self.prog[eng].append(("op", fn, h, 1))
        tok = (h, key, self.cnt[eng])
        self._update(tok, reads, writes)
        self.n_ops += 1
        return tok

    def dma(self, out, in_, q="sp", **kw):
        deps = self._deps([in_], [out])
        self._emit_waits(q, deps)
        k = self.dcnt[q] % self.nds
        self.dcnt[q] += 1
        h, key = self.dsem[q][k]
        prev = 16 * self.duse[q][k]
        if prev > 0 and self.known[q].get(key, 0) < prev:
            self.known[q][key] = prev
            self.prog[q].append(("wait", h, prev))
        self.duse[q][k] += 1
        val = 16 * self.duse[q][k]
        o_ap, i_ap = out.ap, in_.ap
        self.prog[q].append(("op", lambda e: e.dma_start(out=o_ap, in_=i_ap, **kw), h, 16))
        tok = (h, key, val)
        self._update(tok, [in_], [out])
        self.n_ops += 1
        return tok

    def finish_dmas(self):
        for q in ("sp", "pool"):
            for k in range(self.nds):
                h, key = self.dsem[q][k]
                val = 16 * self.duse[q][k]
                if val > 0 and self.known[q].get(key, 0) < val:
                    self.known[q][key] = val
                    self.prog[q].append(("wait", h, val))

    def flush(self, name=None):
        self.finish_dmas()
        nc = self.nc
        prog = self.prog
        self.prog = {e: [] for e in self.ENG}

        def run(items, e):
            for it in items:
                if it[0] == "wait":
                    e.wait_ge(it[1], it[2])
                else:
                    inst = it[1](e)
                    inst.then_inc(it[2], it[3])

        with nc.Block() as block:
            if prog["sp"]:
                @block.sync
                def _(e):
                    run(prog["sp"], e)
            if prog["act"]:
                @block.scalar
                def _(e):
                    run(prog["act"], e)
            if prog["dve"]:
                @block.vector
                def _(e):
                    run(prog["dve"], e)
            if prog["pool"]:
                @block.gpsimd
                def _(e):
                    run(prog["pool"], e)
            if prog["pe"]:
                @block.tensor
                def _(e):
                    run(prog["pe"], e)

    def mm(self, out, pairs):
        n = len(pairs)

        def fn(e):
            inst = None
            for i, (l, r) in enumerate(pairs):
                inst = e.matmul(out.ap, l.ap, r.ap, start=(i == 0), stop=(i == n - 1))
            return inst

        self.op("pe", fn, reads=[p[0] for p in pairs] + [p[1] for p in pairs], writes=[out])

    def transpose(self, out, in_, ident):
        self.op("pe", lambda e: e.transpose(out.ap, in_.ap, ident.ap), reads=[in_, ident], writes=[out])

    def act(self, out, in_, func, bias=None, scale=None, accum=None):
        kw = {}
        reads = [in_]
        writes = [out]
        if bias is not None:
            if isinstance(bias, V):
                kw["bias"] = bias.ap
                reads.append(bias)
            else:
                kw["bias"] = bias
        if scale is not None:
            if isinstance(scale, V):
                kw["scale"] = scale.ap
                reads.append(scale)
            else:
                kw["scale"] = scale
        if accum is not None:
            kw["accum_out"] = accum.ap
            writes.append(accum)
        self.op("act", lambda e: e.activation(out.ap, in_.ap, func, **kw), reads=reads, writes=writes)

    def ts(self, out, in0, s1, s2, op0, op1=None, eng="dve"):
        reads = [in0]
        a1 = s1
        a2 = s2
        if isinstance(s1, V):
            reads.append(s1)
            a1 = s1.ap
        if isinstance(s2, V):
            reads.append(s2)
            a2 = s2.ap
        if op1 is None:
            self.op(eng, lambda e: e.tensor_scalar(out.ap, in0.ap, a1, a2, op0), reads=reads, writes=[out])
        else:
            self.op(eng, lambda e: e.tensor_scalar(out.ap, in0.ap, a1, a2, op0, op1), reads=reads, writes=[out])

    def tt(self, out, in0, in1, op, eng="dve"):
        self.op(eng, lambda e: e.tensor_tensor(out.ap, in0.ap, in1.ap, op), reads=[in0, in1], writes=[out])

    def stt(self, out, in0, scalar, in1, op0, op1):
        reads = [in0, in1]
        sc = scalar
        if isinstance(scalar, V):
            reads.append(scalar)
            sc = scalar.ap
        self.op("dve", lambda e: e.scalar_tensor_tensor(out.ap, in0.ap, sc, in1.ap, op0, op1),
                reads=reads, writes=[out])

    def copy(self, out, in_, eng="dve"):
        if eng == "act":
            self.op("act", lambda e: e.copy(out.ap, in_.ap), reads=[in_], writes=[out])
        else:
            self.op(eng, lambda e: e.tensor_copy(out.ap, in_.ap), reads=[in_], writes=[out])

    def recip(self, out, in_):
        self.op("dve", lambda e: e.reciprocal(out.ap, in_.ap), reads=[in_], writes=[out])

    def memset(self, out, val, eng="dve"):
        self.op(eng, lambda e: e.memset(out.ap, val), reads=[], writes=[out])


class Ctx:
    def __init__(self, nc, sched):
        self.nc = nc
        self.s = sched
        self.uid = 0

    def sb(self, es, name, shape, dtype=F32):
        self.uid += 1
        h = es.enter_context(self.nc.sbuf_tensor("%s_%d" % (name, self.uid), list(shape), dtype))
        return Buf(name, h)

    def ps(self, es, name, shape=(128, 512), dtype=F32):
        self.uid += 1
        h = es.enter_context(self.nc.psum_tensor("%s_%d" % (name, self.uid), list(shape), dtype))
        b = Buf(name, h)
        b.is_psum = True
        return b

    def dram(self, name, shape, dtype=F32, kind="Internal"):
        h = self.nc.dram_tensor(name, list(shape), dtype, kind=kind)
        return Buf(name, h)


def pipeline(items, stages):
    n, k = len(items), len(stages)
    for t in range(n + k - 1):
        for j in range(k - 1, -1, -1):
            i = t - j
            if 0 <= i < n:
                stages[j](items[i])


def bc_mid(v_buf, base_off, pstep, nparts, n_outer, outer_step, n_inner):
    return v_buf.view(base_off, [[pstep, nparts], [outer_step, n_outer], [0, n_inner]])


E_NCOL = 5152
OFF_Z = 0
OFF_XBC = 1024
OFF_DT = 2560
OFF_Q = 2592
OFF_KV = 3616
OFF_G = 4128
OFF_QS = 5152
OFF_KR = 6176
OFF_KSR = 6688
E_NCOL_EXT = 7200


ORDER = ["p1", "p2a", "p2b", "p2c", "p2d", "p2e", "p2f", "p2g", "p2h", "p3", "p4", "p5", "all"]


def build_program(debug=(), stop="all"):
    def go(tag):
        return ORDER.index(tag) <= ORDER.index(stop)
    nc = bass.Bass("TRN2", target_bir_lowering=False)
    s = Sched(nc)
    cx = Ctx(nc, s)
    dbg = set(debug)

    def din(name, shape):
        return Buf(name, nc.dram_tensor(name, list(shape), F32, kind="ExternalInput"))

    def dout(name, shape):
        return Buf(name, nc.dram_tensor(name, list(shape), F32, kind="ExternalOutput"))

    def scratch(name, shape, dtype=F32):
        if name in dbg:
            return dout(name, shape)
        return Buf(name, nc.dram_tensor(name, list(shape), dtype))

    xin = din("xin", [T, D])
    cvecT = din("cvecT", [128, 2, 8])
    consts = din("consts", [128, 6, 512])
    rope = din("rope", [128, 2, L])
    e_ada_w = din("e_ada_w", [D, 3 * D])
    e_ada_b = din("e_ada_b", [1, 3 * D])
    e_norm_wT = din("e_norm_wT", [128, 8])
    e_w_in = din("e_w_in", [D, E_NCOL_EXT])
    e_conv_wT = din("e_conv_wT", [128, 12, 5])
    e_conv_bT = din("e_conv_bT", [128, 12])
    e_dt_bias = din("e_dt_bias", [1, 32])
    e_a_log = din("e_a_log", [1, 32])
    e_d_skip = din("e_d_skip", [1, 16])
    e_ssd_norm_wT = din("e_ssd_norm_wT", [128, 8])
    e_sink = din("e_sink", [128, 8])
    e_w_out = din("e_w_out", [2 * D, D])
    o_ada_w = din("o_ada_w", [D, 3 * D])
    o_ada_b = din("o_ada_b", [1, 3 * D])
    o_norm_wT = din("o_norm_wT", [128, 8])
    o_w_in = din("o_w_in", [D, 2 * D])
    s5_lam = din("s5_lam", [128, 2, 3, 32])
    s5_b = din("s5_b", [128, 2, 32, 16])
    s5_c = din("s5_c", [128, 2, 32, 16])
    o_d_skip = din("o_d_skip", [1, D])
    o_glu_w = din("o_glu_w", [D, D])
    o_glu_b = din("o_glu_b", [1, D])
    o_w_out = din("o_w_out", [D, D])
    final_norm_w = din("final_norm_w", [1, D])
    out_t = dout("out", [L, D])

    XS = scratch("XS", [T, 1024])
    BTOK = scratch("BTOK", [T, 256], BF16)
    BT = scratch("BT", [2, 128, T], BF16)
    CT = scratch("CT", [2, 128, T], BF16)
    SZ = scratch("SZ", [T, 1024])
    QR = scratch("QR", [8, 128, L], BF16)
    QC = scratch("QC", [8, 128, CTX], BF16)
    KR = scratch("KR", [4, 128, L], BF16)
    KC = scratch("KC", [4, 128, CTX], BF16)
    VT = scratch("VT", [T, 256], BF16)
    SG = scratch("SG", [8, 128, T])
    YF = scratch("YF", [T, 1024])
    YT = scratch("YT", [16, 128, T], BF16)
    X1 = scratch("X1", [T, 1024])
    U = scratch("U", [T, 1024])
    SG1 = scratch("SG1", [T, 1024])
    YTOK = scratch("YTOK", [T, 1024])
    KFP = scratch("KFP", [64, 16, 15, 16], BF16)
    KBR = scratch("KBR", [64, 16, 15, 16], BF16)
    HT = scratch("HT", [8, 128, T]) if "HT" in dbg else None
    DTD = scratch("DTD", [T, 32]) if "DTD" in dbg else None
    MODD = scratch("MODD", [4, 128, 24]) if "MODD" in dbg else None

    with ExitStack() as top:
        banks = [cx.ps(top, "bank%d" % i) for i in range(8)]
        cst = cx.sb(top, "cst", [128, 6, 512])
        s.dma(cst.full(), consts.full())
        ident = cst[:, 0, 0:128]
        tri = cst[:, 1, 0:128]
        utri = cst[:, 2, 0:128]
        ones = cst[:, 3, 0:128]
        onesb_t = cx.sb(top, "onesb", [128, 128], BF16)
        s.memset(onesb_t.full(), 1.0)
        onesb = onesb_t.full()
        modT = [[cx.sb(top, "modT%d%d" % (l, w), [128, 24]) for w in range(2)] for l in range(2)]
        gate_bc = [[cx.sb(top, "gate%d%d" % (l, w), [128, 1024]) for w in range(2)] for l in range(2)]
        scs = cx.sb(top, "scs", [128, 2, 8])

        def adaln_phase(layer, ada_w, ada_b):
            with ExitStack() as es:
                aw = [cx.sb(es, "aw%d" % k, [128, 3 * D]) for k in range(8)]
                ab = cx.sb(es, "ab", [1, 3 * D])
                modrow = [cx.sb(es, "modrow%d" % w, [1, 3 * D]) for w in range(2)]
                if layer == 0:
                    cv = cx.sb(es, "cv", [128, 2, 8])
                    s.dma(cv.full(), cvecT.full())
                    s.act(scs.full(), cv.full(), AF.Silu)
                for k in range(8):
                    s.dma(aw[k].full(), ada_w[k * 128:(k + 1) * 128, :])
                s.dma(ab.full(), ada_b.full())
                bi = 0
                for w in range(2):
                    for fg in range(6):
                        bk = banks[bi % 8]
                        bi += 1
                        s.mm(bk[0:1, :], [(scs[:, w, k:k + 1], aw[k][:, fg * 512:(fg + 1) * 512]) for k in range(8)])
                        s.tt(modrow[w][0:1, fg * 512:(fg + 1) * 512], bk[0:1, :], ab[0:1, fg * 512:(fg + 1) * 512], ALU.add)
                for w in range(2):
                    bk = banks[bi % 8]
                    bi += 1
                    for fc in range(24):
                        s.mm(bk[:, 2 * fc:2 * fc + 2], [(modrow[w][0:1, fc * 128:(fc + 1) * 128], cst[0:1, 3, 0:2])])
                    s.copy(modT[layer][w].full(), bk.view(0, [[512, 128], [2, 24]]))
                    for hh in range(2):
                        bk2 = banks[bi % 8]
                        bi += 1
                        s.mm(bk2.full(), [(cst[0:1, 3, 0:128], modrow[w][0:1, 2048 + hh * 512:2048 + (hh + 1) * 512])])
                        s.copy(gate_bc[layer][w][:, hh * 512:(hh + 1) * 512], bk2.full(), eng="act")
                    if MODD is not None:
                        s.dma(MODD[layer * 2 + w], modT[layer][w].full())
                s.flush()

        adaln_phase(0, e_ada_w, e_ada_b)

        with ExitStack() as l0:
            DT = cx.sb(l0, "DT", [128, NT, 32])
            DTA = cx.sb(l0, "DTA", [128, NT, 32])
            nw = cx.sb(l0, "nw", [128, 8])
            sc1 = [cx.sb(l0, "sc1_%d" % w, [128, 8]) for w in range(2)]
            s.dma(nw.full(), e_norm_wT.full())
            for w in range(2):
                s.stt(sc1[w].full(), modT[0][w][:, 8:16], 1.0, nw.full(), ALU.add, ALU.mult)

            wo = [cx.sb(l0, "wo%d" % k, [128, D], BF16) for k in range(16)]
            hts = ExitStack()
            hT = [cx.sb(hts, "hT%d" % k, [128, T], BF16) for k in range(8)]
            with ExitStack() as es:
                xt = [cx.sb(es, "xt%d" % i, [128, D]) for i in range(3)]
                xn = [cx.sb(es, "xn%d" % i, [128, D]) for i in range(3)]
                junk = cx.sb(es, "junk", [128, D])
                st = [cx.sb(es, "st%d" % i, [128, 4]) for i in range(3)]
                def n0(i):
                    x_, n_, st_ = xt[i % 3], xn[i % 3], st[i % 3]
                    s.dma(x_.full(), xin[i * 128:(i + 1) * 128, :])
                    s.act(junk.full(), x_.full(), AF.Square, accum=st_[:, 0:1])
                    s.ts(st_[:, 1:2], st_[:, 0:1], 1.0 / D, EPS, ALU.mult, ALU.add)
                    s.act(st_[:, 2:3], st_[:, 1:2], AF.Sqrt)
                    s.recip(st_[:, 3:4], st_[:, 2:3])
                    s.ts(n_.full(), x_.full(), st_[:, 3:4], None, ALU.mult)

                def n1(i):
                    w = 1 if i < 2 else 0
                    n_ = xn[i % 3]
                    for half in range(2):
                        bk = banks[(2 * i + half) % 8]
                        for kk in range(4):
                            k = half * 4 + kk
                            s.transpose(bk[:, kk * 128:(kk + 1) * 128], n_[:, k * 128:(k + 1) * 128], ident)
                        for kk in range(4):
                            k = half * 4 + kk
                            s.act(hT[k][:, i * 128:(i + 1) * 128], bk[:, kk * 128:(kk + 1) * 128], AF.Identity,
                                  bias=modT[0][w][:, k:k + 1], scale=sc1[w][:, k:k + 1])

                pipeline(list(range(NT)), [n0, n1])
                if HT is not None:
                    for k in range(8):
                        s.dma(HT[k], hT[k].full())
                s.flush()

            with ExitStack() as es:
                WB = 256
                NWB, PF = 6, 4
                wbuf = [cx.sb(es, "wbuf%d" % i, [128, 8, WB], BF16) for i in range(NWB)]
                wplan = [(OFF_XBC + 256 * k, 256) for k in range(6)]
                for qc in range(8):
                    wplan += [(OFF_Q + qc * 128, 128), (OFF_QS + qc * 128, 128)]
                for j in range(4):
                    wplan += [(OFF_KR + j * 128, 128), (OFF_KSR + j * 128, 128)]
                wplan += [(OFF_G + 256 * k, 256) for k in range(4)]
                wplan += [(OFF_Z + 256 * k, 256) for k in range(4)]
                wplan += [(OFF_KV + 256, 256), (OFF_DT, 32)]
                wstate = {"i": 0, "issued": 0}

                def _issue(n):
                    col0, ncol = wplan[n]
                    wb = wbuf[n % NWB]
                    s.dma(wb[:, :, 0:ncol], e_w_in.view(col0, [[E_NCOL_EXT, 128], [128 * E_NCOL_EXT, 8], [1, ncol]]), q="pool")

                def load_w(col0, ncol=WB):
                    i = wstate["i"]
                    wstate["i"] += 1
                    assert wplan[i] == (col0, ncol), (i, wplan[i], col0, ncol)
                    while wstate["issued"] < min(i + PF + 1, len(wplan)):
                        _issue(wstate["issued"])
                        wstate["issued"] += 1
                    return wbuf[i % NWB]

                bstate = {"i": 0}

                def nbank():
                    bk = banks[bstate["i"] % 8]
                    bstate["i"] += 1
                    return bk

                def fm_mm(wb, cc, t0, n):
                    bk = nbank()
                    s.mm(bk[:, 0:n], [(wb[:, k, cc * 128:(cc + 1) * 128], hT[k][:, t0:t0 + n]) for k in range(8)])
                    return bk

                xraws = [cx.sb(es, "xraw%d" % i, [128, T]) for i in range(2)]
                accs = [cx.sb(es, "acc%d" % i, [128, T]) for i in range(2)]
                acc = accs[0]
                accbs = [cx.sb(es, "accb%d" % i, [128, T], BF16) for i in range(2)]
                accb = accbs[0]
                rc_i = {"i": 0}
                tmp1s = [cx.sb(es, "tmp1_%d" % i, [128, 512]) for i in range(2)]
                tmp2s = [cx.sb(es, "tmp2_%d" % i, [128, 512]) for i in range(2)]
                stg = [cx.sb(es, "stg%d" % i, [128, 4, 128]) for i in range(2)]
                stgb = [cx.sb(es, "stgb%d" % i, [128, 4, 128], BF16) for i in range(2)]
                rp = cx.sb(es, "rp", [128, 2, L])
                cw = cx.sb(es, "cw", [128, 12, 5])
                cb = cx.sb(es, "cb", [128, 12])
                dtb = cx.sb(es, "dtb", [128, 32])
                abc = cx.sb(es, "abc", [128, 32])
                s.dma(rp.full(), rope.full())
                s.dma(cw.full(), e_conv_wT.full())
                s.dma(cb.full(), e_conv_bT.full())
                s.dma(dtb.full(), e_dt_bias.view(0, [[0, 128], [1, 32]]))
                s.dma(abc.full(), e_a_log.view(0, [[0, 128], [1, 32]]))
                s.act(abc.full(), abc.full(), AF.Exp)
                s.ts(abc.full(), abc.full(), -1.0, None, ALU.mult)
                stg_i = {"i": 0}

                def transposes_to(dst, col0, src, lowp=False):
                    for i0 in range(0, NT, 4):
                        nb = min(4, NT - i0)
                        bk = nbank()
                        for ii in range(nb):
                            i = i0 + ii
                            s.transpose(bk[:, ii * 128:(ii + 1) * 128], src[:, i * 128:(i + 1) * 128], ident)
                        sg_ = (stgb if lowp else stg)[stg_i["i"] % 2]
                        stg_i["i"] += 1
                        s.copy(sg_[:, 0:nb, :], bk.view(0, [[512, 128], [128, nb], [1, 128]]), eng="act")
                        ncols = dst.h.shape[1]
                        s.dma(dst.view(i0 * 128 * ncols + col0, [[ncols, 128], [128 * ncols, nb], [1, 128]]),
                              sg_[:, 0:nb, :])

                wb_of = {}

                def xa(fc):
                    if fc % 2 == 0:
                        wb_of[fc // 2] = load_w(OFF_XBC + fc * 128)
                    wb = wb_of[fc // 2]
                    xraw = xraws[fc % 2]
                    for (t0, n) in TG:
                        bk = fm_mm(wb, fc % 2, t0, n)
                        s.copy(xraw[:, t0:t0 + n], bk[:, 0:n], eng="act")

                def xb(fc):
                    xraw, acc = xraws[fc % 2], accs[fc % 2]
                    s.ts(acc.full(), xraw.full(), cw[:, fc, 2:3], cb[:, fc:fc + 1], ALU.mult, ALU.add)
                    for kk in (0, 1, 3, 4):
                        d_ = kk - 2
                        for (s0, sl) in ((0, CTX), (CTX, L)):
                            lo = max(s0, s0 - d_)
                            hi = min(s0 + sl, s0 + sl - d_)
                            s.stt(acc[:, lo:hi], xraw[:, lo + d_:hi + d_], cw[:, fc, kk:kk + 1], acc[:, lo:hi],
                                  ALU.mult, ALU.add)
                    s.act(acc.full(), acc.full(), AF.Silu)
                    if fc < 8:
                        transposes_to(XS, fc * 128, acc)
                    elif fc < 10:
                        s.copy(accb.full(), acc.full(), eng="act")
                        s.dma(BT[fc - 8], accb.full())
                        transposes_to(BTOK, (fc - 8) * 128, acc, lowp=True)
                    else:
                        s.copy(accb.full(), acc.full(), eng="act")
                        s.dma(CT[fc - 10], accb.full())

                pipeline(list(range(12 if go('p2a') else 0)), [xa, xb])

                def rope_chunk(col_plain, col_swap, dst_rot, dst_ctx):
                    accb = accbs[rc_i["i"] % 2]
                    rc_i["i"] += 1
                    wa = load_w(col_plain, 128)
                    wsw = load_w(col_swap, 128)
                    for gi, (t0, n) in enumerate(TG):
                        bka = fm_mm(wa, 0, t0, n)
                        if gi == 0:
                            s.copy(accb[:, 0:CTX], bka[:, 0:CTX], eng="act")
                            continue
                        bkb = fm_mm(wsw, 0, t0, n)
                        l0 = t0 - CTX
                        tmp1, tmp2 = tmp1s[gi % 2], tmp2s[gi % 2]
                        s.tt(tmp1.full(), bka.full(), rp[:, 0, l0:l0 + 512], ALU.mult)
                        s.tt(tmp2.full(), bkb.full(), rp[:, 1, l0:l0 + 512], ALU.mult)
                        s.tt(accb[:, t0:t0 + n], tmp1.full(), tmp2.full(), ALU.add)
                    s.dma(dst_ctx, accb[:, 0:CTX])
                    s.dma(dst_rot, accb[:, CTX:T])

                for qc in range(8 if go('p2b') else 0):
                    rope_chunk(OFF_Q + qc * 128, OFF_QS + qc * 128, QR[qc], QC[qc])
                for j in range(4 if go('p2c') else 0):
                    rope_chunk(OFF_KR + j * 128, OFF_KSR + j * 128, KR[j], KC[j])

                for gc in range(8 if go('p2d') else 0):
                    acc = accs[gc % 2]
                    if gc % 2 == 0:
                        wb = load_w(OFF_G + gc * 128)
                    for (t0, n) in TG:
                        bk = fm_mm(wb, gc % 2, t0, n)
                        s.act(acc[:, t0:t0 + n], bk[:, 0:n], AF.Silu)
                    s.dma(SG[gc], acc.full())

                NT_E = NT if go('p2e') else 0
                wz = [load_w(OFF_Z + i * 256) for i in range(4)]
                for i in range(NT_E):
                    z_a = accs[i % 2]
                    for half in range(2):
                        bk = nbank()
                        for q4 in range(2):
                            wbz = wz[half * 2 + q4]
                            s.mm(bk[:, q4 * 256:(q4 + 1) * 256],
                                 [(hT[k][:, i * 128:(i + 1) * 128], wbz[:, k, :]) for k in range(8)])
                        s.act(z_a[:, half * 512:(half + 1) * 512], bk.full(), AF.Silu)
                    s.dma(SZ[i * 128:(i + 1) * 128, :], z_a[:, 0:1024])
                wv = load_w(OFF_KV + 256)
                wdt = load_w(OFF_DT, 32)
                vt = [cx.sb(es, "vt%d" % i, [128, 256], BF16) for i in range(2)]
                for i in range(NT if go('p2f') else 0):
                    bk = nbank()
                    s.mm(bk[:, 0:256], [(hT[k][:, i * 128:(i + 1) * 128], wv[:, k, :]) for k in range(8)])
                    s.copy(vt[i % 2].full(), bk[:, 0:256], eng="act")
                    s.dma(VT[i * 128:(i + 1) * 128, :], vt[i % 2].full())
                for i in range(NT if go('p2g') else 0):
                    bk = nbank()
                    s.mm(bk[:, 0:32], [(hT[k][:, i * 128:(i + 1) * 128], wdt[:, k, 0:32]) for k in range(8)])
                    s.tt(DT[:, i, :], bk[:, 0:32], dtb.full(), ALU.add)
                    if go('p2h'):
                        s.act(DT[:, i, :], DT[:, i, :], AF.Exp)
                        s.ts(DT[:, i, :], DT[:, i, :], 1.0, None, ALU.add)
                        s.act(DT[:, i, :], DT[:, i, :], AF.Ln)
                    s.tt(DTA[:, i, :], DT[:, i, :], abc.full(), ALU.mult)
                    if DTD is not None:
                        s.dma(DTD[i * 128:(i + 1) * 128, :], DT[:, i, :])
                s.flush()
            hts.close()
            for k in range(16):
                s.dma(wo[k].full(), e_w_out[k * 128:(k + 1) * 128, :], q="pool")

            with ExitStack() as es:
                nb_ = {"i": 0}

                def nbank():
                    bk = banks[nb_["i"] % 8]
                    nb_["i"] += 1
                    return bk

                N3 = 3
                N4 = 4
                xs_t = [cx.sb(es, "xs_t%d" % i, [128, 1024]) for i in range(N4)]
                b_t = [cx.sb(es, "b_t%d" % i, [128, 256], BF16) for i in range(N3)]
                bt_t = [cx.sb(es, "bt_t%d" % i, [128, 2, 128], BF16) for i in range(N3)]
                ct_t = [cx.sb(es, "ct_t%d" % i, [128, 2, 128], BF16) for i in range(N3)]
                yf_t = [cx.sb(es, "yf_t%d" % i, [128, 1024]) for i in range(N4)]
                sz_t = [cx.sb(es, "sz_t%d" % i, [128, 1024]) for i in range(2)]
                MTs = [cx.sb(es, "MT%d" % i, [128, 2048], BF16) for i in range(N3)]
                xcs = [cx.sb(es, "xc%d" % i, [128, 1024], BF16) for i in range(N3)]
                xcds = [cx.sb(es, "xcd%d" % i, [128, 1024], BF16) for i in range(N3)]
                tmpos = [cx.sb(es, "tmpo%d" % i, [128, 1024]) for i in range(N3)]
                ytots = [cx.sb(es, "ytot%d" % i, [128, 1024]) for i in range(2)]
                sms = [cx.sb(es, "sm%d" % i, [128, 4, 16]) for i in range(N3)]
                st3s = [cx.sb(es, "st3_%d" % i, [128, 4]) for i in range(N3)]
                ystgs = [cx.sb(es, "ystg%d" % i, [128, 8, 128], BF16) for i in range(2)]
                dtatris = [cx.sb(es, "dtatri%d" % i, [128, 2048]) for i in range(2)]
                decTs = [cx.sb(es, "decT%d" % i, [128, 2048]) for i in range(2)]
                cb_sbs = [cx.sb(es, "cb_sb%d" % i, [128, 256]) for i in range(2)]
                junk = cx.sb(es, "junk3", [128, 1024])
                Hs = [cx.sb(es, "Hs%d" % g, [128, 512]) for g in range(2)]
                Hb = [cx.sb(es, "Hb%d" % g, [128, 512], BF16) for g in range(2)]
                dsk = cx.sb(es, "dsk", [128, 16])
                snw = cx.sb(es, "snw", [128, 8])
                cm1 = cx.sb(es, "cm1", [128, 1])
                s.memset(cm1.full(), -1.0)
                s.dma(dsk.full(), e_d_skip.view(0, [[0, 128], [1, 16]]))
                s.dma(snw.full(), e_ssd_norm_wT.full())

                def bc3(buf, off, pstep, n1, s1, n2, s2):
                    return buf.view(off, [[pstep, 128], [s1, n1], [s2, n2]])

                n_ch = NT if go("p3") else 0
                for d_ in range(2):
                    order = list(range(NT)) if d_ == 0 else [1, 0] + list(range(NT - 1, 1, -1))
                    order = order[:n_ch]
                    TRIoff = 512 if d_ == 0 else 1024
                    TRIv = tri if d_ == 0 else utri
                    negm = cst[:, 4 + d_, :]
                    for g in range(2):
                        s.memset(Hs[g].full(), 0.0)
                        s.memset(Hb[g].full(), 0.0)

                    def stA(item, d_=d_, TRIoff=TRIoff, TRIv=TRIv, negm=negm):
                        ci, i = item
                        p3, p2, p4 = ci % N3, ci % 2, ci % N4
                        xs_, b_, bt_, ct_ = xs_t[p4], b_t[p3], bt_t[p3], ct_t[p3]
                        MT, xc, xcd, sm = MTs[p3], xcs[p3], xcds[p3], sms[p3]
                        dtatri, decT, cb_sb = dtatris[p2], decTs[p2], cb_sbs[p2]
                        s.dma(xs_.full(), XS[i * 128:(i + 1) * 128, :])
                        s.dma(b_.full(), BTOK[i * 128:(i + 1) * 128, :])
                        s.dma(bt_.full(), BT.view(i * 128, [[T, 128], [128 * T, 2], [1, 128]]))
                        s.dma(ct_.full(), CT.view(i * 128, [[T, 128], [128 * T, 2], [1, 128]]))
                        if d_ == 1:
                            s.dma(yf_t[p4].full(), YF[i * 128:(i + 1) * 128, :])
                        dta_i = DTA[:, i, d_ * 16:(d_ + 1) * 16]
                        doff = i * 32 + d_ * 16
                        s.tt(bc3(dtatri, 0, 2048, 16, 128, 128, 1), bc3(DTA, doff, NT * 32, 16, 1, 128, 0),
                             bc3(cst, TRIoff, 3072, 16, 0, 128, 1), ALU.mult, eng="pool")
                        bs = nbank()
                        s.mm(bs[:, 0:16], [(TRIv, dta_i)])
                        s.mm(bs[:, 16:32], [(ones, dta_i)])
                        na, ea, de, cd = sm[:, 0, :], sm[:, 1, :], sm[:, 2, :], sm[:, 3, :]
                        s.ts(na, bs[:, 0:16], -1.0, None, ALU.mult)
                        s.act(ea, bs[:, 0:16], AF.Exp)
                        s.tt(de, bs[:, 16:32], na, ALU.add)
                        s.act(de, de, AF.Exp)
                        s.act(cd, bs[:, 16:32], AF.Exp)
                        for hq in range(4):
                            bq = nbank()
                            s.mm(bq.full(), [(ones, dtatri[:, hq * 512:(hq + 1) * 512]), (ident, negm)])
                            for hh in range(4):
                                h = hq * 4 + hh
                                s.act(decT[:, h * 128:(h + 1) * 128], bq[:, hh * 128:(hh + 1) * 128], AF.Exp,
                                      bias=sm[:, 0, h:h + 1])
                        bc = nbank()
                        for g in range(2):
                            s.mm(bc[:, g * 128:(g + 1) * 128], [(bt_[:, g, :], ct_[:, g, :])])
                        s.copy(cb_sb.full(), bc[:, 0:256], eng="act")
                        for g in range(2):
                            s.tt(bc3(MT, g * 1024, 2048, 8, 128, 128, 1), bc3(decT, g * 1024, 2048, 8, 128, 128, 1),
                                 bc3(cb_sb, g * 128, 256, 8, 0, 128, 1), ALU.mult)
                        s.tt(bc3(xc, 0, 1024, 16, 64, 64, 1), bc3(xs_, 0, 1024, 16, 64, 64, 1),
                             bc3(DT, doff, NT * 32, 16, 1, 64, 0), ALU.mult, eng="pool")
                        s.tt(bc3(xcd, 0, 1024, 16, 64, 64, 1), bc3(xc, 0, 1024, 16, 64, 64, 1),
                             bc3(sm, 32, 64, 16, 1, 64, 0), ALU.mult, eng="pool")
                        if d_ == 1:
                            s.tt(bc3(tmpos[p3], 0, 1024, 16, 64, 64, 1), bc3(xs_, 0, 1024, 16, 64, 64, 1),
                                 bc3(dsk, 0, 16, 16, 1, 64, 0), ALU.mult, eng="pool")
                            s.tt(yf_t[p4].full(), yf_t[p4].full(), tmpos[p3].full(), ALU.add, eng="pool")

                    def stB(item, d_=d_):
                        ci, i = item
                        p3 = ci % N3
                        b_, ct_ = b_t[p3], ct_t[p3]
                        MT, xc, xcd, sm, tmpo, ytot = MTs[p3], xcs[p3], xcds[p3], sms[p3], tmpos[p3], ytots[ci % 2]
                        ydst = yf_t[ci % N4] if d_ == 0 else ytot
                        if d_ == 1:
                            s.dma(sz_t[ci % 2].full(), SZ[i * 128:(i + 1) * 128, :])
                        for g in range(2):
                            by = nbank()
                            for hh in range(8):
                                h = g * 8 + hh
                                s.mm(by[:, hh * 64:(hh + 1) * 64], [(MT[:, h * 128:(h + 1) * 128], xc[:, h * 64:(h + 1) * 64])])
                            bo = nbank()
                            s.mm(bo.full(), [(ct_[:, g, :], Hb[g].full())])
                            s.tt(bc3(tmpo, g * 512, 1024, 8, 64, 64, 1), bc3(bo, 0, 512, 8, 64, 64, 1),
                                 bc3(sm, 16 + g * 8, 64, 8, 1, 64, 0), ALU.mult)
                            s.tt(ydst[:, g * 512:(g + 1) * 512], by.full(), tmpo[:, g * 512:(g + 1) * 512], ALU.add)
                        for g in range(2):
                            bst = nbank()
                            s.mm(bst.full(), [(b_[:, g * 128:(g + 1) * 128], xcd[:, g * 512:(g + 1) * 512])])
                            s.tt(bc3(Hs[g], 0, 512, 8, 64, 64, 1), bc3(Hs[g], 0, 512, 8, 64, 64, 1),
                                 bc3(sm, 48 + g * 8, 64, 8, 1, 64, 0), ALU.mult)
                            s.tt(Hs[g].full(), Hs[g].full(), bst.full(), ALU.add)
                            s.copy(Hb[g].full(), Hs[g].full(), eng="act")
                        if d_ == 0:
                            s.dma(YF[i * 128:(i + 1) * 128, :], yf_t[ci % N4].full())

                    def stC(item, d_=d_):
                        if d_ == 0:
                            return
                        ci, i = item
                        p3, p2 = ci % N3, ci % 2
                        ytot, sz_, st3, ystg = ytots[ci % 2], sz_t[ci % 2], st3s[p3], ystgs[p2]
                        s.tt(ytot.full(), ytot.full(), yf_t[ci % N4].full(), ALU.add)
                        s.tt(ytot.full(), ytot.full(), sz_.full(), ALU.mult)
                        s.act(junk.full(), ytot.full(), AF.Square, accum=st3[:, 0:1])
                        s.ts(st3[:, 1:2], st3[:, 0:1], 1.0 / 1024, EPS, ALU.mult, ALU.add)
                        s.act(st3[:, 2:3], st3[:, 1:2], AF.Sqrt)
                        s.recip(st3[:, 3:4], st3[:, 2:3])
                        s.act(ytot.full(), ytot.full(), AF.Copy, scale=st3[:, 3:4])
                        for half in range(2):
                            bk = nbank()
                            for kk in range(4):
                                k = half * 4 + kk
                                s.transpose(bk[:, kk * 128:(kk + 1) * 128], ytot[:, k * 128:(k + 1) * 128], ident)
                            for kk in range(4):
                                k = half * 4 + kk
                                s.act(ystg[:, k, :], bk[:, kk * 128:(kk + 1) * 128], AF.Copy, scale=snw[:, k:k + 1])
                        s.dma(YT.view(i * 128, [[T, 128], [128 * T, 8], [1, 128]]), ystg.full())

                    pipeline(list(enumerate(order)), [stA, (lambda it: None), stB, stC])
                s.flush()

            with ExitStack() as es:
                nb_ = {"i": 0}

                def nbank():
                    bk = banks[nb_["i"] % 8]
                    nb_["i"] += 1
                    return bk

                J2 = 2
                qr_ts = [cx.sb(es, "qr_t%d" % i, [128, 2, L], BF16) for i in range(J2)]
                qc_ts = [cx.sb(es, "qc_t%d" % i, [128, 2, CTX], BF16) for i in range(J2)]
                kr_ts = [cx.sb(es, "kr_t%d" % i, [128, L], BF16) for i in range(J2)]
                kc_ts = [cx.sb(es, "kc_t%d" % i, [128, CTX], BF16) for i in range(J2)]
                v_ts = [cx.sb(es, "v_t%d" % i, [128, NT, 64], BF16) for i in range(J2)]
                v2s = [cx.sb(es, "v2_%d" % i, [128, NT, 128], BF16) for i in range(J2)]
                sg_ts = [cx.sb(es, "sg_t%d" % i, [128, 2, T]) for i in range(J2)]
                asts = [cx.sb(es, "ast%d" % i, [128, 2, T], BF16) for i in range(J2)]
                NP = 4
                pt = [[cx.sb(es, "pt%d_%d" % (a, b), [128, 512], BF16) for b in range(5)] for a in range(NP)]
                rds = [cx.sb(es, "rd%d" % i, [128, 256]) for i in range(2)]
                aos = [cx.sb(es, "ao%d" % i, [128, 256]) for i in range(2)]
                es_pp = cx.sb(es, "es_pp", [128, 8])
                c8 = cx.sb(es, "c8", [128, 1])
                s.memset(c8.full(), 0.125)
                s.dma(es_pp.full(), e_sink.full())
                s.act(es_pp.full(), es_pp.full(), AF.Exp)
                ATT_DBG = [int(v) for v in os.environ.get("ATT_DBG", "4,18,4").split(",")]
                items = []
                for j in range(ATT_DBG[0] if go("p4") else 0):
                    qbs = ([("c", 0), ("c", 1)] + [("l", b) for b in range(16)])[:ATT_DBG[1]]
                    for qi, (kind, bi) in enumerate(qbs):
                        items.append((len(items), j, kind, bi, qi == 0, qi == len(qbs) - 1))

                def keys_of(kind, bi):
                    keys = [("c", 0, None), ("c", 1, None)]
                    if kind == "l":
                        if bi > 0:
                            keys.append(("l", bi - 1, "prev"))
                        keys.append(("l", bi, None))
                        if bi < 15:
                            keys.append(("l", bi + 1, "next"))
                    return keys

                def atA(item):
                    n, j, kind, bi, first, last = item
                    js = j % J2
                    qr_t, qc_t, kr_t, kc_t, v_t, v2, sg_t = qr_ts[js], qc_ts[js], kr_ts[js], kc_ts[js], v_ts[js], v2s[js], sg_ts[js]
                    if first:
                        s.dma(qr_t.full(), QR.view(2 * j * 128 * L, [[L, 128], [128 * L, 2], [1, L]]))
                        s.dma(qc_t.full(), QC.view(2 * j * 128 * CTX, [[CTX, 128], [128 * CTX, 2], [1, CTX]]))
                        s.dma(kr_t.full(), KR[j])
                        s.dma(kc_t.full(), KC[j])
                        s.dma(v_t.full(), VT.view(j * 64, [[256, 128], [128 * 256, NT], [1, 64]]))
                        s.dma(sg_t.full(), SG.view(2 * j * 128 * T, [[T, 128], [128 * T, 2], [1, T]]))
                        s.copy(v2[:, :, 0:64], v_t.full(), eng="pool")
                        s.copy(v2[:, :, 64:128], v_t.full(), eng="pool")
                    qsrc, q0 = (qc_t, bi * 128) if kind == "c" else (qr_t, bi * 128)
                    pts = pt[n % NP]
                    qw = qsrc.h.shape[2]
                    for ki, (kk, kb, msk) in enumerate(keys_of(kind, bi)):
                        ksrc = kc_t if kk == "c" else kr_t
                        for par in range(2):
                            p0 = par * 64
                            bs = nbank()
                            s.mm(bs[:, 0:256],
                                 [(ksrc[p0:p0 + 64, kb * 128:(kb + 1) * 128],
                                   qsrc.view(p0 * 2 * qw + q0, [[2 * qw, 64], [qw, 2], [1, 128]]))])
                            s.act(pts[ki][:, par * 256:(par + 1) * 256], bs[:, 0:256], AF.Exp, scale=c8[:, 0:1])
                        if msk is not None:
                            moff = 1024 if msk == "prev" else 512
                            s.tt(pts[ki].view(0, [[512, 128], [128, 4], [1, 128]]),
                                 pts[ki].view(0, [[512, 128], [128, 4], [1, 128]]),
                                 cst.view(moff, [[3072, 128], [0, 4], [1, 128]]), ALU.mult, eng="pool")

                def atB(item):
                    n, j, kind, bi, first, last = item
                    js = j % J2
                    v2, sg_t, ast = v2s[js], sg_ts[js], asts[js]
                    tok0 = bi * 128 if kind == "c" else CTX + bi * 128
                    keys = keys_of(kind, bi)
                    pts = pt[n % NP]
                    rd, ao = rds[n % 2], aos[n % 2]
                    vt_idx = [(kb if kk == "c" else 2 + kb) for (kk, kb, _) in keys]
                    bn = nbank()
                    s.mm(bn.full(), [(v2[:, vt_idx[ki], :], pts[ki].full()) for ki in range(len(keys))])
                    bd = nbank()
                    s.mm(bd.full(), [(onesb, pts[ki].full()) for ki in range(len(keys))])
                    for par in range(2):
                        p0 = par * 64
                        for c in range(2):
                            s.ts(rd[p0:p0 + 64, c * 128:(c + 1) * 128],
                                 bd[p0:p0 + 64, par * 256 + c * 128:par * 256 + (c + 1) * 128],
                                 es_pp[p0:p0 + 64, 2 * j + c:2 * j + c + 1], None, ALU.add)
                    s.recip(rd.full(), rd.full())
                    for par in range(2):
                        p0 = par * 64
                        s.tt(ao[p0:p0 + 64, :], bn[p0:p0 + 64, par * 256:(par + 1) * 256], rd[p0:p0 + 64, :], ALU.mult)
                    s.tt(ast.view(tok0, [[2 * T, 128], [T, 2], [1, 128]]),
                         ao.view(0, [[256, 128], [128, 2], [1, 128]]),
                         sg_t.view(tok0, [[2 * T, 128], [T, 2], [1, 128]]), ALU.mult)
                    if last:
                        s.dma(YT.view((8 + 2 * j) * 128 * T, [[T, 128], [128 * T, 2], [1, T]]), ast.full())

                pipeline(items, [atA, (lambda it: None), atB])
                s.flush()

            with ExitStack() as es:
                nb_ = {"i": 0}

                def nbank():
                    bk = banks[nb_["i"] % 8]
                    nb_["i"] += 1
                    return bk

                yt = [cx.sb(es, "yt%d" % i, [128, 16, 128], BF16) for i in range(2)]
                xt = [cx.sb(es, "xt5_%d" % i, [128, D]) for i in range(2)]
                x1t = [cx.sb(es, "x1t%d" % i, [128, D]) for i in range(2)]
                tmp5s = [cx.sb(es, "tmp5_%d" % i, [128, 512]) for i in range(2)]
                for i in range(NT if go("p5") else 0):
                    w = 1 if i < 2 else 0
                    y_, x_, o_ = yt[i % 2], xt[i % 2], x1t[i % 2]
                    s.dma(y_.full(), YT.view(i * 128, [[T, 128], [128 * T, 16], [1, 128]]))
                    s.dma(x_.full(), xin[i * 128:(i + 1) * 128, :])
                    for half in range(2):
                        tmp5 = tmp5s[half]
                        bk = nbank()
                        s.mm(bk.full(), [(y_[:, fc, :], wo[fc][:, half * 512:(half + 1) * 512]) for fc in range(16)])
                        s.tt(tmp5.full(), bk.full(), gate_bc[0][w][:, half * 512:(half + 1) * 512], ALU.mult)
                        s.tt(o_[:, half * 512:(half + 1) * 512], tmp5.full(), x_[:, half * 512:(half + 1) * 512], ALU.add)
                    s.dma(X1[i * 128:(i + 1) * 128, :], o_.full())
                s.flush()

        if go("all"):
            adaln_phase(1, o_ada_w, o_ada_b)
        with ExitStack() as l1:
            if not go("all"):
                return nc
            nb_ = {"i": 0}

            def nbank():
                bk = banks[nb_["i"] % 8]
                nb_["i"] += 1
                return bk

            with ExitStack() as es:
                nw = cx.sb(es, "nw1", [128, 8])
                sc1 = [cx.sb(es, "sc1b_%d" % w, [128, 8]) for w in range(2)]
                s.dma(nw.full(), o_norm_wT.full())
                for w in range(2):
                    s.stt(sc1[w].full(), modT[1][w][:, 8:16], 1.0, nw.full(), ALU.add, ALU.mult)
                hT = [cx.sb(es, "hTb%d" % k, [128, T], BF16) for k in range(8)]
                xt = [cx.sb(es, "xtb%d" % i, [128, D]) for i in range(3)]
                xn = [cx.sb(es, "xnb%d" % i, [128, D]) for i in range(3)]
                junk = cx.sb(es, "junkb", [128, D])
                st = [cx.sb(es, "stb%d" % i, [128, 4]) for i in range(3)]
                def m0(i):
                    x_, n_, st_ = xt[i % 3], xn[i % 3], st[i % 3]
                    s.dma(x_.full(), X1[i * 128:(i + 1) * 128, :])
                    s.act(junk.full(), x_.full(), AF.Square, accum=st_[:, 0:1])
                    s.ts(st_[:, 1:2], st_[:, 0:1], 1.0 / D, EPS, ALU.mult, ALU.add)
                    s.act(st_[:, 2:3], st_[:, 1:2], AF.Sqrt)
                    s.recip(st_[:, 3:4], st_[:, 2:3])
                    s.ts(n_.full(), x_.full(), st_[:, 3:4], None, ALU.mult)

                def m1(i):
                    w = 1 if i < 2 else 0
                    n_ = xn[i % 3]
                    for half in range(2):
                        bk = nbank()
                        for kk in range(4):
                            k = half * 4 + kk
                            s.transpose(bk[:, kk * 128:(kk + 1) * 128], n_[:, k * 128:(k + 1) * 128], ident)
                        for kk in range(4):
                            k = half * 4 + kk
                            s.act(hT[k][:, i * 128:(i + 1) * 128], bk[:, kk * 128:(kk + 1) * 128], AF.Identity,
                                  bias=modT[1][w][:, k:k + 1], scale=sc1[w][:, k:k + 1])

                pipeline(list(range(NT)), [m0, m1])
                wq = [cx.sb(es, "wq%d" % i, [128, 8, 256], BF16) for i in range(8)]
                for q8 in range(8):
                    s.dma(wq[q8].full(), o_w_in.view(q8 * 256, [[2 * D, 128], [128 * 2 * D, 8], [1, 256]]), q="pool")
                ot = [cx.sb(es, "ot%d" % i, [128, D]) for i in range(2)]
                oi = 0
                for which in range(2):
                    for i in range(NT):
                        if which == 1 and i < 2:
                            continue
                        o_ = ot[oi % 2]
                        oi += 1
                        for half in range(2):
                            bk = nbank()
                            for q4 in range(2):
                                s.mm(bk[:, q4 * 256:(q4 + 1) * 256],
                                     [(hT[k][:, i * 128:(i + 1) * 128], wq[which * 4 + half * 2 + q4][:, k, :]) for k in range(8)])
                            if which == 0:
                                s.copy(o_[:, half * 512:(half + 1) * 512], bk.full(), eng="act")
                            else:
                                s.act(o_[:, half * 512:(half + 1) * 512], bk.full(), AF.Silu)
                        s.dma((U if which == 0 else SG1)[i * 128:(i + 1) * 128, :], o_.full())
                s.flush()

            L1S = os.environ.get('L1S', 'z')
            if L1S == 'a':
                return nc
            with ExitStack() as es:
                lam = cx.sb(es, "lam", [128, 2, 3, 32])
                bprm = cx.sb(es, "bprm", [128, 2, 32, 16])
                cprm = cx.sb(es, "cprm", [128, 2, 32, 16])
                s.dma(lam.full(), s5_lam.full())
                s.dma(bprm.full(), s5_b.full())
                s.dma(cprm.full(), s5_c.full())
                kc = cx.sb(es, "kconst", [128, 4])
                s.memset(kc[:, 0:1], 1.0 / 16)
                s.memset(kc[:, 1:2], math.pi / 2)
                s.memset(kc[:, 2:3], 0.0)
                s.memset(kc[:, 3:4], 1.0)
                W64 = [128, 2, 32]

                def t64(name):
                    return cx.sb(es, name, W64)

                def lv(i):
                    return lam.view(i * 32, [[192, 128], [96, 2], [1, 32]])

                dt_ = t64("dt_"); mag = t64("mag"); th = t64("th"); cs = t64("cs"); sn = t64("sn")
                t_a = t64("t_a"); t_b = t64("t_b"); t_c = t64("t_c")
                abre = t64("abre"); abim = t64("abim"); cre = t64("cre"); cim = t64("cim")
                s.act(dt_.full(), lv(2), AF.Exp)
                s.tt(t_a.full(), lv(0), dt_.full(), ALU.mult)
                s.act(mag.full(), t_a.full(), AF.Exp)
                s.tt(th.full(), lv(1), dt_.full(), ALU.mult)
                s.act(sn.full(), th.full(), AF.Sin, scale=kc[:, 0:1])
                s.act(cs.full(), th.full(), AF.Sin, scale=kc[:, 0:1], bias=kc[:, 1:2])
                for _ in range(4):
                    s.tt(t_a.full(), cs.full(), cs.full(), ALU.mult)
                    s.tt(t_b.full(), sn.full(), sn.full(), ALU.mult)
                    s.tt(t_c.full(), sn.full(), cs.full(), ALU.mult)
                    s.tt(cs.full(), t_a.full(), t_b.full(), ALU.subtract)
                    s.ts(sn.full(), t_c.full(), 2.0, None, ALU.mult)
                s.tt(abre.full(), mag.full(), cs.full(), ALU.mult)
                s.tt(abim.full(), mag.full(), sn.full(), ALU.mult)
                PW = cx.sb(es, "PW", [128, 2, 9, 64])

                def pw(ri, k):
                    return PW.view((ri * 9 + k) * 64, [[2 * 9 * 64, 128], [32, 2], [1, 32]])

                s.memset(PW[:, 0, 0, :], 1.0)
                s.memset(PW[:, 1, 0, :], 0.0)
                for k in range(8):
                    s.tt(t_a.full(), pw(0, k), abre.full(), ALU.mult)
                    s.tt(t_b.full(), pw(1, k), abim.full(), ALU.mult)
                    s.tt(pw(0, k + 1), t_a.full(), t_b.full(), ALU.subtract)
                    s.tt(t_a.full(), pw(0, k), abim.full(), ALU.mult)
                    s.tt(t_b.full(), pw(1, k), abre.full(), ALU.mult)
                    s.tt(pw(1, k + 1), t_a.full(), t_b.full(), ALU.add)
                s.ts(t_c.full(), abre.full(), -1.0, None, ALU.add)
                s.tt(t_a.full(), lv(0), lv(0), ALU.mult)
                s.tt(t_b.full(), lv(1), lv(1), ALU.mult)
                s.tt(t_a.full(), t_a.full(), t_b.full(), ALU.add)
                s.recip(dt_.full(), t_a.full())
                s.tt(t_a.full(), t_c.full(), lv(0), ALU.mult)
                s.tt(t_b.full(), abim.full(), lv(1), ALU.mult)
                s.tt(t_a.full(), t_a.full(), t_b.full(), ALU.add)
                s.tt(cre.full(), t_a.full(), dt_.full(), ALU.mult)
                s.tt(t_a.full(), abim.full(), lv(0), ALU.mult)
                s.tt(t_b.full(), t_c.full(), lv(1), ALU.mult)
                s.tt(t_a.full(), t_a.full(), t_b.full(), ALU.subtract)
                s.tt(cim.full(), t_a.full(), dt_.full(), ALU.mult)
                BB = cx.sb(es, "BB", [128, 2, 2, 512])
                tb1 = cx.sb(es, "tb1", [128, 512])
                tb2 = cx.sb(es, "tb2", [128, 512])

                def bb(ri, d_, g0=0, ng=32):
                    return BB.view((ri * 2 + d_) * 512 + g0 * 16, [[2048, 128], [16, ng], [1, 16]])

                def v3(buf, off, pstep, n1, s1, n2, s2):
                    return buf.view(off, [[pstep, 128], [s1, n1], [s2, n2]])

                def prm(buf, ri, g0=0, ng=32):
                    return buf.view(ri * 512 + g0 * 16, [[1024, 128], [16, ng], [1, 16]])

                def cf(buf, d_, g0=0, ng=32, n2=16):
                    return buf.view(d_ * 32 + g0, [[64, 128], [1, ng], [0, n2]])

                t1v = v3(tb1, 0, 512, 32, 16, 16, 1)
                t2v = v3(tb2, 0, 512, 32, 16, 16, 1)
                for d_ in range(2):
                    s.tt(t1v, prm(bprm, 0), cf(cre, d_), ALU.mult)
                    s.tt(t2v, prm(bprm, 1), cf(cim, d_), ALU.mult)
                    s.tt(bb(0, d_), t1v, t2v, ALU.subtract)
                    s.tt(t1v, prm(bprm, 1), cf(cre, d_), ALU.mult)
                    s.tt(t2v, prm(bprm, 0), cf(cim, d_), ALU.mult)
                    s.tt(bb(1, d_), t1v, t2v, ALU.add)
                LA = cx.sb(es, "LA", [128, 2, 32, 2])
                LB = cx.sb(es, "LB", [128, 2, 32, 2])
                for ri in range(2):
                    s.copy(LA.view(ri, [[128, 128], [64, 2], [2, 32]]), pw(0, 8))
                s.ts(LB.view(0, [[128, 128], [64, 2], [2, 32]]), pw(1, 8), -1.0, None, ALU.mult)
                s.copy(LB.view(1, [[128, 128], [64, 2], [2, 32]]), pw(1, 8))
                zt_ = cx.sb(es, "zt_", [16, 112], BF16)
                s.memset(zt_.full(), 0.0)
                s.flush()

                if L1S == 'b':
                    return nc
                for b in range(4 if L1S not in ('c1', 'd1', 'e1', 'f1', 'g1') else 1):
                    g0 = 8 * b
                    with ExitStack() as bs_:
                        CAB = cx.sb(bs_, "CAB", [128, 2, 2, 8 * 144])
                        WST = cx.sb(bs_, "WST", [128, 8, 2, 2, 2, 64], BF16)
                        TF = cx.sb(bs_, "TF", [128, 16, 128], BF16)
                        TB = cx.sb(bs_, "TB", [128, 16, 128], BF16)
                        CABb = cx.sb(bs_, "CABb", [128, 2, 2, 8 * 144], BF16)

                        with ExitStack() as tmp:
                            WT = cx.sb(tmp, "WT", [128, 2, 2, 8 * 128])
                            c1 = cx.sb(tmp, "c1", [128, 128])
                            c2 = cx.sb(tmp, "c2", [128, 128])
                            c1v = v3(c1, 0, 128, 8, 16, 16, 1)
                            c2v = v3(c2, 0, 128, 8, 16, 16, 1)
                            for d_ in range(2):
                                for ss in range(8):
                                    p_ = 7 - ss if d_ == 0 else ss
                                    pr = PW.view((0 * 9 + p_) * 64 + d_ * 32 + g0, [[1152, 128], [1, 8], [0, 16]])
                                    pi_ = PW.view((1 * 9 + p_) * 64 + d_ * 32 + g0, [[1152, 128], [1, 8], [0, 16]])
                                    o_re = WT.view((d_ * 2 + 0) * 1024 + ss * 16, [[4096, 128], [128, 8], [1, 16]])
                                    o_im = WT.view((d_ * 2 + 1) * 1024 + ss * 16, [[4096, 128], [128, 8], [1, 16]])
                                    s.tt(c1v, bb(0, d_, g0, 8), pr, ALU.mult)
                                    s.tt(c2v, bb(1, d_, g0, 8), pi_, ALU.mult)
                                    s.tt(o_re, c1v, c2v, ALU.subtract)
                                    s.tt(c1v, bb(1, d_, g0, 8), pr, ALU.mult)
                                    s.tt(c2v, bb(0, d_, g0, 8), pi_, ALU.mult)
                                    s.tt(o_im, c1v, c2v, ALU.add)
                            for gh in range(2):
                                p0 = gh * 64
                                for gq in range(8):
                                    bk = nbank()
                                    for d_ in range(2):
                                        for ri in range(2):
                                            sl = d_ * 2 + ri
                                            s.transpose(bk[:, sl * 64:(sl + 1) * 64],
                                                        WT.view(p0 * 4096 + (d_ * 2 + ri) * 1024 + gq * 128, [[4096, 64], [1, 128]]),
                                                        cst[p0:p0 + 64, 0, p0:p0 + 64])
                                    s.copy(WST.view(((gq * 2 + gh) * 4) * 64, [[4096, 128], [1, 256]]), bk[:, 0:256], eng="act")
                            s.flush()

                        KSB = cx.sb(bs_, "KSB", [16, 2, 16, 128], BF16)

                        def setup_part2():
                            c1v = ysb.view(0, [[512, 128], [16, 8], [1, 16]])
                            c2v = ysb.view(128, [[512, 128], [16, 8], [1, 16]])
                            for d_ in range(2):
                                for idx in range(9):
                                    p_ = idx if d_ == 0 else 8 - idx
                                    pr = PW.view((0 * 9 + p_) * 64 + d_ * 32 + g0, [[1152, 128], [1, 8], [0, 16]])
                                    pi_ = PW.view((1 * 9 + p_) * 64 + d_ * 32 + g0, [[1152, 128], [1, 8], [0, 16]])
                                    o_re = CAB.view((0 * 2 + d_) * 1152 + idx * 16, [[4608, 128], [144, 8], [1, 16]])
                                    o_im = CAB.view((1 * 2 + d_) * 1152 + idx * 16, [[4608, 128], [144, 8], [1, 16]])
                                    s.tt(c1v, prm(cprm, 0, g0, 8), pr, ALU.mult, eng="pool")
                                    s.tt(c2v, prm(cprm, 1, g0, 8), pi_, ALU.mult, eng="pool")
                                    s.tt(o_re, c1v, c2v, ALU.subtract, eng="pool")
                                    s.tt(c1v, prm(cprm, 0, g0, 8), pi_, ALU.mult, eng="pool")
                                    s.tt(c2v, prm(cprm, 1, g0, 8), pr, ALU.mult, eng="pool")
                                    s.tt(c1v, c1v, c2v, ALU.add, eng="pool")
                                    s.ts(o_im, c1v, -1.0, None, ALU.mult, eng="pool")
                            s.copy(CABb.full(), CAB.full(), eng="pool")
                            for gh in range(2):
                                p0 = gh * 64
                                for d_ in range(2):
                                    for gqq in range(2):
                                        bk = nbank()
                                        for q4 in range(4):
                                            gq = gqq * 4 + q4
                                            i0 = 0 if d_ == 0 else 1
                                            s.mm(bk[0:16, q4 * 128:(q4 + 1) * 128],
                                                 [(BB.view(p0 * 2048 + (0 * 2 + d_) * 512 + (g0 + gq) * 16, [[2048, 64], [1, 16]]),
                                                   CAB.view(p0 * 4608 + (0 * 2 + d_) * 1152 + gq * 144 + i0 * 16, [[4608, 64], [1, 128]])),
                                                  (BB.view(p0 * 2048 + (1 * 2 + d_) * 512 + (g0 + gq) * 16, [[2048, 64], [1, 16]]),
                                                   CAB.view(p0 * 4608 + (1 * 2 + d_) * 1152 + gq * 144 + i0 * 16, [[4608, 64], [1, 128]]))])
                                        s.copy(KSB.view(d_ * 2048 + (2 * gqq * 4 + gh) * 128, [[4096, 16], [256, 4], [1, 128]]),
                                               bk.view(0, [[512, 16], [128, 4], [1, 128]]), eng="act")
                            gbase = 16 * b
                            s.dma(KFP.view(gbase * 3840 + 7 * 16, [[240, 16], [3840, 16], [1, 128]]), KSB[:, 0, :, :])
                            s.dma(KBR.view(gbase * 3840, [[240, 16], [3840, 16], [1, 128]]), KSB[:, 1, :, :])
                            s.dma(KFP.view(gbase * 3840, [[240, 16], [3840, 16], [1, 112]]), zt_.view(0, [[112, 16], [0, 16], [1, 112]]))
                            s.dma(KBR.view(gbase * 3840 + 128, [[240, 16], [3840, 16], [1, 112]]), zt_.view(0, [[112, 16], [0, 16], [1, 112]]))
                            for ss in range(8):
                                s.dma(TF[ss * 16:(ss + 1) * 16, :, :], KFP.view(gbase * 3840 + (7 - ss) * 16, [[240, 16], [3840, 16], [1, 128]]))
                                s.dma(TB[ss * 16:(ss + 1) * 16, :, :], KBR.view(gbase * 3840 + (7 - ss) * 16, [[240, 16], [3840, 16], [1, 128]]))

                        u8b = cx.sb(bs_, "u8b", [128, 8, 256])
                        u8g = cx.sb(bs_, "u8g", [128, 16, 128])
                        U8T = cx.sb(bs_, "U8T", [128, 16, 288], BF16)
                        SSb = [cx.sb(bs_, "SSb%d" % i, [128, 16, 256], BF16) for i in range(2)]
                        NCOL = 326
                        PS = 16 * NCOL
                        SSD = [cx.sb(bs_, "SS%d" % i, [128, 8, 2, NCOL]) for i in range(2)]
                        CAR = [cx.sb(bs_, "CAR%d" % i, [128, 15, 8, 2]) for i in range(2)]
                        A36 = [cx.sb(bs_, "A36_%d" % i, [128, 8, 2]) for i in range(2)]
                        B36 = [cx.sb(bs_, "B36_%d" % i, [128, 8, 2]) for i in range(2)]
                        y8b = cx.sb(bs_, "y8b", [128, 8, 256])
                        ysb = cx.sb(bs_, "ysb", [128, 512])
                        TT1 = [cx.sb(bs_, "TT1_%d" % i, [128, 17, 8, 2]) for i in range(2)]
                        TT2 = [cx.sb(bs_, "TT2_%d" % i, [128, 17, 8, 2]) for i in range(2)]
                        for (j0, nj) in ((0, 32), (32, 128), (160, 128)):
                            s.dma(u8b[0:nj, :, :], U.view(8 * j0 * 1024 + 256 * b, [[8192, nj], [1024, 8], [1, 256]]))
                            s.copy(u8g.view(0, [[2048, nj], [128, 16], [16, 8], [1, 16]]),
                                   u8b.view(0, [[2048, nj], [16, 16], [256, 8], [1, 16]]), eng="act")
                            for gq4 in range(4):
                                bk = nbank()
                                for q4 in range(4):
                                    gi = gq4 * 4 + q4
                                    s.transpose(bk[:, q4 * 128:q4 * 128 + nj],
                                                u8g.view(128 * gi, [[2048, nj], [1, 128]]), cst[0:nj, 0, 0:nj])
                                s.copy(U8T.view(gq4 * 4 * 288 + j0, [[16 * 288, 128], [288, 4], [1, nj]]),
                                       bk.view(0, [[512, 128], [128, 4], [1, nj]]), eng="act")
                        if L1S in ('d', 'd1'):
                            s.flush()
                            continue
                        s.memset(SSD[0].view(0, [[PS, 128], [NCOL, 16], [1, 1]]), 0.0)
                        s.memset(SSD[0].view(289, [[PS, 128], [NCOL, 16], [1, 37]]), 0.0)
                        s.memset(SSD[1].view(288, [[PS, 128], [NCOL, 16], [1, 38]]), 0.0)
                        s.memset(SSD[0].view(289, [[PS, 128], [2 * NCOL, 8], [1, 1]]), 1.0)
                        s.memset(SSD[1].view(288 + 18 - 1, [[PS, 128], [2 * NCOL, 8], [1, 1]]), 1.0)
                        for gq in range(8):
                            for gh in range(2):
                                gi = 2 * gq + gh
                                p0 = gh * 64
                                for d_ in range(2):
                                    for ri in range(2):
                                        bk = nbank()
                                        s.mm(bk[p0:p0 + 64, 0:288],
                                             [(WST.view((((gq * 2 + gh) * 2 + d_) * 2 + ri) * 64, [[4096, 128], [1, 64]]),
                                               U8T[:, gi, :])])
                                        so = p0 * PS + (gq * 2 + ri) * NCOL
                                        if d_ == 0:
                                            s.copy(SSD[0].view(so + 1, [[PS, 64], [1, 288]]), bk[p0:p0 + 64, 0:288], eng="act")
                                        else:
                                            s.copy(SSD[1].view(so + 256, [[PS, 64], [1, 32]]), bk[p0:p0 + 64, 0:32], eng="act")
                                            s.copy(SSD[1].view(so, [[PS, 64], [1, 256]]), bk[p0:p0 + 64, 32:288], eng="act")
                        if L1S in ('e', 'e1'):
                            s.flush()
                            continue
                        DS = 8 * 2 * 289
                        setup_part2()
                        RI, GQ = NCOL, 2 * NCOL
                        SEG, NSEG = 18, 16
                        REC_ENG2 = os.environ.get('REC2', 'dve')

                        def cplx_step(items):
                            engs = ("dve", REC_ENG2)
                            for n_, (pv, psw, cv, ca, cb_, t1_, t2_) in enumerate(items):
                                s.tt(t1_, pv, ca, ALU.mult, eng=engs[n_ % 2])
                                s.tt(t2_, psw, cb_, ALU.mult, eng=engs[n_ % 2])
                            for n_, (pv, psw, cv, ca, cb_, t1_, t2_) in enumerate(items):
                                s.tt(t1_, t1_, t2_, ALU.add, eng=engs[n_ % 2])
                            for n_, (pv, psw, cv, ca, cb_, t1_, t2_) in enumerate(items):
                                if cv is not None:
                                    s.tt(cv, cv, t1_, ALU.add, eng=engs[n_ % 2])

                        def segv(SS, col, nseg):
                            return (SS.view(col, [[PS, 128], [SEG, nseg], [GQ, 8], [RI, 2]]),
                                    SS.view(col + RI, [[PS, 128], [SEG, nseg], [GQ, 8], [-RI, 2]]))

                        def coef(buf, d_, nseg):
                            return buf.view(d_ * 64 + g0 * 2, [[128, 128], [0, nseg], [2, 8], [1, 2]])

                        TTP = 17 * 16

                        for k in range(1, SEG):
                            items = []
                            for d_ in range(2):
                                pc = k if d_ == 0 else SEG - k
                                cc = k + 1 if d_ == 0 else SEG - 1 - k
                                pv, psw = segv(SSD[d_], pc, NSEG + 1)
                                cv, _ = segv(SSD[d_], cc, NSEG + 1)
                                items.append((pv, psw, cv, coef(LA, d_, NSEG + 1), coef(LB, d_, NSEG + 1), TT1[d_].full(), TT2[d_].full()))
                            cplx_step(items)
                        items = []
                        for d_ in range(2):
                            clast = 288 + SEG if d_ == 0 else 288
                            pv, psw = segv(SSD[d_], clast, 1)
                            items.append((pv, psw, None, coef(LA, d_, 1), coef(LB, d_, 1),
                                          TT1[d_].view(0, [[TTP, 128], [16, 1], [2, 8], [1, 2]]),
                                          TT2[d_].view(0, [[TTP, 128], [16, 1], [2, 8], [1, 2]])))
                        cplx_step(items)
                        for d_ in range(2):
                            l36re = TT1[d_].view(0, [[TTP, 128], [2, 8], [0, 2]])
                            s.copy(A36[d_].full(), l36re)
                            s.ts(B36[d_][:, :, 0:1], TT1[d_].view(1, [[TTP, 128], [2, 8], [1, 1]]), -1.0, None, ALU.mult)
                            s.copy(B36[d_][:, :, 1:2], TT1[d_].view(1, [[TTP, 128], [2, 8], [1, 1]]))
                        for step in range(1, NSEG):
                            items = []
                            for d_ in range(2):
                                if d_ == 0:
                                    m = step
                                    cc, pc = SEG * m + SEG, SEG * m
                                else:
                                    m = NSEG - 1 - step
                                    cc, pc = SEG * m, SEG * m + SEG
                                pv, psw = segv(SSD[d_], pc, 1)
                                cv, _ = segv(SSD[d_], cc, 1)
                                items.append((pv, psw, cv,
                                              A36[d_].view(0, [[16, 128], [0, 1], [2, 8], [1, 2]]),
                                              B36[d_].view(0, [[16, 128], [0, 1], [2, 8], [1, 2]]),
                                              TT1[d_].view(0, [[TTP, 128], [16, 1], [2, 8], [1, 2]]),
                                              TT2[d_].view(0, [[TTP, 128], [16, 1], [2, 8], [1, 2]])))
                            cplx_step(items)
                        items = []
                        for d_ in range(2):
                            pv, psw = segv(SSD[d_], SEG, NSEG - 1)
                            items.append((pv, psw, None, coef(LA, d_, NSEG - 1), coef(LB, d_, NSEG - 1),
                                          CAR[d_].full(), TT2[d_].view(0, [[TTP, 128], [16, NSEG - 1], [2, 8], [1, 2]])))
                        cplx_step(items)
                        NI = SEG - 1
                        for d_ in range(2):
                            SS = SSD[d_]
                            sb0 = SEG + 1 if d_ == 0 else 1

                            def sview(ri):
                                return SS.view(sb0 + ri * RI, [[PS, 128], [SEG, NSEG - 1], [GQ, 8], [1, NI]])

                            def tview(ri):
                                return SS.view(289 + ri * RI, [[PS, 128], [0, NSEG - 1], [GQ, 8], [1, NI]])

                            def cview(ri):
                                return CAR[d_].view(ri, [[(NSEG - 1) * 16, 128], [16, NSEG - 1], [2, 8], [0, NI]])

                            wshape = [[2048, 128], [8 * NI, NSEG - 1], [NI, 8], [1, NI]]
                            w1 = (u8g if d_ == 0 else u8b).view(0, wshape)
                            w2 = y8b.view(0, wshape)
                            s.tt(w1, tview(0), cview(0), ALU.mult)
                            s.tt(w2, tview(1), cview(1), ALU.mult)
                            s.tt(w1, w1, w2, ALU.subtract)
                            s.tt(sview(0), sview(0), w1, ALU.add)
                            s.tt(w1, tview(0), cview(1), ALU.mult)
                            s.tt(w2, tview(1), cview(0), ALU.mult)
                            s.tt(w1, w1, w2, ALU.add)
                            s.tt(sview(1), sview(1), w1, ALU.add)
                        s.copy(SSb[0].full(), SSD[0].view(32, [[PS, 128], [NCOL, 16], [1, 256]]), eng="act")
                        s.copy(SSb[1].full(), SSD[1].view(1, [[PS, 128], [NCOL, 16], [1, 256]]), eng="pool")
                        if L1S in ('f', 'f1'):
                            s.flush()
                            continue
                        for tt_ in range(2):
                            j0 = 32 + 128 * tt_
                            m0 = 128 * tt_
                            for gh in range(2):
                                p0 = gh * 64
                                for gqq in range(2):
                                    bx = nbank()
                                    by = nbank()
                                    for q4 in range(4):
                                        gq = gqq * 4 + q4
                                        gi = 2 * gq + gh
                                        s.mm(bx[:, q4 * 128:(q4 + 1) * 128],
                                             [(U8T[:, gi, j0:j0 + 128], TF[:, gi, :]), (U8T[:, gi, j0:j0 + 128], TB[:, gi, :])])
                                        pairs = []
                                        for d_ in range(2):
                                            c0 = m0
                                            i0 = 1 if d_ == 0 else 0
                                            for ri in range(2):
                                                so = p0 * 4096 + (gq * 2 + ri) * 256 + c0
                                                pairs.append((SSb[d_].view(so, [[4096, 64], [1, 128]]),
                                                              CABb.view(p0 * 4608 + (ri * 2 + d_) * 1152 + gq * 144 + i0 * 16, [[4608, 64], [1, 128]])))
                                        s.mm(by[:, q4 * 128:(q4 + 1) * 128], pairs)
                                    s.copy(ysb.full(), by.full(), eng="act")
                                    s.tt(y8b.view(32 * gqq * 4 + 16 * gh, [[2048, 128], [32, 4], [256, 8], [1, 16]]),
                                         bx.view(0, [[512, 128], [128, 4], [16, 8], [1, 16]]),
                                         ysb.view(0, [[512, 128], [128, 4], [16, 8], [1, 16]]), ALU.add)
                            s.dma(YTOK.view((CTX + 8 * m0) * 1024 + 256 * b, [[8192, 128], [1024, 8], [1, 256]]), y8b.full())
                        s.flush()

            if L1S in ('g', 'g1'):
                return nc
            with ExitStack() as es:
                gw = [cx.sb(es, "gw%d" % k, [128, D], BF16) for k in range(8)]
                ow = [cx.sb(es, "ow%d" % k, [128, D], BF16) for k in range(8)]
                dskb = cx.sb(es, "dskb", [128, D])
                glbb = cx.sb(es, "glbb", [128, D])
                fnwb = cx.sb(es, "fnwb", [128, D])
                kg = cx.sb(es, "kg", [128, 1])
                s.memset(kg.full(), 2.0 * math.sqrt(2.0 / math.pi))
                for k in range(8):
                    s.dma(gw[k].full(), o_glu_w[k * 128:(k + 1) * 128, :], q="pool")
                    s.dma(ow[k].full(), o_w_out[k * 128:(k + 1) * 128, :], q="pool")
                s.dma(dskb.full(), o_d_skip.view(0, [[0, 128], [1, D]]))
                s.dma(glbb.full(), o_glu_b.view(0, [[0, 128], [1, D]]))
                s.dma(fnwb.full(), final_norm_w.view(0, [[0, 128], [1, D]]))
                NB3 = 4
                ya = [cx.sb(es, "ya%d" % i, [128, D]) for i in range(NB3)]
                ua = [cx.sb(es, "ua%d" % i, [128, D]) for i in range(NB3)]
                sga = [cx.sb(es, "sga%d" % i, [128, D]) for i in range(NB3)]
                xa = [cx.sb(es, "xa%d" % i, [128, D]) for i in range(NB3)]
                w1s = [cx.sb(es, "w1_%d" % i, [128, D]) for i in range(NB3)]
                w2s = [cx.sb(es, "w2_%d" % i, [128, D]) for i in range(NB3)]
                w3s = [cx.sb(es, "w3_%d" % i, [128, D]) for i in range(NB3)]
                tTs = [cx.sb(es, "tT_%d" % i, [128, 8, 128], BF16) for i in range(2 * NB3)]
                sts = [cx.sb(es, "st10_%d" % i, [128, 4]) for i in range(NB3)]

                def transp8(src, tT):
                    for half in range(2):
                        bk = nbank()
                        for kk in range(4):
                            k = half * 4 + kk
                            s.transpose(bk[:, kk * 128:(kk + 1) * 128], src[:, k * 128:(k + 1) * 128], ident)
                        s.copy(tT[:, half * 4:(half + 1) * 4, :], bk.view(0, [[512, 128], [128, 4], [1, 128]]), eng="act")

                TAILN = int(os.environ.get('TAILN', NT))

                def bufs(i):
                    b_ = i % NB3
                    return ya[b_], ua[b_], sga[b_], xa[b_], w1s[b_], w2s[b_], w3s[b_], tTs[2 * b_], tTs[2 * b_ + 1], sts[b_]

                def stage0(i):
                    y_, u_, g_, x_, w1, w2, w3, tTa, tTb, st = bufs(i)
                    s.dma(y_.full(), YTOK[i * 128:(i + 1) * 128, :])
                    s.dma(u_.full(), U[i * 128:(i + 1) * 128, :])
                    s.dma(g_.full(), SG1[i * 128:(i + 1) * 128, :])
                    s.dma(x_.full(), X1[i * 128:(i + 1) * 128, :])
                    s.tt(w1.full(), u_.full(), dskb.full(), ALU.mult)
                    s.tt(y_.full(), y_.full(), w1.full(), ALU.add)
                    s.tt(w1.full(), y_.full(), y_.full(), ALU.mult)
                    s.ts(w1.full(), w1.full(), 0.044715, 1.0, ALU.mult, ALU.add)
                    s.tt(w1.full(), w1.full(), y_.full(), ALU.mult)
                    s.act(w1.full(), w1.full(), AF.Sigmoid, scale=kg[:, 0:1])
                    s.tt(w2.full(), y_.full(), w1.full(), ALU.mult)
                    transp8(w2, tTa)

                def stage1(i):
                    y_, u_, g_, x_, w1, w2, w3, tTa, tTb, st = bufs(i)
                    for half in range(2):
                        bk = nbank()
                        s.mm(bk.full(), [(tTa[:, k, :], gw[k][:, half * 512:(half + 1) * 512]) for k in range(8)])
                        s.tt(w1[:, half * 512:(half + 1) * 512], bk.full(), glbb[:, half * 512:(half + 1) * 512], ALU.add)
                    s.act(w1.full(), w1.full(), AF.Sigmoid)
                    s.tt(w2.full(), w2.full(), w1.full(), ALU.mult)
                    s.tt(w2.full(), w2.full(), g_.full(), ALU.mult)
                    transp8(w2, tTb)

                def stage2(i):
                    y_, u_, g_, x_, w1, w2, w3, tTa, tTb, st = bufs(i)
                    for half in range(2):
                        bk = nbank()
                        s.mm(bk.full(), [(tTb[:, k, :], ow[k][:, half * 512:(half + 1) * 512]) for k in range(8)])
                        s.tt(w1[:, half * 512:(half + 1) * 512], bk.full(), gate_bc[1][0][:, half * 512:(half + 1) * 512], ALU.mult)
                    s.tt(w3.full(), w1.full(), x_.full(), ALU.add)
                    s.act(w1.full(), w3.full(), AF.Square, accum=st[:, 0:1])
                    s.ts(st[:, 1:2], st[:, 0:1], 1.0 / D, EPS, ALU.mult, ALU.add)
                    s.act(st[:, 2:3], st[:, 1:2], AF.Sqrt)
                    s.recip(st[:, 3:4], st[:, 2:3])
                    s.act(w3.full(), w3.full(), AF.Copy, scale=st[:, 3:4])
                    s.tt(w2.full(), w3.full(), fnwb.full(), ALU.mult)
                    s.dma(out_t[(i - 2) * 128:(i - 1) * 128, :], w2.full())

                pipeline(list(range(2, TAILN)), [stage0, (lambda i: None), stage1, stage2])
                s.flush()

    return nc


def _consts():
    c = np.zeros((128, 6, 512), np.float32)
    j = np.arange(128)[:, None]
    l = np.arange(128)[None, :]
    c[:, 0, :128] = np.eye(128, dtype=np.float32)
    c[:, 1, :128] = (j <= l)
    c[:, 2, :128] = (j >= l)
    c[:, 3, :] = 1.0
    nf = np.where(l < j, -30000.0, 0.0).astype(np.float32)
    nb = np.where(l > j, -30000.0, 0.0).astype(np.float32)
    c[:, 4, :] = np.tile(nf, (1, 4))
    c[:, 5, :] = np.tile(nb, (1, 4))
    return c


def _rope_tables():
    rows = L // 64
    row = np.repeat(np.arange(rows, dtype=np.float32), 64)
    col = np.tile(np.arange(64, dtype=np.float32), rows)
    n_freq = 16
    inv = (np.float32(10000.0) ** (-np.arange(n_freq, dtype=np.float32) / n_freq)).astype(np.float32)
    ang = np.concatenate([row[:, None] * inv, col[:, None] * inv], axis=-1).astype(np.float32)
    cos = np.cos(ang).astype(np.float32)
    sin = np.sin(ang).astype(np.float32)
    cosT = np.zeros((128, L), np.float32)
    sinT = np.zeros((128, L), np.float32)
    for h2 in range(2):
        for half in range(2):
            p0 = h2 * 64 + half * 32
            cosT[p0:p0 + 32] = cos.T
            sinT[p0:p0 + 32] = (-sin.T if half == 0 else sin.T)
    return np.stack([cosT, sinT], axis=1)


def _vecT(v, nchunk):
    return np.ascontiguousarray(np.asarray(v, np.float32).reshape(nchunk, 128).T)


def prep_inputs(b, inp):
    f = lambda a: np.ascontiguousarray(np.asarray(a, np.float32))
    m = {}
    m["xin"] = f(np.concatenate([inp["ctx"][b], inp["x"][b]], axis=0))
    cv = np.stack([inp["c"][b], inp["c_ctx"]], axis=0)
    m["cvecT"] = f(cv.reshape(2, 8, 128).transpose(2, 0, 1))
    m["consts"] = _consts()
    m["rope"] = _rope_tables()
    m["e_ada_w"] = f(inp["e_ada_w"][0])
    m["e_ada_b"] = f(inp["e_ada_b"][0]).reshape(1, -1)
    m["e_norm_wT"] = _vecT(inp["e_norm_w"][0], 8)
    w = f(inp["e_w_in"][0])
    q = w[:, OFF_Q:OFF_Q + 1024].reshape(D, 16, 2, 32)
    qs = q[:, :, ::-1, :].reshape(D, 1024)
    k = w[:, OFF_KV:OFF_KV + 256].reshape(D, 4, 64)
    kr = np.concatenate([k, k], axis=2).reshape(D, 512)
    ks = k.reshape(D, 4, 2, 32)[:, :, ::-1, :].reshape(D, 4, 64)
    ksr = np.concatenate([ks, ks], axis=2).reshape(D, 512)
    m["e_w_in"] = f(np.concatenate([w, qs, kr, ksr], axis=1))
    cw = f(inp["e_conv_w"][0])
    m["e_conv_wT"] = f(cw.reshape(5, 12, 128).transpose(2, 1, 0))
    m["e_conv_bT"] = _vecT(inp["e_conv_b"][0], 12)
    m["e_dt_bias"] = f(inp["e_dt_bias"][0]).reshape(1, 32)
    m["e_a_log"] = f(inp["e_a_log"][0]).reshape(1, 32)
    m["e_d_skip"] = f(inp["e_d_skip"][0]).reshape(1, 16)
    m["e_ssd_norm_wT"] = _vecT(inp["e_ssd_norm_w"][0], 8)
    sk = f(inp["e_sink"][0]).reshape(8, 2)
    m["e_sink"] = f(np.repeat(sk.T[:, None, :], 64, axis=1).reshape(128, 8))
    m["e_w_out"] = f(inp["e_w_out"][0])
    m["o_ada_w"] = f(inp["o_ada_w"][0])
    m["o_ada_b"] = f(inp["o_ada_b"][0]).reshape(1, -1)
    m["o_norm_wT"] = _vecT(inp["o_norm_w"][0], 8)
    m["o_w_in"] = f(inp["o_w_in"][0])

    def gl(a):
        a = np.asarray(a, np.float32)
        rest = a.shape[2:]
        a = a.reshape((32, 2, 64) + rest)
        a = np.moveaxis(a, 0, 2)
        return a.reshape((128, 32) + rest)

    lam = np.zeros((128, 2, 3, 32), np.float32)
    for d_ in range(2):
        lam[:, d_, 0] = gl(inp["o_lam_re"][0][d_])
        lam[:, d_, 1] = gl(inp["o_lam_im"][0][d_])
        lam[:, d_, 2] = gl(np.repeat(np.asarray(inp["o_log_step"][0][d_])[:, None], 64, axis=1))
    m["s5_lam"] = f(lam)
    m["s5_b"] = f(np.stack([gl(inp["o_b_re"][0]), gl(inp["o_b_im"][0])], axis=1))
    cr = np.asarray(inp["o_c_re"][0]).transpose(0, 2, 1)
    ci = np.asarray(inp["o_c_im"][0]).transpose(0, 2, 1)
    m["s5_c"] = f(np.stack([gl(cr), gl(ci)], axis=1))
    m["o_d_skip"] = f(inp["o_d_skip"][0]).reshape(1, -1)
    m["o_glu_w"] = f(inp["o_glu_w"][0])
    m["o_glu_b"] = f(inp["o_glu_b"][0]).reshape(1, -1)
    m["o_w_out"] = f(inp["o_w_out"][0])
    m["final_norm_w"] = f(inp["final_norm_w"]).reshape(1, -1)
    return m


def kernel(**inputs):
    nc = build_program()
    in_maps = [prep_inputs(b, inputs) for b in range(8)]
    res = run_bass_kernel_spmd(nc, in_maps, core_ids=list(range(8)))
    return np.stack([r["out"] for r in res.results], axis=0)
```

```python
import math
import os
from contextlib import ExitStack

import numpy as np
import concourse.bass as bass
import concourse.mybir as mybir
from concourse.bass_utils import run_bass_kernel_spmd

F32 = mybir.dt.float32
BF16 = mybir.dt.bfloat16
AF = mybir.ActivationFunctionType
ALU = mybir.AluOpType

D = 1024
T = 2304
NT = 18
CTX = 256
L = 2048
EPS = 1e-6
TG = [(0, 256), (256, 512), (768, 512), (1280, 512), (1792, 512)]

SES_ALL = os.environ.get('SES', '0') == '1'
SAME_ENGINE_SYNC = {'act': SES_ALL, 'dve': SES_ALL, 'pool': True, 'pe': False, 'sp': True}
SEM_EPOCH = 30000


class V:
    __slots__ = ("buf", "ap")

    def __init__(self, buf, ap):
        self.buf = buf
        self.ap = ap


class Buf:
    def __init__(self, name, h):
        self.name = name
        self.h = h
        self.last_w = None
        self.readers = []
        self.is_psum = False

    def __getitem__(self, idx):
        return V(self, self.h[idx])

    def full(self):
        return V(self, self.h.ap())

    def view(self, offset, pattern):
        return V(self, bass.AP(self.h, offset, [list(p) for p in pattern]))


class Sched:
    ENG = ("pe", "act", "dve", "pool", "sp")

    def __init__(self, nc):
        self.nc = nc
        self.prog = {e: [] for e in self.ENG}
        self.sem = {}
        self.cnt = {}
        self.semid = 0
        self.known = {e: {} for e in self.ENG}
        for e in ("pe", "act", "dve", "pool"):
            self._new_engine_sem(e)
        self.nds = 8
        self.dsem = {}
        self.duse = {}
        self.dcnt = {}
        for q in ("sp", "pool"):
            self.dsem[q] = []
            self.duse[q] = []
            for i in range(self.nds):
                key = "d_%s_%d" % (q, i)
                self.dsem[q].append((nc.alloc_semaphore(key), key))
                self.duse[q].append(0)
            self.dcnt[q] = 0
        self.n_ops = 0

    def _new_engine_sem(self, e):
        self.semid += 1
        key = "s_%s_%d" % (e, self.semid)
        self.sem[e] = (self.nc.alloc_semaphore(key), key)
        self.cnt[e] = 0

    def _deps(self, reads, writes):
        deps = {}

        def add(tok):
            if tok is None:
                return
            h, key, val = tok
            if key not in deps or deps[key][1] < val:
                deps[key] = (h, val)

        for r in reads:
            add(r.buf.last_w)
            if r.buf.is_psum:
                for t in r.buf.readers:
                    add(t)
        for w in writes:
            add(w.buf.last_w)
            for t in w.buf.readers:
                add(t)
        return deps

    def _emit_waits(self, eng, deps, own_key=None):
        kn = self.known[eng]
        for key, (h, val) in deps.items():
            if key == own_key and not SAME_ENGINE_SYNC[eng]:
                continue
            if kn.get(key, 0) >= val:
                continue
            kn[key] = val
            self.prog[eng].append(("wait", h, val))

    def _update(self, tok, reads, writes):
        for w in writes:
            w.buf.last_w = tok
            w.buf.readers = []
        for r in reads:
            if r.buf.last_w is not tok:
                r.buf.readers.append(tok)

    def op(self, eng, fn, reads=(), writes=()):
        reads = [r for r in reads if r is not None]
        writes = list(writes)
        if self.cnt[eng] >= SEM_EPOCH:
            self._new_engine_sem(eng)
        h, key = self.sem[eng]
        own = None if eng == "pe" else key
        deps = self._deps(reads, writes)
        if eng == "pe":
            deps.pop(key, None)
        self._emit_waits(eng, deps, own_key=own)
        self.cnt[eng] += 1
        self.prog[eng].append(("op", fn, h, 1))
        tok = (h, key, self.cnt[eng])
        self._update(tok, reads, writes)
        self.n_ops += 1
        return tok

    def dma(self, out, in_, q="sp", **kw):
        deps = self._deps([in_], [out])
        self._emit_waits(q, deps)
        k = self.dcnt[q] % self.nds
        self.dcnt[q] += 1
        h, key = self.dsem[q][k]
        prev = 16 * self.duse[q][k]
        if prev > 0 and self.known[q].get(key, 0) < prev:
            self.known[q][key] = prev
            self.prog[q].append(("wait", h, prev))
        self.duse[q][k] += 1
        val = 16 * self.duse[q][k]
        o_ap, i_ap = out.ap, in_.ap
        self.prog[q].append(("op", lambda e: e.dma_start(out=o_ap, in_=i_ap, **kw), h, 16))
        tok = (h, key, val)
        self._update(tok, [in_], [out])
        self.n_ops += 1
        return tok

    def finish_dmas(self):
        for q in ("sp", "pool"):
            for k in range(self.nds):
                h, key = self.dsem[q][k]
                val = 16 * self.duse[q][k]
                if val > 0 and self.known[q].get(key, 0) < val:
                    self.known[q][key] = val
                    self.prog[q].append(("wait", h, val))

    def flush(self, name=None):
        self.finish_dmas()
        nc = self.nc
        prog = self.prog
        self.prog = {e: [] for e in self.ENG}

        def run(items, e):
            for it in items:
                if it[0] == "wait":
                    e.wait_ge(it[1], it[2])
                else:
                    inst = it[1](e)
                    inst.then_inc(it[2], it[3])

        with nc.Block() as block:
            if prog["sp"]:
                @block.sync
                def _(e):
                    run(prog["sp"], e)
            if prog["act"]:
                @block.scalar
                def _(e):
                    run(prog["act"], e)
            if prog["dve"]:
                @block.vector
                def _(e):
                    run(prog["dve"], e)
            if prog["pool"]:
                @block.gpsimd
                def _(e):
                    run(prog["pool"], e)
            if prog["pe"]:
                @block.tensor
                def _(e):
                    run(prog["pe"], e)

    def mm(self, out, pairs):
        n = len(pairs)

        def fn(e):
            inst = None
            for i, (l, r) in enumerate(pairs):
                inst = e.matmul(out.ap, l.ap, r.ap, start=(i == 0), stop=(i == n - 1))
            return inst

        self.op("pe", fn, reads=[p[0] for p in pairs] + [p[1] for p in pairs], writes=[out])

    def transpose(self, out, in_, ident):
        self.op("pe", lambda e: e.transpose(out.ap, in_.ap, ident.ap), reads=[in_, ident], writes=[out])

    def act(self, out, in_, func, bias=None, scale=None, accum=None):
        kw = {}
        reads = [in_]
        writes = [out]
        if bias is not None:
            if isinstance(bias, V):
                kw["bias"] = bias.ap
                reads.append(bias)
            else:
                kw["bias"] = bias
        if scale is not None:
            if isinstance(scale, V):
                kw["scale"] = scale.ap
                reads.append(scale)
            else:
                kw["scale"] = scale
        if accum is not None:
            kw["accum_out"] = accum.ap
            writes.append(accum)
        self.op("act", lambda e: e.activation(out.ap, in_.ap, func, **kw), reads=reads, writes=writes)

    def ts(self, out, in0, s1, s2, op0, op1=None, eng="dve"):
        reads = [in0]
        a1 = s1
        a2 = s2
        if isinstance(s1, V):
            reads.append(s1)
            a1 = s1.ap
        if isinstance(s2, V):
            reads.append(s2)
            a2 = s2.ap
        if op1 is None:
            self.op(eng, lambda e: e.tensor_scalar(out.ap, in0.ap, a1, a2, op0), reads=reads, writes=[out])
        else:
            self.op(eng, lambda e: e.tensor_scalar(out.ap, in0.ap, a1, a2, op0, op1), reads=reads, writes=[out])

    def tt(self, out, in0, in1, op, eng="dve"):
        self.op(eng, lambda e: e.tensor_tensor(out.ap, in0.ap, in1.ap, op), reads=[in0, in1], writes=[out])

    def stt(self, out, in0, scalar, in1, op0, op1):
        reads = [in0, in1]
        sc = scalar
        if isinstance(scalar, V):
            reads.append(scalar)
            sc = scalar.ap
        self.op("dve", lambda e: e.scalar_tensor_tensor(out.ap, in0.ap, sc, in1.ap, op0, op1),
                reads=reads, writes=[out])

    def copy(self, out, in_, eng="dve"):
        if eng == "act":
            self.op("act", lambda e: e.copy(out.ap, in_.ap), reads=[in_], writes=[out])
        else:
            self.op(eng, lambda e: e.tensor_copy(out.ap, in_.ap), reads=[in_], writes=[out])

    def recip(self, out, in_):
        self.op("dve", lambda e: e.reciprocal(out.ap, in_.ap), reads=[in_], writes=[out])

    def memset(self, out, val, eng="dve"):
        self.op(eng, lambda e: e.memset(out.ap, val), reads=[], writes=[out])


class Ctx:
    def __init__(self, nc, sched):
        self.nc = nc
        self.s = sched
        self.uid = 0

    def sb(self, es, name, shape, dtype=F32):
        self.uid += 1
        h = es.enter_context(self.nc.sbuf_tensor("%s_%d" % (name, self.uid), list(shape), dtype))
        return Buf(name, h)

    def ps(self, es, name, shape=(128, 512), dtype=F32):
        self.uid += 1
        h = es.enter_context(self.nc.psum_tensor("%s_%d" % (name, self.uid), list(shape), dtype))
        b = Buf(name, h)
        b.is_psum = True
        return b

    def dram(self, name, shape, dtype=F32, kind="Internal"):
        h = self.nc.dram_tensor(name, list(shape), dtype, kind=kind)
        return Buf(name, h)


def pipeline(items, stages):
    n, k = len(items), len(stages)
    for t in range(n + k - 1):
        for j in range(k - 1, -1, -1):
            i = t - j
            if 0 <= i < n:
                stages[j](items[i])


def bc_mid(v_buf, base_off, pstep, nparts, n_outer, outer_step, n_inner):
    return v_buf.view(base_off, [[pstep, nparts], [outer_step, n_outer], [0, n_inner]])


E_NCOL = 5152
OFF_Z = 0
OFF_XBC = 1024
OFF_DT = 2560
OFF_Q = 2592
OFF_KV = 3616
OFF_G = 4128
OFF_QS = 5152
OFF_KR = 6176
OFF_KSR = 6688
E_NCOL_EXT = 7200


ORDER = ["p1", "p2a", "p2b", "p2c", "p2d", "p2e", "p2f", "p2g", "p2h", "p3", "p4", "p5", "all"]


def build_program(debug=(), stop="all"):
    def go(tag):
        return ORDER.index(tag) <= ORDER.index(stop)
    nc = bass.Bass("TRN2", target_bir_lowering=False)
    s = Sched(nc)
    cx = Ctx(nc, s)
    dbg = set(debug)

    def din(name, shape):
        return Buf(name, nc.dram_tensor(name, list(shape), F32, kind="ExternalInput"))

    def dout(name, shape):
        return Buf(name, nc.dram_tensor(name, list(shape), F32, kind="ExternalOutput"))

    def scratch(name, shape, dtype=F32):
        if name in dbg:
            return dout(name, shape)
        return Buf(name, nc.dram_tensor(name, list(shape), dtype))

    xin = din("xin", [T, D])
    cvecT = din("cvecT", [128, 2, 8])
    consts = din("consts", [128, 6, 512])
    rope = din("rope", [128, 2, L])
    e_ada_w = din("e_ada_w", [D, 3 * D])
    e_ada_b = din("e_ada_b", [1, 3 * D])
    e_norm_wT = din("e_norm_wT", [128, 8])
    e_w_in = din("e_w_in", [D, E_NCOL_EXT])
    e_conv_wT = din("e_conv_wT", [128, 12, 5])
    e_conv_bT = din("e_conv_bT", [128, 12])
    e_dt_bias = din("e_dt_bias", [1, 32])
    e_a_log = din("e_a_log", [1, 32])
    e_d_skip = din("e_d_skip", [1, 16])
    e_ssd_norm_wT = din("e_ssd_norm_wT", [128, 8])
    e_sink = din("e_sink", [128, 8])
    e_w_out = din("e_w_out", [2 * D, D])
    o_ada_w = din("o_ada_w", [D, 3 * D])
    o_ada_b = din("o_ada_b", [1, 3 * D])
    o_norm_wT = din("o_norm_wT", [128, 8])
    o_w_in = din("o_w_in", [D, 2 * D])
    s5_lam = din("s5_lam", [128, 2, 3, 32])
    s5_b = din("s5_b", [128, 2, 32, 16])
    s5_c = din("s5_c", [128, 2, 32, 16])
    o_d_skip = din("o_d_skip", [1, D])
    o_glu_w = din("o_glu_w", [D, D])
    o_glu_b = din("o_glu_b", [1, D])
    o_w_out = din("o_w_out", [D, D])
    final_norm_w = din("final_norm_w", [1, D])
    out_t = dout("out", [L, D])

    XS = scratch("XS", [T, 1024])
    BTOK = scratch("BTOK", [T, 256], BF16)
    BT = scratch("BT", [2, 128, T], BF16)
    CT = scratch("CT", [2, 128, T], BF16)
    SZ = scratch("SZ", [T, 1024])
    QR = scratch("QR", [8, 128, L], BF16)
    QC = scratch("QC", [8, 128, CTX], BF16)
    KR = scratch("KR", [4, 128, L], BF16)
    KC = scratch("KC", [4, 128, CTX], BF16)
    VT = scratch("VT", [T, 256], BF16)
    SG = scratch("SG", [8, 128, T])
    YF = scratch("YF", [T, 1024])
    YT = scratch("YT", [16, 128, T], BF16)
    X1 = scratch("X1", [T, 1024])
    U = scratch("U", [T, 1024])
    SG1 = scratch("SG1", [T, 1024])
    YTOK = scratch("YTOK", [T, 1024])
    KFP = scratch("KFP", [64, 16, 15, 16], BF16)
    KBR = scratch("KBR", [64, 16, 15, 16], BF16)
    HT = scratch("HT", [8, 128, T]) if "HT" in dbg else None
    DTD = scratch("DTD", [T, 32]) if "DTD" in dbg else None
    MODD = scratch("MODD", [4, 128, 24]) if "MODD" in dbg else None

    with ExitStack() as top:
        banks = [cx.ps(top, "bank%d" % i) for i in range(8)]
        cst = cx.sb(top, "cst", [128, 6, 512])
        s.dma(cst.full(), consts.full())
        ident = cst[:, 0, 0:128]
        tri = cst[:, 1, 0:128]
        utri = cst[:, 2, 0:128]
        ones = cst[:, 3, 0:128]
        onesb_t = cx.sb(top, "onesb", [128, 128], BF16)
        s.memset(onesb_t.full(), 1.0)
        onesb = onesb_t.full()
        modT = [[cx.sb(top, "modT%d%d" % (l, w), [128, 24]) for w in range(2)] for l in range(2)]
        gate_bc = [[cx.sb(top, "gate%d%d" % (l, w), [128, 1024]) for w in range(2)] for l in range(2)]
        scs = cx.sb(top, "scs", [128, 2, 8])

        def adaln_phase(layer, ada_w, ada_b):
            with ExitStack() as es:
                aw = [cx.sb(es, "aw%d" % k, [128, 3 * D]) for k in range(8)]
                ab = cx.sb(es, "ab", [1, 3 * D])
                modrow = [cx.sb(es, "modrow%d" % w, [1, 3 * D]) for w in range(2)]
                if layer == 0:
                    cv = cx.sb(es, "cv", [128, 2, 8])
                    s.dma(cv.full(), cvecT.full())
                    s.act(scs.full(), cv.full(), AF.Silu)
                for k in range(8):
                    s.dma(aw[k].full(), ada_w[k * 128:(k + 1) * 128, :])
                s.dma(ab.full(), ada_b.full())
                bi = 0
                for w in range(2):
                    for fg in range(6):
                        bk = banks[bi % 8]
                        bi += 1
                        s.mm(bk[0:1, :], [(scs[:, w, k:k + 1], aw[k][:, fg * 512:(fg + 1) * 512]) for k in range(8)])
                        s.tt(modrow[w][0:1, fg * 512:(fg + 1) * 512], bk[0:1, :], ab[0:1, fg * 512:(fg + 1) * 512], ALU.add)
                for w in range(2):
                    bk = banks[bi % 8]
                    bi += 1
                    for fc in range(24):
                        s.mm(bk[:, 2 * fc:2 * fc + 2], [(modrow[w][0:1, fc * 128:(fc + 1) * 128], cst[0:1, 3, 0:2])])
                    s.copy(modT[layer][w].full(), bk.view(0, [[512, 128], [2, 24]]))
                    for hh in range(2):
                        bk2 = banks[bi % 8]
                        bi += 1
                        s.mm(bk2.full(), [(cst[0:1, 3, 0:128], modrow[w][0:1, 2048 + hh * 512:2048 + (hh + 1) * 512])])
                        s.copy(gate_bc[layer][w][:, hh * 512:(hh + 1) * 512], bk2.full(), eng="act")
                    if MODD is not None:
                        s.dma(MODD[layer * 2 + w], modT[layer][w].full())
                s.flush()

        adaln_phase(0, e_ada_w, e_ada_b)

        with ExitStack() as l0:
            DT = cx.sb(l0, "DT", [128, NT, 32])
            DTA = cx.sb(l0, "DTA", [128, NT, 32])
            nw = cx.sb(l0, "nw", [128, 8])
            sc1 = [cx.sb(l0, "sc1_%d" % w, [128, 8]) for w in range(2)]
            s.dma(nw.full(), e_norm_wT.full())
            for w in range(2):
                s.stt(sc1[w].full(), modT[0][w][:, 8:16], 1.0, nw.full(), ALU.add, ALU.mult)

            wo = [cx.sb(l0, "wo%d" % k, [128, D], BF16) for k in range(16)]
            hts = ExitStack()
            hT = [cx.sb(hts, "hT%d" % k, [128, T], BF16) for k in range(8)]
            with ExitStack() as es:
                xt = [cx.sb(es, "xt%d" % i, [128, D]) for i in range(3)]
                xn = [cx.sb(es, "xn%d" % i, [128, D]) for i in range(3)]
                junk = cx.sb(es, "junk", [128, D])
                st = [cx.sb(es, "st%d" % i, [128, 4]) for i in range(3)]
                def n0(i):
                    x_, n_, st_ = xt[i % 3], xn[i % 3], st[i % 3]
                    s.dma(x_.full(), xin[i * 128:(i + 1) * 128, :])
                    s.act(junk.full(), x_.full(), AF.Square, accum=st_[:, 0:1])
                    s.ts(st_[:, 1:2], st_[:, 0:1], 1.0 / D, EPS, ALU.mult, ALU.add)
                    s.act(st_[:, 2:3], st_[:, 1:2], AF.Sqrt)
                    s.recip(st_[:, 3:4], st_[:, 2:3])
                    s.ts(n_.full(), x_.full(), st_[:, 3:4], None, ALU.mult)

                def n1(i):
                    w = 1 if i < 2 else 0
                    n_ = xn[i % 3]
                    for half in range(2):
                        bk = banks[(2 * i + half) % 8]
                        for kk in range(4):
                            k = half * 4 + kk
                            s.transpose(bk[:, kk * 128:(kk + 1) * 128], n_[:, k * 128:(k + 1) * 128], ident)
                        for kk in range(4):
                            k = half * 4 + kk
                            s.act(hT[k][:, i * 128:(i + 1) * 128], bk[:, kk * 128:(kk + 1) * 128], AF.Identity,
                                  bias=modT[0][w][:, k:k + 1], scale=sc1[w][:, k:k + 1])

                pipeline(list(range(NT)), [n0, n1])
                if HT is not None:
                    for k in range(8):
                        s.dma(HT[k], hT[k].full())
                s.flush()

            with ExitStack() as es:
                WB = 256
                NWB, PF = 6, 4
                wbuf = [cx.sb(es, "wbuf%d" % i, [128, 8, WB], BF16) for i in range(NWB)]
                wplan = [(OFF_XBC + 256 * k, 256) for k in range(6)]
                for qc in range(8):
                    wplan += [(OFF_Q + qc * 128, 128), (OFF_QS + qc * 128, 128)]
                for j in range(4):
                    wplan += [(OFF_KR + j * 128, 128), (OFF_KSR + j * 128, 128)]
                wplan += [(OFF_G + 256 * k, 256) for k in range(4)]
                wplan += [(OFF_Z + 256 * k, 256) for k in range(4)]
                wplan += [(OFF_KV + 256, 256), (OFF_DT, 32)]
                wstate = {"i": 0, "issued": 0}

                def _issue(n):
                    col0, ncol = wplan[n]
                    wb = wbuf[n % NWB]
                    s.dma(wb[:, :, 0:ncol], e_w_in.view(col0, [[E_NCOL_EXT, 128], [128 * E_NCOL_EXT, 8], [1, ncol]]), q="pool")

                def load_w(col0, ncol=WB):
                    i = wstate["i"]
                    wstate["i"] += 1
                    assert wplan[i] == (col0, ncol), (i, wplan[i], col0, ncol)
                    while wstate["issued"] < min(i + PF + 1, len(wplan)):
                        _issue(wstate["issued"])
                        wstate["issued"] += 1
                    return wbuf[i % NWB]

                bstate = {"i": 0}

                def nbank():
                    bk = banks[bstate["i"] % 8]
                    bstate["i"] += 1
                    return bk

                def fm_mm(wb, cc, t0, n):
                    bk = nbank()
                    s.mm(bk[:, 0:n], [(wb[:, k, cc * 128:(cc + 1) * 128], hT[k][:, t0:t0 + n]) for k in range(8)])
                    return bk

                xraws = [cx.sb(es, "xraw%d" % i, [128, T]) for i in range(2)]
                accs = [cx.sb(es, "acc%d" % i, [128, T]) for i in range(2)]
                acc = accs[0]
                accbs = [cx.sb(es, "accb%d" % i, [128, T], BF16) for i in range(2)]
                accb = accbs[0]
                rc_i = {"i": 0}
                tmp1s = [cx.sb(es, "tmp1_%d" % i, [128, 512]) for i in range(2)]
                tmp2s = [cx.sb(es, "tmp2_%d" % i, [128, 512]) for i in range(2)]
                stg = [cx.sb(es, "stg%d" % i, [128, 4, 128]) for i in range(2)]
                stgb = [cx.sb(es, "stgb%d" % i, [128, 4, 128], BF16) for i in range(2)]
                rp = cx.sb(es, "rp", [128, 2, L])
                cw = cx.sb(es, "cw", [128, 12, 5])
                cb = cx.sb(es, "cb", [128, 12])
                dtb = cx.sb(es, "dtb", [128, 32])
                abc = cx.sb(es, "abc", [128, 32])
                s.dma(rp.full(), rope.full())
                s.dma(cw.full(), e_conv_wT.full())
                s.dma(cb.full(), e_conv_bT.full())
                s.dma(dtb.full(), e_dt_bias.view(0, [[0, 128], [1, 32]]))
                s.dma(abc.full(), e_a_log.view(0, [[0, 128], [1, 32]]))
                s.act(abc.full(), abc.full(), AF.Exp)
                s.ts(abc.full(), abc.full(), -1.0, None, ALU.mult)
                stg_i = {"i": 0}

                def transposes_to(dst, col0, src, lowp=False):
                    for i0 in range(0, NT, 4):
                        nb = min(4, NT - i0)
                        bk = nbank()
                        for ii in range(nb):
                            i = i0 + ii
                            s.transpose(bk[:, ii * 128:(ii + 1) * 128], src[:, i * 128:(i + 1) * 128], ident)
                        sg_ = (stgb if lowp else stg)[stg_i["i"] % 2]
                        stg_i["i"] += 1
                        s.copy(sg_[:, 0:nb, :], bk.view(0, [[512, 128], [128, nb], [1, 128]]), eng="act")
                        ncols = dst.h.shape[1]
                        s.dma(dst.view(i0 * 128 * ncols + col0, [[ncols, 128], [128 * ncols, nb], [1, 128]]),
                              sg_[:, 0:nb, :])

                wb_of = {}

                def xa(fc):
                    if fc % 2 == 0:
                        wb_of[fc // 2] = load_w(OFF_XBC + fc * 128)
                    wb = wb_of[fc // 2]
                    xraw = xraws[fc % 2]
                    for (t0, n) in TG:
                        bk = fm_mm(wb, fc % 2, t0, n)
                        s.copy(xraw[:, t0:t0 + n], bk[:, 0:n], eng="act")

                def xb(fc):
                    xraw, acc = xraws[fc % 2], accs[fc % 2]
                    s.ts(acc.full(), xraw.full(), cw[:, fc, 2:3], cb[:, fc:fc + 1], ALU.mult, ALU.add)
                    for kk in (0, 1, 3, 4):
                        d_ = kk - 2
                        for (s0, sl) in ((0, CTX), (CTX, L)):
                            lo = max(s0, s0 - d_)
                            hi = min(s0 + sl, s0 + sl - d_)
                            s.stt(acc[:, lo:hi], xraw[:, lo + d_:hi + d_], cw[:, fc, kk:kk + 1], acc[:, lo:hi],
                                  ALU.mult, ALU.add)
                    s.act(acc.full(), acc.full(), AF.Silu)
                    if fc < 8:
                        transposes_to(XS, fc * 128, acc)
                    elif fc < 10:
                        s.copy(accb.full(), acc.full(), eng="act")
                        s.dma(BT[fc - 8], accb.full())
                        transposes_to(BTOK, (fc - 8) * 128, acc, lowp=True)
                    else:
                        s.copy(accb.full(), acc.full(), eng="act")
                        s.dma(CT[fc - 10], accb.full())

                pipeline(list(range(12 if go('p2a') else 0)), [xa, xb])

                def rope_chunk(col_plain, col_swap, dst_rot, dst_ctx):
                    accb = accbs[rc_i["i"] % 2]
                    rc_i["i"] += 1
                    wa = load_w(col_plain, 128)
                    wsw = load_w(col_swap, 128)
                    for gi, (t0, n) in enumerate(TG):
                        bka = fm_mm(wa, 0, t0, n)
                        if gi == 0:
                            s.copy(accb[:, 0:CTX], bka[:, 0:CTX], eng="act")
                            continue
                        bkb = fm_mm(wsw, 0, t0, n)
                        l0 = t0 - CTX
                        tmp1, tmp2 = tmp1s[gi % 2], tmp2s[gi % 2]
                        s.tt(tmp1.full(), bka.full(), rp[:, 0, l0:l0 + 512], ALU.mult)
                        s.tt(tmp2.full(), bkb.full(), rp[:, 1, l0:l0 + 512], ALU.mult)
                        s.tt(accb[:, t0:t0 + n], tmp1.full(), tmp2.full(), ALU.add)
                    s.dma(dst_ctx, accb[:, 0:CTX])
                    s.dma(dst_rot, accb[:, CTX:T])

                for qc in range(8 if go('p2b') else 0):
                    rope_chunk(OFF_Q + qc * 128, OFF_QS + qc * 128, QR[qc], QC[qc])
                for j in range(4 if go('p2c') else 0):
                    rope_chunk(OFF_KR + j * 128, OFF_KSR + j * 128, KR[j], KC[j])

                for gc in range(8 if go('p2d') else 0):
                    acc = accs[gc % 2]
                    if gc % 2 == 0:
                        wb = load_w(OFF_G + gc * 128)
                    for (t0, n) in TG:
                        bk = fm_mm(wb, gc % 2, t0, n)
                        s.act(acc[:, t0:t0 + n], bk[:, 0:n], AF.Silu)
                    s.dma(SG[gc], acc.full())

                NT_E = NT if go('p2e') else 0
                wz = [load_w(OFF_Z + i * 256) for i in range(4)]
                for i in range(NT_E):
                    z_a = accs[i % 2]
                    for half in range(2):
                        bk = nbank()
                        for q4 in range(2):
                            wbz = wz[half * 2 + q4]
                            s.mm(bk[:, q4 * 256:(q4 + 1) * 256],
                                 [(hT[k][:, i * 128:(i + 1) * 128], wbz[:, k, :]) for k in range(8)])
                        s.act(z_a[:, half * 512:(half + 1) * 512], bk.full(), AF.Silu)
                    s.dma(SZ[i * 128:(i + 1) * 128, :], z_a[:, 0:1024])
                wv = load_w(OFF_KV + 256)
                wdt = load_w(OFF_DT, 32)
                vt = [cx.sb(es, "vt%d" % i, [128, 256], BF16) for i in range(2)]
                for i in range(NT if go('p2f') else 0):
                    bk = nbank()
                    s.mm(bk[:, 0:256], [(hT[k][:, i * 128:(i + 1) * 128], wv[:, k, :]) for k in range(8)])
                    s.copy(vt[i % 2].full(), bk[:, 0:256], eng="act")
                    s.dma(VT[i * 128:(i + 1) * 128, :], vt[i % 2].full())
                for i in range(NT if go('p2g') else 0):
                    bk = nbank()
                    s.mm(bk[:, 0:32], [(hT[k][:, i * 128:(i + 1) * 128], wdt[:, k, 0:32]) for k in range(8)])
                    s.tt(DT[:, i, :], bk[:, 0:32], dtb.full(), ALU.add)
                    if go('p2h'):
                        s.act(DT[:, i, :], DT[:, i, :], AF.Exp)
                        s.ts(DT[:, i, :], DT[:, i, :], 1.0, None, ALU.add)
                        s.act(DT[:, i, :], DT[:, i, :], AF.Ln)
                    s.tt(DTA[:, i, :], DT[:, i, :], abc.full(), ALU.mult)
                    if DTD is not None:
                        s.dma(DTD[i * 128:(i + 1) * 128, :], DT[:, i, :])
                s.flush()
            hts.close()
            for k in range(16):
                s.dma(wo[k].full(), e_w_out[k * 128:(k + 1) * 128, :], q="pool")

            with ExitStack() as es:
                nb_ = {"i": 0}

                def nbank():
                    bk = banks[nb_["i"] % 8]
                    nb_["i"] += 1
                    return bk

                N3 = 3
                N4 = 4
                xs_t = [cx.sb(es, "xs_t%d" % i, [128, 1024]) for i in range(N4)]
                b_t = [cx.sb(es, "b_t%d" % i, [128, 256], BF16) for i in range(N3)]
                bt_t = [cx.sb(es, "bt_t%d" % i, [128, 2, 128], BF16) for i in range(N3)]
                ct_t = [cx.sb(es, "ct_t%d" % i, [128, 2, 128], BF16) for i in range(N3)]
                yf_t = [cx.sb(es, "yf_t%d" % i, [128, 1024]) for i in range(N4)]
                sz_t = [cx.sb(es, "sz_t%d" % i, [128, 1024]) for i in range(2)]
                MTs = [cx.sb(es, "MT%d" % i, [128, 2048], BF16) for i in range(N3)]
                xcs = [cx.sb(es, "xc%d" % i, [128, 1024], BF16) for i in range(N3)]
                xcds = [cx.sb(es, "xcd%d" % i, [128, 1024], BF16) for i in range(N3)]
                tmpos = [cx.sb(es, "tmpo%d" % i, [128, 1024]) for i in range(N3)]
                ytots = [cx.sb(es, "ytot%d" % i, [128, 1024]) for i in range(2)]
                sms = [cx.sb(es, "sm%d" % i, [128, 4, 16]) for i in range(N3)]
                st3s = [cx.sb(es, "st3_%d" % i, [128, 4]) for i in range(N3)]
                ystgs = [cx.sb(es, "ystg%d" % i, [128, 8, 128], BF16) for i in range(2)]
                dtatris = [cx.sb(es, "dtatri%d" % i, [128, 2048]) for i in range(2)]
                decTs = [cx.sb(es, "decT%d" % i, [128, 2048]) for i in range(2)]
                cb_sbs = [cx.sb(es, "cb_sb%d" % i, [128, 256]) for i in range(2)]
                junk = cx.sb(es, "junk3", [128, 1024])
                Hs = [cx.sb(es, "Hs%d" % g, [128, 512]) for g in range(2)]
                Hb = [cx.sb(es, "Hb%d" % g, [128, 512], BF16) for g in range(2)]
                dsk = cx.sb(es, "dsk", [128, 16])
                snw = cx.sb(es, "snw", [128, 8])
                cm1 = cx.sb(es, "cm1", [128, 1])
                s.memset(cm1.full(), -1.0)
                s.dma(dsk.full(), e_d_skip.view(0, [[0, 128], [1, 16]]))
                s.dma(snw.full(), e_ssd_norm_wT.full())

                def bc3(buf, off, pstep, n1, s1, n2, s2):
                    return buf.view(off, [[pstep, 128], [s1, n1], [s2, n2]])

                n_ch = NT if go("p3") else 0
                for d_ in range(2):
                    order = list(range(NT)) if d_ == 0 else [1, 0] + list(range(NT - 1, 1, -1))
                    order = order[:n_ch]
                    TRIoff = 512 if d_ == 0 else 1024
                    TRIv = tri if d_ == 0 else utri
                    negm = cst[:, 4 + d_, :]
                    for g in range(2):
                        s.memset(Hs[g].full(), 0.0)
                        s.memset(Hb[g].full(), 0.0)

                    def stA(item, d_=d_, TRIoff=TRIoff, TRIv=TRIv, negm=negm):
                        ci, i = item
                        p3, p2, p4 = ci % N3, ci % 2, ci % N4
                        xs_, b_, bt_, ct_ = xs_t[p4], b_t[p3], bt_t[p3], ct_t[p3]
                        MT, xc, xcd, sm = MTs[p3], xcs[p3], xcds[p3], sms[p3]
                        dtatri, decT, cb_sb = dtatris[p2], decTs[p2], cb_sbs[p2]
                        s.dma(xs_.full(), XS[i * 128:(i + 1) * 128, :])
                        s.dma(b_.full(), BTOK[i * 128:(i + 1) * 128, :])
                        s.dma(bt_.full(), BT.view(i * 128, [[T, 128], [128 * T, 2], [1, 128]]))
                        s.dma(ct_.full(), CT.view(i * 128, [[T, 128], [128 * T, 2], [1, 128]]))
                        if d_ == 1:
                            s.dma(yf_t[p4].full(), YF[i * 128:(i + 1) * 128, :])
                        dta_i = DTA[:, i, d_ * 16:(d_ + 1) * 16]
                        doff = i * 32 + d_ * 16
                        s.tt(bc3(dtatri, 0, 2048, 16, 128, 128, 1), bc3(DTA, doff, NT * 32, 16, 1, 128, 0),
                             bc3(cst, TRIoff, 3072, 16, 0, 128, 1), ALU.mult, eng="pool")
                        bs = nbank()
                        s.mm(bs[:, 0:16], [(TRIv, dta_i)])
                        s.mm(bs[:, 16:32], [(ones, dta_i)])
                        na, ea, de, cd = sm[:, 0, :], sm[:, 1, :], sm[:, 2, :], sm[:, 3, :]
                        s.ts(na, bs[:, 0:16], -1.0, None, ALU.mult)
                        s.act(ea, bs[:, 0:16], AF.Exp)
                        s.tt(de, bs[:, 16:32], na, ALU.add)
                        s.act(de, de, AF.Exp)
                        s.act(cd, bs[:, 16:32], AF.Exp)
                        for hq in range(4):
                            bq = nbank()
                            s.mm(bq.full(), [(ones, dtatri[:, hq * 512:(hq + 1) * 512]), (ident, negm)])
                            for hh in range(4):
                                h = hq * 4 + hh
                                s.act(decT[:, h * 128:(h + 1) * 128], bq[:, hh * 128:(hh + 1) * 128], AF.Exp,
                                      bias=sm[:, 0, h:h + 1])
                        bc = nbank()
                        for g in range(2):
                            s.mm(bc[:, g * 128:(g + 1) * 128], [(bt_[:, g, :], ct_[:, g, :])])
                        s.copy(cb_sb.full(), bc[:, 0:256], eng="act")
                        for g in range(2):
                            s.tt(bc3(MT, g * 1024, 2048, 8, 128, 128, 1), bc3(decT, g * 1024, 2048, 8, 128, 128, 1),
                                 bc3(cb_sb, g * 128, 256, 8, 0, 128, 1), ALU.mult)
                        s.tt(bc3(xc, 0, 1024, 16, 64, 64, 1), bc3(xs_, 0, 1024, 16, 64, 64, 1),
                             bc3(DT, doff, NT * 32, 16, 1, 64, 0), ALU.mult, eng="pool")
                        s.tt(bc3(xcd, 0, 1024, 16, 64, 64, 1), bc3(xc, 0, 1024, 16, 64, 64, 1),
                             bc3(sm, 32, 64, 16, 1, 64, 0), ALU.mult, eng="pool")
                        if d_ == 1:
                            s.tt(bc3(tmpos[p3], 0, 1024, 16, 64, 64, 1), bc3(xs_, 0, 1024, 16, 64, 64, 1),
                                 bc3(dsk, 0, 16, 16, 1, 64, 0), ALU.mult, eng="pool")
                            s.tt(yf_t[p4].full(), yf_t[p4].full(), tmpos[p3].full(), ALU.add, eng="pool")

                    def stB(item, d_=d_):
                        ci, i = item
                        p3 = ci % N3
                        b_, ct_ = b_t[p3], ct_t[p3]
                        MT, xc, xcd, sm, tmpo, ytot = MTs[p3], xcs[p3], xcds[p3], sms[p3], tmpos[p3], ytots[ci % 2]
                        ydst = yf_t[ci % N4] if d_ == 0 else ytot
                        if d_ == 1:
                            s.dma(sz_t[ci % 2].full(), SZ[i * 128:(i + 1) * 128, :])
                        for g in range(2):
                            by = nbank()
                            for hh in range(8):
                                h = g * 8 + hh
                                s.mm(by[:, hh * 64:(hh + 1) * 64], [(MT[:, h * 128:(h + 1) * 128], xc[:, h * 64:(h + 1) * 64])])
                            bo = nbank()
                            s.mm(bo.full(), [(ct_[:, g, :], Hb[g].full())])
                            s.tt(bc3(tmpo, g * 512, 1024, 8, 64, 64, 1), bc3(bo, 0, 512, 8, 64, 64, 1),
                                 bc3(sm, 16 + g * 8, 64, 8, 1, 64, 0), ALU.mult)
                            s.tt(ydst[:, g * 512:(g + 1) * 512], by.full(), tmpo[:, g * 512:(g + 1) * 512], ALU.add)
                        for g in range(2):
                            bst = nbank()
                            s.mm(bst.full(), [(b_[:, g * 128:(g + 1) * 128], xcd[:, g * 512:(g + 1) * 512])])
                            s.tt(bc3(Hs[g], 0, 512, 8, 64, 64, 1), bc3(Hs[g], 0, 512, 8, 64, 64, 1),
                                 bc3(sm, 48 + g * 8, 64, 8, 1, 64, 0), ALU.mult)
                            s.tt(Hs[g].full(), Hs[g].full(), bst.full(), ALU.add)
                            s.copy(Hb[g].full(), Hs[g].full(), eng="act")
                        if d_ == 0:
                            s.dma(YF[i * 128:(i + 1) * 128, :], yf_t[ci % N4].full())

                    def stC(item, d_=d_):
                        if d_ == 0:
                            return
                        ci, i = item
                        p3, p2 = ci % N3, ci % 2
                        ytot, sz_, st3, ystg = ytots[ci % 2], sz_t[ci % 2], st3s[p3], ystgs[p2]
                        s.tt(ytot.full(), ytot.full(), yf_t[ci % N4].full(), ALU.add)
                        s.tt(ytot.full(), ytot.full(), sz_.full(), ALU.mult)
                        s.act(junk.full(), ytot.full(), AF.Square, accum=st3[:, 0:1])
                        s.ts(st3[:, 1:2], st3[:, 0:1], 1.0 / 1024, EPS, ALU.mult, ALU.add)
                        s.act(st3[:, 2:3], st3[:, 1:2], AF.Sqrt)
                        s.recip(st3[:, 3:4], st3[:, 2:3])
                        s.act(ytot.full(), ytot.full(), AF.Copy, scale=st3[:, 3:4])
                        for half in range(2):
                            bk = nbank()
                            for kk in range(4):
                                k = half * 4 + kk
                                s.transpose(bk[:, kk * 128:(kk + 1) * 128], ytot[:, k * 128:(k + 1) * 128], ident)
                            for kk in range(4):
                                k = half * 4 + kk
                                s.act(ystg[:, k, :], bk[:, kk * 128:(kk + 1) * 128], AF.Copy, scale=snw[:, k:k + 1])
                        s.dma(YT.view(i * 128, [[T, 128], [128 * T, 8], [1, 128]]), ystg.full())

                    pipeline(list(enumerate(order)), [stA, (lambda it: None), stB, stC])
                s.flush()

            with ExitStack() as es:
                nb_ = {"i": 0}

                def nbank():
                    bk = banks[nb_["i"] % 8]
                    nb_["i"] += 1
                    return bk

                J2 = 2
                qr_ts = [cx.sb(es, "qr_t%d" % i, [128, 2, L], BF16) for i in range(J2)]
                qc_ts = [cx.sb(es, "qc_t%d" % i, [128, 2, CTX], BF16) for i in range(J2)]
                kr_ts = [cx.sb(es, "kr_t%d" % i, [128, L], BF16) for i in range(J2)]
                kc_ts = [cx.sb(es, "kc_t%d" % i, [128, CTX], BF16) for i in range(J2)]
                v_ts = [cx.sb(es, "v_t%d" % i, [128, NT, 64], BF16) for i in range(J2)]
                v2s = [cx.sb(es, "v2_%d" % i, [128, NT, 128], BF16) for i in range(J2)]
                sg_ts = [cx.sb(es, "sg_t%d" % i, [128, 2, T]) for i in range(J2)]
                asts = [cx.sb(es, "ast%d" % i, [128, 2, T], BF16) for i in range(J2)]
                NP = 4
                pt = [[cx.sb(es, "pt%d_%d" % (a, b), [128, 512], BF16) for b in range(5)] for a in range(NP)]
                rds = [cx.sb(es, "rd%d" % i, [128, 256]) for i in range(2)]
                aos = [cx.sb(es, "ao%d" % i, [128, 256]) for i in range(2)]
                es_pp = cx.sb(es, "es_pp", [128, 8])
                c8 = cx.sb(es, "c8", [128, 1])
                s.memset(c8.full(), 0.125)
                s.dma(es_pp.full(), e_sink.full())
                s.act(es_pp.full(), es_pp.full(), AF.Exp)
                ATT_DBG = [int(v) for v in os.environ.get("ATT_DBG", "4,18,4").split(",")]
                items = []
                for j in range(ATT_DBG[0] if go("p4") else 0):
                    qbs = ([("c", 0), ("c", 1)] + [("l", b) for b in range(16)])[:ATT_DBG[1]]
                    for qi, (kind, bi) in enumerate(qbs):
                        items.append((len(items), j, kind, bi, qi == 0, qi == len(qbs) - 1))

                def keys_of(kind, bi):
                    keys = [("c", 0, None), ("c", 1, None)]
                    if kind == "l":
                        if bi > 0:
                            keys.append(("l", bi - 1, "prev"))
                        keys.append(("l", bi, None))
                        if bi < 15:
                            keys.append(("l", bi + 1, "next"))
                    return keys

                def atA(item):
                    n, j, kind, bi, first, last = item
                    js = j % J2
                    qr_t, qc_t, kr_t, kc_t, v_t, v2, sg_t = qr_ts[js], qc_ts[js], kr_ts[js], kc_ts[js], v_ts[js], v2s[js], sg_ts[js]
                    if first:
                        s.dma(qr_t.full(), QR.view(2 * j * 128 * L, [[L, 128], [128 * L, 2], [1, L]]))
                        s.dma(qc_t.full(), QC.view(2 * j * 128 * CTX, [[CTX, 128], [128 * CTX, 2], [1, CTX]]))
                        s.dma(kr_t.full(), KR[j])
                        s.dma(kc_t.full(), KC[j])
                        s.dma(v_t.full(), VT.view(j * 64, [[256, 128], [128 * 256, NT], [1, 64]]))
                        s.dma(sg_t.full(), SG.view(2 * j * 128 * T, [[T, 128], [128 * T, 2], [1, T]]))
                        s.copy(v2[:, :, 0:64], v_t.full(), eng="pool")
                        s.copy(v2[:, :, 64:128], v_t.full(), eng="pool")
                    qsrc, q0 = (qc_t, bi * 128) if kind == "c" else (qr_t, bi * 128)
                    pts = pt[n % NP]
                    qw = qsrc.h.shape[2]
                    for ki, (kk, kb, msk) in enumerate(keys_of(kind, bi)):
                        ksrc = kc_t if kk == "c" else kr_t
                        for par in range(2):
                            p0 = par * 64
                            bs = nbank()
                            s.mm(bs[:, 0:256],
                                 [(ksrc[p0:p0 + 64, kb * 128:(kb + 1) * 128],
                                   qsrc.view(p0 * 2 * qw + q0, [[2 * qw, 64], [qw, 2], [1, 128]]))])
                            s.act(pts[ki][:, par * 256:(par + 1) * 256], bs[:, 0:256], AF.Exp, scale=c8[:, 0:1])
                        if msk is not None:
                            moff = 1024 if msk == "prev" else 512
                            s.tt(pts[ki].view(0, [[512, 128], [128, 4], [1, 128]]),
                                 pts[ki].view(0, [[512, 128], [128, 4], [1, 128]]),
                                 cst.view(moff, [[3072, 128], [0, 4], [1, 128]]), ALU.mult, eng="pool")

                def atB(item):
                    n, j, kind, bi, first, last = item
                    js = j % J2
                    v2, sg_t, ast = v2s[js], sg_ts[js], asts[js]
                    tok0 = bi * 128 if kind == "c" else CTX + bi * 128
                    keys = keys_of(kind, bi)
                    pts = pt[n % NP]
                    rd, ao = rds[n % 2], aos[n % 2]
                    vt_idx = [(kb if kk == "c" else 2 + kb) for (kk, kb, _) in keys]
                    bn = nbank()
                    s.mm(bn.full(), [(v2[:, vt_idx[ki], :], pts[ki].full()) for ki in range(len(keys))])
                    bd = nbank()
                    s.mm(bd.full(), [(onesb, pts[ki].full()) for ki in range(len(keys))])
                    for par in range(2):
                        p0 = par * 64
                        for c in range(2):
                            s.ts(rd[p0:p0 + 64, c * 128:(c + 1) * 128],
                                 bd[p0:p0 + 64, par * 256 + c * 128:par * 256 + (c + 1) * 128],
                                 es_pp[p0:p0 + 64, 2 * j + c:2 * j + c + 1], None, ALU.add)
                    s.recip(rd.full(), rd.full())
                    for par in range(2):
                        p0 = par * 64
                        s.tt(ao[p0:p0 + 64, :], bn[p0:p0 + 64, par * 256:(par + 1) * 256], rd[p0:p0 + 64, :], ALU.mult)
                    s.tt(ast.view(tok0, [[2 * T, 128], [T, 2], [1, 128]]),
                         ao.view(0, [[256, 128], [128, 2], [1, 128]]),
                         sg_t.view(tok0, [[2 * T, 128], [T, 2], [1, 128]]), ALU.mult)
                    if last:
                        s.dma(YT.view((8 + 2 * j) * 128 * T, [[T, 128], [128 * T, 2], [1, T]]), ast.full())

                pipeline(items, [atA, (lambda it: None), atB])
                s.flush()

            with ExitStack() as es:
                nb_ = {"i": 0}

                def nbank():
                    bk = banks[nb_["i"] % 8]
                    nb_["i"] += 1
                    return bk

                ytg = [cx.sb(es, "ytg%d" % i, [128, 16, 512], BF16) for i in range(2)]
                xt = [cx.sb(es, "xt5_%d" % i, [128, D]) for i in range(3)]
                x1t = [cx.sb(es, "x1t%d" % i, [128, D]) for i in range(3)]
                tmp5s = [cx.sb(es, "tmp5_%d" % i, [128, 512]) for i in range(2)]
                for i in range(NT if go("p5") else 0):
                    w = 1 if i < 2 else 0
                    gi_, ii = i // 4, i % 4
                    yg = ytg[gi_ % 2]
                    if ii == 0:
                        nt4 = min(4, NT - i)
                        s.dma(yg[:, :, 0:nt4 * 128], YT.view(i * 128, [[T, 128], [128 * T, 16], [1, nt4 * 128]]))
                    x_, o_ = xt[i % 3], x1t[i % 3]
                    s.dma(x_.full(), xin[i * 128:(i + 1) * 128, :])
                    for half in range(2):
                        tmp5 = tmp5s[half]
                        bk = nbank()
                        s.mm(bk.full(), [(yg[:, fc, ii * 128:(ii + 1) * 128], wo[fc][:, half * 512:(half + 1) * 512]) for fc in range(16)])
                        s.tt(tmp5.full(), bk.full(), gate_bc[0][w][:, half * 512:(half + 1) * 512], ALU.mult)
                        s.tt(o_[:, half * 512:(half + 1) * 512], tmp5.full(), x_[:, half * 512:(half + 1) * 512], ALU.add)
                    s.dma(X1[i * 128:(i + 1) * 128, :], o_.full())
                s.flush()

        if go("all"):
            adaln_phase(1, o_ada_w, o_ada_b)
        with ExitStack() as l1:
            if not go("all"):
                return nc
            nb_ = {"i": 0}

            def nbank():
                bk = banks[nb_["i"] % 8]
                nb_["i"] += 1
                return bk

            with ExitStack() as es:
                nw = cx.sb(es, "nw1", [128, 8])
                sc1 = [cx.sb(es, "sc1b_%d" % w, [128, 8]) for w in range(2)]
                s.dma(nw.full(), o_norm_wT.full())
                for w in range(2):
                    s.stt(sc1[w].full(), modT[1][w][:, 8:16], 1.0, nw.full(), ALU.add, ALU.mult)
                hT = [cx.sb(es, "hTb%d" % k, [128, T], BF16) for k in range(8)]
                xt = [cx.sb(es, "xtb%d" % i, [128, D]) for i in range(3)]
                xn = [cx.sb(es, "xnb%d" % i, [128, D]) for i in range(3)]
                junk = cx.sb(es, "junkb", [128, D])
                st = [cx.sb(es, "stb%d" % i, [128, 4]) for i in range(3)]
                def m0(i):
                    x_, n_, st_ = xt[i % 3], xn[i % 3], st[i % 3]
                    s.dma(x_.full(), X1[i * 128:(i + 1) * 128, :])
                    s.act(junk.full(), x_.full(), AF.Square, accum=st_[:, 0:1])
                    s.ts(st_[:, 1:2], st_[:, 0:1], 1.0 / D, EPS, ALU.mult, ALU.add)
                    s.act(st_[:, 2:3], st_[:, 1:2], AF.Sqrt)
                    s.recip(st_[:, 3:4], st_[:, 2:3])
                    s.ts(n_.full(), x_.full(), st_[:, 3:4], None, ALU.mult)

                def m1(i):
                    w = 1 if i < 2 else 0
                    n_ = xn[i % 3]
                    for half in range(2):
                        bk = nbank()
                        for kk in range(4):
                            k = half * 4 + kk
                            s.transpose(bk[:, kk * 128:(kk + 1) * 128], n_[:, k * 128:(k + 1) * 128], ident)
                        for kk in range(4):
                            k = half * 4 + kk
                            s.act(hT[k][:, i * 128:(i + 1) * 128], bk[:, kk * 128:(kk + 1) * 128], AF.Identity,
                                  bias=modT[1][w][:, k:k + 1], scale=sc1[w][:, k:k + 1])

                pipeline(list(range(NT)), [m0, m1])
                wq = [cx.sb(es, "wq%d" % i, [128, 8, 256], BF16) for i in range(8)]
                for q8 in range(8):
                    s.dma(wq[q8].full(), o_w_in.view(q8 * 256, [[2 * D, 128], [128 * 2 * D, 8], [1, 256]]), q="pool")
                ot = [cx.sb(es, "ot%d" % i, [128, D]) for i in range(2)]
                oi = 0
                for which in range(2):
                    for i in range(NT):
                        if which == 1 and i < 2:
                            continue
                        o_ = ot[oi % 2]
                        oi += 1
                        for half in range(2):
                            bk = nbank()
                            for q4 in range(2):
                                s.mm(bk[:, q4 * 256:(q4 + 1) * 256],
                                     [(hT[k][:, i * 128:(i + 1) * 128], wq[which * 4 + half * 2 + q4][:, k, :]) for k in range(8)])
                            if which == 0:
                                s.copy(o_[:, half * 512:(half + 1) * 512], bk.full(), eng="act")
                            else:
                                s.act(o_[:, half * 512:(half + 1) * 512], bk.full(), AF.Silu)
                        s.dma((U if which == 0 else SG1)[i * 128:(i + 1) * 128, :], o_.full())
                s.flush()

            L1S = os.environ.get('L1S', 'z')
            if L1S == 'a':
                return nc
            with ExitStack() as es:
                lam = cx.sb(es, "lam", [128, 2, 3, 32])
                bprm = cx.sb(es, "bprm", [128, 2, 32, 16])
                cprm = cx.sb(es, "cprm", [128, 2, 32, 16])
                s.dma(lam.full(), s5_lam.full())
                s.dma(bprm.full(), s5_b.full())
                s.dma(cprm.full(), s5_c.full())
                kc = cx.sb(es, "kconst", [128, 4])
                s.memset(kc[:, 0:1], 1.0 / 16)
                s.memset(kc[:, 1:2], math.pi / 2)
                s.memset(kc[:, 2:3], 0.0)
                s.memset(kc[:, 3:4], 1.0)
                W64 = [128, 2, 32]

                def t64(name):
                    return cx.sb(es, name, W64)

                def lv(i):
                    return lam.view(i * 32, [[192, 128], [96, 2], [1, 32]])

                dt_ = t64("dt_"); mag = t64("mag"); th = t64("th"); cs = t64("cs"); sn = t64("sn")
                t_a = t64("t_a"); t_b = t64("t_b"); t_c = t64("t_c")
                abre = t64("abre"); abim = t64("abim"); cre = t64("cre"); cim = t64("cim")
                s.act(dt_.full(), lv(2), AF.Exp)
                s.tt(t_a.full(), lv(0), dt_.full(), ALU.mult)
                s.act(mag.full(), t_a.full(), AF.Exp)
                s.tt(th.full(), lv(1), dt_.full(), ALU.mult)
                s.act(sn.full(), th.full(), AF.Sin, scale=kc[:, 0:1])
                s.act(cs.full(), th.full(), AF.Sin, scale=kc[:, 0:1], bias=kc[:, 1:2])
                for _ in range(4):
                    s.tt(t_a.full(), cs.full(), cs.full(), ALU.mult)
                    s.tt(t_b.full(), sn.full(), sn.full(), ALU.mult)
                    s.tt(t_c.full(), sn.full(), cs.full(), ALU.mult)
                    s.tt(cs.full(), t_a.full(), t_b.full(), ALU.subtract)
                    s.ts(sn.full(), t_c.full(), 2.0, None, ALU.mult)
                s.tt(abre.full(), mag.full(), cs.full(), ALU.mult)
                s.tt(abim.full(), mag.full(), sn.full(), ALU.mult)
                PW = cx.sb(es, "PW", [128, 2, 9, 64])

                def pw(ri, k):
                    return PW.view((ri * 9 + k) * 64, [[2 * 9 * 64, 128], [32, 2], [1, 32]])

                s.memset(PW[:, 0, 0, :], 1.0)
                s.memset(PW[:, 1, 0, :], 0.0)
                for k in range(8):
                    s.tt(t_a.full(), pw(0, k), abre.full(), ALU.mult)
                    s.tt(t_b.full(), pw(1, k), abim.full(), ALU.mult)
                    s.tt(pw(0, k + 1), t_a.full(), t_b.full(), ALU.subtract)
                    s.tt(t_a.full(), pw(0, k), abim.full(), ALU.mult)
                    s.tt(t_b.full(), pw(1, k), abre.full(), ALU.mult)
                    s.tt(pw(1, k + 1), t_a.full(), t_b.full(), ALU.add)
                s.ts(t_c.full(), abre.full(), -1.0, None, ALU.add)
                s.tt(t_a.full(), lv(0), lv(0), ALU.mult)
                s.tt(t_b.full(), lv(1), lv(1), ALU.mult)
                s.tt(t_a.full(), t_a.full(), t_b.full(), ALU.add)
                s.recip(dt_.full(), t_a.full())
                s.tt(t_a.full(), t_c.full(), lv(0), ALU.mult)
                s.tt(t_b.full(), abim.full(), lv(1), ALU.mult)
                s.tt(t_a.full(), t_a.full(), t_b.full(), ALU.add)
                s.tt(cre.full(), t_a.full(), dt_.full(), ALU.mult)
                s.tt(t_a.full(), abim.full(), lv(0), ALU.mult)
                s.tt(t_b.full(), t_c.full(), lv(1), ALU.mult)
                s.tt(t_a.full(), t_a.full(), t_b.full(), ALU.subtract)
                s.tt(cim.full(), t_a.full(), dt_.full(), ALU.mult)
                BB = cx.sb(es, "BB", [128, 2, 2, 512])
                tb1 = cx.sb(es, "tb1", [128, 512])
                tb2 = cx.sb(es, "tb2", [128, 512])

                def bb(ri, d_, g0=0, ng=32):
                    return BB.view((ri * 2 + d_) * 512 + g0 * 16, [[2048, 128], [16, ng], [1, 16]])

                def v3(buf, off, pstep, n1, s1, n2, s2):
                    return buf.view(off, [[pstep, 128], [s1, n1], [s2, n2]])

                def prm(buf, ri, g0=0, ng=32):
                    return buf.view(ri * 512 + g0 * 16, [[1024, 128], [16, ng], [1, 16]])

                def cf(buf, d_, g0=0, ng=32, n2=16):
                    return buf.view(d_ * 32 + g0, [[64, 128], [1, ng], [0, n2]])

                t1v = v3(tb1, 0, 512, 32, 16, 16, 1)
                t2v = v3(tb2, 0, 512, 32, 16, 16, 1)
                for d_ in range(2):
                    s.tt(t1v, prm(bprm, 0), cf(cre, d_), ALU.mult)
                    s.tt(t2v, prm(bprm, 1), cf(cim, d_), ALU.mult)
                    s.tt(bb(0, d_), t1v, t2v, ALU.subtract)
                    s.tt(t1v, prm(bprm, 1), cf(cre, d_), ALU.mult)
                    s.tt(t2v, prm(bprm, 0), cf(cim, d_), ALU.mult)
                    s.tt(bb(1, d_), t1v, t2v, ALU.add)
                LA = cx.sb(es, "LA", [128, 2, 32, 2])
                LB = cx.sb(es, "LB", [128, 2, 32, 2])
                for ri in range(2):
                    s.copy(LA.view(ri, [[128, 128], [64, 2], [2, 32]]), pw(0, 8))
                s.ts(LB.view(0, [[128, 128], [64, 2], [2, 32]]), pw(1, 8), -1.0, None, ALU.mult)
                s.copy(LB.view(1, [[128, 128], [64, 2], [2, 32]]), pw(1, 8))
                zt_ = cx.sb(es, "zt_", [16, 112], BF16)
                s.memset(zt_.full(), 0.0)
                s.flush()

                if L1S == 'b':
                    return nc
                for b in range(4 if L1S not in ('c1', 'd1', 'e1', 'f1', 'g1') else 1):
                    g0 = 8 * b
                    with ExitStack() as bs_:
                        CAB = cx.sb(bs_, "CAB", [128, 2, 2, 8 * 144])
                        WST = cx.sb(bs_, "WST", [128, 8, 2, 2, 2, 64], BF16)
                        TF = cx.sb(bs_, "TF", [128, 16, 128], BF16)
                        TB = cx.sb(bs_, "TB", [128, 16, 128], BF16)
                        CABb = cx.sb(bs_, "CABb", [128, 2, 2, 8 * 144], BF16)

                        with ExitStack() as tmp:
                            WT = cx.sb(tmp, "WT", [128, 2, 2, 8 * 128])
                            c1 = cx.sb(tmp, "c1", [128, 128])
                            c2 = cx.sb(tmp, "c2", [128, 128])
                            c1v = v3(c1, 0, 128, 8, 16, 16, 1)
                            c2v = v3(c2, 0, 128, 8, 16, 16, 1)
                            for d_ in range(2):
                                for ss in range(8):
                                    p_ = 7 - ss if d_ == 0 else ss
                                    pr = PW.view((0 * 9 + p_) * 64 + d_ * 32 + g0, [[1152, 128], [1, 8], [0, 16]])
                                    pi_ = PW.view((1 * 9 + p_) * 64 + d_ * 32 + g0, [[1152, 128], [1, 8], [0, 16]])
                                    o_re = WT.view((d_ * 2 + 0) * 1024 + ss * 16, [[4096, 128], [128, 8], [1, 16]])
                                    o_im = WT.view((d_ * 2 + 1) * 1024 + ss * 16, [[4096, 128], [128, 8], [1, 16]])
                                    s.tt(c1v, bb(0, d_, g0, 8), pr, ALU.mult)
                                    s.tt(c2v, bb(1, d_, g0, 8), pi_, ALU.mult)
                                    s.tt(o_re, c1v, c2v, ALU.subtract)
                                    s.tt(c1v, bb(1, d_, g0, 8), pr, ALU.mult)
                                    s.tt(c2v, bb(0, d_, g0, 8), pi_, ALU.mult)
                                    s.tt(o_im, c1v, c2v, ALU.add)
                            for gh in range(2):
                                p0 = gh * 64
                                for gq in range(8):
                                    bk = nbank()
                                    for d_ in range(2):
                                        for ri in range(2):
                                            sl = d_ * 2 + ri
                                            s.transpose(bk[:, sl * 64:(sl + 1) * 64],
                                                        WT.view(p0 * 4096 + (d_ * 2 + ri) * 1024 + gq * 128, [[4096, 64], [1, 128]]),
                                                        cst[p0:p0 + 64, 0, p0:p0 + 64])
                                    s.copy(WST.view(((gq * 2 + gh) * 4) * 64, [[4096, 128], [1, 256]]), bk[:, 0:256], eng="act")
                            s.flush()

                        KSB = cx.sb(bs_, "KSB", [16, 2, 16, 128], BF16)

                        def setup_part2():
                            c1v = ysb.view(0, [[512, 128], [16, 8], [1, 16]])
                            c2v = ysb.view(128, [[512, 128], [16, 8], [1, 16]])
                            for d_ in range(2):
                                for idx in range(9):
                                    p_ = idx if d_ == 0 else 8 - idx
                                    pr = PW.view((0 * 9 + p_) * 64 + d_ * 32 + g0, [[1152, 128], [1, 8], [0, 16]])
                                    pi_ = PW.view((1 * 9 + p_) * 64 + d_ * 32 + g0, [[1152, 128], [1, 8], [0, 16]])
                                    o_re = CAB.view((0 * 2 + d_) * 1152 + idx * 16, [[4608, 128], [144, 8], [1, 16]])
                                    o_im = CAB.view((1 * 2 + d_) * 1152 + idx * 16, [[4608, 128], [144, 8], [1, 16]])
                                    s.tt(c1v, prm(cprm, 0, g0, 8), pr, ALU.mult, eng="pool")
                                    s.tt(c2v, prm(cprm, 1, g0, 8), pi_, ALU.mult, eng="pool")
                                    s.tt(o_re, c1v, c2v, ALU.subtract, eng="pool")
                                    s.tt(c1v, prm(cprm, 0, g0, 8), pi_, ALU.mult, eng="pool")
                                    s.tt(c2v, prm(cprm, 1, g0, 8), pr, ALU.mult, eng="pool")
                                    s.tt(c1v, c1v, c2v, ALU.add, eng="pool")
                                    s.ts(o_im, c1v, -1.0, None, ALU.mult, eng="pool")
                            s.copy(CABb.full(), CAB.full(), eng="pool")
                            for gh in range(2):
                                p0 = gh * 64
                                for d_ in range(2):
                                    for gqq in range(2):
                                        bk = nbank()
                                        for q4 in range(4):
                                            gq = gqq * 4 + q4
                                            i0 = 0 if d_ == 0 else 1
                                            s.mm(bk[0:16, q4 * 128:(q4 + 1) * 128],
                                                 [(BB.view(p0 * 2048 + (0 * 2 + d_) * 512 + (g0 + gq) * 16, [[2048, 64], [1, 16]]),
                                                   CAB.view(p0 * 4608 + (0 * 2 + d_) * 1152 + gq * 144 + i0 * 16, [[4608, 64], [1, 128]])),
                                                  (BB.view(p0 * 2048 + (1 * 2 + d_) * 512 + (g0 + gq) * 16, [[2048, 64], [1, 16]]),
                                                   CAB.view(p0 * 4608 + (1 * 2 + d_) * 1152 + gq * 144 + i0 * 16, [[4608, 64], [1, 128]]))])
                                        s.copy(KSB.view(d_ * 2048 + (2 * gqq * 4 + gh) * 128, [[4096, 16], [256, 4], [1, 128]]),
                                               bk.view(0, [[512, 16], [128, 4], [1, 128]]), eng="act")
                            gbase = 16 * b
                            s.dma(KFP.view(gbase * 3840 + 7 * 16, [[240, 16], [3840, 16], [1, 128]]), KSB[:, 0, :, :])
                            s.dma(KBR.view(gbase * 3840, [[240, 16], [3840, 16], [1, 128]]), KSB[:, 1, :, :])
                            s.dma(KFP.view(gbase * 3840, [[240, 16], [3840, 16], [1, 112]]), zt_.view(0, [[112, 16], [0, 16], [1, 112]]))
                            s.dma(KBR.view(gbase * 3840 + 128, [[240, 16], [3840, 16], [1, 112]]), zt_.view(0, [[112, 16], [0, 16], [1, 112]]))
                            for ss in range(8):
                                s.dma(TF[ss * 16:(ss + 1) * 16, :, :], KFP.view(gbase * 3840 + (7 - ss) * 16, [[240, 16], [3840, 16], [1, 128]]))
                                s.dma(TB[ss * 16:(ss + 1) * 16, :, :], KBR.view(gbase * 3840 + (7 - ss) * 16, [[240, 16], [3840, 16], [1, 128]]))

                        u8b = cx.sb(bs_, "u8b", [128, 8, 256])
                        u8g = cx.sb(bs_, "u8g", [128, 16, 128])
                        U8T = cx.sb(bs_, "U8T", [128, 16, 288], BF16)
                        SSb = [cx.sb(bs_, "SSb%d" % i, [128, 16, 256], BF16) for i in range(2)]
                        NCOL = 326
                        PS = 16 * NCOL
                        SSD = [cx.sb(bs_, "SS%d" % i, [128, 8, 2, NCOL]) for i in range(2)]
                        CAR = [cx.sb(bs_, "CAR%d" % i, [128, 15, 8, 2]) for i in range(2)]
                        A36 = [cx.sb(bs_, "A36_%d" % i, [128, 8, 2]) for i in range(2)]
                        B36 = [cx.sb(bs_, "B36_%d" % i, [128, 8, 2]) for i in range(2)]
                        y8b = cx.sb(bs_, "y8b", [128, 8, 256])
                        ysb = cx.sb(bs_, "ysb", [128, 512])
                        TT1 = [cx.sb(bs_, "TT1_%d" % i, [128, 17, 8, 2]) for i in range(2)]
                        TT2 = [cx.sb(bs_, "TT2_%d" % i, [128, 17, 8, 2]) for i in range(2)]
                        for (j0, nj) in ((0, 32), (32, 128), (160, 128)):
                            s.dma(u8b[0:nj, :, :], U.view(8 * j0 * 1024 + 256 * b, [[8192, nj], [1024, 8], [1, 256]]))
                            s.copy(u8g.view(0, [[2048, nj], [128, 16], [16, 8], [1, 16]]),
                                   u8b.view(0, [[2048, nj], [16, 16], [256, 8], [1, 16]]), eng="act")
                            for gq4 in range(4):
                                bk = nbank()
                                for q4 in range(4):
                                    gi = gq4 * 4 + q4
                                    s.transpose(bk[:, q4 * 128:q4 * 128 + nj],
                                                u8g.view(128 * gi, [[2048, nj], [1, 128]]), cst[0:nj, 0, 0:nj])
                                s.copy(U8T.view(gq4 * 4 * 288 + j0, [[16 * 288, 128], [288, 4], [1, nj]]),
                                       bk.view(0, [[512, 128], [128, 4], [1, nj]]), eng="act")
                        if L1S in ('d', 'd1'):
                            s.flush()
                            continue
                        s.memset(SSD[0].view(0, [[PS, 128], [NCOL, 16], [1, 1]]), 0.0)
                        s.memset(SSD[0].view(289, [[PS, 128], [NCOL, 16], [1, 37]]), 0.0)
                        s.memset(SSD[1].view(288, [[PS, 128], [NCOL, 16], [1, 38]]), 0.0)
                        s.memset(SSD[0].view(289, [[PS, 128], [2 * NCOL, 8], [1, 1]]), 1.0)
                        s.memset(SSD[1].view(288 + 18 - 1, [[PS, 128], [2 * NCOL, 8], [1, 1]]), 1.0)
                        for gq in range(8):
                            for gh in range(2):
                                gi = 2 * gq + gh
                                p0 = gh * 64
                                for d_ in range(2):
                                    for ri in range(2):
                                        bk = nbank()
                                        s.mm(bk[p0:p0 + 64, 0:288],
                                             [(WST.view((((gq * 2 + gh) * 2 + d_) * 2 + ri) * 64, [[4096, 128], [1, 64]]),
                                               U8T[:, gi, :])])
                                        so = p0 * PS + (gq * 2 + ri) * NCOL
                                        if d_ == 0:
                                            s.copy(SSD[0].view(so + 1, [[PS, 64], [1, 288]]), bk[p0:p0 + 64, 0:288], eng="act")
                                        else:
                                            s.copy(SSD[1].view(so + 256, [[PS, 64], [1, 32]]), bk[p0:p0 + 64, 0:32], eng="act")
                                            s.copy(SSD[1].view(so, [[PS, 64], [1, 256]]), bk[p0:p0 + 64, 32:288], eng="act")
                        if L1S in ('e', 'e1'):
                            s.flush()
                            continue
                        DS = 8 * 2 * 289
                        setup_part2()
                        RI, GQ = NCOL, 2 * NCOL
                        SEG, NSEG = 18, 16
                        REC_ENG2 = os.environ.get('REC2', 'dve')

                        def cplx_step(items):
                            engs = ("dve", REC_ENG2)
                            for n_, (pv, psw, cv, ca, cb_, t1_, t2_) in enumerate(items):
                                s.tt(t1_, pv, ca, ALU.mult, eng=engs[n_ % 2])
                                s.tt(t2_, psw, cb_, ALU.mult, eng=engs[n_ % 2])
                            for n_, (pv, psw, cv, ca, cb_, t1_, t2_) in enumerate(items):
                                s.tt(t1_, t1_, t2_, ALU.add, eng=engs[n_ % 2])
                            for n_, (pv, psw, cv, ca, cb_, t1_, t2_) in enumerate(items):
                                if cv is not None:
                                    s.tt(cv, cv, t1_, ALU.add, eng=engs[n_ % 2])

                        def segv(SS, col, nseg):
                            return (SS.view(col, [[PS, 128], [SEG, nseg], [GQ, 8], [RI, 2]]),
                                    SS.view(col + RI, [[PS, 128], [SEG, nseg], [GQ, 8], [-RI, 2]]))

                        def coef(buf, d_, nseg):
                            return buf.view(d_ * 64 + g0 * 2, [[128, 128], [0, nseg], [2, 8], [1, 2]])

                        TTP = 17 * 16

                        for k in range(1, SEG):
                            items = []
                            for d_ in range(2):
                                pc = k if d_ == 0 else SEG - k
                                cc = k + 1 if d_ == 0 else SEG - 1 - k
                                pv, psw = segv(SSD[d_], pc, NSEG + 1)
                                cv, _ = segv(SSD[d_], cc, NSEG + 1)
                                items.append((pv, psw, cv, coef(LA, d_, NSEG + 1), coef(LB, d_, NSEG + 1), TT1[d_].full(), TT2[d_].full()))
                            cplx_step(items)
                        items = []
                        for d_ in range(2):
                            clast = 288 + SEG if d_ == 0 else 288
                            pv, psw = segv(SSD[d_], clast, 1)
                            items.append((pv, psw, None, coef(LA, d_, 1), coef(LB, d_, 1),
                                          TT1[d_].view(0, [[TTP, 128], [16, 1], [2, 8], [1, 2]]),
                                          TT2[d_].view(0, [[TTP, 128], [16, 1], [2, 8], [1, 2]])))
                        cplx_step(items)
                        for d_ in range(2):
                            l36re = TT1[d_].view(0, [[TTP, 128], [2, 8], [0, 2]])
                            s.copy(A36[d_].full(), l36re)
                            s.ts(B36[d_][:, :, 0:1], TT1[d_].view(1, [[TTP, 128], [2, 8], [1, 1]]), -1.0, None, ALU.mult)
                            s.copy(B36[d_][:, :, 1:2], TT1[d_].view(1, [[TTP, 128], [2, 8], [1, 1]]))
                        for step in range(1, NSEG):
                            items = []
                            for d_ in range(2):
                                if d_ == 0:
                                    m = step
                                    cc, pc = SEG * m + SEG, SEG * m
                                else:
                                    m = NSEG - 1 - step
                                    cc, pc = SEG * m, SEG * m + SEG
                                pv, psw = segv(SSD[d_], pc, 1)
                                cv, _ = segv(SSD[d_], cc, 1)
                                items.append((pv, psw, cv,
                                              A36[d_].view(0, [[16, 128], [0, 1], [2, 8], [1, 2]]),
                                              B36[d_].view(0, [[16, 128], [0, 1], [2, 8], [1, 2]]),
                                              TT1[d_].view(0, [[TTP, 128], [16, 1], [2, 8], [1, 2]]),
                                              TT2[d_].view(0, [[TTP, 128], [16, 1], [2, 8], [1, 2]])))
                            cplx_step(items)
                        items = []
                        for d_ in range(2):
                            pv, psw = segv(SSD[d_], SEG, NSEG - 1)
                            items.append((pv, psw, None, coef(LA, d_, NSEG - 1), coef(LB, d_, NSEG - 1),
                                          CAR[d_].full(), TT2[d_].view(0, [[TTP, 128], [16, NSEG - 1], [2, 8], [1, 2]])))
                        cplx_step(items)
                        NI = SEG - 1
                        for d_ in range(2):
                            SS = SSD[d_]
                            sb0 = SEG + 1 if d_ == 0 else 1

                            def sview(ri):
                                return SS.view(sb0 + ri * RI, [[PS, 128], [SEG, NSEG - 1], [GQ, 8], [1, NI]])

                            def tview(ri):
                                return SS.view(289 + ri * RI, [[PS, 128], [0, NSEG - 1], [GQ, 8], [1, NI]])

                            def cview(ri):
                                return CAR[d_].view(ri, [[(NSEG - 1) * 16, 128], [16, NSEG - 1], [2, 8], [0, NI]])

                            wshape = [[2048, 128], [8 * NI, NSEG - 1], [NI, 8], [1, NI]]
                            w1 = (u8g if d_ == 0 else u8b).view(0, wshape)
                            w2 = y8b.view(0, wshape)
                            s.tt(w1, tview(0), cview(0), ALU.mult)
                            s.tt(w2, tview(1), cview(1), ALU.mult)
                            s.tt(w1, w1, w2, ALU.subtract)
                            s.tt(sview(0), sview(0), w1, ALU.add)
                            s.tt(w1, tview(0), cview(1), ALU.mult)
                            s.tt(w2, tview(1), cview(0), ALU.mult)
                            s.tt(w1, w1, w2, ALU.add)
                            s.tt(sview(1), sview(1), w1, ALU.add)
                        s.copy(SSb[0].full(), SSD[0].view(32, [[PS, 128], [NCOL, 16], [1, 256]]), eng="act")
                        s.copy(SSb[1].full(), SSD[1].view(1, [[PS, 128], [NCOL, 16], [1, 256]]), eng="pool")
                        if L1S in ('f', 'f1'):
                            s.flush()
                            continue
                        for tt_ in range(2):
                            j0 = 32 + 128 * tt_
                            m0 = 128 * tt_
                            for gh in range(2):
                                p0 = gh * 64
                                for gqq in range(2):
                                    bx = nbank()
                                    by = nbank()
                                    for q4 in range(4):
                                        gq = gqq * 4 + q4
                                        gi = 2 * gq + gh
                                        s.mm(bx[:, q4 * 128:(q4 + 1) * 128],
                                             [(U8T[:, gi, j0:j0 + 128], TF[:, gi, :]), (U8T[:, gi, j0:j0 + 128], TB[:, gi, :])])
                                        pairs = []
                                        for d_ in range(2):
                                            c0 = m0
                                            i0 = 1 if d_ == 0 else 0
                                            for ri in range(2):
                                                so = p0 * 4096 + (gq * 2 + ri) * 256 + c0
                                                pairs.append((SSb[d_].view(so, [[4096, 64], [1, 128]]),
                                                              CABb.view(p0 * 4608 + (ri * 2 + d_) * 1152 + gq * 144 + i0 * 16, [[4608, 64], [1, 128]])))
                                        s.mm(by[:, q4 * 128:(q4 + 1) * 128], pairs)
                                    s.copy(ysb.full(), by.full(), eng="act")
                                    s.tt(y8b.view(32 * gqq * 4 + 16 * gh, [[2048, 128], [32, 4], [256, 8], [1, 16]]),
                                         bx.view(0, [[512, 128], [128, 4], [16, 8], [1, 16]]),
                                         ysb.view(0, [[512, 128], [128, 4], [16, 8], [1, 16]]), ALU.add)
                            s.dma(YTOK.view((CTX + 8 * m0) * 1024 + 256 * b, [[8192, 128], [1024, 8], [1, 256]]), y8b.full())
                        s.flush()

            if L1S in ('g', 'g1'):
                return nc
            with ExitStack() as es:
                gw = [cx.sb(es, "gw%d" % k, [128, D], BF16) for k in range(8)]
                ow = [cx.sb(es, "ow%d" % k, [128, D], BF16) for k in range(8)]
                dskb = cx.sb(es, "dskb", [128, D])
                glbb = cx.sb(es, "glbb", [128, D])
                fnwb = cx.sb(es, "fnwb", [128, D])
                kg = cx.sb(es, "kg", [128, 1])
                s.memset(kg.full(), 2.0 * math.sqrt(2.0 / math.pi))
                for k in range(8):
                    s.dma(gw[k].full(), o_glu_w[k * 128:(k + 1) * 128, :], q="pool")
                    s.dma(ow[k].full(), o_w_out[k * 128:(k + 1) * 128, :], q="pool")
                s.dma(dskb.full(), o_d_skip.view(0, [[0, 128], [1, D]]))
                s.dma(glbb.full(), o_glu_b.view(0, [[0, 128], [1, D]]))
                s.dma(fnwb.full(), final_norm_w.view(0, [[0, 128], [1, D]]))
                NB3 = 4
                ya = [cx.sb(es, "ya%d" % i, [128, D]) for i in range(NB3)]
                ua = [cx.sb(es, "ua%d" % i, [128, D]) for i in range(NB3)]
                sga = [cx.sb(es, "sga%d" % i, [128, D]) for i in range(NB3)]
                xa = [cx.sb(es, "xa%d" % i, [128, D]) for i in range(NB3)]
                w1s = [cx.sb(es, "w1_%d" % i, [128, D]) for i in range(NB3)]
                w2s = [cx.sb(es, "w2_%d" % i, [128, D]) for i in range(NB3)]
                w3s = [cx.sb(es, "w3_%d" % i, [128, D]) for i in range(NB3)]
                tTs = [cx.sb(es, "tT_%d" % i, [128, 8, 128], BF16) for i in range(2 * NB3)]
                sts = [cx.sb(es, "st10_%d" % i, [128, 4]) for i in range(NB3)]

                def transp8(src, tT):
                    for half in range(2):
                        bk = nbank()
                        for kk in range(4):
                            k = half * 4 + kk
                            s.transpose(bk[:, kk * 128:(kk + 1) * 128], src[:, k * 128:(k + 1) * 128], ident)
                        s.copy(tT[:, half * 4:(half + 1) * 4, :], bk.view(0, [[512, 128], [128, 4], [1, 128]]), eng="act")

                TAILN = int(os.environ.get('TAILN', NT))

                def bufs(i):
                    b_ = i % NB3
                    return ya[b_], ua[b_], sga[b_], xa[b_], w1s[b_], w2s[b_], w3s[b_], tTs[2 * b_], tTs[2 * b_ + 1], sts[b_]

                def stage0(i):
                    y_, u_, g_, x_, w1, w2, w3, tTa, tTb, st = bufs(i)
                    s.dma(y_.full(), YTOK[i * 128:(i + 1) * 128, :])
                    s.dma(u_.full(), U[i * 128:(i + 1) * 128, :])
                    s.dma(g_.full(), SG1[i * 128:(i + 1) * 128, :])
                    s.dma(x_.full(), X1[i * 128:(i + 1) * 128, :])
                    s.tt(w1.full(), u_.full(), dskb.full(), ALU.mult)
                    s.tt(y_.full(), y_.full(), w1.full(), ALU.add)
                    s.tt(w1.full(), y_.full(), y_.full(), ALU.mult)
                    s.ts(w1.full(), w1.full(), 0.044715, 1.0, ALU.mult, ALU.add)
                    s.tt(w1.full(), w1.full(), y_.full(), ALU.mult)
                    s.act(w1.full(), w1.full(), AF.Sigmoid, scale=kg[:, 0:1])
                    s.tt(w2.full(), y_.full(), w1.full(), ALU.mult)
                    transp8(w2, tTa)

                def stage1(i):
                    y_, u_, g_, x_, w1, w2, w3, tTa, tTb, st = bufs(i)
                    for half in range(2):
                        bk = nbank()
                        s.mm(bk.full(), [(tTa[:, k, :], gw[k][:, half * 512:(half + 1) * 512]) for k in range(8)])
                        s.tt(w1[:, half * 512:(half + 1) * 512], bk.full(), glbb[:, half * 512:(half + 1) * 512], ALU.add)
                    s.act(w1.full(), w1.full(), AF.Sigmoid)
                    s.tt(w2.full(), w2.full(), w1.full(), ALU.mult)
                    s.tt(w2.full(), w2.full(), g_.full(), ALU.mult)
                    transp8(w2, tTb)

                def stage2(i):
                    y_, u_, g_, x_, w1, w2, w3, tTa, tTb, st = bufs(i)
                    for half in range(2):
                        bk = nbank()
                        s.mm(bk.full(), [(tTb[:, k, :], ow[k][:, half * 512:(half + 1) * 512]) for k in range(8)])
                        s.tt(w1[:, half * 512:(half + 1) * 512], bk.full(), gate_bc[1][0][:, half * 512:(half + 1) * 512], ALU.mult)
                    s.tt(w3.full(), w1.full(), x_.full(), ALU.add)
                    s.act(w1.full(), w3.full(), AF.Square, accum=st[:, 0:1])
                    s.ts(st[:, 1:2], st[:, 0:1], 1.0 / D, EPS, ALU.mult, ALU.add)
                    s.act(st[:, 2:3], st[:, 1:2], AF.Sqrt)
                    s.recip(st[:, 3:4], st[:, 2:3])
                    s.act(w3.full(), w3.full(), AF.Copy, scale=st[:, 3:4])
                    s.tt(w2.full(), w3.full(), fnwb.full(), ALU.mult)
                    s.dma(out_t[(i - 2) * 128:(i - 1) * 128, :], w2.full())

                pipeline(list(range(2, TAILN)), [stage0, (lambda i: None), stage1, stage2])
                s.flush()

    return nc


def _consts():
    c = np.zeros((128, 6, 512), np.float32)
    j = np.arange(128)[:, None]
    l = np.arange(128)[None, :]
    c[:, 0, :128] = np.eye(128, dtype=np.float32)
    c[:, 1, :128] = (j <= l)
    c[:, 2, :128] = (j >= l)
    c[:, 3, :] = 1.0
    nf = np.where(l < j, -30000.0, 0.0).astype(np.float32)
    nb = np.where(l > j, -30000.0, 0.0).astype(np.float32)
    c[:, 4, :] = np.tile(nf, (1, 4))
    c[:, 5, :] = np.tile(nb, (1, 4))
    return c


def _rope_tables():
    rows = L // 64
    row = np.repeat(np.arange(rows, dtype=np.float32), 64)
    col = np.tile(np.arange(64, dtype=np.float32), rows)
    n_freq = 16
    inv = (np.float32(10000.0) ** (-np.arange(n_freq, dtype=np.float32) / n_freq)).astype(np.float32)
    ang = np.concatenate([row[:, None] * inv, col[:, None] * inv], axis=-1).astype(np.float32)
    cos = np.cos(ang).astype(np.float32)
    sin = np.sin(ang).astype(np.float32)
    cosT = np.zeros((128, L), np.float32)
    sinT = np.zeros((128, L), np.float32)
    for h2 in range(2):
        for half in range(2):
            p0 = h2 * 64 + half * 32
            cosT[p0:p0 + 32] = cos.T
            sinT[p0:p0 + 32] = (-sin.T if half == 0 else sin.T)
    return np.stack([cosT, sinT], axis=1)


def _vecT(v, nchunk):
    return np.ascontiguousarray(np.asarray(v, np.float32).reshape(nchunk, 128).T)


def prep_inputs(b, inp):
    f = lambda a: np.ascontiguousarray(np.asarray(a, np.float32))
    m = {}
    m["xin"] = f(np.concatenate([inp["ctx"][b], inp["x"][b]], axis=0))
    cv = np.stack([inp["c"][b], inp["c_ctx"]], axis=0)
    m["cvecT"] = f(cv.reshape(2, 8, 128).transpose(2, 0, 1))
    m["consts"] = _consts()
    m["rope"] = _rope_tables()
    m["e_ada_w"] = f(inp["e_ada_w"][0])
    m["e_ada_b"] = f(inp["e_ada_b"][0]).reshape(1, -1)
    m["e_norm_wT"] = _vecT(inp["e_norm_w"][0], 8)
    w = f(inp["e_w_in"][0])
    q = w[:, OFF_Q:OFF_Q + 1024].reshape(D, 16, 2, 32)
    qs = q[:, :, ::-1, :].reshape(D, 1024)
    k = w[:, OFF_KV:OFF_KV + 256].reshape(D, 4, 64)
    kr = np.concatenate([k, k], axis=2).reshape(D, 512)
    ks = k.reshape(D, 4, 2, 32)[:, :, ::-1, :].reshape(D, 4, 64)
    ksr = np.concatenate([ks, ks], axis=2).reshape(D, 512)
    m["e_w_in"] = f(np.concatenate([w, qs, kr, ksr], axis=1))
    cw = f(inp["e_conv_w"][0])
    m["e_conv_wT"] = f(cw.reshape(5, 12, 128).transpose(2, 1, 0))
    m["e_conv_bT"] = _vecT(inp["e_conv_b"][0], 12)
    m["e_dt_bias"] = f(inp["e_dt_bias"][0]).reshape(1, 32)
    m["e_a_log"] = f(inp["e_a_log"][0]).reshape(1, 32)
    m["e_d_skip"] = f(inp["e_d_skip"][0]).reshape(1, 16)
    m["e_ssd_norm_wT"] = _vecT(inp["e_ssd_norm_w"][0], 8)
    sk = f(inp["e_sink"][0]).reshape(8, 2)
    m["e_sink"] = f(np.repeat(sk.T[:, None, :], 64, axis=1).reshape(128, 8))
    m["e_w_out"] = f(inp["e_w_out"][0])
    m["o_ada_w"] = f(inp["o_ada_w"][0])
    m["o_ada_b"] = f(inp["o_ada_b"][0]).reshape(1, -1)
    m["o_norm_wT"] = _vecT(inp["o_norm_w"][0], 8)
    m["o_w_in"] = f(inp["o_w_in"][0])

    def gl(a):
        a = np.asarray(a, np.float32)
        rest = a.shape[2:]
        a = a.reshape((32, 2, 64) + rest)
        a = np.moveaxis(a, 0, 2)
        return a.reshape((128, 32) + rest)

    lam = np.zeros((128, 2, 3, 32), np.float32)
    for d_ in range(2):
        lam[:, d_, 0] = gl(inp["o_lam_re"][0][d_])
        lam[:, d_, 1] = gl(inp["o_lam_im"][0][d_])
        lam[:, d_, 2] = gl(np.repeat(np.asarray(inp["o_log_step"][0][d_])[:, None], 64, axis=1))
    m["s5_lam"] = f(lam)
    m["s5_b"] = f(np.stack([gl(inp["o_b_re"][0]), gl(inp["o_b_im"][0])], axis=1))
    cr = np.asarray(inp["o_c_re"][0]).transpose(0, 2, 1)
    ci = np.asarray(inp["o_c_im"][0]).transpose(0, 2, 1)
    m["s5_c"] = f(np.stack([gl(cr), gl(ci)], axis=1))
    m["o_d_skip"] = f(inp["o_d_skip"][0]).reshape(1, -1)
    m["o_glu_w"] = f(inp["o_glu_w"][0])
    m["o_glu_b"] = f(inp["o_glu_b"][0]).reshape(1, -1)
    m["o_w_out"] = f(inp["o_w_out"][0])
    m["final_norm_w"] = f(inp["final_norm_w"]).reshape(1, -1)
    return m


def kernel(**inputs):
    nc = build_program()
    in_maps = [prep_inputs(b, inputs) for b in range(8)]
    res = run_bass_kernel_spmd(nc, in_maps, core_ids=list(range(8)))
    return np.stack([r["out"] for r in res.results], axis=0)
```

```python
import math
import os
from contextlib import ExitStack

import numpy as np
import concourse.bass as bass
import concourse.mybir as mybir
from concourse.bass_utils import run_bass_kernel_spmd

F32 = mybir.dt.float32
BF16 = mybir.dt.bfloat16
AF = mybir.ActivationFunctionType
ALU = mybir.AluOpType

D = 1024
T = 2304
NT = 18
CTX = 256
L = 2048
EPS = 1e-6
TG = [(0, 256), (256, 512), (768, 512), (1280, 512), (1792, 512)]

SES_ALL = os.environ.get('SES', '0') == '1'
SAME_ENGINE_SYNC = {'act': SES_ALL, 'dve': SES_ALL, 'pool': True, 'pe': False, 'sp': True}
SEM_EPOCH = 30000


class V:
    __slots__ = ("buf", "ap")

    def __init__(self, buf, ap):
        self.buf = buf
        self.ap = ap


class Buf:
    def __init__(self, name, h):
        self.name = name
        self.h = h
        self.last_w = None
        self.readers = []
        self.is_psum = False

    def __getitem__(self, idx):
        return V(self, self.h[idx])

    def full(self):
        return V(self, self.h.ap())

    def view(self, offset, pattern):
        return V(self, bass.AP(self.h, offset, [list(p) for p in pattern]))


class Sched:
    ENG = ("pe", "act", "dve", "pool", "sp")

    def __init__(self, nc):
        self.nc = nc
        self.prog = {e: [] for e in self.ENG}
        self.sem = {}
        self.cnt = {}
        self.semid = 0
        self.known = {e: {} for e in self.ENG}
        for e in ("pe", "act", "dve", "pool"):
            self._new_engine_sem(e)
        self.nds = 8
        self.dsem = {}
        self.duse = {}
        self.dcnt = {}
        for q in ("sp", "pool"):
            self.dsem[q] = []
            self.duse[q] = []
            for i in range(self.nds):
                key = "d_%s_%d" % (q, i)
                self.dsem[q].append((nc.alloc_semaphore(key), key))
                self.duse[q].append(0)
            self.dcnt[q] = 0
        self.n_ops = 0

    def _new_engine_sem(self, e):
        self.semid += 1
        key = "s_%s_%d" % (e, self.semid)
        self.sem[e] = (self.nc.alloc_semaphore(key), key)
        self.cnt[e] = 0

    def _deps(self, reads, writes):
        deps = {}

        def add(tok):
            if tok is None:
                return
            h, key, val = tok
            if key not in deps or deps[key][1] < val:
                deps[key] = (h, val)

        for r in reads:
            add(r.buf.last_w)
            if r.buf.is_psum:
                for t in r.buf.readers:
                    add(t)
        for w in writes:
            add(w.buf.last_w)
            for t in w.buf.readers:
                add(t)
        return deps

    def _emit_waits(self, eng, deps, own_key=None):
        kn = self.known[eng]
        for key, (h, val) in deps.items():
            if key == own_key and not SAME_ENGINE_SYNC[eng]:
                continue
            if kn.get(key, 0) >= val:
                continue
            kn[key] = val
            self.prog[eng].append(("wait", h, val))

    def _update(self, tok, reads, writes):
        for w in writes:
            w.buf.last_w = tok
            w.buf.readers = []
        for r in reads:
            if r.buf.last_w is not tok:
                r.buf.readers.append(tok)

    def op(self, eng, fn, reads=(), writes=()):
        reads = [r for r in reads if r is not None]
        writes = list(writes)
        if self.cnt[eng] >= SEM_EPOCH:
            self._new_engine_sem(eng)
        h, key = self.sem[eng]
        own = None if eng == "pe" else key
        deps = self._deps(reads, writes)
        if eng == "pe":
            deps.pop(key, None)
        self._emit_waits(eng, deps, own_key=own)
        self.cnt[eng] += 1
        self.prog[eng].append(("op", fn, h, 1))
        tok = (h, key, self.cnt[eng])
        self._update(tok, reads, writes)
        self.n_ops += 1
        return tok

    def dma(self, out, in_, q="sp", **kw):
        deps = self._deps([in_], [out])
        self._emit_waits(q, deps)
        k = self.dcnt[q] % self.nds
        self.dcnt[q] += 1
        h, key = self.dsem[q][k]
        prev = 16 * self.duse[q][k]
        if prev > 0 and self.known[q].get(key, 0) < prev:
            self.known[q][key] = prev
            self.prog[q].append(("wait", h, prev))
        self.duse[q][k] += 1
        val = 16 * self.duse[q][k]
        o_ap, i_ap = out.ap, in_.ap
        self.prog[q].append(("op", lambda e: e.dma_start(out=o_ap, in_=i_ap, **kw), h, 16))
        tok = (h, key, val)
        self._update(tok, [in_], [out])
        self.n_ops += 1
        return tok

    def finish_dmas(self):
        for q in ("sp", "pool"):
            for k in range(self.nds):
                h, key = self.dsem[q][k]
                val = 16 * self.duse[q][k]
                if val > 0 and self.known[q].get(key, 0) < val:
                    self.known[q][key] = val
                    self.prog[q].append(("wait", h, val))

    def flush(self, name=None):
        self.finish_dmas()
        nc = self.nc
        prog = self.prog
        self.prog = {e: [] for e in self.ENG}

        def run(items, e):
            for it in items:
                if it[0] == "wait":
                    e.wait_ge(it[1], it[2])
                else:
                    inst = it[1](e)
                    inst.then_inc(it[2], it[3])

        with nc.Block() as block:
            if prog["sp"]:
                @block.sync
                def _(e):
                    run(prog["sp"], e)
            if prog["act"]:
                @block.scalar
                def _(e):
                    run(prog["act"], e)
            if prog["dve"]:
                @block.vector
                def _(e):
                    run(prog["dve"], e)
            if prog["pool"]:
                @block.gpsimd
                def _(e):
                    run(prog["pool"], e)
            if prog["pe"]:
                @block.tensor
                def _(e):
                    run(prog["pe"], e)

    def mm(self, out, pairs):
        n = len(pairs)

        def fn(e):
            inst = None
            for i, (l, r) in enumerate(pairs):
                inst = e.matmul(out.ap, l.ap, r.ap, start=(i == 0), stop=(i == n - 1))
            return inst

        self.op("pe", fn, reads=[p[0] for p in pairs] + [p[1] for p in pairs], writes=[out])

    def mm1(self, out, l, r, start, stop):
        self.op("pe", lambda e: e.matmul(out.ap, l.ap, r.ap, start=start, stop=stop), reads=[l, r], writes=[out])

    def transpose(self, out, in_, ident):
        self.op("pe", lambda e: e.transpose(out.ap, in_.ap, ident.ap), reads=[in_, ident], writes=[out])

    def act(self, out, in_, func, bias=None, scale=None, accum=None):
        kw = {}
        reads = [in_]
        writes = [out]
        if bias is not None:
            if isinstance(bias, V):
                kw["bias"] = bias.ap
                reads.append(bias)
            else:
                kw["bias"] = bias
        if scale is not None:
            if isinstance(scale, V):
                kw["scale"] = scale.ap
                reads.append(scale)
            else:
                kw["scale"] = scale
        if accum is not None:
            kw["accum_out"] = accum.ap
            writes.append(accum)
        self.op("act", lambda e: e.activation(out.ap, in_.ap, func, **kw), reads=reads, writes=writes)

    def ts(self, out, in0, s1, s2, op0, op1=None, eng="dve"):
        reads = [in0]
        a1 = s1
        a2 = s2
        if isinstance(s1, V):
            reads.append(s1)
            a1 = s1.ap
        if isinstance(s2, V):
            reads.append(s2)
            a2 = s2.ap
        if op1 is None:
            self.op(eng, lambda e: e.tensor_scalar(out.ap, in0.ap, a1, a2, op0), reads=reads, writes=[out])
        else:
            self.op(eng, lambda e: e.tensor_scalar(out.ap, in0.ap, a1, a2, op0, op1), reads=reads, writes=[out])

    def tt(self, out, in0, in1, op, eng="dve"):
        self.op(eng, lambda e: e.tensor_tensor(out.ap, in0.ap, in1.ap, op), reads=[in0, in1], writes=[out])

    def stt(self, out, in0, scalar, in1, op0, op1):
        reads = [in0, in1]
        sc = scalar
        if isinstance(scalar, V):
            reads.append(scalar)
            sc = scalar.ap
        self.op("dve", lambda e: e.scalar_tensor_tensor(out.ap, in0.ap, sc, in1.ap, op0, op1),
                reads=reads, writes=[out])

    def copy(self, out, in_, eng="dve"):
        if eng == "act":
            self.op("act", lambda e: e.copy(out.ap, in_.ap), reads=[in_], writes=[out])
        else:
            self.op(eng, lambda e: e.tensor_copy(out.ap, in_.ap), reads=[in_], writes=[out])

    def recip(self, out, in_):
        self.op("dve", lambda e: e.reciprocal(out.ap, in_.ap), reads=[in_], writes=[out])

    def memset(self, out, val, eng="dve"):
        self.op(eng, lambda e: e.memset(out.ap, val), reads=[], writes=[out])


class Ctx:
    def __init__(self, nc, sched):
        self.nc = nc
        self.s = sched
        self.uid = 0

    def sb(self, es, name, shape, dtype=F32):
        self.uid += 1
        h = es.enter_context(self.nc.sbuf_tensor("%s_%d" % (name, self.uid), list(shape), dtype))
        return Buf(name, h)

    def ps(self, es, name, shape=(128, 512), dtype=F32):
        self.uid += 1
        h = es.enter_context(self.nc.psum_tensor("%s_%d" % (name, self.uid), list(shape), dtype))
        b = Buf(name, h)
        b.is_psum = True
        return b

    def dram(self, name, shape, dtype=F32, kind="Internal"):
        h = self.nc.dram_tensor(name, list(shape), dtype, kind=kind)
        return Buf(name, h)


def pipeline(items, stages):
    n, k = len(items), len(stages)
    for t in range(n + k - 1):
        for j in range(k - 1, -1, -1):
            i = t - j
            if 0 <= i < n:
                stages[j](items[i])


def bc_mid(v_buf, base_off, pstep, nparts, n_outer, outer_step, n_inner):
    return v_buf.view(base_off, [[pstep, nparts], [outer_step, n_outer], [0, n_inner]])


E_NCOL = 5152
OFF_Z = 0
OFF_XBC = 1024
OFF_DT = 2560
OFF_Q = 2592
OFF_KV = 3616
OFF_G = 4128
OFF_QS = 5152
OFF_KR = 6176
OFF_KSR = 6688
E_NCOL_EXT = 7200


ORDER = ["p1", "p2a", "p2b", "p2c", "p2d", "p2e", "p2f", "p2g", "p2h", "p3", "p4", "p5", "all"]


def build_program(debug=(), stop="all"):
    def go(tag):
        return ORDER.index(tag) <= ORDER.index(stop)
    nc = bass.Bass("TRN2", target_bir_lowering=False)
    s = Sched(nc)
    cx = Ctx(nc, s)
    dbg = set(debug)

    def din(name, shape):
        return Buf(name, nc.dram_tensor(name, list(shape), F32, kind="ExternalInput"))

    def dout(name, shape):
        return Buf(name, nc.dram_tensor(name, list(shape), F32, kind="ExternalOutput"))

    def scratch(name, shape, dtype=F32):
        if name in dbg:
            return dout(name, shape)
        return Buf(name, nc.dram_tensor(name, list(shape), dtype))

    xin = din("xin", [T, D])
    cvecT = din("cvecT", [128, 2, 8])
    consts = din("consts", [128, 6, 512])
    rope = din("rope", [128, 2, L])
    e_ada_w = din("e_ada_w", [D, 3 * D])
    e_ada_b = din("e_ada_b", [1, 3 * D])
    e_norm_wT = din("e_norm_wT", [128, 8])
    e_w_in = din("e_w_in", [D, E_NCOL_EXT])
    e_conv_wT = din("e_conv_wT", [128, 12, 5])
    e_conv_bT = din("e_conv_bT", [128, 12])
    e_dt_bias = din("e_dt_bias", [1, 32])
    e_a_log = din("e_a_log", [1, 32])
    e_d_skip = din("e_d_skip", [1, 16])
    e_ssd_norm_wT = din("e_ssd_norm_wT", [128, 8])
    e_sink = din("e_sink", [128, 8])
    e_w_out = din("e_w_out", [2 * D, D])
    o_ada_w = din("o_ada_w", [D, 3 * D])
    o_ada_b = din("o_ada_b", [1, 3 * D])
    o_norm_wT = din("o_norm_wT", [128, 8])
    o_w_in = din("o_w_in", [D, 2 * D])
    s5_lam = din("s5_lam", [128, 2, 3, 32])
    s5_b = din("s5_b", [128, 2, 32, 16])
    s5_c = din("s5_c", [128, 2, 32, 16])
    o_d_skip = din("o_d_skip", [1, D])
    o_glu_w = din("o_glu_w", [D, D])
    o_glu_b = din("o_glu_b", [1, D])
    o_w_out = din("o_w_out", [D, D])
    final_norm_w = din("final_norm_w", [1, D])
    out_t = dout("out", [L, D])

    XS = scratch("XS", [T, 1024])
    BTOK = scratch("BTOK", [T, 256], BF16)
    BT = scratch("BT", [2, 128, T], BF16)
    CT = scratch("CT", [2, 128, T], BF16)
    SZ = scratch("SZ", [T, 1024])
    QR = scratch("QR", [8, 128, L], BF16)
    QC = scratch("QC", [8, 128, CTX], BF16)
    KR = scratch("KR", [4, 128, L], BF16)
    KC = scratch("KC", [4, 128, CTX], BF16)
    VT = scratch("VT", [T, 256], BF16)
    SG = scratch("SG", [8, 128, T])
    YF = scratch("YF", [T, 1024])
    YT = scratch("YT", [16, 128, T], BF16)
    X1 = scratch("X1", [T, 1024])
    U = scratch("U", [T, 1024])
    SG1 = scratch("SG1", [T, 1024])
    YTOK = scratch("YTOK", [T, 1024])
    KFP = scratch("KFP", [64, 16, 15, 16], BF16)
    KBR = scratch("KBR", [64, 16, 15, 16], BF16)
    HT = scratch("HT", [8, 128, T]) if "HT" in dbg else None
    DTD = scratch("DTD", [T, 32]) if "DTD" in dbg else None
    MODD = scratch("MODD", [4, 128, 24]) if "MODD" in dbg else None

    with ExitStack() as top:
        banks = [cx.ps(top, "bank%d" % i) for i in range(8)]
        cst = cx.sb(top, "cst", [128, 6, 512])
        s.dma(cst.full(), consts.full())
        ident = cst[:, 0, 0:128]
        tri = cst[:, 1, 0:128]
        utri = cst[:, 2, 0:128]
        ones = cst[:, 3, 0:128]
        onesb_t = cx.sb(top, "onesb", [128, 128], BF16)
        s.memset(onesb_t.full(), 1.0)
        onesb = onesb_t.full()
        modT = [[cx.sb(top, "modT%d%d" % (l, w), [128, 24]) for w in range(2)] for l in range(2)]
        gate_bc = [[cx.sb(top, "gate%d%d" % (l, w), [128, 1024]) for w in range(2)] for l in range(2)]
        scs = cx.sb(top, "scs", [128, 2, 8])

        def adaln_phase(layer, ada_w, ada_b):
            with ExitStack() as es:
                aw = [cx.sb(es, "aw%d" % k, [128, 3 * D]) for k in range(8)]
                ab2 = cx.sb(es, "ab2", [2, 3 * D])
                modrow2 = cx.sb(es, "modrow2", [2, 3 * D])
                if layer == 0:
                    cv = cx.sb(es, "cv", [128, 2, 8])
                    s.dma(cv.full(), cvecT.full())
                    s.act(scs.full(), cv.full(), AF.Silu)
                s.dma(ab2[0:1, :], ada_b.full())
                s.dma(ab2[1:2, :], ada_b.full())
                for k in range(8):
                    s.dma(aw[k].full(), ada_w[k * 128:(k + 1) * 128, :])
                for k in range(8):
                    for fg in range(6):
                        s.mm1(banks[fg][0:2, :], scs.view(k, [[16, 128], [8, 2]]), aw[k][:, fg * 512:(fg + 1) * 512],
                              start=(k == 0), stop=(k == 7))
                for fg in range(6):
                    s.tt(modrow2[0:2, fg * 512:(fg + 1) * 512], banks[fg][0:2, :], ab2[0:2, fg * 512:(fg + 1) * 512], ALU.add)
                bk = banks[6]
                for fc in range(24):
                    s.mm(bk[:, 2 * fc:2 * fc + 2], [(modrow2[0:2, fc * 128:(fc + 1) * 128], cst[0:2, 0, 0:2])])
                for w in range(2):
                    s.copy(modT[layer][w].full(), bk.view(w, [[512, 128], [2, 24]]))
                bi = 0
                for w in range(2):
                    selw = cst[0:2, 0, 128 + 128 * w:256 + 128 * w]
                    for hh in range(2):
                        bk2 = banks[(7 + bi) % 8]
                        bi += 1
                        s.mm(bk2.full(), [(selw, modrow2[0:2, 2048 + hh * 512:2048 + (hh + 1) * 512])])
                        s.copy(gate_bc[layer][w][:, hh * 512:(hh + 1) * 512], bk2.full(), eng="act")
                    if MODD is not None:
                        s.dma(MODD[layer * 2 + w], modT[layer][w].full())
                s.flush()

        adaln_phase(0, e_ada_w, e_ada_b)

        with ExitStack() as l0:
            DT = cx.sb(l0, "DT", [128, NT, 32])
            DTA = cx.sb(l0, "DTA", [128, NT, 32])
            nw = cx.sb(l0, "nw", [128, 8])
            sc1 = [cx.sb(l0, "sc1_%d" % w, [128, 8]) for w in range(2)]
            s.dma(nw.full(), e_norm_wT.full())
            for w in range(2):
                s.stt(sc1[w].full(), modT[0][w][:, 8:16], 1.0, nw.full(), ALU.add, ALU.mult)

            wo = [cx.sb(l0, "wo%d" % k, [128, D], BF16) for k in range(16)]
            hts = ExitStack()
            hT = [cx.sb(hts, "hT%d" % k, [128, T], BF16) for k in range(8)]
            with ExitStack() as es:
                xt = [cx.sb(es, "xt%d" % i, [128, D]) for i in range(3)]
                xn = [cx.sb(es, "xn%d" % i, [128, D]) for i in range(3)]
                junk = cx.sb(es, "junk", [128, D])
                st = [cx.sb(es, "st%d" % i, [128, 4]) for i in range(3)]
                def n0(i):
                    x_, n_, st_ = xt[i % 3], xn[i % 3], st[i % 3]
                    s.dma(x_.full(), xin[i * 128:(i + 1) * 128, :])
                    s.act(junk.full(), x_.full(), AF.Square, accum=st_[:, 0:1])
                    s.ts(st_[:, 1:2], st_[:, 0:1], 1.0 / D, EPS, ALU.mult, ALU.add)
                    s.act(st_[:, 2:3], st_[:, 1:2], AF.Sqrt)
                    s.recip(st_[:, 3:4], st_[:, 2:3])
                    s.ts(n_.full(), x_.full(), st_[:, 3:4], None, ALU.mult)

                def n1(i):
                    w = 1 if i < 2 else 0
                    n_ = xn[i % 3]
                    for half in range(2):
                        bk = banks[(2 * i + half) % 8]
                        for kk in range(4):
                            k = half * 4 + kk
                            s.transpose(bk[:, kk * 128:(kk + 1) * 128], n_[:, k * 128:(k + 1) * 128], ident)
                        for kk in range(4):
                            k = half * 4 + kk
                            s.act(hT[k][:, i * 128:(i + 1) * 128], bk[:, kk * 128:(kk + 1) * 128], AF.Identity,
                                  bias=modT[0][w][:, k:k + 1], scale=sc1[w][:, k:k + 1])

                pipeline(list(range(NT)), [n0, n1])
                if HT is not None:
                    for k in range(8):
                        s.dma(HT[k], hT[k].full())
                s.flush()

            with ExitStack() as es:
                WB = 256
                NWB, PF = 6, 4
                wbuf = [cx.sb(es, "wbuf%d" % i, [128, 8, WB], BF16) for i in range(NWB)]
                wplan = [(OFF_XBC + 256 * k, 256) for k in range(6)]
                for qc in range(8):
                    wplan += [(OFF_Q + qc * 128, 128), (OFF_QS + qc * 128, 128)]
                for j in range(4):
                    wplan += [(OFF_KR + j * 128, 128), (OFF_KSR + j * 128, 128)]
                wplan += [(OFF_G + 256 * k, 256) for k in range(4)]
                wplan += [(OFF_Z + 256 * k, 256) for k in range(4)]
                wplan += [(OFF_KV + 256, 256), (OFF_DT, 32)]
                wstate = {"i": 0, "issued": 0}

                def _issue(n):
                    col0, ncol = wplan[n]
                    wb = wbuf[n % NWB]
                    s.dma(wb[:, :, 0:ncol], e_w_in.view(col0, [[E_NCOL_EXT, 128], [128 * E_NCOL_EXT, 8], [1, ncol]]), q="pool")

                def load_w(col0, ncol=WB):
                    i = wstate["i"]
                    wstate["i"] += 1
                    assert wplan[i] == (col0, ncol), (i, wplan[i], col0, ncol)
                    while wstate["issued"] < min(i + PF + 1, len(wplan)):
                        _issue(wstate["issued"])
                        wstate["issued"] += 1
                    return wbuf[i % NWB]

                bstate = {"i": 0}

                def nbank():
                    bk = banks[bstate["i"] % 8]
                    bstate["i"] += 1
                    return bk

                def fm_mm(wb, cc, t0, n):
                    bk = nbank()
                    s.mm(bk[:, 0:n], [(wb[:, k, cc * 128:(cc + 1) * 128], hT[k][:, t0:t0 + n]) for k in range(8)])
                    return bk

                xraws = [cx.sb(es, "xraw%d" % i, [128, T]) for i in range(2)]
                accs = [cx.sb(es, "acc%d" % i, [128, T]) for i in range(2)]
                acc = accs[0]
                accbs = [cx.sb(es, "accb%d" % i, [128, T], BF16) for i in range(2)]
                accb = accbs[0]
                rc_i = {"i": 0}
                tmp1s = [cx.sb(es, "tmp1_%d" % i, [128, 512]) for i in range(2)]
                tmp2s = [cx.sb(es, "tmp2_%d" % i, [128, 512]) for i in range(2)]
                stg = [cx.sb(es, "stg%d" % i, [128, 4, 128]) for i in range(2)]
                stgb = [cx.sb(es, "stgb%d" % i, [128, 4, 128], BF16) for i in range(2)]
                rp = cx.sb(es, "rp", [128, 2, L])
                cw = cx.sb(es, "cw", [128, 12, 5])
                cb = cx.sb(es, "cb", [128, 12])
                dtb = cx.sb(es, "dtb", [128, 32])
                abc = cx.sb(es, "abc", [128, 32])
                s.dma(rp.full(), rope.full())
                s.dma(cw.full(), e_conv_wT.full())
                s.dma(cb.full(), e_conv_bT.full())
                s.dma(dtb.full(), e_dt_bias.view(0, [[0, 128], [1, 32]]))
                s.dma(abc.full(), e_a_log.view(0, [[0, 128], [1, 32]]))
                s.act(abc.full(), abc.full(), AF.Exp)
                s.ts(abc.full(), abc.full(), -1.0, None, ALU.mult)
                stg_i = {"i": 0}

                def transposes_to(dst, col0, src, lowp=False):
                    for i0 in range(0, NT, 4):
                        nb = min(4, NT - i0)
                        bk = nbank()
                        for ii in range(nb):
                            i = i0 + ii
                            s.transpose(bk[:, ii * 128:(ii + 1) * 128], src[:, i * 128:(i + 1) * 128], ident)
                        sg_ = (stgb if lowp else stg)[stg_i["i"] % 2]
                        stg_i["i"] += 1
                        s.copy(sg_[:, 0:nb, :], bk.view(0, [[512, 128], [128, nb], [1, 128]]), eng="act")
                        ncols = dst.h.shape[1]
                        s.dma(dst.view(i0 * 128 * ncols + col0, [[ncols, 128], [128 * ncols, nb], [1, 128]]),
                              sg_[:, 0:nb, :])

                wb_of = {}

                def xa(fc):
                    if fc % 2 == 0:
                        wb_of[fc // 2] = load_w(OFF_XBC + fc * 128)
                    wb = wb_of[fc // 2]
                    xraw = xraws[fc % 2]
                    for (t0, n) in TG:
                        bk = fm_mm(wb, fc % 2, t0, n)
                        s.copy(xraw[:, t0:t0 + n], bk[:, 0:n], eng="act")

                def xb(fc):
                    xraw, acc = xraws[fc % 2], accs[fc % 2]
                    s.ts(acc.full(), xraw.full(), cw[:, fc, 2:3], cb[:, fc:fc + 1], ALU.mult, ALU.add)
                    for kk in (0, 1, 3, 4):
                        d_ = kk - 2
                        for (s0, sl) in ((0, CTX), (CTX, L)):
                            lo = max(s0, s0 - d_)
                            hi = min(s0 + sl, s0 + sl - d_)
                            s.stt(acc[:, lo:hi], xraw[:, lo + d_:hi + d_], cw[:, fc, kk:kk + 1], acc[:, lo:hi],
                                  ALU.mult, ALU.add)
                    s.act(acc.full(), acc.full(), AF.Silu)
                    if fc < 8:
                        transposes_to(XS, fc * 128, acc)
                    elif fc < 10:
                        s.copy(accb.full(), acc.full(), eng="act")
                        s.dma(BT[fc - 8], accb.full())
                        transposes_to(BTOK, (fc - 8) * 128, acc, lowp=True)
                    else:
                        s.copy(accb.full(), acc.full(), eng="act")
                        s.dma(CT[fc - 10], accb.full())

                pipeline(list(range(12 if go('p2a') else 0)), [xa, xb])

                def rope_chunk(col_plain, col_swap, dst_rot, dst_ctx):
                    accb = accbs[rc_i["i"] % 2]
                    rc_i["i"] += 1
                    wa = load_w(col_plain, 128)
                    wsw = load_w(col_swap, 128)
                    for gi, (t0, n) in enumerate(TG):
                        bka = fm_mm(wa, 0, t0, n)
                        if gi == 0:
                            s.copy(accb[:, 0:CTX], bka[:, 0:CTX], eng="act")
                            continue
                        bkb = fm_mm(wsw, 0, t0, n)
                        l0 = t0 - CTX
                        tmp1, tmp2 = tmp1s[gi % 2], tmp2s[gi % 2]
                        s.tt(tmp1.full(), bka.full(), rp[:, 0, l0:l0 + 512], ALU.mult)
                        s.tt(tmp2.full(), bkb.full(), rp[:, 1, l0:l0 + 512], ALU.mult)
                        s.tt(accb[:, t0:t0 + n], tmp1.full(), tmp2.full(), ALU.add)
                    s.dma(dst_ctx, accb[:, 0:CTX])
                    s.dma(dst_rot, accb[:, CTX:T])

                for qc in range(8 if go('p2b') else 0):
                    rope_chunk(OFF_Q + qc * 128, OFF_QS + qc * 128, QR[qc], QC[qc])
                for j in range(4 if go('p2c') else 0):
                    rope_chunk(OFF_KR + j * 128, OFF_KSR + j * 128, KR[j], KC[j])

                for gc in range(8 if go('p2d') else 0):
                    acc = accs[gc % 2]
                    if gc % 2 == 0:
                        wb = load_w(OFF_G + gc * 128)
                    for (t0, n) in TG:
                        bk = fm_mm(wb, gc % 2, t0, n)
                        s.act(acc[:, t0:t0 + n], bk[:, 0:n], AF.Silu)
                    s.dma(SG[gc], acc.full())

                NT_E = NT if go('p2e') else 0
                wz = [load_w(OFF_Z + i * 256) for i in range(4)]
                for i in range(NT_E):
                    z_a = accs[i % 2]
                    for half in range(2):
                        bk = nbank()
                        for q4 in range(2):
                            wbz = wz[half * 2 + q4]
                            s.mm(bk[:, q4 * 256:(q4 + 1) * 256],
                                 [(hT[k][:, i * 128:(i + 1) * 128], wbz[:, k, :]) for k in range(8)])
                        s.act(z_a[:, half * 512:(half + 1) * 512], bk.full(), AF.Silu)
                    s.dma(SZ[i * 128:(i + 1) * 128, :], z_a[:, 0:1024])
                wv = load_w(OFF_KV + 256)
                wdt = load_w(OFF_DT, 32)
                vt = [cx.sb(es, "vt%d" % i, [128, 256], BF16) for i in range(2)]
                for i in range(NT if go('p2f') else 0):
                    bk = nbank()
                    s.mm(bk[:, 0:256], [(hT[k][:, i * 128:(i + 1) * 128], wv[:, k, :]) for k in range(8)])
                    s.copy(vt[i % 2].full(), bk[:, 0:256], eng="act")
                    s.dma(VT[i * 128:(i + 1) * 128, :], vt[i % 2].full())
                for i in range(NT if go('p2g') else 0):
                    bk = nbank()
                    s.mm(bk[:, 0:32], [(hT[k][:, i * 128:(i + 1) * 128], wdt[:, k, 0:32]) for k in range(8)])
                    s.tt(DT[:, i, :], bk[:, 0:32], dtb.full(), ALU.add)
                    if go('p2h'):
                        s.act(DT[:, i, :], DT[:, i, :], AF.Exp)
                        s.ts(DT[:, i, :], DT[:, i, :], 1.0, None, ALU.add)
                        s.act(DT[:, i, :], DT[:, i, :], AF.Ln)
                    s.tt(DTA[:, i, :], DT[:, i, :], abc.full(), ALU.mult)
                    if DTD is not None:
                        s.dma(DTD[i * 128:(i + 1) * 128, :], DT[:, i, :])
                s.flush()
            hts.close()
            for k in range(16):
                s.dma(wo[k].full(), e_w_out[k * 128:(k + 1) * 128, :], q="pool")

            with ExitStack() as es:
                nb_ = {"i": 0}

                def nbank():
                    bk = banks[nb_["i"] % 8]
                    nb_["i"] += 1
                    return bk

                N3 = 3
                N4 = 4
                xs_t = [cx.sb(es, "xs_t%d" % i, [128, 1024]) for i in range(N4)]
                b_t = [cx.sb(es, "b_t%d" % i, [128, 256], BF16) for i in range(N3)]
                bt_t = [cx.sb(es, "bt_t%d" % i, [128, 2, 128], BF16) for i in range(N3)]
                ct_t = [cx.sb(es, "ct_t%d" % i, [128, 2, 128], BF16) for i in range(N3)]
                yf_t = [cx.sb(es, "yf_t%d" % i, [128, 1024]) for i in range(N4)]
                sz_t = [cx.sb(es, "sz_t%d" % i, [128, 1024]) for i in range(2)]
                MTs = [cx.sb(es, "MT%d" % i, [128, 2048], BF16) for i in range(N3)]
                xcs = [cx.sb(es, "xc%d" % i, [128, 1024], BF16) for i in range(N3)]
                xcds = [cx.sb(es, "xcd%d" % i, [128, 1024], BF16) for i in range(N3)]
                tmpos = [cx.sb(es, "tmpo%d" % i, [128, 1024]) for i in range(N3)]
                ytots = [cx.sb(es, "ytot%d" % i, [128, 1024]) for i in range(2)]
                sms = [cx.sb(es, "sm%d" % i, [128, 4, 16]) for i in range(N3)]
                st3s = [cx.sb(es, "st3_%d" % i, [128, 4]) for i in range(N3)]
                ystgs = [cx.sb(es, "ystg%d" % i, [128, 8, 128], BF16) for i in range(2)]
                dtatris = [cx.sb(es, "dtatri%d" % i, [128, 2048]) for i in range(2)]
                decTs = [cx.sb(es, "decT%d" % i, [128, 2048]) for i in range(2)]
                cb_sbs = [cx.sb(es, "cb_sb%d" % i, [128, 256]) for i in range(2)]
                junk = cx.sb(es, "junk3", [128, 1024])
                Hs = [cx.sb(es, "Hs%d" % g, [128, 512]) for g in range(2)]
                Hb = [cx.sb(es, "Hb%d" % g, [128, 512], BF16) for g in range(2)]
                dsk = cx.sb(es, "dsk", [128, 16])
                snw = cx.sb(es, "snw", [128, 8])
                cm1 = cx.sb(es, "cm1", [128, 1])
                s.memset(cm1.full(), -1.0)
                s.dma(dsk.full(), e_d_skip.view(0, [[0, 128], [1, 16]]))
                s.dma(snw.full(), e_ssd_norm_wT.full())

                def bc3(buf, off, pstep, n1, s1, n2, s2):
                    return buf.view(off, [[pstep, 128], [s1, n1], [s2, n2]])

                n_ch = NT if go("p3") else 0
                for d_ in range(2):
                    order = list(range(NT)) if d_ == 0 else [1, 0] + list(range(NT - 1, 1, -1))
                    order = order[:n_ch]
                    TRIoff = 512 if d_ == 0 else 1024
                    TRIv = tri if d_ == 0 else utri
                    negm = cst[:, 4 + d_, :]
                    for g in range(2):
                        s.memset(Hs[g].full(), 0.0)
                        s.memset(Hb[g].full(), 0.0)

                    def stA(item, d_=d_, TRIoff=TRIoff, TRIv=TRIv, negm=negm):
                        ci, i = item
                        p3, p2, p4 = ci % N3, ci % 2, ci % N4
                        xs_, b_, bt_, ct_ = xs_t[p4], b_t[p3], bt_t[p3], ct_t[p3]
                        MT, xc, xcd, sm = MTs[p3], xcs[p3], xcds[p3], sms[p3]
                        dtatri, decT, cb_sb = dtatris[p2], decTs[p2], cb_sbs[p2]
                        s.dma(xs_.full(), XS[i * 128:(i + 1) * 128, :])
                        s.dma(b_.full(), BTOK[i * 128:(i + 1) * 128, :])
                        s.dma(bt_.full(), BT.view(i * 128, [[T, 128], [128 * T, 2], [1, 128]]))
                        s.dma(ct_.full(), CT.view(i * 128, [[T, 128], [128 * T, 2], [1, 128]]))
                        if d_ == 1:
                            s.dma(yf_t[p4].full(), YF[i * 128:(i + 1) * 128, :])
                        dta_i = DTA[:, i, d_ * 16:(d_ + 1) * 16]
                        doff = i * 32 + d_ * 16
                        s.tt(bc3(dtatri, 0, 2048, 16, 128, 128, 1), bc3(DTA, doff, NT * 32, 16, 1, 128, 0),
                             bc3(cst, TRIoff, 3072, 16, 0, 128, 1), ALU.mult, eng="pool")
                        bs = nbank()
                        s.mm(bs[:, 0:16], [(TRIv, dta_i)])
                        s.mm(bs[:, 16:32], [(ones, dta_i)])
                        na, ea, de, cd = sm[:, 0, :], sm[:, 1, :], sm[:, 2, :], sm[:, 3, :]
                        s.ts(na, bs[:, 0:16], -1.0, None, ALU.mult)
                        s.act(ea, bs[:, 0:16], AF.Exp)
                        s.tt(de, bs[:, 16:32], na, ALU.add)
                        s.act(de, de, AF.Exp)
                        s.act(cd, bs[:, 16:32], AF.Exp)
                        for hq in range(4):
                            bq = nbank()
                            s.mm(bq.full(), [(ones, dtatri[:, hq * 512:(hq + 1) * 512]), (ident, negm)])
                            for hh in range(4):
                                h = hq * 4 + hh
                                s.act(decT[:, h * 128:(h + 1) * 128], bq[:, hh * 128:(hh + 1) * 128], AF.Exp,
                                      bias=sm[:, 0, h:h + 1])
                        bc = nbank()
                        for g in range(2):
                            s.mm(bc[:, g * 128:(g + 1) * 128], [(bt_[:, g, :], ct_[:, g, :])])
                        s.copy(cb_sb.full(), bc[:, 0:256], eng="act")
                        for g in range(2):
                            s.tt(bc3(MT, g * 1024, 2048, 8, 128, 128, 1), bc3(decT, g * 1024, 2048, 8, 128, 128, 1),
                                 bc3(cb_sb, g * 128, 256, 8, 0, 128, 1), ALU.mult)
                        s.tt(bc3(xc, 0, 1024, 16, 64, 64, 1), bc3(xs_, 0, 1024, 16, 64, 64, 1),
                             bc3(DT, doff, NT * 32, 16, 1, 64, 0), ALU.mult, eng="pool")
                        s.tt(bc3(xcd, 0, 1024, 16, 64, 64, 1), bc3(xc, 0, 1024, 16, 64, 64, 1),
                             bc3(sm, 32, 64, 16, 1, 64, 0), ALU.mult, eng="pool")
                        if d_ == 1:
                            s.tt(bc3(tmpos[p3], 0, 1024, 16, 64, 64, 1), bc3(xs_, 0, 1024, 16, 64, 64, 1),
                                 bc3(dsk, 0, 16, 16, 1, 64, 0), ALU.mult, eng="pool")
                            s.tt(yf_t[p4].full(), yf_t[p4].full(), tmpos[p3].full(), ALU.add, eng="pool")

                    def stB(item, d_=d_):
                        ci, i = item
                        p3 = ci % N3
                        b_, ct_ = b_t[p3], ct_t[p3]
                        MT, xc, xcd, sm, tmpo, ytot = MTs[p3], xcs[p3], xcds[p3], sms[p3], tmpos[p3], ytots[ci % 2]
                        ydst = yf_t[ci % N4] if d_ == 0 else ytot
                        if d_ == 1:
                            s.dma(sz_t[ci % 2].full(), SZ[i * 128:(i + 1) * 128, :])
                        for g in range(2):
                            by = nbank()
                            for hh in range(8):
                                h = g * 8 + hh
                                s.mm(by[:, hh * 64:(hh + 1) * 64], [(MT[:, h * 128:(h + 1) * 128], xc[:, h * 64:(h + 1) * 64])])
                            bo = nbank()
                            s.mm(bo.full(), [(ct_[:, g, :], Hb[g].full())])
                            s.tt(bc3(tmpo, g * 512, 1024, 8, 64, 64, 1), bc3(bo, 0, 512, 8, 64, 64, 1),
                                 bc3(sm, 16 + g * 8, 64, 8, 1, 64, 0), ALU.mult)
                            s.tt(ydst[:, g * 512:(g + 1) * 512], by.full(), tmpo[:, g * 512:(g + 1) * 512], ALU.add)
                        for g in range(2):
                            bst = nbank()
                            s.mm(bst.full(), [(b_[:, g * 128:(g + 1) * 128], xcd[:, g * 512:(g + 1) * 512])])
                            s.tt(bc3(Hs[g], 0, 512, 8, 64, 64, 1), bc3(Hs[g], 0, 512, 8, 64, 64, 1),
                                 bc3(sm, 48 + g * 8, 64, 8, 1, 64, 0), ALU.mult)
                            s.tt(Hs[g].full(), Hs[g].full(), bst.full(), ALU.add)
                            s.copy(Hb[g].full(), Hs[g].full(), eng="act")
                        if d_ == 0:
                            s.dma(YF[i * 128:(i + 1) * 128, :], yf_t[ci % N4].full())

                    def stC(item, d_=d_):
                        if d_ == 0:
                            return
                        ci, i = item
                        p3, p2 = ci % N3, ci % 2
                        ytot, sz_, st3, ystg = ytots[ci % 2], sz_t[ci % 2], st3s[p3], ystgs[p2]
                        s.tt(ytot.full(), ytot.full(), yf_t[ci % N4].full(), ALU.add)
                        s.tt(ytot.full(), ytot.full(), sz_.full(), ALU.mult)
                        s.act(junk.full(), ytot.full(), AF.Square, accum=st3[:, 0:1])
                        s.ts(st3[:, 1:2], st3[:, 0:1], 1.0 / 1024, EPS, ALU.mult, ALU.add)
                        s.act(st3[:, 2:3], st3[:, 1:2], AF.Sqrt)
                        s.recip(st3[:, 3:4], st3[:, 2:3])
                        s.act(ytot.full(), ytot.full(), AF.Copy, scale=st3[:, 3:4])
                        for half in range(2):
                            bk = nbank()
                            for kk in range(4):
                                k = half * 4 + kk
                                s.transpose(bk[:, kk * 128:(kk + 1) * 128], ytot[:, k * 128:(k + 1) * 128], ident)
                            for kk in range(4):
                                k = half * 4 + kk
                                s.act(ystg[:, k, :], bk[:, kk * 128:(kk + 1) * 128], AF.Copy, scale=snw[:, k:k + 1])
                        s.dma(YT.view(i * 128, [[T, 128], [128 * T, 8], [1, 128]]), ystg.full())

                    pipeline(list(enumerate(order)), [stA, (lambda it: None), stB, stC])
                s.flush()

            with ExitStack() as es:
                nb_ = {"i": 0}

                def nbank():
                    bk = banks[nb_["i"] % 8]
                    nb_["i"] += 1
                    return bk

                J2 = 2
                qr_ts = [cx.sb(es, "qr_t%d" % i, [128, 2, L], BF16) for i in range(J2)]
                qc_ts = [cx.sb(es, "qc_t%d" % i, [128, 2, CTX], BF16) for i in range(J2)]
                kr_ts = [cx.sb(es, "kr_t%d" % i, [128, L], BF16) for i in range(J2)]
                kc_ts = [cx.sb(es, "kc_t%d" % i, [128, CTX], BF16) for i in range(J2)]
                v_ts = [cx.sb(es, "v_t%d" % i, [128, NT, 64], BF16) for i in range(J2)]
                v2s = [cx.sb(es, "v2_%d" % i, [128, NT, 128], BF16) for i in range(J2)]
                sg_ts = [cx.sb(es, "sg_t%d" % i, [128, 2, T]) for i in range(J2)]
                asts = [cx.sb(es, "ast%d" % i, [128, 2, T], BF16) for i in range(J2)]
                NP = 4
                pt = [[cx.sb(es, "pt%d_%d" % (a, b), [128, 512], BF16) for b in range(5)] for a in range(NP)]
                rds = [cx.sb(es, "rd%d" % i, [128, 256]) for i in range(2)]
                aos = [cx.sb(es, "ao%d" % i, [128, 256]) for i in range(2)]
                es_pp = cx.sb(es, "es_pp", [128, 8])
                c8 = cx.sb(es, "c8", [128, 1])
                s.memset(c8.full(), 0.125)
                s.dma(es_pp.full(), e_sink.full())
                s.act(es_pp.full(), es_pp.full(), AF.Exp)
                ATT_DBG = [int(v) for v in os.environ.get("ATT_DBG", "4,18,4").split(",")]
                items = []
                for j in range(ATT_DBG[0] if go("p4") else 0):
                    qbs = ([("c", 0), ("c", 1)] + [("l", b) for b in range(16)])[:ATT_DBG[1]]
                    for qi, (kind, bi) in enumerate(qbs):
                        items.append((len(items), j, kind, bi, qi == 0, qi == len(qbs) - 1))

                def keys_of(kind, bi):
                    keys = [("c", 0, None), ("c", 1, None)]
                    if kind == "l":
                        if bi > 0:
                            keys.append(("l", bi - 1, "prev"))
                        keys.append(("l", bi, None))
                        if bi < 15:
                            keys.append(("l", bi + 1, "next"))
                    return keys

                def atA(item):
                    n, j, kind, bi, first, last = item
                    js = j % J2
                    qr_t, qc_t, kr_t, kc_t, v_t, v2, sg_t = qr_ts[js], qc_ts[js], kr_ts[js], kc_ts[js], v_ts[js], v2s[js], sg_ts[js]
                    if first:
                        s.dma(qr_t.full(), QR.view(2 * j * 128 * L, [[L, 128], [128 * L, 2], [1, L]]))
                        s.dma(qc_t.full(), QC.view(2 * j * 128 * CTX, [[CTX, 128], [128 * CTX, 2], [1, CTX]]))
                        s.dma(kr_t.full(), KR[j])
                        s.dma(kc_t.full(), KC[j])
                        s.dma(v_t.full(), VT.view(j * 64, [[256, 128], [128 * 256, NT], [1, 64]]))
                        s.dma(sg_t.full(), SG.view(2 * j * 128 * T, [[T, 128], [128 * T, 2], [1, T]]))
                        s.copy(v2[:, :, 0:64], v_t.full(), eng="pool")
                        s.copy(v2[:, :, 64:128], v_t.full(), eng="pool")
                    qsrc, q0 = (qc_t, bi * 128) if kind == "c" else (qr_t, bi * 128)
                    pts = pt[n % NP]
                    qw = qsrc.h.shape[2]
                    for ki, (kk, kb, msk) in enumerate(keys_of(kind, bi)):
                        ksrc = kc_t if kk == "c" else kr_t
                        for par in range(2):
                            p0 = par * 64
                            bs = nbank()
                            s.mm(bs[:, 0:256],
                                 [(ksrc[p0:p0 + 64, kb * 128:(kb + 1) * 128],
                                   qsrc.view(p0 * 2 * qw + q0, [[2 * qw, 64], [qw, 2], [1, 128]]))])
                            s.act(pts[ki][:, par * 256:(par + 1) * 256], bs[:, 0:256], AF.Exp, scale=c8[:, 0:1])
                        if msk is not None:
                            moff = 1024 if msk == "prev" else 512
                            s.tt(pts[ki].view(0, [[512, 128], [128, 4], [1, 128]]),
                                 pts[ki].view(0, [[512, 128], [128, 4], [1, 128]]),
                                 cst.view(moff, [[3072, 128], [0, 4], [1, 128]]), ALU.mult, eng="pool")

                def atB(item):
                    n, j, kind, bi, first, last = item
                    js = j % J2
                    v2, sg_t, ast = v2s[js], sg_ts[js], asts[js]
                    tok0 = bi * 128 if kind == "c" else CTX + bi * 128
                    keys = keys_of(kind, bi)
                    pts = pt[n % NP]
                    rd, ao = rds[n % 2], aos[n % 2]
                    vt_idx = [(kb if kk == "c" else 2 + kb) for (kk, kb, _) in keys]
                    bn = nbank()
                    s.mm(bn.full(), [(v2[:, vt_idx[ki], :], pts[ki].full()) for ki in range(len(keys))])
                    bd = nbank()
                    s.mm(bd.full(), [(onesb, pts[ki].full()) for ki in range(len(keys))])
                    for par in range(2):
                        p0 = par * 64
                        for c in range(2):
                            s.ts(rd[p0:p0 + 64, c * 128:(c + 1) * 128],
                                 bd[p0:p0 + 64, par * 256 + c * 128:par * 256 + (c + 1) * 128],
                                 es_pp[p0:p0 + 64, 2 * j + c:2 * j + c + 1], None, ALU.add)
                    s.recip(rd.full(), rd.full())
                    for par in range(2):
                        p0 = par * 64
                        s.tt(ao[p0:p0 + 64, :], bn[p0:p0 + 64, par * 256:(par + 1) * 256], rd[p0:p0 + 64, :], ALU.mult)
                    s.tt(ast.view(tok0, [[2 * T, 128], [T, 2], [1, 128]]),
                         ao.view(0, [[256, 128], [128, 2], [1, 128]]),
                         sg_t.view(tok0, [[2 * T, 128], [T, 2], [1, 128]]), ALU.mult)
                    if last:
                        s.dma(YT.view((8 + 2 * j) * 128 * T, [[T, 128], [128 * T, 2], [1, T]]), ast.full())

                pipeline(items, [atA, (lambda it: None), atB])
                s.flush()

            with ExitStack() as es:
                nb_ = {"i": 0}

                def nbank():
                    bk = banks[nb_["i"] % 8]
                    nb_["i"] += 1
                    return bk

                ytg = [cx.sb(es, "ytg%d" % i, [128, 16, 512], BF16) for i in range(2)]
                xt = [cx.sb(es, "xt5_%d" % i, [128, D]) for i in range(3)]
                x1t = [cx.sb(es, "x1t%d" % i, [128, D]) for i in range(3)]
                tmp5s = [cx.sb(es, "tmp5_%d" % i, [128, 512]) for i in range(2)]
                for i in range(NT if go("p5") else 0):
                    w = 1 if i < 2 else 0
                    gi_, ii = i // 4, i % 4
                    yg = ytg[gi_ % 2]
                    if ii == 0:
                        nt4 = min(4, NT - i)
                        s.dma(yg[:, :, 0:nt4 * 128], YT.view(i * 128, [[T, 128], [128 * T, 16], [1, nt4 * 128]]))
                    x_, o_ = xt[i % 3], x1t[i % 3]
                    s.dma(x_.full(), xin[i * 128:(i + 1) * 128, :])
                    for half in range(2):
                        tmp5 = tmp5s[half]
                        bk = nbank()
                        s.mm(bk.full(), [(yg[:, fc, ii * 128:(ii + 1) * 128], wo[fc][:, half * 512:(half + 1) * 512]) for fc in range(16)])
                        s.tt(tmp5.full(), bk.full(), gate_bc[0][w][:, half * 512:(half + 1) * 512], ALU.mult)
                        s.tt(o_[:, half * 512:(half + 1) * 512], tmp5.full(), x_[:, half * 512:(half + 1) * 512], ALU.add)
                    s.dma(X1[i * 128:(i + 1) * 128, :], o_.full())
                s.flush()

        if go("all"):
            adaln_phase(1, o_ada_w, o_ada_b)
        with ExitStack() as l1:
            if not go("all"):
                return nc
            nb_ = {"i": 0}

            def nbank():
                bk = banks[nb_["i"] % 8]
                nb_["i"] += 1
                return bk

            with ExitStack() as es:
                nw = cx.sb(es, "nw1", [128, 8])
                sc1 = [cx.sb(es, "sc1b_%d" % w, [128, 8]) for w in range(2)]
                s.dma(nw.full(), o_norm_wT.full())
                for w in range(2):
                    s.stt(sc1[w].full(), modT[1][w][:, 8:16], 1.0, nw.full(), ALU.add, ALU.mult)
                hT = [cx.sb(es, "hTb%d" % k, [128, T], BF16) for k in range(8)]
                xt = [cx.sb(es, "xtb%d" % i, [128, D]) for i in range(3)]
                xn = [cx.sb(es, "xnb%d" % i, [128, D]) for i in range(3)]
                junk = cx.sb(es, "junkb", [128, D])
                st = [cx.sb(es, "stb%d" % i, [128, 4]) for i in range(3)]
                def m0(i):
                    x_, n_, st_ = xt[i % 3], xn[i % 3], st[i % 3]
                    s.dma(x_.full(), X1[i * 128:(i + 1) * 128, :])
                    s.act(junk.full(), x_.full(), AF.Square, accum=st_[:, 0:1])
                    s.ts(st_[:, 1:2], st_[:, 0:1], 1.0 / D, EPS, ALU.mult, ALU.add)
                    s.act(st_[:, 2:3], st_[:, 1:2], AF.Sqrt)
                    s.recip(st_[:, 3:4], st_[:, 2:3])
                    s.ts(n_.full(), x_.full(), st_[:, 3:4], None, ALU.mult)

                def m1(i):
                    w = 1 if i < 2 else 0
                    n_ = xn[i % 3]
                    for half in range(2):
                        bk = nbank()
                        for kk in range(4):
                            k = half * 4 + kk
                            s.transpose(bk[:, kk * 128:(kk + 1) * 128], n_[:, k * 128:(k + 1) * 128], ident)
                        for kk in range(4):
                            k = half * 4 + kk
                            s.act(hT[k][:, i * 128:(i + 1) * 128], bk[:, kk * 128:(kk + 1) * 128], AF.Identity,
                                  bias=modT[1][w][:, k:k + 1], scale=sc1[w][:, k:k + 1])

                pipeline(list(range(NT)), [m0, m1])
                wq = [cx.sb(es, "wq%d" % i, [128, 8, 256], BF16) for i in range(8)]
                for q8 in range(8):
                    s.dma(wq[q8].full(), o_w_in.view(q8 * 256, [[2 * D, 128], [128 * 2 * D, 8], [1, 256]]), q="pool")
                ot = [cx.sb(es, "ot%d" % i, [128, D]) for i in range(2)]
                oi = 0
                for which in range(2):
                    for i in range(NT):
                        if which == 1 and i < 2:
                            continue
                        o_ = ot[oi % 2]
                        oi += 1
                        for half in range(2):
                            bk = nbank()
                            for q4 in range(2):
                                s.mm(bk[:, q4 * 256:(q4 + 1) * 256],
                                     [(hT[k][:, i * 128:(i + 1) * 128], wq[which * 4 + half * 2 + q4][:, k, :]) for k in range(8)])
                            if which == 0:
                                s.copy(o_[:, half * 512:(half + 1) * 512], bk.full(), eng="act")
                            else:
                                s.act(o_[:, half * 512:(half + 1) * 512], bk.full(), AF.Silu)
                        s.dma((U if which == 0 else SG1)[i * 128:(i + 1) * 128, :], o_.full())
                s.flush()

            L1S = os.environ.get('L1S', 'z')
            if L1S == 'a':
                return nc
            with ExitStack() as es:
                lam = cx.sb(es, "lam", [128, 2, 3, 32])
                bprm = cx.sb(es, "bprm", [128, 2, 32, 16])
                cprm = cx.sb(es, "cprm", [128, 2, 32, 16])
                s.dma(lam.full(), s5_lam.full())
                s.dma(bprm.full(), s5_b.full())
                s.dma(cprm.full(), s5_c.full())
                kc = cx.sb(es, "kconst", [128, 4])
                s.memset(kc[:, 0:1], 1.0 / 16)
                s.memset(kc[:, 1:2], math.pi / 2)
                s.memset(kc[:, 2:3], 0.0)
                s.memset(kc[:, 3:4], 1.0)
                W64 = [128, 2, 32]

                def t64(name):
                    return cx.sb(es, name, W64)

                def lv(i):
                    return lam.view(i * 32, [[192, 128], [96, 2], [1, 32]])

                dt_ = t64("dt_"); mag = t64("mag"); th = t64("th"); cs = t64("cs"); sn = t64("sn")
                t_a = t64("t_a"); t_b = t64("t_b"); t_c = t64("t_c")
                abre = t64("abre"); abim = t64("abim"); cre = t64("cre"); cim = t64("cim")
                s.act(dt_.full(), lv(2), AF.Exp)
                s.tt(t_a.full(), lv(0), dt_.full(), ALU.mult)
                s.act(mag.full(), t_a.full(), AF.Exp)
                s.tt(th.full(), lv(1), dt_.full(), ALU.mult)
                s.act(sn.full(), th.full(), AF.Sin, scale=kc[:, 0:1])
                s.act(cs.full(), th.full(), AF.Sin, scale=kc[:, 0:1], bias=kc[:, 1:2])
                for _ in range(4):
                    s.tt(t_a.full(), cs.full(), cs.full(), ALU.mult)
                    s.tt(t_b.full(), sn.full(), sn.full(), ALU.mult)
                    s.tt(t_c.full(), sn.full(), cs.full(), ALU.mult)
                    s.tt(cs.full(), t_a.full(), t_b.full(), ALU.subtract)
                    s.ts(sn.full(), t_c.full(), 2.0, None, ALU.mult)
                s.tt(abre.full(), mag.full(), cs.full(), ALU.mult)
                s.tt(abim.full(), mag.full(), sn.full(), ALU.mult)
                PW = cx.sb(es, "PW", [128, 2, 9, 64])

                def pw(ri, k):
                    return PW.view((ri * 9 + k) * 64, [[2 * 9 * 64, 128], [32, 2], [1, 32]])

                s.memset(PW[:, 0, 0, :], 1.0)
                s.memset(PW[:, 1, 0, :], 0.0)
                for k in range(8):
                    s.tt(t_a.full(), pw(0, k), abre.full(), ALU.mult)
                    s.tt(t_b.full(), pw(1, k), abim.full(), ALU.mult)
                    s.tt(pw(0, k + 1), t_a.full(), t_b.full(), ALU.subtract)
                    s.tt(t_a.full(), pw(0, k), abim.full(), ALU.mult)
                    s.tt(t_b.full(), pw(1, k), abre.full(), ALU.mult)
                    s.tt(pw(1, k + 1), t_a.full(), t_b.full(), ALU.add)
                s.ts(t_c.full(), abre.full(), -1.0, None, ALU.add)
                s.tt(t_a.full(), lv(0), lv(0), ALU.mult)
                s.tt(t_b.full(), lv(1), lv(1), ALU.mult)
                s.tt(t_a.full(), t_a.full(), t_b.full(), ALU.add)
                s.recip(dt_.full(), t_a.full())
                s.tt(t_a.full(), t_c.full(), lv(0), ALU.mult)
                s.tt(t_b.full(), abim.full(), lv(1), ALU.mult)
                s.tt(t_a.full(), t_a.full(), t_b.full(), ALU.add)
                s.tt(cre.full(), t_a.full(), dt_.full(), ALU.mult)
                s.tt(t_a.full(), abim.full(), lv(0), ALU.mult)
                s.tt(t_b.full(), t_c.full(), lv(1), ALU.mult)
                s.tt(t_a.full(), t_a.full(), t_b.full(), ALU.subtract)
                s.tt(cim.full(), t_a.full(), dt_.full(), ALU.mult)
                BB = cx.sb(es, "BB", [128, 2, 2, 512])
                tb1 = cx.sb(es, "tb1", [128, 512])
                tb2 = cx.sb(es, "tb2", [128, 512])

                def bb(ri, d_, g0=0, ng=32):
                    return BB.view((ri * 2 + d_) * 512 + g0 * 16, [[2048, 128], [16, ng], [1, 16]])

                def v3(buf, off, pstep, n1, s1, n2, s2):
                    return buf.view(off, [[pstep, 128], [s1, n1], [s2, n2]])

                def prm(buf, ri, g0=0, ng=32):
                    return buf.view(ri * 512 + g0 * 16, [[1024, 128], [16, ng], [1, 16]])

                def cf(buf, d_, g0=0, ng=32, n2=16):
                    return buf.view(d_ * 32 + g0, [[64, 128], [1, ng], [0, n2]])

                t1v = v3(tb1, 0, 512, 32, 16, 16, 1)
                t2v = v3(tb2, 0, 512, 32, 16, 16, 1)
                for d_ in range(2):
                    s.tt(t1v, prm(bprm, 0), cf(cre, d_), ALU.mult)
                    s.tt(t2v, prm(bprm, 1), cf(cim, d_), ALU.mult)
                    s.tt(bb(0, d_), t1v, t2v, ALU.subtract)
                    s.tt(t1v, prm(bprm, 1), cf(cre, d_), ALU.mult)
                    s.tt(t2v, prm(bprm, 0), cf(cim, d_), ALU.mult)
                    s.tt(bb(1, d_), t1v, t2v, ALU.add)
                LA = cx.sb(es, "LA", [128, 2, 32, 2])
                LB = cx.sb(es, "LB", [128, 2, 32, 2])
                for ri in range(2):
                    s.copy(LA.view(ri, [[128, 128], [64, 2], [2, 32]]), pw(0, 8))
                s.ts(LB.view(0, [[128, 128], [64, 2], [2, 32]]), pw(1, 8), -1.0, None, ALU.mult)
                s.copy(LB.view(1, [[128, 128], [64, 2], [2, 32]]), pw(1, 8))
                zt_ = cx.sb(es, "zt_", [16, 112], BF16)
                s.memset(zt_.full(), 0.0)
                s.flush()

                if L1S == 'b':
                    return nc
                for b in range(4 if L1S not in ('c1', 'd1', 'e1', 'f1', 'g1') else 1):
                    g0 = 8 * b
                    with ExitStack() as bs_:
                        CAB = cx.sb(bs_, "CAB", [128, 2, 2, 8 * 144])
                        WST = cx.sb(bs_, "WST", [128, 8, 2, 2, 2, 64], BF16)
                        TF = cx.sb(bs_, "TF", [128, 16, 128], BF16)
                        TB = cx.sb(bs_, "TB", [128, 16, 128], BF16)
                        CABb = cx.sb(bs_, "CABb", [128, 2, 2, 8 * 144], BF16)

                        with ExitStack() as tmp:
                            WT = cx.sb(tmp, "WT", [128, 2, 2, 8 * 128])
                            c1 = cx.sb(tmp, "c1", [128, 128])
                            c2 = cx.sb(tmp, "c2", [128, 128])
                            c1v = v3(c1, 0, 128, 8, 16, 16, 1)
                            c2v = v3(c2, 0, 128, 8, 16, 16, 1)
                            for d_ in range(2):
                                for ss in range(8):
                                    p_ = 7 - ss if d_ == 0 else ss
                                    pr = PW.view((0 * 9 + p_) * 64 + d_ * 32 + g0, [[1152, 128], [1, 8], [0, 16]])
                                    pi_ = PW.view((1 * 9 + p_) * 64 + d_ * 32 + g0, [[1152, 128], [1, 8], [0, 16]])
                                    o_re = WT.view((d_ * 2 + 0) * 1024 + ss * 16, [[4096, 128], [128, 8], [1, 16]])
                                    o_im = WT.view((d_ * 2 + 1) * 1024 + ss * 16, [[4096, 128], [128, 8], [1, 16]])
                                    s.tt(c1v, bb(0, d_, g0, 8), pr, ALU.mult)
                                    s.tt(c2v, bb(1, d_, g0, 8), pi_, ALU.mult)
                                    s.tt(o_re, c1v, c2v, ALU.subtract)
                                    s.tt(c1v, bb(1, d_, g0, 8), pr, ALU.mult)
                                    s.tt(c2v, bb(0, d_, g0, 8), pi_, ALU.mult)
                                    s.tt(o_im, c1v, c2v, ALU.add)
                            for gh in range(2):
                                p0 = gh * 64
                                for gq in range(8):
                                    bk = nbank()
                                    for d_ in range(2):
                                        for ri in range(2):
                                            sl = d_ * 2 + ri
                                            s.transpose(bk[:, sl * 64:(sl + 1) * 64],
                                                        WT.view(p0 * 4096 + (d_ * 2 + ri) * 1024 + gq * 128, [[4096, 64], [1, 128]]),
                                                        cst[p0:p0 + 64, 0, p0:p0 + 64])
                                    s.copy(WST.view(((gq * 2 + gh) * 4) * 64, [[4096, 128], [1, 256]]), bk[:, 0:256], eng="act")
                            s.flush()

                        KSB = cx.sb(bs_, "KSB", [16, 2, 16, 128], BF16)

                        def setup_part2():
                            c1v = ysb.view(0, [[512, 128], [16, 8], [1, 16]])
                            c2v = ysb.view(128, [[512, 128], [16, 8], [1, 16]])
                            for d_ in range(2):
                                for idx in range(9):
                                    p_ = idx if d_ == 0 else 8 - idx
                                    pr = PW.view((0 * 9 + p_) * 64 + d_ * 32 + g0, [[1152, 128], [1, 8], [0, 16]])
                                    pi_ = PW.view((1 * 9 + p_) * 64 + d_ * 32 + g0, [[1152, 128], [1, 8], [0, 16]])
                                    o_re = CAB.view((0 * 2 + d_) * 1152 + idx * 16, [[4608, 128], [144, 8], [1, 16]])
                                    o_im = CAB.view((1 * 2 + d_) * 1152 + idx * 16, [[4608, 128], [144, 8], [1, 16]])
                                    s.tt(c1v, prm(cprm, 0, g0, 8), pr, ALU.mult, eng="pool")
                                    s.tt(c2v, prm(cprm, 1, g0, 8), pi_, ALU.mult, eng="pool")
                                    s.tt(o_re, c1v, c2v, ALU.subtract, eng="pool")
                                    s.tt(c1v, prm(cprm, 0, g0, 8), pi_, ALU.mult, eng="pool")
                                    s.tt(c2v, prm(cprm, 1, g0, 8), pr, ALU.mult, eng="pool")
                                    s.tt(c1v, c1v, c2v, ALU.add, eng="pool")
                                    s.ts(o_im, c1v, -1.0, None, ALU.mult, eng="pool")
                            s.copy(CABb.full(), CAB.full(), eng="pool")
                            for gh in range(2):
                                p0 = gh * 64
                                for d_ in range(2):
                                    for gqq in range(2):
                                        bk = nbank()
                                        for q4 in range(4):
                                            gq = gqq * 4 + q4
                                            i0 = 0 if d_ == 0 else 1
                                            s.mm(bk[0:16, q4 * 128:(q4 + 1) * 128],
                                                 [(BB.view(p0 * 2048 + (0 * 2 + d_) * 512 + (g0 + gq) * 16, [[2048, 64], [1, 16]]),
                                                   CAB.view(p0 * 4608 + (0 * 2 + d_) * 1152 + gq * 144 + i0 * 16, [[4608, 64], [1, 128]])),
                                                  (BB.view(p0 * 2048 + (1 * 2 + d_) * 512 + (g0 + gq) * 16, [[2048, 64], [1, 16]]),
                                                   CAB.view(p0 * 4608 + (1 * 2 + d_) * 1152 + gq * 144 + i0 * 16, [[4608, 64], [1, 128]]))])
                                        s.copy(KSB.view(d_ * 2048 + (2 * gqq * 4 + gh) * 128, [[4096, 16], [256, 4], [1, 128]]),
                                               bk.view(0, [[512, 16], [128, 4], [1, 128]]), eng="act")
                            gbase = 16 * b
                            s.dma(KFP.view(gbase * 3840 + 7 * 16, [[240, 16], [3840, 16], [1, 128]]), KSB[:, 0, :, :])
                            s.dma(KBR.view(gbase * 3840, [[240, 16], [3840, 16], [1, 128]]), KSB[:, 1, :, :])
                            s.dma(KFP.view(gbase * 3840, [[240, 16], [3840, 16], [1, 112]]), zt_.view(0, [[112, 16], [0, 16], [1, 112]]))
                            s.dma(KBR.view(gbase * 3840 + 128, [[240, 16], [3840, 16], [1, 112]]), zt_.view(0, [[112, 16], [0, 16], [1, 112]]))
                            for ss in range(8):
                                s.dma(TF[ss * 16:(ss + 1) * 16, :, :], KFP.view(gbase * 3840 + (7 - ss) * 16, [[240, 16], [3840, 16], [1, 128]]))
                                s.dma(TB[ss * 16:(ss + 1) * 16, :, :], KBR.view(gbase * 3840 + (7 - ss) * 16, [[240, 16], [3840, 16], [1, 128]]))

                        u8b = cx.sb(bs_, "u8b", [128, 8, 256])
                        u8g = cx.sb(bs_, "u8g", [128, 16, 128])
                        U8T = cx.sb(bs_, "U8T", [128, 16, 288], BF16)
                        SSb = [cx.sb(bs_, "SSb%d" % i, [128, 16, 256], BF16) for i in range(2)]
                        NCOL = 326
                        PS = 16 * NCOL
                        SSD = [cx.sb(bs_, "SS%d" % i, [128, 8, 2, NCOL]) for i in range(2)]
                        CAR = [cx.sb(bs_, "CAR%d" % i, [128, 15, 8, 2]) for i in range(2)]
                        A36 = [cx.sb(bs_, "A36_%d" % i, [128, 8, 2]) for i in range(2)]
                        B36 = [cx.sb(bs_, "B36_%d" % i, [128, 8, 2]) for i in range(2)]
                        y8b = cx.sb(bs_, "y8b", [128, 8, 256])
                        ysb = cx.sb(bs_, "ysb", [128, 512])
                        TT1 = [cx.sb(bs_, "TT1_%d" % i, [128, 17, 8, 2]) for i in range(2)]
                        TT2 = [cx.sb(bs_, "TT2_%d" % i, [128, 17, 8, 2]) for i in range(2)]
                        for (j0, nj) in ((0, 32), (32, 128), (160, 128)):
                            s.dma(u8b[0:nj, :, :], U.view(8 * j0 * 1024 + 256 * b, [[8192, nj], [1024, 8], [1, 256]]))
                            s.copy(u8g.view(0, [[2048, nj], [128, 16], [16, 8], [1, 16]]),
                                   u8b.view(0, [[2048, nj], [16, 16], [256, 8], [1, 16]]), eng="act")
                            for gq4 in range(4):
                                bk = nbank()
                                for q4 in range(4):
                                    gi = gq4 * 4 + q4
                                    s.transpose(bk[:, q4 * 128:q4 * 128 + nj],
                                                u8g.view(128 * gi, [[2048, nj], [1, 128]]), cst[0:nj, 0, 0:nj])
                                s.copy(U8T.view(gq4 * 4 * 288 + j0, [[16 * 288, 128], [288, 4], [1, nj]]),
                                       bk.view(0, [[512, 128], [128, 4], [1, nj]]), eng="act")
                        if L1S in ('d', 'd1'):
                            s.flush()
                            continue
                        s.memset(SSD[0].view(0, [[PS, 128], [NCOL, 16], [1, 1]]), 0.0)
                        s.memset(SSD[0].view(289, [[PS, 128], [NCOL, 16], [1, 37]]), 0.0)
                        s.memset(SSD[1].view(288, [[PS, 128], [NCOL, 16], [1, 38]]), 0.0)
                        s.memset(SSD[0].view(289, [[PS, 128], [2 * NCOL, 8], [1, 1]]), 1.0)
                        s.memset(SSD[1].view(288 + 18 - 1, [[PS, 128], [2 * NCOL, 8], [1, 1]]), 1.0)
                        for gq in range(8):
                            for gh in range(2):
                                gi = 2 * gq + gh
                                p0 = gh * 64
                                for d_ in range(2):
                                    for ri in range(2):
                                        bk = nbank()
                                        s.mm(bk[p0:p0 + 64, 0:288],
                                             [(WST.view((((gq * 2 + gh) * 2 + d_) * 2 + ri) * 64, [[4096, 128], [1, 64]]),
                                               U8T[:, gi, :])])
                                        so = p0 * PS + (gq * 2 + ri) * NCOL
                                        if d_ == 0:
                                            s.copy(SSD[0].view(so + 1, [[PS, 64], [1, 288]]), bk[p0:p0 + 64, 0:288], eng="act")
                                        else:
                                            s.copy(SSD[1].view(so + 256, [[PS, 64], [1, 32]]), bk[p0:p0 + 64, 0:32], eng="act")
                                            s.copy(SSD[1].view(so, [[PS, 64], [1, 256]]), bk[p0:p0 + 64, 32:288], eng="act")
                        if L1S in ('e', 'e1'):
                            s.flush()
                            continue
                        DS = 8 * 2 * 289
                        setup_part2()
                        RI, GQ = NCOL, 2 * NCOL
                        SEG, NSEG = 18, 16
                        REC_ENG2 = os.environ.get('REC2', 'dve')

                        def cplx_step(items):
                            engs = ("dve", REC_ENG2)
                            for n_, (pv, psw, cv, ca, cb_, t1_, t2_) in enumerate(items):
                                s.tt(t1_, pv, ca, ALU.mult, eng=engs[n_ % 2])
                                s.tt(t2_, psw, cb_, ALU.mult, eng=engs[n_ % 2])
                            for n_, (pv, psw, cv, ca, cb_, t1_, t2_) in enumerate(items):
                                s.tt(t1_, t1_, t2_, ALU.add, eng=engs[n_ % 2])
                            for n_, (pv, psw, cv, ca, cb_, t1_, t2_) in enumerate(items):
                                if cv is not None:
                                    s.tt(cv, cv, t1_, ALU.add, eng=engs[n_ % 2])

                        def segv(SS, col, nseg):
                            return (SS.view(col, [[PS, 128], [SEG, nseg], [GQ, 8], [RI, 2]]),
                                    SS.view(col + RI, [[PS, 128], [SEG, nseg], [GQ, 8], [-RI, 2]]))

                        def coef(buf, d_, nseg):
                            return buf.view(d_ * 64 + g0 * 2, [[128, 128], [0, nseg], [2, 8], [1, 2]])

                        TTP = 17 * 16

                        for k in range(1, SEG):
                            items = []
                            for d_ in range(2):
                                pc = k if d_ == 0 else SEG - k
                                cc = k + 1 if d_ == 0 else SEG - 1 - k
                                pv, psw = segv(SSD[d_], pc, NSEG + 1)
                                cv, _ = segv(SSD[d_], cc, NSEG + 1)
                                items.append((pv, psw, cv, coef(LA, d_, NSEG + 1), coef(LB, d_, NSEG + 1), TT1[d_].full(), TT2[d_].full()))
                            cplx_step(items)
                        items = []
                        for d_ in range(2):
                            clast = 288 + SEG if d_ == 0 else 288
                            pv, psw = segv(SSD[d_], clast, 1)
                            items.append((pv, psw, None, coef(LA, d_, 1), coef(LB, d_, 1),
                                          TT1[d_].view(0, [[TTP, 128], [16, 1], [2, 8], [1, 2]]),
                                          TT2[d_].view(0, [[TTP, 128], [16, 1], [2, 8], [1, 2]])))
                        cplx_step(items)
                        for d_ in range(2):
                            l36re = TT1[d_].view(0, [[TTP, 128], [2, 8], [0, 2]])
                            s.copy(A36[d_].full(), l36re)
                            s.ts(B36[d_][:, :, 0:1], TT1[d_].view(1, [[TTP, 128], [2, 8], [1, 1]]), -1.0, None, ALU.mult)
                            s.copy(B36[d_][:, :, 1:2], TT1[d_].view(1, [[TTP, 128], [2, 8], [1, 1]]))
                        for step in range(1, NSEG):
                            items = []
                            for d_ in range(2):
                                if d_ == 0:
                                    m = step
                                    cc, pc = SEG * m + SEG, SEG * m
                                else:
                                    m = NSEG - 1 - step
                                    cc, pc = SEG * m, SEG * m + SEG
                                pv, psw = segv(SSD[d_], pc, 1)
                                cv, _ = segv(SSD[d_], cc, 1)
                                items.append((pv, psw, cv,
                                              A36[d_].view(0, [[16, 128], [0, 1], [2, 8], [1, 2]]),
                                              B36[d_].view(0, [[16, 128], [0, 1], [2, 8], [1, 2]]),
                                              TT1[d_].view(0, [[TTP, 128], [16, 1], [2, 8], [1, 2]]),
                                              TT2[d_].view(0, [[TTP, 128], [16, 1], [2, 8], [1, 2]])))
                            cplx_step(items)
                        items = []
                        for d_ in range(2):
                            pv, psw = segv(SSD[d_], SEG, NSEG - 1)
                            items.append((pv, psw, None, coef(LA, d_, NSEG - 1), coef(LB, d_, NSEG - 1),
                                          CAR[d_].full(), TT2[d_].view(0, [[TTP, 128], [16, NSEG - 1], [2, 8], [1, 2]])))
                        cplx_step(items)
                        NI = SEG - 1
                        for d_ in range(2):
                            SS = SSD[d_]
                            sb0 = SEG + 1 if d_ == 0 else 1

                            def sview(ri):
                                return SS.view(sb0 + ri * RI, [[PS, 128], [SEG, NSEG - 1], [GQ, 8], [1, NI]])

                            def tview(ri):
                                return SS.view(289 + ri * RI, [[PS, 128], [0, NSEG - 1], [GQ, 8], [1, NI]])

                            def cview(ri):
                                return CAR[d_].view(ri, [[(NSEG - 1) * 16, 128], [16, NSEG - 1], [2, 8], [0, NI]])

                            wshape = [[2048, 128], [8 * NI, NSEG - 1], [NI, 8], [1, NI]]
                            w1 = (u8g if d_ == 0 else u8b).view(0, wshape)
                            w2 = y8b.view(0, wshape)
                            s.tt(w1, tview(0), cview(0), ALU.mult)
                            s.tt(w2, tview(1), cview(1), ALU.mult)
                            s.tt(w1, w1, w2, ALU.subtract)
                            s.tt(sview(0), sview(0), w1, ALU.add)
                            s.tt(w1, tview(0), cview(1), ALU.mult)
                            s.tt(w2, tview(1), cview(0), ALU.mult)
                            s.tt(w1, w1, w2, ALU.add)
                            s.tt(sview(1), sview(1), w1, ALU.add)
                        s.copy(SSb[0].full(), SSD[0].view(32, [[PS, 128], [NCOL, 16], [1, 256]]), eng="act")
                        s.copy(SSb[1].full(), SSD[1].view(1, [[PS, 128], [NCOL, 16], [1, 256]]), eng="pool")
                        if L1S in ('f', 'f1'):
                            s.flush()
                            continue
                        for tt_ in range(2):
                            j0 = 32 + 128 * tt_
                            m0 = 128 * tt_
                            for gh in range(2):
                                p0 = gh * 64
                                for gqq in range(2):
                                    bx = nbank()
                                    by = nbank()
                                    for q4 in range(4):
                                        gq = gqq * 4 + q4
                                        gi = 2 * gq + gh
                                        s.mm(bx[:, q4 * 128:(q4 + 1) * 128],
                                             [(U8T[:, gi, j0:j0 + 128], TF[:, gi, :]), (U8T[:, gi, j0:j0 + 128], TB[:, gi, :])])
                                        pairs = []
                                        for d_ in range(2):
                                            c0 = m0
                                            i0 = 1 if d_ == 0 else 0
                                            for ri in range(2):
                                                so = p0 * 4096 + (gq * 2 + ri) * 256 + c0
                                                pairs.append((SSb[d_].view(so, [[4096, 64], [1, 128]]),
                                                              CABb.view(p0 * 4608 + (ri * 2 + d_) * 1152 + gq * 144 + i0 * 16, [[4608, 64], [1, 128]])))
                                        s.mm(by[:, q4 * 128:(q4 + 1) * 128], pairs)
                                    s.copy(ysb.full(), by.full(), eng="act")
                                    s.tt(y8b.view(32 * gqq * 4 + 16 * gh, [[2048, 128], [32, 4], [256, 8], [1, 16]]),
                                         bx.view(0, [[512, 128], [128, 4], [16, 8], [1, 16]]),
                                         ysb.view(0, [[512, 128], [128, 4], [16, 8], [1, 16]]), ALU.add)
                            s.dma(YTOK.view((CTX + 8 * m0) * 1024 + 256 * b, [[8192, 128], [1024, 8], [1, 256]]), y8b.full())
                        s.flush()

            if L1S in ('g', 'g1'):
                return nc
            with ExitStack() as es:
                gw = [cx.sb(es, "gw%d" % k, [128, D], BF16) for k in range(8)]
                ow = [cx.sb(es, "ow%d" % k, [128, D], BF16) for k in range(8)]
                dskb = cx.sb(es, "dskb", [128, D])
                glbb = cx.sb(es, "glbb", [128, D])
                fnwb = cx.sb(es, "fnwb", [128, D])
                kg = cx.sb(es, "kg", [128, 1])
                s.memset(kg.full(), 2.0 * math.sqrt(2.0 / math.pi))
                for k in range(8):
                    s.dma(gw[k].full(), o_glu_w[k * 128:(k + 1) * 128, :], q="pool")
                    s.dma(ow[k].full(), o_w_out[k * 128:(k + 1) * 128, :], q="pool")
                s.dma(dskb.full(), o_d_skip.view(0, [[0, 128], [1, D]]))
                s.dma(glbb.full(), o_glu_b.view(0, [[0, 128], [1, D]]))
                s.dma(fnwb.full(), final_norm_w.view(0, [[0, 128], [1, D]]))
                NB3 = 4
                ya = [cx.sb(es, "ya%d" % i, [128, D]) for i in range(NB3)]
                ua = [cx.sb(es, "ua%d" % i, [128, D]) for i in range(NB3)]
                sga = [cx.sb(es, "sga%d" % i, [128, D]) for i in range(NB3)]
                xa = [cx.sb(es, "xa%d" % i, [128, D]) for i in range(NB3)]
                w1s = [cx.sb(es, "w1_%d" % i, [128, D]) for i in range(NB3)]
                w2s = [cx.sb(es, "w2_%d" % i, [128, D]) for i in range(NB3)]
                w3s = [cx.sb(es, "w3_%d" % i, [128, D]) for i in range(NB3)]
                tTs = [cx.sb(es, "tT_%d" % i, [128, 8, 128], BF16) for i in range(2 * NB3)]
                sts = [cx.sb(es, "st10_%d" % i, [128, 4]) for i in range(NB3)]

                def transp8(src, tT):
                    for half in range(2):
                        bk = nbank()
                        for kk in range(4):
                            k = half * 4 + kk
                            s.transpose(bk[:, kk * 128:(kk + 1) * 128], src[:, k * 128:(k + 1) * 128], ident)
                        s.copy(tT[:, half * 4:(half + 1) * 4, :], bk.view(0, [[512, 128], [128, 4], [1, 128]]), eng="act")

                TAILN = int(os.environ.get('TAILN', NT))

                def bufs(i):
                    b_ = i % NB3
                    return ya[b_], ua[b_], sga[b_], xa[b_], w1s[b_], w2s[b_], w3s[b_], tTs[2 * b_], tTs[2 * b_ + 1], sts[b_]

                def stage0(i):
                    y_, u_, g_, x_, w1, w2, w3, tTa, tTb, st = bufs(i)
                    s.dma(y_.full(), YTOK[i * 128:(i + 1) * 128, :])
                    s.dma(u_.full(), U[i * 128:(i + 1) * 128, :])
                    s.dma(g_.full(), SG1[i * 128:(i + 1) * 128, :])
                    s.dma(x_.full(), X1[i * 128:(i + 1) * 128, :])
                    s.tt(w1.full(), u_.full(), dskb.full(), ALU.mult)
                    s.tt(y_.full(), y_.full(), w1.full(), ALU.add)
                    s.tt(w1.full(), y_.full(), y_.full(), ALU.mult)
                    s.ts(w1.full(), w1.full(), 0.044715, 1.0, ALU.mult, ALU.add)
                    s.tt(w1.full(), w1.full(), y_.full(), ALU.mult)
                    s.act(w1.full(), w1.full(), AF.Sigmoid, scale=kg[:, 0:1])
                    s.tt(w2.full(), y_.full(), w1.full(), ALU.mult)
                    transp8(w2, tTa)

                def stage1(i):
                    y_, u_, g_, x_, w1, w2, w3, tTa, tTb, st = bufs(i)
                    for half in range(2):
                        bk = nbank()
                        s.mm(bk.full(), [(tTa[:, k, :], gw[k][:, half * 512:(half + 1) * 512]) for k in range(8)])
                        s.tt(w1[:, half * 512:(half + 1) * 512], bk.full(), glbb[:, half * 512:(half + 1) * 512], ALU.add)
                    s.act(w1.full(), w1.full(), AF.Sigmoid)
                    s.tt(w2.full(), w2.full(), w1.full(), ALU.mult)
                    s.tt(w2.full(), w2.full(), g_.full(), ALU.mult)
                    transp8(w2, tTb)

                def stage2(i):
                    y_, u_, g_, x_, w1, w2, w3, tTa, tTb, st = bufs(i)
                    for half in range(2):
                        bk = nbank()
                        s.mm(bk.full(), [(tTb[:, k, :], ow[k][:, half * 512:(half + 1) * 512]) for k in range(8)])
                        s.tt(w1[:, half * 512:(half + 1) * 512], bk.full(), gate_bc[1][0][:, half * 512:(half + 1) * 512], ALU.mult)
                    s.tt(w3.full(), w1.full(), x_.full(), ALU.add)
                    s.act(w1.full(), w3.full(), AF.Square, accum=st[:, 0:1])
                    s.ts(st[:, 1:2], st[:, 0:1], 1.0 / D, EPS, ALU.mult, ALU.add)
                    s.act(st[:, 2:3], st[:, 1:2], AF.Sqrt)
                    s.recip(st[:, 3:4], st[:, 2:3])
                    s.act(w3.full(), w3.full(), AF.Copy, scale=st[:, 3:4])
                    s.tt(w2.full(), w3.full(), fnwb.full(), ALU.mult)
                    s.dma(out_t[(i - 2) * 128:(i - 1) * 128, :], w2.full())

                pipeline(list(range(2, TAILN)), [stage0, (lambda i: None), stage1, stage2])
                s.flush()

    return nc


def _consts():
    c = np.zeros((128, 6, 512), np.float32)
    j = np.arange(128)[:, None]
    l = np.arange(128)[None, :]
    c[:, 0, :128] = np.eye(128, dtype=np.float32)
    c[0, 0, 128:256] = 1.0
    c[1, 0, 256:384] = 1.0
    c[:, 1, :128] = (j <= l)
    c[:, 2, :128] = (j >= l)
    c[:, 3, :] = 1.0
    nf = np.where(l < j, -30000.0, 0.0).astype(np.float32)
    nb = np.where(l > j, -30000.0, 0.0).astype(np.float32)
    c[:, 4, :] = np.tile(nf, (1, 4))
    c[:, 5, :] = np.tile(nb, (1, 4))
    return c


def _rope_tables():
    rows = L // 64
    row = np.repeat(np.arange(rows, dtype=np.float32), 64)
    col = np.tile(np.arange(64, dtype=np.float32), rows)
    n_freq = 16
    inv = (np.float32(10000.0) ** (-np.arange(n_freq, dtype=np.float32) / n_freq)).astype(np.float32)
    ang = np.concatenate([row[:, None] * inv, col[:, None] * inv], axis=-1).astype(np.float32)
    cos = np.cos(ang).astype(np.float32)
    sin = np.sin(ang).astype(np.float32)
    cosT = np.zeros((128, L), np.float32)
    sinT = np.zeros((128, L), np.float32)
    for h2 in range(2):
        for half in range(2):
            p0 = h2 * 64 + half * 32
            cosT[p0:p0 + 32] = cos.T
            sinT[p0:p0 + 32] = (-sin.T if half == 0 else sin.T)
    return np.stack([cosT, sinT], axis=1)


def _vecT(v, nchunk):
    return np.ascontiguousarray(np.asarray(v, np.float32).reshape(nchunk, 128).T)


def prep_inputs(b, inp):
    f = lambda a: np.ascontiguousarray(np.asarray(a, np.float32))
    m = {}
    m["xin"] = f(np.concatenate([inp["ctx"][b], inp["x"][b]], axis=0))
    cv = np.stack([inp["c"][b], inp["c_ctx"]], axis=0)
    m["cvecT"] = f(cv.reshape(2, 8, 128).transpose(2, 0, 1))
    m["consts"] = _consts()
    m["rope"] = _rope_tables()
    m["e_ada_w"] = f(inp["e_ada_w"][0])
    m["e_ada_b"] = f(inp["e_ada_b"][0]).reshape(1, -1)
    m["e_norm_wT"] = _vecT(inp["e_norm_w"][0], 8)
    w = f(inp["e_w_in"][0])
    q = w[:, OFF_Q:OFF_Q + 1024].reshape(D, 16, 2, 32)
    qs = q[:, :, ::-1, :].reshape(D, 1024)
    k = w[:, OFF_KV:OFF_KV + 256].reshape(D, 4, 64)
    kr = np.concatenate([k, k], axis=2).reshape(D, 512)
    ks = k.reshape(D, 4, 2, 32)[:, :, ::-1, :].reshape(D, 4, 64)
    ksr = np.concatenate([ks, ks], axis=2).reshape(D, 512)
    m["e_w_in"] = f(np.concatenate([w, qs, kr, ksr], axis=1))
    cw = f(inp["e_conv_w"][0])
    m["e_conv_wT"] = f(cw.reshape(5, 12, 128).transpose(2, 1, 0))
    m["e_conv_bT"] = _vecT(inp["e_conv_b"][0], 12)
    m["e_dt_bias"] = f(inp["e_dt_bias"][0]).reshape(1, 32)
    m["e_a_log"] = f(inp["e_a_log"][0]).reshape(1, 32)
    m["e_d_skip"] = f(inp["e_d_skip"][0]).reshape(1, 16)
    m["e_ssd_norm_wT"] = _vecT(inp["e_ssd_norm_w"][0], 8)
    sk = f(inp["e_sink"][0]).reshape(8, 2)
    m["e_sink"] = f(np.repeat(sk.T[:, None, :], 64, axis=1).reshape(128, 8))
    m["e_w_out"] = f(inp["e_w_out"][0])
    m["o_ada_w"] = f(inp["o_ada_w"][0])
    m["o_ada_b"] = f(inp["o_ada_b"][0]).reshape(1, -1)
    m["o_norm_wT"] = _vecT(inp["o_norm_w"][0], 8)
    m["o_w_in"] = f(inp["o_w_in"][0])

    def gl(a):
        a = np.asarray(a, np.float32)
        rest = a.shape[2:]
        a = a.reshape((32, 2, 64) + rest)
        a = np.moveaxis(a, 0, 2)
        return a.reshape((128, 32) + rest)

    lam = np.zeros((128, 2, 3, 32), np.float32)
    for d_ in range(2):
        lam[:, d_, 0] = gl(inp["o_lam_re"][0][d_])
        lam[:, d_, 1] = gl(inp["o_lam_im"][0][d_])
        lam[:, d_, 2] = gl(np.repeat(np.asarray(inp["o_log_step"][0][d_])[:, None], 64, axis=1))
    m["s5_lam"] = f(lam)
    m["s5_b"] = f(np.stack([gl(inp["o_b_re"][0]), gl(inp["o_b_im"][0])], axis=1))
    cr = np.asarray(inp["o_c_re"][0]).transpose(0, 2, 1)
    ci = np.asarray(inp["o_c_im"][0]).transpose(0, 2, 1)
    m["s5_c"] = f(np.stack([gl(cr), gl(ci)], axis=1))
    m["o_d_skip"] = f(inp["o_d_skip"][0]).reshape(1, -1)
    m["o_glu_w"] = f(inp["o_glu_w"][0])
    m["o_glu_b"] = f(inp["o_glu_b"][0]).reshape(1, -1)
    m["o_w_out"] = f(inp["o_w_out"][0])
    m["final_norm_w"] = f(inp["final_norm_w"]).reshape(1, -1)
    return m


def kernel(**inputs):
    nc = build_program()
    in_maps = [prep_inputs(b, inputs) for b in range(8)]
    res = run_bass_kernel_spmd(nc, in_maps, core_ids=list(range(8)))
    return np.stack([r["out"] for r in res.results], axis=0)
```

```python
import math
import os
from contextlib import ExitStack

import numpy as np
import concourse.bass as bass
import concourse.mybir as mybir
from concourse.bass_utils import run_bass_kernel_spmd

F32 = mybir.dt.float32
BF16 = mybir.dt.bfloat16
AF = mybir.ActivationFunctionType
ALU = mybir.AluOpType

D = 1024
T = 2304
NT = 18
CTX = 256
L = 2048
EPS = 1e-6
TG = [(0, 256), (256, 512), (768, 512), (1280, 512), (1792, 512)]

SES_ALL = os.environ.get('SES', '0') == '1'
SAME_ENGINE_SYNC = {'act': SES_ALL, 'dve': SES_ALL, 'pool': True, 'pe': False, 'sp': True}
SEM_EPOCH = 30000


class V:
    __slots__ = ("buf", "ap")

    def __init__(self, buf, ap):
        self.buf = buf
        self.ap = ap


class Buf:
    def __init__(self, name, h):
        self.name = name
        self.h = h
        self.last_w = None
        self.readers = []
        self.is_psum = False

    def __getitem__(self, idx):
        return V(self, self.h[idx])

    def full(self):
        return V(self, self.h.ap())

    def view(self, offset, pattern):
        return V(self, bass.AP(self.h, offset, [list(p) for p in pattern]))


class Sched:
    ENG = ("pe", "act", "dve", "pool", "sp")

    def __init__(self, nc):
        self.nc = nc
        self.prog = {e: [] for e in self.ENG}
        self.sem = {}
        self.cnt = {}
        self.semid = 0
        self.known = {e: {} for e in self.ENG}
        for e in ("pe", "act", "dve", "pool"):
            self._new_engine_sem(e)
        self.nds = 8
        self.dsem = {}
        self.duse = {}
        self.dcnt = {}
        for q in ("sp", "pool"):
            self.dsem[q] = []
            self.duse[q] = []
            for i in range(self.nds):
                key = "d_%s_%d" % (q, i)
                self.dsem[q].append((nc.alloc_semaphore(key), key))
                self.duse[q].append(0)
            self.dcnt[q] = 0
        self.n_ops = 0

    def _new_engine_sem(self, e):
        self.semid += 1
        key = "s_%s_%d" % (e, self.semid)
        self.sem[e] = (self.nc.alloc_semaphore(key), key)
        self.cnt[e] = 0

    def _deps(self, reads, writes):
        deps = {}

        def add(tok):
            if tok is None:
                return
            h, key, val = tok
            if key not in deps or deps[key][1] < val:
                deps[key] = (h, val)

        for r in reads:
            add(r.buf.last_w)
            if r.buf.is_psum:
                for t in r.buf.readers:
                    add(t)
        for w in writes:
            add(w.buf.last_w)
            for t in w.buf.readers:
                add(t)
        return deps

    def _emit_waits(self, eng, deps, own_key=None):
        kn = self.known[eng]
        for key, (h, val) in deps.items():
            if key == own_key and not SAME_ENGINE_SYNC[eng]:
                continue
            if kn.get(key, 0) >= val:
                continue
            kn[key] = val
            self.prog[eng].append(("wait", h, val))

    def _update(self, tok, reads, writes):
        for w in writes:
            w.buf.last_w = tok
            w.buf.readers = []
        for r in reads:
            if r.buf.last_w is not tok:
                r.buf.readers.append(tok)

    def op(self, eng, fn, reads=(), writes=()):
        reads = [r for r in reads if r is not None]
        writes = list(writes)
        if self.cnt[eng] >= SEM_EPOCH:
            self._new_engine_sem(eng)
        h, key = self.sem[eng]
        own = None if eng == "pe" else key
        deps = self._deps(reads, writes)
        if eng == "pe":
            deps.pop(key, None)
        self._emit_waits(eng, deps, own_key=own)
        self.cnt[eng] += 1
        self.prog[eng].append(("op", fn, h, 1))
        tok = (h, key, self.cnt[eng])
        self._update(tok, reads, writes)
        self.n_ops += 1
        return tok

    def dma(self, out, in_, q="sp", **kw):
        deps = self._deps([in_], [out])
        self._emit_waits(q, deps)
        k = self.dcnt[q] % self.nds
        self.dcnt[q] += 1
        h, key = self.dsem[q][k]
        prev = 16 * self.duse[q][k]
        if prev > 0 and self.known[q].get(key, 0) < prev:
            self.known[q][key] = prev
            self.prog[q].append(("wait", h, prev))
        self.duse[q][k] += 1
        val = 16 * self.duse[q][k]
        o_ap, i_ap = out.ap, in_.ap
        self.prog[q].append(("op", lambda e: e.dma_start(out=o_ap, in_=i_ap, **kw), h, 16))
        tok = (h, key, val)
        self._update(tok, [in_], [out])
        self.n_ops += 1
        return tok

    def finish_dmas(self):
        for q in ("sp", "pool"):
            for k in range(self.nds):
                h, key = self.dsem[q][k]
                val = 16 * self.duse[q][k]
                if val > 0 and self.known[q].get(key, 0) < val:
                    self.known[q][key] = val
                    self.prog[q].append(("wait", h, val))

    def flush(self, name=None):
        self.finish_dmas()
        nc = self.nc
        prog = self.prog
        self.prog = {e: [] for e in self.ENG}

        def run(items, e):
            for it in items:
                if it[0] == "wait":
                    e.wait_ge(it[1], it[2])
                else:
                    inst = it[1](e)
                    inst.then_inc(it[2], it[3])

        with nc.Block() as block:
            if prog["sp"]:
                @block.sync
                def _(e):
                    run(prog["sp"], e)
            if prog["act"]:
                @block.scalar
                def _(e):
                    run(prog["act"], e)
            if prog["dve"]:
                @block.vector
                def _(e):
                    run(prog["dve"], e)
            if prog["pool"]:
                @block.gpsimd
                def _(e):
                    run(prog["pool"], e)
            if prog["pe"]:
                @block.tensor
                def _(e):
                    run(prog["pe"], e)

    def mm(self, out, pairs):
        n = len(pairs)

        def fn(e):
            inst = None
            for i, (l, r) in enumerate(pairs):
                inst = e.matmul(out.ap, l.ap, r.ap, start=(i == 0), stop=(i == n - 1))
            return inst

        self.op("pe", fn, reads=[p[0] for p in pairs] + [p[1] for p in pairs], writes=[out])

    def mm1(self, out, l, r, start, stop):
        self.op("pe", lambda e: e.matmul(out.ap, l.ap, r.ap, start=start, stop=stop), reads=[l, r], writes=[out])

    def transpose(self, out, in_, ident):
        self.op("pe", lambda e: e.transpose(out.ap, in_.ap, ident.ap), reads=[in_, ident], writes=[out])

    def act(self, out, in_, func, bias=None, scale=None, accum=None):
        kw = {}
        reads = [in_]
        writes = [out]
        if bias is not None:
            if isinstance(bias, V):
                kw["bias"] = bias.ap
                reads.append(bias)
            else:
                kw["bias"] = bias
        if scale is not None:
            if isinstance(scale, V):
                kw["scale"] = scale.ap
                reads.append(scale)
            else:
                kw["scale"] = scale
        if accum is not None:
            kw["accum_out"] = accum.ap
            writes.append(accum)
        self.op("act", lambda e: e.activation(out.ap, in_.ap, func, **kw), reads=reads, writes=writes)

    def ts(self, out, in0, s1, s2, op0, op1=None, eng="dve"):
        reads = [in0]
        a1 = s1
        a2 = s2
        if isinstance(s1, V):
            reads.append(s1)
            a1 = s1.ap
        if isinstance(s2, V):
            reads.append(s2)
            a2 = s2.ap
        if op1 is None:
            self.op(eng, lambda e: e.tensor_scalar(out.ap, in0.ap, a1, a2, op0), reads=reads, writes=[out])
        else:
            self.op(eng, lambda e: e.tensor_scalar(out.ap, in0.ap, a1, a2, op0, op1), reads=reads, writes=[out])

    def tt(self, out, in0, in1, op, eng="dve"):
        self.op(eng, lambda e: e.tensor_tensor(out.ap, in0.ap, in1.ap, op), reads=[in0, in1], writes=[out])

    def stt(self, out, in0, scalar, in1, op0, op1):
        reads = [in0, in1]
        sc = scalar
        if isinstance(scalar, V):
            reads.append(scalar)
            sc = scalar.ap
        self.op("dve", lambda e: e.scalar_tensor_tensor(out.ap, in0.ap, sc, in1.ap, op0, op1),
                reads=reads, writes=[out])

    def copy(self, out, in_, eng="dve"):
        if eng == "act":
            self.op("act", lambda e: e.copy(out.ap, in_.ap), reads=[in_], writes=[out])
        else:
            self.op(eng, lambda e: e.tensor_copy(out.ap, in_.ap), reads=[in_], writes=[out])

    def recip(self, out, in_):
        self.op("dve", lambda e: e.reciprocal(out.ap, in_.ap), reads=[in_], writes=[out])

    def memset(self, out, val, eng="dve"):
        self.op(eng, lambda e: e.memset(out.ap, val), reads=[], writes=[out])


class Ctx:
    def __init__(self, nc, sched):
        self.nc = nc
        self.s = sched
        self.uid = 0

    def sb(self, es, name, shape, dtype=F32):
        self.uid += 1
        h = es.enter_context(self.nc.sbuf_tensor("%s_%d" % (name, self.uid), list(shape), dtype))
        return Buf(name, h)

    def ps(self, es, name, shape=(128, 512), dtype=F32):
        self.uid += 1
        h = es.enter_context(self.nc.psum_tensor("%s_%d" % (name, self.uid), list(shape), dtype))
        b = Buf(name, h)
        b.is_psum = True
        return b

    def dram(self, name, shape, dtype=F32, kind="Internal"):
        h = self.nc.dram_tensor(name, list(shape), dtype, kind=kind)
        return Buf(name, h)


def pipeline(items, stages):
    n, k = len(items), len(stages)
    for t in range(n + k - 1):
        for j in range(k - 1, -1, -1):
            i = t - j
            if 0 <= i < n:
                stages[j](items[i])


def bc_mid(v_buf, base_off, pstep, nparts, n_outer, outer_step, n_inner):
    return v_buf.view(base_off, [[pstep, nparts], [outer_step, n_outer], [0, n_inner]])


E_NCOL = 5152
OFF_Z = 0
OFF_XBC = 1024
OFF_DT = 2560
OFF_Q = 2592
OFF_KV = 3616
OFF_G = 4128
OFF_QS = 5152
OFF_KR = 6176
OFF_KSR = 6688
E_NCOL_EXT = 7200


ORDER = ["p1", "p2a", "p2b", "p2c", "p2d", "p2e", "p2f", "p2g", "p2h", "p3", "p4", "p5", "all"]


def build_program(debug=(), stop="all"):
    def go(tag):
        return ORDER.index(tag) <= ORDER.index(stop)
    nc = bass.Bass("TRN2", target_bir_lowering=False)
    s = Sched(nc)
    cx = Ctx(nc, s)
    dbg = set(debug)

    def din(name, shape):
        return Buf(name, nc.dram_tensor(name, list(shape), F32, kind="ExternalInput"))

    def dout(name, shape):
        return Buf(name, nc.dram_tensor(name, list(shape), F32, kind="ExternalOutput"))

    def scratch(name, shape, dtype=F32):
        if name in dbg:
            return dout(name, shape)
        return Buf(name, nc.dram_tensor(name, list(shape), dtype))

    xin = din("xin", [T, D])
    cvecT = din("cvecT", [128, 2, 8])
    consts = din("consts", [128, 6, 512])
    rope = din("rope", [128, 2, L])
    e_ada_w = din("e_ada_w", [D, 3 * D])
    e_ada_b = din("e_ada_b", [1, 3 * D])
    e_norm_wT = din("e_norm_wT", [128, 8])
    e_w_in = din("e_w_in", [D, E_NCOL_EXT])
    e_conv_wT = din("e_conv_wT", [128, 12, 5])
    e_conv_bT = din("e_conv_bT", [128, 12])
    e_dt_bias = din("e_dt_bias", [1, 32])
    e_a_log = din("e_a_log", [1, 32])
    e_d_skip = din("e_d_skip", [1, 16])
    e_ssd_norm_wT = din("e_ssd_norm_wT", [128, 8])
    e_sink = din("e_sink", [128, 8])
    e_w_out = din("e_w_out", [2 * D, D])
    o_ada_w = din("o_ada_w", [D, 3 * D])
    o_ada_b = din("o_ada_b", [1, 3 * D])
    o_norm_wT = din("o_norm_wT", [128, 8])
    o_w_in = din("o_w_in", [D, 2 * D])
    s5_lam = din("s5_lam", [128, 2, 3, 32])
    s5_b = din("s5_b", [128, 2, 32, 16])
    s5_c = din("s5_c", [128, 2, 32, 16])
    o_d_skip = din("o_d_skip", [1, D])
    o_glu_w = din("o_glu_w", [D, D])
    o_glu_b = din("o_glu_b", [1, D])
    o_w_out = din("o_w_out", [D, D])
    final_norm_w = din("final_norm_w", [1, D])
    out_t = dout("out", [L, D])

    XS = scratch("XS", [T, 1024])
    BTOK = scratch("BTOK", [T, 256], BF16)
    BT = scratch("BT", [2, 128, T], BF16)
    CT = scratch("CT", [2, 128, T], BF16)
    SZ = scratch("SZ", [T, 1024])
    QR = scratch("QR", [8, 128, L], BF16)
    QC = scratch("QC", [8, 128, CTX], BF16)
    KR = scratch("KR", [4, 128, L], BF16)
    KC = scratch("KC", [4, 128, CTX], BF16)
    VT = scratch("VT", [T, 256], BF16)
    SG = scratch("SG", [8, 128, T])
    YF = scratch("YF", [T, 1024])
    YT = scratch("YT", [16, 128, T], BF16)
    X1 = scratch("X1", [T, 1024])
    U = scratch("U", [T, 1024])
    SG1 = scratch("SG1", [T, 1024])
    YTOK = scratch("YTOK", [T, 1024])
    KFP = scratch("KFP", [64, 16, 15, 16], BF16)
    KBR = scratch("KBR", [64, 16, 15, 16], BF16)
    HT = scratch("HT", [8, 128, T]) if "HT" in dbg else None
    DTD = scratch("DTD", [T, 32]) if "DTD" in dbg else None
    MODD = scratch("MODD", [4, 128, 24]) if "MODD" in dbg else None

    with ExitStack() as top:
        banks = [cx.ps(top, "bank%d" % i) for i in range(8)]
        cst = cx.sb(top, "cst", [128, 6, 512])
        s.dma(cst.full(), consts.full())
        ident = cst[:, 0, 0:128]
        tri = cst[:, 1, 0:128]
        utri = cst[:, 2, 0:128]
        ones = cst[:, 3, 0:128]
        onesb_t = cx.sb(top, "onesb", [128, 128], BF16)
        s.memset(onesb_t.full(), 1.0)
        onesb = onesb_t.full()
        modT = [[cx.sb(top, "modT%d%d" % (l, w), [128, 24]) for w in range(2)] for l in range(2)]
        gate_bc = [[cx.sb(top, "gate%d%d" % (l, w), [128, 1024]) for w in range(2)] for l in range(2)]
        scs = cx.sb(top, "scs", [128, 2, 8])

        def adaln_phase(layer, ada_w, ada_b):
            with ExitStack() as es:
                aw = [cx.sb(es, "aw%d" % k, [128, 3 * D]) for k in range(8)]
                ab2 = cx.sb(es, "ab2", [2, 3 * D])
                modrow2 = cx.sb(es, "modrow2", [2, 3 * D])
                if layer == 0:
                    cv = cx.sb(es, "cv", [128, 2, 8])
                    s.dma(cv.full(), cvecT.full())
                    s.act(scs.full(), cv.full(), AF.Silu)
                s.dma(ab2[0:1, :], ada_b.full())
                s.dma(ab2[1:2, :], ada_b.full())
                for k in range(8):
                    s.dma(aw[k].full(), ada_w[k * 128:(k + 1) * 128, :])
                for k in range(8):
                    for fg in range(6):
                        s.mm1(banks[fg][0:2, :], scs.view(k, [[16, 128], [8, 2]]), aw[k][:, fg * 512:(fg + 1) * 512],
                              start=(k == 0), stop=(k == 7))
                for fg in range(6):
                    s.tt(modrow2[0:2, fg * 512:(fg + 1) * 512], banks[fg][0:2, :], ab2[0:2, fg * 512:(fg + 1) * 512], ALU.add)
                bk = banks[6]
                for fc in range(24):
                    s.mm(bk[:, 2 * fc:2 * fc + 2], [(modrow2[0:2, fc * 128:(fc + 1) * 128], cst[0:2, 0, 0:2])])
                for w in range(2):
                    s.copy(modT[layer][w].full(), bk.view(w, [[512, 128], [2, 24]]))
                bi = 0
                for w in range(2):
                    selw = cst[0:2, 0, 128 + 128 * w:256 + 128 * w]
                    for hh in range(2):
                        bk2 = banks[(7 + bi) % 8]
                        bi += 1
                        s.mm(bk2.full(), [(selw, modrow2[0:2, 2048 + hh * 512:2048 + (hh + 1) * 512])])
                        s.copy(gate_bc[layer][w][:, hh * 512:(hh + 1) * 512], bk2.full(), eng="act")
                    if MODD is not None:
                        s.dma(MODD[layer * 2 + w], modT[layer][w].full())
                s.flush()

        adaln_phase(0, e_ada_w, e_ada_b)

        with ExitStack() as l0:
            DT = cx.sb(l0, "DT", [128, NT, 32])
            DTA = cx.sb(l0, "DTA", [128, NT, 32])
            nw = cx.sb(l0, "nw", [128, 8])
            sc1 = [cx.sb(l0, "sc1_%d" % w, [128, 8]) for w in range(2)]
            s.dma(nw.full(), e_norm_wT.full())
            for w in range(2):
                s.stt(sc1[w].full(), modT[0][w][:, 8:16], 1.0, nw.full(), ALU.add, ALU.mult)

            wo = [cx.sb(l0, "wo%d" % k, [128, D], BF16) for k in range(16)]
            hts = ExitStack()
            hT = [cx.sb(hts, "hT%d" % k, [128, T], BF16) for k in range(8)]
            with ExitStack() as es:
                xt = [cx.sb(es, "xt%d" % i, [128, D]) for i in range(3)]
                xn = [cx.sb(es, "xn%d" % i, [128, D]) for i in range(3)]
                junk = cx.sb(es, "junk", [128, D])
                st = [cx.sb(es, "st%d" % i, [128, 4]) for i in range(3)]
                def n0(i):
                    x_, n_, st_ = xt[i % 3], xn[i % 3], st[i % 3]
                    s.dma(x_.full(), xin[i * 128:(i + 1) * 128, :])
                    s.act(junk.full(), x_.full(), AF.Square, accum=st_[:, 0:1])
                    s.ts(st_[:, 1:2], st_[:, 0:1], 1.0 / D, EPS, ALU.mult, ALU.add)
                    s.act(st_[:, 2:3], st_[:, 1:2], AF.Sqrt)
                    s.recip(st_[:, 3:4], st_[:, 2:3])
                    s.ts(n_.full(), x_.full(), st_[:, 3:4], None, ALU.mult)

                def n1(i):
                    w = 1 if i < 2 else 0
                    n_ = xn[i % 3]
                    for half in range(2):
                        bk = banks[(2 * i + half) % 8]
                        for kk in range(4):
                            k = half * 4 + kk
                            s.transpose(bk[:, kk * 128:(kk + 1) * 128], n_[:, k * 128:(k + 1) * 128], ident)
                        for kk in range(4):
                            k = half * 4 + kk
                            s.act(hT[k][:, i * 128:(i + 1) * 128], bk[:, kk * 128:(kk + 1) * 128], AF.Identity,
                                  bias=modT[0][w][:, k:k + 1], scale=sc1[w][:, k:k + 1])

                pipeline(list(range(NT)), [n0, n1])
                if HT is not None:
                    for k in range(8):
                        s.dma(HT[k], hT[k].full())
                s.flush()

            with ExitStack() as es:
                WB = 256
                NWB, PF = 6, 4
                wbuf = [cx.sb(es, "wbuf%d" % i, [128, 8, WB], BF16) for i in range(NWB)]
                wplan = [(OFF_XBC + 256 * k, 256) for k in range(6)]
                for qc in range(8):
                    wplan += [(OFF_Q + qc * 128, 128), (OFF_QS + qc * 128, 128)]
                for j in range(4):
                    wplan += [(OFF_KR + j * 128, 128), (OFF_KSR + j * 128, 128)]
                wplan += [(OFF_G + 256 * k, 256) for k in range(4)]
                wplan += [(OFF_Z + 256 * k, 256) for k in range(4)]
                wplan += [(OFF_KV + 256, 256), (OFF_DT, 32)]
                wstate = {"i": 0, "issued": 0}

                def _issue(n):
                    col0, ncol = wplan[n]
                    wb = wbuf[n % NWB]
                    s.dma(wb[:, :, 0:ncol], e_w_in.view(col0, [[E_NCOL_EXT, 128], [128 * E_NCOL_EXT, 8], [1, ncol]]), q="pool")

                def load_w(col0, ncol=WB):
                    i = wstate["i"]
                    wstate["i"] += 1
                    assert wplan[i] == (col0, ncol), (i, wplan[i], col0, ncol)
                    while wstate["issued"] < min(i + PF + 1, len(wplan)):
                        _issue(wstate["issued"])
                        wstate["issued"] += 1
                    return wbuf[i % NWB]

                bstate = {"i": 0}

                def nbank():
                    bk = banks[bstate["i"] % 8]
                    bstate["i"] += 1
                    return bk

                def fm_mm(wb, cc, t0, n):
                    bk = nbank()
                    s.mm(bk[:, 0:n], [(wb[:, k, cc * 128:(cc + 1) * 128], hT[k][:, t0:t0 + n]) for k in range(8)])
                    return bk

                xraws = [cx.sb(es, "xraw%d" % i, [128, T]) for i in range(2)]
                accs = [cx.sb(es, "acc%d" % i, [128, T]) for i in range(2)]
                acc = accs[0]
                accbs = [cx.sb(es, "accb%d" % i, [128, T], BF16) for i in range(2)]
                accb = accbs[0]
                rc_i = {"i": 0}
                tmp1s = [cx.sb(es, "tmp1_%d" % i, [128, 512]) for i in range(2)]
                tmp2s = [cx.sb(es, "tmp2_%d" % i, [128, 512]) for i in range(2)]
                stg = [cx.sb(es, "stg%d" % i, [128, 4, 128]) for i in range(2)]
                stgb = [cx.sb(es, "stgb%d" % i, [128, 4, 128], BF16) for i in range(2)]
                rp = cx.sb(es, "rp", [128, 2, L])
                cw = cx.sb(es, "cw", [128, 12, 5])
                cb = cx.sb(es, "cb", [128, 12])
                dtb = cx.sb(es, "dtb", [128, 32])
                abc = cx.sb(es, "abc", [128, 32])
                s.dma(rp.full(), rope.full())
                s.dma(cw.full(), e_conv_wT.full())
                s.dma(cb.full(), e_conv_bT.full())
                s.dma(dtb.full(), e_dt_bias.view(0, [[0, 128], [1, 32]]))
                s.dma(abc.full(), e_a_log.view(0, [[0, 128], [1, 32]]))
                s.act(abc.full(), abc.full(), AF.Exp)
                s.ts(abc.full(), abc.full(), -1.0, None, ALU.mult)
                stg_i = {"i": 0}

                def transposes_to(dst, col0, src, lowp=False):
                    for i0 in range(0, NT, 4):
                        nb = min(4, NT - i0)
                        bk = nbank()
                        for ii in range(nb):
                            i = i0 + ii
                            s.transpose(bk[:, ii * 128:(ii + 1) * 128], src[:, i * 128:(i + 1) * 128], ident)
                        sg_ = (stgb if lowp else stg)[stg_i["i"] % 2]
                        stg_i["i"] += 1
                        s.copy(sg_[:, 0:nb, :], bk.view(0, [[512, 128], [128, nb], [1, 128]]), eng="act")
                        ncols = dst.h.shape[1]
                        s.dma(dst.view(i0 * 128 * ncols + col0, [[ncols, 128], [128 * ncols, nb], [1, 128]]),
                              sg_[:, 0:nb, :])

                wb_of = {}

                def xa(fc):
                    if fc % 2 == 0:
                        wb_of[fc // 2] = load_w(OFF_XBC + fc * 128)
                    wb = wb_of[fc // 2]
                    xraw = xraws[fc % 2]
                    for (t0, n) in TG:
                        bk = fm_mm(wb, fc % 2, t0, n)
                        s.copy(xraw[:, t0:t0 + n], bk[:, 0:n], eng="act")

                def xb(fc):
                    xraw, acc = xraws[fc % 2], accs[fc % 2]
                    s.ts(acc.full(), xraw.full(), cw[:, fc, 2:3], cb[:, fc:fc + 1], ALU.mult, ALU.add)
                    for kk in (0, 1, 3, 4):
                        d_ = kk - 2
                        for (s0, sl) in ((0, CTX), (CTX, L)):
                            lo = max(s0, s0 - d_)
                            hi = min(s0 + sl, s0 + sl - d_)
                            s.stt(acc[:, lo:hi], xraw[:, lo + d_:hi + d_], cw[:, fc, kk:kk + 1], acc[:, lo:hi],
                                  ALU.mult, ALU.add)
                    s.act(acc.full(), acc.full(), AF.Silu)
                    if fc < 8:
                        transposes_to(XS, fc * 128, acc)
                    elif fc < 10:
                        s.copy(accb.full(), acc.full(), eng="act")
                        s.dma(BT[fc - 8], accb.full())
                        transposes_to(BTOK, (fc - 8) * 128, acc, lowp=True)
                    else:
                        s.copy(accb.full(), acc.full(), eng="act")
                        s.dma(CT[fc - 10], accb.full())

                pipeline(list(range(12 if go('p2a') else 0)), [xa, xb])

                def rope_chunk(col_plain, col_swap, dst_rot, dst_ctx):
                    accb = accbs[rc_i["i"] % 2]
                    rc_i["i"] += 1
                    wa = load_w(col_plain, 128)
                    wsw = load_w(col_swap, 128)
                    for gi, (t0, n) in enumerate(TG):
                        bka = fm_mm(wa, 0, t0, n)
                        if gi == 0:
                            s.copy(accb[:, 0:CTX], bka[:, 0:CTX], eng="act")
                            continue
                        bkb = fm_mm(wsw, 0, t0, n)
                        l0 = t0 - CTX
                        tmp1, tmp2 = tmp1s[gi % 2], tmp2s[gi % 2]
                        s.tt(tmp1.full(), bka.full(), rp[:, 0, l0:l0 + 512], ALU.mult)
                        s.tt(tmp2.full(), bkb.full(), rp[:, 1, l0:l0 + 512], ALU.mult)
                        s.tt(accb[:, t0:t0 + n], tmp1.full(), tmp2.full(), ALU.add)
                    s.dma(dst_ctx, accb[:, 0:CTX])
                    s.dma(dst_rot, accb[:, CTX:T])

                for qc in range(8 if go('p2b') else 0):
                    rope_chunk(OFF_Q + qc * 128, OFF_QS + qc * 128, QR[qc], QC[qc])
                for j in range(4 if go('p2c') else 0):
                    rope_chunk(OFF_KR + j * 128, OFF_KSR + j * 128, KR[j], KC[j])

                for gc in range(8 if go('p2d') else 0):
                    acc = accs[gc % 2]
                    if gc % 2 == 0:
                        wb = load_w(OFF_G + gc * 128)
                    for (t0, n) in TG:
                        bk = fm_mm(wb, gc % 2, t0, n)
                        s.act(acc[:, t0:t0 + n], bk[:, 0:n], AF.Silu)
                    s.dma(SG[gc], acc.full())

                NT_E = NT if go('p2e') else 0
                wz = [load_w(OFF_Z + i * 256) for i in range(4)]
                for i in range(NT_E):
                    z_a = accs[i % 2]
                    for half in range(2):
                        bk = nbank()
                        for q4 in range(2):
                            wbz = wz[half * 2 + q4]
                            s.mm(bk[:, q4 * 256:(q4 + 1) * 256],
                                 [(hT[k][:, i * 128:(i + 1) * 128], wbz[:, k, :]) for k in range(8)])
                        s.act(z_a[:, half * 512:(half + 1) * 512], bk.full(), AF.Silu)
                    s.dma(SZ[i * 128:(i + 1) * 128, :], z_a[:, 0:1024])
                wv = load_w(OFF_KV + 256)
                wdt = load_w(OFF_DT, 32)
                vt = [cx.sb(es, "vt%d" % i, [128, 256], BF16) for i in range(2)]
                for i in range(NT if go('p2f') else 0):
                    bk = nbank()
                    s.mm(bk[:, 0:256], [(hT[k][:, i * 128:(i + 1) * 128], wv[:, k, :]) for k in range(8)])
                    s.copy(vt[i % 2].full(), bk[:, 0:256], eng="act")
                    s.dma(VT[i * 128:(i + 1) * 128, :], vt[i % 2].full())
                for i in range(NT if go('p2g') else 0):
                    bk = nbank()
                    s.mm(bk[:, 0:32], [(hT[k][:, i * 128:(i + 1) * 128], wdt[:, k, 0:32]) for k in range(8)])
                    s.tt(DT[:, i, :], bk[:, 0:32], dtb.full(), ALU.add)
                    if go('p2h'):
                        s.act(DT[:, i, :], DT[:, i, :], AF.Exp)
                        s.ts(DT[:, i, :], DT[:, i, :], 1.0, None, ALU.add)
                        s.act(DT[:, i, :], DT[:, i, :], AF.Ln)
                    s.tt(DTA[:, i, :], DT[:, i, :], abc.full(), ALU.mult)
                    if DTD is not None:
                        s.dma(DTD[i * 128:(i + 1) * 128, :], DT[:, i, :])
                s.flush()
            hts.close()
            for k in range(16):
                s.dma(wo[k].full(), e_w_out[k * 128:(k + 1) * 128, :], q="pool")

            with ExitStack() as es:
                nb_ = {"i": 0}

                def nbank():
                    bk = banks[nb_["i"] % 8]
                    nb_["i"] += 1
                    return bk

                N3 = 3
                N4 = 4
                xs_t = [cx.sb(es, "xs_t%d" % i, [128, 1024]) for i in range(N4)]
                b_t = [cx.sb(es, "b_t%d" % i, [128, 256], BF16) for i in range(N3)]
                bt_t = [cx.sb(es, "bt_t%d" % i, [128, 2, 128], BF16) for i in range(N3)]
                ct_t = [cx.sb(es, "ct_t%d" % i, [128, 2, 128], BF16) for i in range(N3)]
                yf_t = [cx.sb(es, "yf_t%d" % i, [128, 1024]) for i in range(N4)]
                sz_t = [cx.sb(es, "sz_t%d" % i, [128, 1024]) for i in range(2)]
                MTs = [cx.sb(es, "MT%d" % i, [128, 2048], BF16) for i in range(N3)]
                xcs = [cx.sb(es, "xc%d" % i, [128, 1024], BF16) for i in range(N3)]
                xcds = [cx.sb(es, "xcd%d" % i, [128, 1024], BF16) for i in range(N3)]
                tmpos = [cx.sb(es, "tmpo%d" % i, [128, 1024]) for i in range(N3)]
                ytots = [cx.sb(es, "ytot%d" % i, [128, 1024]) for i in range(2)]
                sms = [cx.sb(es, "sm%d" % i, [128, 4, 16]) for i in range(N3)]
                st3s = [cx.sb(es, "st3_%d" % i, [128, 4]) for i in range(N3)]
                ystgs = [cx.sb(es, "ystg%d" % i, [128, 8, 128], BF16) for i in range(2)]
                dtatris = [cx.sb(es, "dtatri%d" % i, [128, 2048]) for i in range(2)]
                decTs = [cx.sb(es, "decT%d" % i, [128, 2048]) for i in range(2)]
                cb_sbs = [cx.sb(es, "cb_sb%d" % i, [128, 256]) for i in range(2)]
                junk = cx.sb(es, "junk3", [128, 1024])
                Hs = [cx.sb(es, "Hs%d" % g, [128, 512]) for g in range(2)]
                Hb = [cx.sb(es, "Hb%d" % g, [128, 512], BF16) for g in range(2)]
                dsk = cx.sb(es, "dsk", [128, 16])
                snw = cx.sb(es, "snw", [128, 8])
                cm1 = cx.sb(es, "cm1", [128, 1])
                s.memset(cm1.full(), -1.0)
                s.dma(dsk.full(), e_d_skip.view(0, [[0, 128], [1, 16]]))
                s.dma(snw.full(), e_ssd_norm_wT.full())
                cmh = cx.sb(es, "cmh", [128, 1])
                s.memset(cmh.full(), -0.5)

                def bc3(buf, off, pstep, n1, s1, n2, s2):
                    return buf.view(off, [[pstep, 128], [s1, n1], [s2, n2]])

                n_ch = NT if go("p3") else 0
                for d_ in range(2):
                    order = list(range(NT)) if d_ == 0 else [1, 0] + list(range(NT - 1, 1, -1))
                    order = order[:n_ch]
                    TRIoff = 512 if d_ == 0 else 1024
                    TRIv = tri if d_ == 0 else utri
                    negm = cst[:, 4 + d_, :]
                    for g in range(2):
                        s.memset(Hs[g].full(), 0.0)
                        s.memset(Hb[g].full(), 0.0)

                    def stA(item, d_=d_, TRIoff=TRIoff, TRIv=TRIv, negm=negm):
                        ci, i = item
                        p3, p2, p4 = ci % N3, ci % 2, ci % N4
                        xs_, b_, bt_, ct_ = xs_t[p4], b_t[p3], bt_t[p3], ct_t[p3]
                        MT, xc, xcd, sm = MTs[p3], xcs[p3], xcds[p3], sms[p3]
                        dtatri, decT, cb_sb = dtatris[p2], decTs[p2], cb_sbs[p2]
                        s.dma(xs_.full(), XS[i * 128:(i + 1) * 128, :])
                        s.dma(b_.full(), BTOK[i * 128:(i + 1) * 128, :])
                        s.dma(bt_.full(), BT.view(i * 128, [[T, 128], [128 * T, 2], [1, 128]]))
                        s.dma(ct_.full(), CT.view(i * 128, [[T, 128], [128 * T, 2], [1, 128]]))
                        if d_ == 1:
                            s.dma(yf_t[p4].full(), YF[i * 128:(i + 1) * 128, :])
                        dta_i = DTA[:, i, d_ * 16:(d_ + 1) * 16]
                        doff = i * 32 + d_ * 16
                        s.tt(bc3(dtatri, 0, 2048, 16, 128, 128, 1), bc3(DTA, doff, NT * 32, 16, 1, 128, 0),
                             bc3(cst, TRIoff, 3072, 16, 0, 128, 1), ALU.mult, eng="pool")
                        bs = nbank()
                        s.mm(bs[:, 0:16], [(TRIv, dta_i)])
                        s.mm(bs[:, 16:32], [(ones, dta_i)])
                        na, ea, de, cd = sm[:, 0, :], sm[:, 1, :], sm[:, 2, :], sm[:, 3, :]
                        s.ts(na, bs[:, 0:16], -1.0, None, ALU.mult)
                        s.act(ea, bs[:, 0:16], AF.Exp)
                        s.tt(de, bs[:, 16:32], na, ALU.add)
                        s.act(de, de, AF.Exp)
                        s.act(cd, bs[:, 16:32], AF.Exp)
                        for hq in range(4):
                            bq = nbank()
                            s.mm(bq.full(), [(ones, dtatri[:, hq * 512:(hq + 1) * 512]), (ident, negm)])
                            for hh in range(4):
                                h = hq * 4 + hh
                                s.act(decT[:, h * 128:(h + 1) * 128], bq[:, hh * 128:(hh + 1) * 128], AF.Exp,
                                      bias=sm[:, 0, h:h + 1])
                        bc = nbank()
                        for g in range(2):
                            s.mm(bc[:, g * 128:(g + 1) * 128], [(bt_[:, g, :], ct_[:, g, :])])
                        s.copy(cb_sb.full(), bc[:, 0:256], eng="act")
                        for g in range(2):
                            s.tt(bc3(MT, g * 1024, 2048, 8, 128, 128, 1), bc3(decT, g * 1024, 2048, 8, 128, 128, 1),
                                 bc3(cb_sb, g * 128, 256, 8, 0, 128, 1), ALU.mult)
                        s.tt(bc3(xc, 0, 1024, 16, 64, 64, 1), bc3(xs_, 0, 1024, 16, 64, 64, 1),
                             bc3(DT, doff, NT * 32, 16, 1, 64, 0), ALU.mult, eng="pool")
                        s.tt(bc3(xcd, 0, 1024, 16, 64, 64, 1), bc3(xc, 0, 1024, 16, 64, 64, 1),
                             bc3(sm, 32, 64, 16, 1, 64, 0), ALU.mult, eng="pool")
                        if d_ == 1:
                            s.tt(bc3(tmpos[p3], 0, 1024, 16, 64, 64, 1), bc3(xs_, 0, 1024, 16, 64, 64, 1),
                                 bc3(dsk, 0, 16, 16, 1, 64, 0), ALU.mult, eng="pool")
                            s.tt(yf_t[p4].full(), yf_t[p4].full(), tmpos[p3].full(), ALU.add, eng="pool")

                    def stB(item, d_=d_):
                        ci, i = item
                        p3 = ci % N3
                        b_, ct_ = b_t[p3], ct_t[p3]
                        MT, xc, xcd, sm, tmpo, ytot = MTs[p3], xcs[p3], xcds[p3], sms[p3], tmpos[p3], ytots[ci % 2]
                        ydst = yf_t[ci % N4] if d_ == 0 else ytot
                        if d_ == 1:
                            s.dma(sz_t[ci % 2].full(), SZ[i * 128:(i + 1) * 128, :])
                        for g in range(2):
                            by = nbank()
                            for hh in range(8):
                                h = g * 8 + hh
                                s.mm(by[:, hh * 64:(hh + 1) * 64], [(MT[:, h * 128:(h + 1) * 128], xc[:, h * 64:(h + 1) * 64])])
                            bo = nbank()
                            s.mm(bo.full(), [(ct_[:, g, :], Hb[g].full())])
                            s.tt(bc3(tmpo, g * 512, 1024, 8, 64, 64, 1), bc3(bo, 0, 512, 8, 64, 64, 1),
                                 bc3(sm, 16 + g * 8, 64, 8, 1, 64, 0), ALU.mult)
                            s.tt(ydst[:, g * 512:(g + 1) * 512], by.full(), tmpo[:, g * 512:(g + 1) * 512], ALU.add)
                        for g in range(2):
                            bst = nbank()
                            s.mm(bst.full(), [(b_[:, g * 128:(g + 1) * 128], xcd[:, g * 512:(g + 1) * 512])])
                            s.tt(bc3(Hs[g], 0, 512, 8, 64, 64, 1), bc3(Hs[g], 0, 512, 8, 64, 64, 1),
                                 bc3(sm, 48 + g * 8, 64, 8, 1, 64, 0), ALU.mult)
                            s.tt(Hs[g].full(), Hs[g].full(), bst.full(), ALU.add)
                            s.copy(Hb[g].full(), Hs[g].full(), eng="act")
                        if d_ == 0:
                            s.dma(YF[i * 128:(i + 1) * 128, :], yf_t[ci % N4].full())

                    def stC(item, d_=d_):
                        if d_ == 0:
                            return
                        ci, i = item
                        p3, p2 = ci % N3, ci % 2
                        ytot, sz_, st3, ystg = ytots[ci % 2], sz_t[ci % 2], st3s[p3], ystgs[p2]
                        s.tt(ytot.full(), ytot.full(), yf_t[ci % N4].full(), ALU.add)
                        s.tt(ytot.full(), ytot.full(), sz_.full(), ALU.mult)
                        s.act(junk.full(), ytot.full(), AF.Square, accum=st3[:, 0:1])
                        s.ts(st3[:, 1:2], st3[:, 0:1], 1.0 / 1024, EPS, ALU.mult, ALU.add)
                        s.tt(st3[:, 3:4], st3[:, 1:2], cmh.full(), ALU.pow, eng="pool")
                        s.act(ytot.full(), ytot.full(), AF.Copy, scale=st3[:, 3:4])
                        for half in range(2):
                            bk = nbank()
                            for kk in range(4):
                                k = half * 4 + kk
                                s.transpose(bk[:, kk * 128:(kk + 1) * 128], ytot[:, k * 128:(k + 1) * 128], ident)
                            for kk in range(4):
                                k = half * 4 + kk
                                s.act(ystg[:, k, :], bk[:, kk * 128:(kk + 1) * 128], AF.Copy, scale=snw[:, k:k + 1])
                        s.dma(YT.view(i * 128, [[T, 128], [128 * T, 8], [1, 128]]), ystg.full())

                    pipeline(list(enumerate(order)), [stA, (lambda it: None), stB, stC])
                s.flush()

            with ExitStack() as es:
                nb_ = {"i": 0}

                def nbank():
                    bk = banks[nb_["i"] % 8]
                    nb_["i"] += 1
                    return bk

                J2 = 2
                qr_ts = [cx.sb(es, "qr_t%d" % i, [128, 2, L], BF16) for i in range(J2)]
                qc_ts = [cx.sb(es, "qc_t%d" % i, [128, 2, CTX], BF16) for i in range(J2)]
                kr_ts = [cx.sb(es, "kr_t%d" % i, [128, L], BF16) for i in range(J2)]
                kc_ts = [cx.sb(es, "kc_t%d" % i, [128, CTX], BF16) for i in range(J2)]
                v_ts = [cx.sb(es, "v_t%d" % i, [128, NT, 64], BF16) for i in range(J2)]
                v2s = [cx.sb(es, "v2_%d" % i, [128, NT, 128], BF16) for i in range(J2)]
                sg_ts = [cx.sb(es, "sg_t%d" % i, [128, 2, T]) for i in range(J2)]
                asts = [cx.sb(es, "ast%d" % i, [128, 2, T], BF16) for i in range(J2)]
                NP = 5
                pt = [[cx.sb(es, "pt%d_%d" % (a, b), [128, 512], BF16) for b in range(5)] for a in range(NP)]
                rds = [cx.sb(es, "rd%d" % i, [128, 256]) for i in range(2)]
                aos = [cx.sb(es, "ao%d" % i, [128, 256]) for i in range(2)]
                es_pp = cx.sb(es, "es_pp", [128, 8])
                c8 = cx.sb(es, "c8", [128, 1])
                s.memset(c8.full(), 0.125)
                s.dma(es_pp.full(), e_sink.full())
                s.act(es_pp.full(), es_pp.full(), AF.Exp)
                ATT_DBG = [int(v) for v in os.environ.get("ATT_DBG", "4,18,4").split(",")]
                items = []
                for j in range(ATT_DBG[0] if go("p4") else 0):
                    qbs = ([("c", 0), ("c", 1)] + [("l", b) for b in range(16)])[:ATT_DBG[1]]
                    for qi, (kind, bi) in enumerate(qbs):
                        items.append((len(items), j, kind, bi, qi == 0, qi == len(qbs) - 1))

                def keys_of(kind, bi):
                    keys = [("c", 0, None), ("c", 1, None)]
                    if kind == "l":
                        if bi > 0:
                            keys.append(("l", bi - 1, "prev"))
                        keys.append(("l", bi, None))
                        if bi < 15:
                            keys.append(("l", bi + 1, "next"))
                    return keys

                def atA(item):
                    n, j, kind, bi, first, last = item
                    js = j % J2
                    qr_t, qc_t, kr_t, kc_t, v_t, v2, sg_t = qr_ts[js], qc_ts[js], kr_ts[js], kc_ts[js], v_ts[js], v2s[js], sg_ts[js]
                    if first:
                        s.dma(qr_t.full(), QR.view(2 * j * 128 * L, [[L, 128], [128 * L, 2], [1, L]]))
                        s.dma(qc_t.full(), QC.view(2 * j * 128 * CTX, [[CTX, 128], [128 * CTX, 2], [1, CTX]]))
                        s.dma(kr_t.full(), KR[j])
                        s.dma(kc_t.full(), KC[j])
                        s.dma(v_t.full(), VT.view(j * 64, [[256, 128], [128 * 256, NT], [1, 64]]))
                        s.dma(sg_t.full(), SG.view(2 * j * 128 * T, [[T, 128], [128 * T, 2], [1, T]]))
                        s.copy(v2[:, :, 0:64], v_t.full(), eng="pool")
                        s.copy(v2[:, :, 64:128], v_t.full(), eng="pool")
                    qsrc, q0 = (qc_t, bi * 128) if kind == "c" else (qr_t, bi * 128)
                    pts = pt[n % NP]
                    qw = qsrc.h.shape[2]
                    for ki, (kk, kb, msk) in enumerate(keys_of(kind, bi)):
                        ksrc = kc_t if kk == "c" else kr_t
                        for par in range(2):
                            p0 = par * 64
                            bs = nbank()
                            s.mm(bs[:, 0:256],
                                 [(ksrc[p0:p0 + 64, kb * 128:(kb + 1) * 128],
                                   qsrc.view(p0 * 2 * qw + q0, [[2 * qw, 64], [qw, 2], [1, 128]]))])
                            s.act(pts[ki][:, par * 256:(par + 1) * 256], bs[:, 0:256], AF.Exp, scale=c8[:, 0:1])
                        if msk is not None:
                            moff = 1024 if msk == "prev" else 512
                            s.tt(pts[ki].view(0, [[512, 128], [128, 4], [1, 128]]),
                                 pts[ki].view(0, [[512, 128], [128, 4], [1, 128]]),
                                 cst.view(moff, [[3072, 128], [0, 4], [1, 128]]), ALU.mult, eng="pool")

                def atB(item):
                    n, j, kind, bi, first, last = item
                    js = j % J2
                    v2, sg_t, ast = v2s[js], sg_ts[js], asts[js]
                    tok0 = bi * 128 if kind == "c" else CTX + bi * 128
                    keys = keys_of(kind, bi)
                    pts = pt[n % NP]
                    rd, ao = rds[n % 2], aos[n % 2]
                    vt_idx = [(kb if kk == "c" else 2 + kb) for (kk, kb, _) in keys]
                    bn = nbank()
                    s.mm(bn.full(), [(v2[:, vt_idx[ki], :], pts[ki].full()) for ki in range(len(keys))])
                    bd = nbank()
                    s.mm(bd.full(), [(onesb, pts[ki].full()) for ki in range(len(keys))])
                    for par in range(2):
                        p0 = par * 64
                        for c in range(2):
                            s.ts(rd[p0:p0 + 64, c * 128:(c + 1) * 128],
                                 bd[p0:p0 + 64, par * 256 + c * 128:par * 256 + (c + 1) * 128],
                                 es_pp[p0:p0 + 64, 2 * j + c:2 * j + c + 1], None, ALU.add)
                    s.recip(rd.full(), rd.full())
                    for par in range(2):
                        p0 = par * 64
                        s.tt(ao[p0:p0 + 64, :], bn[p0:p0 + 64, par * 256:(par + 1) * 256], rd[p0:p0 + 64, :], ALU.mult)
                    s.tt(ast.view(tok0, [[2 * T, 128], [T, 2], [1, 128]]),
                         ao.view(0, [[256, 128], [128, 2], [1, 128]]),
                         sg_t.view(tok0, [[2 * T, 128], [T, 2], [1, 128]]), ALU.mult)
                    if last:
                        s.dma(YT.view((8 + 2 * j) * 128 * T, [[T, 128], [128 * T, 2], [1, T]]), ast.full())

                pipeline(items, [atA, (lambda it: None), (lambda it: None), atB])
                s.flush()

            with ExitStack() as es:
                nb_ = {"i": 0}

                def nbank():
                    bk = banks[nb_["i"] % 8]
                    nb_["i"] += 1
                    return bk

                ytg = [cx.sb(es, "ytg%d" % i, [128, 16, 512], BF16) for i in range(2)]
                xt = [cx.sb(es, "xt5_%d" % i, [128, D]) for i in range(3)]
                x1t = [cx.sb(es, "x1t%d" % i, [128, D]) for i in range(3)]
                tmp5s = [cx.sb(es, "tmp5_%d" % i, [128, 512]) for i in range(2)]
                for i in range(NT if go("p5") else 0):
                    w = 1 if i < 2 else 0
                    gi_, ii = i // 4, i % 4
                    yg = ytg[gi_ % 2]
                    if ii == 0:
                        nt4 = min(4, NT - i)
                        s.dma(yg[:, :, 0:nt4 * 128], YT.view(i * 128, [[T, 128], [128 * T, 16], [1, nt4 * 128]]))
                    x_, o_ = xt[i % 3], x1t[i % 3]
                    s.dma(x_.full(), xin[i * 128:(i + 1) * 128, :])
                    for half in range(2):
                        tmp5 = tmp5s[half]
                        bk = nbank()
                        s.mm(bk.full(), [(yg[:, fc, ii * 128:(ii + 1) * 128], wo[fc][:, half * 512:(half + 1) * 512]) for fc in range(16)])
                        s.tt(tmp5.full(), bk.full(), gate_bc[0][w][:, half * 512:(half + 1) * 512], ALU.mult)
                        s.tt(o_[:, half * 512:(half + 1) * 512], tmp5.full(), x_[:, half * 512:(half + 1) * 512], ALU.add)
                    s.dma(X1[i * 128:(i + 1) * 128, :], o_.full())
                s.flush()

        if go("all"):
            adaln_phase(1, o_ada_w, o_ada_b)
        with ExitStack() as l1:
            if not go("all"):
                return nc
            nb_ = {"i": 0}

            def nbank():
                bk = banks[nb_["i"] % 8]
                nb_["i"] += 1
                return bk

            with ExitStack() as es:
                nw = cx.sb(es, "nw1", [128, 8])
                sc1 = [cx.sb(es, "sc1b_%d" % w, [128, 8]) for w in range(2)]
                s.dma(nw.full(), o_norm_wT.full())
                for w in range(2):
                    s.stt(sc1[w].full(), modT[1][w][:, 8:16], 1.0, nw.full(), ALU.add, ALU.mult)
                hT = [cx.sb(es, "hTb%d" % k, [128, T], BF16) for k in range(8)]
                xt = [cx.sb(es, "xtb%d" % i, [128, D]) for i in range(3)]
                xn = [cx.sb(es, "xnb%d" % i, [128, D]) for i in range(3)]
                junk = cx.sb(es, "junkb", [128, D])
                st = [cx.sb(es, "stb%d" % i, [128, 4]) for i in range(3)]
                def m0(i):
                    x_, n_, st_ = xt[i % 3], xn[i % 3], st[i % 3]
                    s.dma(x_.full(), X1[i * 128:(i + 1) * 128, :])
                    s.act(junk.full(), x_.full(), AF.Square, accum=st_[:, 0:1])
                    s.ts(st_[:, 1:2], st_[:, 0:1], 1.0 / D, EPS, ALU.mult, ALU.add)
                    s.act(st_[:, 2:3], st_[:, 1:2], AF.Sqrt)
                    s.recip(st_[:, 3:4], st_[:, 2:3])
                    s.ts(n_.full(), x_.full(), st_[:, 3:4], None, ALU.mult)

                def m1(i):
                    w = 1 if i < 2 else 0
                    n_ = xn[i % 3]
                    for half in range(2):
                        bk = nbank()
                        for kk in range(4):
                            k = half * 4 + kk
                            s.transpose(bk[:, kk * 128:(kk + 1) * 128], n_[:, k * 128:(k + 1) * 128], ident)
                        for kk in range(4):
                            k = half * 4 + kk
                            s.act(hT[k][:, i * 128:(i + 1) * 128], bk[:, kk * 128:(kk + 1) * 128], AF.Identity,
                                  bias=modT[1][w][:, k:k + 1], scale=sc1[w][:, k:k + 1])

                pipeline(list(range(NT)), [m0, m1])
                wq = [cx.sb(es, "wq%d" % i, [128, 8, 512], BF16) for i in range(4)]
                for q4_ in range(4):
                    s.dma(wq[q4_].full(), o_w_in.view(q4_ * 512, [[2 * D, 128], [128 * 2 * D, 8], [1, 512]]), q="pool")
                ot = [cx.sb(es, "ot%d" % i, [128, D]) for i in range(2)]
                oi = 0
                for which in range(2):
                    for i in range(NT):
                        if which == 1 and i < 2:
                            continue
                        o_ = ot[oi % 2]
                        oi += 1
                        for half in range(2):
                            bk = nbank()
                            s.mm(bk.full(), [(hT[k][:, i * 128:(i + 1) * 128], wq[which * 2 + half][:, k, :]) for k in range(8)])
                            if which == 0:
                                s.copy(o_[:, half * 512:(half + 1) * 512], bk.full(), eng="act")
                            else:
                                s.act(o_[:, half * 512:(half + 1) * 512], bk.full(), AF.Silu)
                        s.dma((U if which == 0 else SG1)[i * 128:(i + 1) * 128, :], o_.full())
                s.flush()

            L1S = os.environ.get('L1S', 'z')
            if L1S == 'a':
                return nc
            with ExitStack() as es:
                lam = cx.sb(es, "lam", [128, 2, 3, 32])
                bprm = cx.sb(es, "bprm", [128, 2, 32, 16])
                cprm = cx.sb(es, "cprm", [128, 2, 32, 16])
                s.dma(lam.full(), s5_lam.full())
                s.dma(bprm.full(), s5_b.full())
                s.dma(cprm.full(), s5_c.full())
                kc = cx.sb(es, "kconst", [128, 4])
                s.memset(kc[:, 0:1], 1.0 / 16)
                s.memset(kc[:, 1:2], math.pi / 2)
                s.memset(kc[:, 2:3], 0.0)
                s.memset(kc[:, 3:4], 1.0)
                W64 = [128, 2, 32]

                def t64(name):
                    return cx.sb(es, name, W64)

                def lv(i):
                    return lam.view(i * 32, [[192, 128], [96, 2], [1, 32]])

                dt_ = t64("dt_"); mag = t64("mag"); th = t64("th"); cs = t64("cs"); sn = t64("sn")
                t_a = t64("t_a"); t_b = t64("t_b"); t_c = t64("t_c")
                abre = t64("abre"); abim = t64("abim"); cre = t64("cre"); cim = t64("cim")
                s.act(dt_.full(), lv(2), AF.Exp)
                s.tt(t_a.full(), lv(0), dt_.full(), ALU.mult)
                s.act(mag.full(), t_a.full(), AF.Exp)
                s.tt(th.full(), lv(1), dt_.full(), ALU.mult)
                s.act(sn.full(), th.full(), AF.Sin, scale=kc[:, 0:1])
                s.act(cs.full(), th.full(), AF.Sin, scale=kc[:, 0:1], bias=kc[:, 1:2])
                for _ in range(4):
                    s.tt(t_a.full(), cs.full(), cs.full(), ALU.mult)
                    s.tt(t_b.full(), sn.full(), sn.full(), ALU.mult)
                    s.tt(t_c.full(), sn.full(), cs.full(), ALU.mult)
                    s.tt(cs.full(), t_a.full(), t_b.full(), ALU.subtract)
                    s.ts(sn.full(), t_c.full(), 2.0, None, ALU.mult)
                s.tt(abre.full(), mag.full(), cs.full(), ALU.mult)
                s.tt(abim.full(), mag.full(), sn.full(), ALU.mult)
                PW = cx.sb(es, "PW", [128, 2, 9, 64])

                def pw(ri, k):
                    return PW.view((ri * 9 + k) * 64, [[2 * 9 * 64, 128], [32, 2], [1, 32]])

                s.memset(PW[:, 0, 0, :], 1.0)
                s.memset(PW[:, 1, 0, :], 0.0)
                for k in range(8):
                    s.tt(t_a.full(), pw(0, k), abre.full(), ALU.mult)
                    s.tt(t_b.full(), pw(1, k), abim.full(), ALU.mult)
                    s.tt(pw(0, k + 1), t_a.full(), t_b.full(), ALU.subtract)
                    s.tt(t_a.full(), pw(0, k), abim.full(), ALU.mult)
                    s.tt(t_b.full(), pw(1, k), abre.full(), ALU.mult)
                    s.tt(pw(1, k + 1), t_a.full(), t_b.full(), ALU.add)
                s.ts(t_c.full(), abre.full(), -1.0, None, ALU.add)
                s.tt(t_a.full(), lv(0), lv(0), ALU.mult)
                s.tt(t_b.full(), lv(1), lv(1), ALU.mult)
                s.tt(t_a.full(), t_a.full(), t_b.full(), ALU.add)
                s.recip(dt_.full(), t_a.full())
                s.tt(t_a.full(), t_c.full(), lv(0), ALU.mult)
                s.tt(t_b.full(), abim.full(), lv(1), ALU.mult)
                s.tt(t_a.full(), t_a.full(), t_b.full(), ALU.add)
                s.tt(cre.full(), t_a.full(), dt_.full(), ALU.mult)
                s.tt(t_a.full(), abim.full(), lv(0), ALU.mult)
                s.tt(t_b.full(), t_c.full(), lv(1), ALU.mult)
                s.tt(t_a.full(), t_a.full(), t_b.full(), ALU.subtract)
                s.tt(cim.full(), t_a.full(), dt_.full(), ALU.mult)
                BB = cx.sb(es, "BB", [128, 2, 2, 512])
                tb1 = cx.sb(es, "tb1", [128, 512])
                tb2 = cx.sb(es, "tb2", [128, 512])

                def bb(ri, d_, g0=0, ng=32):
                    return BB.view((ri * 2 + d_) * 512 + g0 * 16, [[2048, 128], [16, ng], [1, 16]])

                def v3(buf, off, pstep, n1, s1, n2, s2):
                    return buf.view(off, [[pstep, 128], [s1, n1], [s2, n2]])

                def prm(buf, ri, g0=0, ng=32):
                    return buf.view(ri * 512 + g0 * 16, [[1024, 128], [16, ng], [1, 16]])

                def cf(buf, d_, g0=0, ng=32, n2=16):
                    return buf.view(d_ * 32 + g0, [[64, 128], [1, ng], [0, n2]])

                t1v = v3(tb1, 0, 512, 32, 16, 16, 1)
                t2v = v3(tb2, 0, 512, 32, 16, 16, 1)
                for d_ in range(2):
                    s.tt(t1v, prm(bprm, 0), cf(cre, d_), ALU.mult)
                    s.tt(t2v, prm(bprm, 1), cf(cim, d_), ALU.mult)
                    s.tt(bb(0, d_), t1v, t2v, ALU.subtract)
                    s.tt(t1v, prm(bprm, 1), cf(cre, d_), ALU.mult)
                    s.tt(t2v, prm(bprm, 0), cf(cim, d_), ALU.mult)
                    s.tt(bb(1, d_), t1v, t2v, ALU.add)
                LA = cx.sb(es, "LA", [128, 2, 32, 2])
                LB = cx.sb(es, "LB", [128, 2, 32, 2])
                for ri in range(2):
                    s.copy(LA.view(ri, [[128, 128], [64, 2], [2, 32]]), pw(0, 8))
                s.ts(LB.view(0, [[128, 128], [64, 2], [2, 32]]), pw(1, 8), -1.0, None, ALU.mult)
                s.copy(LB.view(1, [[128, 128], [64, 2], [2, 32]]), pw(1, 8))
                zt_ = cx.sb(es, "zt_", [16, 112], BF16)
                s.memset(zt_.full(), 0.0)
                s.flush()

                if L1S == 'b':
                    return nc
                for b in range(4 if L1S not in ('c1', 'd1', 'e1', 'f1', 'g1') else 1):
                    g0 = 8 * b
                    with ExitStack() as bs_:
                        CAB = cx.sb(bs_, "CAB", [128, 2, 2, 8 * 144])
                        WST = cx.sb(bs_, "WST", [128, 8, 2, 2, 2, 64], BF16)
                        TF = cx.sb(bs_, "TF", [128, 16, 128], BF16)
                        TB = cx.sb(bs_, "TB", [128, 16, 128], BF16)
                        CABb = cx.sb(bs_, "CABb", [128, 2, 2, 8 * 144], BF16)

                        with ExitStack() as tmp:
                            WT = cx.sb(tmp, "WT", [128, 2, 2, 8 * 128])
                            c1 = cx.sb(tmp, "c1", [128, 128])
                            c2 = cx.sb(tmp, "c2", [128, 128])
                            c1v = v3(c1, 0, 128, 8, 16, 16, 1)
                            c2v = v3(c2, 0, 128, 8, 16, 16, 1)
                            for d_ in range(2):
                                for ss in range(8):
                                    p_ = 7 - ss if d_ == 0 else ss
                                    pr = PW.view((0 * 9 + p_) * 64 + d_ * 32 + g0, [[1152, 128], [1, 8], [0, 16]])
                                    pi_ = PW.view((1 * 9 + p_) * 64 + d_ * 32 + g0, [[1152, 128], [1, 8], [0, 16]])
                                    o_re = WT.view((d_ * 2 + 0) * 1024 + ss * 16, [[4096, 128], [128, 8], [1, 16]])
                                    o_im = WT.view((d_ * 2 + 1) * 1024 + ss * 16, [[4096, 128], [128, 8], [1, 16]])
                                    s.tt(c1v, bb(0, d_, g0, 8), pr, ALU.mult)
                                    s.tt(c2v, bb(1, d_, g0, 8), pi_, ALU.mult)
                                    s.tt(o_re, c1v, c2v, ALU.subtract)
                                    s.tt(c1v, bb(1, d_, g0, 8), pr, ALU.mult)
                                    s.tt(c2v, bb(0, d_, g0, 8), pi_, ALU.mult)
                                    s.tt(o_im, c1v, c2v, ALU.add)
                            for gh in range(2):
                                p0 = gh * 64
                                for gq in range(8):
                                    bk = nbank()
                                    for d_ in range(2):
                                        for ri in range(2):
                                            sl = d_ * 2 + ri
                                            s.transpose(bk[:, sl * 64:(sl + 1) * 64],
                                                        WT.view(p0 * 4096 + (d_ * 2 + ri) * 1024 + gq * 128, [[4096, 64], [1, 128]]),
                                                        cst[p0:p0 + 64, 0, p0:p0 + 64])
                                    s.copy(WST.view(((gq * 2 + gh) * 4) * 64, [[4096, 128], [1, 256]]), bk[:, 0:256], eng="act")
                            s.flush()

                        KSB = cx.sb(bs_, "KSB", [16, 2, 16, 128], BF16)

                        def setup_part2():
                            c1v = ysb.view(0, [[512, 128], [16, 8], [1, 16]])
                            c2v = ysb.view(128, [[512, 128], [16, 8], [1, 16]])
                            for d_ in range(2):
                                for idx in range(9):
                                    p_ = idx if d_ == 0 else 8 - idx
                                    pr = PW.view((0 * 9 + p_) * 64 + d_ * 32 + g0, [[1152, 128], [1, 8], [0, 16]])
                                    pi_ = PW.view((1 * 9 + p_) * 64 + d_ * 32 + g0, [[1152, 128], [1, 8], [0, 16]])
                                    o_re = CAB.view((0 * 2 + d_) * 1152 + idx * 16, [[4608, 128], [144, 8], [1, 16]])
                                    o_im = CAB.view((1 * 2 + d_) * 1152 + idx * 16, [[4608, 128], [144, 8], [1, 16]])
                                    s.tt(c1v, prm(cprm, 0, g0, 8), pr, ALU.mult, eng="pool")
                                    s.tt(c2v, prm(cprm, 1, g0, 8), pi_, ALU.mult, eng="pool")
                                    s.tt(o_re, c1v, c2v, ALU.subtract, eng="pool")
                                    s.tt(c1v, prm(cprm, 0, g0, 8), pi_, ALU.mult, eng="pool")
                                    s.tt(c2v, prm(cprm, 1, g0, 8), pr, ALU.mult, eng="pool")
                                    s.tt(c1v, c1v, c2v, ALU.add, eng="pool")
                                    s.ts(o_im, c1v, -1.0, None, ALU.mult, eng="pool")
                            s.copy(CABb.full(), CAB.full(), eng="pool")
                            for gh in range(2):
                                p0 = gh * 64
                                for d_ in range(2):
                                    for gqq in range(2):
                                        bk = nbank()
                                        for q4 in range(4):
                                            gq = gqq * 4 + q4
                                            i0 = 0 if d_ == 0 else 1
                                            s.mm(bk[0:16, q4 * 128:(q4 + 1) * 128],
                                                 [(BB.view(p0 * 2048 + (0 * 2 + d_) * 512 + (g0 + gq) * 16, [[2048, 64], [1, 16]]),
                                                   CAB.view(p0 * 4608 + (0 * 2 + d_) * 1152 + gq * 144 + i0 * 16, [[4608, 64], [1, 128]])),
                                                  (BB.view(p0 * 2048 + (1 * 2 + d_) * 512 + (g0 + gq) * 16, [[2048, 64], [1, 16]]),
                                                   CAB.view(p0 * 4608 + (1 * 2 + d_) * 1152 + gq * 144 + i0 * 16, [[4608, 64], [1, 128]]))])
                                        s.copy(KSB.view(d_ * 2048 + (2 * gqq * 4 + gh) * 128, [[4096, 16], [256, 4], [1, 128]]),
                                               bk.view(0, [[512, 16], [128, 4], [1, 128]]), eng="act")
                            gbase = 16 * b
                            s.dma(KFP.view(gbase * 3840 + 7 * 16, [[240, 16], [3840, 16], [1, 128]]), KSB[:, 0, :, :])
                            s.dma(KBR.view(gbase * 3840, [[240, 16], [3840, 16], [1, 128]]), KSB[:, 1, :, :])
                            s.dma(KFP.view(gbase * 3840, [[240, 16], [3840, 16], [1, 112]]), zt_.view(0, [[112, 16], [0, 16], [1, 112]]))
                            s.dma(KBR.view(gbase * 3840 + 128, [[240, 16], [3840, 16], [1, 112]]), zt_.view(0, [[112, 16], [0, 16], [1, 112]]))
                            for ss in range(8):
                                s.dma(TF[ss * 16:(ss + 1) * 16, :, :], KFP.view(gbase * 3840 + (7 - ss) * 16, [[240, 16], [3840, 16], [1, 128]]))
                                s.dma(TB[ss * 16:(ss + 1) * 16, :, :], KBR.view(gbase * 3840 + (7 - ss) * 16, [[240, 16], [3840, 16], [1, 128]]))

                        u8b = cx.sb(bs_, "u8b", [128, 8, 256])
                        u8g = cx.sb(bs_, "u8g", [128, 16, 128])
                        U8T = cx.sb(bs_, "U8T", [128, 16, 288], BF16)
                        SSb = [cx.sb(bs_, "SSb%d" % i, [128, 16, 256], BF16) for i in range(2)]
                        NCOL = 326
                        PS = 16 * NCOL
                        SSD = [cx.sb(bs_, "SS%d" % i, [128, 8, 2, NCOL]) for i in range(2)]
                        CAR = [cx.sb(bs_, "CAR%d" % i, [128, 15, 8, 2]) for i in range(2)]
                        A36 = [cx.sb(bs_, "A36_%d" % i, [128, 8, 2]) for i in range(2)]
                        B36 = [cx.sb(bs_, "B36_%d" % i, [128, 8, 2]) for i in range(2)]
                        y8b = cx.sb(bs_, "y8b", [128, 8, 256])
                        ysb = cx.sb(bs_, "ysb", [128, 512])
                        TT1 = [cx.sb(bs_, "TT1_%d" % i, [128, 17, 8, 2]) for i in range(2)]
                        TT2 = [cx.sb(bs_, "TT2_%d" % i, [128, 17, 8, 2]) for i in range(2)]
                        for (j0, nj) in ((0, 32), (32, 128), (160, 128)):
                            s.dma(u8b[0:nj, :, :], U.view(8 * j0 * 1024 + 256 * b, [[8192, nj], [1024, 8], [1, 256]]))
                            s.copy(u8g.view(0, [[2048, nj], [128, 16], [16, 8], [1, 16]]),
                                   u8b.view(0, [[2048, nj], [16, 16], [256, 8], [1, 16]]), eng="act")
                            for gq4 in range(4):
                                bk = nbank()
                                for q4 in range(4):
                                    gi = gq4 * 4 + q4
                                    s.transpose(bk[:, q4 * 128:q4 * 128 + nj],
                                                u8g.view(128 * gi, [[2048, nj], [1, 128]]), cst[0:nj, 0, 0:nj])
                                s.copy(U8T.view(gq4 * 4 * 288 + j0, [[16 * 288, 128], [288, 4], [1, nj]]),
                                       bk.view(0, [[512, 128], [128, 4], [1, nj]]), eng="act")
                        if L1S in ('d', 'd1'):
                            s.flush()
                            continue
                        s.memset(SSD[0].view(0, [[PS, 128], [NCOL, 16], [1, 1]]), 0.0)
                        s.memset(SSD[0].view(289, [[PS, 128], [NCOL, 16], [1, 37]]), 0.0)
                        s.memset(SSD[1].view(288, [[PS, 128], [NCOL, 16], [1, 38]]), 0.0)
                        s.memset(SSD[0].view(289, [[PS, 128], [2 * NCOL, 8], [1, 1]]), 1.0)
                        s.memset(SSD[1].view(288 + 18 - 1, [[PS, 128], [2 * NCOL, 8], [1, 1]]), 1.0)
                        for gq in range(8):
                            for gh in range(2):
                                gi = 2 * gq + gh
                                p0 = gh * 64
                                for d_ in range(2):
                                    for ri in range(2):
                                        bk = nbank()
                                        s.mm(bk[p0:p0 + 64, 0:288],
                                             [(WST.view((((gq * 2 + gh) * 2 + d_) * 2 + ri) * 64, [[4096, 128], [1, 64]]),
                                               U8T[:, gi, :])])
                                        so = p0 * PS + (gq * 2 + ri) * NCOL
                                        if d_ == 0:
                                            s.copy(SSD[0].view(so + 1, [[PS, 64], [1, 288]]), bk[p0:p0 + 64, 0:288], eng="act")
                                        else:
                                            s.copy(SSD[1].view(so + 256, [[PS, 64], [1, 32]]), bk[p0:p0 + 64, 0:32], eng="act")
                                            s.copy(SSD[1].view(so, [[PS, 64], [1, 256]]), bk[p0:p0 + 64, 32:288], eng="act")
                        if L1S in ('e', 'e1'):
                            s.flush()
                            continue
                        DS = 8 * 2 * 289
                        setup_part2()
                        RI, GQ = NCOL, 2 * NCOL
                        SEG, NSEG = 18, 16
                        REC_ENG2 = os.environ.get('REC2', 'dve')

                        def cplx_step(items):
                            engs = ("dve", REC_ENG2)
                            for n_, (pv, psw, cv, ca, cb_, t1_, t2_) in enumerate(items):
                                s.tt(t1_, pv, ca, ALU.mult, eng=engs[n_ % 2])
                                s.tt(t2_, psw, cb_, ALU.mult, eng=engs[n_ % 2])
                            for n_, (pv, psw, cv, ca, cb_, t1_, t2_) in enumerate(items):
                                s.tt(t1_, t1_, t2_, ALU.add, eng=engs[n_ % 2])
                            for n_, (pv, psw, cv, ca, cb_, t1_, t2_) in enumerate(items):
                                if cv is not None:
                                    s.tt(cv, cv, t1_, ALU.add, eng=engs[n_ % 2])

                        def segv(SS, col, nseg):
                            return (SS.view(col, [[PS, 128], [SEG, nseg], [GQ, 8], [RI, 2]]),
                                    SS.view(col + RI, [[PS, 128], [SEG, nseg], [GQ, 8], [-RI, 2]]))

                        def coef(buf, d_, nseg):
                            return buf.view(d_ * 64 + g0 * 2, [[128, 128], [0, nseg], [2, 8], [1, 2]])

                        TTP = 17 * 16

                        for k in range(1, SEG):
                            items = []
                            for d_ in range(2):
                                pc = k if d_ == 0 else SEG - k
                                cc = k + 1 if d_ == 0 else SEG - 1 - k
                                pv, psw = segv(SSD[d_], pc, NSEG + 1)
                                cv, _ = segv(SSD[d_], cc, NSEG + 1)
                                items.append((pv, psw, cv, coef(LA, d_, NSEG + 1), coef(LB, d_, NSEG + 1), TT1[d_].full(), TT2[d_].full()))
                            cplx_step(items)
                        items = []
                        for d_ in range(2):
                            clast = 288 + SEG if d_ == 0 else 288
                            pv, psw = segv(SSD[d_], clast, 1)
                            items.append((pv, psw, None, coef(LA, d_, 1), coef(LB, d_, 1),
                                          TT1[d_].view(0, [[TTP, 128], [16, 1], [2, 8], [1, 2]]),
                                          TT2[d_].view(0, [[TTP, 128], [16, 1], [2, 8], [1, 2]])))
                        cplx_step(items)
                        for d_ in range(2):
                            l36re = TT1[d_].view(0, [[TTP, 128], [2, 8], [0, 2]])
                            s.copy(A36[d_].full(), l36re)
                            s.ts(B36[d_][:, :, 0:1], TT1[d_].view(1, [[TTP, 128], [2, 8], [1, 1]]), -1.0, None, ALU.mult)
                            s.copy(B36[d_][:, :, 1:2], TT1[d_].view(1, [[TTP, 128], [2, 8], [1, 1]]))
                        for step in range(1, NSEG):
                            items = []
                            for d_ in range(2):
                                if d_ == 0:
                                    m = step
                                    cc, pc = SEG * m + SEG, SEG * m
                                else:
                                    m = NSEG - 1 - step
                                    cc, pc = SEG * m, SEG * m + SEG
                                pv, psw = segv(SSD[d_], pc, 1)
                                cv, _ = segv(SSD[d_], cc, 1)
                                items.append((pv, psw, cv,
                                              A36[d_].view(0, [[16, 128], [0, 1], [2, 8], [1, 2]]),
                                              B36[d_].view(0, [[16, 128], [0, 1], [2, 8], [1, 2]]),
                                              TT1[d_].view(0, [[TTP, 128], [16, 1], [2, 8], [1, 2]]),
                                              TT2[d_].view(0, [[TTP, 128], [16, 1], [2, 8], [1, 2]])))
                            cplx_step(items)
                        items = []
                        for d_ in range(2):
                            pv, psw = segv(SSD[d_], SEG, NSEG - 1)
                            items.append((pv, psw, None, coef(LA, d_, NSEG - 1), coef(LB, d_, NSEG - 1),
                                          CAR[d_].full(), TT2[d_].view(0, [[TTP, 128], [16, NSEG - 1], [2, 8], [1, 2]])))
                        cplx_step(items)
                        NI = SEG - 1
                        for d_ in range(2):
                            SS = SSD[d_]
                            sb0 = SEG + 1 if d_ == 0 else 1

                            def sview(ri):
                                return SS.view(sb0 + ri * RI, [[PS, 128], [SEG, NSEG - 1], [GQ, 8], [1, NI]])

                            def tview(ri):
                                return SS.view(289 + ri * RI, [[PS, 128], [0, NSEG - 1], [GQ, 8], [1, NI]])

                            def cview(ri):
                                return CAR[d_].view(ri, [[(NSEG - 1) * 16, 128], [16, NSEG - 1], [2, 8], [0, NI]])

                            wshape = [[2048, 128], [8 * NI, NSEG - 1], [NI, 8], [1, NI]]
                            w1 = (u8g if d_ == 0 else u8b).view(0, wshape)
                            w2 = y8b.view(0, wshape)
                            s.tt(w1, tview(0), cview(0), ALU.mult)
                            s.tt(w2, tview(1), cview(1), ALU.mult)
                            s.tt(w1, w1, w2, ALU.subtract)
                            s.tt(sview(0), sview(0), w1, ALU.add)
                            s.tt(w1, tview(0), cview(1), ALU.mult)
                            s.tt(w2, tview(1), cview(0), ALU.mult)
                            s.tt(w1, w1, w2, ALU.add)
                            s.tt(sview(1), sview(1), w1, ALU.add)
                        s.copy(SSb[0].full(), SSD[0].view(32, [[PS, 128], [NCOL, 16], [1, 256]]), eng="act")
                        s.copy(SSb[1].full(), SSD[1].view(1, [[PS, 128], [NCOL, 16], [1, 256]]), eng="pool")
                        if L1S in ('f', 'f1'):
                            s.flush()
                            continue
                        for tt_ in range(2):
                            j0 = 32 + 128 * tt_
                            m0 = 128 * tt_
                            for gh in range(2):
                                p0 = gh * 64
                                for gqq in range(2):
                                    bx = nbank()
                                    by = nbank()
                                    for q4 in range(4):
                                        gq = gqq * 4 + q4
                                        gi = 2 * gq + gh
                                        s.mm(bx[:, q4 * 128:(q4 + 1) * 128],
                                             [(U8T[:, gi, j0:j0 + 128], TF[:, gi, :]), (U8T[:, gi, j0:j0 + 128], TB[:, gi, :])])
                                        pairs = []
                                        for d_ in range(2):
                                            c0 = m0
                                            i0 = 1 if d_ == 0 else 0
                                            for ri in range(2):
                                                so = p0 * 4096 + (gq * 2 + ri) * 256 + c0
                                                pairs.append((SSb[d_].view(so, [[4096, 64], [1, 128]]),
                                                              CABb.view(p0 * 4608 + (ri * 2 + d_) * 1152 + gq * 144 + i0 * 16, [[4608, 64], [1, 128]])))
                                        s.mm(by[:, q4 * 128:(q4 + 1) * 128], pairs)
                                    s.copy(ysb.full(), by.full(), eng="act")
                                    s.tt(y8b.view(32 * gqq * 4 + 16 * gh, [[2048, 128], [32, 4], [256, 8], [1, 16]]),
                                         bx.view(0, [[512, 128], [128, 4], [16, 8], [1, 16]]),
                                         ysb.view(0, [[512, 128], [128, 4], [16, 8], [1, 16]]), ALU.add)
                            s.dma(YTOK.view((CTX + 8 * m0) * 1024 + 256 * b, [[8192, 128], [1024, 8], [1, 256]]), y8b.full())
                        s.flush()

            if L1S in ('g', 'g1'):
                return nc
            with ExitStack() as es:
                gw = [cx.sb(es, "gw%d" % k, [128, D], BF16) for k in range(8)]
                ow = [cx.sb(es, "ow%d" % k, [128, D], BF16) for k in range(8)]
                dskb = cx.sb(es, "dskb", [128, D])
                glbb = cx.sb(es, "glbb", [128, D])
                fnwb = cx.sb(es, "fnwb", [128, D])
                kg = cx.sb(es, "kg", [128, 1])
                s.memset(kg.full(), 2.0 * math.sqrt(2.0 / math.pi))
                kmh = cx.sb(es, "kmh", [128, 1])
                s.memset(kmh.full(), -0.5)
                for k in range(8):
                    s.dma(gw[k].full(), o_glu_w[k * 128:(k + 1) * 128, :], q="pool")
                    s.dma(ow[k].full(), o_w_out[k * 128:(k + 1) * 128, :], q="pool")
                s.dma(dskb.full(), o_d_skip.view(0, [[0, 128], [1, D]]))
                s.dma(glbb.full(), o_glu_b.view(0, [[0, 128], [1, D]]))
                s.dma(fnwb.full(), final_norm_w.view(0, [[0, 128], [1, D]]))
                NB3 = 4
                ya = [cx.sb(es, "ya%d" % i, [128, D]) for i in range(NB3)]
                ua = [cx.sb(es, "ua%d" % i, [128, D]) for i in range(NB3)]
                sga = [cx.sb(es, "sga%d" % i, [128, D]) for i in range(NB3)]
                xa = [cx.sb(es, "xa%d" % i, [128, D]) for i in range(NB3)]
                w1s = [cx.sb(es, "w1_%d" % i, [128, D]) for i in range(NB3)]
                w2s = [cx.sb(es, "w2_%d" % i, [128, D]) for i in range(NB3)]
                w3s = [cx.sb(es, "w3_%d" % i, [128, D]) for i in range(NB3)]
                tTs = [cx.sb(es, "tT_%d" % i, [128, 8, 128], BF16) for i in range(2 * NB3)]
                sts = [cx.sb(es, "st10_%d" % i, [128, 4]) for i in range(NB3)]

                def transp8(src, tT):
                    for half in range(2):
                        bk = nbank()
                        for kk in range(4):
                            k = half * 4 + kk
                            s.transpose(bk[:, kk * 128:(kk + 1) * 128], src[:, k * 128:(k + 1) * 128], ident)
                        s.copy(tT[:, half * 4:(half + 1) * 4, :], bk.view(0, [[512, 128], [128, 4], [1, 128]]), eng="act")

                TAILN = int(os.environ.get('TAILN', NT))

                def bufs(i):
                    b_ = i % NB3
                    return ya[b_], ua[b_], sga[b_], xa[b_], w1s[b_], w2s[b_], w3s[b_], tTs[2 * b_], tTs[2 * b_ + 1], sts[b_]

                def stage0(i):
                    y_, u_, g_, x_, w1, w2, w3, tTa, tTb, st = bufs(i)
                    s.dma(y_.full(), YTOK[i * 128:(i + 1) * 128, :])
                    s.dma(u_.full(), U[i * 128:(i + 1) * 128, :])
                    s.dma(g_.full(), SG1[i * 128:(i + 1) * 128, :])
                    s.dma(x_.full(), X1[i * 128:(i + 1) * 128, :])
                    s.tt(w1.full(), u_.full(), dskb.full(), ALU.mult)
                    s.tt(y_.full(), y_.full(), w1.full(), ALU.add)
                    s.tt(w1.full(), y_.full(), y_.full(), ALU.mult)
                    s.ts(w1.full(), w1.full(), 0.044715, 1.0, ALU.mult, ALU.add)
                    s.tt(w1.full(), w1.full(), y_.full(), ALU.mult)
                    s.act(w1.full(), w1.full(), AF.Sigmoid, scale=kg[:, 0:1])
                    s.tt(w2.full(), y_.full(), w1.full(), ALU.mult)
                    transp8(w2, tTa)

                def stage1(i):
                    y_, u_, g_, x_, w1, w2, w3, tTa, tTb, st = bufs(i)
                    for half in range(2):
                        bk = nbank()
                        s.mm(bk.full(), [(tTa[:, k, :], gw[k][:, half * 512:(half + 1) * 512]) for k in range(8)])
                        s.tt(w1[:, half * 512:(half + 1) * 512], bk.full(), glbb[:, half * 512:(half + 1) * 512], ALU.add)
                    s.act(w1.full(), w1.full(), AF.Sigmoid)
                    s.tt(w2.full(), w2.full(), w1.full(), ALU.mult)
                    s.tt(w2.full(), w2.full(), g_.full(), ALU.mult)
                    transp8(w2, tTb)

                def stage2(i):
                    y_, u_, g_, x_, w1, w2, w3, tTa, tTb, st = bufs(i)
                    for half in range(2):
                        bk = nbank()
                        s.mm(bk.full(), [(tTb[:, k, :], ow[k][:, half * 512:(half + 1) * 512]) for k in range(8)])
                        s.tt(w1[:, half * 512:(half + 1) * 512], bk.full(), gate_bc[1][0][:, half * 512:(half + 1) * 512], ALU.mult)
                    s.tt(w3.full(), w1.full(), x_.full(), ALU.add)
                    s.act(w1.full(), w3.full(), AF.Square, accum=st[:, 0:1])
                    s.ts(st[:, 1:2], st[:, 0:1], 1.0 / D, EPS, ALU.mult, ALU.add)
                    s.tt(st[:, 3:4], st[:, 1:2], kmh.full(), ALU.pow, eng="pool")
                    s.act(w3.full(), w3.full(), AF.Copy, scale=st[:, 3:4])
                    s.tt(w2.full(), w3.full(), fnwb.full(), ALU.mult)
                    s.dma(out_t[(i - 2) * 128:(i - 1) * 128, :], w2.full())

                pipeline(list(range(2, TAILN)), [stage0, (lambda i: None), stage1, stage2])
                s.flush()

    return nc


def _consts():
    c = np.zeros((128, 6, 512), np.float32)
    j = np.arange(128)[:, None]
    l = np.arange(128)[None, :]
    c[:, 0, :128] = np.eye(128, dtype=np.float32)
    c[0, 0, 128:256] = 1.0
    c[1, 0, 256:384] = 1.0
    c[:, 1, :128] = (j <= l)
    c[:, 2, :128] = (j >= l)
    c[:, 3, :] = 1.0
    nf = np.where(l < j, -30000.0, 0.0).astype(np.float32)
    nb = np.where(l > j, -30000.0, 0.0).astype(np.float32)
    c[:, 4, :] = np.tile(nf, (1, 4))
    c[:, 5, :] = np.tile(nb, (1, 4))
    return c


def _rope_tables():
    rows = L // 64
    row = np.repeat(np.arange(rows, dtype=np.float32), 64)
    col = np.tile(np.arange(64, dtype=np.float32), rows)
    n_freq = 16
    inv = (np.float32(10000.0) ** (-np.arange(n_freq, dtype=np.float32) / n_freq)).astype(np.float32)
    ang = np.concatenate([row[:, None] * inv, col[:, None] * inv], axis=-1).astype(np.float32)
    cos = np.cos(ang).astype(np.float32)
    sin = np.sin(ang).astype(np.float32)
    cosT = np.zeros((128, L), np.float32)
    sinT = np.zeros((128, L), np.float32)
    for h2 in range(2):
        for half in range(2):
            p0 = h2 * 64 + half * 32
            cosT[p0:p0 + 32] = cos.T
            sinT[p0:p0 + 32] = (-sin.T if half == 0 else sin.T)
    return np.stack([cosT, sinT], axis=1)


def _vecT(v, nchunk):
    return np.ascontiguousarray(np.asarray(v, np.float32).reshape(nchunk, 128).T)


def prep_inputs(b, inp):
    f = lambda a: np.ascontiguousarray(np.asarray(a, np.float32))
    m = {}
    m["xin"] = f(np.concatenate([inp["ctx"][b], inp["x"][b]], axis=0))
    cv = np.stack([inp["c"][b], inp["c_ctx"]], axis=0)
    m["cvecT"] = f(cv.reshape(2, 8, 128).transpose(2, 0, 1))
    m["consts"] = _consts()
    m["rope"] = _rope_tables()
    m["e_ada_w"] = f(inp["e_ada_w"][0])
    m["e_ada_b"] = f(inp["e_ada_b"][0]).reshape(1, -1)
    m["e_norm_wT"] = _vecT(inp["e_norm_w"][0], 8)
    w = f(inp["e_w_in"][0])
    q = w[:, OFF_Q:OFF_Q + 1024].reshape(D, 16, 2, 32)
    qs = q[:, :, ::-1, :].reshape(D, 1024)
    k = w[:, OFF_KV:OFF_KV + 256].reshape(D, 4, 64)
    kr = np.concatenate([k, k], axis=2).reshape(D, 512)
    ks = k.reshape(D, 4, 2, 32)[:, :, ::-1, :].reshape(D, 4, 64)
    ksr = np.concatenate([ks, ks], axis=2).reshape(D, 512)
    m["e_w_in"] = f(np.concatenate([w, qs, kr, ksr], axis=1))
    cw = f(inp["e_conv_w"][0])
    m["e_conv_wT"] = f(cw.reshape(5, 12, 128).transpose(2, 1, 0))
    m["e_conv_bT"] = _vecT(inp["e_conv_b"][0], 12)
    m["e_dt_bias"] = f(inp["e_dt_bias"][0]).reshape(1, 32)
    m["e_a_log"] = f(inp["e_a_log"][0]).reshape(1, 32)
    m["e_d_skip"] = f(inp["e_d_skip"][0]).reshape(1, 16)
    m["e_ssd_norm_wT"] = _vecT(inp["e_ssd_norm_w"][0], 8)
    sk = f(inp["e_sink"][0]).reshape(8, 2)
    m["e_sink"] = f(np.repeat(sk.T[:, None, :], 64, axis=1).reshape(128, 8))
    m["e_w_out"] = f(inp["e_w_out"][0])
    m["o_ada_w"] = f(inp["o_ada_w"][0])
    m["o_ada_b"] = f(inp["o_ada_b"][0]).reshape(1, -1)
    m["o_norm_wT"] = _vecT(inp["o_norm_w"][0], 8)
    m["o_w_in"] = f(inp["o_w_in"][0])

    def gl(a):
        a = np.asarray(a, np.float32)
        rest = a.shape[2:]
        a = a.reshape((32, 2, 64) + rest)
        a = np.moveaxis(a, 0, 2)
        return a.reshape((128, 32) + rest)

    lam = np.zeros((128, 2, 3, 32), np.float32)
    for d_ in range(2):
        lam[:, d_, 0] = gl(inp["o_lam_re"][0][d_])
        lam[:, d_, 1] = gl(inp["o_lam_im"][0][d_])
        lam[:, d_, 2] = gl(np.repeat(np.asarray(inp["o_log_step"][0][d_])[:, None], 64, axis=1))
    m["s5_lam"] = f(lam)
    m["s5_b"] = f(np.stack([gl(inp["o_b_re"][0]), gl(inp["o_b_im"][0])], axis=1))
    cr = np.asarray(inp["o_c_re"][0]).transpose(0, 2, 1)
    ci = np.asarray(inp["o_c_im"][0]).transpose(0, 2, 1)
    m["s5_c"] = f(np.stack([gl(cr), gl(ci)], axis=1))
    m["o_d_skip"] = f(inp["o_d_skip"][0]).reshape(1, -1)
    m["o_glu_w"] = f(inp["o_glu_w"][0])
    m["o_glu_b"] = f(inp["o_glu_b"][0]).reshape(1, -1)
    m["o_w_out"] = f(inp["o_w_out"][0])
    m["final_norm_w"] = f(inp["final_norm_w"]).reshape(1, -1)
    return m


def kernel(**inputs):
    nc = build_program()
    in_maps = [prep_inputs(b, inputs) for b in range(8)]
    res = run_bass_kernel_spmd(nc, in_maps, core_ids=list(range(8)))
    return np.stack([r["out"] for r in res.results], axis=0)
```

```python
import math
import os
from contextlib import ExitStack

import numpy as np
import concourse.bass as bass
import concourse.mybir as mybir
from concourse.bass_utils import run_bass_kernel_spmd

F32 = mybir.dt.float32
BF16 = mybir.dt.bfloat16
AF = mybir.ActivationFunctionType
ALU = mybir.AluOpType

D = 1024
T = 2304
NT = 18
CTX = 256
L = 2048
EPS = 1e-6
TG = [(0, 256), (256, 512), (768, 512), (1280, 512), (1792, 512)]

SES_ALL = os.environ.get('SES', '0') == '1'
SAME_ENGINE_SYNC = {'act': SES_ALL, 'dve': SES_ALL, 'pool': True, 'pe': False, 'sp': True}
SEM_EPOCH = 30000


class V:
    __slots__ = ("buf", "ap")

    def __init__(self, buf, ap):
        self.buf = buf
        self.ap = ap


class Buf:
    def __init__(self, name, h):
        self.name = name
        self.h = h
        self.last_w = None
        self.readers = []
        self.is_psum = False

    def __getitem__(self, idx):
        return V(self, self.h[idx])

    def full(self):
        return V(self, self.h.ap())

    def view(self, offset, pattern):
        return V(self, bass.AP(self.h, offset, [list(p) for p in pattern]))


class Sched:
    ENG = ("pe", "act", "dve", "pool", "sp")

    def __init__(self, nc):
        self.nc = nc
        self.prog = {e: [] for e in self.ENG}
        self.sem = {}
        self.cnt = {}
        self.semid = 0
        self.known = {e: {} for e in self.ENG}
        for e in ("pe", "act", "dve", "pool"):
            self._new_engine_sem(e)
        self.nds = 8
        self.dsem = {}
        self.duse = {}
        self.dcnt = {}
        for q in ("sp", "pool"):
            self.dsem[q] = []
            self.duse[q] = []
            for i in range(self.nds):
                key = "d_%s_%d" % (q, i)
                self.dsem[q].append((nc.alloc_semaphore(key), key))
                self.duse[q].append(0)
            self.dcnt[q] = 0
        self.n_ops = 0

    def _new_engine_sem(self, e):
        self.semid += 1
        key = "s_%s_%d" % (e, self.semid)
        self.sem[e] = (self.nc.alloc_semaphore(key), key)
        self.cnt[e] = 0

    def _deps(self, reads, writes):
        deps = {}

        def add(tok):
            if tok is None:
                return
            h, key, val = tok
            if key not in deps or deps[key][1] < val:
                deps[key] = (h, val)

        for r in reads:
            add(r.buf.last_w)
            if r.buf.is_psum:
                for t in r.buf.readers:
                    add(t)
        for w in writes:
            add(w.buf.last_w)
            for t in w.buf.readers:
                add(t)
        return deps

    def _emit_waits(self, eng, deps, own_key=None):
        kn = self.known[eng]
        for key, (h, val) in deps.items():
            if key == own_key and not SAME_ENGINE_SYNC[eng]:
                continue
            if kn.get(key, 0) >= val:
                continue
            kn[key] = val
            self.prog[eng].append(("wait", h, val))

    def _update(self, tok, reads, writes):
        for w in writes:
            w.buf.last_w = tok
            w.buf.readers = []
        for r in reads:
            if r.buf.last_w is not tok:
                r.buf.readers.append(tok)

    def op(self, eng, fn, reads=(), writes=()):
        reads = [r for r in reads if r is not None]
        writes = list(writes)
        if self.cnt[eng] >= SEM_EPOCH:
            self._new_engine_sem(eng)
        h, key = self.sem[eng]
        own = None if eng == "pe" else key
        deps = self._deps(reads, writes)
        if eng == "pe":
            deps.pop(key, None)
        self._emit_waits(eng, deps, own_key=own)
        self.cnt[eng] += 1
        self.prog[eng].append(("op", fn, h, 1))
        tok = (h, key, self.cnt[eng])
        self._update(tok, reads, writes)
        self.n_ops += 1
        return tok

    def dma(self, out, in_, q="sp", **kw):
        deps = self._deps([in_], [out])
        self._emit_waits(q, deps)
        k = self.dcnt[q] % self.nds
        self.dcnt[q] += 1
        h, key = self.dsem[q][k]
        prev = 16 * self.duse[q][k]
        if prev > 0 and self.known[q].get(key, 0) < prev:
            self.known[q][key] = prev
            self.prog[q].append(("wait", h, prev))
        self.duse[q][k] += 1
        val = 16 * self.duse[q][k]
        o_ap, i_ap = out.ap, in_.ap
        self.prog[q].append(("op", lambda e: e.dma_start(out=o_ap, in_=i_ap, **kw), h, 16))
        tok = (h, key, val)
        self._update(tok, [in_], [out])
        self.n_ops += 1
        return tok

    def finish_dmas(self):
        for q in ("sp", "pool"):
            for k in range(self.nds):
                h, key = self.dsem[q][k]
                val = 16 * self.duse[q][k]
                if val > 0 and self.known[q].get(key, 0) < val:
                    self.known[q][key] = val
                    self.prog[q].append(("wait", h, val))

    def flush(self, name=None):
        self.finish_dmas()
        nc = self.nc
        prog = self.prog
        self.prog = {e: [] for e in self.ENG}

        def run(items, e):
            for it in items:
                if it[0] == "wait":
                    e.wait_ge(it[1], it[2])
                else:
                    inst = it[1](e)
                    inst.then_inc(it[2], it[3])

        with nc.Block() as block:
            if prog["sp"]:
                @block.sync
                def _(e):
                    run(prog["sp"], e)
            if prog["act"]:
                @block.scalar
                def _(e):
                    run(prog["act"], e)
            if prog["dve"]:
                @block.vector
                def _(e):
                    run(prog["dve"], e)
            if prog["pool"]:
                @block.gpsimd
                def _(e):
                    run(prog["pool"], e)
            if prog["pe"]:
                @block.tensor
                def _(e):
                    run(prog["pe"], e)

    def mm(self, out, pairs):
        n = len(pairs)

        def fn(e):
            inst = None
            for i, (l, r) in enumerate(pairs):
                inst = e.matmul(out.ap, l.ap, r.ap, start=(i == 0), stop=(i == n - 1))
            return inst

        self.op("pe", fn, reads=[p[0] for p in pairs] + [p[1] for p in pairs], writes=[out])

    def mm1(self, out, l, r, start, stop):
        self.op("pe", lambda e: e.matmul(out.ap, l.ap, r.ap, start=start, stop=stop), reads=[l, r], writes=[out])

    def transpose(self, out, in_, ident):
        self.op("pe", lambda e: e.transpose(out.ap, in_.ap, ident.ap), reads=[in_, ident], writes=[out])

    def act(self, out, in_, func, bias=None, scale=None, accum=None):
        kw = {}
        reads = [in_]
        writes = [out]
        if bias is not None:
            if isinstance(bias, V):
                kw["bias"] = bias.ap
                reads.append(bias)
            else:
                kw["bias"] = bias
        if scale is not None:
            if isinstance(scale, V):
                kw["scale"] = scale.ap
                reads.append(scale)
            else:
                kw["scale"] = scale
        if accum is not None:
            kw["accum_out"] = accum.ap
            writes.append(accum)
        self.op("act", lambda e: e.activation(out.ap, in_.ap, func, **kw), reads=reads, writes=writes)

    def ts(self, out, in0, s1, s2, op0, op1=None, eng="dve"):
        reads = [in0]
        a1 = s1
        a2 = s2
        if isinstance(s1, V):
            reads.append(s1)
            a1 = s1.ap
        if isinstance(s2, V):
            reads.append(s2)
            a2 = s2.ap
        if op1 is None:
            self.op(eng, lambda e: e.tensor_scalar(out.ap, in0.ap, a1, a2, op0), reads=reads, writes=[out])
        else:
            self.op(eng, lambda e: e.tensor_scalar(out.ap, in0.ap, a1, a2, op0, op1), reads=reads, writes=[out])

    def tt(self, out, in0, in1, op, eng="dve"):
        self.op(eng, lambda e: e.tensor_tensor(out.ap, in0.ap, in1.ap, op), reads=[in0, in1], writes=[out])

    def stt(self, out, in0, scalar, in1, op0, op1):
        reads = [in0, in1]
        sc = scalar
        if isinstance(scalar, V):
            reads.append(scalar)
            sc = scalar.ap
        self.op("dve", lambda e: e.scalar_tensor_tensor(out.ap, in0.ap, sc, in1.ap, op0, op1),
                reads=reads, writes=[out])

    def copy(self, out, in_, eng="dve"):
        if eng == "act":
            self.op("act", lambda e: e.copy(out.ap, in_.ap), reads=[in_], writes=[out])
        else:
            self.op(eng, lambda e: e.tensor_copy(out.ap, in_.ap), reads=[in_], writes=[out])

    def recip(self, out, in_):
        self.op("dve", lambda e: e.reciprocal(out.ap, in_.ap), reads=[in_], writes=[out])

    def memset(self, out, val, eng="dve"):
        self.op(eng, lambda e: e.memset(out.ap, val), reads=[], writes=[out])


class Ctx:
    def __init__(self, nc, sched):
        self.nc = nc
        self.s = sched
        self.uid = 0

    def sb(self, es, name, shape, dtype=F32):
        self.uid += 1
        h = es.enter_context(self.nc.sbuf_tensor("%s_%d" % (name, self.uid), list(shape), dtype))
        return Buf(name, h)

    def ps(self, es, name, shape=(128, 512), dtype=F32):
        self.uid += 1
        h = es.enter_context(self.nc.psum_tensor("%s_%d" % (name, self.uid), list(shape), dtype))
        b = Buf(name, h)
        b.is_psum = True
        return b

    def dram(self, name, shape, dtype=F32, kind="Internal"):
        h = self.nc.dram_tensor(name, list(shape), dtype, kind=kind)
        return Buf(name, h)


def pipeline(items, stages):
    n, k = len(items), len(stages)
    for t in range(n + k - 1):
        for j in range(k - 1, -1, -1):
            i = t - j
            if 0 <= i < n:
                stages[j](items[i])


def bc_mid(v_buf, base_off, pstep, nparts, n_outer, outer_step, n_inner):
    return v_buf.view(base_off, [[pstep, nparts], [outer_step, n_outer], [0, n_inner]])


E_NCOL = 5152
OFF_Z = 0
OFF_XBC = 1024
OFF_DT = 2560
OFF_Q = 2592
OFF_KV = 3616
OFF_G = 4128
OFF_QS = 5152
OFF_KR = 6176
OFF_KSR = 6688
E_NCOL_EXT = 7200


ORDER = ["p1", "p2a", "p2b", "p2c", "p2d", "p2e", "p2f", "p2g", "p2h", "p3", "p4", "p5", "all"]


def build_program(debug=(), stop="all"):
    def go(tag):
        return ORDER.index(tag) <= ORDER.index(stop)
    nc = bass.Bass("TRN2", target_bir_lowering=False)
    s = Sched(nc)
    cx = Ctx(nc, s)
    dbg = set(debug)

    def din(name, shape):
        return Buf(name, nc.dram_tensor(name, list(shape), F32, kind="ExternalInput"))

    def dout(name, shape):
        return Buf(name, nc.dram_tensor(name, list(shape), F32, kind="ExternalOutput"))

    def scratch(name, shape, dtype=F32):
        if name in dbg:
            return dout(name, shape)
        return Buf(name, nc.dram_tensor(name, list(shape), dtype))

    xin = din("xin", [T, D])
    cvecT = din("cvecT", [128, 2, 8])
    consts = din("consts", [128, 6, 512])
    rope = din("rope", [128, 2, L])
    e_ada_w = din("e_ada_w", [D, 3 * D])
    e_ada_b = din("e_ada_b", [1, 3 * D])
    e_norm_wT = din("e_norm_wT", [128, 8])
    e_w_in = din("e_w_in", [D, E_NCOL_EXT])
    e_conv_wT = din("e_conv_wT", [128, 12, 5])
    e_conv_bT = din("e_conv_bT", [128, 12])
    e_dt_bias = din("e_dt_bias", [1, 32])
    e_a_log = din("e_a_log", [1, 32])
    e_d_skip = din("e_d_skip", [1, 16])
    e_ssd_norm_wT = din("e_ssd_norm_wT", [128, 8])
    e_sink = din("e_sink", [128, 8])
    e_w_out = din("e_w_out", [2 * D, D])
    o_ada_w = din("o_ada_w", [D, 3 * D])
    o_ada_b = din("o_ada_b", [1, 3 * D])
    o_norm_wT = din("o_norm_wT", [128, 8])
    o_w_in = din("o_w_in", [D, 2 * D])
    s5_lam = din("s5_lam", [128, 2, 3, 32])
    s5_b = din("s5_b", [128, 2, 32, 16])
    s5_c = din("s5_c", [128, 2, 32, 16])
    o_d_skip = din("o_d_skip", [1, D])
    o_glu_w = din("o_glu_w", [D, D])
    o_glu_b = din("o_glu_b", [1, D])
    o_w_out = din("o_w_out", [D, D])
    final_norm_w = din("final_norm_w", [1, D])
    out_t = dout("out", [L, D])

    XS = scratch("XS", [T, 1024])
    BTOK = scratch("BTOK", [T, 256], BF16)
    BT = scratch("BT", [2, 128, T], BF16)
    CT = scratch("CT", [2, 128, T], BF16)
    SZ = scratch("SZ", [T, 1024])
    QR = scratch("QR", [8, 128, L], BF16)
    QC = scratch("QC", [8, 128, CTX], BF16)
    KR = scratch("KR", [4, 128, L], BF16)
    KC = scratch("KC", [4, 128, CTX], BF16)
    VT = scratch("VT", [T, 256], BF16)
    SG = scratch("SG", [8, 128, T])
    YF = scratch("YF", [T, 1024])
    YT = scratch("YT", [16, 128, T], BF16)
    X1 = scratch("X1", [T, 1024])
    U = scratch("U", [T, 1024])
    SG1 = scratch("SG1", [T, 1024])
    YTOK = scratch("YTOK", [T, 1024])
    KFP = scratch("KFP", [64, 16, 15, 16], BF16)
    KBR = scratch("KBR", [64, 16, 15, 16], BF16)
    HT = scratch("HT", [8, 128, T]) if "HT" in dbg else None
    DTD = scratch("DTD", [T, 32]) if "DTD" in dbg else None
    MODD = scratch("MODD", [4, 128, 24]) if "MODD" in dbg else None

    with ExitStack() as top:
        banks = [cx.ps(top, "bank%d" % i) for i in range(8)]
        cst = cx.sb(top, "cst", [128, 6, 512])
        s.dma(cst.full(), consts.full())
        ident = cst[:, 0, 0:128]
        tri = cst[:, 1, 0:128]
        utri = cst[:, 2, 0:128]
        ones = cst[:, 3, 0:128]
        onesb_t = cx.sb(top, "onesb", [128, 128], BF16)
        s.memset(onesb_t.full(), 1.0)
        onesb = onesb_t.full()
        modT = [[cx.sb(top, "modT%d%d" % (l, w), [128, 24]) for w in range(2)] for l in range(2)]
        gate_bc = [[cx.sb(top, "gate%d%d" % (l, w), [128, 1024]) for w in range(2)] for l in range(2)]
        scs = cx.sb(top, "scs", [128, 2, 8])

        def adaln_phase(layer, ada_w, ada_b):
            with ExitStack() as es:
                aw = [cx.sb(es, "aw%d" % k, [128, 3 * D]) for k in range(8)]
                ab2 = cx.sb(es, "ab2", [2, 3 * D])
                modrow2 = cx.sb(es, "modrow2", [2, 3 * D])
                if layer == 0:
                    cv = cx.sb(es, "cv", [128, 2, 8])
                    s.dma(cv.full(), cvecT.full())
                    s.act(scs.full(), cv.full(), AF.Silu)
                s.dma(ab2[0:1, :], ada_b.full())
                s.dma(ab2[1:2, :], ada_b.full())
                for k in range(8):
                    s.dma(aw[k].full(), ada_w[k * 128:(k + 1) * 128, :])
                for k in range(8):
                    for fg in range(6):
                        s.mm1(banks[fg][0:2, :], scs.view(k, [[16, 128], [8, 2]]), aw[k][:, fg * 512:(fg + 1) * 512],
                              start=(k == 0), stop=(k == 7))
                for fg in range(6):
                    s.tt(modrow2[0:2, fg * 512:(fg + 1) * 512], banks[fg][0:2, :], ab2[0:2, fg * 512:(fg + 1) * 512], ALU.add)
                bk = banks[6]
                for fc in range(24):
                    s.mm(bk[:, 2 * fc:2 * fc + 2], [(modrow2[0:2, fc * 128:(fc + 1) * 128], cst[0:2, 0, 0:2])])
                for w in range(2):
                    s.copy(modT[layer][w].full(), bk.view(w, [[512, 128], [2, 24]]))
                bi = 0
                for w in range(2):
                    selw = cst[0:2, 0, 128 + 128 * w:256 + 128 * w]
                    for hh in range(2):
                        bk2 = banks[(7 + bi) % 8]
                        bi += 1
                        s.mm(bk2.full(), [(selw, modrow2[0:2, 2048 + hh * 512:2048 + (hh + 1) * 512])])
                        s.copy(gate_bc[layer][w][:, hh * 512:(hh + 1) * 512], bk2.full(), eng="act")
                    if MODD is not None:
                        s.dma(MODD[layer * 2 + w], modT[layer][w].full())
                s.flush()

        adaln_phase(0, e_ada_w, e_ada_b)

        with ExitStack() as l0:
            DT = cx.sb(l0, "DT", [128, NT, 32])
            DTA = cx.sb(l0, "DTA", [128, NT, 32])
            nw = cx.sb(l0, "nw", [128, 8])
            sc1 = [cx.sb(l0, "sc1_%d" % w, [128, 8]) for w in range(2)]
            s.dma(nw.full(), e_norm_wT.full())
            for w in range(2):
                s.stt(sc1[w].full(), modT[0][w][:, 8:16], 1.0, nw.full(), ALU.add, ALU.mult)

            wo = [cx.sb(l0, "wo%d" % k, [128, D], BF16) for k in range(16)]
            hts = ExitStack()
            hT = [cx.sb(hts, "hT%d" % k, [128, T], BF16) for k in range(8)]
            with ExitStack() as es:
                xt = [cx.sb(es, "xt%d" % i, [128, D]) for i in range(3)]
                xn = [cx.sb(es, "xn%d" % i, [128, D]) for i in range(3)]
                junk = cx.sb(es, "junk", [128, D])
                st = [cx.sb(es, "st%d" % i, [128, 4]) for i in range(3)]
                def n0(i):
                    x_, n_, st_ = xt[i % 3], xn[i % 3], st[i % 3]
                    s.dma(x_.full(), xin[i * 128:(i + 1) * 128, :])
                    s.act(junk.full(), x_.full(), AF.Square, accum=st_[:, 0:1])
                    s.ts(st_[:, 1:2], st_[:, 0:1], 1.0 / D, EPS, ALU.mult, ALU.add)
                    s.act(st_[:, 2:3], st_[:, 1:2], AF.Sqrt)
                    s.recip(st_[:, 3:4], st_[:, 2:3])
                    s.ts(n_.full(), x_.full(), st_[:, 3:4], None, ALU.mult)

                def n1(i):
                    w = 1 if i < 2 else 0
                    n_ = xn[i % 3]
                    for half in range(2):
                        bk = banks[(2 * i + half) % 8]
                        for kk in range(4):
                            k = half * 4 + kk
                            s.transpose(bk[:, kk * 128:(kk + 1) * 128], n_[:, k * 128:(k + 1) * 128], ident)
                        for kk in range(4):
                            k = half * 4 + kk
                            s.act(hT[k][:, i * 128:(i + 1) * 128], bk[:, kk * 128:(kk + 1) * 128], AF.Identity,
                                  bias=modT[0][w][:, k:k + 1], scale=sc1[w][:, k:k + 1])

                pipeline(list(range(NT)), [n0, n1])
                if HT is not None:
                    for k in range(8):
                        s.dma(HT[k], hT[k].full())
                s.flush()

            with ExitStack() as es:
                WB = 256
                NWB, PF = 6, 4
                wbuf = [cx.sb(es, "wbuf%d" % i, [128, 8, WB], BF16) for i in range(NWB)]
                wplan = [(OFF_XBC + 256 * k, 256) for k in range(6)]
                for qc in range(8):
                    wplan += [(OFF_Q + qc * 128, 128), (OFF_QS + qc * 128, 128)]
                for j in range(4):
                    wplan += [(OFF_KR + j * 128, 128), (OFF_KSR + j * 128, 128)]
                wplan += [(OFF_G + 256 * k, 256) for k in range(4)]
                wplan += [(OFF_Z + 256 * k, 256) for k in range(4)]
                wplan += [(OFF_KV + 256, 256), (OFF_DT, 32)]
                wstate = {"i": 0, "issued": 0}

                def _issue(n):
                    col0, ncol = wplan[n]
                    wb = wbuf[n % NWB]
                    s.dma(wb[:, :, 0:ncol], e_w_in.view(col0, [[E_NCOL_EXT, 128], [128 * E_NCOL_EXT, 8], [1, ncol]]), q="pool")

                def load_w(col0, ncol=WB):
                    i = wstate["i"]
                    wstate["i"] += 1
                    assert wplan[i] == (col0, ncol), (i, wplan[i], col0, ncol)
                    while wstate["issued"] < min(i + PF + 1, len(wplan)):
                        _issue(wstate["issued"])
                        wstate["issued"] += 1
                    return wbuf[i % NWB]

                bstate = {"i": 0}

                def nbank():
                    bk = banks[bstate["i"] % 8]
                    bstate["i"] += 1
                    return bk

                def fm_mm(wb, cc, t0, n):
                    bk = nbank()
                    s.mm(bk[:, 0:n], [(wb[:, k, cc * 128:(cc + 1) * 128], hT[k][:, t0:t0 + n]) for k in range(8)])
                    return bk

                xraws = [cx.sb(es, "xraw%d" % i, [128, T]) for i in range(2)]
                accs = [cx.sb(es, "acc%d" % i, [128, T]) for i in range(2)]
                acc = accs[0]
                accbs = [cx.sb(es, "accb%d" % i, [128, T], BF16) for i in range(2)]
                accb = accbs[0]
                rc_i = {"i": 0}
                tmp1s = [cx.sb(es, "tmp1_%d" % i, [128, 512]) for i in range(2)]
                tmp2s = [cx.sb(es, "tmp2_%d" % i, [128, 512]) for i in range(2)]
                stg = [cx.sb(es, "stg%d" % i, [128, 4, 128]) for i in range(2)]
                stgb = [cx.sb(es, "stgb%d" % i, [128, 4, 128], BF16) for i in range(2)]
                rp = cx.sb(es, "rp", [128, 2, L])
                cw = cx.sb(es, "cw", [128, 12, 5])
                cb = cx.sb(es, "cb", [128, 12])
                dtb = cx.sb(es, "dtb", [128, 32])
                abc = cx.sb(es, "abc", [128, 32])
                s.dma(rp.full(), rope.full())
                s.dma(cw.full(), e_conv_wT.full())
                s.dma(cb.full(), e_conv_bT.full())
                s.dma(dtb.full(), e_dt_bias.view(0, [[0, 128], [1, 32]]))
                s.dma(abc.full(), e_a_log.view(0, [[0, 128], [1, 32]]))
                s.act(abc.full(), abc.full(), AF.Exp)
                s.ts(abc.full(), abc.full(), -1.0, None, ALU.mult)
                stg_i = {"i": 0}

                def transposes_to(dst, col0, src, lowp=False):
                    for i0 in range(0, NT, 4):
                        nb = min(4, NT - i0)
                        bk = nbank()
                        for ii in range(nb):
                            i = i0 + ii
                            s.transpose(bk[:, ii * 128:(ii + 1) * 128], src[:, i * 128:(i + 1) * 128], ident)
                        sg_ = (stgb if lowp else stg)[stg_i["i"] % 2]
                        stg_i["i"] += 1
                        s.copy(sg_[:, 0:nb, :], bk.view(0, [[512, 128], [128, nb], [1, 128]]), eng="act")
                        ncols = dst.h.shape[1]
                        s.dma(dst.view(i0 * 128 * ncols + col0, [[ncols, 128], [128 * ncols, nb], [1, 128]]),
                              sg_[:, 0:nb, :])

                wb_of = {}

                def xa(fc):
                    if fc % 2 == 0:
                        wb_of[fc // 2] = load_w(OFF_XBC + fc * 128)
                    wb = wb_of[fc // 2]
                    xraw = xraws[fc % 2]
                    for (t0, n) in TG:
                        bk = fm_mm(wb, fc % 2, t0, n)
                        s.copy(xraw[:, t0:t0 + n], bk[:, 0:n], eng="act")

                def xb(fc):
                    xraw, acc = xraws[fc % 2], accs[fc % 2]
                    s.ts(acc.full(), xraw.full(), cw[:, fc, 2:3], cb[:, fc:fc + 1], ALU.mult, ALU.add)
                    for kk in (0, 1, 3, 4):
                        d_ = kk - 2
                        for (s0, sl) in ((0, CTX), (CTX, L)):
                            lo = max(s0, s0 - d_)
                            hi = min(s0 + sl, s0 + sl - d_)
                            s.stt(acc[:, lo:hi], xraw[:, lo + d_:hi + d_], cw[:, fc, kk:kk + 1], acc[:, lo:hi],
                                  ALU.mult, ALU.add)
                    s.act(acc.full(), acc.full(), AF.Silu)
                    if fc < 8:
                        transposes_to(XS, fc * 128, acc)
                    elif fc < 10:
                        s.copy(accb.full(), acc.full(), eng="act")
                        s.dma(BT[fc - 8], accb.full())
                        transposes_to(BTOK, (fc - 8) * 128, acc, lowp=True)
                    else:
                        s.copy(accb.full(), acc.full(), eng="act")
                        s.dma(CT[fc - 10], accb.full())

                pipeline(list(range(12 if go('p2a') else 0)), [xa, xb])

                def rope_chunk(col_plain, col_swap, dst_rot, dst_ctx):
                    accb = accbs[rc_i["i"] % 2]
                    rc_i["i"] += 1
                    wa = load_w(col_plain, 128)
                    wsw = load_w(col_swap, 128)
                    for gi, (t0, n) in enumerate(TG):
                        bka = fm_mm(wa, 0, t0, n)
                        if gi == 0:
                            s.copy(accb[:, 0:CTX], bka[:, 0:CTX], eng="act")
                            continue
                        bkb = fm_mm(wsw, 0, t0, n)
                        l0 = t0 - CTX
                        tmp1, tmp2 = tmp1s[gi % 2], tmp2s[gi % 2]
                        s.tt(tmp1.full(), bka.full(), rp[:, 0, l0:l0 + 512], ALU.mult)
                        s.tt(tmp2.full(), bkb.full(), rp[:, 1, l0:l0 + 512], ALU.mult)
                        s.tt(accb[:, t0:t0 + n], tmp1.full(), tmp2.full(), ALU.add)
                    s.dma(dst_ctx, accb[:, 0:CTX])
                    s.dma(dst_rot, accb[:, CTX:T])

                for qc in range(8 if go('p2b') else 0):
                    rope_chunk(OFF_Q + qc * 128, OFF_QS + qc * 128, QR[qc], QC[qc])
                for j in range(4 if go('p2c') else 0):
                    rope_chunk(OFF_KR + j * 128, OFF_KSR + j * 128, KR[j], KC[j])

                for gc in range(8 if go('p2d') else 0):
                    acc = accs[gc % 2]
                    if gc % 2 == 0:
                        wb = load_w(OFF_G + gc * 128)
                    for (t0, n) in TG:
                        bk = fm_mm(wb, gc % 2, t0, n)
                        s.act(acc[:, t0:t0 + n], bk[:, 0:n], AF.Silu)
                    s.dma(SG[gc], acc.full())

                NT_E = NT if go('p2e') else 0
                wz = [load_w(OFF_Z + i * 256) for i in range(4)]
                for i in range(NT_E):
                    z_a = accs[i % 2]
                    for half in range(2):
                        bk = nbank()
                        for q4 in range(2):
                            wbz = wz[half * 2 + q4]
                            s.mm(bk[:, q4 * 256:(q4 + 1) * 256],
                                 [(hT[k][:, i * 128:(i + 1) * 128], wbz[:, k, :]) for k in range(8)])
                        s.act(z_a[:, half * 512:(half + 1) * 512], bk.full(), AF.Silu)
                    s.dma(SZ[i * 128:(i + 1) * 128, :], z_a[:, 0:1024])
                wv = load_w(OFF_KV + 256)
                wdt = load_w(OFF_DT, 32)
                vt = [cx.sb(es, "vt%d" % i, [128, 256], BF16) for i in range(2)]
                for i in range(NT if go('p2f') else 0):
                    bk = nbank()
                    s.mm(bk[:, 0:256], [(hT[k][:, i * 128:(i + 1) * 128], wv[:, k, :]) for k in range(8)])
                    s.copy(vt[i % 2].full(), bk[:, 0:256], eng="act")
                    s.dma(VT[i * 128:(i + 1) * 128, :], vt[i % 2].full())
                for i in range(NT if go('p2g') else 0):
                    bk = nbank()
                    s.mm(bk[:, 0:32], [(hT[k][:, i * 128:(i + 1) * 128], wdt[:, k, 0:32]) for k in range(8)])
                    s.tt(DT[:, i, :], bk[:, 0:32], dtb.full(), ALU.add)
                    if go('p2h'):
                        s.act(DT[:, i, :], DT[:, i, :], AF.Exp)
                        s.ts(DT[:, i, :], DT[:, i, :], 1.0, None, ALU.add)
                        s.act(DT[:, i, :], DT[:, i, :], AF.Ln)
                    s.tt(DTA[:, i, :], DT[:, i, :], abc.full(), ALU.mult)
                    if DTD is not None:
                        s.dma(DTD[i * 128:(i + 1) * 128, :], DT[:, i, :])
                s.flush()
            hts.close()
            for k in range(16):
                s.dma(wo[k].full(), e_w_out[k * 128:(k + 1) * 128, :], q="pool")

            with ExitStack() as es:
                nb_ = {"i": 0}

                def nbank():
                    bk = banks[nb_["i"] % 8]
                    nb_["i"] += 1
                    return bk

                N3 = 3
                NX, NY, NB4 = 2, 5, 4
                xs_t = [cx.sb(es, "xs_t%d" % i, [128, 1024]) for i in range(NX)]
                b_t = [cx.sb(es, "b_t%d" % i, [128, 256], BF16) for i in range(NB4)]
                bt_t = [cx.sb(es, "bt_t%d" % i, [128, 2, 128], BF16) for i in range(NB4)]
                ct_t = [cx.sb(es, "ct_t%d" % i, [128, 2, 128], BF16) for i in range(NB4)]
                yf_t = [cx.sb(es, "yf_t%d" % i, [128, 1024]) for i in range(NY)]
                sz_t = [cx.sb(es, "sz_t%d" % i, [128, 1024]) for i in range(2)]
                MTs = [cx.sb(es, "MT%d" % i, [128, 2048], BF16) for i in range(N3)]
                xcs = [cx.sb(es, "xc%d" % i, [128, 1024], BF16) for i in range(N3)]
                xcds = [cx.sb(es, "xcd%d" % i, [128, 1024], BF16) for i in range(N3)]
                tmpos = [cx.sb(es, "tmpo%d" % i, [128, 1024]) for i in range(N3)]
                ytots = [cx.sb(es, "ytot%d" % i, [128, 1024]) for i in range(2)]
                sms = [cx.sb(es, "sm%d" % i, [128, 4, 16]) for i in range(N3)]
                st3s = [cx.sb(es, "st3_%d" % i, [128, 4]) for i in range(N3)]
                ystgs = [cx.sb(es, "ystg%d" % i, [128, 8, 128], BF16) for i in range(2)]
                dtatris = [cx.sb(es, "dtatri%d" % i, [128, 2048]) for i in range(2)]
                decTs = [cx.sb(es, "decT%d" % i, [128, 2048]) for i in range(2)]
                cb_sbs = [cx.sb(es, "cb_sb%d" % i, [128, 256]) for i in range(2)]
                junk = cx.sb(es, "junk3", [128, 1024])
                Hs = [cx.sb(es, "Hs%d" % g, [128, 512]) for g in range(2)]
                Hb = [cx.sb(es, "Hb%d" % g, [128, 512], BF16) for g in range(2)]
                dsk = cx.sb(es, "dsk", [128, 16])
                snw = cx.sb(es, "snw", [128, 8])
                cm1 = cx.sb(es, "cm1", [128, 1])
                s.memset(cm1.full(), -1.0)
                s.dma(dsk.full(), e_d_skip.view(0, [[0, 128], [1, 16]]))
                s.dma(snw.full(), e_ssd_norm_wT.full())
                cmh = cx.sb(es, "cmh", [128, 1])
                s.memset(cmh.full(), -0.5)

                def bc3(buf, off, pstep, n1, s1, n2, s2):
                    return buf.view(off, [[pstep, 128], [s1, n1], [s2, n2]])

                n_ch = NT if go("p3") else 0
                for d_ in range(2):
                    order = list(range(NT)) if d_ == 0 else [1, 0] + list(range(NT - 1, 1, -1))
                    order = order[:n_ch]
                    TRIoff = 512 if d_ == 0 else 1024
                    TRIv = tri if d_ == 0 else utri
                    negm = cst[:, 4 + d_, :]
                    for g in range(2):
                        s.memset(Hs[g].full(), 0.0)
                        s.memset(Hb[g].full(), 0.0)

                    def stA(item, d_=d_, TRIoff=TRIoff, TRIv=TRIv, negm=negm):
                        ci, i = item
                        p3, p2, p4 = ci % N3, ci % 2, ci % NY
                        xs_, b_, bt_, ct_ = xs_t[ci % NX], b_t[ci % NB4], bt_t[ci % NB4], ct_t[ci % NB4]
                        MT, xc, xcd, sm = MTs[p3], xcs[p3], xcds[p3], sms[p3]
                        dtatri, decT, cb_sb = dtatris[p2], decTs[p2], cb_sbs[p2]
                        dta_i = DTA[:, i, d_ * 16:(d_ + 1) * 16]
                        doff = i * 32 + d_ * 16
                        s.tt(bc3(dtatri, 0, 2048, 16, 128, 128, 1), bc3(DTA, doff, NT * 32, 16, 1, 128, 0),
                             bc3(cst, TRIoff, 3072, 16, 0, 128, 1), ALU.mult, eng="pool")
                        bs = nbank()
                        s.mm(bs[:, 0:16], [(TRIv, dta_i)])
                        s.mm(bs[:, 16:32], [(ones, dta_i)])
                        na, ea, de, cd = sm[:, 0, :], sm[:, 1, :], sm[:, 2, :], sm[:, 3, :]
                        s.ts(na, bs[:, 0:16], -1.0, None, ALU.mult)
                        s.act(ea, bs[:, 0:16], AF.Exp)
                        s.tt(de, bs[:, 16:32], na, ALU.add)
                        s.act(de, de, AF.Exp)
                        s.act(cd, bs[:, 16:32], AF.Exp)
                        for hq in range(4):
                            bq = nbank()
                            s.mm(bq.full(), [(ones, dtatri[:, hq * 512:(hq + 1) * 512]), (ident, negm)])
                            for hh in range(4):
                                h = hq * 4 + hh
                                s.act(decT[:, h * 128:(h + 1) * 128], bq[:, hh * 128:(hh + 1) * 128], AF.Exp,
                                      bias=sm[:, 0, h:h + 1])
                        bc = nbank()
                        for g in range(2):
                            s.mm(bc[:, g * 128:(g + 1) * 128], [(bt_[:, g, :], ct_[:, g, :])])
                        s.copy(cb_sb.full(), bc[:, 0:256], eng="act")
                        for g in range(2):
                            s.tt(bc3(MT, g * 1024, 2048, 8, 128, 128, 1), bc3(decT, g * 1024, 2048, 8, 128, 128, 1),
                                 bc3(cb_sb, g * 128, 256, 8, 0, 128, 1), ALU.mult)
                        s.tt(bc3(xc, 0, 1024, 16, 64, 64, 1), bc3(xs_, 0, 1024, 16, 64, 64, 1),
                             bc3(DT, doff, NT * 32, 16, 1, 64, 0), ALU.mult, eng="pool")
                        s.tt(bc3(xcd, 0, 1024, 16, 64, 64, 1), bc3(xc, 0, 1024, 16, 64, 64, 1),
                             bc3(sm, 32, 64, 16, 1, 64, 0), ALU.mult, eng="pool")
                        if d_ == 1:
                            s.tt(bc3(tmpos[p3], 0, 1024, 16, 64, 64, 1), bc3(xs_, 0, 1024, 16, 64, 64, 1),
                                 bc3(dsk, 0, 16, 16, 1, 64, 0), ALU.mult, eng="pool")
                            s.tt(yf_t[p4].full(), yf_t[p4].full(), tmpos[p3].full(), ALU.add, eng="pool")

                    def stL(item, d_=d_):
                        ci, i = item
                        s.dma(xs_t[ci % NX].full(), XS[i * 128:(i + 1) * 128, :])
                        s.dma(b_t[ci % NB4].full(), BTOK[i * 128:(i + 1) * 128, :])
                        s.dma(bt_t[ci % NB4].full(), BT.view(i * 128, [[T, 128], [128 * T, 2], [1, 128]]))
                        s.dma(ct_t[ci % NB4].full(), CT.view(i * 128, [[T, 128], [128 * T, 2], [1, 128]]))
                        if d_ == 1:
                            s.dma(yf_t[ci % NY].full(), YF[i * 128:(i + 1) * 128, :])

                    def stB(item, d_=d_):
                        ci, i = item
                        p3 = ci % N3
                        b_, ct_ = b_t[ci % NB4], ct_t[ci % NB4]
                        MT, xc, xcd, sm, tmpo, ytot = MTs[p3], xcs[p3], xcds[p3], sms[p3], tmpos[p3], ytots[ci % 2]
                        ydst = yf_t[ci % NY] if d_ == 0 else ytot
                        if d_ == 1:
                            s.dma(sz_t[ci % 2].full(), SZ[i * 128:(i + 1) * 128, :])
                        for g in range(2):
                            by = nbank()
                            for hh in range(8):
                                h = g * 8 + hh
                                s.mm(by[:, hh * 64:(hh + 1) * 64], [(MT[:, h * 128:(h + 1) * 128], xc[:, h * 64:(h + 1) * 64])])
                            bo = nbank()
                            s.mm(bo.full(), [(ct_[:, g, :], Hb[g].full())])
                            s.tt(bc3(tmpo, g * 512, 1024, 8, 64, 64, 1), bc3(bo, 0, 512, 8, 64, 64, 1),
                                 bc3(sm, 16 + g * 8, 64, 8, 1, 64, 0), ALU.mult)
                            s.tt(ydst[:, g * 512:(g + 1) * 512], by.full(), tmpo[:, g * 512:(g + 1) * 512], ALU.add)
                        for g in range(2):
                            bst = nbank()
                            s.mm(bst.full(), [(b_[:, g * 128:(g + 1) * 128], xcd[:, g * 512:(g + 1) * 512])])
                            s.tt(bc3(Hs[g], 0, 512, 8, 64, 64, 1), bc3(Hs[g], 0, 512, 8, 64, 64, 1),
                                 bc3(sm, 48 + g * 8, 64, 8, 1, 64, 0), ALU.mult)
                            s.tt(Hs[g].full(), Hs[g].full(), bst.full(), ALU.add)
                            s.copy(Hb[g].full(), Hs[g].full(), eng="act")
                        if d_ == 0:
                            s.dma(YF[i * 128:(i + 1) * 128, :], yf_t[ci % NY].full())

                    def stC(item, d_=d_):
                        if d_ == 0:
                            return
                        ci, i = item
                        p3, p2 = ci % N3, ci % 2
                        ytot, sz_, st3, ystg = ytots[ci % 2], sz_t[ci % 2], st3s[p3], ystgs[p2]
                        s.tt(ytot.full(), ytot.full(), yf_t[ci % NY].full(), ALU.add)
                        s.tt(ytot.full(), ytot.full(), sz_.full(), ALU.mult)
                        s.act(junk.full(), ytot.full(), AF.Square, accum=st3[:, 0:1])
                        s.ts(st3[:, 1:2], st3[:, 0:1], 1.0 / 1024, EPS, ALU.mult, ALU.add)
                        s.tt(st3[:, 3:4], st3[:, 1:2], cmh.full(), ALU.pow, eng="pool")
                        s.act(ytot.full(), ytot.full(), AF.Copy, scale=st3[:, 3:4])
                        for half in range(2):
                            bk = nbank()
                            for kk in range(4):
                                k = half * 4 + kk
                                s.transpose(bk[:, kk * 128:(kk + 1) * 128], ytot[:, k * 128:(k + 1) * 128], ident)
                            for kk in range(4):
                                k = half * 4 + kk
                                s.act(ystg[:, k, :], bk[:, kk * 128:(kk + 1) * 128], AF.Copy, scale=snw[:, k:k + 1])
                        s.dma(YT.view(i * 128, [[T, 128], [128 * T, 8], [1, 128]]), ystg.full())

                    its = list(enumerate(order))
                    nit = len(its)
                    for t in range(-1, nit + 3):
                        if 0 <= t + 1 < nit:
                            stL(its[t + 1])
                        if 0 <= t - 3 < nit:
                            stC(its[t - 3])
                        if 0 <= t - 2 < nit:
                            stB(its[t - 2])
                        if 0 <= t < nit:
                            stA(its[t])
                s.flush()

            with ExitStack() as es:
                nb_ = {"i": 0}

                def nbank():
                    bk = banks[nb_["i"] % 8]
                    nb_["i"] += 1
                    return bk

                J2 = 2
                qr_ts = [cx.sb(es, "qr_t%d" % i, [128, 2, L], BF16) for i in range(J2)]
                qc_ts = [cx.sb(es, "qc_t%d" % i, [128, 2, CTX], BF16) for i in range(J2)]
                kr_ts = [cx.sb(es, "kr_t%d" % i, [128, L], BF16) for i in range(J2)]
                kc_ts = [cx.sb(es, "kc_t%d" % i, [128, CTX], BF16) for i in range(J2)]
                v_ts = [cx.sb(es, "v_t%d" % i, [128, NT, 64], BF16) for i in range(J2)]
                v2s = [cx.sb(es, "v2_%d" % i, [128, NT, 128], BF16) for i in range(J2)]
                sg_ts = [cx.sb(es, "sg_t%d" % i, [128, 2, T]) for i in range(J2)]
                asts = [cx.sb(es, "ast%d" % i, [128, 2, T], BF16) for i in range(J2)]
                NP = 5
                pt = [[cx.sb(es, "pt%d_%d" % (a, b), [128, 512], BF16) for b in range(5)] for a in range(NP)]
                rds = [cx.sb(es, "rd%d" % i, [128, 256]) for i in range(2)]
                aos = [cx.sb(es, "ao%d" % i, [128, 256]) for i in range(2)]
                es_pp = cx.sb(es, "es_pp", [128, 8])
                c8 = cx.sb(es, "c8", [128, 1])
                s.memset(c8.full(), 0.125)
                s.dma(es_pp.full(), e_sink.full())
                s.act(es_pp.full(), es_pp.full(), AF.Exp)
                ATT_DBG = [int(v) for v in os.environ.get("ATT_DBG", "4,18,4").split(",")]
                items = []
                for j in range(ATT_DBG[0] if go("p4") else 0):
                    qbs = ([("c", 0), ("c", 1)] + [("l", b) for b in range(16)])[:ATT_DBG[1]]
                    for qi, (kind, bi) in enumerate(qbs):
                        items.append((len(items), j, kind, bi, qi == 0, qi == len(qbs) - 1))

                def keys_of(kind, bi):
                    keys = [("c", 0, None), ("c", 1, None)]
                    if kind == "l":
                        if bi > 0:
                            keys.append(("l", bi - 1, "prev"))
                        keys.append(("l", bi, None))
                        if bi < 15:
                            keys.append(("l", bi + 1, "next"))
                    return keys

                def atA(item):
                    n, j, kind, bi, first, last = item
                    js = j % J2
                    qr_t, qc_t, kr_t, kc_t, v_t, v2, sg_t = qr_ts[js], qc_ts[js], kr_ts[js], kc_ts[js], v_ts[js], v2s[js], sg_ts[js]
                    if first:
                        s.dma(qr_t.full(), QR.view(2 * j * 128 * L, [[L, 128], [128 * L, 2], [1, L]]))
                        s.dma(qc_t.full(), QC.view(2 * j * 128 * CTX, [[CTX, 128], [128 * CTX, 2], [1, CTX]]))
                        s.dma(kr_t.full(), KR[j])
                        s.dma(kc_t.full(), KC[j])
                        s.dma(v_t.full(), VT.view(j * 64, [[256, 128], [128 * 256, NT], [1, 64]]))
                        s.dma(sg_t.full(), SG.view(2 * j * 128 * T, [[T, 128], [128 * T, 2], [1, T]]))
                        s.copy(v2[:, :, 0:64], v_t.full(), eng="pool")
                        s.copy(v2[:, :, 64:128], v_t.full(), eng="pool")
                    qsrc, q0 = (qc_t, bi * 128) if kind == "c" else (qr_t, bi * 128)
                    pts = pt[n % NP]
                    qw = qsrc.h.shape[2]
                    for ki, (kk, kb, msk) in enumerate(keys_of(kind, bi)):
                        ksrc = kc_t if kk == "c" else kr_t
                        for par in range(2):
                            p0 = par * 64
                            bs = nbank()
                            s.mm(bs[:, 0:256],
                                 [(ksrc[p0:p0 + 64, kb * 128:(kb + 1) * 128],
                                   qsrc.view(p0 * 2 * qw + q0, [[2 * qw, 64], [qw, 2], [1, 128]]))])
                            s.act(pts[ki][:, par * 256:(par + 1) * 256], bs[:, 0:256], AF.Exp, scale=c8[:, 0:1])
                        if msk is not None:
                            moff = 1024 if msk == "prev" else 512
                            s.tt(pts[ki].view(0, [[512, 128], [128, 4], [1, 128]]),
                                 pts[ki].view(0, [[512, 128], [128, 4], [1, 128]]),
                                 cst.view(moff, [[3072, 128], [0, 4], [1, 128]]), ALU.mult, eng="pool")

                def atB(item):
                    n, j, kind, bi, first, last = item
                    js = j % J2
                    v2, sg_t, ast = v2s[js], sg_ts[js], asts[js]
                    tok0 = bi * 128 if kind == "c" else CTX + bi * 128
                    keys = keys_of(kind, bi)
                    pts = pt[n % NP]
                    rd, ao = rds[n % 2], aos[n % 2]
                    vt_idx = [(kb if kk == "c" else 2 + kb) for (kk, kb, _) in keys]
                    bn = nbank()
                    s.mm(bn.full(), [(v2[:, vt_idx[ki], :], pts[ki].full()) for ki in range(len(keys))])
                    bd = nbank()
                    s.mm(bd.full(), [(onesb, pts[ki].full()) for ki in range(len(keys))])
                    for par in range(2):
                        p0 = par * 64
                        for c in range(2):
                            s.ts(rd[p0:p0 + 64, c * 128:(c + 1) * 128],
                                 bd[p0:p0 + 64, par * 256 + c * 128:par * 256 + (c + 1) * 128],
                                 es_pp[p0:p0 + 64, 2 * j + c:2 * j + c + 1], None, ALU.add)
                    s.recip(rd.full(), rd.full())
                    for par in range(2):
                        p0 = par * 64
                        s.tt(ao[p0:p0 + 64, :], bn[p0:p0 + 64, par * 256:(par + 1) * 256], rd[p0:p0 + 64, :], ALU.mult)
                    s.tt(ast.view(tok0, [[2 * T, 128], [T, 2], [1, 128]]),
                         ao.view(0, [[256, 128], [128, 2], [1, 128]]),
                         sg_t.view(tok0, [[2 * T, 128], [T, 2], [1, 128]]), ALU.mult)
                    if last:
                        s.dma(YT.view((8 + 2 * j) * 128 * T, [[T, 128], [128 * T, 2], [1, T]]), ast.full())

                pipeline(items, [atA, (lambda it: None), (lambda it: None), atB])
                s.flush()

            with ExitStack() as es:
                nb_ = {"i": 0}

                def nbank():
                    bk = banks[nb_["i"] % 8]
                    nb_["i"] += 1
                    return bk

                ytg = [cx.sb(es, "ytg%d" % i, [128, 16, 512], BF16) for i in range(2)]
                xt = [cx.sb(es, "xt5_%d" % i, [128, D]) for i in range(3)]
                x1t = [cx.sb(es, "x1t%d" % i, [128, D]) for i in range(3)]
                tmp5s = [cx.sb(es, "tmp5_%d" % i, [128, 512]) for i in range(2)]
                for i in range(NT if go("p5") else 0):
                    w = 1 if i < 2 else 0
                    gi_, ii = i // 4, i % 4
                    yg = ytg[gi_ % 2]
                    if ii == 0:
                        nt4 = min(4, NT - i)
                        s.dma(yg[:, :, 0:nt4 * 128], YT.view(i * 128, [[T, 128], [128 * T, 16], [1, nt4 * 128]]))
                    x_, o_ = xt[i % 3], x1t[i % 3]
                    s.dma(x_.full(), xin[i * 128:(i + 1) * 128, :])
                    for half in range(2):
                        tmp5 = tmp5s[half]
                        bk = nbank()
                        s.mm(bk.full(), [(yg[:, fc, ii * 128:(ii + 1) * 128], wo[fc][:, half * 512:(half + 1) * 512]) for fc in range(16)])
                        s.tt(tmp5.full(), bk.full(), gate_bc[0][w][:, half * 512:(half + 1) * 512], ALU.mult)
                        s.tt(o_[:, half * 512:(half + 1) * 512], tmp5.full(), x_[:, half * 512:(half + 1) * 512], ALU.add)
                    s.dma(X1[i * 128:(i + 1) * 128, :], o_.full(), q="pool")
                s.flush()

        if go("all"):
            adaln_phase(1, o_ada_w, o_ada_b)
        with ExitStack() as l1:
            if not go("all"):
                return nc
            nb_ = {"i": 0}

            def nbank():
                bk = banks[nb_["i"] % 8]
                nb_["i"] += 1
                return bk

            with ExitStack() as es:
                nw = cx.sb(es, "nw1", [128, 8])
                sc1 = [cx.sb(es, "sc1b_%d" % w, [128, 8]) for w in range(2)]
                s.dma(nw.full(), o_norm_wT.full())
                for w in range(2):
                    s.stt(sc1[w].full(), modT[1][w][:, 8:16], 1.0, nw.full(), ALU.add, ALU.mult)
                hT = [cx.sb(es, "hTb%d" % k, [128, T], BF16) for k in range(8)]
                xt = [cx.sb(es, "xtb%d" % i, [128, D]) for i in range(3)]
                xn = [cx.sb(es, "xnb%d" % i, [128, D]) for i in range(3)]
                junk = cx.sb(es, "junkb", [128, D])
                st = [cx.sb(es, "stb%d" % i, [128, 4]) for i in range(3)]
                def m0(i):
                    x_, n_, st_ = xt[i % 3], xn[i % 3], st[i % 3]
                    s.dma(x_.full(), X1[i * 128:(i + 1) * 128, :])
                    s.act(junk.full(), x_.full(), AF.Square, accum=st_[:, 0:1])
                    s.ts(st_[:, 1:2], st_[:, 0:1], 1.0 / D, EPS, ALU.mult, ALU.add)
                    s.act(st_[:, 2:3], st_[:, 1:2], AF.Sqrt)
                    s.recip(st_[:, 3:4], st_[:, 2:3])
                    s.ts(n_.full(), x_.full(), st_[:, 3:4], None, ALU.mult)

                def m1(i):
                    w = 1 if i < 2 else 0
                    n_ = xn[i % 3]
                    for half in range(2):
                        bk = nbank()
                        for kk in range(4):
                            k = half * 4 + kk
                            s.transpose(bk[:, kk * 128:(kk + 1) * 128], n_[:, k * 128:(k + 1) * 128], ident)
                        for kk in range(4):
                            k = half * 4 + kk
                            s.act(hT[k][:, i * 128:(i + 1) * 128], bk[:, kk * 128:(kk + 1) * 128], AF.Identity,
                                  bias=modT[1][w][:, k:k + 1], scale=sc1[w][:, k:k + 1])

                pipeline(list(range(NT)), [m0, m1])
                wq = [cx.sb(es, "wq%d" % i, [128, 8, 512], BF16) for i in range(4)]
                for q4_ in range(4):
                    s.dma(wq[q4_].full(), o_w_in.view(q4_ * 512, [[2 * D, 128], [128 * 2 * D, 8], [1, 512]]), q="pool")
                ot = [cx.sb(es, "ot%d" % i, [128, D]) for i in range(2)]
                oi = 0
                for which in range(2):
                    for i in range(NT):
                        if which == 1 and i < 2:
                            continue
                        o_ = ot[oi % 2]
                        oi += 1
                        for half in range(2):
                            bk = nbank()
                            s.mm(bk.full(), [(hT[k][:, i * 128:(i + 1) * 128], wq[which * 2 + half][:, k, :]) for k in range(8)])
                            if which == 0:
                                s.copy(o_[:, half * 512:(half + 1) * 512], bk.full(), eng="act")
                            else:
                                s.act(o_[:, half * 512:(half + 1) * 512], bk.full(), AF.Silu)
                        s.dma((U if which == 0 else SG1)[i * 128:(i + 1) * 128, :], o_.full())
                s.flush()

            L1S = os.environ.get('L1S', 'z')
            if L1S == 'a':
                return nc
            with ExitStack() as es:
                lam = cx.sb(es, "lam", [128, 2, 3, 32])
                bprm = cx.sb(es, "bprm", [128, 2, 32, 16])
                cprm = cx.sb(es, "cprm", [128, 2, 32, 16])
                s.dma(lam.full(), s5_lam.full())
                s.dma(bprm.full(), s5_b.full())
                s.dma(cprm.full(), s5_c.full())
                kc = cx.sb(es, "kconst", [128, 4])
                s.memset(kc[:, 0:1], 1.0 / 16)
                s.memset(kc[:, 1:2], math.pi / 2)
                s.memset(kc[:, 2:3], 0.0)
                s.memset(kc[:, 3:4], 1.0)
                W64 = [128, 2, 32]

                def t64(name):
                    return cx.sb(es, name, W64)

                def lv(i):
                    return lam.view(i * 32, [[192, 128], [96, 2], [1, 32]])

                dt_ = t64("dt_"); mag = t64("mag"); th = t64("th"); cs = t64("cs"); sn = t64("sn")
                t_a = t64("t_a"); t_b = t64("t_b"); t_c = t64("t_c")
                abre = t64("abre"); abim = t64("abim"); cre = t64("cre"); cim = t64("cim")
                s.act(dt_.full(), lv(2), AF.Exp)
                s.tt(t_a.full(), lv(0), dt_.full(), ALU.mult)
                s.act(mag.full(), t_a.full(), AF.Exp)
                s.tt(th.full(), lv(1), dt_.full(), ALU.mult)
                s.act(sn.full(), th.full(), AF.Sin, scale=kc[:, 0:1])
                s.act(cs.full(), th.full(), AF.Sin, scale=kc[:, 0:1], bias=kc[:, 1:2])
                for _ in range(4):
                    s.tt(t_a.full(), cs.full(), cs.full(), ALU.mult)
                    s.tt(t_b.full(), sn.full(), sn.full(), ALU.mult)
                    s.tt(t_c.full(), sn.full(), cs.full(), ALU.mult)
                    s.tt(cs.full(), t_a.full(), t_b.full(), ALU.subtract)
                    s.ts(sn.full(), t_c.full(), 2.0, None, ALU.mult)
                s.tt(abre.full(), mag.full(), cs.full(), ALU.mult)
                s.tt(abim.full(), mag.full(), sn.full(), ALU.mult)
                PW = cx.sb(es, "PW", [128, 2, 9, 64])

                def pw(ri, k):
                    return PW.view((ri * 9 + k) * 64, [[2 * 9 * 64, 128], [32, 2], [1, 32]])

                s.memset(PW[:, 0, 0, :], 1.0)
                s.memset(PW[:, 1, 0, :], 0.0)
                for k in range(8):
                    s.tt(t_a.full(), pw(0, k), abre.full(), ALU.mult)
                    s.tt(t_b.full(), pw(1, k), abim.full(), ALU.mult)
                    s.tt(pw(0, k + 1), t_a.full(), t_b.full(), ALU.subtract)
                    s.tt(t_a.full(), pw(0, k), abim.full(), ALU.mult)
                    s.tt(t_b.full(), pw(1, k), abre.full(), ALU.mult)
                    s.tt(pw(1, k + 1), t_a.full(), t_b.full(), ALU.add)
                s.ts(t_c.full(), abre.full(), -1.0, None, ALU.add)
                s.tt(t_a.full(), lv(0), lv(0), ALU.mult)
                s.tt(t_b.full(), lv(1), lv(1), ALU.mult)
                s.tt(t_a.full(), t_a.full(), t_b.full(), ALU.add)
                s.recip(dt_.full(), t_a.full())
                s.tt(t_a.full(), t_c.full(), lv(0), ALU.mult)
                s.tt(t_b.full(), abim.full(), lv(1), ALU.mult)
                s.tt(t_a.full(), t_a.full(), t_b.full(), ALU.add)
                s.tt(cre.full(), t_a.full(), dt_.full(), ALU.mult)
                s.tt(t_a.full(), abim.full(), lv(0), ALU.mult)
                s.tt(t_b.full(), t_c.full(), lv(1), ALU.mult)
                s.tt(t_a.full(), t_a.full(), t_b.full(), ALU.subtract)
                s.tt(cim.full(), t_a.full(), dt_.full(), ALU.mult)
                BB = cx.sb(es, "BB", [128, 2, 2, 512])
                tb1 = cx.sb(es, "tb1", [128, 512])
                tb2 = cx.sb(es, "tb2", [128, 512])

                def bb(ri, d_, g0=0, ng=32):
                    return BB.view((ri * 2 + d_) * 512 + g0 * 16, [[2048, 128], [16, ng], [1, 16]])

                def v3(buf, off, pstep, n1, s1, n2, s2):
                    return buf.view(off, [[pstep, 128], [s1, n1], [s2, n2]])

                def prm(buf, ri, g0=0, ng=32):
                    return buf.view(ri * 512 + g0 * 16, [[1024, 128], [16, ng], [1, 16]])

                def cf(buf, d_, g0=0, ng=32, n2=16):
                    return buf.view(d_ * 32 + g0, [[64, 128], [1, ng], [0, n2]])

                t1v = v3(tb1, 0, 512, 32, 16, 16, 1)
                t2v = v3(tb2, 0, 512, 32, 16, 16, 1)
                for d_ in range(2):
                    s.tt(t1v, prm(bprm, 0), cf(cre, d_), ALU.mult)
                    s.tt(t2v, prm(bprm, 1), cf(cim, d_), ALU.mult)
                    s.tt(bb(0, d_), t1v, t2v, ALU.subtract)
                    s.tt(t1v, prm(bprm, 1), cf(cre, d_), ALU.mult)
                    s.tt(t2v, prm(bprm, 0), cf(cim, d_), ALU.mult)
                    s.tt(bb(1, d_), t1v, t2v, ALU.add)
                LA = cx.sb(es, "LA", [128, 2, 32, 2])
                LB = cx.sb(es, "LB", [128, 2, 32, 2])
                for ri in range(2):
                    s.copy(LA.view(ri, [[128, 128], [64, 2], [2, 32]]), pw(0, 8))
                s.ts(LB.view(0, [[128, 128], [64, 2], [2, 32]]), pw(1, 8), -1.0, None, ALU.mult)
                s.copy(LB.view(1, [[128, 128], [64, 2], [2, 32]]), pw(1, 8))
                zt_ = cx.sb(es, "zt_", [16, 112], BF16)
                s.memset(zt_.full(), 0.0)
                s.flush()

                if L1S == 'b':
                    return nc
                for b in range(4 if L1S not in ('c1', 'd1', 'e1', 'f1', 'g1') else 1):
                    g0 = 8 * b
                    with ExitStack() as bs_:
                        CAB = cx.sb(bs_, "CAB", [128, 2, 2, 8 * 144])
                        WST = cx.sb(bs_, "WST", [128, 8, 2, 2, 2, 64], BF16)
                        TF = cx.sb(bs_, "TF", [128, 16, 128], BF16)
                        TB = cx.sb(bs_, "TB", [128, 16, 128], BF16)
                        CABb = cx.sb(bs_, "CABb", [128, 2, 2, 8 * 144], BF16)

                        with ExitStack() as tmp:
                            WT = cx.sb(tmp, "WT", [128, 2, 2, 8 * 128])
                            c1 = cx.sb(tmp, "c1", [128, 128])
                            c2 = cx.sb(tmp, "c2", [128, 128])
                            c1v = v3(c1, 0, 128, 8, 16, 16, 1)
                            c2v = v3(c2, 0, 128, 8, 16, 16, 1)
                            for d_ in range(2):
                                for ss in range(8):
                                    p_ = 7 - ss if d_ == 0 else ss
                                    pr = PW.view((0 * 9 + p_) * 64 + d_ * 32 + g0, [[1152, 128], [1, 8], [0, 16]])
                                    pi_ = PW.view((1 * 9 + p_) * 64 + d_ * 32 + g0, [[1152, 128], [1, 8], [0, 16]])
                                    o_re = WT.view((d_ * 2 + 0) * 1024 + ss * 16, [[4096, 128], [128, 8], [1, 16]])
                                    o_im = WT.view((d_ * 2 + 1) * 1024 + ss * 16, [[4096, 128], [128, 8], [1, 16]])
                                    s.tt(c1v, bb(0, d_, g0, 8), pr, ALU.mult)
                                    s.tt(c2v, bb(1, d_, g0, 8), pi_, ALU.mult)
                                    s.tt(o_re, c1v, c2v, ALU.subtract)
                                    s.tt(c1v, bb(1, d_, g0, 8), pr, ALU.mult)
                                    s.tt(c2v, bb(0, d_, g0, 8), pi_, ALU.mult)
                                    s.tt(o_im, c1v, c2v, ALU.add)
                            for gh in range(2):
                                p0 = gh * 64
                                for gq in range(8):
                                    bk = nbank()
                                    for d_ in range(2):
                                        for ri in range(2):
                                            sl = d_ * 2 + ri
                                            s.transpose(bk[:, sl * 64:(sl + 1) * 64],
                                                        WT.view(p0 * 4096 + (d_ * 2 + ri) * 1024 + gq * 128, [[4096, 64], [1, 128]]),
                                                        cst[p0:p0 + 64, 0, p0:p0 + 64])
                                    s.copy(WST.view(((gq * 2 + gh) * 4) * 64, [[4096, 128], [1, 256]]), bk[:, 0:256], eng="act")
                            s.flush()

                        KSB = cx.sb(bs_, "KSB", [16, 2, 16, 128], BF16)

                        def setup_part2():
                            c1v = ysb.view(0, [[512, 128], [16, 8], [1, 16]])
                            c2v = ysb.view(128, [[512, 128], [16, 8], [1, 16]])
                            for d_ in range(2):
                                for idx in range(9):
                                    p_ = idx if d_ == 0 else 8 - idx
                                    pr = PW.view((0 * 9 + p_) * 64 + d_ * 32 + g0, [[1152, 128], [1, 8], [0, 16]])
                                    pi_ = PW.view((1 * 9 + p_) * 64 + d_ * 32 + g0, [[1152, 128], [1, 8], [0, 16]])
                                    o_re = CAB.view((0 * 2 + d_) * 1152 + idx * 16, [[4608, 128], [144, 8], [1, 16]])
                                    o_im = CAB.view((1 * 2 + d_) * 1152 + idx * 16, [[4608, 128], [144, 8], [1, 16]])
                                    s.tt(c1v, prm(cprm, 0, g0, 8), pr, ALU.mult, eng="pool")
                                    s.tt(c2v, prm(cprm, 1, g0, 8), pi_, ALU.mult, eng="pool")
                                    s.tt(o_re, c1v, c2v, ALU.subtract, eng="pool")
                                    s.tt(c1v, prm(cprm, 0, g0, 8), pi_, ALU.mult, eng="pool")
                                    s.tt(c2v, prm(cprm, 1, g0, 8), pr, ALU.mult, eng="pool")
                                    s.tt(c1v, c1v, c2v, ALU.add, eng="pool")
                                    s.ts(o_im, c1v, -1.0, None, ALU.mult, eng="pool")
                            s.copy(CABb.full(), CAB.full(), eng="pool")
                            for gh in range(2):
                                p0 = gh * 64
                                for d_ in range(2):
                                    for gqq in range(2):
                                        bk = nbank()
                                        for q4 in range(4):
                                            gq = gqq * 4 + q4
                                            i0 = 0 if d_ == 0 else 1
                                            s.mm(bk[0:16, q4 * 128:(q4 + 1) * 128],
                                                 [(BB.view(p0 * 2048 + (0 * 2 + d_) * 512 + (g0 + gq) * 16, [[2048, 64], [1, 16]]),
                                                   CAB.view(p0 * 4608 + (0 * 2 + d_) * 1152 + gq * 144 + i0 * 16, [[4608, 64], [1, 128]])),
                                                  (BB.view(p0 * 2048 + (1 * 2 + d_) * 512 + (g0 + gq) * 16, [[2048, 64], [1, 16]]),
                                                   CAB.view(p0 * 4608 + (1 * 2 + d_) * 1152 + gq * 144 + i0 * 16, [[4608, 64], [1, 128]]))])
                                        s.copy(KSB.view(d_ * 2048 + (2 * gqq * 4 + gh) * 128, [[4096, 16], [256, 4], [1, 128]]),
                                               bk.view(0, [[512, 16], [128, 4], [1, 128]]), eng="act")
                            gbase = 16 * b
                            s.dma(KFP.view(gbase * 3840 + 7 * 16, [[240, 16], [3840, 16], [1, 128]]), KSB[:, 0, :, :])
                            s.dma(KBR.view(gbase * 3840, [[240, 16], [3840, 16], [1, 128]]), KSB[:, 1, :, :])
                            s.dma(KFP.view(gbase * 3840, [[240, 16], [3840, 16], [1, 112]]), zt_.view(0, [[112, 16], [0, 16], [1, 112]]))
                            s.dma(KBR.view(gbase * 3840 + 128, [[240, 16], [3840, 16], [1, 112]]), zt_.view(0, [[112, 16], [0, 16], [1, 112]]))
                            for ss in range(8):
                                s.dma(TF[ss * 16:(ss + 1) * 16, :, :], KFP.view(gbase * 3840 + (7 - ss) * 16, [[240, 16], [3840, 16], [1, 128]]))
                                s.dma(TB[ss * 16:(ss + 1) * 16, :, :], KBR.view(gbase * 3840 + (7 - ss) * 16, [[240, 16], [3840, 16], [1, 128]]))

                        u8b = cx.sb(bs_, "u8b", [128, 8, 256])
                        u8g = cx.sb(bs_, "u8g", [128, 16, 128])
                        U8T = cx.sb(bs_, "U8T", [128, 16, 288], BF16)
                        SSb = [cx.sb(bs_, "SSb%d" % i, [128, 16, 256], BF16) for i in range(2)]
                        NCOL = 326
                        PS = 16 * NCOL
                        SSD = [cx.sb(bs_, "SS%d" % i, [128, 8, 2, NCOL]) for i in range(2)]
                        CAR = [cx.sb(bs_, "CAR%d" % i, [128, 15, 8, 2]) for i in range(2)]
                        A36 = [cx.sb(bs_, "A36_%d" % i, [128, 8, 2]) for i in range(2)]
                        B36 = [cx.sb(bs_, "B36_%d" % i, [128, 8, 2]) for i in range(2)]
                        y8b = cx.sb(bs_, "y8b", [128, 8, 256])
                        ysb = cx.sb(bs_, "ysb", [128, 512])
                        TT1 = [cx.sb(bs_, "TT1_%d" % i, [128, 17, 8, 2]) for i in range(2)]
                        TT2 = [cx.sb(bs_, "TT2_%d" % i, [128, 17, 8, 2]) for i in range(2)]
                        for (j0, nj) in ((0, 32), (32, 128), (160, 128)):
                            s.dma(u8b[0:nj, :, :], U.view(8 * j0 * 1024 + 256 * b, [[8192, nj], [1024, 8], [1, 256]]))
                            s.copy(u8g.view(0, [[2048, nj], [128, 16], [16, 8], [1, 16]]),
                                   u8b.view(0, [[2048, nj], [16, 16], [256, 8], [1, 16]]), eng="act")
                            for gq4 in range(4):
                                bk = nbank()
                                for q4 in range(4):
                                    gi = gq4 * 4 + q4
                                    s.transpose(bk[:, q4 * 128:q4 * 128 + nj],
                                                u8g.view(128 * gi, [[2048, nj], [1, 128]]), cst[0:nj, 0, 0:nj])
                                s.copy(U8T.view(gq4 * 4 * 288 + j0, [[16 * 288, 128], [288, 4], [1, nj]]),
                                       bk.view(0, [[512, 128], [128, 4], [1, nj]]), eng="act")
                        if L1S in ('d', 'd1'):
                            s.flush()
                            continue
                        s.memset(SSD[0].view(0, [[PS, 128], [NCOL, 16], [1, 1]]), 0.0)
                        s.memset(SSD[0].view(289, [[PS, 128], [NCOL, 16], [1, 37]]), 0.0)
                        s.memset(SSD[1].view(288, [[PS, 128], [NCOL, 16], [1, 38]]), 0.0)
                        s.memset(SSD[0].view(289, [[PS, 128], [2 * NCOL, 8], [1, 1]]), 1.0)
                        s.memset(SSD[1].view(288 + 18 - 1, [[PS, 128], [2 * NCOL, 8], [1, 1]]), 1.0)
                        for gq in range(8):
                            for gh in range(2):
                                gi = 2 * gq + gh
                                p0 = gh * 64
                                for d_ in range(2):
                                    for ri in range(2):
                                        bk = nbank()
                                        s.mm(bk[p0:p0 + 64, 0:288],
                                             [(WST.view((((gq * 2 + gh) * 2 + d_) * 2 + ri) * 64, [[4096, 128], [1, 64]]),
                                               U8T[:, gi, :])])
                                        so = p0 * PS + (gq * 2 + ri) * NCOL
                                        if d_ == 0:
                                            s.copy(SSD[0].view(so + 1, [[PS, 64], [1, 288]]), bk[p0:p0 + 64, 0:288], eng="act")
                                        else:
                                            s.copy(SSD[1].view(so + 256, [[PS, 64], [1, 32]]), bk[p0:p0 + 64, 0:32], eng="act")
                                            s.copy(SSD[1].view(so, [[PS, 64], [1, 256]]), bk[p0:p0 + 64, 32:288], eng="act")
                        if L1S in ('e', 'e1'):
                            s.flush()
                            continue
                        DS = 8 * 2 * 289
                        setup_part2()
                        RI, GQ = NCOL, 2 * NCOL
                        SEG, NSEG = 18, 16
                        REC_ENG2 = os.environ.get('REC2', 'dve')

                        def cplx_step(items):
                            engs = ("dve", REC_ENG2)
                            for n_, (pv, psw, cv, ca, cb_, t1_, t2_) in enumerate(items):
                                s.tt(t1_, pv, ca, ALU.mult, eng=engs[n_ % 2])
                                s.tt(t2_, psw, cb_, ALU.mult, eng=engs[n_ % 2])
                            for n_, (pv, psw, cv, ca, cb_, t1_, t2_) in enumerate(items):
                                s.tt(t1_, t1_, t2_, ALU.add, eng=engs[n_ % 2])
                            for n_, (pv, psw, cv, ca, cb_, t1_, t2_) in enumerate(items):
                                if cv is not None:
                                    s.tt(cv, cv, t1_, ALU.add, eng=engs[n_ % 2])

                        def segv(SS, col, nseg):
                            return (SS.view(col, [[PS, 128], [SEG, nseg], [GQ, 8], [RI, 2]]),
                                    SS.view(col + RI, [[PS, 128], [SEG, nseg], [GQ, 8], [-RI, 2]]))

                        def coef(buf, d_, nseg):
                            return buf.view(d_ * 64 + g0 * 2, [[128, 128], [0, nseg], [2, 8], [1, 2]])

                        TTP = 17 * 16

                        for k in range(1, SEG):
                            items = []
                            for d_ in range(2):
                                pc = k if d_ == 0 else SEG - k
                                cc = k + 1 if d_ == 0 else SEG - 1 - k
                                pv, psw = segv(SSD[d_], pc, NSEG + 1)
                                cv, _ = segv(SSD[d_], cc, NSEG + 1)
                                items.append((pv, psw, cv, coef(LA, d_, NSEG + 1), coef(LB, d_, NSEG + 1), TT1[d_].full(), TT2[d_].full()))
                            cplx_step(items)
                        items = []
                        for d_ in range(2):
                            clast = 288 + SEG if d_ == 0 else 288
                            pv, psw = segv(SSD[d_], clast, 1)
                            items.append((pv, psw, None, coef(LA, d_, 1), coef(LB, d_, 1),
                                          TT1[d_].view(0, [[TTP, 128], [16, 1], [2, 8], [1, 2]]),
                                          TT2[d_].view(0, [[TTP, 128], [16, 1], [2, 8], [1, 2]])))
                        cplx_step(items)
                        for d_ in range(2):
                            l36re = TT1[d_].view(0, [[TTP, 128], [2, 8], [0, 2]])
                            s.copy(A36[d_].full(), l36re)
                            s.ts(B36[d_][:, :, 0:1], TT1[d_].view(1, [[TTP, 128], [2, 8], [1, 1]]), -1.0, None, ALU.mult)
                            s.copy(B36[d_][:, :, 1:2], TT1[d_].view(1, [[TTP, 128], [2, 8], [1, 1]]))
                        for step in range(1, NSEG):
                            items = []
                            for d_ in range(2):
                                if d_ == 0:
                                    m = step
                                    cc, pc = SEG * m + SEG, SEG * m
                                else:
                                    m = NSEG - 1 - step
                                    cc, pc = SEG * m, SEG * m + SEG
                                pv, psw = segv(SSD[d_], pc, 1)
                                cv, _ = segv(SSD[d_], cc, 1)
                                items.append((pv, psw, cv,
                                              A36[d_].view(0, [[16, 128], [0, 1], [2, 8], [1, 2]]),
                                              B36[d_].view(0, [[16, 128], [0, 1], [2, 8], [1, 2]]),
                                              TT1[d_].view(0, [[TTP, 128], [16, 1], [2, 8], [1, 2]]),
                                              TT2[d_].view(0, [[TTP, 128], [16, 1], [2, 8], [1, 2]])))
                            cplx_step(items)
                        items = []
                        for d_ in range(2):
                            pv, psw = segv(SSD[d_], SEG, NSEG - 1)
                            items.append((pv, psw, None, coef(LA, d_, NSEG - 1), coef(LB, d_, NSEG - 1),
                                          CAR[d_].full(), TT2[d_].view(0, [[TTP, 128], [16, NSEG - 1], [2, 8], [1, 2]])))
                        cplx_step(items)
                        NI = SEG - 1
                        for d_ in range(2):
                            SS = SSD[d_]
                            sb0 = SEG + 1 if d_ == 0 else 1

                            def sview(ri):
                                return SS.view(sb0 + ri * RI, [[PS, 128], [SEG, NSEG - 1], [GQ, 8], [1, NI]])

                            def tview(ri):
                                return SS.view(289 + ri * RI, [[PS, 128], [0, NSEG - 1], [GQ, 8], [1, NI]])

                            def cview(ri):
                                return CAR[d_].view(ri, [[(NSEG - 1) * 16, 128], [16, NSEG - 1], [2, 8], [0, NI]])

                            wshape = [[2048, 128], [8 * NI, NSEG - 1], [NI, 8], [1, NI]]
                            w1 = (u8g if d_ == 0 else u8b).view(0, wshape)
                            w2 = y8b.view(0, wshape)
                            s.tt(w1, tview(0), cview(0), ALU.mult)
                            s.tt(w2, tview(1), cview(1), ALU.mult)
                            s.tt(w1, w1, w2, ALU.subtract)
                            s.tt(sview(0), sview(0), w1, ALU.add)
                            s.tt(w1, tview(0), cview(1), ALU.mult)
                            s.tt(w2, tview(1), cview(0), ALU.mult)
                            s.tt(w1, w1, w2, ALU.add)
                            s.tt(sview(1), sview(1), w1, ALU.add)
                        s.copy(SSb[0].full(), SSD[0].view(32, [[PS, 128], [NCOL, 16], [1, 256]]), eng="act")
                        s.copy(SSb[1].full(), SSD[1].view(1, [[PS, 128], [NCOL, 16], [1, 256]]), eng="pool")
                        if L1S in ('f', 'f1'):
                            s.flush()
                            continue
                        for tt_ in range(2):
                            j0 = 32 + 128 * tt_
                            m0 = 128 * tt_
                            for gh in range(2):
                                p0 = gh * 64
                                for gqq in range(2):
                                    bx = nbank()
                                    by = nbank()
                                    for q4 in range(4):
                                        gq = gqq * 4 + q4
                                        gi = 2 * gq + gh
                                        s.mm(bx[:, q4 * 128:(q4 + 1) * 128],
                                             [(U8T[:, gi, j0:j0 + 128], TF[:, gi, :]), (U8T[:, gi, j0:j0 + 128], TB[:, gi, :])])
                                        pairs = []
                                        for d_ in range(2):
                                            c0 = m0
                                            i0 = 1 if d_ == 0 else 0
                                            for ri in range(2):
                                                so = p0 * 4096 + (gq * 2 + ri) * 256 + c0
                                                pairs.append((SSb[d_].view(so, [[4096, 64], [1, 128]]),
                                                              CABb.view(p0 * 4608 + (ri * 2 + d_) * 1152 + gq * 144 + i0 * 16, [[4608, 64], [1, 128]])))
                                        s.mm(by[:, q4 * 128:(q4 + 1) * 128], pairs)
                                    s.copy(ysb.full(), by.full(), eng="act")
                                    s.tt(y8b.view(32 * gqq * 4 + 16 * gh, [[2048, 128], [32, 4], [256, 8], [1, 16]]),
                                         bx.view(0, [[512, 128], [128, 4], [16, 8], [1, 16]]),
                                         ysb.view(0, [[512, 128], [128, 4], [16, 8], [1, 16]]), ALU.add)
                            s.dma(YTOK.view((CTX + 8 * m0) * 1024 + 256 * b, [[8192, 128], [1024, 8], [1, 256]]), y8b.full())
                        s.flush()

            if L1S in ('g', 'g1'):
                return nc
            with ExitStack() as es:
                gw = [cx.sb(es, "gw%d" % k, [128, D], BF16) for k in range(8)]
                ow = [cx.sb(es, "ow%d" % k, [128, D], BF16) for k in range(8)]
                dskb = cx.sb(es, "dskb", [128, D])
                glbb = cx.sb(es, "glbb", [128, D])
                fnwb = cx.sb(es, "fnwb", [128, D])
                kg = cx.sb(es, "kg", [128, 1])
                s.memset(kg.full(), 2.0 * math.sqrt(2.0 / math.pi))
                kmh = cx.sb(es, "kmh", [128, 1])
                s.memset(kmh.full(), -0.5)
                for k in range(8):
                    s.dma(gw[k].full(), o_glu_w[k * 128:(k + 1) * 128, :], q="pool")
                    s.dma(ow[k].full(), o_w_out[k * 128:(k + 1) * 128, :], q="pool")
                s.dma(dskb.full(), o_d_skip.view(0, [[0, 128], [1, D]]))
                s.dma(glbb.full(), o_glu_b.view(0, [[0, 128], [1, D]]))
                s.dma(fnwb.full(), final_norm_w.view(0, [[0, 128], [1, D]]))
                NB3 = 4
                ya = [cx.sb(es, "ya%d" % i, [128, D]) for i in range(NB3)]
                ua = [cx.sb(es, "ua%d" % i, [128, D]) for i in range(NB3)]
                sga = [cx.sb(es, "sga%d" % i, [128, D]) for i in range(NB3)]
                xa = [cx.sb(es, "xa%d" % i, [128, D]) for i in range(NB3)]
                w1s = [cx.sb(es, "w1_%d" % i, [128, D]) for i in range(NB3)]
                w2s = [cx.sb(es, "w2_%d" % i, [128, D]) for i in range(NB3)]
                w3s = [cx.sb(es, "w3_%d" % i, [128, D]) for i in range(NB3)]
                tTs = [cx.sb(es, "tT_%d" % i, [128, 8, 128], BF16) for i in range(2 * NB3)]
                sts = [cx.sb(es, "st10_%d" % i, [128, 4]) for i in range(NB3)]

                def transp8(src, tT):
                    for half in range(2):
                        bk = nbank()
                        for kk in range(4):
                            k = half * 4 + kk
                            s.transpose(bk[:, kk * 128:(kk + 1) * 128], src[:, k * 128:(k + 1) * 128], ident)
                        s.copy(tT[:, half * 4:(half + 1) * 4, :], bk.view(0, [[512, 128], [128, 4], [1, 128]]), eng="act")

                TAILN = int(os.environ.get('TAILN', NT))

                def bufs(i):
                    b_ = i % NB3
                    return ya[b_], ua[b_], sga[b_], xa[b_], w1s[b_], w2s[b_], w3s[b_], tTs[2 * b_], tTs[2 * b_ + 1], sts[b_]

                def stageL(i):
                    y_, u_, g_, x_, w1, w2, w3, tTa, tTb, st = bufs(i)
                    s.dma(y_.full(), YTOK[i * 128:(i + 1) * 128, :])
                    s.dma(u_.full(), U[i * 128:(i + 1) * 128, :])
                    s.dma(g_.full(), SG1[i * 128:(i + 1) * 128, :])
                    s.dma(x_.full(), X1[i * 128:(i + 1) * 128, :])

                def stage0(i):
                    y_, u_, g_, x_, w1, w2, w3, tTa, tTb, st = bufs(i)
                    s.tt(w1.full(), u_.full(), dskb.full(), ALU.mult)
                    s.tt(y_.full(), y_.full(), w1.full(), ALU.add)
                    s.tt(w1.full(), y_.full(), y_.full(), ALU.mult)
                    s.ts(w1.full(), w1.full(), 0.044715, 1.0, ALU.mult, ALU.add)
                    s.tt(w1.full(), w1.full(), y_.full(), ALU.mult)
                    s.act(w1.full(), w1.full(), AF.Sigmoid, scale=kg[:, 0:1])
                    s.tt(w2.full(), y_.full(), w1.full(), ALU.mult)
                    transp8(w2, tTa)

                def stage1(i):
                    y_, u_, g_, x_, w1, w2, w3, tTa, tTb, st = bufs(i)
                    for half in range(2):
                        bk = nbank()
                        s.mm(bk.full(), [(tTa[:, k, :], gw[k][:, half * 512:(half + 1) * 512]) for k in range(8)])
                        s.tt(w1[:, half * 512:(half + 1) * 512], bk.full(), glbb[:, half * 512:(half + 1) * 512], ALU.add)
                    s.act(w1.full(), w1.full(), AF.Sigmoid)
                    s.tt(w2.full(), w2.full(), w1.full(), ALU.mult)
                    s.tt(w2.full(), w2.full(), g_.full(), ALU.mult)
                    transp8(w2, tTb)

                def stage2(i):
                    y_, u_, g_, x_, w1, w2, w3, tTa, tTb, st = bufs(i)
                    for half in range(2):
                        bk = nbank()
                        s.mm(bk.full(), [(tTb[:, k, :], ow[k][:, half * 512:(half + 1) * 512]) for k in range(8)])
                        s.tt(w1[:, half * 512:(half + 1) * 512], bk.full(), gate_bc[1][0][:, half * 512:(half + 1) * 512], ALU.mult)
                    s.tt(w3.full(), w1.full(), x_.full(), ALU.add)
                    s.act(w1.full(), w3.full(), AF.Square, accum=st[:, 0:1])
                    s.ts(st[:, 1:2], st[:, 0:1], 1.0 / D, EPS, ALU.mult, ALU.add)
                    s.tt(st[:, 3:4], st[:, 1:2], kmh.full(), ALU.pow, eng="pool")
                    s.act(w3.full(), w3.full(), AF.Copy, scale=st[:, 3:4])
                    s.tt(w2.full(), w3.full(), fnwb.full(), ALU.mult)
                    s.dma(out_t[(i - 2) * 128:(i - 1) * 128, :], w2.full(), q="pool")

                pipeline(list(range(2, TAILN)), [stageL, stage0, stage1, stage2])
                s.flush()

    return nc


def _consts():
    c = np.zeros((128, 6, 512), np.float32)
    j = np.arange(128)[:, None]
    l = np.arange(128)[None, :]
    c[:, 0, :128] = np.eye(128, dtype=np.float32)
    c[0, 0, 128:256] = 1.0
    c[1, 0, 256:384] = 1.0
    c[:, 1, :128] = (j <= l)
    c[:, 2, :128] = (j >= l)
    c[:, 3, :] = 1.0
    nf = np.where(l < j, -30000.0, 0.0).astype(np.float32)
    nb = np.where(l > j, -30000.0, 0.0).astype(np.float32)
    c[:, 4, :] = np.tile(nf, (1, 4))
    c[:, 5, :] = np.tile(nb, (1, 4))
    return c


def _rope_tables():
    rows = L // 64
    row = np.repeat(np.arange(rows, dtype=np.float32), 64)
    col = np.tile(np.arange(64, dtype=np.float32), rows)
    n_freq = 16
    inv = (np.float32(10000.0) ** (-np.arange(n_freq, dtype=np.float32) / n_freq)).astype(np.float32)
    ang = np.concatenate([row[:, None] * inv, col[:, None] * inv], axis=-1).astype(np.float32)
    cos = np.cos(ang).astype(np.float32)
    sin = np.sin(ang).astype(np.float32)
    cosT = np.zeros((128, L), np.float32)
    sinT = np.zeros((128, L), np.float32)
    for h2 in range(2):
        for half in range(2):
            p0 = h2 * 64 + half * 32
            cosT[p0:p0 + 32] = cos.T
            sinT[p0:p0 + 32] = (-sin.T if half == 0 else sin.T)
    return np.stack([cosT, sinT], axis=1)


def _vecT(v, nchunk):
    return np.ascontiguousarray(np.asarray(v, np.float32).reshape(nchunk, 128).T)


def prep_inputs(b, inp):
    f = lambda a: np.ascontiguousarray(np.asarray(a, np.float32))
    m = {}
    m["xin"] = f(np.concatenate([inp["ctx"][b], inp["x"][b]], axis=0))
    cv = np.stack([inp["c"][b], inp["c_ctx"]], axis=0)
    m["cvecT"] = f(cv.reshape(2, 8, 128).transpose(2, 0, 1))
    m["consts"] = _consts()
    m["rope"] = _rope_tables()
    m["e_ada_w"] = f(inp["e_ada_w"][0])
    m["e_ada_b"] = f(inp["e_ada_b"][0]).reshape(1, -1)
    m["e_norm_wT"] = _vecT(inp["e_norm_w"][0], 8)
    w = f(inp["e_w_in"][0])
    q = w[:, OFF_Q:OFF_Q + 1024].reshape(D, 16, 2, 32)
    qs = q[:, :, ::-1, :].reshape(D, 1024)
    k = w[:, OFF_KV:OFF_KV + 256].reshape(D, 4, 64)
    kr = np.concatenate([k, k], axis=2).reshape(D, 512)
    ks = k.reshape(D, 4, 2, 32)[:, :, ::-1, :].reshape(D, 4, 64)
    ksr = np.concatenate([ks, ks], axis=2).reshape(D, 512)
    m["e_w_in"] = f(np.concatenate([w, qs, kr, ksr], axis=1))
    cw = f(inp["e_conv_w"][0])
    m["e_conv_wT"] = f(cw.reshape(5, 12, 128).transpose(2, 1, 0))
    m["e_conv_bT"] = _vecT(inp["e_conv_b"][0], 12)
    m["e_dt_bias"] = f(inp["e_dt_bias"][0]).reshape(1, 32)
    m["e_a_log"] = f(inp["e_a_log"][0]).reshape(1, 32)
    m["e_d_skip"] = f(inp["e_d_skip"][0]).reshape(1, 16)
    m["e_ssd_norm_wT"] = _vecT(inp["e_ssd_norm_w"][0], 8)
    sk = f(inp["e_sink"][0]).reshape(8, 2)
    m["e_sink"] = f(np.repeat(sk.T[:, None, :], 64, axis=1).reshape(128, 8))
    m["e_w_out"] = f(inp["e_w_out"][0])
    m["o_ada_w"] = f(inp["o_ada_w"][0])
    m["o_ada_b"] = f(inp["o_ada_b"][0]).reshape(1, -1)
    m["o_norm_wT"] = _vecT(inp["o_norm_w"][0], 8)
    m["o_w_in"] = f(inp["o_w_in"][0])

    def gl(a):
        a = np.asarray(a, np.float32)
        rest = a.shape[2:]
        a = a.reshape((32, 2, 64) + rest)
        a = np.moveaxis(a, 0, 2)
        return a.reshape((128, 32) + rest)

    lam = np.zeros((128, 2, 3, 32), np.float32)
    for d_ in range(2):
        lam[:, d_, 0] = gl(inp["o_lam_re"][0][d_])
        lam[:, d_, 1] = gl(inp["o_lam_im"][0][d_])
        lam[:, d_, 2] = gl(np.repeat(np.asarray(inp["o_log_step"][0][d_])[:, None], 64, axis=1))
    m["s5_lam"] = f(lam)
    m["s5_b"] = f(np.stack([gl(inp["o_b_re"][0]), gl(inp["o_b_im"][0])], axis=1))
    cr = np.asarray(inp["o_c_re"][0]).transpose(0, 2, 1)
    ci = np.asarray(inp["o_c_im"][0]).transpose(0, 2, 1)
    m["s5_c"] = f(np.stack([gl(cr), gl(ci)], axis=1))
    m["o_d_skip"] = f(inp["o_d_skip"][0]).reshape(1, -1)
    m["o_glu_w"] = f(inp["o_glu_w"][0])
    m["o_glu_b"] = f(inp["o_glu_b"][0]).reshape(1, -1)
    m["o_w_out"] = f(inp["o_w_out"][0])
    m["final_norm_w"] = f(inp["final_norm_w"]).reshape(1, -1)
    return m


def kernel(**inputs):
    nc = build_program()
    in_maps = [prep_inputs(b, inputs) for b in range(8)]
    res = run_bass_kernel_spmd(nc, in_maps, core_ids=list(range(8)))
    return np.stack([r["out"] for r in res.results], axis=0)
```

```python
import math
import os
from contextlib import ExitStack

import numpy as np
import concourse.bass as bass
import concourse.mybir as mybir
from concourse.bass_utils import run_bass_kernel_spmd

F32 = mybir.dt.float32
BF16 = mybir.dt.bfloat16
AF = mybir.ActivationFunctionType
ALU = mybir.AluOpType

D = 1024
T = 2304
NT = 18
CTX = 256
L = 2048
EPS = 1e-6
TG = [(0, 256), (256, 512), (768, 512), (1280, 512), (1792, 512)]

SES_ALL = os.environ.get('SES', '0') == '1'
SAME_ENGINE_SYNC = {'act': SES_ALL, 'dve': SES_ALL, 'pool': True, 'pe': False, 'sp': True}
SEM_EPOCH = 30000


class V:
    __slots__ = ("buf", "ap")

    def __init__(self, buf, ap):
        self.buf = buf
        self.ap = ap


class Buf:
    def __init__(self, name, h):
        self.name = name
        self.h = h
        self.last_w = None
        self.readers = []
        self.is_psum = False

    def __getitem__(self, idx):
        return V(self, self.h[idx])

    def full(self):
        return V(self, self.h.ap())

    def view(self, offset, pattern):
        return V(self, bass.AP(self.h, offset, [list(p) for p in pattern]))


class Sched:
    ENG = ("pe", "act", "dve", "pool", "sp")

    def __init__(self, nc):
        self.nc = nc
        self.prog = {e: [] for e in self.ENG}
        self.sem = {}
        self.cnt = {}
        self.semid = 0
        self.known = {e: {} for e in self.ENG}
        for e in ("pe", "act", "dve", "pool"):
            self._new_engine_sem(e)
        self.nds = 8
        self.dsem = {}
        self.duse = {}
        self.dcnt = {}
        for q in ("sp", "pool"):
            self.dsem[q] = []
            self.duse[q] = []
            for i in range(self.nds):
                key = "d_%s_%d" % (q, i)
                self.dsem[q].append((nc.alloc_semaphore(key), key))
                self.duse[q].append(0)
            self.dcnt[q] = 0
        self.n_ops = 0

    def _new_engine_sem(self, e):
        self.semid += 1
        key = "s_%s_%d" % (e, self.semid)
        self.sem[e] = (self.nc.alloc_semaphore(key), key)
        self.cnt[e] = 0

    def _deps(self, reads, writes):
        deps = {}

        def add(tok):
            if tok is None:
                return
            h, key, val = tok
            if key not in deps or deps[key][1] < val:
                deps[key] = (h, val)

        for r in reads:
            add(r.buf.last_w)
            if r.buf.is_psum:
                for t in r.buf.readers:
                    add(t)
        for w in writes:
            add(w.buf.last_w)
            for t in w.buf.readers:
                add(t)
        return deps

    def _emit_waits(self, eng, deps, own_key=None):
        kn = self.known[eng]
        for key, (h, val) in deps.items():
            if key == own_key and not SAME_ENGINE_SYNC[eng]:
                continue
            if kn.get(key, 0) >= val:
                continue
            kn[key] = val
            self.prog[eng].append(("wait", h, val))

    def _update(self, tok, reads, writes):
        for w in writes:
            w.buf.last_w = tok
            w.buf.readers = []
        for r in reads:
            if r.buf.last_w is not tok:
                r.buf.readers.append(tok)

    def op(self, eng, fn, reads=(), writes=()):
        reads = [r for r in reads if r is not None]
        writes = list(writes)
        if self.cnt[eng] >= SEM_EPOCH:
            self._new_engine_sem(eng)
        h, key = self.sem[eng]
        own = None if eng == "pe" else key
        deps = self._deps(reads, writes)
        if eng == "pe":
            deps.pop(key, None)
        self._emit_waits(eng, deps, own_key=own)
        self.cnt[eng] += 1
        self.prog[eng].append(("op", fn, h, 1))
        tok = (h, key, self.cnt[eng])
        self._update(tok, reads, writes)
        self.n_ops += 1
        return tok

    def dma(self, out, in_, q="sp", **kw):
        deps = self._deps([in_], [out])
        self._emit_waits(q, deps)
        k = self.dcnt[q] % self.nds
        self.dcnt[q] += 1
        h, key = self.dsem[q][k]
        prev = 16 * self.duse[q][k]
        if prev > 0 and self.known[q].get(key, 0) < prev:
            self.known[q][key] = prev
            self.prog[q].append(("wait", h, prev))
        self.duse[q][k] += 1
        val = 16 * self.duse[q][k]
        o_ap, i_ap = out.ap, in_.ap
        self.prog[q].append(("op", lambda e: e.dma_start(out=o_ap, in_=i_ap, **kw), h, 16))
        tok = (h, key, val)
        self._update(tok, [in_], [out])
        self.n_ops += 1
        return tok

    def finish_dmas(self):
        for q in ("sp", "pool"):
            for k in range(self.nds):
                h, key = self.dsem[q][k]
                val = 16 * self.duse[q][k]
                if val > 0 and self.known[q].get(key, 0) < val:
                    self.known[q][key] = val
                    self.prog[q].append(("wait", h, val))

    def flush(self, name=None):
        self.finish_dmas()
        nc = self.nc
        prog = self.prog
        self.prog = {e: [] for e in self.ENG}

        def run(items, e):
            for it in items:
                if it[0] == "wait":
                    e.wait_ge(it[1], it[2])
                else:
                    inst = it[1](e)
                    inst.then_inc(it[2], it[3])

        with nc.Block() as block:
            if prog["sp"]:
                @block.sync
                def _(e):
                    run(prog["sp"], e)
            if prog["act"]:
                @block.scalar
                def _(e):
                    run(prog["act"], e)
            if prog["dve"]:
                @block.vector
                def _(e):
                    run(prog["dve"], e)
            if prog["pool"]:
                @block.gpsimd
                def _(e):
                    run(prog["pool"], e)
            if prog["pe"]:
                @block.tensor
                def _(e):
                    run(prog["pe"], e)

    def mm(self, out, pairs):
        n = len(pairs)

        def fn(e):
            inst = None
            for i, (l, r) in enumerate(pairs):
                inst = e.matmul(out.ap, l.ap, r.ap, start=(i == 0), stop=(i == n - 1))
            return inst

        self.op("pe", fn, reads=[p[0] for p in pairs] + [p[1] for p in pairs], writes=[out])

    def mm1(self, out, l, r, start, stop):
        self.op("pe", lambda e: e.matmul(out.ap, l.ap, r.ap, start=start, stop=stop), reads=[l, r], writes=[out])

    def transpose(self, out, in_, ident):
        self.op("pe", lambda e: e.transpose(out.ap, in_.ap, ident.ap), reads=[in_, ident], writes=[out])

    def act(self, out, in_, func, bias=None, scale=None, accum=None):
        kw = {}
        reads = [in_]
        writes = [out]
        if bias is not None:
            if isinstance(bias, V):
                kw["bias"] = bias.ap
                reads.append(bias)
            else:
                kw["bias"] = bias
        if scale is not None:
            if isinstance(scale, V):
                kw["scale"] = scale.ap
                reads.append(scale)
            else:
                kw["scale"] = scale
        if accum is not None:
            kw["accum_out"] = accum.ap
            writes.append(accum)
        self.op("act", lambda e: e.activation(out.ap, in_.ap, func, **kw), reads=reads, writes=writes)

    def ts(self, out, in0, s1, s2, op0, op1=None, eng="dve"):
        reads = [in0]
        a1 = s1
        a2 = s2
        if isinstance(s1, V):
            reads.append(s1)
            a1 = s1.ap
        if isinstance(s2, V):
            reads.append(s2)
            a2 = s2.ap
        if op1 is None:
            self.op(eng, lambda e: e.tensor_scalar(out.ap, in0.ap, a1, a2, op0), reads=reads, writes=[out])
        else:
            self.op(eng, lambda e: e.tensor_scalar(out.ap, in0.ap, a1, a2, op0, op1), reads=reads, writes=[out])

    def tt(self, out, in0, in1, op, eng="dve"):
        self.op(eng, lambda e: e.tensor_tensor(out.ap, in0.ap, in1.ap, op), reads=[in0, in1], writes=[out])

    def stt(self, out, in0, scalar, in1, op0, op1):
        reads = [in0, in1]
        sc = scalar
        if isinstance(scalar, V):
            reads.append(scalar)
            sc = scalar.ap
        self.op("dve", lambda e: e.scalar_tensor_tensor(out.ap, in0.ap, sc, in1.ap, op0, op1),
                reads=reads, writes=[out])

    def copy(self, out, in_, eng="dve"):
        if eng == "act":
            self.op("act", lambda e: e.copy(out.ap, in_.ap), reads=[in_], writes=[out])
        else:
            self.op(eng, lambda e: e.tensor_copy(out.ap, in_.ap), reads=[in_], writes=[out])

    def recip(self, out, in_):
        self.op("dve", lambda e: e.reciprocal(out.ap, in_.ap), reads=[in_], writes=[out])

    def memset(self, out, val, eng="dve"):
        self.op(eng, lambda e: e.memset(out.ap, val), reads=[], writes=[out])


class Ctx:
    def __init__(self, nc, sched):
        self.nc = nc
        self.s = sched
        self.uid = 0

    def sb(self, es, name, shape, dtype=F32):
        self.uid += 1
        h = es.enter_context(self.nc.sbuf_tensor("%s_%d" % (name, self.uid), list(shape), dtype))
        return Buf(name, h)

    def ps(self, es, name, shape=(128, 512), dtype=F32):
        self.uid += 1
        h = es.enter_context(self.nc.psum_tensor("%s_%d" % (name, self.uid), list(shape), dtype))
        b = Buf(name, h)
        b.is_psum = True
        return b

    def dram(self, name, shape, dtype=F32, kind="Internal"):
        h = self.nc.dram_tensor(name, list(shape), dtype, kind=kind)
        return Buf(name, h)


def pipeline(items, stages):
    n, k = len(items), len(stages)
    for t in range(n + k - 1):
        for j in range(k - 1, -1, -1):
            i = t - j
            if 0 <= i < n:
                stages[j](items[i])


def bc_mid(v_buf, base_off, pstep, nparts, n_outer, outer_step, n_inner):
    return v_buf.view(base_off, [[pstep, nparts], [outer_step, n_outer], [0, n_inner]])


E_NCOL = 5152
OFF_Z = 0
OFF_XBC = 1024
OFF_DT = 2560
OFF_Q = 2592
OFF_KV = 3616
OFF_G = 4128
OFF_QS = 5152
OFF_KR = 6176
OFF_KSR = 6688
E_NCOL_EXT = 7200


ORDER = ["p1", "p2a", "p2b", "p2c", "p2d", "p2e", "p2f", "p2g", "p2h", "p3", "p4", "p5", "all"]


def build_program(debug=(), stop="all"):
    def go(tag):
        return ORDER.index(tag) <= ORDER.index(stop)
    nc = bass.Bass("TRN2", target_bir_lowering=False)
    s = Sched(nc)
    cx = Ctx(nc, s)
    dbg = set(debug)

    def din(name, shape):
        return Buf(name, nc.dram_tensor(name, list(shape), F32, kind="ExternalInput"))

    def dout(name, shape):
        return Buf(name, nc.dram_tensor(name, list(shape), F32, kind="ExternalOutput"))

    def scratch(name, shape, dtype=F32):
        if name in dbg:
            return dout(name, shape)
        return Buf(name, nc.dram_tensor(name, list(shape), dtype))

    xin = din("xin", [T, D])
    cvecT = din("cvecT", [128, 2, 8])
    consts = din("consts", [128, 6, 512])
    rope = din("rope", [128, 2, L])
    e_ada_w = din("e_ada_w", [D, 3 * D])
    e_ada_b = din("e_ada_b", [1, 3 * D])
    e_norm_wT = din("e_norm_wT", [128, 8])
    e_w_in = din("e_w_in", [D, E_NCOL_EXT])
    e_conv_wT = din("e_conv_wT", [128, 12, 5])
    e_conv_bT = din("e_conv_bT", [128, 12])
    e_dt_bias = din("e_dt_bias", [1, 32])
    e_a_log = din("e_a_log", [1, 32])
    e_d_skip = din("e_d_skip", [1, 16])
    e_ssd_norm_wT = din("e_ssd_norm_wT", [128, 8])
    e_sink = din("e_sink", [128, 8])
    e_w_out = din("e_w_out", [2 * D, D])
    o_ada_w = din("o_ada_w", [D, 3 * D])
    o_ada_b = din("o_ada_b", [1, 3 * D])
    o_norm_wT = din("o_norm_wT", [128, 8])
    o_w_in = din("o_w_in", [D, 2 * D])
    s5_lam = din("s5_lam", [128, 2, 3, 32])
    s5_b = din("s5_b", [128, 2, 32, 16])
    s5_c = din("s5_c", [128, 2, 32, 16])
    o_d_skip = din("o_d_skip", [1, D])
    o_glu_w = din("o_glu_w", [D, D])
    o_glu_b = din("o_glu_b", [1, D])
    o_w_out = din("o_w_out", [D, D])
    final_norm_w = din("final_norm_w", [1, D])
    out_t = dout("out", [L, D])

    XS = scratch("XS", [T, 1024])
    BTOK = scratch("BTOK", [T, 256], BF16)
    BT = scratch("BT", [2, 128, T], BF16)
    CT = scratch("CT", [2, 128, T], BF16)
    SZ = scratch("SZ", [T, 1024])
    QR = scratch("QR", [8, 128, L], BF16)
    QC = scratch("QC", [8, 128, CTX], BF16)
    KR = scratch("KR", [4, 128, L], BF16)
    KC = scratch("KC", [4, 128, CTX], BF16)
    VT = scratch("VT", [T, 256], BF16)
    SG = scratch("SG", [8, 128, T])
    YF = scratch("YF", [T, 1024])
    YT = scratch("YT", [16, 128, T], BF16)
    X1 = scratch("X1", [T, 1024])
    U = scratch("U", [T, 1024])
    SG1 = scratch("SG1", [T, 1024])
    YTOK = scratch("YTOK", [T, 1024])
    KFP = scratch("KFP", [64, 16, 15, 16], BF16)
    KBR = scratch("KBR", [64, 16, 15, 16], BF16)
    HT = scratch("HT", [8, 128, T]) if "HT" in dbg else None
    DTD = scratch("DTD", [T, 32]) if "DTD" in dbg else None
    MODD = scratch("MODD", [4, 128, 24]) if "MODD" in dbg else None

    with ExitStack() as top:
        banks = [cx.ps(top, "bank%d" % i) for i in range(8)]
        cst = cx.sb(top, "cst", [128, 6, 512])
        s.dma(cst.full(), consts.full())
        ident = cst[:, 0, 0:128]
        tri = cst[:, 1, 0:128]
        utri = cst[:, 2, 0:128]
        ones = cst[:, 3, 0:128]
        onesb_t = cx.sb(top, "onesb", [128, 128], BF16)
        s.memset(onesb_t.full(), 1.0)
        onesb = onesb_t.full()
        modT = [[cx.sb(top, "modT%d%d" % (l, w), [128, 24]) for w in range(2)] for l in range(2)]
        gate_bc = [[cx.sb(top, "gate%d%d" % (l, w), [128, 1024]) for w in range(2)] for l in range(2)]
        scs = cx.sb(top, "scs", [128, 2, 8])

        def adaln_phase(layer, ada_w, ada_b):
            with ExitStack() as es:
                aw = [cx.sb(es, "aw%d" % k, [128, 3 * D]) for k in range(8)]
                ab2 = cx.sb(es, "ab2", [2, 3 * D])
                modrow2 = cx.sb(es, "modrow2", [2, 3 * D])
                if layer == 0:
                    cv = cx.sb(es, "cv", [128, 2, 8])
                    s.dma(cv.full(), cvecT.full())
                    s.act(scs.full(), cv.full(), AF.Silu)
                s.dma(ab2[0:1, :], ada_b.full())
                s.dma(ab2[1:2, :], ada_b.full())
                for k in range(8):
                    s.dma(aw[k].full(), ada_w[k * 128:(k + 1) * 128, :])
                for k in range(8):
                    for fg in range(6):
                        s.mm1(banks[fg][0:2, :], scs.view(k, [[16, 128], [8, 2]]), aw[k][:, fg * 512:(fg + 1) * 512],
                              start=(k == 0), stop=(k == 7))
                for fg in range(6):
                    s.tt(modrow2[0:2, fg * 512:(fg + 1) * 512], banks[fg][0:2, :], ab2[0:2, fg * 512:(fg + 1) * 512], ALU.add)
                bk = banks[6]
                for fc in range(24):
                    s.mm(bk[:, 2 * fc:2 * fc + 2], [(modrow2[0:2, fc * 128:(fc + 1) * 128], cst[0:2, 0, 0:2])])
                for w in range(2):
                    s.copy(modT[layer][w].full(), bk.view(w, [[512, 128], [2, 24]]))
                bi = 0
                for w in range(2):
                    selw = cst[0:2, 0, 128 + 128 * w:256 + 128 * w]
                    for hh in range(2):
                        bk2 = banks[(7 + bi) % 8]
                        bi += 1
                        s.mm(bk2.full(), [(selw, modrow2[0:2, 2048 + hh * 512:2048 + (hh + 1) * 512])])
                        s.copy(gate_bc[layer][w][:, hh * 512:(hh + 1) * 512], bk2.full(), eng="act")
                    if MODD is not None:
                        s.dma(MODD[layer * 2 + w], modT[layer][w].full())
                s.flush()

        adaln_phase(0, e_ada_w, e_ada_b)

        with ExitStack() as l0:
            DT = cx.sb(l0, "DT", [128, NT, 32])
            DTA = cx.sb(l0, "DTA", [128, NT, 32])
            nw = cx.sb(l0, "nw", [128, 8])
            sc1 = [cx.sb(l0, "sc1_%d" % w, [128, 8]) for w in range(2)]
            s.dma(nw.full(), e_norm_wT.full())
            for w in range(2):
                s.stt(sc1[w].full(), modT[0][w][:, 8:16], 1.0, nw.full(), ALU.add, ALU.mult)

            wo = [cx.sb(l0, "wo%d" % k, [128, D], BF16) for k in range(16)]
            hts = ExitStack()
            hT = [cx.sb(hts, "hT%d" % k, [128, T], BF16) for k in range(8)]
            with ExitStack() as es:
                xt = [cx.sb(es, "xt%d" % i, [128, D]) for i in range(3)]
                xn = [cx.sb(es, "xn%d" % i, [128, D]) for i in range(3)]
                junk = cx.sb(es, "junk", [128, D])
                st = [cx.sb(es, "st%d" % i, [128, 4]) for i in range(3)]
                def n0(i):
                    x_, n_, st_ = xt[i % 3], xn[i % 3], st[i % 3]
                    s.dma(x_.full(), xin[i * 128:(i + 1) * 128, :])
                    s.act(junk.full(), x_.full(), AF.Square, accum=st_[:, 0:1])
                    s.ts(st_[:, 1:2], st_[:, 0:1], 1.0 / D, EPS, ALU.mult, ALU.add)
                    s.act(st_[:, 2:3], st_[:, 1:2], AF.Sqrt)
                    s.recip(st_[:, 3:4], st_[:, 2:3])
                    s.ts(n_.full(), x_.full(), st_[:, 3:4], None, ALU.mult)

                def n1(i):
                    w = 1 if i < 2 else 0
                    n_ = xn[i % 3]
                    for half in range(2):
                        bk = banks[(2 * i + half) % 8]
                        for kk in range(4):
                            k = half * 4 + kk
                            s.transpose(bk[:, kk * 128:(kk + 1) * 128], n_[:, k * 128:(k + 1) * 128], ident)
                        for kk in range(4):
                            k = half * 4 + kk
                            s.act(hT[k][:, i * 128:(i + 1) * 128], bk[:, kk * 128:(kk + 1) * 128], AF.Identity,
                                  bias=modT[0][w][:, k:k + 1], scale=sc1[w][:, k:k + 1])

                pipeline(list(range(NT)), [n0, n1])
                if HT is not None:
                    for k in range(8):
                        s.dma(HT[k], hT[k].full())
                s.flush()

            with ExitStack() as es:
                WB = 256
                NWB, PF = 6, 4
                wbuf = [cx.sb(es, "wbuf%d" % i, [128, 8, WB], BF16) for i in range(NWB)]
                wplan = [(OFF_XBC + 256 * k, 256) for k in range(6)]
                for qc in range(8):
                    wplan += [(OFF_Q + qc * 128, 128), (OFF_QS + qc * 128, 128)]
                for j in range(4):
                    wplan += [(OFF_KR + j * 128, 128), (OFF_KSR + j * 128, 128)]
                wplan += [(OFF_G + 256 * k, 256) for k in range(4)]
                wplan += [(OFF_Z + 256 * k, 256) for k in range(4)]
                wplan += [(OFF_KV + 256, 256), (OFF_DT, 32)]
                wstate = {"i": 0, "issued": 0}

                def _issue(n):
                    col0, ncol = wplan[n]
                    wb = wbuf[n % NWB]
                    s.dma(wb[:, :, 0:ncol], e_w_in.view(col0, [[E_NCOL_EXT, 128], [128 * E_NCOL_EXT, 8], [1, ncol]]), q="pool")

                def load_w(col0, ncol=WB):
                    i = wstate["i"]
                    wstate["i"] += 1
                    assert wplan[i] == (col0, ncol), (i, wplan[i], col0, ncol)
                    while wstate["issued"] < min(i + PF + 1, len(wplan)):
                        _issue(wstate["issued"])
                        wstate["issued"] += 1
                    return wbuf[i % NWB]

                bstate = {"i": 0}

                def nbank():
                    bk = banks[bstate["i"] % 8]
                    bstate["i"] += 1
                    return bk

                def fm_mm(wb, cc, t0, n):
                    bk = nbank()
                    s.mm(bk[:, 0:n], [(wb[:, k, cc * 128:(cc + 1) * 128], hT[k][:, t0:t0 + n]) for k in range(8)])
                    return bk

                xraws = [cx.sb(es, "xraw%d" % i, [128, T]) for i in range(2)]
                accs = [cx.sb(es, "acc%d" % i, [128, T]) for i in range(2)]
                acc = accs[0]
                accbs = [cx.sb(es, "accb%d" % i, [128, T], BF16) for i in range(2)]
                accb = accbs[0]
                rc_i = {"i": 0}
                tmp1s = [cx.sb(es, "tmp1_%d" % i, [128, 512]) for i in range(2)]
                tmp2s = [cx.sb(es, "tmp2_%d" % i, [128, 512]) for i in range(2)]
                stg = [cx.sb(es, "stg%d" % i, [128, 4, 128]) for i in range(2)]
                stgb = [cx.sb(es, "stgb%d" % i, [128, 4, 128], BF16) for i in range(2)]
                rp = cx.sb(es, "rp", [128, 2, L])
                cw = cx.sb(es, "cw", [128, 12, 5])
                cb = cx.sb(es, "cb", [128, 12])
                dtb = cx.sb(es, "dtb", [128, 32])
                abc = cx.sb(es, "abc", [128, 32])
                s.dma(rp.full(), rope.full())
                s.dma(cw.full(), e_conv_wT.full())
                s.dma(cb.full(), e_conv_bT.full())
                s.dma(dtb.full(), e_dt_bias.view(0, [[0, 128], [1, 32]]))
                s.dma(abc.full(), e_a_log.view(0, [[0, 128], [1, 32]]))
                s.act(abc.full(), abc.full(), AF.Exp)
                s.ts(abc.full(), abc.full(), -1.0, None, ALU.mult)
                stg_i = {"i": 0}

                def transposes_to(dst, col0, src, lowp=False):
                    for i0 in range(0, NT, 4):
                        nb = min(4, NT - i0)
                        bk = nbank()
                        for ii in range(nb):
                            i = i0 + ii
                            s.transpose(bk[:, ii * 128:(ii + 1) * 128], src[:, i * 128:(i + 1) * 128], ident)
                        sg_ = (stgb if lowp else stg)[stg_i["i"] % 2]
                        stg_i["i"] += 1
                        s.copy(sg_[:, 0:nb, :], bk.view(0, [[512, 128], [128, nb], [1, 128]]), eng="act")
                        ncols = dst.h.shape[1]
                        s.dma(dst.view(i0 * 128 * ncols + col0, [[ncols, 128], [128 * ncols, nb], [1, 128]]),
                              sg_[:, 0:nb, :])

                wb_of = {}

                def xa(fc):
                    if fc % 2 == 0:
                        wb_of[fc // 2] = load_w(OFF_XBC + fc * 128)
                    wb = wb_of[fc // 2]
                    xraw = xraws[fc % 2]
                    for (t0, n) in TG:
                        bk = fm_mm(wb, fc % 2, t0, n)
                        s.copy(xraw[:, t0:t0 + n], bk[:, 0:n], eng="act")

                def xb(fc):
                    xraw, acc = xraws[fc % 2], accs[fc % 2]
                    s.ts(acc.full(), xraw.full(), cw[:, fc, 2:3], cb[:, fc:fc + 1], ALU.mult, ALU.add)
                    for kk in (0, 1, 3, 4):
                        d_ = kk - 2
                        for (s0, sl) in ((0, CTX), (CTX, L)):
                            lo = max(s0, s0 - d_)
                            hi = min(s0 + sl, s0 + sl - d_)
                            s.stt(acc[:, lo:hi], xraw[:, lo + d_:hi + d_], cw[:, fc, kk:kk + 1], acc[:, lo:hi],
                                  ALU.mult, ALU.add)
                    s.act(acc.full(), acc.full(), AF.Silu)
                    if fc < 8:
                        transposes_to(XS, fc * 128, acc)
                    elif fc < 10:
                        s.copy(accb.full(), acc.full(), eng="act")
                        s.dma(BT[fc - 8], accb.full())
                        transposes_to(BTOK, (fc - 8) * 128, acc, lowp=True)
                    else:
                        s.copy(accb.full(), acc.full(), eng="act")
                        s.dma(CT[fc - 10], accb.full())

                pipeline(list(range(12 if go('p2a') else 0)), [xa, xb])

                def rope_chunk(col_plain, col_swap, dst_rot, dst_ctx):
                    accb = accbs[rc_i["i"] % 2]
                    rc_i["i"] += 1
                    wa = load_w(col_plain, 128)
                    wsw = load_w(col_swap, 128)
                    for gi, (t0, n) in enumerate(TG):
                        bka = fm_mm(wa, 0, t0, n)
                        if gi == 0:
                            s.copy(accb[:, 0:CTX], bka[:, 0:CTX], eng="act")
                            continue
                        bkb = fm_mm(wsw, 0, t0, n)
                        l0 = t0 - CTX
                        tmp1, tmp2 = tmp1s[gi % 2], tmp2s[gi % 2]
                        s.tt(tmp1.full(), bka.full(), rp[:, 0, l0:l0 + 512], ALU.mult)
                        s.tt(tmp2.full(), bkb.full(), rp[:, 1, l0:l0 + 512], ALU.mult)
                        s.tt(accb[:, t0:t0 + n], tmp1.full(), tmp2.full(), ALU.add)
                    s.dma(dst_ctx, accb[:, 0:CTX])
                    s.dma(dst_rot, accb[:, CTX:T])

                for qc in range(8 if go('p2b') else 0):
                    rope_chunk(OFF_Q + qc * 128, OFF_QS + qc * 128, QR[qc], QC[qc])
                for j in range(4 if go('p2c') else 0):
                    rope_chunk(OFF_KR + j * 128, OFF_KSR + j * 128, KR[j], KC[j])

                for gc in range(8 if go('p2d') else 0):
                    acc = accs[gc % 2]
                    if gc % 2 == 0:
                        wb = load_w(OFF_G + gc * 128)
                    for (t0, n) in TG:
                        bk = fm_mm(wb, gc % 2, t0, n)
                        s.act(acc[:, t0:t0 + n], bk[:, 0:n], AF.Silu)
                    s.dma(SG[gc], acc.full())

                NT_E = NT if go('p2e') else 0
                wz = [load_w(OFF_Z + i * 256) for i in range(4)]
                for i in range(NT_E):
                    z_a = accs[i % 2]
                    for half in range(2):
                        bk = nbank()
                        for q4 in range(2):
                            wbz = wz[half * 2 + q4]
                            s.mm(bk[:, q4 * 256:(q4 + 1) * 256],
                                 [(hT[k][:, i * 128:(i + 1) * 128], wbz[:, k, :]) for k in range(8)])
                        s.act(z_a[:, half * 512:(half + 1) * 512], bk.full(), AF.Silu)
                    s.dma(SZ[i * 128:(i + 1) * 128, :], z_a[:, 0:1024])
                wv = load_w(OFF_KV + 256)
                wdt = load_w(OFF_DT, 32)
                vt = [cx.sb(es, "vt%d" % i, [128, 256], BF16) for i in range(2)]
                for i in range(NT if go('p2f') else 0):
                    bk = nbank()
                    s.mm(bk[:, 0:256], [(hT[k][:, i * 128:(i + 1) * 128], wv[:, k, :]) for k in range(8)])
                    s.copy(vt[i % 2].full(), bk[:, 0:256], eng="act")
                    s.dma(VT[i * 128:(i + 1) * 128, :], vt[i % 2].full())
                for i in range(NT if go('p2g') else 0):
                    bk = nbank()
                    s.mm(bk[:, 0:32], [(hT[k][:, i * 128:(i + 1) * 128], wdt[:, k, 0:32]) for k in range(8)])
                    s.tt(DT[:, i, :], bk[:, 0:32], dtb.full(), ALU.add)
                    if go('p2h'):
                        s.act(DT[:, i, :], DT[:, i, :], AF.Exp)
                        s.ts(DT[:, i, :], DT[:, i, :], 1.0, None, ALU.add)
                        s.act(DT[:, i, :], DT[:, i, :], AF.Ln)
                    s.tt(DTA[:, i, :], DT[:, i, :], abc.full(), ALU.mult)
                    if DTD is not None:
                        s.dma(DTD[i * 128:(i + 1) * 128, :], DT[:, i, :])
                s.flush()
            hts.close()
            for k in range(16):
                s.dma(wo[k].full(), e_w_out[k * 128:(k + 1) * 128, :], q="pool")

            with ExitStack() as es:
                nb_ = {"i": 0}

                def nbank():
                    bk = banks[nb_["i"] % 8]
                    nb_["i"] += 1
                    return bk

                N3 = 3
                NX, NY, NB4 = 2, 5, 4
                xs_t = [cx.sb(es, "xs_t%d" % i, [128, 1024]) for i in range(NX)]
                b_t = [cx.sb(es, "b_t%d" % i, [128, 256], BF16) for i in range(NB4)]
                bt_t = [cx.sb(es, "bt_t%d" % i, [128, 2, 128], BF16) for i in range(NB4)]
                ct_t = [cx.sb(es, "ct_t%d" % i, [128, 2, 128], BF16) for i in range(NB4)]
                yf_t = [cx.sb(es, "yf_t%d" % i, [128, 1024]) for i in range(NY)]
                sz_t = [cx.sb(es, "sz_t%d" % i, [128, 1024]) for i in range(2)]
                MTs = [cx.sb(es, "MT%d" % i, [128, 2048], BF16) for i in range(N3)]
                xcs = [cx.sb(es, "xc%d" % i, [128, 1024], BF16) for i in range(N3)]
                xcds = [cx.sb(es, "xcd%d" % i, [128, 1024], BF16) for i in range(N3)]
                tmpos = [cx.sb(es, "tmpo%d" % i, [128, 1024]) for i in range(N3)]
                ytots = [cx.sb(es, "ytot%d" % i, [128, 1024]) for i in range(2)]
                sms = [cx.sb(es, "sm%d" % i, [128, 4, 16]) for i in range(N3)]
                st3s = [cx.sb(es, "st3_%d" % i, [128, 4]) for i in range(N3)]
                ystgs = [cx.sb(es, "ystg%d" % i, [128, 8, 128], BF16) for i in range(2)]
                dtatris = [cx.sb(es, "dtatri%d" % i, [128, 2048]) for i in range(2)]
                decTs = [cx.sb(es, "decT%d" % i, [128, 2048], BF16) for i in range(2)]
                cb_sbs = [cx.sb(es, "cb_sb%d" % i, [128, 256], BF16) for i in range(2)]
                junk = cx.sb(es, "junk3", [128, 1024])
                Hs = [cx.sb(es, "Hs%d" % g, [128, 512]) for g in range(2)]
                Hb = [cx.sb(es, "Hb%d" % g, [128, 512], BF16) for g in range(2)]
                dsk = cx.sb(es, "dsk", [128, 16])
                snw = cx.sb(es, "snw", [128, 8])
                cm1 = cx.sb(es, "cm1", [128, 1])
                s.memset(cm1.full(), -1.0)
                s.dma(dsk.full(), e_d_skip.view(0, [[0, 128], [1, 16]]))
                s.dma(snw.full(), e_ssd_norm_wT.full())
                cmh = cx.sb(es, "cmh", [128, 1])
                s.memset(cmh.full(), -0.5)

                def bc3(buf, off, pstep, n1, s1, n2, s2):
                    return buf.view(off, [[pstep, 128], [s1, n1], [s2, n2]])

                n_ch = NT if go("p3") else 0
                for d_ in range(2):
                    order = list(range(NT)) if d_ == 0 else [1, 0] + list(range(NT - 1, 1, -1))
                    order = order[:n_ch]
                    TRIoff = 512 if d_ == 0 else 1024
                    TRIv = tri if d_ == 0 else utri
                    negm = cst[:, 4 + d_, :]
                    for g in range(2):
                        s.memset(Hs[g].full(), 0.0)
                        s.memset(Hb[g].full(), 0.0)

                    def stA(item, d_=d_, TRIoff=TRIoff, TRIv=TRIv, negm=negm):
                        ci, i = item
                        p3, p2, p4 = ci % N3, ci % 2, ci % NY
                        xs_, b_, bt_, ct_ = xs_t[ci % NX], b_t[ci % NB4], bt_t[ci % NB4], ct_t[ci % NB4]
                        MT, xc, xcd, sm = MTs[p3], xcs[p3], xcds[p3], sms[p3]
                        dtatri, decT, cb_sb = dtatris[p2], decTs[p2], cb_sbs[p2]
                        dta_i = DTA[:, i, d_ * 16:(d_ + 1) * 16]
                        doff = i * 32 + d_ * 16
                        s.tt(bc3(dtatri, 0, 2048, 16, 128, 128, 1), bc3(DTA, doff, NT * 32, 16, 1, 128, 0),
                             bc3(cst, TRIoff, 3072, 16, 0, 128, 1), ALU.mult, eng="pool")
                        bs = nbank()
                        s.mm(bs[:, 0:16], [(TRIv, dta_i)])
                        s.mm(bs[:, 16:32], [(ones, dta_i)])
                        na, ea, de, cd = sm[:, 0, :], sm[:, 1, :], sm[:, 2, :], sm[:, 3, :]
                        s.ts(na, bs[:, 0:16], -1.0, None, ALU.mult)
                        s.act(ea, bs[:, 0:16], AF.Exp)
                        s.tt(de, bs[:, 16:32], na, ALU.add)
                        s.act(de, de, AF.Exp)
                        s.act(cd, bs[:, 16:32], AF.Exp)
                        for hq in range(4):
                            bq = nbank()
                            s.mm(bq.full(), [(ones, dtatri[:, hq * 512:(hq + 1) * 512]), (ident, negm)])
                            for hh in range(4):
                                h = hq * 4 + hh
                                s.act(decT[:, h * 128:(h + 1) * 128], bq[:, hh * 128:(hh + 1) * 128], AF.Exp,
                                      bias=sm[:, 0, h:h + 1])
                        bc = nbank()
                        for g in range(2):
                            s.mm(bc[:, g * 128:(g + 1) * 128], [(bt_[:, g, :], ct_[:, g, :])])
                        s.copy(cb_sb.full(), bc[:, 0:256], eng="act")
                        for g in range(2):
                            s.tt(bc3(MT, g * 1024, 2048, 8, 128, 128, 1), bc3(decT, g * 1024, 2048, 8, 128, 128, 1),
                                 bc3(cb_sb, g * 128, 256, 8, 0, 128, 1), ALU.mult)
                        s.tt(bc3(xc, 0, 1024, 16, 64, 64, 1), bc3(xs_, 0, 1024, 16, 64, 64, 1),
                             bc3(DT, doff, NT * 32, 16, 1, 64, 0), ALU.mult, eng="pool")
                        s.tt(bc3(xcd, 0, 1024, 16, 64, 64, 1), bc3(xc, 0, 1024, 16, 64, 64, 1),
                             bc3(sm, 32, 64, 16, 1, 64, 0), ALU.mult, eng="pool")
                        if d_ == 1:
                            s.tt(bc3(tmpos[p3], 0, 1024, 16, 64, 64, 1), bc3(xs_, 0, 1024, 16, 64, 64, 1),
                                 bc3(dsk, 0, 16, 16, 1, 64, 0), ALU.mult, eng="pool")
                            s.tt(yf_t[p4].full(), yf_t[p4].full(), tmpos[p3].full(), ALU.add, eng="pool")

                    def stL(item, d_=d_):
                        ci, i = item
                        s.dma(xs_t[ci % NX].full(), XS[i * 128:(i + 1) * 128, :])
                        s.dma(b_t[ci % NB4].full(), BTOK[i * 128:(i + 1) * 128, :])
                        s.dma(bt_t[ci % NB4].full(), BT.view(i * 128, [[T, 128], [128 * T, 2], [1, 128]]))
                        s.dma(ct_t[ci % NB4].full(), CT.view(i * 128, [[T, 128], [128 * T, 2], [1, 128]]))
                        if d_ == 1:
                            s.dma(yf_t[ci % NY].full(), YF[i * 128:(i + 1) * 128, :])

                    def stB(item, d_=d_):
                        ci, i = item
                        p3 = ci % N3
                        b_, ct_ = b_t[ci % NB4], ct_t[ci % NB4]
                        MT, xc, xcd, sm, tmpo, ytot = MTs[p3], xcs[p3], xcds[p3], sms[p3], tmpos[p3], ytots[ci % 2]
                        ydst = yf_t[ci % NY] if d_ == 0 else ytot
                        if d_ == 1:
                            s.dma(sz_t[ci % 2].full(), SZ[i * 128:(i + 1) * 128, :])
                        for g in range(2):
                            by = nbank()
                            for hh in range(8):
                                h = g * 8 + hh
                                s.mm(by[:, hh * 64:(hh + 1) * 64], [(MT[:, h * 128:(h + 1) * 128], xc[:, h * 64:(h + 1) * 64])])
                            bo = nbank()
                            s.mm(bo.full(), [(ct_[:, g, :], Hb[g].full())])
                            s.tt(bc3(tmpo, g * 512, 1024, 8, 64, 64, 1), bc3(bo, 0, 512, 8, 64, 64, 1),
                                 bc3(sm, 16 + g * 8, 64, 8, 1, 64, 0), ALU.mult)
                            s.tt(ydst[:, g * 512:(g + 1) * 512], by.full(), tmpo[:, g * 512:(g + 1) * 512], ALU.add)
                        for g in range(2):
                            bst = nbank()
                            s.mm(bst.full(), [(b_[:, g * 128:(g + 1) * 128], xcd[:, g * 512:(g + 1) * 512])])
                            s.tt(bc3(Hs[g], 0, 512, 8, 64, 64, 1), bc3(Hs[g], 0, 512, 8, 64, 64, 1),
                                 bc3(sm, 48 + g * 8, 64, 8, 1, 64, 0), ALU.mult)
                            s.tt(Hs[g].full(), Hs[g].full(), bst.full(), ALU.add)
                            s.copy(Hb[g].full(), Hs[g].full(), eng="act")
                        if d_ == 0:
                            s.dma(YF[i * 128:(i + 1) * 128, :], yf_t[ci % NY].full())

                    def stC(item, d_=d_):
                        if d_ == 0:
                            return
                        ci, i = item
                        p3, p2 = ci % N3, ci % 2
                        ytot, sz_, st3, ystg = ytots[ci % 2], sz_t[ci % 2], st3s[p3], ystgs[p2]
                        s.tt(ytot.full(), ytot.full(), yf_t[ci % NY].full(), ALU.add)
                        s.tt(ytot.full(), ytot.full(), sz_.full(), ALU.mult)
                        s.act(junk.full(), ytot.full(), AF.Square, accum=st3[:, 0:1])
                        s.ts(st3[:, 1:2], st3[:, 0:1], 1.0 / 1024, EPS, ALU.mult, ALU.add)
                        s.tt(st3[:, 3:4], st3[:, 1:2], cmh.full(), ALU.pow, eng="pool")
                        s.act(ytot.full(), ytot.full(), AF.Copy, scale=st3[:, 3:4])
                        for half in range(2):
                            bk = nbank()
                            for kk in range(4):
                                k = half * 4 + kk
                                s.transpose(bk[:, kk * 128:(kk + 1) * 128], ytot[:, k * 128:(k + 1) * 128], ident)
                            for kk in range(4):
                                k = half * 4 + kk
                                s.act(ystg[:, k, :], bk[:, kk * 128:(kk + 1) * 128], AF.Copy, scale=snw[:, k:k + 1])
                        s.dma(YT.view(i * 128, [[T, 128], [128 * T, 8], [1, 128]]), ystg.full())

                    its = list(enumerate(order))
                    nit = len(its)
                    for t in range(-1, nit + 3):
                        if 0 <= t + 1 < nit:
                            stL(its[t + 1])
                        if 0 <= t - 3 < nit:
                            stC(its[t - 3])
                        if 0 <= t - 2 < nit:
                            stB(its[t - 2])
                        if 0 <= t < nit:
                            stA(its[t])
                s.flush()

            with ExitStack() as es:
                nb_ = {"i": 0}

                def nbank():
                    bk = banks[nb_["i"] % 8]
                    nb_["i"] += 1
                    return bk

                J2 = 2
                qr_ts = [cx.sb(es, "qr_t%d" % i, [128, 2, L], BF16) for i in range(J2)]
                qc_ts = [cx.sb(es, "qc_t%d" % i, [128, 2, CTX], BF16) for i in range(J2)]
                kr_ts = [cx.sb(es, "kr_t%d" % i, [128, L], BF16) for i in range(J2)]
                kc_ts = [cx.sb(es, "kc_t%d" % i, [128, CTX], BF16) for i in range(J2)]
                v_ts = [cx.sb(es, "v_t%d" % i, [128, NT, 64], BF16) for i in range(J2)]
                v2s = [cx.sb(es, "v2_%d" % i, [128, NT, 128], BF16) for i in range(J2)]
                sg_ts = [cx.sb(es, "sg_t%d" % i, [128, 2, T]) for i in range(J2)]
                asts = [cx.sb(es, "ast%d" % i, [128, 2, T], BF16) for i in range(J2)]
                NP = 5
                pt = [[cx.sb(es, "pt%d_%d" % (a, b), [128, 512], BF16) for b in range(5)] for a in range(NP)]
                rds = [cx.sb(es, "rd%d" % i, [128, 256]) for i in range(2)]
                aos = [cx.sb(es, "ao%d" % i, [128, 256]) for i in range(2)]
                es_pp = cx.sb(es, "es_pp", [128, 8])
                c8 = cx.sb(es, "c8", [128, 1])
                s.memset(c8.full(), 0.125)
                s.dma(es_pp.full(), e_sink.full())
                s.act(es_pp.full(), es_pp.full(), AF.Exp)
                ATT_DBG = [int(v) for v in os.environ.get("ATT_DBG", "4,18,4").split(",")]
                items = []
                for j in range(ATT_DBG[0] if go("p4") else 0):
                    qbs = ([("c", 0), ("c", 1)] + [("l", b) for b in range(16)])[:ATT_DBG[1]]
                    for qi, (kind, bi) in enumerate(qbs):
                        items.append((len(items), j, kind, bi, qi == 0, qi == len(qbs) - 1))

                def keys_of(kind, bi):
                    keys = [("c", 0, None), ("c", 1, None)]
                    if kind == "l":
                        if bi > 0:
                            keys.append(("l", bi - 1, "prev"))
                        keys.append(("l", bi, None))
                        if bi < 15:
                            keys.append(("l", bi + 1, "next"))
                    return keys

                def atA(item):
                    n, j, kind, bi, first, last = item
                    js = j % J2
                    qr_t, qc_t, kr_t, kc_t, v_t, v2, sg_t = qr_ts[js], qc_ts[js], kr_ts[js], kc_ts[js], v_ts[js], v2s[js], sg_ts[js]
                    if first:
                        s.dma(qr_t.full(), QR.view(2 * j * 128 * L, [[L, 128], [128 * L, 2], [1, L]]))
                        s.dma(qc_t.full(), QC.view(2 * j * 128 * CTX, [[CTX, 128], [128 * CTX, 2], [1, CTX]]))
                        s.dma(kr_t.full(), KR[j])
                        s.dma(kc_t.full(), KC[j])
                        s.dma(v_t.full(), VT.view(j * 64, [[256, 128], [128 * 256, NT], [1, 64]]))
                        s.dma(sg_t.full(), SG.view(2 * j * 128 * T, [[T, 128], [128 * T, 2], [1, T]]))
                        s.copy(v2[:, :, 0:64], v_t.full(), eng="pool")
                        s.copy(v2[:, :, 64:128], v_t.full(), eng="pool")
                    qsrc, q0 = (qc_t, bi * 128) if kind == "c" else (qr_t, bi * 128)
                    pts = pt[n % NP]
                    qw = qsrc.h.shape[2]
                    for ki, (kk, kb, msk) in enumerate(keys_of(kind, bi)):
                        ksrc = kc_t if kk == "c" else kr_t
                        for par in range(2):
                            p0 = par * 64
                            bs = nbank()
                            s.mm(bs[:, 0:256],
                                 [(ksrc[p0:p0 + 64, kb * 128:(kb + 1) * 128],
                                   qsrc.view(p0 * 2 * qw + q0, [[2 * qw, 64], [qw, 2], [1, 128]]))])
                            s.act(pts[ki][:, par * 256:(par + 1) * 256], bs[:, 0:256], AF.Exp, scale=c8[:, 0:1])
                        if msk is not None:
                            moff = 1024 if msk == "prev" else 512
                            s.tt(pts[ki].view(0, [[512, 128], [128, 4], [1, 128]]),
                                 pts[ki].view(0, [[512, 128], [128, 4], [1, 128]]),
                                 cst.view(moff, [[3072, 128], [0, 4], [1, 128]]), ALU.mult, eng="pool")

                def atB(item):
                    n, j, kind, bi, first, last = item
                    js = j % J2
                    v2, sg_t, ast = v2s[js], sg_ts[js], asts[js]
                    tok0 = bi * 128 if kind == "c" else CTX + bi * 128
                    keys = keys_of(kind, bi)
                    pts = pt[n % NP]
                    rd, ao = rds[n % 2], aos[n % 2]
                    vt_idx = [(kb if kk == "c" else 2 + kb) for (kk, kb, _) in keys]
                    bn = nbank()
                    s.mm(bn.full(), [(v2[:, vt_idx[ki], :], pts[ki].full()) for ki in range(len(keys))])
                    bd = nbank()
                    s.mm(bd.full(), [(onesb, pts[ki].full()) for ki in range(len(keys))])
                    for par in range(2):
                        p0 = par * 64
                        for c in range(2):
                            s.ts(rd[p0:p0 + 64, c * 128:(c + 1) * 128],
                                 bd[p0:p0 + 64, par * 256 + c * 128:par * 256 + (c + 1) * 128],
                                 es_pp[p0:p0 + 64, 2 * j + c:2 * j + c + 1], None, ALU.add)
                    s.recip(rd.full(), rd.full())
                    for par in range(2):
                        p0 = par * 64
                        s.tt(ao[p0:p0 + 64, :], bn[p0:p0 + 64, par * 256:(par + 1) * 256], rd[p0:p0 + 64, :], ALU.mult)
                    s.tt(ast.view(tok0, [[2 * T, 128], [T, 2], [1, 128]]),
                         ao.view(0, [[256, 128], [128, 2], [1, 128]]),
                         sg_t.view(tok0, [[2 * T, 128], [T, 2], [1, 128]]), ALU.mult)
                    if last:
                        s.dma(YT.view((8 + 2 * j) * 128 * T, [[T, 128], [128 * T, 2], [1, T]]), ast.full())

                pipeline(items, [atA, (lambda it: None), (lambda it: None), atB])
                s.flush()

            with ExitStack() as es:
                nb_ = {"i": 0}

                def nbank():
                    bk = banks[nb_["i"] % 8]
                    nb_["i"] += 1
                    return bk

                ytg = [cx.sb(es, "ytg%d" % i, [128, 16, 512], BF16) for i in range(2)]
                xt = [cx.sb(es, "xt5_%d" % i, [128, D]) for i in range(3)]
                x1t = [cx.sb(es, "x1t%d" % i, [128, D]) for i in range(3)]
                tmp5s = [cx.sb(es, "tmp5_%d" % i, [128, 512]) for i in range(2)]
                for i in range(NT if go("p5") else 0):
                    w = 1 if i < 2 else 0
                    gi_, ii = i // 4, i % 4
                    yg = ytg[gi_ % 2]
                    if ii == 0:
                        nt4 = min(4, NT - i)
                        s.dma(yg[:, :, 0:nt4 * 128], YT.view(i * 128, [[T, 128], [128 * T, 16], [1, nt4 * 128]]))
                    x_, o_ = xt[i % 3], x1t[i % 3]
                    s.dma(x_.full(), xin[i * 128:(i + 1) * 128, :])
                    for half in range(2):
                        tmp5 = tmp5s[half]
                        bk = nbank()
                        s.mm(bk.full(), [(yg[:, fc, ii * 128:(ii + 1) * 128], wo[fc][:, half * 512:(half + 1) * 512]) for fc in range(16)])
                        s.tt(tmp5.full(), bk.full(), gate_bc[0][w][:, half * 512:(half + 1) * 512], ALU.mult)
                        s.tt(o_[:, half * 512:(half + 1) * 512], tmp5.full(), x_[:, half * 512:(half + 1) * 512], ALU.add)
                    s.dma(X1[i * 128:(i + 1) * 128, :], o_.full(), q="pool")
                s.flush()

        if go("all"):
            adaln_phase(1, o_ada_w, o_ada_b)
        with ExitStack() as l1:
            if not go("all"):
                return nc
            nb_ = {"i": 0}

            def nbank():
                bk = banks[nb_["i"] % 8]
                nb_["i"] += 1
                return bk

            with ExitStack() as es:
                nw = cx.sb(es, "nw1", [128, 8])
                sc1 = [cx.sb(es, "sc1b_%d" % w, [128, 8]) for w in range(2)]
                s.dma(nw.full(), o_norm_wT.full())
                for w in range(2):
                    s.stt(sc1[w].full(), modT[1][w][:, 8:16], 1.0, nw.full(), ALU.add, ALU.mult)
                hT = [cx.sb(es, "hTb%d" % k, [128, T], BF16) for k in range(8)]
                xt = [cx.sb(es, "xtb%d" % i, [128, D]) for i in range(3)]
                xn = [cx.sb(es, "xnb%d" % i, [128, D]) for i in range(3)]
                junk = cx.sb(es, "junkb", [128, D])
                st = [cx.sb(es, "stb%d" % i, [128, 4]) for i in range(3)]
                def m0(i):
                    x_, n_, st_ = xt[i % 3], xn[i % 3], st[i % 3]
                    s.dma(x_.full(), X1[i * 128:(i + 1) * 128, :])
                    s.act(junk.full(), x_.full(), AF.Square, accum=st_[:, 0:1])
                    s.ts(st_[:, 1:2], st_[:, 0:1], 1.0 / D, EPS, ALU.mult, ALU.add)
                    s.act(st_[:, 2:3], st_[:, 1:2], AF.Sqrt)
                    s.recip(st_[:, 3:4], st_[:, 2:3])
                    s.ts(n_.full(), x_.full(), st_[:, 3:4], None, ALU.mult)

                def m1(i):
                    w = 1 if i < 2 else 0
                    n_ = xn[i % 3]
                    for half in range(2):
                        bk = nbank()
                        for kk in range(4):
                            k = half * 4 + kk
                            s.transpose(bk[:, kk * 128:(kk + 1) * 128], n_[:, k * 128:(k + 1) * 128], ident)
                        for kk in range(4):
                            k = half * 4 + kk
                            s.act(hT[k][:, i * 128:(i + 1) * 128], bk[:, kk * 128:(kk + 1) * 128], AF.Identity,
                                  bias=modT[1][w][:, k:k + 1], scale=sc1[w][:, k:k + 1])

                pipeline(list(range(NT)), [m0, m1])
                wq = [cx.sb(es, "wq%d" % i, [128, 8, 512], BF16) for i in range(4)]
                for q4_ in range(4):
                    s.dma(wq[q4_].full(), o_w_in.view(q4_ * 512, [[2 * D, 128], [128 * 2 * D, 8], [1, 512]]), q="pool")
                ot = [cx.sb(es, "ot%d" % i, [128, D]) for i in range(2)]
                oi = 0
                for which in range(2):
                    for i in range(NT):
                        if which == 1 and i < 2:
                            continue
                        o_ = ot[oi % 2]
                        oi += 1
                        for half in range(2):
                            bk = nbank()
                            s.mm(bk.full(), [(hT[k][:, i * 128:(i + 1) * 128], wq[which * 2 + half][:, k, :]) for k in range(8)])
                            if which == 0:
                                s.copy(o_[:, half * 512:(half + 1) * 512], bk.full(), eng="act")
                            else:
                                s.act(o_[:, half * 512:(half + 1) * 512], bk.full(), AF.Silu)
                        s.dma((U if which == 0 else SG1)[i * 128:(i + 1) * 128, :], o_.full())
                s.flush()

            L1S = os.environ.get('L1S', 'z')
            if L1S == 'a':
                return nc
            with ExitStack() as es:
                lam = cx.sb(es, "lam", [128, 2, 3, 32])
                bprm = cx.sb(es, "bprm", [128, 2, 32, 16])
                cprm = cx.sb(es, "cprm", [128, 2, 32, 16])
                s.dma(lam.full(), s5_lam.full())
                s.dma(bprm.full(), s5_b.full())
                s.dma(cprm.full(), s5_c.full())
                kc = cx.sb(es, "kconst", [128, 4])
                s.memset(kc[:, 0:1], 1.0 / 16)
                s.memset(kc[:, 1:2], math.pi / 2)
                s.memset(kc[:, 2:3], 0.0)
                s.memset(kc[:, 3:4], 1.0)
                W64 = [128, 2, 32]

                def t64(name):
                    return cx.sb(es, name, W64)

                def lv(i):
                    return lam.view(i * 32, [[192, 128], [96, 2], [1, 32]])

                dt_ = t64("dt_"); mag = t64("mag"); th = t64("th"); cs = t64("cs"); sn = t64("sn")
                t_a = t64("t_a"); t_b = t64("t_b"); t_c = t64("t_c")
                abre = t64("abre"); abim = t64("abim"); cre = t64("cre"); cim = t64("cim")
                s.act(dt_.full(), lv(2), AF.Exp)
                s.tt(t_a.full(), lv(0), dt_.full(), ALU.mult)
                s.act(mag.full(), t_a.full(), AF.Exp)
                s.tt(th.full(), lv(1), dt_.full(), ALU.mult)
                s.act(sn.full(), th.full(), AF.Sin, scale=kc[:, 0:1])
                s.act(cs.full(), th.full(), AF.Sin, scale=kc[:, 0:1], bias=kc[:, 1:2])
                for _ in range(4):
                    s.tt(t_a.full(), cs.full(), cs.full(), ALU.mult)
                    s.tt(t_b.full(), sn.full(), sn.full(), ALU.mult)
                    s.tt(t_c.full(), sn.full(), cs.full(), ALU.mult)
                    s.tt(cs.full(), t_a.full(), t_b.full(), ALU.subtract)
                    s.ts(sn.full(), t_c.full(), 2.0, None, ALU.mult)
                s.tt(abre.full(), mag.full(), cs.full(), ALU.mult)
                s.tt(abim.full(), mag.full(), sn.full(), ALU.mult)
                PW = cx.sb(es, "PW", [128, 2, 9, 64])

                def pw(ri, k):
                    return PW.view((ri * 9 + k) * 64, [[2 * 9 * 64, 128], [32, 2], [1, 32]])

                s.memset(PW[:, 0, 0, :], 1.0)
                s.memset(PW[:, 1, 0, :], 0.0)
                for k in range(8):
                    s.tt(t_a.full(), pw(0, k), abre.full(), ALU.mult)
                    s.tt(t_b.full(), pw(1, k), abim.full(), ALU.mult)
                    s.tt(pw(0, k + 1), t_a.full(), t_b.full(), ALU.subtract)
                    s.tt(t_a.full(), pw(0, k), abim.full(), ALU.mult)
                    s.tt(t_b.full(), pw(1, k), abre.full(), ALU.mult)
                    s.tt(pw(1, k + 1), t_a.full(), t_b.full(), ALU.add)
                s.ts(t_c.full(), abre.full(), -1.0, None, ALU.add)
                s.tt(t_a.full(), lv(0), lv(0), ALU.mult)
                s.tt(t_b.full(), lv(1), lv(1), ALU.mult)
                s.tt(t_a.full(), t_a.full(), t_b.full(), ALU.add)
                s.recip(dt_.full(), t_a.full())
                s.tt(t_a.full(), t_c.full(), lv(0), ALU.mult)
                s.tt(t_b.full(), abim.full(), lv(1), ALU.mult)
                s.tt(t_a.full(), t_a.full(), t_b.full(), ALU.add)
                s.tt(cre.full(), t_a.full(), dt_.full(), ALU.mult)
                s.tt(t_a.full(), abim.full(), lv(0), ALU.mult)
                s.tt(t_b.full(), t_c.full(), lv(1), ALU.mult)
                s.tt(t_a.full(), t_a.full(), t_b.full(), ALU.subtract)
                s.tt(cim.full(), t_a.full(), dt_.full(), ALU.mult)
                BB = cx.sb(es, "BB", [128, 2, 2, 512])
                tb1 = cx.sb(es, "tb1", [128, 512])
                tb2 = cx.sb(es, "tb2", [128, 512])

                def bb(ri, d_, g0=0, ng=32):
                    return BB.view((ri * 2 + d_) * 512 + g0 * 16, [[2048, 128], [16, ng], [1, 16]])

                def v3(buf, off, pstep, n1, s1, n2, s2):
                    return buf.view(off, [[pstep, 128], [s1, n1], [s2, n2]])

                def prm(buf, ri, g0=0, ng=32):
                    return buf.view(ri * 512 + g0 * 16, [[1024, 128], [16, ng], [1, 16]])

                def cf(buf, d_, g0=0, ng=32, n2=16):
                    return buf.view(d_ * 32 + g0, [[64, 128], [1, ng], [0, n2]])

                t1v = v3(tb1, 0, 512, 32, 16, 16, 1)
                t2v = v3(tb2, 0, 512, 32, 16, 16, 1)
                for d_ in range(2):
                    s.tt(t1v, prm(bprm, 0), cf(cre, d_), ALU.mult)
                    s.tt(t2v, prm(bprm, 1), cf(cim, d_), ALU.mult)
                    s.tt(bb(0, d_), t1v, t2v, ALU.subtract)
                    s.tt(t1v, prm(bprm, 1), cf(cre, d_), ALU.mult)
                    s.tt(t2v, prm(bprm, 0), cf(cim, d_), ALU.mult)
                    s.tt(bb(1, d_), t1v, t2v, ALU.add)
                LA = cx.sb(es, "LA", [128, 2, 32, 2])
                LB = cx.sb(es, "LB", [128, 2, 32, 2])
                for ri in range(2):
                    s.copy(LA.view(ri, [[128, 128], [64, 2], [2, 32]]), pw(0, 8))
                s.ts(LB.view(0, [[128, 128], [64, 2], [2, 32]]), pw(1, 8), -1.0, None, ALU.mult)
                s.copy(LB.view(1, [[128, 128], [64, 2], [2, 32]]), pw(1, 8))
                zt_ = cx.sb(es, "zt_", [16, 112], BF16)
                s.memset(zt_.full(), 0.0)
                s.flush()

                if L1S == 'b':
                    return nc
                for b in range(4 if L1S not in ('c1', 'd1', 'e1', 'f1', 'g1') else 1):
                    g0 = 8 * b
                    with ExitStack() as bs_:
                        CAB = cx.sb(bs_, "CAB", [128, 2, 2, 8 * 144])
                        WST = cx.sb(bs_, "WST", [128, 8, 2, 2, 2, 64], BF16)
                        TF = cx.sb(bs_, "TF", [128, 16, 128], BF16)
                        TB = cx.sb(bs_, "TB", [128, 16, 128], BF16)
                        CABb = cx.sb(bs_, "CABb", [128, 2, 2, 8 * 144], BF16)

                        with ExitStack() as tmp:
                            WT = cx.sb(tmp, "WT", [128, 2, 2, 8 * 128])
                            c1 = cx.sb(tmp, "c1", [128, 128])
                            c2 = cx.sb(tmp, "c2", [128, 128])
                            c1v = v3(c1, 0, 128, 8, 16, 16, 1)
                            c2v = v3(c2, 0, 128, 8, 16, 16, 1)
                            for d_ in range(2):
                                for ss in range(8):
                                    p_ = 7 - ss if d_ == 0 else ss
                                    pr = PW.view((0 * 9 + p_) * 64 + d_ * 32 + g0, [[1152, 128], [1, 8], [0, 16]])
                                    pi_ = PW.view((1 * 9 + p_) * 64 + d_ * 32 + g0, [[1152, 128], [1, 8], [0, 16]])
                                    o_re = WT.view((d_ * 2 + 0) * 1024 + ss * 16, [[4096, 128], [128, 8], [1, 16]])
                                    o_im = WT.view((d_ * 2 + 1) * 1024 + ss * 16, [[4096, 128], [128, 8], [1, 16]])
                                    s.tt(c1v, bb(0, d_, g0, 8), pr, ALU.mult)
                                    s.tt(c2v, bb(1, d_, g0, 8), pi_, ALU.mult)
                                    s.tt(o_re, c1v, c2v, ALU.subtract)
                                    s.tt(c1v, bb(1, d_, g0, 8), pr, ALU.mult)
                                    s.tt(c2v, bb(0, d_, g0, 8), pi_, ALU.mult)
                                    s.tt(o_im, c1v, c2v, ALU.add)
                            for gh in range(2):
                                p0 = gh * 64
                                for gq in range(8):
                                    bk = nbank()
                                    for d_ in range(2):
                                        for ri in range(2):
                                            sl = d_ * 2 + ri
                                            s.transpose(bk[:, sl * 64:(sl + 1) * 64],
                                                        WT.view(p0 * 4096 + (d_ * 2 + ri) * 1024 + gq * 128, [[4096, 64], [1, 128]]),
                                                        cst[p0:p0 + 64, 0, p0:p0 + 64])
                                    s.copy(WST.view(((gq * 2 + gh) * 4) * 64, [[4096, 128], [1, 256]]), bk[:, 0:256], eng="act")
                            s.flush()

                        KSB = cx.sb(bs_, "KSB", [16, 2, 16, 128], BF16)

                        def setup_part2():
                            c1v = ysb.view(0, [[512, 128], [16, 8], [1, 16]])
                            c2v = ysb.view(128, [[512, 128], [16, 8], [1, 16]])
                            for d_ in range(2):
                                for idx in range(9):
                                    p_ = idx if d_ == 0 else 8 - idx
                                    pr = PW.view((0 * 9 + p_) * 64 + d_ * 32 + g0, [[1152, 128], [1, 8], [0, 16]])
                                    pi_ = PW.view((1 * 9 + p_) * 64 + d_ * 32 + g0, [[1152, 128], [1, 8], [0, 16]])
                                    o_re = CAB.view((0 * 2 + d_) * 1152 + idx * 16, [[4608, 128], [144, 8], [1, 16]])
                                    o_im = CAB.view((1 * 2 + d_) * 1152 + idx * 16, [[4608, 128], [144, 8], [1, 16]])
                                    s.tt(c1v, prm(cprm, 0, g0, 8), pr, ALU.mult, eng="pool")
                                    s.tt(c2v, prm(cprm, 1, g0, 8), pi_, ALU.mult, eng="pool")
                                    s.tt(o_re, c1v, c2v, ALU.subtract, eng="pool")
                                    s.tt(c1v, prm(cprm, 0, g0, 8), pi_, ALU.mult, eng="pool")
                                    s.tt(c2v, prm(cprm, 1, g0, 8), pr, ALU.mult, eng="pool")
                                    s.tt(c1v, c1v, c2v, ALU.add, eng="pool")
                                    s.ts(o_im, c1v, -1.0, None, ALU.mult, eng="pool")
                            s.copy(CABb.full(), CAB.full(), eng="pool")
                            for gh in range(2):
                                p0 = gh * 64
                                for d_ in range(2):
                                    for gqq in range(2):
                                        bk = nbank()
                                        for q4 in range(4):
                                            gq = gqq * 4 + q4
                                            i0 = 0 if d_ == 0 else 1
                                            s.mm(bk[0:16, q4 * 128:(q4 + 1) * 128],
                                                 [(BB.view(p0 * 2048 + (0 * 2 + d_) * 512 + (g0 + gq) * 16, [[2048, 64], [1, 16]]),
                                                   CAB.view(p0 * 4608 + (0 * 2 + d_) * 1152 + gq * 144 + i0 * 16, [[4608, 64], [1, 128]])),
                                                  (BB.view(p0 * 2048 + (1 * 2 + d_) * 512 + (g0 + gq) * 16, [[2048, 64], [1, 16]]),
                                                   CAB.view(p0 * 4608 + (1 * 2 + d_) * 1152 + gq * 144 + i0 * 16, [[4608, 64], [1, 128]]))])
                                        s.copy(KSB.view(d_ * 2048 + (2 * gqq * 4 + gh) * 128, [[4096, 16], [256, 4], [1, 128]]),
                                               bk.view(0, [[512, 16], [128, 4], [1, 128]]), eng="act")
                            gbase = 16 * b
                            s.dma(KFP.view(gbase * 3840 + 7 * 16, [[240, 16], [3840, 16], [1, 128]]), KSB[:, 0, :, :])
                            s.dma(KBR.view(gbase * 3840, [[240, 16], [3840, 16], [1, 128]]), KSB[:, 1, :, :])
                            s.dma(KFP.view(gbase * 3840, [[240, 16], [3840, 16], [1, 112]]), zt_.view(0, [[112, 16], [0, 16], [1, 112]]))
                            s.dma(KBR.view(gbase * 3840 + 128, [[240, 16], [3840, 16], [1, 112]]), zt_.view(0, [[112, 16], [0, 16], [1, 112]]))
                            for ss in range(8):
                                s.dma(TF[ss * 16:(ss + 1) * 16, :, :], KFP.view(gbase * 3840 + (7 - ss) * 16, [[240, 16], [3840, 16], [1, 128]]))
                                s.dma(TB[ss * 16:(ss + 1) * 16, :, :], KBR.view(gbase * 3840 + (7 - ss) * 16, [[240, 16], [3840, 16], [1, 128]]))

                        u8b = cx.sb(bs_, "u8b", [128, 8, 256])
                        u8g = cx.sb(bs_, "u8g", [128, 16, 128])
                        U8T = cx.sb(bs_, "U8T", [128, 16, 288], BF16)
                        SSb = [cx.sb(bs_, "SSb%d" % i, [128, 16, 256], BF16) for i in range(2)]
                        NCOL = 326
                        PS = 16 * NCOL
                        SSD = [cx.sb(bs_, "SS%d" % i, [128, 8, 2, NCOL]) for i in range(2)]
                        CAR = [cx.sb(bs_, "CAR%d" % i, [128, 15, 8, 2]) for i in range(2)]
                        A36 = [cx.sb(bs_, "A36_%d" % i, [128, 8, 2]) for i in range(2)]
                        B36 = [cx.sb(bs_, "B36_%d" % i, [128, 8, 2]) for i in range(2)]
                        y8b = cx.sb(bs_, "y8b", [128, 8, 256])
                        ysb = cx.sb(bs_, "ysb", [128, 512])
                        TT1 = [cx.sb(bs_, "TT1_%d" % i, [128, 17, 8, 2]) for i in range(2)]
                        TT2 = [cx.sb(bs_, "TT2_%d" % i, [128, 17, 8, 2]) for i in range(2)]
                        for (j0, nj) in ((0, 32), (32, 128), (160, 128)):
                            s.dma(u8b[0:nj, :, :], U.view(8 * j0 * 1024 + 256 * b, [[8192, nj], [1024, 8], [1, 256]]))
                            s.copy(u8g.view(0, [[2048, nj], [128, 16], [16, 8], [1, 16]]),
                                   u8b.view(0, [[2048, nj], [16, 16], [256, 8], [1, 16]]), eng="act")
                            for gq4 in range(4):
                                bk = nbank()
                                for q4 in range(4):
                                    gi = gq4 * 4 + q4
                                    s.transpose(bk[:, q4 * 128:q4 * 128 + nj],
                                                u8g.view(128 * gi, [[2048, nj], [1, 128]]), cst[0:nj, 0, 0:nj])
                                s.copy(U8T.view(gq4 * 4 * 288 + j0, [[16 * 288, 128], [288, 4], [1, nj]]),
                                       bk.view(0, [[512, 128], [128, 4], [1, nj]]), eng="act")
                        if L1S in ('d', 'd1'):
                            s.flush()
                            continue
                        s.memset(SSD[0].view(0, [[PS, 128], [NCOL, 16], [1, 1]]), 0.0)
                        s.memset(SSD[0].view(289, [[PS, 128], [NCOL, 16], [1, 37]]), 0.0)
                        s.memset(SSD[1].view(288, [[PS, 128], [NCOL, 16], [1, 38]]), 0.0)
                        s.memset(SSD[0].view(289, [[PS, 128], [2 * NCOL, 8], [1, 1]]), 1.0)
                        s.memset(SSD[1].view(288 + 18 - 1, [[PS, 128], [2 * NCOL, 8], [1, 1]]), 1.0)
                        for gq in range(8):
                            for gh in range(2):
                                gi = 2 * gq + gh
                                p0 = gh * 64
                                for d_ in range(2):
                                    for ri in range(2):
                                        bk = nbank()
                                        s.mm(bk[p0:p0 + 64, 0:288],
                                             [(WST.view((((gq * 2 + gh) * 2 + d_) * 2 + ri) * 64, [[4096, 128], [1, 64]]),
                                               U8T[:, gi, :])])
                                        so = p0 * PS + (gq * 2 + ri) * NCOL
                                        if d_ == 0:
                                            s.copy(SSD[0].view(so + 1, [[PS, 64], [1, 288]]), bk[p0:p0 + 64, 0:288], eng="act")
                                        else:
                                            s.copy(SSD[1].view(so + 256, [[PS, 64], [1, 32]]), bk[p0:p0 + 64, 0:32], eng="act")
                                            s.copy(SSD[1].view(so, [[PS, 64], [1, 256]]), bk[p0:p0 + 64, 32:288], eng="act")
                        if L1S in ('e', 'e1'):
                            s.flush()
                            continue
                        DS = 8 * 2 * 289
                        setup_part2()
                        RI, GQ = NCOL, 2 * NCOL
                        SEG, NSEG = 18, 16
                        REC_ENG2 = os.environ.get('REC2', 'dve')

                        def cplx_step(items):
                            engs = ("dve", REC_ENG2)
                            for n_, (pv, psw, cv, ca, cb_, t1_, t2_) in enumerate(items):
                                s.tt(t1_, pv, ca, ALU.mult, eng=engs[n_ % 2])
                                s.tt(t2_, psw, cb_, ALU.mult, eng=engs[n_ % 2])
                            for n_, (pv, psw, cv, ca, cb_, t1_, t2_) in enumerate(items):
                                s.tt(t1_, t1_, t2_, ALU.add, eng=engs[n_ % 2])
                            for n_, (pv, psw, cv, ca, cb_, t1_, t2_) in enumerate(items):
                                if cv is not None:
                                    s.tt(cv, cv, t1_, ALU.add, eng=engs[n_ % 2])

                        def segv(SS, col, nseg):
                            return (SS.view(col, [[PS, 128], [SEG, nseg], [GQ, 8], [RI, 2]]),
                                    SS.view(col + RI, [[PS, 128], [SEG, nseg], [GQ, 8], [-RI, 2]]))

                        def coef(buf, d_, nseg):
                            return buf.view(d_ * 64 + g0 * 2, [[128, 128], [0, nseg], [2, 8], [1, 2]])

                        TTP = 17 * 16

                        for k in range(1, SEG):
                            items = []
                            for d_ in range(2):
                                pc = k if d_ == 0 else SEG - k
                                cc = k + 1 if d_ == 0 else SEG - 1 - k
                                pv, psw = segv(SSD[d_], pc, NSEG + 1)
                                cv, _ = segv(SSD[d_], cc, NSEG + 1)
                                items.append((pv, psw, cv, coef(LA, d_, NSEG + 1), coef(LB, d_, NSEG + 1), TT1[d_].full(), TT2[d_].full()))
                            cplx_step(items)
                        items = []
                        for d_ in range(2):
                            clast = 288 + SEG if d_ == 0 else 288
                            pv, psw = segv(SSD[d_], clast, 1)
                            items.append((pv, psw, None, coef(LA, d_, 1), coef(LB, d_, 1),
                                          TT1[d_].view(0, [[TTP, 128], [16, 1], [2, 8], [1, 2]]),
                                          TT2[d_].view(0, [[TTP, 128], [16, 1], [2, 8], [1, 2]])))
                        cplx_step(items)
                        for d_ in range(2):
                            l36re = TT1[d_].view(0, [[TTP, 128], [2, 8], [0, 2]])
                            s.copy(A36[d_].full(), l36re)
                            s.ts(B36[d_][:, :, 0:1], TT1[d_].view(1, [[TTP, 128], [2, 8], [1, 1]]), -1.0, None, ALU.mult)
                            s.copy(B36[d_][:, :, 1:2], TT1[d_].view(1, [[TTP, 128], [2, 8], [1, 1]]))
                        for step in range(1, NSEG):
                            items = []
                            for d_ in range(2):
                                if d_ == 0:
                                    m = step
                                    cc, pc = SEG * m + SEG, SEG * m
                                else:
                                    m = NSEG - 1 - step
                                    cc, pc = SEG * m, SEG * m + SEG
                                pv, psw = segv(SSD[d_], pc, 1)
                                cv, _ = segv(SSD[d_], cc, 1)
                                items.append((pv, psw, cv,
                                              A36[d_].view(0, [[16, 128], [0, 1], [2, 8], [1, 2]]),
                                              B36[d_].view(0, [[16, 128], [0, 1], [2, 8], [1, 2]]),
                                              TT1[d_].view(0, [[TTP, 128], [16, 1], [2, 8], [1, 2]]),
                                              TT2[d_].view(0, [[TTP, 128], [16, 1], [2, 8], [1, 2]])))
                            cplx_step(items)
                        items = []
                        for d_ in range(2):
                            pv, psw = segv(SSD[d_], SEG, NSEG - 1)
                            items.append((pv, psw, None, coef(LA, d_, NSEG - 1), coef(LB, d_, NSEG - 1),
                                          CAR[d_].full(), TT2[d_].view(0, [[TTP, 128], [16, NSEG - 1], [2, 8], [1, 2]])))
                        cplx_step(items)
                        NI = SEG - 1
                        for d_ in range(2):
                            SS = SSD[d_]
                            sb0 = SEG + 1 if d_ == 0 else 1

                            def sview(ri):
                                return SS.view(sb0 + ri * RI, [[PS, 128], [SEG, NSEG - 1], [GQ, 8], [1, NI]])

                            def tview(ri):
                                return SS.view(289 + ri * RI, [[PS, 128], [0, NSEG - 1], [GQ, 8], [1, NI]])

                            def cview(ri):
                                return CAR[d_].view(ri, [[(NSEG - 1) * 16, 128], [16, NSEG - 1], [2, 8], [0, NI]])

                            wshape = [[2048, 128], [8 * NI, NSEG - 1], [NI, 8], [1, NI]]
                            w1 = (u8g if d_ == 0 else u8b).view(0, wshape)
                            w2 = y8b.view(0, wshape)
                            s.tt(w1, tview(0), cview(0), ALU.mult)
                            s.tt(w2, tview(1), cview(1), ALU.mult)
                            s.tt(w1, w1, w2, ALU.subtract)
                            s.tt(sview(0), sview(0), w1, ALU.add)
                            s.tt(w1, tview(0), cview(1), ALU.mult)
                            s.tt(w2, tview(1), cview(0), ALU.mult)
                            s.tt(w1, w1, w2, ALU.add)
                            s.tt(sview(1), sview(1), w1, ALU.add)
                        s.copy(SSb[0].full(), SSD[0].view(32, [[PS, 128], [NCOL, 16], [1, 256]]), eng="act")
                        s.copy(SSb[1].full(), SSD[1].view(1, [[PS, 128], [NCOL, 16], [1, 256]]), eng="pool")
                        if L1S in ('f', 'f1'):
                            s.flush()
                            continue
                        for tt_ in range(2):
                            j0 = 32 + 128 * tt_
                            m0 = 128 * tt_
                            for gh in range(2):
                                p0 = gh * 64
                                for gqq in range(2):
                                    bx = nbank()
                                    by = nbank()
                                    for q4 in range(4):
                                        gq = gqq * 4 + q4
                                        gi = 2 * gq + gh
                                        s.mm(bx[:, q4 * 128:(q4 + 1) * 128],
                                             [(U8T[:, gi, j0:j0 + 128], TF[:, gi, :]), (U8T[:, gi, j0:j0 + 128], TB[:, gi, :])])
                                        pairs = []
                                        for d_ in range(2):
                                            c0 = m0
                                            i0 = 1 if d_ == 0 else 0
                                            for ri in range(2):
                                                so = p0 * 4096 + (gq * 2 + ri) * 256 + c0
                                                pairs.append((SSb[d_].view(so, [[4096, 64], [1, 128]]),
                                                              CABb.view(p0 * 4608 + (ri * 2 + d_) * 1152 + gq * 144 + i0 * 16, [[4608, 64], [1, 128]])))
                                        s.mm(by[:, q4 * 128:(q4 + 1) * 128], pairs)
                                    s.copy(ysb.full(), by.full(), eng="act")
                                    s.tt(y8b.view(32 * gqq * 4 + 16 * gh, [[2048, 128], [32, 4], [256, 8], [1, 16]]),
                                         bx.view(0, [[512, 128], [128, 4], [16, 8], [1, 16]]),
                                         ysb.view(0, [[512, 128], [128, 4], [16, 8], [1, 16]]), ALU.add)
                            s.dma(YTOK.view((CTX + 8 * m0) * 1024 + 256 * b, [[8192, 128], [1024, 8], [1, 256]]), y8b.full())
                        s.flush()

            if L1S in ('g', 'g1'):
                return nc
            with ExitStack() as es:
                gw = [cx.sb(es, "gw%d" % k, [128, D], BF16) for k in range(8)]
                ow = [cx.sb(es, "ow%d" % k, [128, D], BF16) for k in range(8)]
                dskb = cx.sb(es, "dskb", [128, D])
                glbb = cx.sb(es, "glbb", [128, D])
                fnwb = cx.sb(es, "fnwb", [128, D])
                kg = cx.sb(es, "kg", [128, 1])
                s.memset(kg.full(), 2.0 * math.sqrt(2.0 / math.pi))
                kmh = cx.sb(es, "kmh", [128, 1])
                s.memset(kmh.full(), -0.5)
                for k in range(8):
                    s.dma(gw[k].full(), o_glu_w[k * 128:(k + 1) * 128, :], q="pool")
                    s.dma(ow[k].full(), o_w_out[k * 128:(k + 1) * 128, :], q="pool")
                s.dma(dskb.full(), o_d_skip.view(0, [[0, 128], [1, D]]))
                s.dma(glbb.full(), o_glu_b.view(0, [[0, 128], [1, D]]))
                s.dma(fnwb.full(), final_norm_w.view(0, [[0, 128], [1, D]]))
                NB3 = 4
                ya = [cx.sb(es, "ya%d" % i, [128, D]) for i in range(NB3)]
                ua = [cx.sb(es, "ua%d" % i, [128, D]) for i in range(NB3)]
                sga = [cx.sb(es, "sga%d" % i, [128, D]) for i in range(NB3)]
                xa = [cx.sb(es, "xa%d" % i, [128, D]) for i in range(NB3)]
                w1s = [cx.sb(es, "w1_%d" % i, [128, D]) for i in range(NB3)]
                w2s = [cx.sb(es, "w2_%d" % i, [128, D]) for i in range(NB3)]
                w3s = [cx.sb(es, "w3_%d" % i, [128, D]) for i in range(NB3)]
                tTs = [cx.sb(es, "tT_%d" % i, [128, 8, 128], BF16) for i in range(2 * NB3)]
                sts = [cx.sb(es, "st10_%d" % i, [128, 4]) for i in range(NB3)]

                def transp8(src, tT):
                    for half in range(2):
                        bk = nbank()
                        for kk in range(4):
                            k = half * 4 + kk
                            s.transpose(bk[:, kk * 128:(kk + 1) * 128], src[:, k * 128:(k + 1) * 128], ident)
                        s.copy(tT[:, half * 4:(half + 1) * 4, :], bk.view(0, [[512, 128], [128, 4], [1, 128]]), eng="act")

                TAILN = int(os.environ.get('TAILN', NT))

                def bufs(i):
                    b_ = i % NB3
                    return ya[b_], ua[b_], sga[b_], xa[b_], w1s[b_], w2s[b_], w3s[b_], tTs[2 * b_], tTs[2 * b_ + 1], sts[b_]

                def stageL(i):
                    y_, u_, g_, x_, w1, w2, w3, tTa, tTb, st = bufs(i)
                    s.dma(y_.full(), YTOK[i * 128:(i + 1) * 128, :])
                    s.dma(u_.full(), U[i * 128:(i + 1) * 128, :])
                    s.dma(g_.full(), SG1[i * 128:(i + 1) * 128, :])
                    s.dma(x_.full(), X1[i * 128:(i + 1) * 128, :])

                def stage0(i):
                    y_, u_, g_, x_, w1, w2, w3, tTa, tTb, st = bufs(i)
                    s.tt(w1.full(), u_.full(), dskb.full(), ALU.mult)
                    s.tt(y_.full(), y_.full(), w1.full(), ALU.add)
                    s.tt(w1.full(), y_.full(), y_.full(), ALU.mult)
                    s.ts(w1.full(), w1.full(), 0.044715, 1.0, ALU.mult, ALU.add)
                    s.tt(w1.full(), w1.full(), y_.full(), ALU.mult)
                    s.act(w1.full(), w1.full(), AF.Sigmoid, scale=kg[:, 0:1])
                    s.tt(w2.full(), y_.full(), w1.full(), ALU.mult)
                    transp8(w2, tTa)

                def stage1(i):
                    y_, u_, g_, x_, w1, w2, w3, tTa, tTb, st = bufs(i)
                    for half in range(2):
                        bk = nbank()
                        s.mm(bk.full(), [(tTa[:, k, :], gw[k][:, half * 512:(half + 1) * 512]) for k in range(8)])
                        s.tt(w1[:, half * 512:(half + 1) * 512], bk.full(), glbb[:, half * 512:(half + 1) * 512], ALU.add)
                    s.act(w1.full(), w1.full(), AF.Sigmoid)
                    s.tt(w2.full(), w2.full(), w1.full(), ALU.mult)
                    s.tt(w2.full(), w2.full(), g_.full(), ALU.mult)
                    transp8(w2, tTb)

                def stage2(i):
                    y_, u_, g_, x_, w1, w2, w3, tTa, tTb, st = bufs(i)
                    for half in range(2):
                        bk = nbank()
                        s.mm(bk.full(), [(tTb[:, k, :], ow[k][:, half * 512:(half + 1) * 512]) for k in range(8)])
                        s.tt(w1[:, half * 512:(half + 1) * 512], bk.full(), gate_bc[1][0][:, half * 512:(half + 1) * 512], ALU.mult)
                    s.tt(w3.full(), w1.full(), x_.full(), ALU.add)
                    s.act(w1.full(), w3.full(), AF.Square, accum=st[:, 0:1])
                    s.ts(st[:, 1:2], st[:, 0:1], 1.0 / D, EPS, ALU.mult, ALU.add)
                    s.tt(st[:, 3:4], st[:, 1:2], kmh.full(), ALU.pow, eng="pool")
                    s.act(w3.full(), w3.full(), AF.Copy, scale=st[:, 3:4])
                    s.tt(w2.full(), w3.full(), fnwb.full(), ALU.mult)
                    s.dma(out_t[(i - 2) * 128:(i - 1) * 128, :], w2.full(), q="pool")

                pipeline(list(range(2, TAILN)), [stageL, stage0, stage1, stage2])
                s.flush()

    return nc


def _consts():
    c = np.zeros((128, 6, 512), np.float32)
    j = np.arange(128)[:, None]
    l = np.arange(128)[None, :]
    c[:, 0, :128] = np.eye(128, dtype=np.float32)
    c[0, 0, 128:256] = 1.0
    c[1, 0, 256:384] = 1.0
    c[:, 1, :128] = (j <= l)
    c[:, 2, :128] = (j >= l)
    c[:, 3, :] = 1.0
    nf = np.where(l < j, -30000.0, 0.0).astype(np.float32)
    nb = np.where(l > j, -30000.0, 0.0).astype(np.float32)
    c[:, 4, :] = np.tile(nf, (1, 4))
    c[:, 5, :] = np.tile(nb, (1, 4))
    return c


def _rope_tables():
    rows = L // 64
    row = np.repeat(np.arange(rows, dtype=np.float32), 64)
    col = np.tile(np.arange(64, dtype=np.float32), rows)
    n_freq = 16
    inv = (np.float32(10000.0) ** (-np.arange(n_freq, dtype=np.float32) / n_freq)).astype(np.float32)
    ang = np.concatenate([row[:, None] * inv, col[:, None] * inv], axis=-1).astype(np.float32)
    cos = np.cos(ang).astype(np.float32)
    sin = np.sin(ang).astype(np.float32)
    cosT = np.zeros((128, L), np.float32)
    sinT = np.zeros((128, L), np.float32)
    for h2 in range(2):
        for half in range(2):
            p0 = h2 * 64 + half * 32
            cosT[p0:p0 + 32] = cos.T
            sinT[p0:p0 + 32] = (-sin.T if half == 0 else sin.T)
    return np.stack([cosT, sinT], axis=1)


def _vecT(v, nchunk):
    return np.ascontiguousarray(np.asarray(v, np.float32).reshape(nchunk, 128).T)


def prep_inputs(b, inp):
    f = lambda a: np.ascontiguousarray(np.asarray(a, np.float32))
    m = {}
    m["xin"] = f(np.concatenate([inp["ctx"][b], inp["x"][b]], axis=0))
    cv = np.stack([inp["c"][b], inp["c_ctx"]], axis=0)
    m["cvecT"] = f(cv.reshape(2, 8, 128).transpose(2, 0, 1))
    m["consts"] = _consts()
    m["rope"] = _rope_tables()
    m["e_ada_w"] = f(inp["e_ada_w"][0])
    m["e_ada_b"] = f(inp["e_ada_b"][0]).reshape(1, -1)
    m["e_norm_wT"] = _vecT(inp["e_norm_w"][0], 8)
    w = f(inp["e_w_in"][0])
    q = w[:, OFF_Q:OFF_Q + 1024].reshape(D, 16, 2, 32)
    qs = q[:, :, ::-1, :].reshape(D, 1024)
    k = w[:, OFF_KV:OFF_KV + 256].reshape(D, 4, 64)
    kr = np.concatenate([k, k], axis=2).reshape(D, 512)
    ks = k.reshape(D, 4, 2, 32)[:, :, ::-1, :].reshape(D, 4, 64)
    ksr = np.concatenate([ks, ks], axis=2).reshape(D, 512)
    m["e_w_in"] = f(np.concatenate([w, qs, kr, ksr], axis=1))
    cw = f(inp["e_conv_w"][0])
    m["e_conv_wT"] = f(cw.reshape(5, 12, 128).transpose(2, 1, 0))
    m["e_conv_bT"] = _vecT(inp["e_conv_b"][0], 12)
    m["e_dt_bias"] = f(inp["e_dt_bias"][0]).reshape(1, 32)
    m["e_a_log"] = f(inp["e_a_log"][0]).reshape(1, 32)
    m["e_d_skip"] = f(inp["e_d_skip"][0]).reshape(1, 16)
    m["e_ssd_norm_wT"] = _vecT(inp["e_ssd_norm_w"][0], 8)
    sk = f(inp["e_sink"][0]).reshape(8, 2)
    m["e_sink"] = f(np.repeat(sk.T[:, None, :], 64, axis=1).reshape(128, 8))
    m["e_w_out"] = f(inp["e_w_out"][0])
    m["o_ada_w"] = f(inp["o_ada_w"][0])
    m["o_ada_b"] = f(inp["o_ada_b"][0]).reshape(1, -1)
    m["o_norm_wT"] = _vecT(inp["o_norm_w"][0], 8)
    m["o_w_in"] = f(inp["o_w_in"][0])

    def gl(a):
        a = np.asarray(a, np.float32)
        rest = a.shape[2:]
        a = a.reshape((32, 2, 64) + rest)
        a = np.moveaxis(a, 0, 2)
        return a.reshape((128, 32) + rest)

    lam = np.zeros((128, 2, 3, 32), np.float32)
    for d_ in range(2):
        lam[:, d_, 0] = gl(inp["o_lam_re"][0][d_])
        lam[:, d_, 1] = gl(inp["o_lam_im"][0][d_])
        lam[:, d_, 2] = gl(np.repeat(np.asarray(inp["o_log_step"][0][d_])[:, None], 64, axis=1))
    m["s5_lam"] = f(lam)
    m["s5_b"] = f(np.stack([gl(inp["o_b_re"][0]), gl(inp["o_b_im"][0])], axis=1))
    cr = np.asarray(inp["o_c_re"][0]).transpose(0, 2, 1)
    ci = np.asarray(inp["o_c_im"][0]).transpose(0, 2, 1)
    m["s5_c"] = f(np.stack([gl(cr), gl(ci)], axis=1))
    m["o_d_skip"] = f(inp["o_d_skip"][0]).reshape(1, -1)
    m["o_glu_w"] = f(inp["o_glu_w"][0])
    m["o_glu_b"] = f(inp["o_glu_b"][0]).reshape(1, -1)
    m["o_w_out"] = f(inp["o_w_out"][0])
    m["final_norm_w"] = f(inp["final_norm_w"]).reshape(1, -1)
    return m


def kernel(**inputs):
    nc = build_program()
    in_maps = [prep_inputs(b, inputs) for b in range(8)]
    res = run_bass_kernel_spmd(nc, in_maps, core_ids=list(range(8)))
    return np.stack([r["out"] for r in res.results], axis=0)
```

```python
import math
import os
from contextlib import ExitStack

import numpy as np
import concourse.bass as bass
import concourse.mybir as mybir
from concourse.bass_utils import run_bass_kernel_spmd

F32 = mybir.dt.float32
BF16 = mybir.dt.bfloat16
AF = mybir.ActivationFunctionType
ALU = mybir.AluOpType

D = 1024
T = 2304
NT = 18
CTX = 256
L = 2048
EPS = 1e-6
TG = [(0, 256), (256, 512), (768, 512), (1280, 512), (1792, 512)]

SES_ALL = os.environ.get('SES', '0') == '1'
SAME_ENGINE_SYNC = {'act': SES_ALL, 'dve': SES_ALL, 'pool': True, 'pe': False, 'sp': True}
SEM_EPOCH = 30000


class V:
    __slots__ = ("buf", "ap")

    def __init__(self, buf, ap):
        self.buf = buf
        self.ap = ap


class Buf:
    def __init__(self, name, h):
        self.name = name
        self.h = h
        self.last_w = None
        self.readers = []
        self.is_psum = False

    def __getitem__(self, idx):
        return V(self, self.h[idx])

    def full(self):
        return V(self, self.h.ap())

    def view(self, offset, pattern):
        return V(self, bass.AP(self.h, offset, [list(p) for p in pattern]))


class Sched:
    ENG = ("pe", "act", "dve", "pool", "sp")

    def __init__(self, nc):
        self.nc = nc
        self.prog = {e: [] for e in self.ENG}
        self.sem = {}
        self.cnt = {}
        self.semid = 0
        self.known = {e: {} for e in self.ENG}
        for e in ("pe", "act", "dve", "pool"):
            self._new_engine_sem(e)
        self.nds = 8
        self.dsem = {}
        self.duse = {}
        self.dcnt = {}
        for q in ("sp", "pool"):
            self.dsem[q] = []
            self.duse[q] = []
            for i in range(self.nds):
                key = "d_%s_%d" % (q, i)
                self.dsem[q].append((nc.alloc_semaphore(key), key))
                self.duse[q].append(0)
            self.dcnt[q] = 0
        self.n_ops = 0

    def _new_engine_sem(self, e):
        self.semid += 1
        key = "s_%s_%d" % (e, self.semid)
        self.sem[e] = (self.nc.alloc_semaphore(key), key)
        self.cnt[e] = 0

    def _deps(self, reads, writes):
        deps = {}

        def add(tok):
            if tok is None:
                return
            h, key, val = tok
            if key not in deps or deps[key][1] < val:
                deps[key] = (h, val)

        for r in reads:
            add(r.buf.last_w)
            if r.buf.is_psum:
                for t in r.buf.readers:
                    add(t)
        for w in writes:
            add(w.buf.last_w)
            for t in w.buf.readers:
                add(t)
        return deps

    def _emit_waits(self, eng, deps, own_key=None):
        kn = self.known[eng]
        for key, (h, val) in deps.items():
            if key == own_key and not SAME_ENGINE_SYNC[eng]:
                continue
            if kn.get(key, 0) >= val:
                continue
            kn[key] = val
            self.prog[eng].append(("wait", h, val))

    def _update(self, tok, reads, writes):
        for w in writes:
            w.buf.last_w = tok
            w.buf.readers = []
        for r in reads:
            if r.buf.last_w is not tok:
                r.buf.readers.append(tok)

    def op(self, eng, fn, reads=(), writes=()):
        reads = [r for r in reads if r is not None]
        writes = list(writes)
        if self.cnt[eng] >= SEM_EPOCH:
            self._new_engine_sem(eng)
        h, key = self.sem[eng]
        own = None if eng == "pe" else key
        deps = self._deps(reads, writes)
        if eng == "pe":
            deps.pop(key, None)
        self._emit_waits(eng, deps, own_key=own)
        self.cnt[eng] += 1
        self.prog[eng].append(("op", fn, h, 1))
        tok = (h, key, self.cnt[eng])
        self._update(tok, reads, writes)
        self.n_ops += 1
        return tok

    def dma(self, out, in_, q="sp", **kw):
        deps = self._deps([in_], [out])
        self._emit_waits(q, deps)
        k = self.dcnt[q] % self.nds
        self.dcnt[q] += 1
        h, key = self.dsem[q][k]
        prev = 16 * self.duse[q][k]
        if prev > 0 and self.known[q].get(key, 0) < prev:
            self.known[q][key] = prev
            self.prog[q].append(("wait", h, prev))
        self.duse[q][k] += 1
        val = 16 * self.duse[q][k]
        o_ap, i_ap = out.ap, in_.ap
        self.prog[q].append(("op", lambda e: e.dma_start(out=o_ap, in_=i_ap, **kw), h, 16))
        tok = (h, key, val)
        self._update(tok, [in_], [out])
        self.n_ops += 1
        return tok

    def finish_dmas(self):
        for q in ("sp", "pool"):
            for k in range(self.nds):
                h, key = self.dsem[q][k]
                val = 16 * self.duse[q][k]
                if val > 0 and self.known[q].get(key, 0) < val:
                    self.known[q][key] = val
                    self.prog[q].append(("wait", h, val))

    def flush(self, name=None):
        self.finish_dmas()
        nc = self.nc
        prog = self.prog
        self.prog = {e: [] for e in self.ENG}

        def run(items, e):
            for it in items:
                if it[0] == "wait":
                    e.wait_ge(it[1], it[2])
                else:
                    inst = it[1](e)
                    inst.then_inc(it[2], it[3])

        with nc.Block() as block:
            if prog["sp"]:
                @block.sync
                def _(e):
                    run(prog["sp"], e)
            if prog["act"]:
                @block.scalar
                def _(e):
                    run(prog["act"], e)
            if prog["dve"]:
                @block.vector
                def _(e):
                    run(prog["dve"], e)
            if prog["pool"]:
                @block.gpsimd
                def _(e):
                    run(prog["pool"], e)
            if prog["pe"]:
                @block.tensor
                def _(e):
                    run(prog["pe"], e)

    def mm(self, out, pairs):
        n = len(pairs)

        def fn(e):
            inst = None
            for i, (l, r) in enumerate(pairs):
                inst = e.matmul(out.ap, l.ap, r.ap, start=(i == 0), stop=(i == n - 1))
            return inst

        self.op("pe", fn, reads=[p[0] for p in pairs] + [p[1] for p in pairs], writes=[out])

    def mm1(self, out, l, r, start, stop):
        self.op("pe", lambda e: e.matmul(out.ap, l.ap, r.ap, start=start, stop=stop), reads=[l, r], writes=[out])

    def transpose(self, out, in_, ident):
        self.op("pe", lambda e: e.transpose(out.ap, in_.ap, ident.ap), reads=[in_, ident], writes=[out])

    def act(self, out, in_, func, bias=None, scale=None, accum=None):
        kw = {}
        reads = [in_]
        writes = [out]
        if bias is not None:
            if isinstance(bias, V):
                kw["bias"] = bias.ap
                reads.append(bias)
            else:
                kw["bias"] = bias
        if scale is not None:
            if isinstance(scale, V):
                kw["scale"] = scale.ap
                reads.append(scale)
            else:
                kw["scale"] = scale
        if accum is not None:
            kw["accum_out"] = accum.ap
            writes.append(accum)
        self.op("act", lambda e: e.activation(out.ap, in_.ap, func, **kw), reads=reads, writes=writes)

    def ts(self, out, in0, s1, s2, op0, op1=None, eng="dve"):
        reads = [in0]
        a1 = s1
        a2 = s2
        if isinstance(s1, V):
            reads.append(s1)
            a1 = s1.ap
        if isinstance(s2, V):
            reads.append(s2)
            a2 = s2.ap
        if op1 is None:
            self.op(eng, lambda e: e.tensor_scalar(out.ap, in0.ap, a1, a2, op0), reads=reads, writes=[out])
        else:
            self.op(eng, lambda e: e.tensor_scalar(out.ap, in0.ap, a1, a2, op0, op1), reads=reads, writes=[out])

    def tt(self, out, in0, in1, op, eng="dve"):
        self.op(eng, lambda e: e.tensor_tensor(out.ap, in0.ap, in1.ap, op), reads=[in0, in1], writes=[out])

    def stt(self, out, in0, scalar, in1, op0, op1):
        reads = [in0, in1]
        sc = scalar
        if isinstance(scalar, V):
            reads.append(scalar)
            sc = scalar.ap
        self.op("dve", lambda e: e.scalar_tensor_tensor(out.ap, in0.ap, sc, in1.ap, op0, op1),
                reads=reads, writes=[out])

    def copy(self, out, in_, eng="dve"):
        if eng == "act":
            self.op("act", lambda e: e.copy(out.ap, in_.ap), reads=[in_], writes=[out])
        else:
            self.op(eng, lambda e: e.tensor_copy(out.ap, in_.ap), reads=[in_], writes=[out])

    def recip(self, out, in_):
        self.op("dve", lambda e: e.reciprocal(out.ap, in_.ap), reads=[in_], writes=[out])

    def memset(self, out, val, eng="dve"):
        self.op(eng, lambda e: e.memset(out.ap, val), reads=[], writes=[out])


class Ctx:
    def __init__(self, nc, sched):
        self.nc = nc
        self.s = sched
        self.uid = 0

    def sb(self, es, name, shape, dtype=F32):
        self.uid += 1
        h = es.enter_context(self.nc.sbuf_tensor("%s_%d" % (name, self.uid), list(shape), dtype))
        return Buf(name, h)

    def ps(self, es, name, shape=(128, 512), dtype=F32):
        self.uid += 1
        h = es.enter_context(self.nc.psum_tensor("%s_%d" % (name, self.uid), list(shape), dtype))
        b = Buf(name, h)
        b.is_psum = True
        return b

    def dram(self, name, shape, dtype=F32, kind="Internal"):
        h = self.nc.dram_tensor(name, list(shape), dtype, kind=kind)
        return Buf(name, h)


def pipeline(items, stages):
    n, k = len(items), len(stages)
    for t in range(n + k - 1):
        for j in range(k - 1, -1, -1):
            i = t - j
            if 0 <= i < n:
                stages[j](items[i])


def bc_mid(v_buf, base_off, pstep, nparts, n_outer, outer_step, n_inner):
    return v_buf.view(base_off, [[pstep, nparts], [outer_step, n_outer], [0, n_inner]])


E_NCOL = 5152
OFF_Z = 0
OFF_XBC = 1024
OFF_DT = 2560
OFF_Q = 2592
OFF_KV = 3616
OFF_G = 4128
OFF_QS = 5152
OFF_KR = 6176
OFF_KSR = 6688
E_NCOL_EXT = 7200


ORDER = ["p1", "p2a", "p2b", "p2c", "p2d", "p2e", "p2f", "p2g", "p2h", "p3", "p4", "p5", "all"]


def build_program(debug=(), stop="all"):
    def go(tag):
        return ORDER.index(tag) <= ORDER.index(stop)
    nc = bass.Bass("TRN2", target_bir_lowering=False)
    s = Sched(nc)
    cx = Ctx(nc, s)
    dbg = set(debug)

    def din(name, shape):
        return Buf(name, nc.dram_tensor(name, list(shape), F32, kind="ExternalInput"))

    def dout(name, shape):
        return Buf(name, nc.dram_tensor(name, list(shape), F32, kind="ExternalOutput"))

    def scratch(name, shape, dtype=F32):
        if name in dbg:
            return dout(name, shape)
        return Buf(name, nc.dram_tensor(name, list(shape), dtype))

    xin = din("xin", [T, D])
    cvecT = din("cvecT", [128, 2, 8])
    consts = din("consts", [128, 6, 512])
    rope = din("rope", [128, 2, L])
    e_ada_w = din("e_ada_w", [D, 3 * D])
    e_ada_b = din("e_ada_b", [1, 3 * D])
    e_norm_wT = din("e_norm_wT", [128, 8])
    e_w_in = din("e_w_in", [D, E_NCOL_EXT])
    e_conv_wT = din("e_conv_wT", [128, 12, 5])
    e_conv_bT = din("e_conv_bT", [128, 12])
    e_dt_bias = din("e_dt_bias", [1, 32])
    e_a_log = din("e_a_log", [1, 32])
    e_d_skip = din("e_d_skip", [1, 16])
    e_ssd_norm_wT = din("e_ssd_norm_wT", [128, 8])
    e_sink = din("e_sink", [128, 8])
    e_w_out = din("e_w_out", [2 * D, D])
    o_ada_w = din("o_ada_w", [D, 3 * D])
    o_ada_b = din("o_ada_b", [1, 3 * D])
    o_norm_wT = din("o_norm_wT", [128, 8])
    o_w_in = din("o_w_in", [D, 2 * D])
    s5_lam = din("s5_lam", [128, 2, 3, 32])
    s5_b = din("s5_b", [128, 2, 32, 16])
    s5_c = din("s5_c", [128, 2, 32, 16])
    o_d_skip = din("o_d_skip", [1, D])
    o_glu_w = din("o_glu_w", [D, D])
    o_glu_b = din("o_glu_b", [1, D])
    o_w_out = din("o_w_out", [D, D])
    final_norm_w = din("final_norm_w", [1, D])
    out_t = dout("out", [L, D])

    XS = scratch("XS", [T, 1024])
    BTOK = scratch("BTOK", [T, 256], BF16)
    BT = scratch("BT", [2, 128, T], BF16)
    CT = scratch("CT", [2, 128, T], BF16)
    SZ = scratch("SZ", [T, 1024])
    QR = scratch("QR", [8, 128, L], BF16)
    QC = scratch("QC", [8, 128, CTX], BF16)
    KR = scratch("KR", [4, 128, L], BF16)
    KC = scratch("KC", [4, 128, CTX], BF16)
    VT = scratch("VT", [T, 256], BF16)
    SG = scratch("SG", [8, 128, T])
    YF = scratch("YF", [T, 1024])
    YT = scratch("YT", [16, 128, T], BF16)
    X1 = scratch("X1", [T, 1024])
    U = scratch("U", [T, 1024])
    SG1 = scratch("SG1", [T, 1024])
    YTOK = scratch("YTOK", [T, 1024])
    KFP = scratch("KFP", [64, 16, 15, 16], BF16)
    KBR = scratch("KBR", [64, 16, 15, 16], BF16)
    HT = scratch("HT", [8, 128, T]) if "HT" in dbg else None
    DTD = scratch("DTD", [T, 32]) if "DTD" in dbg else None
    MODD = scratch("MODD", [4, 128, 24]) if "MODD" in dbg else None

    with ExitStack() as top:
        banks = [cx.ps(top, "bank%d" % i) for i in range(8)]
        cst = cx.sb(top, "cst", [128, 6, 512])
        s.dma(cst.full(), consts.full())
        ident = cst[:, 0, 0:128]
        tri = cst[:, 1, 0:128]
        utri = cst[:, 2, 0:128]
        ones = cst[:, 3, 0:128]
        onesb_t = cx.sb(top, "onesb", [128, 128], BF16)
        s.memset(onesb_t.full(), 1.0)
        onesb = onesb_t.full()
        modT = [[cx.sb(top, "modT%d%d" % (l, w), [128, 24]) for w in range(2)] for l in range(2)]
        gate_bc = [[cx.sb(top, "gate%d%d" % (l, w), [128, 1024]) for w in range(2)] for l in range(2)]
        scs = cx.sb(top, "scs", [128, 2, 8])

        def adaln_phase(layer, ada_w, ada_b):
            with ExitStack() as es:
                aw = [cx.sb(es, "aw%d" % k, [128, 3 * D]) for k in range(8)]
                ab2 = cx.sb(es, "ab2", [2, 3 * D])
                modrow2 = cx.sb(es, "modrow2", [2, 3 * D])
                if layer == 0:
                    cv = cx.sb(es, "cv", [128, 2, 8])
                    s.dma(cv.full(), cvecT.full())
                    s.act(scs.full(), cv.full(), AF.Silu)
                s.dma(ab2[0:1, :], ada_b.full())
                s.dma(ab2[1:2, :], ada_b.full())
                for k in range(8):
                    s.dma(aw[k].full(), ada_w[k * 128:(k + 1) * 128, :])
                for k in range(8):
                    for fg in range(6):
                        s.mm1(banks[fg][0:2, :], scs.view(k, [[16, 128], [8, 2]]), aw[k][:, fg * 512:(fg + 1) * 512],
                              start=(k == 0), stop=(k == 7))
                for fg in range(6):
                    s.tt(modrow2[0:2, fg * 512:(fg + 1) * 512], banks[fg][0:2, :], ab2[0:2, fg * 512:(fg + 1) * 512], ALU.add)
                bk = banks[6]
                for fc in range(24):
                    s.mm(bk[:, 2 * fc:2 * fc + 2], [(modrow2[0:2, fc * 128:(fc + 1) * 128], cst[0:2, 0, 0:2])])
                for w in range(2):
                    s.copy(modT[layer][w].full(), bk.view(w, [[512, 128], [2, 24]]))
                bi = 0
                for w in range(2):
                    selw = cst[0:2, 0, 128 + 128 * w:256 + 128 * w]
                    for hh in range(2):
                        bk2 = banks[(7 + bi) % 8]
                        bi += 1
                        s.mm(bk2.full(), [(selw, modrow2[0:2, 2048 + hh * 512:2048 + (hh + 1) * 512])])
                        s.copy(gate_bc[layer][w][:, hh * 512:(hh + 1) * 512], bk2.full(), eng="act")
                    if MODD is not None:
                        s.dma(MODD[layer * 2 + w], modT[layer][w].full())
                s.flush()

        adaln_phase(0, e_ada_w, e_ada_b)

        with ExitStack() as l0:
            DT = cx.sb(l0, "DT", [128, NT, 32])
            DTA = cx.sb(l0, "DTA", [128, NT, 32])
            nw = cx.sb(l0, "nw", [128, 8])
            sc1 = [cx.sb(l0, "sc1_%d" % w, [128, 8]) for w in range(2)]
            s.dma(nw.full(), e_norm_wT.full())
            for w in range(2):
                s.stt(sc1[w].full(), modT[0][w][:, 8:16], 1.0, nw.full(), ALU.add, ALU.mult)

            wo = [cx.sb(l0, "wo%d" % k, [128, D], BF16) for k in range(16)]
            hts = ExitStack()
            hT = [cx.sb(hts, "hT%d" % k, [128, T], BF16) for k in range(8)]
            with ExitStack() as es:
                xt = [cx.sb(es, "xt%d" % i, [128, D]) for i in range(3)]
                xn = [cx.sb(es, "xn%d" % i, [128, D]) for i in range(3)]
                junk = cx.sb(es, "junk", [128, D])
                st = [cx.sb(es, "st%d" % i, [128, 4]) for i in range(3)]
                def n0(i):
                    x_, n_, st_ = xt[i % 3], xn[i % 3], st[i % 3]
                    s.dma(x_.full(), xin[i * 128:(i + 1) * 128, :])
                    s.act(junk.full(), x_.full(), AF.Square, accum=st_[:, 0:1])
                    s.ts(st_[:, 1:2], st_[:, 0:1], 1.0 / D, EPS, ALU.mult, ALU.add)
                    s.act(st_[:, 2:3], st_[:, 1:2], AF.Sqrt)
                    s.recip(st_[:, 3:4], st_[:, 2:3])
                    s.ts(n_.full(), x_.full(), st_[:, 3:4], None, ALU.mult)

                def n1(i):
                    w = 1 if i < 2 else 0
                    n_ = xn[i % 3]
                    for half in range(2):
                        bk = banks[(2 * i + half) % 8]
                        for kk in range(4):
                            k = half * 4 + kk
                            s.transpose(bk[:, kk * 128:(kk + 1) * 128], n_[:, k * 128:(k + 1) * 128], ident)
                        for kk in range(4):
                            k = half * 4 + kk
                            s.act(hT[k][:, i * 128:(i + 1) * 128], bk[:, kk * 128:(kk + 1) * 128], AF.Identity,
                                  bias=modT[0][w][:, k:k + 1], scale=sc1[w][:, k:k + 1])

                pipeline(list(range(NT)), [n0, n1])
                if HT is not None:
                    for k in range(8):
                        s.dma(HT[k], hT[k].full())
                s.flush()

            with ExitStack() as es:
                WB = 256
                NWB, PF = 6, 4
                wbuf = [cx.sb(es, "wbuf%d" % i, [128, 8, WB], BF16) for i in range(NWB)]
                wplan = [(OFF_XBC + 256 * k, 256) for k in range(6)]
                for qc in range(8):
                    wplan += [(OFF_Q + qc * 128, 128), (OFF_QS + qc * 128, 128)]
                for j in range(4):
                    wplan += [(OFF_KR + j * 128, 128), (OFF_KSR + j * 128, 128)]
                wplan += [(OFF_G + 256 * k, 256) for k in range(4)]
                wplan += [(OFF_Z + 256 * k, 256) for k in range(4)]
                wplan += [(OFF_KV + 256, 256), (OFF_DT, 32)]
                wstate = {"i": 0, "issued": 0}

                def _issue(n):
                    col0, ncol = wplan[n]
                    wb = wbuf[n % NWB]
                    s.dma(wb[:, :, 0:ncol], e_w_in.view(col0, [[E_NCOL_EXT, 128], [128 * E_NCOL_EXT, 8], [1, ncol]]), q="pool")

                def load_w(col0, ncol=WB):
                    i = wstate["i"]
                    wstate["i"] += 1
                    assert wplan[i] == (col0, ncol), (i, wplan[i], col0, ncol)
                    while wstate["issued"] < min(i + PF + 1, len(wplan)):
                        _issue(wstate["issued"])
                        wstate["issued"] += 1
                    return wbuf[i % NWB]

                bstate = {"i": 0}

                def nbank():
                    bk = banks[bstate["i"] % 8]
                    bstate["i"] += 1
                    return bk

                def fm_mm(wb, cc, t0, n):
                    bk = nbank()
                    s.mm(bk[:, 0:n], [(wb[:, k, cc * 128:(cc + 1) * 128], hT[k][:, t0:t0 + n]) for k in range(8)])
                    return bk

                xraws = [cx.sb(es, "xraw%d" % i, [128, T]) for i in range(2)]
                accs = [cx.sb(es, "acc%d" % i, [128, T]) for i in range(2)]
                acc = accs[0]
                accbs = [cx.sb(es, "accb%d" % i, [128, T], BF16) for i in range(2)]
                accb = accbs[0]
                rc_i = {"i": 0}
                tmp1s = [cx.sb(es, "tmp1_%d" % i, [128, 512]) for i in range(2)]
                tmp2s = [cx.sb(es, "tmp2_%d" % i, [128, 512]) for i in range(2)]
                stg = [cx.sb(es, "stg%d" % i, [128, 4, 128]) for i in range(2)]
                stgb = [cx.sb(es, "stgb%d" % i, [128, 4, 128], BF16) for i in range(2)]
                rp = cx.sb(es, "rp", [128, 2, L])
                cw = cx.sb(es, "cw", [128, 12, 5])
                cb = cx.sb(es, "cb", [128, 12])
                dtb = cx.sb(es, "dtb", [128, 32])
                abc = cx.sb(es, "abc", [128, 32])
                s.dma(rp.full(), rope.full())
                s.dma(cw.full(), e_conv_wT.full())
                s.dma(cb.full(), e_conv_bT.full())
                s.dma(dtb.full(), e_dt_bias.view(0, [[0, 128], [1, 32]]))
                s.dma(abc.full(), e_a_log.view(0, [[0, 128], [1, 32]]))
                s.act(abc.full(), abc.full(), AF.Exp)
                s.ts(abc.full(), abc.full(), -1.0, None, ALU.mult)
                stg_i = {"i": 0}

                def transposes_to(dst, col0, src, lowp=False):
                    for i0 in range(0, NT, 4):
                        nb = min(4, NT - i0)
                        bk = nbank()
                        for ii in range(nb):
                            i = i0 + ii
                            s.transpose(bk[:, ii * 128:(ii + 1) * 128], src[:, i * 128:(i + 1) * 128], ident)
                        sg_ = (stgb if lowp else stg)[stg_i["i"] % 2]
                        stg_i["i"] += 1
                        s.copy(sg_[:, 0:nb, :], bk.view(0, [[512, 128], [128, nb], [1, 128]]), eng="act")
                        ncols = dst.h.shape[1]
                        s.dma(dst.view(i0 * 128 * ncols + col0, [[ncols, 128], [128 * ncols, nb], [1, 128]]),
                              sg_[:, 0:nb, :])

                wb_of = {}

                def xa(fc):
                    if fc % 2 == 0:
                        wb_of[fc // 2] = load_w(OFF_XBC + fc * 128)
                    wb = wb_of[fc // 2]
                    xraw = xraws[fc % 2]
                    for (t0, n) in TG:
                        bk = fm_mm(wb, fc % 2, t0, n)
                        s.copy(xraw[:, t0:t0 + n], bk[:, 0:n], eng="act")

                def xb(fc):
                    xraw, acc = xraws[fc % 2], accs[fc % 2]
                    s.ts(acc.full(), xraw.full(), cw[:, fc, 2:3], cb[:, fc:fc + 1], ALU.mult, ALU.add)
                    for kk in (0, 1, 3, 4):
                        d_ = kk - 2
                        for (s0, sl) in ((0, CTX), (CTX, L)):
                            lo = max(s0, s0 - d_)
                            hi = min(s0 + sl, s0 + sl - d_)
                            s.stt(acc[:, lo:hi], xraw[:, lo + d_:hi + d_], cw[:, fc, kk:kk + 1], acc[:, lo:hi],
                                  ALU.mult, ALU.add)
                    s.act(acc.full(), acc.full(), AF.Silu)
                    if fc < 8:
                        transposes_to(XS, fc * 128, acc)
                    elif fc < 10:
                        s.copy(accb.full(), acc.full(), eng="act")
                        s.dma(BT[fc - 8], accb.full())
                        transposes_to(BTOK, (fc - 8) * 128, acc, lowp=True)
                    else:
                        s.copy(accb.full(), acc.full(), eng="act")
                        s.dma(CT[fc - 10], accb.full())

                pipeline(list(range(12 if go('p2a') else 0)), [xa, xb])

                def rope_chunk(col_plain, col_swap, dst_rot, dst_ctx):
                    accb = accbs[rc_i["i"] % 2]
                    rc_i["i"] += 1
                    wa = load_w(col_plain, 128)
                    wsw = load_w(col_swap, 128)
                    for gi, (t0, n) in enumerate(TG):
                        bka = fm_mm(wa, 0, t0, n)
                        if gi == 0:
                            s.copy(accb[:, 0:CTX], bka[:, 0:CTX], eng="act")
                            continue
                        bkb = fm_mm(wsw, 0, t0, n)
                        l0 = t0 - CTX
                        tmp1, tmp2 = tmp1s[gi % 2], tmp2s[gi % 2]
                        s.tt(tmp1.full(), bka.full(), rp[:, 0, l0:l0 + 512], ALU.mult)
                        s.tt(tmp2.full(), bkb.full(), rp[:, 1, l0:l0 + 512], ALU.mult)
                        s.tt(accb[:, t0:t0 + n], tmp1.full(), tmp2.full(), ALU.add)
                    s.dma(dst_ctx, accb[:, 0:CTX])
                    s.dma(dst_rot, accb[:, CTX:T])

                for qc in range(8 if go('p2b') else 0):
                    rope_chunk(OFF_Q + qc * 128, OFF_QS + qc * 128, QR[qc], QC[qc])
                for j in range(4 if go('p2c') else 0):
                    rope_chunk(OFF_KR + j * 128, OFF_KSR + j * 128, KR[j], KC[j])

                for gc in range(8 if go('p2d') else 0):
                    acc = accs[gc % 2]
                    if gc % 2 == 0:
                        wb = load_w(OFF_G + gc * 128)
                    for (t0, n) in TG:
                        bk = fm_mm(wb, gc % 2, t0, n)
                        s.act(acc[:, t0:t0 + n], bk[:, 0:n], AF.Silu)
                    s.dma(SG[gc], acc.full())

                NT_E = NT if go('p2e') else 0
                wz = [load_w(OFF_Z + i * 256) for i in range(4)]
                for i in range(NT_E):
                    z_a = accs[i % 2]
                    for half in range(2):
                        bk = nbank()
                        for q4 in range(2):
                            wbz = wz[half * 2 + q4]
                            s.mm(bk[:, q4 * 256:(q4 + 1) * 256],
                                 [(hT[k][:, i * 128:(i + 1) * 128], wbz[:, k, :]) for k in range(8)])
                        s.act(z_a[:, half * 512:(half + 1) * 512], bk.full(), AF.Silu)
                    s.dma(SZ[i * 128:(i + 1) * 128, :], z_a[:, 0:1024])
                wv = load_w(OFF_KV + 256)
                wdt = load_w(OFF_DT, 32)
                vt = [cx.sb(es, "vt%d" % i, [128, 256], BF16) for i in range(2)]
                for i in range(NT if go('p2f') else 0):
                    bk = nbank()
                    s.mm(bk[:, 0:256], [(hT[k][:, i * 128:(i + 1) * 128], wv[:, k, :]) for k in range(8)])
                    s.copy(vt[i % 2].full(), bk[:, 0:256], eng="act")
                    s.dma(VT[i * 128:(i + 1) * 128, :], vt[i % 2].full())
                for i in range(NT if go('p2g') else 0):
                    bk = nbank()
                    s.mm(bk[:, 0:32], [(hT[k][:, i * 128:(i + 1) * 128], wdt[:, k, 0:32]) for k in range(8)])
                    s.tt(DT[:, i, :], bk[:, 0:32], dtb.full(), ALU.add)
                    if go('p2h'):
                        s.act(DT[:, i, :], DT[:, i, :], AF.Exp)
                        s.ts(DT[:, i, :], DT[:, i, :], 1.0, None, ALU.add)
                        s.act(DT[:, i, :], DT[:, i, :], AF.Ln)
                    s.tt(DTA[:, i, :], DT[:, i, :], abc.full(), ALU.mult)
                    if DTD is not None:
                        s.dma(DTD[i * 128:(i + 1) * 128, :], DT[:, i, :])
                s.flush()
            hts.close()
            for k in range(16):
                s.dma(wo[k].full(), e_w_out[k * 128:(k + 1) * 128, :], q="pool")

            with ExitStack() as es:
                nb_ = {"i": 0}

                def nbank():
                    bk = banks[nb_["i"] % 8]
                    nb_["i"] += 1
                    return bk

                N3 = 3
                NX, NY, NB4 = 2, 5, 4
                xs_t = [cx.sb(es, "xs_t%d" % i, [128, 1024]) for i in range(NX)]
                b_t = [cx.sb(es, "b_t%d" % i, [128, 256], BF16) for i in range(NB4)]
                bt_t = [cx.sb(es, "bt_t%d" % i, [128, 2, 128], BF16) for i in range(NB4)]
                ct_t = [cx.sb(es, "ct_t%d" % i, [128, 2, 128], BF16) for i in range(NB4)]
                yf_t = [cx.sb(es, "yf_t%d" % i, [128, 1024]) for i in range(NY)]
                sz_t = [cx.sb(es, "sz_t%d" % i, [128, 1024]) for i in range(2)]
                MTs = [cx.sb(es, "MT%d" % i, [128, 2048], BF16) for i in range(N3)]
                xcs = [cx.sb(es, "xc%d" % i, [128, 1024], BF16) for i in range(N3)]
                xcds = [cx.sb(es, "xcd%d" % i, [128, 1024], BF16) for i in range(N3)]
                tmpos = [cx.sb(es, "tmpo%d" % i, [128, 1024]) for i in range(N3)]
                ytots = [cx.sb(es, "ytot%d" % i, [128, 1024]) for i in range(2)]
                sms = [cx.sb(es, "sm%d" % i, [128, 4, 16]) for i in range(N3)]
                st3s = [cx.sb(es, "st3_%d" % i, [128, 4]) for i in range(N3)]
                ystgs = [cx.sb(es, "ystg%d" % i, [128, 8, 128], BF16) for i in range(2)]
                dtatris = [cx.sb(es, "dtatri%d" % i, [128, 2048]) for i in range(2)]
                decTs = [cx.sb(es, "decT%d" % i, [128, 2048], BF16) for i in range(2)]
                cb_sbs = [cx.sb(es, "cb_sb%d" % i, [128, 256], BF16) for i in range(2)]
                junk = cx.sb(es, "junk3", [128, 1024])
                Hs = [cx.sb(es, "Hs%d" % g, [128, 512]) for g in range(2)]
                Hb = [cx.sb(es, "Hb%d" % g, [128, 512], BF16) for g in range(2)]
                dsk = cx.sb(es, "dsk", [128, 16])
                snw = cx.sb(es, "snw", [128, 8])
                cm1 = cx.sb(es, "cm1", [128, 1])
                s.memset(cm1.full(), -1.0)
                s.dma(dsk.full(), e_d_skip.view(0, [[0, 128], [1, 16]]))
                s.dma(snw.full(), e_ssd_norm_wT.full())
                cmh = cx.sb(es, "cmh", [128, 1])
                s.memset(cmh.full(), -0.5)

                def bc3(buf, off, pstep, n1, s1, n2, s2):
                    return buf.view(off, [[pstep, 128], [s1, n1], [s2, n2]])

                n_ch = NT if go("p3") else 0
                for d_ in range(2):
                    order = list(range(NT)) if d_ == 0 else [1, 0] + list(range(NT - 1, 1, -1))
                    order = order[:n_ch]
                    TRIoff = 512 if d_ == 0 else 1024
                    TRIv = tri if d_ == 0 else utri
                    negm = cst[:, 4 + d_, :]
                    for g in range(2):
                        s.memset(Hs[g].full(), 0.0)
                        s.memset(Hb[g].full(), 0.0)

                    def stA(item, d_=d_, TRIoff=TRIoff, TRIv=TRIv, negm=negm):
                        ci, i = item
                        p3, p2, p4 = ci % N3, ci % 2, ci % NY
                        xs_, b_, bt_, ct_ = xs_t[ci % NX], b_t[ci % NB4], bt_t[ci % NB4], ct_t[ci % NB4]
                        MT, xc, xcd, sm = MTs[p3], xcs[p3], xcds[p3], sms[p3]
                        dtatri, decT, cb_sb = dtatris[p2], decTs[p2], cb_sbs[p2]
                        dta_i = DTA[:, i, d_ * 16:(d_ + 1) * 16]
                        doff = i * 32 + d_ * 16
                        s.tt(bc3(dtatri, 0, 2048, 16, 128, 128, 1), bc3(DTA, doff, NT * 32, 16, 1, 128, 0),
                             bc3(cst, TRIoff, 3072, 16, 0, 128, 1), ALU.mult, eng="pool")
                        bs = nbank()
                        s.mm(bs[:, 0:16], [(TRIv, dta_i)])
                        s.mm(bs[:, 16:32], [(ones, dta_i)])
                        na, ea, de, cd = sm[:, 0, :], sm[:, 1, :], sm[:, 2, :], sm[:, 3, :]
                        s.ts(na, bs[:, 0:16], -1.0, None, ALU.mult)
                        s.act(ea, bs[:, 0:16], AF.Exp)
                        s.tt(de, bs[:, 16:32], na, ALU.add)
                        s.act(de, de, AF.Exp)
                        s.act(cd, bs[:, 16:32], AF.Exp)
                        for hq in range(4):
                            bq = nbank()
                            s.mm(bq.full(), [(ones, dtatri[:, hq * 512:(hq + 1) * 512]), (ident, negm)])
                            for hh in range(4):
                                h = hq * 4 + hh
                                s.act(decT[:, h * 128:(h + 1) * 128], bq[:, hh * 128:(hh + 1) * 128], AF.Exp,
                                      bias=sm[:, 0, h:h + 1])
                        bc = nbank()
                        for g in range(2):
                            s.mm(bc[:, g * 128:(g + 1) * 128], [(bt_[:, g, :], ct_[:, g, :])])
                        s.copy(cb_sb.full(), bc[:, 0:256], eng="act")
                        for g in range(2):
                            s.tt(bc3(MT, g * 1024, 2048, 8, 128, 128, 1), bc3(decT, g * 1024, 2048, 8, 128, 128, 1),
                                 bc3(cb_sb, g * 128, 256, 8, 0, 128, 1), ALU.mult)
                        s.tt(bc3(xc, 0, 1024, 16, 64, 64, 1), bc3(xs_, 0, 1024, 16, 64, 64, 1),
                             bc3(DT, doff, NT * 32, 16, 1, 64, 0), ALU.mult, eng="pool")
                        s.tt(bc3(xcd, 0, 1024, 16, 64, 64, 1), bc3(xc, 0, 1024, 16, 64, 64, 1),
                             bc3(sm, 32, 64, 16, 1, 64, 0), ALU.mult, eng="pool")
                        if d_ == 1:
                            s.tt(bc3(tmpos[p3], 0, 1024, 16, 64, 64, 1), bc3(xs_, 0, 1024, 16, 64, 64, 1),
                                 bc3(dsk, 0, 16, 16, 1, 64, 0), ALU.mult, eng="pool")
                            s.tt(yf_t[p4].full(), yf_t[p4].full(), tmpos[p3].full(), ALU.add, eng="pool")

                    def stL(item, d_=d_):
                        ci, i = item
                        s.dma(xs_t[ci % NX].full(), XS[i * 128:(i + 1) * 128, :])
                        s.dma(b_t[ci % NB4].full(), BTOK[i * 128:(i + 1) * 128, :])
                        s.dma(bt_t[ci % NB4].full(), BT.view(i * 128, [[T, 128], [128 * T, 2], [1, 128]]))
                        s.dma(ct_t[ci % NB4].full(), CT.view(i * 128, [[T, 128], [128 * T, 2], [1, 128]]))
                        if d_ == 1:
                            s.dma(yf_t[ci % NY].full(), YF[i * 128:(i + 1) * 128, :])

                    def stB(item, d_=d_):
                        ci, i = item
                        p3 = ci % N3
                        b_, ct_ = b_t[ci % NB4], ct_t[ci % NB4]
                        MT, xc, xcd, sm, tmpo, ytot = MTs[p3], xcs[p3], xcds[p3], sms[p3], tmpos[p3], ytots[ci % 2]
                        ydst = yf_t[ci % NY] if d_ == 0 else ytot
                        if d_ == 1:
                            s.dma(sz_t[ci % 2].full(), SZ[i * 128:(i + 1) * 128, :])
                        for g in range(2):
                            by = nbank()
                            for hh in range(8):
                                h = g * 8 + hh
                                s.mm(by[:, hh * 64:(hh + 1) * 64], [(MT[:, h * 128:(h + 1) * 128], xc[:, h * 64:(h + 1) * 64])])
                            bo = nbank()
                            s.mm(bo.full(), [(ct_[:, g, :], Hb[g].full())])
                            s.tt(bc3(tmpo, g * 512, 1024, 8, 64, 64, 1), bc3(bo, 0, 512, 8, 64, 64, 1),
                                 bc3(sm, 16 + g * 8, 64, 8, 1, 64, 0), ALU.mult)
                            s.tt(ydst[:, g * 512:(g + 1) * 512], by.full(), tmpo[:, g * 512:(g + 1) * 512], ALU.add)
                        for g in range(2):
                            bst = nbank()
                            s.mm(bst.full(), [(b_[:, g * 128:(g + 1) * 128], xcd[:, g * 512:(g + 1) * 512])])
                            s.tt(bc3(Hs[g], 0, 512, 8, 64, 64, 1), bc3(Hs[g], 0, 512, 8, 64, 64, 1),
                                 bc3(sm, 48 + g * 8, 64, 8, 1, 64, 0), ALU.mult)
                            s.tt(Hs[g].full(), Hs[g].full(), bst.full(), ALU.add)
                            s.copy(Hb[g].full(), Hs[g].full(), eng="act")
                        if d_ == 0:
                            s.dma(YF[i * 128:(i + 1) * 128, :], yf_t[ci % NY].full())

                    def stC(item, d_=d_):
                        if d_ == 0:
                            return
                        ci, i = item
                        p3, p2 = ci % N3, ci % 2
                        ytot, sz_, st3, ystg = ytots[ci % 2], sz_t[ci % 2], st3s[p3], ystgs[p2]
                        s.tt(ytot.full(), ytot.full(), yf_t[ci % NY].full(), ALU.add)
                        s.tt(ytot.full(), ytot.full(), sz_.full(), ALU.mult)
                        s.act(junk.full(), ytot.full(), AF.Square, accum=st3[:, 0:1])
                        s.ts(st3[:, 1:2], st3[:, 0:1], 1.0 / 1024, EPS, ALU.mult, ALU.add)
                        s.tt(st3[:, 3:4], st3[:, 1:2], cmh.full(), ALU.pow, eng="pool")
                        s.act(ytot.full(), ytot.full(), AF.Copy, scale=st3[:, 3:4])
                        for half in range(2):
                            bk = nbank()
                            for kk in range(4):
                                k = half * 4 + kk
                                s.transpose(bk[:, kk * 128:(kk + 1) * 128], ytot[:, k * 128:(k + 1) * 128], ident)
                            for kk in range(4):
                                k = half * 4 + kk
                                s.act(ystg[:, k, :], bk[:, kk * 128:(kk + 1) * 128], AF.Copy, scale=snw[:, k:k + 1])
                        s.dma(YT.view(i * 128, [[T, 128], [128 * T, 8], [1, 128]]), ystg.full())

                    its = list(enumerate(order))
                    nit = len(its)
                    for t in range(-1, nit + 3):
                        if 0 <= t + 1 < nit:
                            stL(its[t + 1])
                        if 0 <= t - 3 < nit:
                            stC(its[t - 3])
                        if 0 <= t - 2 < nit:
                            stB(its[t - 2])
                        if 0 <= t < nit:
                            stA(its[t])
                s.flush()

            with ExitStack() as es:
                nb_ = {"i": 0}

                def nbank():
                    bk = banks[nb_["i"] % 8]
                    nb_["i"] += 1
                    return bk

                J2 = 2
                qr_ts = [cx.sb(es, "qr_t%d" % i, [128, 2, L], BF16) for i in range(J2)]
                qc_ts = [cx.sb(es, "qc_t%d" % i, [128, 2, CTX], BF16) for i in range(J2)]
                kr_ts = [cx.sb(es, "kr_t%d" % i, [128, L], BF16) for i in range(J2)]
                kc_ts = [cx.sb(es, "kc_t%d" % i, [128, CTX], BF16) for i in range(J2)]
                v_ts = [cx.sb(es, "v_t%d" % i, [128, NT, 64], BF16) for i in range(J2)]
                v2s = [cx.sb(es, "v2_%d" % i, [128, NT, 128], BF16) for i in range(J2)]
                sg_ts = [cx.sb(es, "sg_t%d" % i, [128, 2, T]) for i in range(J2)]
                asts = [cx.sb(es, "ast%d" % i, [128, 2, T], BF16) for i in range(J2)]
                NP = 5
                pt = [[cx.sb(es, "pt%d_%d" % (a, b), [128, 512], BF16) for b in range(5)] for a in range(NP)]
                rds = [cx.sb(es, "rd%d" % i, [128, 256]) for i in range(2)]
                aos = [cx.sb(es, "ao%d" % i, [128, 256]) for i in range(2)]
                es_pp = cx.sb(es, "es_pp", [128, 8])
                c8 = cx.sb(es, "c8", [128, 1])
                s.memset(c8.full(), 0.125)
                cstb = cx.sb(es, "cstb", [128, 2, 128], BF16)
                s.copy(cstb[:, 0, :], tri)
                s.copy(cstb[:, 1, :], utri)
                s.dma(es_pp.full(), e_sink.full())
                s.act(es_pp.full(), es_pp.full(), AF.Exp)
                ATT_DBG = [int(v) for v in os.environ.get("ATT_DBG", "4,18,4").split(",")]
                items = []
                for j in range(ATT_DBG[0] if go("p4") else 0):
                    qbs = ([("c", 0), ("c", 1)] + [("l", b) for b in range(16)])[:ATT_DBG[1]]
                    for qi, (kind, bi) in enumerate(qbs):
                        items.append((len(items), j, kind, bi, qi == 0, qi == len(qbs) - 1))

                def keys_of(kind, bi):
                    keys = [("c", 0, None), ("c", 1, None)]
                    if kind == "l":
                        if bi > 0:
                            keys.append(("l", bi - 1, "prev"))
                        keys.append(("l", bi, None))
                        if bi < 15:
                            keys.append(("l", bi + 1, "next"))
                    return keys

                def atA(item):
                    n, j, kind, bi, first, last = item
                    js = j % J2
                    qr_t, qc_t, kr_t, kc_t, v_t, v2, sg_t = qr_ts[js], qc_ts[js], kr_ts[js], kc_ts[js], v_ts[js], v2s[js], sg_ts[js]
                    if first:
                        s.dma(qr_t.full(), QR.view(2 * j * 128 * L, [[L, 128], [128 * L, 2], [1, L]]))
                        s.dma(qc_t.full(), QC.view(2 * j * 128 * CTX, [[CTX, 128], [128 * CTX, 2], [1, CTX]]))
                        s.dma(kr_t.full(), KR[j])
                        s.dma(kc_t.full(), KC[j])
                        s.dma(v_t.full(), VT.view(j * 64, [[256, 128], [128 * 256, NT], [1, 64]]))
                        s.dma(sg_t.full(), SG.view(2 * j * 128 * T, [[T, 128], [128 * T, 2], [1, T]]))
                        s.copy(v2[:, :, 0:64], v_t.full(), eng="pool")
                        s.copy(v2[:, :, 64:128], v_t.full(), eng="pool")
                    qsrc, q0 = (qc_t, bi * 128) if kind == "c" else (qr_t, bi * 128)
                    pts = pt[n % NP]
                    qw = qsrc.h.shape[2]
                    for ki, (kk, kb, msk) in enumerate(keys_of(kind, bi)):
                        ksrc = kc_t if kk == "c" else kr_t
                        for par in range(2):
                            p0 = par * 64
                            bs = nbank()
                            s.mm(bs[:, 0:256],
                                 [(ksrc[p0:p0 + 64, kb * 128:(kb + 1) * 128],
                                   qsrc.view(p0 * 2 * qw + q0, [[2 * qw, 64], [qw, 2], [1, 128]]))])
                            s.act(pts[ki][:, par * 256:(par + 1) * 256], bs[:, 0:256], AF.Exp, scale=c8[:, 0:1])
                        if msk is not None:
                            moff = 128 if msk == "prev" else 0
                            s.tt(pts[ki].view(0, [[512, 128], [128, 4], [1, 128]]),
                                 pts[ki].view(0, [[512, 128], [128, 4], [1, 128]]),
                                 cstb.view(moff, [[256, 128], [0, 4], [1, 128]]), ALU.mult, eng="pool")

                def atB(item):
                    n, j, kind, bi, first, last = item
                    js = j % J2
                    v2, sg_t, ast = v2s[js], sg_ts[js], asts[js]
                    tok0 = bi * 128 if kind == "c" else CTX + bi * 128
                    keys = keys_of(kind, bi)
                    pts = pt[n % NP]
                    rd, ao = rds[n % 2], aos[n % 2]
                    vt_idx = [(kb if kk == "c" else 2 + kb) for (kk, kb, _) in keys]
                    bn = nbank()
                    s.mm(bn.full(), [(v2[:, vt_idx[ki], :], pts[ki].full()) for ki in range(len(keys))])
                    bd = nbank()
                    s.mm(bd.full(), [(onesb, pts[ki].full()) for ki in range(len(keys))])
                    for par in range(2):
                        p0 = par * 64
                        for c in range(2):
                            s.ts(rd[p0:p0 + 64, c * 128:(c + 1) * 128],
                                 bd[p0:p0 + 64, par * 256 + c * 128:par * 256 + (c + 1) * 128],
                                 es_pp[p0:p0 + 64, 2 * j + c:2 * j + c + 1], None, ALU.add)
                    s.recip(rd.full(), rd.full())
                    for par in range(2):
                        p0 = par * 64
                        s.tt(ao[p0:p0 + 64, :], bn[p0:p0 + 64, par * 256:(par + 1) * 256], rd[p0:p0 + 64, :], ALU.mult)
                    s.tt(ast.view(tok0, [[2 * T, 128], [T, 2], [1, 128]]),
                         ao.view(0, [[256, 128], [128, 2], [1, 128]]),
                         sg_t.view(tok0, [[2 * T, 128], [T, 2], [1, 128]]), ALU.mult)
                    if last:
                        s.dma(YT.view((8 + 2 * j) * 128 * T, [[T, 128], [128 * T, 2], [1, T]]), ast.full())

                pipeline(items, [atA, (lambda it: None), (lambda it: None), atB])
                s.flush()

            with ExitStack() as es:
                nb_ = {"i": 0}

                def nbank():
                    bk = banks[nb_["i"] % 8]
                    nb_["i"] += 1
                    return bk

                ytg = [cx.sb(es, "ytg%d" % i, [128, 16, 512], BF16) for i in range(2)]
                xt = [cx.sb(es, "xt5_%d" % i, [128, D]) for i in range(3)]
                x1t = [cx.sb(es, "x1t%d" % i, [128, D]) for i in range(3)]
                tmp5s = [cx.sb(es, "tmp5_%d" % i, [128, 512]) for i in range(2)]
                for i in range(NT if go("p5") else 0):
                    w = 1 if i < 2 else 0
                    gi_, ii = i // 4, i % 4
                    yg = ytg[gi_ % 2]
                    if ii == 0:
                        nt4 = min(4, NT - i)
                        s.dma(yg[:, :, 0:nt4 * 128], YT.view(i * 128, [[T, 128], [128 * T, 16], [1, nt4 * 128]]))
                    x_, o_ = xt[i % 3], x1t[i % 3]
                    s.dma(x_.full(), xin[i * 128:(i + 1) * 128, :])
                    for half in range(2):
                        tmp5 = tmp5s[half]
                        bk = nbank()
                        s.mm(bk.full(), [(yg[:, fc, ii * 128:(ii + 1) * 128], wo[fc][:, half * 512:(half + 1) * 512]) for fc in range(16)])
                        s.tt(tmp5.full(), bk.full(), gate_bc[0][w][:, half * 512:(half + 1) * 512], ALU.mult)
                        s.tt(o_[:, half * 512:(half + 1) * 512], tmp5.full(), x_[:, half * 512:(half + 1) * 512], ALU.add)
                    s.dma(X1[i * 128:(i + 1) * 128, :], o_.full(), q="pool")
                s.flush()

        if go("all"):
            adaln_phase(1, o_ada_w, o_ada_b)
        with ExitStack() as l1:
            if not go("all"):
                return nc
            nb_ = {"i": 0}

            def nbank():
                bk = banks[nb_["i"] % 8]
                nb_["i"] += 1
                return bk

            with ExitStack() as es:
                nw = cx.sb(es, "nw1", [128, 8])
                sc1 = [cx.sb(es, "sc1b_%d" % w, [128, 8]) for w in range(2)]
                s.dma(nw.full(), o_norm_wT.full())
                for w in range(2):
                    s.stt(sc1[w].full(), modT[1][w][:, 8:16], 1.0, nw.full(), ALU.add, ALU.mult)
                hT = [cx.sb(es, "hTb%d" % k, [128, T], BF16) for k in range(8)]
                xt = [cx.sb(es, "xtb%d" % i, [128, D]) for i in range(3)]
                xn = [cx.sb(es, "xnb%d" % i, [128, D]) for i in range(3)]
                junk = cx.sb(es, "junkb", [128, D])
                st = [cx.sb(es, "stb%d" % i, [128, 4]) for i in range(3)]
                def m0(i):
                    x_, n_, st_ = xt[i % 3], xn[i % 3], st[i % 3]
                    s.dma(x_.full(), X1[i * 128:(i + 1) * 128, :])
                    s.act(junk.full(), x_.full(), AF.Square, accum=st_[:, 0:1])
                    s.ts(st_[:, 1:2], st_[:, 0:1], 1.0 / D, EPS, ALU.mult, ALU.add)
                    s.act(st_[:, 2:3], st_[:, 1:2], AF.Sqrt)
                    s.recip(st_[:, 3:4], st_[:, 2:3])
                    s.ts(n_.full(), x_.full(), st_[:, 3:4], None, ALU.mult)

                def m1(i):
                    w = 1 if i < 2 else 0
                    n_ = xn[i % 3]
                    for half in range(2):
                        bk = nbank()
                        for kk in range(4):
                            k = half * 4 + kk
                            s.transpose(bk[:, kk * 128:(kk + 1) * 128], n_[:, k * 128:(k + 1) * 128], ident)
                        for kk in range(4):
                            k = half * 4 + kk
                            s.act(hT[k][:, i * 128:(i + 1) * 128], bk[:, kk * 128:(kk + 1) * 128], AF.Identity,
                                  bias=modT[1][w][:, k:k + 1], scale=sc1[w][:, k:k + 1])

                pipeline(list(range(NT)), [m0, m1])
                wq = [cx.sb(es, "wq%d" % i, [128, 8, 512], BF16) for i in range(4)]
                for q4_ in range(4):
                    s.dma(wq[q4_].full(), o_w_in.view(q4_ * 512, [[2 * D, 128], [128 * 2 * D, 8], [1, 512]]), q="pool")
                ot = [cx.sb(es, "ot%d" % i, [128, D]) for i in range(2)]
                oi = 0
                for which in range(2):
                    for i in range(NT):
                        if which == 1 and i < 2:
                            continue
                        o_ = ot[oi % 2]
                        oi += 1
                        for half in range(2):
                            bk = nbank()
                            s.mm(bk.full(), [(hT[k][:, i * 128:(i + 1) * 128], wq[which * 2 + half][:, k, :]) for k in range(8)])
                            if which == 0:
                                s.copy(o_[:, half * 512:(half + 1) * 512], bk.full(), eng="act")
                            else:
                                s.act(o_[:, half * 512:(half + 1) * 512], bk.full(), AF.Silu)
                        s.dma((U if which == 0 else SG1)[i * 128:(i + 1) * 128, :], o_.full())
                s.flush()

            L1S = os.environ.get('L1S', 'z')
            if L1S == 'a':
                return nc
            with ExitStack() as es:
                lam = cx.sb(es, "lam", [128, 2, 3, 32])
                bprm = cx.sb(es, "bprm", [128, 2, 32, 16])
                cprm = cx.sb(es, "cprm", [128, 2, 32, 16])
                s.dma(lam.full(), s5_lam.full())
                s.dma(bprm.full(), s5_b.full())
                s.dma(cprm.full(), s5_c.full())
                kc = cx.sb(es, "kconst", [128, 4])
                s.memset(kc[:, 0:1], 1.0 / 16)
                s.memset(kc[:, 1:2], math.pi / 2)
                s.memset(kc[:, 2:3], 0.0)
                s.memset(kc[:, 3:4], 1.0)
                W64 = [128, 2, 32]

                def t64(name):
                    return cx.sb(es, name, W64)

                def lv(i):
                    return lam.view(i * 32, [[192, 128], [96, 2], [1, 32]])

                dt_ = t64("dt_"); mag = t64("mag"); th = t64("th"); cs = t64("cs"); sn = t64("sn")
                t_a = t64("t_a"); t_b = t64("t_b"); t_c = t64("t_c")
                abre = t64("abre"); abim = t64("abim"); cre = t64("cre"); cim = t64("cim")
                s.act(dt_.full(), lv(2), AF.Exp)
                s.tt(t_a.full(), lv(0), dt_.full(), ALU.mult)
                s.act(mag.full(), t_a.full(), AF.Exp)
                s.tt(th.full(), lv(1), dt_.full(), ALU.mult)
                s.act(sn.full(), th.full(), AF.Sin, scale=kc[:, 0:1])
                s.act(cs.full(), th.full(), AF.Sin, scale=kc[:, 0:1], bias=kc[:, 1:2])
                for _ in range(4):
                    s.tt(t_a.full(), cs.full(), cs.full(), ALU.mult)
                    s.tt(t_b.full(), sn.full(), sn.full(), ALU.mult)
                    s.tt(t_c.full(), sn.full(), cs.full(), ALU.mult)
                    s.tt(cs.full(), t_a.full(), t_b.full(), ALU.subtract)
                    s.ts(sn.full(), t_c.full(), 2.0, None, ALU.mult)
                s.tt(abre.full(), mag.full(), cs.full(), ALU.mult)
                s.tt(abim.full(), mag.full(), sn.full(), ALU.mult)
                PW = cx.sb(es, "PW", [128, 2, 9, 64])

                def pw(ri, k):
                    return PW.view((ri * 9 + k) * 64, [[2 * 9 * 64, 128], [32, 2], [1, 32]])

                s.memset(PW[:, 0, 0, :], 1.0)
                s.memset(PW[:, 1, 0, :], 0.0)
                for k in range(8):
                    s.tt(t_a.full(), pw(0, k), abre.full(), ALU.mult)
                    s.tt(t_b.full(), pw(1, k), abim.full(), ALU.mult)
                    s.tt(pw(0, k + 1), t_a.full(), t_b.full(), ALU.subtract)
                    s.tt(t_a.full(), pw(0, k), abim.full(), ALU.mult)
                    s.tt(t_b.full(), pw(1, k), abre.full(), ALU.mult)
                    s.tt(pw(1, k + 1), t_a.full(), t_b.full(), ALU.add)
                s.ts(t_c.full(), abre.full(), -1.0, None, ALU.add)
                s.tt(t_a.full(), lv(0), lv(0), ALU.mult)
                s.tt(t_b.full(), lv(1), lv(1), ALU.mult)
                s.tt(t_a.full(), t_a.full(), t_b.full(), ALU.add)
                s.recip(dt_.full(), t_a.full())
                s.tt(t_a.full(), t_c.full(), lv(0), ALU.mult)
                s.tt(t_b.full(), abim.full(), lv(1), ALU.mult)
                s.tt(t_a.full(), t_a.full(), t_b.full(), ALU.add)
                s.tt(cre.full(), t_a.full(), dt_.full(), ALU.mult)
                s.tt(t_a.full(), abim.full(), lv(0), ALU.mult)
                s.tt(t_b.full(), t_c.full(), lv(1), ALU.mult)
                s.tt(t_a.full(), t_a.full(), t_b.full(), ALU.subtract)
                s.tt(cim.full(), t_a.full(), dt_.full(), ALU.mult)
                BB = cx.sb(es, "BB", [128, 2, 2, 512])
                tb1 = cx.sb(es, "tb1", [128, 512])
                tb2 = cx.sb(es, "tb2", [128, 512])

                def bb(ri, d_, g0=0, ng=32):
                    return BB.view((ri * 2 + d_) * 512 + g0 * 16, [[2048, 128], [16, ng], [1, 16]])

                def v3(buf, off, pstep, n1, s1, n2, s2):
                    return buf.view(off, [[pstep, 128], [s1, n1], [s2, n2]])

                def prm(buf, ri, g0=0, ng=32):
                    return buf.view(ri * 512 + g0 * 16, [[1024, 128], [16, ng], [1, 16]])

                def cf(buf, d_, g0=0, ng=32, n2=16):
                    return buf.view(d_ * 32 + g0, [[64, 128], [1, ng], [0, n2]])

                t1v = v3(tb1, 0, 512, 32, 16, 16, 1)
                t2v = v3(tb2, 0, 512, 32, 16, 16, 1)
                for d_ in range(2):
                    s.tt(t1v, prm(bprm, 0), cf(cre, d_), ALU.mult)
                    s.tt(t2v, prm(bprm, 1), cf(cim, d_), ALU.mult)
                    s.tt(bb(0, d_), t1v, t2v, ALU.subtract)
                    s.tt(t1v, prm(bprm, 1), cf(cre, d_), ALU.mult)
                    s.tt(t2v, prm(bprm, 0), cf(cim, d_), ALU.mult)
                    s.tt(bb(1, d_), t1v, t2v, ALU.add)
                LA = cx.sb(es, "LA", [128, 2, 32, 2])
                LB = cx.sb(es, "LB", [128, 2, 32, 2])
                for ri in range(2):
                    s.copy(LA.view(ri, [[128, 128], [64, 2], [2, 32]]), pw(0, 8))
                s.ts(LB.view(0, [[128, 128], [64, 2], [2, 32]]), pw(1, 8), -1.0, None, ALU.mult)
                s.copy(LB.view(1, [[128, 128], [64, 2], [2, 32]]), pw(1, 8))
                zt_ = cx.sb(es, "zt_", [16, 112], BF16)
                s.memset(zt_.full(), 0.0)
                s.flush()

                if L1S == 'b':
                    return nc
                for b in range(4 if L1S not in ('c1', 'd1', 'e1', 'f1', 'g1') else 1):
                    g0 = 8 * b
                    with ExitStack() as bs_:
                        CAB = cx.sb(bs_, "CAB", [128, 2, 2, 8 * 144])
                        WST = cx.sb(bs_, "WST", [128, 8, 2, 2, 2, 64], BF16)
                        TF = cx.sb(bs_, "TF", [128, 16, 128], BF16)
                        TB = cx.sb(bs_, "TB", [128, 16, 128], BF16)
                        CABb = cx.sb(bs_, "CABb", [128, 2, 2, 8 * 144], BF16)

                        with ExitStack() as tmp:
                            WT = cx.sb(tmp, "WT", [128, 2, 2, 8 * 128])
                            c1 = cx.sb(tmp, "c1", [128, 128])
                            c2 = cx.sb(tmp, "c2", [128, 128])
                            c1v = v3(c1, 0, 128, 8, 16, 16, 1)
                            c2v = v3(c2, 0, 128, 8, 16, 16, 1)
                            for d_ in range(2):
                                for ss in range(8):
                                    p_ = 7 - ss if d_ == 0 else ss
                                    pr = PW.view((0 * 9 + p_) * 64 + d_ * 32 + g0, [[1152, 128], [1, 8], [0, 16]])
                                    pi_ = PW.view((1 * 9 + p_) * 64 + d_ * 32 + g0, [[1152, 128], [1, 8], [0, 16]])
                                    o_re = WT.view((d_ * 2 + 0) * 1024 + ss * 16, [[4096, 128], [128, 8], [1, 16]])
                                    o_im = WT.view((d_ * 2 + 1) * 1024 + ss * 16, [[4096, 128], [128, 8], [1, 16]])
                                    s.tt(c1v, bb(0, d_, g0, 8), pr, ALU.mult)
                                    s.tt(c2v, bb(1, d_, g0, 8), pi_, ALU.mult)
                                    s.tt(o_re, c1v, c2v, ALU.subtract)
                                    s.tt(c1v, bb(1, d_, g0, 8), pr, ALU.mult)
                                    s.tt(c2v, bb(0, d_, g0, 8), pi_, ALU.mult)
                                    s.tt(o_im, c1v, c2v, ALU.add)
                            for gh in range(2):
                                p0 = gh * 64
                                for gq in range(8):
                                    bk = nbank()
                                    for d_ in range(2):
                                        for ri in range(2):
                                            sl = d_ * 2 + ri
                                            s.transpose(bk[:, sl * 64:(sl + 1) * 64],
                                                        WT.view(p0 * 4096 + (d_ * 2 + ri) * 1024 + gq * 128, [[4096, 64], [1, 128]]),
                                                        cst[p0:p0 + 64, 0, p0:p0 + 64])
                                    s.copy(WST.view(((gq * 2 + gh) * 4) * 64, [[4096, 128], [1, 256]]), bk[:, 0:256], eng="act")
                            s.flush()

                        KSB = cx.sb(bs_, "KSB", [16, 2, 16, 128], BF16)

                        def setup_part2():
                            c1v = ysb.view(0, [[512, 128], [16, 8], [1, 16]])
                            c2v = ysb.view(128, [[512, 128], [16, 8], [1, 16]])
                            for d_ in range(2):
                                for idx in range(9):
                                    p_ = idx if d_ == 0 else 8 - idx
                                    pr = PW.view((0 * 9 + p_) * 64 + d_ * 32 + g0, [[1152, 128], [1, 8], [0, 16]])
                                    pi_ = PW.view((1 * 9 + p_) * 64 + d_ * 32 + g0, [[1152, 128], [1, 8], [0, 16]])
                                    o_re = CAB.view((0 * 2 + d_) * 1152 + idx * 16, [[4608, 128], [144, 8], [1, 16]])
                                    o_im = CAB.view((1 * 2 + d_) * 1152 + idx * 16, [[4608, 128], [144, 8], [1, 16]])
                                    s.tt(c1v, prm(cprm, 0, g0, 8), pr, ALU.mult, eng="pool")
                                    s.tt(c2v, prm(cprm, 1, g0, 8), pi_, ALU.mult, eng="pool")
                                    s.tt(o_re, c1v, c2v, ALU.subtract, eng="pool")
                                    s.tt(c1v, prm(cprm, 0, g0, 8), pi_, ALU.mult, eng="pool")
                                    s.tt(c2v, prm(cprm, 1, g0, 8), pr, ALU.mult, eng="pool")
                                    s.tt(c1v, c1v, c2v, ALU.add, eng="pool")
                                    s.ts(o_im, c1v, -1.0, None, ALU.mult, eng="pool")
                            s.copy(CABb.full(), CAB.full(), eng="pool")
                            for gh in range(2):
                                p0 = gh * 64
                                for d_ in range(2):
                                    for gqq in range(2):
                                        bk = nbank()
                                        for q4 in range(4):
                                            gq = gqq * 4 + q4
                                            i0 = 0 if d_ == 0 else 1
                                            s.mm(bk[0:16, q4 * 128:(q4 + 1) * 128],
                                                 [(BB.view(p0 * 2048 + (0 * 2 + d_) * 512 + (g0 + gq) * 16, [[2048, 64], [1, 16]]),
                                                   CAB.view(p0 * 4608 + (0 * 2 + d_) * 1152 + gq * 144 + i0 * 16, [[4608, 64], [1, 128]])),
                                                  (BB.view(p0 * 2048 + (1 * 2 + d_) * 512 + (g0 + gq) * 16, [[2048, 64], [1, 16]]),
                                                   CAB.view(p0 * 4608 + (1 * 2 + d_) * 1152 + gq * 144 + i0 * 16, [[4608, 64], [1, 128]]))])
                                        s.copy(KSB.view(d_ * 2048 + (2 * gqq * 4 + gh) * 128, [[4096, 16], [256, 4], [1, 128]]),
                                               bk.view(0, [[512, 16], [128, 4], [1, 128]]), eng="act")
                            gbase = 16 * b
                            s.dma(KFP.view(gbase * 3840 + 7 * 16, [[240, 16], [3840, 16], [1, 128]]), KSB[:, 0, :, :])
                            s.dma(KBR.view(gbase * 3840, [[240, 16], [3840, 16], [1, 128]]), KSB[:, 1, :, :])
                            s.dma(KFP.view(gbase * 3840, [[240, 16], [3840, 16], [1, 112]]), zt_.view(0, [[112, 16], [0, 16], [1, 112]]))
                            s.dma(KBR.view(gbase * 3840 + 128, [[240, 16], [3840, 16], [1, 112]]), zt_.view(0, [[112, 16], [0, 16], [1, 112]]))
                            for ss in range(8):
                                s.dma(TF[ss * 16:(ss + 1) * 16, :, :], KFP.view(gbase * 3840 + (7 - ss) * 16, [[240, 16], [3840, 16], [1, 128]]))
                                s.dma(TB[ss * 16:(ss + 1) * 16, :, :], KBR.view(gbase * 3840 + (7 - ss) * 16, [[240, 16], [3840, 16], [1, 128]]))

                        u8b = cx.sb(bs_, "u8b", [128, 8, 256])
                        u8g = cx.sb(bs_, "u8g", [128, 16, 128])
                        U8T = cx.sb(bs_, "U8T", [128, 16, 288], BF16)
                        SSb = [cx.sb(bs_, "SSb%d" % i, [128, 16, 256], BF16) for i in range(2)]
                        NCOL = 326
                        PS = 16 * NCOL
                        SSD = [cx.sb(bs_, "SS%d" % i, [128, 8, 2, NCOL]) for i in range(2)]
                        CAR = [cx.sb(bs_, "CAR%d" % i, [128, 15, 8, 2]) for i in range(2)]
                        A36 = [cx.sb(bs_, "A36_%d" % i, [128, 8, 2]) for i in range(2)]
                        B36 = [cx.sb(bs_, "B36_%d" % i, [128, 8, 2]) for i in range(2)]
                        y8b = cx.sb(bs_, "y8b", [128, 8, 256])
                        ysb = cx.sb(bs_, "ysb", [128, 512])
                        TT1 = [cx.sb(bs_, "TT1_%d" % i, [128, 17, 8, 2]) for i in range(2)]
                        TT2 = [cx.sb(bs_, "TT2_%d" % i, [128, 17, 8, 2]) for i in range(2)]
                        for (j0, nj) in ((0, 32), (32, 128), (160, 128)):
                            s.dma(u8b[0:nj, :, :], U.view(8 * j0 * 1024 + 256 * b, [[8192, nj], [1024, 8], [1, 256]]))
                            s.copy(u8g.view(0, [[2048, nj], [128, 16], [16, 8], [1, 16]]),
                                   u8b.view(0, [[2048, nj], [16, 16], [256, 8], [1, 16]]), eng="act")
                            for gq4 in range(4):
                                bk = nbank()
                                for q4 in range(4):
                                    gi = gq4 * 4 + q4
                                    s.transpose(bk[:, q4 * 128:q4 * 128 + nj],
                                                u8g.view(128 * gi, [[2048, nj], [1, 128]]), cst[0:nj, 0, 0:nj])
                                s.copy(U8T.view(gq4 * 4 * 288 + j0, [[16 * 288, 128], [288, 4], [1, nj]]),
                                       bk.view(0, [[512, 128], [128, 4], [1, nj]]), eng="act")
                        if L1S in ('d', 'd1'):
                            s.flush()
                            continue
                        s.memset(SSD[0].view(0, [[PS, 128], [NCOL, 16], [1, 1]]), 0.0)
                        s.memset(SSD[0].view(289, [[PS, 128], [NCOL, 16], [1, 37]]), 0.0)
                        s.memset(SSD[1].view(288, [[PS, 128], [NCOL, 16], [1, 38]]), 0.0)
                        s.memset(SSD[0].view(289, [[PS, 128], [2 * NCOL, 8], [1, 1]]), 1.0)
                        s.memset(SSD[1].view(288 + 18 - 1, [[PS, 128], [2 * NCOL, 8], [1, 1]]), 1.0)
                        for gq in range(8):
                            for gh in range(2):
                                gi = 2 * gq + gh
                                p0 = gh * 64
                                for d_ in range(2):
                                    for ri in range(2):
                                        bk = nbank()
                                        s.mm(bk[p0:p0 + 64, 0:288],
                                             [(WST.view((((gq * 2 + gh) * 2 + d_) * 2 + ri) * 64, [[4096, 128], [1, 64]]),
                                               U8T[:, gi, :])])
                                        so = p0 * PS + (gq * 2 + ri) * NCOL
                                        if d_ == 0:
                                            s.copy(SSD[0].view(so + 1, [[PS, 64], [1, 288]]), bk[p0:p0 + 64, 0:288], eng="act")
                                        else:
                                            s.copy(SSD[1].view(so + 256, [[PS, 64], [1, 32]]), bk[p0:p0 + 64, 0:32], eng="act")
                                            s.copy(SSD[1].view(so, [[PS, 64], [1, 256]]), bk[p0:p0 + 64, 32:288], eng="act")
                        if L1S in ('e', 'e1'):
                            s.flush()
                            continue
                        DS = 8 * 2 * 289
                        setup_part2()
                        RI, GQ = NCOL, 2 * NCOL
                        SEG, NSEG = 18, 16
                        REC_ENG2 = os.environ.get('REC2', 'dve')

                        def cplx_step(items):
                            engs = ("dve", REC_ENG2)
                            for n_, (pv, psw, cv, ca, cb_, t1_, t2_) in enumerate(items):
                                s.tt(t1_, pv, ca, ALU.mult, eng=engs[n_ % 2])
                                s.tt(t2_, psw, cb_, ALU.mult, eng=engs[n_ % 2])
                            for n_, (pv, psw, cv, ca, cb_, t1_, t2_) in enumerate(items):
                                s.tt(t1_, t1_, t2_, ALU.add, eng=engs[n_ % 2])
                            for n_, (pv, psw, cv, ca, cb_, t1_, t2_) in enumerate(items):
                                if cv is not None:
                                    s.tt(cv, cv, t1_, ALU.add, eng=engs[n_ % 2])

                        def segv(SS, col, nseg):
                            return (SS.view(col, [[PS, 128], [SEG, nseg], [GQ, 8], [RI, 2]]),
                                    SS.view(col + RI, [[PS, 128], [SEG, nseg], [GQ, 8], [-RI, 2]]))

                        def coef(buf, d_, nseg):
                            return buf.view(d_ * 64 + g0 * 2, [[128, 128], [0, nseg], [2, 8], [1, 2]])

                        TTP = 17 * 16

                        for k in range(1, SEG):
                            items = []
                            for d_ in range(2):
                                pc = k if d_ == 0 else SEG - k
                                cc = k + 1 if d_ == 0 else SEG - 1 - k
                                pv, psw = segv(SSD[d_], pc, NSEG + 1)
                                cv, _ = segv(SSD[d_], cc, NSEG + 1)
                                items.append((pv, psw, cv, coef(LA, d_, NSEG + 1), coef(LB, d_, NSEG + 1), TT1[d_].full(), TT2[d_].full()))
                            cplx_step(items)
                        items = []
                        for d_ in range(2):
                            clast = 288 + SEG if d_ == 0 else 288
                            pv, psw = segv(SSD[d_], clast, 1)
                            items.append((pv, psw, None, coef(LA, d_, 1), coef(LB, d_, 1),
                                          TT1[d_].view(0, [[TTP, 128], [16, 1], [2, 8], [1, 2]]),
                                          TT2[d_].view(0, [[TTP, 128], [16, 1], [2, 8], [1, 2]])))
                        cplx_step(items)
                        for d_ in range(2):
                            l36re = TT1[d_].view(0, [[TTP, 128], [2, 8], [0, 2]])
                            s.copy(A36[d_].full(), l36re)
                            s.ts(B36[d_][:, :, 0:1], TT1[d_].view(1, [[TTP, 128], [2, 8], [1, 1]]), -1.0, None, ALU.mult)
                            s.copy(B36[d_][:, :, 1:2], TT1[d_].view(1, [[TTP, 128], [2, 8], [1, 1]]))
                        for step in range(1, NSEG):
                            items = []
                            for d_ in range(2):
                                if d_ == 0:
                                    m = step
                                    cc, pc = SEG * m + SEG, SEG * m
                                else:
                                    m = NSEG - 1 - step
                                    cc, pc = SEG * m, SEG * m + SEG
                                pv, psw = segv(SSD[d_], pc, 1)
                                cv, _ = segv(SSD[d_], cc, 1)
                                items.append((pv, psw, cv,
                                              A36[d_].view(0, [[16, 128], [0, 1], [2, 8], [1, 2]]),
                                              B36[d_].view(0, [[16, 128], [0, 1], [2, 8], [1, 2]]),
                                              TT1[d_].view(0, [[TTP, 128], [16, 1], [2, 8], [1, 2]]),
                                              TT2[d_].view(0, [[TTP, 128], [16, 1], [2, 8], [1, 2]])))
                            cplx_step(items)
                        items = []
                        for d_ in range(2):
                            pv, psw = segv(SSD[d_], SEG, NSEG - 1)
                            items.append((pv, psw, None, coef(LA, d_, NSEG - 1), coef(LB, d_, NSEG - 1),
                                          CAR[d_].full(), TT2[d_].view(0, [[TTP, 128], [16, NSEG - 1], [2, 8], [1, 2]])))
                        cplx_step(items)
                        NI = SEG - 1
                        for d_ in range(2):
                            SS = SSD[d_]
                            sb0 = SEG + 1 if d_ == 0 else 1

                            def sview(ri):
                                return SS.view(sb0 + ri * RI, [[PS, 128], [SEG, NSEG - 1], [GQ, 8], [1, NI]])

                            def tview(ri):
                                return SS.view(289 + ri * RI, [[PS, 128], [0, NSEG - 1], [GQ, 8], [1, NI]])

                            def cview(ri):
                                return CAR[d_].view(ri, [[(NSEG - 1) * 16, 128], [16, NSEG - 1], [2, 8], [0, NI]])

                            wshape = [[2048, 128], [8 * NI, NSEG - 1], [NI, 8], [1, NI]]
                            w1 = (u8g if d_ == 0 else u8b).view(0, wshape)
                            w2 = y8b.view(0, wshape)
                            s.tt(w1, tview(0), cview(0), ALU.mult)
                            s.tt(w2, tview(1), cview(1), ALU.mult)
                            s.tt(w1, w1, w2, ALU.subtract)
                            s.tt(sview(0), sview(0), w1, ALU.add)
                            s.tt(w1, tview(0), cview(1), ALU.mult)
                            s.tt(w2, tview(1), cview(0), ALU.mult)
                            s.tt(w1, w1, w2, ALU.add)
                            s.tt(sview(1), sview(1), w1, ALU.add)
                        s.copy(SSb[0].full(), SSD[0].view(32, [[PS, 128], [NCOL, 16], [1, 256]]), eng="act")
                        s.copy(SSb[1].full(), SSD[1].view(1, [[PS, 128], [NCOL, 16], [1, 256]]), eng="pool")
                        if L1S in ('f', 'f1'):
                            s.flush()
                            continue
                        for tt_ in range(2):
                            j0 = 32 + 128 * tt_
                            m0 = 128 * tt_
                            for gh in range(2):
                                p0 = gh * 64
                                for gqq in range(2):
                                    bx = nbank()
                                    by = nbank()
                                    for q4 in range(4):
                                        gq = gqq * 4 + q4
                                        gi = 2 * gq + gh
                                        s.mm(bx[:, q4 * 128:(q4 + 1) * 128],
                                             [(U8T[:, gi, j0:j0 + 128], TF[:, gi, :]), (U8T[:, gi, j0:j0 + 128], TB[:, gi, :])])
                                        pairs = []
                                        for d_ in range(2):
                                            c0 = m0
                                            i0 = 1 if d_ == 0 else 0
                                            for ri in range(2):
                                                so = p0 * 4096 + (gq * 2 + ri) * 256 + c0
                                                pairs.append((SSb[d_].view(so, [[4096, 64], [1, 128]]),
                                                              CABb.view(p0 * 4608 + (ri * 2 + d_) * 1152 + gq * 144 + i0 * 16, [[4608, 64], [1, 128]])))
                                        s.mm(by[:, q4 * 128:(q4 + 1) * 128], pairs)
                                    s.copy(ysb.full(), by.full(), eng="act")
                                    s.tt(y8b.view(32 * gqq * 4 + 16 * gh, [[2048, 128], [32, 4], [256, 8], [1, 16]]),
                                         bx.view(0, [[512, 128], [128, 4], [16, 8], [1, 16]]),
                                         ysb.view(0, [[512, 128], [128, 4], [16, 8], [1, 16]]), ALU.add)
                            s.dma(YTOK.view((CTX + 8 * m0) * 1024 + 256 * b, [[8192, 128], [1024, 8], [1, 256]]), y8b.full())
                        s.flush()

            if L1S in ('g', 'g1'):
                return nc
            with ExitStack() as es:
                gw = [cx.sb(es, "gw%d" % k, [128, D], BF16) for k in range(8)]
                ow = [cx.sb(es, "ow%d" % k, [128, D], BF16) for k in range(8)]
                dskb = cx.sb(es, "dskb", [128, D])
                glbb = cx.sb(es, "glbb", [128, D])
                fnwb = cx.sb(es, "fnwb", [128, D])
                kg = cx.sb(es, "kg", [128, 1])
                s.memset(kg.full(), 2.0 * math.sqrt(2.0 / math.pi))
                kmh = cx.sb(es, "kmh", [128, 1])
                s.memset(kmh.full(), -0.5)
                for k in range(8):
                    s.dma(gw[k].full(), o_glu_w[k * 128:(k + 1) * 128, :], q="pool")
                    s.dma(ow[k].full(), o_w_out[k * 128:(k + 1) * 128, :], q="pool")
                s.dma(dskb.full(), o_d_skip.view(0, [[0, 128], [1, D]]))
                s.dma(glbb.full(), o_glu_b.view(0, [[0, 128], [1, D]]))
                s.dma(fnwb.full(), final_norm_w.view(0, [[0, 128], [1, D]]))
                NB3 = 4
                ya = [cx.sb(es, "ya%d" % i, [128, D]) for i in range(NB3)]
                ua = [cx.sb(es, "ua%d" % i, [128, D]) for i in range(NB3)]
                sga = [cx.sb(es, "sga%d" % i, [128, D]) for i in range(NB3)]
                xa = [cx.sb(es, "xa%d" % i, [128, D]) for i in range(NB3)]
                w1s = [cx.sb(es, "w1_%d" % i, [128, D]) for i in range(NB3)]
                w2s = [cx.sb(es, "w2_%d" % i, [128, D]) for i in range(NB3)]
                w3s = [cx.sb(es, "w3_%d" % i, [128, D]) for i in range(NB3)]
                tTs = [cx.sb(es, "tT_%d" % i, [128, 8, 128], BF16) for i in range(2 * NB3)]
                sts = [cx.sb(es, "st10_%d" % i, [128, 4]) for i in range(NB3)]

                def transp8(src, tT):
                    for half in range(2):
                        bk = nbank()
                        for kk in range(4):
                            k = half * 4 + kk
                            s.transpose(bk[:, kk * 128:(kk + 1) * 128], src[:, k * 128:(k + 1) * 128], ident)
                        s.copy(tT[:, half * 4:(half + 1) * 4, :], bk.view(0, [[512, 128], [128, 4], [1, 128]]), eng="act")

                TAILN = int(os.environ.get('TAILN', NT))

                def bufs(i):
                    b_ = i % NB3
                    return ya[b_], ua[b_], sga[b_], xa[b_], w1s[b_], w2s[b_], w3s[b_], tTs[2 * b_], tTs[2 * b_ + 1], sts[b_]

                def stageL(i):
                    y_, u_, g_, x_, w1, w2, w3, tTa, tTb, st = bufs(i)
                    s.dma(y_.full(), YTOK[i * 128:(i + 1) * 128, :])
                    s.dma(u_.full(), U[i * 128:(i + 1) * 128, :])
                    s.dma(g_.full(), SG1[i * 128:(i + 1) * 128, :])
                    s.dma(x_.full(), X1[i * 128:(i + 1) * 128, :])

                def stage0(i):
                    y_, u_, g_, x_, w1, w2, w3, tTa, tTb, st = bufs(i)
                    s.tt(w1.full(), u_.full(), dskb.full(), ALU.mult)
                    s.tt(y_.full(), y_.full(), w1.full(), ALU.add)
                    s.tt(w1.full(), y_.full(), y_.full(), ALU.mult)
                    s.ts(w1.full(), w1.full(), 0.044715, 1.0, ALU.mult, ALU.add)
                    s.tt(w1.full(), w1.full(), y_.full(), ALU.mult)
                    s.act(w1.full(), w1.full(), AF.Sigmoid, scale=kg[:, 0:1])
                    s.tt(w2.full(), y_.full(), w1.full(), ALU.mult)
                    transp8(w2, tTa)

                def stage1(i):
                    y_, u_, g_, x_, w1, w2, w3, tTa, tTb, st = bufs(i)
                    for half in range(2):
                        bk = nbank()
                        s.mm(bk.full(), [(tTa[:, k, :], gw[k][:, half * 512:(half + 1) * 512]) for k in range(8)])
                        s.tt(w1[:, half * 512:(half + 1) * 512], bk.full(), glbb[:, half * 512:(half + 1) * 512], ALU.add)
                    s.act(w1.full(), w1.full(), AF.Sigmoid)
                    s.tt(w2.full(), w2.full(), w1.full(), ALU.mult)
                    s.tt(w2.full(), w2.full(), g_.full(), ALU.mult)
                    transp8(w2, tTb)

                def stage2(i):
                    y_, u_, g_, x_, w1, w2, w3, tTa, tTb, st = bufs(i)
                    for half in range(2):
                        bk = nbank()
                        s.mm(bk.full(), [(tTb[:, k, :], ow[k][:, half * 512:(half + 1) * 512]) for k in range(8)])
                        s.tt(w1[:, half * 512:(half + 1) * 512], bk.full(), gate_bc[1][0][:, half * 512:(half + 1) * 512], ALU.mult)
                    s.tt(w3.full(), w1.full(), x_.full(), ALU.add)
                    s.act(w1.full(), w3.full(), AF.Square, accum=st[:, 0:1])
                    s.ts(st[:, 1:2], st[:, 0:1], 1.0 / D, EPS, ALU.mult, ALU.add)
                    s.tt(st[:, 3:4], st[:, 1:2], kmh.full(), ALU.pow, eng="pool")
                    s.act(w3.full(), w3.full(), AF.Copy, scale=st[:, 3:4])
                    s.tt(w2.full(), w3.full(), fnwb.full(), ALU.mult)
                    s.dma(out_t[(i - 2) * 128:(i - 1) * 128, :], w2.full(), q="pool")

                pipeline(list(range(2, TAILN)), [stageL, stage0, stage1, stage2])
                s.flush()

    return nc


def _consts():
    c = np.zeros((128, 6, 512), np.float32)
    j = np.arange(128)[:, None]
    l = np.arange(128)[None, :]
    c[:, 0, :128] = np.eye(128, dtype=np.float32)
    c[0, 0, 128:256] = 1.0
    c[1, 0, 256:384] = 1.0
    c[:, 1, :128] = (j <= l)
    c[:, 2, :128] = (j >= l)
    c[:, 3, :] = 1.0
    nf = np.where(l < j, -30000.0, 0.0).astype(np.float32)
    nb = np.where(l > j, -30000.0, 0.0).astype(np.float32)
    c[:, 4, :] = np.tile(nf, (1, 4))
    c[:, 5, :] = np.tile(nb, (1, 4))
    return c


def _rope_tables():
    rows = L // 64
    row = np.repeat(np.arange(rows, dtype=np.float32), 64)
    col = np.tile(np.arange(64, dtype=np.float32), rows)
    n_freq = 16
    inv = (np.float32(10000.0) ** (-np.arange(n_freq, dtype=np.float32) / n_freq)).astype(np.float32)
    ang = np.concatenate([row[:, None] * inv, col[:, None] * inv], axis=-1).astype(np.float32)
    cos = np.cos(ang).astype(np.float32)
    sin = np.sin(ang).astype(np.float32)
    cosT = np.zeros((128, L), np.float32)
    sinT = np.zeros((128, L), np.float32)
    for h2 in range(2):
        for half in range(2):
            p0 = h2 * 64 + half * 32
            cosT[p0:p0 + 32] = cos.T
            sinT[p0:p0 + 32] = (-sin.T if half == 0 else sin.T)
    return np.stack([cosT, sinT], axis=1)


def _vecT(v, nchunk):
    return np.ascontiguousarray(np.asarray(v, np.float32).reshape(nchunk, 128).T)


def prep_inputs(b, inp):
    f = lambda a: np.ascontiguousarray(np.asarray(a, np.float32))
    m = {}
    m["xin"] = f(np.concatenate([inp["ctx"][b], inp["x"][b]], axis=0))
    cv = np.stack([inp["c"][b], inp["c_ctx"]], axis=0)
    m["cvecT"] = f(cv.reshape(2, 8, 128).transpose(2, 0, 1))
    m["consts"] = _consts()
    m["rope"] = _rope_tables()
    m["e_ada_w"] = f(inp["e_ada_w"][0])
    m["e_ada_b"] = f(inp["e_ada_b"][0]).reshape(1, -1)
    m["e_norm_wT"] = _vecT(inp["e_norm_w"][0], 8)
    w = f(inp["e_w_in"][0])
    q = w[:, OFF_Q:OFF_Q + 1024].reshape(D, 16, 2, 32)
    qs = q[:, :, ::-1, :].reshape(D, 1024)
    k = w[:, OFF_KV:OFF_KV + 256].reshape(D, 4, 64)
    kr = np.concatenate([k, k], axis=2).reshape(D, 512)
    ks = k.reshape(D, 4, 2, 32)[:, :, ::-1, :].reshape(D, 4, 64)
    ksr = np.concatenate([ks, ks], axis=2).reshape(D, 512)
    m["e_w_in"] = f(np.concatenate([w, qs, kr, ksr], axis=1))
    cw = f(inp["e_conv_w"][0])
    m["e_conv_wT"] = f(cw.reshape(5, 12, 128).transpose(2, 1, 0))
    m["e_conv_bT"] = _vecT(inp["e_conv_b"][0], 12)
    m["e_dt_bias"] = f(inp["e_dt_bias"][0]).reshape(1, 32)
    m["e_a_log"] = f(inp["e_a_log"][0]).reshape(1, 32)
    m["e_d_skip"] = f(inp["e_d_skip"][0]).reshape(1, 16)
    m["e_ssd_norm_wT"] = _vecT(inp["e_ssd_norm_w"][0], 8)
    sk = f(inp["e_sink"][0]).reshape(8, 2)
    m["e_sink"] = f(np.repeat(sk.T[:, None, :], 64, axis=1).reshape(128, 8))
    m["e_w_out"] = f(inp["e_w_out"][0])
    m["o_ada_w"] = f(inp["o_ada_w"][0])
    m["o_ada_b"] = f(inp["o_ada_b"][0]).reshape(1, -1)
    m["o_norm_wT"] = _vecT(inp["o_norm_w"][0], 8)
    m["o_w_in"] = f(inp["o_w_in"][0])

    def gl(a):
        a = np.asarray(a, np.float32)
        rest = a.shape[2:]
        a = a.reshape((32, 2, 64) + rest)
        a = np.moveaxis(a, 0, 2)
        return a.reshape((128, 32) + rest)

    lam = np.zeros((128, 2, 3, 32), np.float32)
    for d_ in range(2):
        lam[:, d_, 0] = gl(inp["o_lam_re"][0][d_])
        lam[:, d_, 1] = gl(inp["o_lam_im"][0][d_])
        lam[:, d_, 2] = gl(np.repeat(np.asarray(inp["o_log_step"][0][d_])[:, None], 64, axis=1))
    m["s5_lam"] = f(lam)
    m["s5_b"] = f(np.stack([gl(inp["o_b_re"][0]), gl(inp["o_b_im"][0])], axis=1))
    cr = np.asarray(inp["o_c_re"][0]).transpose(0, 2, 1)
    ci = np.asarray(inp["o_c_im"][0]).transpose(0, 2, 1)
    m["s5_c"] = f(np.stack([gl(cr), gl(ci)], axis=1))
    m["o_d_skip"] = f(inp["o_d_skip"][0]).reshape(1, -1)
    m["o_glu_w"] = f(inp["o_glu_w"][0])
    m["o_glu_b"] = f(inp["o_glu_b"][0]).reshape(1, -1)
    m["o_w_out"] = f(inp["o_w_out"][0])
    m["final_norm_w"] = f(inp["final_norm_w"]).reshape(1, -1)
    return m


def kernel(**inputs):
    nc = build_program()
    in_maps = [prep_inputs(b, inputs) for b in range(8)]
    res = run_bass_kernel_spmd(nc, in_maps, core_ids=list(range(8)))
    return np.stack([r["out"] for r in res.results], axis=0)
```
